# Optimizing a Trainium2 kernel written in Bass

```python
import jax, jax.numpy as jnp
from jax import lax
import numpy as np

D_MODEL = 1024
BATCH = 8
SEQ = 2048
DEPTH = 2

HEAD_DIM = 64
N_SB = 6
N_DSA = 6
N_IDX = 8
IDX_DIM = 64
TOPK_MAX = 256
N_HG = 4
HG_DK = 128
HG_DV = 64
D_FF = 4 * D_MODEL
ROPE_THETA = 500000.0
ROT_DIM = HEAD_DIM // 4
Q_BLOCK = 128
HG_CHUNK = 64
EPS = 1e-6
F_MIN = 1e-12
NEG = -1e30
W_SB = N_SB * HEAD_DIM
W_DSA = N_DSA * HEAD_DIM
W_HF = N_HG * HG_DK
W_HV = N_HG * HG_DV
IDX_SCALE = (IDX_DIM * N_IDX) ** -0.5
SPLITS = (W_SB, W_SB, W_SB,
          W_DSA, HEAD_DIM, HEAD_DIM,
          N_IDX * IDX_DIM, IDX_DIM, N_IDX,
          W_HF, W_HF, W_HV, W_HV,
          D_MODEL, D_MODEL, D_MODEL)
D_IN = sum(SPLITS)

kernel_name = 'hybrid_sb_dsa_hgrn2_block'


def _split_points():
    return [int(v) for v in np.cumsum(np.array(SPLITS))[:-1]]


def rms_norm(x, gain):
    xf = x.astype(jnp.float32)
    y = xf * lax.rsqrt(jnp.mean(xf * xf, axis=-1, keepdims=True) + EPS)
    return (y * gain.astype(jnp.float32)).astype(x.dtype)


def partial_rope(x, pos):
    half = ROT_DIM // 2
    inv = ROPE_THETA ** (-(jnp.arange(half, dtype=jnp.float32) * 2.0) / ROT_DIM)
    ang = pos.astype(jnp.float32)[:, None] * inv[None, :]
    cos = jnp.cos(ang)[None, :, None, :]
    sin = jnp.sin(ang)[None, :, None, :]
    xr = x[..., :ROT_DIM].astype(jnp.float32)
    x1, x2 = xr[..., :half], xr[..., half:]
    rot = jnp.concatenate([x1 * cos - x2 * sin, x2 * cos + x1 * sin], axis=-1).astype(x.dtype)
    return jnp.concatenate([rot, x[..., ROT_DIM:]], axis=-1)


def stick_breaking_attention(q, k, v):
    B, T, H, D = q.shape
    scale = D ** -0.5
    outs = []
    for blk in range(T // Q_BLOCK):
        q0, q1 = blk * Q_BLOCK, (blk + 1) * Q_BLOCK
        z = jnp.einsum('bqhd,bkhd->bhqk', q[:, q0:q1], k[:, :q1]).astype(jnp.float32) * scale
        t_pos = q0 + jnp.arange(Q_BLOCK)[:, None]
        s_pos = jnp.arange(q1)[None, :]
        past = s_pos < t_pos
        log_1m = jnp.where(past, jax.nn.log_sigmoid(-z), 0.0)
        later = lax.cumsum(log_1m, axis=3, reverse=True) - log_1m
        log_a = jnp.where(past, jax.nn.log_sigmoid(z) + later, NEG)
        a = jnp.exp(log_a)
        outs.append(jnp.einsum('bhqk,bkhd->bqhd', a.astype(v.dtype), v[:, :q1]))
    return jnp.concatenate(outs, axis=1)


def dsa_attention(q, k, v, q_idx, k_idx, w_idx):
    B, T, H, D = q.shape
    k_top = min(TOPK_MAX, T // 4)
    gather = jax.vmap(lambda arr, idx: arr[idx])
    outs = []
    for blk in range(T // Q_BLOCK):
        q0, q1 = blk * Q_BLOCK, (blk + 1) * Q_BLOCK
        kl = min(T, max(q1, k_top))
        rel = jax.nn.relu(jnp.einsum('bqjd,bkd->bqjk', q_idx[:, q0:q1], k_idx[:, :kl]).astype(jnp.float32))
        score = jnp.einsum('bqj,bqjk->bqk', w_idx[:, q0:q1].astype(jnp.float32) * IDX_SCALE, rel)
        t_pos = q0 + jnp.arange(Q_BLOCK)[:, None]
        s_pos = jnp.arange(kl)[None, :]
        score = jnp.where(s_pos <= t_pos, score, NEG)
        top_score, top_idx = lax.top_k(score, k_top)
        k_sel = gather(k, top_idx)
        v_sel = gather(v, top_idx)
        logits = jnp.einsum('bqhd,bqkd->bhqk', q[:, q0:q1], k_sel).astype(jnp.float32) * (D ** -0.5)
        valid = (top_score > 0.5 * NEG)[:, None]
        logits = jnp.where(valid, logits, NEG)
        p = jax.nn.softmax(logits, axis=-1)
        outs.append(jnp.einsum('bhqk,bqkd->bqhd', p.astype(v.dtype), v_sel))
    return jnp.concatenate(outs, axis=1)


def hgrn2(q, f_pre, inp, lb):
    B, T, H, DK = q.shape
    DV = inp.shape[-1]
    C = HG_CHUNK
    N = T // C
    q = jax.nn.silu(q.astype(jnp.float32))
    lb = lb.astype(jnp.float32)
    f = lb + (1.0 - lb) * jax.nn.sigmoid(f_pre.astype(jnp.float32))
    log_f = jnp.log(jnp.maximum(f, F_MIN))
    k = 1.0 - f

    def to_chunks(a):
        return a.reshape(B, N, C, H, a.shape[-1]).transpose(1, 0, 3, 2, 4)

    qc, kc, vc, gc = (to_chunks(a) for a in (q, k, inp.astype(jnp.float32), log_f))
    causal = jnp.tril(jnp.ones((C, C), dtype=bool))[:, :, None]

    def step(S, xs):
        qt, kt, vt, gt = xs
        b = jnp.cumsum(gt, axis=2)
        diff = b[:, :, :, None, :] - b[:, :, None, :, :]
        decay = jnp.exp(jnp.where(causal, diff, NEG))
        att = jnp.einsum('bhtk,bhsk,bhtsk->bhts', qt, kt, decay)
        o = jnp.einsum('bhts,bhsv->bhtv', att, vt) + jnp.einsum('bhtk,bhkv->bhtv', qt * jnp.exp(b), S)
        b_last = b[:, :, -1:, :]
        S = S * jnp.exp(b_last[:, :, 0, :, None]) + jnp.einsum('bhsk,bhsv->bhkv', kt * jnp.exp(b_last - b), vt)
        return S, o

    S0 = jnp.zeros((B, H, DK, DV), jnp.float32)
    _, o = lax.scan(step, S0, (qc, kc, vc, gc))
    return o.transpose(1, 0, 3, 2, 4).reshape(B, T, H, DV).astype(inp.dtype)


def hybrid_mixer(h, w_in_l, qn, kn, lb, onorm, w_sb_l, w_dsa_l, w_hg_l, w_out_l):
    B, T, _ = h.shape
    pos = jnp.arange(T)
    (sq, sk, sv, dq, dk, dv, iq, ik, iw, hq, hf, hi, hg,
     g_sb, g_dsa, g_hg) = jnp.split(h @ w_in_l, _split_points(), axis=-1)
    heads = lambda a, n: a.reshape(B, T, n, -1)
    y_sb = stick_breaking_attention(heads(sq, N_SB), heads(sk, N_SB), heads(sv, N_SB)).reshape(B, T, W_SB) @ w_sb_l
    q = partial_rope(rms_norm(heads(dq, N_DSA), qn), pos)
    k = partial_rope(rms_norm(dk[:, :, None, :], kn), pos)[:, :, 0]
    q_i = partial_rope(heads(iq, N_IDX), pos)
    k_i = partial_rope(ik[:, :, None, :], pos)[:, :, 0]
    y_dsa = dsa_attention(q, k, dv, q_i, k_i, iw).reshape(B, T, W_DSA) @ w_dsa_l
    o = hgrn2(heads(hq, N_HG), heads(hf, N_HG), heads(hi, N_HG), lb.reshape(N_HG, HG_DK))
    o = rms_norm(o, onorm) * jax.nn.silu(heads(hg, N_HG))
    y_hg = o.reshape(B, T, W_HV) @ w_hg_l
    mixed = jax.nn.sigmoid(g_sb) * y_sb + jax.nn.sigmoid(g_dsa) * y_dsa + jax.nn.sigmoid(g_hg) * y_hg
    return mixed @ w_out_l


def setup_inputs(seed: int = 0) -> dict:
    key = jax.random.key(seed)
    ks = jax.random.split(key, 16)
    f32 = jnp.float32
    res = (2 * DEPTH) ** -0.5

    def nrm(k, shape, fan_in, g=1.0):
        return jax.random.normal(k, shape, f32) * (g * fan_in ** -0.5)

    def gain(k, shape):
        return 1.0 + 0.02 * jax.random.normal(k, shape, f32)

    return {
        'x': jax.random.normal(ks[0], (BATCH, SEQ, D_MODEL), f32),
        'norm_mix': gain(ks[1], (DEPTH, D_MODEL)),
        'w_in': nrm(ks[2], (DEPTH, D_MODEL, D_IN), D_MODEL),
        'qn_dsa': gain(ks[3], (DEPTH, HEAD_DIM)),
        'kn_dsa': gain(ks[4], (DEPTH, HEAD_DIM)),
        'hgrn_lb': 0.5 * jax.random.normal(ks[5], (DEPTH, W_HF), f32),
        'hgrn_onorm': gain(ks[6], (DEPTH, HG_DV)),
        'w_br_sb': nrm(ks[7], (DEPTH, W_SB, D_MODEL), W_SB),
        'w_br_dsa': nrm(ks[8], (DEPTH, W_DSA, D_MODEL), W_DSA),
        'w_br_hgrn': nrm(ks[9], (DEPTH, W_HV, D_MODEL), W_HV),
        'w_out': nrm(ks[10], (DEPTH, D_MODEL, D_MODEL), D_MODEL, res),
        'norm_mlp': gain(ks[11], (DEPTH, D_MODEL)),
        'w_up': nrm(ks[12], (DEPTH, D_MODEL, D_FF), D_MODEL),
        'w_down': nrm(ks[13], (DEPTH, D_FF, D_MODEL), D_FF, res),
    }


def reference(x, norm_mix, w_in, qn_dsa, kn_dsa, hgrn_lb, hgrn_onorm, w_br_sb, w_br_dsa,
              w_br_hgrn, w_out, norm_mlp, w_up, w_down):
    p_lb = jax.nn.softmax(hgrn_lb.astype(jnp.float32), axis=0)
    lbs = jnp.cumsum(p_lb, axis=0) - p_lb[0:1]
    for l in range(DEPTH):
        h = rms_norm(x, norm_mix[l])
        x = x + hybrid_mixer(h, w_in[l], qn_dsa[l], kn_dsa[l], lbs[l], hgrn_onorm[l],
                             w_br_sb[l], w_br_dsa[l], w_br_hgrn[l], w_out[l])
        h2 = rms_norm(x, norm_mlp[l])
        x = x + jnp.square(jax.nn.relu(h2 @ w_up[l])) @ w_down[l]
    return x
```

```python
import math
import numpy as np
from contextlib import ExitStack, contextmanager
import concourse.bass as bass
import concourse.mybir as mybir
from concourse.bass_utils import run_bass_kernel_spmd

F32 = mybir.dt.float32
BF16 = mybir.dt.bfloat16
AF = mybir.ActivationFunctionType
ALU = mybir.AluOpType
AX = mybir.AxisListType

T = 2048
D = 1024
NB = 16
DC = 8
DIN = 6856
DFF = 4096
DEPTH = 2
EPS = 1e-6
F_MIN = 1e-12
IDX_SCALE = (64 * 8) ** -0.5
C_SQ, C_SK, C_SV = 0, 384, 768
C_DQ, C_DK, C_DV = 1152, 1536, 1600
C_IQ, C_IK, C_IW = 1664, 2176, 2240
C_HQ, C_HF, C_HI, C_HG = 2248, 2760, 3272, 3528
C_G = 3784
N_BISECT = 16


class Buf:
    __slots__ = ("name", "w", "r", "dsem", "excl")

    def __init__(self, name, excl=False):
        self.name = name
        self.w = None
        self.r = {}
        self.dsem = None
        self.excl = excl


class Prog:
    ENG = ("pe", "act", "dve", "pool", "sp")
    CLEAR_NS = 330.0
    FILL_NS = {"dve": 66.0, "act": 190.0, "pool": 125.0}
    EST = {"dve": (60.0, 0.26), "act": (185.0, 0.83), "pool": (120.0, 0.8), "pe": (0.0, 0.0), "sp": (0.0, 0.0)}

    def __init__(self, nc, stack, needed=None):
        self.needed = needed
        self.used = set()
        self.remap = {}
        self.sig = {}
        self.fill = {}
        self.nc = nc
        self.stack = stack
        self.eng = {"pe": nc.tensor, "act": nc.scalar, "dve": nc.vector, "pool": nc.gpsimd, "sp": nc.sync}
        self.cnt = {e: 0 for e in self.ENG}
        self.known = {e: {} for e in self.ENG}
        self.sems = {}
        self.semval = {}
        for e in ("pe", "act", "dve", "pool"):
            self.sems["E_" + e] = stack.enter_context(nc.semaphore("sem_" + e))
            self.semval["E_" + e] = 0
        self.ndsem = 0
        self.free_dsems = []
        self.nwaits = 0
        self.tcum = {e: 0.0 for e in self.ENG}
        self.tend = {e: {} for e in self.ENG}

    def _dsem(self, buf):
        if buf.dsem is None:
            if self.free_dsems:
                key = self.free_dsems.pop()
            else:
                key = "D%d" % self.ndsem
                self.ndsem += 1
                self.sems[key] = self.stack.enter_context(self.nc.semaphore("dsem%d" % (self.ndsem - 1)))
                self.semval[key] = 0
            buf.dsem = key
        return buf.dsem

    def release(self, bufs):
        for b in bufs:
            if b.dsem is not None:
                self.free_dsems.append(b.dsem)
                b.dsem = None

    def _waits(self, eng, deps):
        need = {}
        own = "E_" + eng
        for (k, v) in deps:
            if eng == "pe" and k == "E_pe":
                continue
            if k == own and eng in ("act", "dve", "pool"):
                te = self.tend[eng].get(v)
                if te is not None and eng in self.fill:
                    gap = self.CLEAR_NS - (self.tcum[eng] - te)
                    if gap > 0:
                        n = int(math.ceil(gap / self.FILL_NS[eng]))
                        for _ in range(n):
                            self.fill[eng](self.eng[eng])
                        self.tcum[eng] += n * self.FILL_NS[eng]
                        self.nfill = getattr(self, "nfill", 0) + n
                continue
            if v > need.get(k, 0):
                need[k] = v
        out = []
        kn = self.known[eng]
        for k, v in need.items():
            if kn.get(k, 0) < v:
                kn[k] = v
                out.append((k, v))
        return out

    @staticmethod
    def _deps(reads, writes):
        deps = []
        for b in reads:
            if b.w is not None:
                deps.append(b.w)
            if b.excl:
                deps.extend(b.r.items())
        for b in writes:
            if b.w is not None:
                deps.append(b.w)
            deps.extend(b.r.items())
        return deps

    def _emit_waits(self, eng, waits):
        e = self.eng[eng]
        for (k, v) in waits:
            if k.startswith("E_"):
                self.used.add((k, v))
                if self.needed is not None:
                    v = self.remap[(k, v)]
            e.wait_ge(self.sems[k], v)
            self.nwaits += 1

    def _mark(self, ev, reads, writes):
        k, v = ev
        for b in reads:
            if b.r.get(k, 0) < v:
                b.r[k] = v
        for b in writes:
            b.w = ev
            b.r = {}

    def op(self, eng, fn, reads=(), writes=(), n=0):
        self.group(eng, [fn], reads, writes, n)

    def group(self, eng, fns, reads=(), writes=(), n=0):
        self._emit_waits(eng, self._waits(eng, self._deps(reads, writes)))
        e = self.eng[eng]
        for fn in fns[:-1]:
            fn(e)
        self.cnt[eng] += 1
        ov, pe_ = self.EST[eng]
        self.tcum[eng] += ov + pe_ * n
        td = self.tend[eng]
        td[self.cnt[eng]] = self.tcum[eng]
        if len(td) > 64:
            for k_ in sorted(td)[:32]:
                del td[k_]
        key = "E_" + eng
        self.semval[key] = self.cnt[eng]
        if self.needed is None or (key, self.cnt[eng]) in self.needed:
            self.sig[key] = self.sig.get(key, 0) + 1
            self.remap[(key, self.cnt[eng])] = self.sig[key]
            fns[-1](e).then_inc(self.sems[key], 1)
        else:
            fns[-1](e)
        self._mark((key, self.cnt[eng]), reads, writes)

    def dma(self, eng, fn, reads=(), writes=()):
        assert len(writes) == 1
        wb = writes[0]
        deps = self._deps(reads, writes)
        if eng == "pool" and getattr(self, "last_swdge", None) is not None:
            deps.append(self.last_swdge)
        self._emit_waits(eng, self._waits(eng, deps))
        key = self._dsem(wb)
        self.semval[key] += 16
        fn(self.eng[eng]).then_inc(self.sems[key], 16)
        if eng == "pool":
            self.last_swdge = (key, self.semval[key])
        self._mark((key, self.semval[key]), reads, writes)

    def barrier(self):
        deps = [(k, v) for k, v in self.semval.items() if v > 0]
        for eng in self.ENG:
            self._emit_waits(eng, self._waits(eng, deps))

    def wait_bufs(self, eng, bufs):
        deps = []
        for b in bufs:
            if b.w is not None:
                deps.append(b.w)
            deps.extend(b.r.items())
        self._emit_waits(eng, self._waits(eng, deps))


class G:
    pass


def bufs(prefix, *dims):
    if len(dims) == 1:
        return [Buf("%s%d" % (prefix, i)) for i in range(dims[0])]
    return [bufs("%s%d_" % (prefix, i), *dims[1:]) for i in range(dims[0])]


def build_program(stage=99, dump=None, needed=None):
    nc = bass.Bass("TRN2", target_bir_lowering=False)
    g = G()
    g.nc = nc
    g.stage = stage
    g.dump = dump
    import os
    g.ntl = int(os.environ.get("NTL", "9"))
    g.dbg_d = None
    if dump is not None:
        g.dbg_d = nc.dram_tensor("dbg", [128, 8 * T], BF16, kind="ExternalOutput").ap()
        g.dbgb = Buf("dbg")
    dt = lambda name, shape, kind, d=F32: nc.dram_tensor(name, shape, d, kind=kind).ap()
    g.x_d = dt("x", [T, D], "ExternalInput")
    g.norm_mix = dt("norm_mix", [DEPTH, D], "ExternalInput")
    g.w_in = dt("w_in", [DEPTH, D, DIN], "ExternalInput")
    g.qn = dt("qn_dsa", [DEPTH, 64], "ExternalInput")
    g.kn = dt("kn_dsa", [DEPTH, 64], "ExternalInput")
    g.lb_d = dt("hgrn_lb", [DEPTH, 512], "ExternalInput")
    g.onorm = dt("hgrn_onorm", [DEPTH, 64], "ExternalInput")
    g.w_sb = dt("w_br_sb", [DEPTH, 384, D], "ExternalInput")
    g.w_dsa = dt("w_br_dsa", [DEPTH, 384, D], "ExternalInput")
    g.w_hg = dt("w_br_hgrn", [DEPTH, 256, D], "ExternalInput")
    g.w_out = dt("w_out", [DEPTH, D, D], "ExternalInput")
    g.norm_mlp = dt("norm_mlp", [DEPTH, D], "ExternalInput")
    g.w_up = dt("w_up", [DEPTH, D, DFF], "ExternalInput")
    g.w_down = dt("w_down", [DEPTH, DFF, D], "ExternalInput")
    g.cs_d = dt("rope_cos", [T, 8], "ExternalInput")
    g.sn_d = dt("rope_sin", [T, 8], "ExternalInput")
    g.out_d = dt("out", [T, D], "ExternalOutput")
    g.xres_d = dt("xres", [T, D], "Internal")
    g.xin_b = bufs("xin", NB)
    g.xres_b = bufs("xres", NB)
    g.out_b = bufs("outb", NB)

    with ExitStack() as gs:
        P = Prog(nc, gs, needed)
        g.P = P
        fa = gs.enter_context(nc.sbuf_tensor("fill_a", [128, 2], F32))
        fd = gs.enter_context(nc.sbuf_tensor("fill_d", [128, 2], F32))
        nc.vector.memset(fd[:], 0.0)
        nc.vector.memset(fa[:], 0.0)
        P.fill["dve"] = lambda e: e.memset(fd[:, 0:1], 0.0)
        P.fill["act"] = lambda e: e.activation(out=fa[:, 0:1], in_=fa[:, 1:2], func=AF.Copy)
        fp = gs.enter_context(nc.sbuf_tensor("fill_p", [128, 2], F32))
        nc.gpsimd.memset(fp[:], 0.0)
        P.fill["pool"] = lambda e: e.memset(fp[:, 0:1], 0.0)
        g.uid = 0
        g.psF = [gs.enter_context(nc.psum_tensor("psF%d" % i, [128, 512], F32)) for i in range(6)]
        g.psFb = [Buf("psF%d" % i, excl=True) for i in range(6)]
        g.psB = [gs.enter_context(nc.psum_tensor("psB%d" % i, [128, 1024], BF16)) for i in range(2)]
        g.psBb = [Buf("psB%d" % i, excl=True) for i in range(2)]
        g.rotc = {}
        build_consts(g, gs)
        for l in range(DEPTH):
            if g.stage >= 1:
                build_layer(g, l)
        if g.dbg_d is not None:
            P.wait_bufs("sp", [g.dbgb])
        P.wait_bufs("sp", g.out_b)
        P.barrier()
        g.used = P.used
        print("ops", P.cnt, "signals", P.sig, "waits", P.nwaits, "fillers", getattr(P, "nfill", 0), "dsems", P.ndsem, flush=True)
    return nc, P.used


def build_two_pass(stage=99, dump=None):
    _, used = build_program(stage, dump, None)
    nc, _ = build_program(stage, dump, used)
    return nc


def pipeline(stages, ntiles):
    ns = len(stages)
    for t in range(ntiles + ns - 1):
        for k, f in enumerate(stages):
            i = t - k
            if 0 <= i < ntiles:
                f(i)


def rot(g, role, items):
    i = g.rotc.get(role, 0)
    g.rotc[role] = i + 1
    return items[i % len(items)]


def psf(g, role, banks):
    b = rot(g, role, banks)
    return g.psF[b], g.psFb[b]


def psb(g, role="pb"):
    b = rot(g, role, [0, 1])
    return g.psB[b], g.psBb[b]


@contextmanager
def scope(g):
    st = ExitStack()
    st.tbufs = []
    try:
        yield st
    finally:
        g.P.barrier()
        g.P.release(st.tbufs)
        st.close()


def sb(g, st, shape, dtype, name=None):
    g.uid += 1
    return st.enter_context(g.nc.sbuf_tensor("%s_%d" % (name or "t", g.uid), shape, dtype))


def nb(st, name):
    b = Buf(name)
    st.tbufs.append(b)
    return b


def nbs(st, prefix, *dims):
    r = bufs(prefix, *dims)

    def flat(x):
        if isinstance(x, Buf):
            st.tbufs.append(x)
        else:
            for y in x:
                flat(y)
    flat(r)
    return r


def _fs(ap):
    try:
        return int(ap.free_size())
    except Exception:
        return 0


def ACT(g, out, in_, func, reads, writes, **kw):
    g.P.op("act", lambda e: e.activation(out=out, in_=in_, func=func, **kw), reads, writes, _fs(out))


def TT(g, eng, out, in0, in1, op, reads, writes):
    g.P.op(eng, lambda e: e.tensor_tensor(out=out, in0=in0, in1=in1, op=op), reads, writes, _fs(out))


def TS(g, eng, out, in0, s1, s2, op0, op1, reads, writes, **kw):
    if op1 is None:
        s2 = 0.0 if isinstance(s1, (int, float)) else g.zc[0:in0.shape[0], 0:1]
        g.P.op(eng, lambda e: e.tensor_scalar(out=out, in0=in0, scalar1=s1, scalar2=s2, op0=op0, op1=ALU.add, **kw), reads, writes, _fs(out))
    else:
        g.P.op(eng, lambda e: e.tensor_scalar(out=out, in0=in0, scalar1=s1, scalar2=s2, op0=op0, op1=op1, **kw), reads, writes, _fs(out))


def STT(g, out, in0, scalar, in1, op0, op1, reads, writes):
    g.P.op("dve", lambda e: e.scalar_tensor_tensor(out=out, in0=in0, scalar=scalar, in1=in1, op0=op0, op1=op1), reads, writes, _fs(out))


def CP(g, eng, out, in_, reads, writes):
    if eng == "act":
        g.P.op("act", lambda e: e.activation(out=out, in_=in_, func=AF.Copy), reads, writes, _fs(out))
    else:
        g.P.op(eng, lambda e: e.tensor_copy(out, in_), reads, writes, _fs(out))


def MS(g, eng, ap, val, writes):
    g.P.op(eng, lambda e: e.memset(ap, val), (), writes, _fs(ap))


def ASEL(g, out, in_, pattern, cmp, fill, base, cm, reads, writes):
    g.P.op("pool", lambda e: e.affine_select(out=out, in_=in_, pattern=pattern, compare_op=cmp, fill=fill, base=base,
                                             channel_multiplier=cm), reads, writes, _fs(out))


def MM(g, outs_fns, reads, writes):
    g.P.group("pe", outs_fns, reads, writes)


def mmf(out, lhsT, rhs, start, stop):
    return lambda e: e.matmul(out, lhsT=lhsT, rhs=rhs, start=start, stop=stop)


def trf(out, in_, ident):
    return lambda e: e.transpose(out, in_, ident)


def DMA(g, eng, out, in_, reads, writes, **kw):
    g.P.dma(eng, lambda e: e.dma_start(out=out, in_=in_, **kw), reads, writes)


def build_consts(g, gs):
    nc = g.nc
    mk = lambda name, shape, d: gs.enter_context(nc.sbuf_tensor(name, shape, d))
    g.ident = mk("ident", [128, 128], BF16)
    g.negtri = mk("negtri", [128, 128], BF16)
    g.negones = mk("negones", [128, 128], BF16)
    g.onesb = mk("onesb", [128, 128], BF16)
    g.maskbd = mk("maskbd", [128, 128], F32)
    g.onesf = mk("onesf", [128, 128], F32)
    g.resetm = mk("resetm", [128, T], BF16)
    g.zc = mk("zc", [128, 1], F32)
    g.negbig = mk("negbig", [128, 1], F32)
    g.cs = mk("cs", [128, NB, 8], F32)
    g.sn = mk("sn", [128, NB, 8], F32)
    g.lbraw = mk("lbraw", [128, 2, 4], F32)
    g.lbv = mk("lbv", [128, 2, 4], F32)
    g.oml = mk("oml", [128, 2, 4], F32)
    g.cb = Buf("consts")
    g.csb = Buf("cs")
    g.snb = Buf("sn")
    g.lbb = Buf("lbraw")
    cb = [g.cb]
    MS(g, "pool", g.onesb[:], 1.0, cb)
    MS(g, "pool", g.negones[:], -1.0, cb)
    MS(g, "pool", g.onesf[:], 1.0, cb)
    MS(g, "pool", g.zc[:], 0.0, cb)
    MS(g, "pool", g.negbig[:], -1e29, cb)
    ASEL(g, g.ident[:], g.onesb[:], [[1, 128]], ALU.is_equal, 0.0, 0, -1, cb, cb)
    ASEL(g, g.negtri[:], g.negones[:], [[-1, 128]], ALU.is_ge, 0.0, 0, 1, cb, cb)
    ASEL(g, g.maskbd[:], g.onesf[:], [[1, 128]], ALU.is_ge, 0.0, 0, -1, cb, cb)
    MS(g, "pool", g.maskbd[0:64, 64:128], 0.0, cb)
    g.ones512 = mk("ones512", [128, 512], BF16)
    g.mlt = mk("mlt", [128, 512], BF16)
    MS(g, "pool", g.ones512[:], 1.0, cb)
    ASEL(g, g.mlt[:], g.ones512[:], [[1, 512]], ALU.is_gt, 0.0, 0, -1, cb, cb)
    g.caus01 = mk("caus01", [128, 128], F32)
    g.negfill = mk("negfill", [128, 128], F32)
    ASEL(g, g.caus01[:], g.onesf[:], [[-1, 128]], ALU.is_ge, 0.0, 0, 1, cb, cb)
    TS(g, "pool", g.negfill[:], g.caus01[:], -1.0, 1e30, ALU.add, ALU.mult, cb, cb)
    MS(g, "pool", g.resetm[:], 1.0, cb)
    MS(g, "pool", g.resetm[:].rearrange("p (c j) -> p c j", j=64)[:, :, 0:1], 0.0, cb)
    DMA(g, "sp", g.cs[:], g.cs_d.rearrange("(b p) i -> p b i", p=128), (), [g.csb])
    DMA(g, "sp", g.sn[:], g.sn_d.rearrange("(b p) i -> p b i", p=128), (), [g.snb])
    DMA(g, "sp", g.lbraw[:], g.lb_d.rearrange("l (h k) -> k l h", k=128), (), [g.lbb], allow_slow_non_contiguous=True)
    MS(g, "dve", g.lbv[:], 0.0, cb)
    TT(g, "dve", g.lbv[:, 1, :], g.lbraw[:, 1, :], g.lbraw[:, 0, :], ALU.subtract, [g.lbb], cb)
    ACT(g, g.lbv[:, 1, :], g.lbv[:, 1, :], AF.Sigmoid, cb, cb)
    TS(g, "dve", g.oml[:], g.lbv[:], -1.0, 1.0, ALU.mult, ALU.add, cb, cb)
    g.P.barrier()


def norm_T(g, st, src_ap, src_bufs, gain_d_row, hT, hTb):
    P = g.P
    gain = sb(g, st, [128, D], F32, "gain")
    gb = nb(st, "gain")
    DMA(g, "sp", gain[:], gain_d_row.to_broadcast([128, D]), (), [gb])
    xbs = [sb(g, st, [128, D], F32, "xb") for _ in range(4)]
    xbb = nbs(st, "xb", 4)
    junk = sb(g, st, [128, D], BF16, "junk")
    jb = nb(st, "junk")
    hbs = [sb(g, st, [128, D], BF16, "hb") for _ in range(2)]
    hbb = nbs(st, "hb", 2)
    ss = sb(g, st, [128, NB], F32, "ss")
    ssb = nbs(st, "ss", NB)
    MS(g, "dve", ss[:], 0.0, ssb)
    for tb in range(NB):
        xb, xbuf = xbs[tb % 4], xbb[tb % 4]
        hb, hbuf = hbs[tb % 2], hbb[tb % 2]
        DMA(g, "sp", xb[:], src_ap[tb * 128:(tb + 1) * 128, :], [src_bufs[tb]], [xbuf])
        s1 = ss[:, tb:tb + 1]
        ACT(g, junk[:], xb[:], AF.Square, [xbuf, ssb[tb]], [jb, ssb[tb]], accum_out=s1)
        ACT(g, s1, s1, AF.Ln, [ssb[tb]], [ssb[tb]], scale=1.0 / D, bias=EPS)
        ACT(g, s1, s1, AF.Exp, [ssb[tb]], [ssb[tb]], scale=-0.5)
        if g.ntl < 2:
            continue
        STT(g, hb[:], xb[:], s1, gain[:], ALU.mult, ALU.mult, [xbuf, ssb[tb], gb], [hbuf])
        if g.ntl < 3:
            continue
        for half in range(2):
            pt, ptb = psb(g)
            MM(g, [trf(pt[:, m * 128:(m + 1) * 128], hb[:, (half * 4 + m) * 128:(half * 4 + m + 1) * 128], g.ident[:])
                   for m in range(4)], [hbuf, g.cb], [ptb])
            CP(g, "act" if half == 0 else "dve", hT[:, half * 4:half * 4 + 4, tb * 128:(tb + 1) * 128],
               pt[:, 0:512].rearrange("p (m j) -> p m j", j=128), [ptb], [hTb[tb]])


def win_cols(g, l, c0, c1):
    return g.w_in[l].rearrange("(c p) n -> p c n", p=128)[:, :, c0:c1]


def mm_fm(g, ps, M, n, w, wb, col0, hT, hTb, tok0, role_bufs):
    MM(g, [mmf(ps[0:M, 0:n], w[:, c, col0:col0 + M], hT[:, c, tok0:tok0 + n], c == 0, c == DC - 1) for c in range(DC)],
       [wb] + hTb[tok0 // 128:(tok0 + n + 127) // 128], [role_bufs])


def mm_tm(g, ps, N, w, wb, col0, hT, hTb, tb, psbuf):
    MM(g, [mmf(ps[:, 0:N], hT[:, c, tb * 128:(tb + 1) * 128], w[:, c, col0:col0 + N], c == 0, c == DC - 1) for c in range(DC)],
       [wb, hTb[tb]], [psbuf])


def phase_sb(g, l, hT, hTb, osbT, osbb):
    with scope(g) as st:
        ws = []
        for i, c0 in enumerate((C_SQ, C_SK, C_SV)):
            w = sb(g, st, [128, DC, 384], BF16, "wsb")
            wb = nb(st, "wsb%d" % i)
            DMA(g, "pool", w[:], win_cols(g, l, c0, c0 + 384), (), [wb])
            ws.append((w, wb))
        sqT = sb(g, st, [128, 3, T], BF16, "sqT")
        skT = sb(g, st, [128, 3, T], BF16, "skT")
        sqb = nbs(st, "sq", 3, 4)
        skb = nbs(st, "sk", 3, 4)
        k = 0
        for (dst, dstb, (w, wb), scl) in ((sqT, sqb, ws[0], 0.125), (skT, skb, ws[1], 1.0)):
            for hp in range(3):
                for tc in range(4):
                    ps, pb = psf(g, "proj", [0, 1, 2, 3, 4, 5])
                    mm_fm(g, ps, 128, 512, w, wb, hp * 128, hT, hTb, tc * 512, pb)
                    o = dst[:, hp, tc * 512:(tc + 1) * 512]
                    if k % 2 == 0:
                        ACT(g, o, ps[:, :], AF.Copy, [pb], [dstb[hp][tc]], scale=scl)
                    else:
                        TS(g, "dve", o, ps[:, :], scl, None, ALU.mult, None, [pb], [dstb[hp][tc]])
                    k += 1
        svp = [sb(g, st, [128, NB, 384], BF16, "svp") for _ in range(2)]
        svb = nbs(st, "sv", 2, NB)
        for s_ in range(2):
            MS(g, "pool", svp[s_][:].rearrange("p t c -> p (t c)"), 0.0, svb[s_])
        for tb in range(NB):
            ps, pb = psf(g, "proj", [0, 1, 2, 3, 4, 5])
            mm_tm(g, ps, 384, ws[2][0], ws[2][1], 0, hT, hTb, tb, pb)
            src = ps[:, 0:384].rearrange("p (m s d) -> p m s d", s=2, d=64)
            for s_ in range(2):
                dst = svp[s_][:, tb, :].rearrange("p (m s d) -> p m s d", s=2, d=64)
                CP(g, "act" if s_ == 0 else "dve", dst[:, :, s_, :], src[:, :, s_, :], [pb], [svb[s_][tb]])
        R = 4
        mk2 = lambda shape, dt_, nm: [[sb(g, st, shape, dt_, nm) for _ in range(R)] for _ in range(2)]
        Et, SPt, SPs, At = mk2([128, 512], F32, "Et"), mk2([128, 512], BF16, "SPt"), mk2([128, 512], BF16, "SPs"), mk2([128, 512], BF16, "At")
        Etb, SPb, SPsb, Atb = nbs(st, "Et", 2, R), nbs(st, "SPt", 2, R), nbs(st, "SPs", 2, R), nbs(st, "At", 2, R)
        steps = []
        for m in range(3):
            for qc in range(4):
                for n_, kb in enumerate(range(4 * qc + 3, -1, -1)):
                    steps.append((m, qc, kb, n_))
        stt = {}
        pso = {}

        def info(t):
            m, qc, kb, n_ = steps[t]
            j0 = max(0, kb * 128 - qc * 512)
            return m, qc, kb, n_, j0, kb >= 4 * qc, qc * 512 + j0 - kb * 128, n_ == 0

        def opnds(t):
            m, qc, kb, n_, j0, diag, base, first = info(t)
            kk = [skT[64 * s_:64 * s_ + 64, m, kb * 128:(kb + 1) * 128] for s_ in range(2)]
            qq = [sqT[64 * s_:64 * s_ + 64, m, qc * 512 + j0:(qc + 1) * 512] for s_ in range(2)]
            return kk, qq, skb[m][kb // 4], sqb[m][qc]

        def s1(t):
            m, qc, kb, n_, j0, diag, base, first = info(t)
            kk, qq, rk, rq = opnds(t)
            pz = [psf(g, "sbZ", [0, 1, 2]) for _ in range(2)]
            stt[t] = {"pz": pz}
            for s_ in range(2):
                MM(g, [mmf(pz[s_][0][:, j0:512], kk[s_], qq[s_], True, True)], [rk, rq], [pz[s_][1]])

        def s2(t):
            m, qc, kb, n_, j0, diag, base, first = info(t)
            ib = t % R
            pz = stt[t]["pz"]
            for s_ in range(2):
                ACT(g, Et[s_][ib][:, j0:512], pz[s_][0][:, j0:512], AF.Exp, [pz[s_][1]], [Etb[s_][ib]])

        def s3(t):
            m, qc, kb, n_, j0, diag, base, first = info(t)
            ib = t % R
            for s_ in range(2):
                ACT(g, SPt[s_][ib][:, j0:512], Et[s_][ib][:, j0:512], AF.Ln, [Etb[s_][ib]], [SPb[s_][ib]], bias=1.0)
            if diag:
                for s_ in range(2):
                    S = SPt[s_][ib]
                    assert base == 0
                    TT(g, "pool", S[:, j0:512], S[:, j0:512], g.mlt[:, 0:512 - j0], ALU.mult, [SPb[s_][ib], g.cb], [SPb[s_][ib]])
            if kb > 0:
                for s_ in range(2):
                    S, Sb = SPt[s_][ib], SPb[s_][ib]
                    Sn, Snb = SPs[s_][n_ % R], SPsb[s_][n_ % R]
                    if first:
                        if j0 > 0:
                            MS(g, "pool", Sn[:, 0:j0], 0.0, [Snb])
                        CP(g, "pool", Sn[:, j0:512], S[:, j0:512], [Sb], [Snb])
                    else:
                        So, Sob = SPs[s_][(n_ - 1) % R], SPsb[s_][(n_ - 1) % R]
                        if j0 > 0:
                            CP(g, "pool", Sn[:, 0:j0], So[:, 0:j0], [Sob], [Snb])
                        TT(g, "dve", Sn[:, j0:512], So[:, j0:512], S[:, j0:512], ALU.add, [Sob, Sb], [Snb])

        def s4(t):
            m, qc, kb, n_, j0, diag, base, first = info(t)
            ib = t % R
            kk, qq, rk, rq = opnds(t)
            pc = [psf(g, "sbC", [3, 4]) for _ in range(2)]
            stt[t]["pc"] = pc
            for s_ in range(2):
                S = SPt[s_][ib]
                fns = [mmf(pc[s_][0][:, j0:512], kk[s_], qq[s_], True, False),
                       mmf(pc[s_][0][:, j0:512], g.negtri[:], S[:, j0:512], False, first)]
                rd = [rk, rq, SPb[s_][ib], g.cb]
                if not first:
                    So, Sob = SPs[s_][(n_ - 1) % R], SPsb[s_][(n_ - 1) % R]
                    fns.append(mmf(pc[s_][0][:, j0:512], g.negones[:], So[:, j0:512], False, True))
                    rd.append(Sob)
                MM(g, fns, rd, [pc[s_][1]])

        def s5(t):
            m, qc, kb, n_, j0, diag, base, first = info(t)
            ib = t % R
            pc = stt[t]["pc"]
            for s_ in range(2):
                ACT(g, At[s_][ib][:, j0:512], pc[s_][0][:, j0:512], AF.Exp, [pc[s_][1]], [Atb[s_][ib]])
            for s_ in range(2):
                A, Ab = At[s_][ib], Atb[s_][ib]
                if diag:
                    TT(g, "pool", A[:, j0:512], A[:, j0:512], g.mlt[:, 0:512 - j0], ALU.mult, [Ab, g.cb], [Ab])
                if first and j0 > 0:
                    MS(g, "pool", A[:, 0:j0], 0.0, [Ab])

        def s6(t):
            m, qc, kb, n_, j0, diag, base, first = info(t)
            ib = t % R
            if first:
                pso[(m, qc)] = psf(g, "sbO", [5])
            psO, pOb = pso[(m, qc)]
            for s_ in range(2):
                A, Ab = At[s_][ib], Atb[s_][ib]
                vv = svp[s_][:, kb, m * 128:(m + 1) * 128]
                c0 = 0 if first else j0
                MM(g, [mmf(psO[:, c0:512], vv, A[:, c0:512], first and s_ == 0, kb == 0 and s_ == 1)], [svb[s_][kb], Ab], [pOb])
            if kb == 0:
                CP(g, "act" if (m * 4 + qc) % 2 == 0 else "dve", osbT[:, m, qc * 512:(qc + 1) * 512], psO[:, :], [pOb], [osbb[m][qc]])
            del stt[t]

        nst = len(steps)
        stages = [s1, s2, s3, s4, s5, s6]
        for e in range(nst + len(stages) - 1):
            for k in range(len(stages) - 1, -1, -1):
                t = e - k
                if 0 <= t < nst:
                    stages[k](t)


def phase_dsa(g, l, hT, hTb, odT, odb):
    P = g.P
    with scope(g) as st:
        featT = sb(g, st, [128, 9, T], BF16, "featT")
        fb = nbs(st, "feat", NB)
        dvx = sb(g, st, [128, NB, 128], BF16, "dvx")
        dvb = nbs(st, "dvx", NB)
        sgn = sb(g, st, [128, NB, 8], F32, "sgn")
        sgb = nbs(st, "sgn", NB)
        qkg = sb(g, st, [128, 7, 64], F32, "qkg")
        qkgb = nbs(st, "qkg", 7)
        for hh in range(7):
            src = (g.qn if hh < 6 else g.kn)[l:l + 1, :].to_broadcast([128, 64])
            DMA(g, "sp", qkg[:, hh, :], src, (), [qkgb[hh]])
        with scope(g) as s2:
            wA = sb(g, s2, [128, DC, 512], BF16, "wA")
            wB = sb(g, s2, [128, DC, 512], BF16, "wB")
            wC = sb(g, s2, [128, DC, 72], BF16, "wC")
            wAb, wBb, wCb = nb(s2, "wA"), nb(s2, "wB"), nb(s2, "wC")
            DMA(g, "pool", wA[:], win_cols(g, l, C_DQ, C_DQ + 512), (), [wAb])
            DMA(g, "pool", wB[:], win_cols(g, l, C_IQ, C_IQ + 512), (), [wBb])
            DMA(g, "pool", wC[:], win_cols(g, l, C_IK, C_IK + 72), (), [wCb])
            tq = [sb(g, s2, [128, 18, 64], F32, "tokq") for _ in range(2)]
            tqb = nbs(s2, "tokq", 2)
            tbq = [sb(g, s2, [128, 18, 64], BF16, "tokb") for _ in range(2)]
            tbb = nbs(s2, "tokb", 2)
            sqt = sb(g, s2, [128, 448], F32, "sqt")
            sqtb = nb(s2, "sqt")
            sm = sb(g, s2, [128, 32], F32, "small")
            smb = nb(s2, "small")
            rt = [sb(g, s2, [128, 18, 8], F32, "ropet") for _ in range(4)]
            rtb = nbs(s2, "ropet", 4)
            for tb in range(NB):
                MS(g, "pool", dvx[:, tb, 64:128], 1.0, [dvb[tb]])
                tk, tkb = tq[tb % 2], tqb[tb % 2]
                tkh, tkhb = tbq[tb % 2], tbb[tb % 2]
                MS(g, "pool", tk[:, 7, :], 0.0, [tkb])
                MS(g, "pool", tk[:, 17, :], 0.0, [tkb])
                psA, pAb = psf(g, "dA", [0, 1])
                psBq, pBb = psf(g, "dB", [2, 3])
                psC, pCb = psf(g, "dC", [4, 5])
                mm_tm(g, psA, 512, wA, wAb, 0, hT, hTb, tb, pAb)
                mm_tm(g, psBq, 512, wB, wBb, 0, hT, hTb, tb, pBb)
                mm_tm(g, psC, 72, wC, wCb, 0, hT, hTb, tb, pCb)
                ACT(g, sqt[:], psA[:, 0:448], AF.Square, [pAb], [sqtb])
                ss = sm[:, 0:7]
                P.op("dve", lambda e, ss=ss: e.tensor_reduce(out=ss, in_=sqt[:].rearrange("p (h d) -> p h d", d=64), axis=AX.X,
                                                             op=ALU.add), [sqtb], [smb])
                ACT(g, ss, ss, AF.Ln, [smb], [smb], scale=1.0 / 64, bias=EPS)
                ACT(g, ss, ss, AF.Exp, [smb], [smb], scale=-0.5)
                TT(g, "dve", tk[:, 0:7, :], psA[:, 0:448].rearrange("p (h d) -> p h d", d=64),
                   ss.unsqueeze(2).to_broadcast([128, 7, 64]), ALU.mult, [pAb, smb], [tkb])
                TT(g, "dve", tk[:, 0:7, :], tk[:, 0:7, :], qkg[:], ALU.mult, [tkb] + qkgb, [tkb])
                aw = sm[:, 8:16]
                TS(g, "dve", sgn[:, tb, :], psC[:, 64:72], 0.0, 2.0, ALU.is_gt, ALU.mult, [pCb], [sgb[tb]])
                TS(g, "dve", sgn[:, tb, :], sgn[:, tb, :], -1.0, 0.0, ALU.add, ALU.add, [sgb[tb]], [sgb[tb]])
                STT(g, aw, psC[:, 64:72], IDX_SCALE, sgn[:, tb, :], ALU.mult, ALU.mult, [pCb, sgb[tb]], [smb])
                TT(g, "dve", tk[:, 8:16, :], psBq[:, :].rearrange("p (h d) -> p h d", d=64),
                   aw.unsqueeze(2).to_broadcast([128, 8, 64]), ALU.mult, [pBb, smb], [tkb])
                CP(g, "act", tk[:, 16, :], psC[:, 0:64], [pCb], [tkb])
                CP(g, "act", dvx[:, tb, 0:64], psA[:, 448:512], [pAb], [dvb[tb]])
                x1, x2 = tk[:, :, 0:8], tk[:, :, 8:16]
                cb_ = g.cs[:, tb, :].unsqueeze(1).to_broadcast([128, 18, 8])
                sb_ = g.sn[:, tb, :].unsqueeze(1).to_broadcast([128, 18, 8])
                TT(g, "dve", rt[0][:], x1, cb_, ALU.mult, [tkb, g.csb], [rtb[0]])
                TT(g, "pool", rt[1][:], x2, sb_, ALU.mult, [tkb, g.snb], [rtb[1]])
                TT(g, "dve", rt[2][:], x2, cb_, ALU.mult, [tkb, g.csb], [rtb[2]])
                TT(g, "pool", rt[3][:], x1, sb_, ALU.mult, [tkb, g.snb], [rtb[3]])
                TT(g, "dve", x1, rt[0][:], rt[1][:], ALU.subtract, [rtb[0], rtb[1]], [tkb])
                TT(g, "pool", x2, rt[2][:], rt[3][:], ALU.add, [rtb[2], rtb[3]], [tkb])
                CP(g, "act", tkh[:], tk[:], [tkb], [tkhb])
                CP(g, "pool", tkh[:, 7, :], tkh[:, 6, :], [tkhb], [tkhb])
                CP(g, "pool", tkh[:, 17, :], tkh[:, 16, :], [tkhb], [tkhb])
                flat = tkh[:].rearrange("p h d -> p (h d)")
                pt, ptb = psb(g)
                MM(g, [trf(pt[:, m * 128:(m + 1) * 128], flat[:, m * 128:(m + 1) * 128], g.ident[:]) for m in range(8)],
                   [tkhb, g.cb], [ptb])
                CP(g, "dve", featT[:, 0:8, tb * 128:(tb + 1) * 128], pt[:, :].rearrange("p (m j) -> p m j", j=128), [ptb], [fb[tb]])
                pt2, ptb2 = psb(g)
                MM(g, [trf(pt2[:, 0:128], flat[:, 1024:1152], g.ident[:])], [tkhb, g.cb], [ptb2])
                CP(g, "act", featT[:, 8, tb * 128:(tb + 1) * 128], pt2[:, 0:128], [ptb2], [fb[tb]])
        sc = [sb(g, st, [128, T], F32, "sc") for _ in range(4)]
        scb = nbs(st, "sc", 4, 4)
        junk = sb(g, st, [128, T], BF16, "junk")
        maskq = [sb(g, st, [128, T], BF16, "maskq") for _ in range(2)]
        mqb = nbs(st, "maskq", 2)
        maskT = sb(g, st, [128, NB, 512], BF16, "maskT")
        mTb = nbs(st, "maskT", 4)
        rj = [sb(g, st, [128, 512], BF16, "rj") for _ in range(4)]
        rjb = nbs(st, "rj", 4)
        dg = [sb(g, st, [128, 8, 128], BF16, "dg") for _ in range(2)]
        dgb = nbs(st, "dg", 2)
        sm = [sb(g, st, [128, 8 + 2 * N_BISECT], F32, "bis") for _ in range(2)]
        smb = nbs(st, "bis", 2)
        cvec = sb(g, st, [128, N_BISECT], F32, "cvec")
        c255 = sb(g, st, [128, 1], F32, "c255")
        cvb = nb(st, "cvec")
        for n_ in range(N_BISECT):
            MS(g, "pool", cvec[:, n_:n_ + 1], 2.0 ** -(n_ + 1), [cvb])
        MS(g, "pool", c255[:], 255.5, [cvb])
        Pt = [sb(g, st, [128, 512], BF16, "Pt") for _ in range(4)]
        Ptb = nbs(st, "Pt", 4)
        Pm = [sb(g, st, [128, 512], BF16, "Pm") for _ in range(4)]
        Pmb = nbs(st, "Pm", 4)
        rs = sb(g, st, [64, 512], F32, "rs")
        rsb = nb(st, "rs")
        cnt_ = {"ri": 0, "pi": 0, "pm": 0}

        def idx_blocks(blocks):
            tiles = []
            for i in blocks:
                nk = (i + 1) * 128
                d_, d_b = dg[i % 2], dgb[i % 2]
                for j in range(8):
                    TS(g, "pool", d_[:, j, :], g.ident[:], sgn[:, i, j:j + 1], None, ALU.mult, None, [g.cb, sgb[i]], [d_b])
                for kc in range((nk + 511) // 512):
                    n = min(512, nk - kc * 512)
                    for j in range(8):
                        tiles.append((i, kc, n, j))
            stt = {}

            def s1(t):
                i, kc, n, j = tiles[t]
                po = 64 * (j % 2)
                psZ, pZb = psf(g, "ixZ", [0, 1, 2])
                stt[t] = [psZ, pZb]
                MM(g, [mmf(psZ[:, 0:n], featT[po:po + 64, 4 + j // 2, i * 128:(i + 1) * 128],
                           featT[po:po + 64, 8, kc * 512:kc * 512 + n], True, True)],
                   [fb[i]] + fb[kc * 4:(kc * 512 + n) // 128], [pZb])

            def s2(t):
                i, kc, n, j = tiles[t]
                psZ, pZb = stt[t]
                r_, r_b = rj[cnt_["ri"] % 4], rjb[cnt_["ri"] % 4]
                cnt_["ri"] += 1
                stt[t] += [r_, r_b]
                ACT(g, r_[:, 0:n], psZ[:, 0:n], AF.Relu, [pZb], [r_b])

            def s3(t):
                i, kc, n, j = tiles[t]
                r_, r_b = stt[t][2], stt[t][3]
                if j == 0:
                    cnt_["psS"] = psf(g, "ixS", [3, 4])
                psS, pSb = cnt_["psS"]
                d_, d_b = dg[i % 2], dgb[i % 2]
                MM(g, [mmf(psS[:, 0:n], d_[:, j, :], r_[:, 0:n], j == 0, j == 7)], [d_b, r_b], [pSb])
                if j == 7:
                    CP(g, "act", sc[i % 4][:, kc * 512:kc * 512 + n], psS[:, 0:n], [pSb], [scb[i % 4][kc]])
                del stt[t]

            pipeline([s1, s2, s3], len(tiles))

        def bis_pair(p):
            blocks = [2 * p, 2 * p + 1]
            st_ = []
            for bi, i in enumerate(blocks):
                nk = (i + 1) * 128
                s_, s_b = sc[i % 4], scb[i % 4]
                nkc = (nk + 511) // 512
                srd = s_b[0:nkc]
                m_, m_b = sm[bi], smb[bi]
                rmax, rmin, step0, mid, cntv, tt = (m_[:, c:c + 1] for c in range(6))
                stepc = m_[:, 8:8 + N_BISECT]
                if nk > 256:
                    P.op("dve", lambda e, s_=s_, nk=nk, rmax=rmax: e.tensor_reduce(out=rmax, in_=s_[:, 0:nk], axis=AX.X, op=ALU.max), srd, [m_b], nk)
                    P.op("dve", lambda e, s_=s_, nk=nk, rmin=rmin: e.tensor_reduce(out=rmin, in_=s_[:, 0:nk], axis=AX.X, op=ALU.min), srd, [m_b], nk)
                dsl = s_[:, i * 128:(i + 1) * 128]
                TT(g, "pool", dsl, dsl, g.caus01[:], ALU.mult, [s_b[i // 4], g.cb], [s_b[i // 4]])
                TT(g, "pool", dsl, dsl, g.negfill[:], ALU.add, [s_b[i // 4], g.cb], [s_b[i // 4]])
                st_.append((i, nk, s_, srd, m_, m_b, rmax, rmin, step0, mid, cntv, tt, stepc))
            act = [x for x in st_ if x[1] > 256]
            for (i, nk, s_, srd, m_, m_b, rmax, rmin, step0, mid, cntv, tt, stepc) in act:
                TT(g, "dve", step0, rmax, rmin, ALU.subtract, [m_b], [m_b])
            for (i, nk, s_, srd, m_, m_b, rmax, rmin, step0, mid, cntv, tt, stepc) in act:
                TS(g, "dve", stepc, cvec[:], step0, None, ALU.mult, None, [m_b, cvb], [m_b])
            for (i, nk, s_, srd, m_, m_b, rmax, rmin, step0, mid, cntv, tt, stepc) in act:
                TS(g, "dve", mid, stepc[:, 0:1], rmin, g.zc[:, 0:1], ALU.add, ALU.add, [m_b, g.cb], [m_b])
            for n_ in range(N_BISECT):
                for (i, nk, s_, srd, m_, m_b, rmax, rmin, step0, mid, cntv, tt, stepc) in act:
                    TS(g, "dve", junk[:, 0:nk], s_[:, 0:nk], mid, g.zc[:, 0:1], ALU.is_ge, ALU.add, srd + [m_b, g.cb], [m_b], accum_out=cntv)
                for (i, nk, s_, srd, m_, m_b, rmax, rmin, step0, mid, cntv, tt, stepc) in act:
                    TS(g, "dve", tt, cntv, c255[:, 0:1], stepc[:, n_:n_ + 1], ALU.is_ge, ALU.mult, [m_b, cvb], [m_b])
                for (i, nk, s_, srd, m_, m_b, rmax, rmin, step0, mid, cntv, tt, stepc) in act:
                    nn = min(n_ + 1, N_BISECT - 1)
                    TS(g, "dve", mid, tt, stepc[:, nn:nn + 1], mid, ALU.subtract, ALU.add, [m_b], [m_b])
            for (i, nk, s_, srd, m_, m_b, rmax, rmin, step0, mid, cntv, tt, stepc) in st_:
                thr = mid if nk > 256 else g.negbig[:, 0:1]
                mq, mq_b = maskq[i % 2], mqb[i % 2]
                TS(g, "dve", mq[:, 0:nk], s_[:, 0:nk], thr, None, ALU.is_ge, None, srd + [m_b, g.cb], [mq_b])

        def mT_pair(p):
            for i in (2 * p, 2 * p + 1):
                ii = i % 4
                mq, mq_b = maskq[i % 2], mqb[i % 2]
                for k0 in range(0, i + 1, 8):
                    k1 = min(i + 1, k0 + 8)
                    pt, ptb = psb(g)
                    MM(g, [trf(pt[:, (kb - k0) * 128:(kb - k0 + 1) * 128], mq[:, kb * 128:(kb + 1) * 128], g.ident[:])
                           for kb in range(k0, k1)], [mq_b, g.cb], [ptb])
                    CP(g, "act", maskT[:, k0:k1, ii * 128:(ii + 1) * 128],
                       pt[:, 0:(k1 - k0) * 128].rearrange("p (m j) -> p m j", j=128), [ptb], [mTb[ii]])

        def att_chunk(qc):
            last = 4 * qc + 3
            tiles = [(h, kb) for h in range(6) for kb in range(last + 1)]
            stt = {}
            pso = {}

            def s1(t):
                h, kb = tiles[t]
                hp, po = h // 2, 64 * (h % 2)
                j0 = max(0, kb * 128 - qc * 512)
                psL, pLb = psf(g, "dsL", [0, 1, 2])
                stt[t] = [psL, pLb]
                MM(g, [mmf(psL[:, j0:512], featT[po:po + 64, 3, kb * 128:(kb + 1) * 128],
                           featT[po:po + 64, hp, qc * 512 + j0:(qc + 1) * 512], True, True)],
                   [fb[kb]] + fb[qc * 4:qc * 4 + 4], [pLb])

            def s2(t):
                h, kb = tiles[t]
                j0 = max(0, kb * 128 - qc * 512)
                psL, pLb = stt[t][0], stt[t][1]
                pi = cnt_["pi"]
                cnt_["pi"] += 1
                p_, p_b = Pt[pi % 4], Ptb[pi % 4]
                stt[t] += [p_, p_b]
                ACT(g, p_[:, j0:512], psL[:, j0:512], AF.Exp, [pLb], [p_b], scale=0.125)

            def s3(t):
                h, kb = tiles[t]
                j0 = max(0, kb * 128 - qc * 512)
                p_, p_b = stt[t][2], stt[t][3]
                pm = cnt_["pm"]
                cnt_["pm"] += 1
                m_, m_b = Pm[pm % 4], Pmb[pm % 4]
                stt[t] += [m_, m_b]
                TT(g, "pool", m_[:, j0:512], p_[:, j0:512], maskT[:, kb, j0:512], ALU.mult, [p_b] + mTb[j0 // 128:4], [m_b])

            def s4(t):
                h, kb = tiles[t]
                hp, po = h // 2, 64 * (h % 2)
                j0 = max(0, kb * 128 - qc * 512)
                m_, m_b = stt[t][4], stt[t][5]
                if kb == 0:
                    pso[h] = psf(g, "dsO", [4, 5])
                psO, pOb = pso[h]
                MM(g, [mmf(psO[:, j0:512], dvx[:, kb, :], m_[:, j0:512], kb == 0, kb == last)], [dvb[kb], m_b], [pOb])
                if kb == last:
                    P.op("dve", lambda e, psO=psO: e.reciprocal(out=rs[0:64, :], in_=psO[64:128, :]), [pOb], [rsb], 512)
                    TT(g, "dve", odT[po:po + 64, hp, qc * 512:(qc + 1) * 512], psO[0:64, :], rs[0:64, :], ALU.mult, [pOb, rsb],
                       [odb[hp][qc]])
                del stt[t]

            pipeline([s1, s2, s3, s4], len(tiles))

        idx_blocks([0, 1])
        for p in range(8):
            if p + 1 < 8:
                idx_blocks([2 * p + 2, 2 * p + 3])
            bis_pair(p)
            mT_pair(p)
            if p % 2 == 1:
                att_chunk(p // 2)


def phase_hgrn(g, l, hT, hTb, ohT, ohb):
    P = g.P
    import os
    if int(os.environ.get("HGL", "9")) == 0:
        return
    with scope(g) as st:
        hi_tm = sb(g, st, [128, NB, 256], BF16, "hi_tm")
        hib = nbs(st, "hi", NB)
        hgs = sb(g, st, [128, NB, 256], BF16, "hgs")
        hgb = nbs(st, "hgs", NB)
        onb = sb(g, st, [128, 64], F32, "onorm")
        onbb = nb(st, "onorm")
        DMA(g, "sp", onb[:], g.onorm[l:l + 1, :].to_broadcast([128, 64]), (), [onbb])
        with scope(g) as s2:
            w = sb(g, s2, [128, DC, 512], BF16, "whihg")
            wb = nb(s2, "whihg")
            DMA(g, "pool", w[:], win_cols(g, l, C_HI, C_HI + 512), (), [wb])
            sgs = [sb(g, s2, [128, 256], F32, "sgs") for _ in range(2)]
            sgsb = nbs(s2, "sgs", 2)
            for tb in range(NB):
                ps, pb = psf(g, "proj", [0, 1, 2, 3, 4, 5])
                hgv = int(os.environ.get("HGV", "15"))
                if hgv & 8:
                    mm_tm(g, ps, 512, w, wb, 0, hT, hTb, tb, pb)
                if hgv & 1:
                    CP(g, "dve", hi_tm[:, tb, :], ps[:, 0:256], [pb], [hib[tb]])
                sgt, sgtb = sgs[tb % 2], sgsb[tb % 2]
                if hgv & 2:
                    ACT(g, sgt[:], ps[:, 256:512], AF.Exp if hgv & 16 else AF.Sigmoid, [pb], [sgtb])
                if hgv & 4:
                    TT(g, "dve", hgs[:, tb, :], ps[:, 256:512], sgt[:], ALU.mult, [pb, sgtb], [hgb[tb]])
        with scope(g) as s3:
            NH = 4
            R = 4
            qtT = sb(g, s3, [128, NH, T], BF16, "qtT")
            ktT = sb(g, s3, [128, NH, T], BF16, "ktT")
            qtb = nbs(s3, "qt", NH)
            ktb = nbs(s3, "kt", NH)
            kt_tm = sb(g, s3, [128, NB, NH * 128], BF16, "kt_tm")
            kttb = nbs(s3, "kttm", NH)
            t1 = sb(g, s3, [128, T], F32, "t1")
            t2 = sb(g, s3, [128, T], F32, "t2")
            t3 = sb(g, s3, [128, T], F32, "t3")
            t1b, t2b, t3b = nb(s3, "t1"), nb(s3, "t2"), nb(s3, "t3")
            ebl = sb(g, s3, [128, NH, 32], F32, "ebl")
            eblb = nb(s3, "ebl")
            W = [sb(g, s3, [128, NH, 64], F32, "W") for _ in range(2)]
            Wb = nbs(s3, "W", 2)
            Sbf = [sb(g, s3, [128, NH, 64], BF16, "Sbf") for _ in range(R)]
            Sbfb = nbs(s3, "Sbf", R)
            attm = [sb(g, s3, [128, NH, 128], BF16, "attm") for _ in range(3)]
            attb = nbs(s3, "attm", 3)
            o_tm = [sb(g, s3, [128, NH * 64], F32, "o_tm") for _ in range(3)]
            otb = nbs(s3, "otm", 3)
            osq = sb(g, s3, [128, NH * 64], F32, "osq")
            osqb = nb(s3, "osq")
            og = [sb(g, s3, [128, NH * 64], BF16, "og") for _ in range(2)]
            ogb = nbs(s3, "og", 2)
            sm = [sb(g, s3, [128, 4], F32, "hsm") for _ in range(2)]
            smb = nbs(s3, "hsm", 2)
            wq = [sb(g, s3, [128, DC, 128], BF16, "wq") for _ in range(2)]
            wqb = nbs(s3, "wq", 2)
            wf = [sb(g, s3, [128, DC, 128], BF16, "wf") for _ in range(2)]
            wfb = nbs(s3, "wf", 2)

            def load_head(hd):
                DMA(g, "pool", wf[hd % 2][:], win_cols(g, l, C_HF + hd * 128, C_HF + hd * 128 + 128), (), [wfb[hd % 2]])
                DMA(g, "pool", wq[hd % 2][:], win_cols(g, l, C_HQ + hd * 128, C_HQ + hd * 128 + 128), (), [wqb[hd % 2]])

            load_head(0)
            for hd in range(NH):
                if hd + 1 < NH:
                    load_head(hd + 1)
                w_f, w_fb, w_q, w_qb = wf[hd % 2], wfb[hd % 2], wq[hd % 2], wqb[hd % 2]
                for tc in range(4):
                    ps, pb = psf(g, "proj", [0, 1, 2, 3, 4, 5])
                    mm_fm(g, ps, 128, 512, w_f, w_fb, 0, hT, hTb, tc * 512, pb)
                    ACT(g, t1[:, tc * 512:(tc + 1) * 512], ps[:, :], AF.Sigmoid, [pb], [t1b])
                TS(g, "dve", t1[:], t1[:], g.oml[:, l, hd:hd + 1], g.lbv[:, l, hd:hd + 1], ALU.mult, ALU.add, [t1b, g.cb], [t1b])
                TS(g, "pool", t2[:], t1[:], -1.0, 1.0, ALU.mult, ALU.add, [t1b], [t2b])
                TS(g, "dve", t1[:], t1[:], F_MIN, None, ALU.max, None, [t1b], [t1b])
                ACT(g, t1[:], t1[:], AF.Ln, [t1b], [t1b])
                P.op("dve", lambda e: e.tensor_tensor_scan(out=t3[:], data0=g.resetm[:], data1=t1[:], initial=0.0,
                                                           op0=ALU.mult, op1=ALU.add), [t1b, g.cb], [t3b], 2 * T)
                TS(g, "pool", t3[:], t3[:], -80.0, None, ALU.max, None, [t3b], [t3b])
                ACT(g, t1[:], t3[:], AF.Exp, [t3b], [t1b])
                CP(g, "pool", ebl[:, hd, :].unsqueeze(2), t1[:].rearrange("p (c j) -> p c j", j=64)[:, :, 63:64], [t1b], [eblb])
                ACT(g, t3[:], t3[:], AF.Exp, [t3b], [t3b], scale=-1.0)
                TT(g, "dve", ktT[:, hd, :], t2[:], t3[:], ALU.mult, [t2b, t3b], [ktb[hd]])
                for tc in range(4):
                    ps, pb = psf(g, "proj", [0, 1, 2, 3, 4, 5])
                    mm_fm(g, ps, 128, 512, w_q, w_qb, 0, hT, hTb, tc * 512, pb)
                    ACT(g, t2[:, tc * 512:(tc + 1) * 512], ps[:, :], AF.Sigmoid, [pb], [t2b])
                    TT(g, "dve", t2[:, tc * 512:(tc + 1) * 512], ps[:, :], t2[:, tc * 512:(tc + 1) * 512], ALU.mult, [pb, t2b], [t2b])
                TT(g, "dve", qtT[:, hd, :], t2[:], t1[:], ALU.mult, [t2b, t1b], [qtb[hd]])
                for k0 in (0, 8):
                    pt, ptb = psb(g)
                    MM(g, [trf(pt[:, m * 128:(m + 1) * 128], ktT[:, hd, (k0 + m) * 128:(k0 + m + 1) * 128], g.ident[:])
                           for m in range(8)], [ktb[hd], g.cb], [ptb])
                    CP(g, "act" if k0 == 0 else "dve", kt_tm[:, k0:k0 + 8, hd * 128:(hd + 1) * 128],
                       pt[:, :].rearrange("p (m j) -> p m j", j=128), [ptb], [kttb[hd]])

            NCH = 32
            xps = {}

            def emit_X(c):
                tb, pr = c // 2, (c % 2) * 64
                psX, pXb = psf(g, "hgX", [0, 1, 2])
                xps[c] = (psX, pXb)
                MM(g, [mmf(psX[:, hh * 64:(hh + 1) * 64], kt_tm[pr:pr + 64, tb, hh * 128:(hh + 1) * 128],
                           hi_tm[pr:pr + 64, tb, hh * 64:(hh + 1) * 64], True, True) for hh in range(NH)], kttb + [hib[tb]], [pXb])

            def emit_A(tb):
                psA, pAb = psf(g, "hgA", [3])
                MM(g, [mmf(psA[:, hh * 128:(hh + 1) * 128], ktT[:, hh, tb * 128:(tb + 1) * 128],
                           qtT[:, hh, tb * 128:(tb + 1) * 128], True, True) for hh in range(NH)], ktb + qtb, [pAb])
                TT(g, "dve", attm[tb % 3][:], psA[:, :].rearrange("p (h t) -> p h t", t=128),
                   g.maskbd[:].unsqueeze(1).to_broadcast([128, NH, 128]), ALU.mult, [pAb, g.cb], [attb[tb % 3]])

            emit_X(0)
            emit_X(1)
            emit_A(0)
            CP(g, "dve", W[0][:], xps[0][0][:, 0:NH * 64].rearrange("p (h v) -> p h v", v=64), [xps[0][1]], [Wb[0]])
            MS(g, "pool", Sbf[0][:], 0.0, [Sbfb[0]])
            for c in range(NCH):
                tb, half = c // 2, c % 2
                pr = half * 64
                if c + 2 < NCH:
                    emit_X(c + 2)
                if half == 0 and tb + 1 < NB:
                    emit_A(tb + 1)
                if c + 1 < NCH:
                    eb_ = ebl[:, :, c:c + 1].to_broadcast([128, NH, 64])
                    TT(g, "pool", Sbf[(c + 1) % R][:], W[c % 2][:], eb_, ALU.mult, [Wb[c % 2], eblb], [Sbfb[(c + 1) % R]])
                    psX, pXb = xps.pop(c + 1)
                    for hh in range(NH):
                        STT(g, W[(c + 1) % 2][:, hh, :], W[c % 2][:, hh, :], ebl[:, hh, c:c + 1], psX[:, hh * 64:(hh + 1) * 64],
                            ALU.mult, ALU.add, [Wb[c % 2], eblb, pXb], [Wb[(c + 1) % 2]])
                am, amb = attm[tb % 3], attb[tb % 3]
                ot, otbuf = o_tm[tb % 3], otb[tb % 3]
                psO, pOb = psf(g, "hgO", [4, 5])
                fns = []
                for hh in range(NH):
                    fns.append(mmf(psO[0:64, hh * 64:(hh + 1) * 64], am[pr:pr + 64, hh, pr:pr + 64],
                                   hi_tm[pr:pr + 64, tb, hh * 64:(hh + 1) * 64], True, False))
                    fns.append(mmf(psO[0:64, hh * 64:(hh + 1) * 64], qtT[:, hh, c * 64:(c + 1) * 64], Sbf[c % R][:, hh, :], False, True))
                MM(g, fns, [amb, hib[tb], Sbfb[c % R]] + qtb, [pOb])
                CP(g, "act", ot[pr:pr + 64, :], psO[0:64, 0:NH * 64], [pOb], [otbuf])
                if half == 1:
                    sm_, sm_b = sm[tb % 2], smb[tb % 2]
                    og_, og_b = og[tb % 2], ogb[tb % 2]
                    TT(g, "pool", osq[:], ot[:], ot[:], ALU.mult, [otbuf], [osqb])
                    ss = sm_[:, 0:NH]
                    P.op("dve", lambda e, ss=ss: e.tensor_reduce(out=ss, in_=osq[:].rearrange("p (h d) -> p h d", d=64), axis=AX.X,
                                                                 op=ALU.add), [osqb], [sm_b], NH * 64)
                    ACT(g, ss, ss, AF.Ln, [sm_b], [sm_b], scale=1.0 / 64, bias=EPS)
                    ACT(g, ss, ss, AF.Exp, [sm_b], [sm_b], scale=-0.5)
                    o3 = ot[:].rearrange("p (h d) -> p h d", d=64)
                    TT(g, "dve", o3, o3, ss.unsqueeze(2).to_broadcast([128, NH, 64]), ALU.mult, [otbuf, sm_b], [otbuf])
                    TT(g, "dve", o3, o3, onb[:].unsqueeze(1).to_broadcast([128, NH, 64]), ALU.mult, [otbuf, onbb], [otbuf])
                    TT(g, "dve", og_[:], ot[:], hgs[:, tb, :], ALU.mult, [otbuf, hgb[tb]], [og_b])
                    pt, ptb = psb(g)
                    MM(g, [trf(pt[:, m * 128:(m + 1) * 128], og_[:, m * 128:(m + 1) * 128], g.ident[:]) for m in range(2)],
                       [og_b, g.cb], [ptb])
                    CP(g, "act", ohT[:, 0:2, tb * 128:(tb + 1) * 128], pt[:, 0:256].rearrange("p (m j) -> p m j", j=128), [ptb],
                       [ohb[0][tb // 4], ohb[1][tb // 4]])


def phase_mix(g, l, hT, hTb, osbT, osbb, odT, odb, ohT, ohb, src_ap, src_bufs):
    P = g.P
    with scope(g) as st:
        mixT = sb(g, st, [128, DC, T], BF16, "mixT")
        mxb = nbs(st, "mix", DC, 4)
        wg = [sb(g, st, [128, DC, 3, 128], BF16, "wg") for _ in range(2)]
        wgb = nbs(st, "wg", 2, 3)
        wy = [sb(g, st, [128, 8, 128], BF16, "wy") for _ in range(2)]
        wyb = nbs(st, "wy", 2, 3)
        sg = [sb(g, st, [128, 512], F32, "sg") for _ in range(2)]
        sgb = nbs(st, "sg", 2)
        acc = [sb(g, st, [128, 512], F32, "acc") for _ in range(2)]
        accb = nbs(st, "acc", 2)
        tm = [sb(g, st, [128, 512], F32, "tm") for _ in range(2)]
        tmb = nbs(st, "tm", 2)
        k = 0

        def load_dc(dc):
            w_, w_b = wg[dc % 2], wgb[dc % 2]
            y_, y_b = wy[dc % 2], wyb[dc % 2]
            for gi in range(3):
                c0 = C_G + gi * 1024 + dc * 128
                DMA(g, "pool", w_[:, :, gi, :], win_cols(g, l, c0, c0 + 128), (), [w_b[gi]])
            DMA(g, "pool", y_[:, 0:3, :], g.w_sb[l].rearrange("(c p) n -> p c n", p=128)[:, :, dc * 128:(dc + 1) * 128], (), [y_b[0]])
            DMA(g, "pool", y_[:, 3:6, :], g.w_dsa[l].rearrange("(c p) n -> p c n", p=128)[:, :, dc * 128:(dc + 1) * 128], (), [y_b[1]])
            DMA(g, "pool", y_[:, 6:8, :], g.w_hg[l].rearrange("(c p) n -> p c n", p=128)[:, :, dc * 128:(dc + 1) * 128], (), [y_b[2]])

        load_dc(0)
        wo = sb(g, st, [128, DC, D], BF16, "wo")
        wob = nbs(st, "wo", 2)
        for dc in range(DC):
            w_, w_b = wg[dc % 2], wgb[dc % 2]
            y_, y_b = wy[dc % 2], wyb[dc % 2]
            if dc + 1 < DC:
                load_dc(dc + 1)
            else:
                for nh in range(2):
                    DMA(g, "pool", wo[:, :, nh * 512:(nh + 1) * 512],
                        g.w_out[l].rearrange("(c p) n -> p c n", p=128)[:, :, nh * 512:(nh + 1) * 512], (), [wob[nh]])
            for tc in range(4):
                a_, a_b = acc[k % 2], accb[k % 2]
                for gi, (oT, obufs, nch, c0) in enumerate(((osbT, osbb, 3, 0), (odT, odb, 3, 3), (ohT, ohb, 2, 6))):
                    psG, pGb = psf(g, "mxG", [0, 1, 2])
                    MM(g, [mmf(psG[:, :], w_[:, c, gi, :], hT[:, c, tc * 512:(tc + 1) * 512], c == 0, c == DC - 1) for c in range(DC)],
                       [w_b[gi]] + hTb[tc * 4:tc * 4 + 4], [pGb])
                    s_, s_b = sg[(k * 3 + gi) % 2], sgb[(k * 3 + gi) % 2]
                    ACT(g, s_[:], psG[:, :], AF.Sigmoid, [pGb], [s_b])
                    psY, pYb = psf(g, "mxY", [3, 4, 5])
                    MM(g, [mmf(psY[:, :], y_[:, c0 + c, :], oT[:, c, tc * 512:(tc + 1) * 512], c == 0, c == nch - 1) for c in range(nch)],
                       [y_b[gi]] + [obufs[c][tc] for c in range(nch)], [pYb])
                    if gi == 0:
                        TT(g, "dve", a_[:], psY[:, :], s_[:], ALU.mult, [pYb, s_b], [a_b])
                    else:
                        t_, t_b = tm[gi % 2], tmb[gi % 2]
                        TT(g, "dve", t_[:], psY[:, :], s_[:], ALU.mult, [pYb, s_b], [t_b])
                        if gi == 1:
                            TT(g, "pool", a_[:], a_[:], t_[:], ALU.add, [a_b, t_b], [a_b])
                        else:
                            TT(g, "pool", mixT[:, dc, tc * 512:(tc + 1) * 512], a_[:], t_[:], ALU.add, [a_b, t_b], [mxb[dc][tc]])
                k += 1
        xbs = [sb(g, st, [128, D], F32, "xb") for _ in range(4)]
        xbb = nbs(st, "xb", 4)
        for tb in range(NB):
            xb, xbuf = xbs[tb % 4], xbb[tb % 4]
            DMA(g, "sp", xb[:], src_ap[tb * 128:(tb + 1) * 128, :], [src_bufs[tb]], [xbuf])
            for nh in range(2):
                ps, pb = psf(g, "mxO", [0, 1, 2, 3, 4, 5])
                MM(g, [mmf(ps[:, :], mixT[:, c, tb * 128:(tb + 1) * 128], wo[:, c, nh * 512:(nh + 1) * 512], c == 0, c == DC - 1)
                       for c in range(DC)], [wob[nh]] + [mxb[c][tb // 4] for c in range(DC)], [pb])
                xs = xb[:, nh * 512:(nh + 1) * 512]
                TT(g, "dve", xs, ps[:, :], xs, ALU.add, [pb, xbuf], [xbuf])
            DMA(g, "sp", g.xres_d[tb * 128:(tb + 1) * 128, :], xb[:], [xbuf], [g.xres_b[tb]])


def phase_ffn(g, l, hT, hTb, dst_ap, dst_bufs):
    with scope(g) as st:
        norm_T(g, st, g.xres_d, g.xres_b, g.norm_mlp[l:l + 1, :], hT, hTb)
    for half in range(2):
        last = half == 1
        with scope(g) as st:
            uT = sb(g, st, [128, 16, T], BF16, "uT")
            ub = nbs(st, "uT", 16, 4)
            wd = sb(g, st, [128, 16, D], BF16, "wd")
            wdb = nbs(st, "wd", 2)
            wdv = g.w_down[l].rearrange("(f p) n -> p f n", p=128)
            wu = [sb(g, st, [128, DC, 512], BF16, "wu") for _ in range(2)]
            wub = nbs(st, "wu", 2)
            rt = [sb(g, st, [128, 512], BF16, "rt") for _ in range(2)]
            rtb = nbs(st, "rt", 2)
            k = 0

            def load_wu(g4):
                c0 = half * 2048 + g4 * 512
                DMA(g, "pool", wu[g4 % 2][:], g.w_up[l].rearrange("(c p) n -> p c n", p=128)[:, :, c0:c0 + 512], (), [wub[g4 % 2]])

            load_wu(0)
            for g4 in range(4):
                w_, w_b = wu[g4 % 2], wub[g4 % 2]
                if g4 + 1 < 4:
                    load_wu(g4 + 1)
                if g4 == 1:
                    for nh in range(2):
                        DMA(g, "pool", wd[:, :, nh * 512:(nh + 1) * 512], wdv[:, half * 16:(half + 1) * 16, nh * 512:(nh + 1) * 512], (),
                            [wdb[nh]])
                for fcl in range(4):
                    fc = g4 * 4 + fcl
                    for tc in range(4):
                        ps, pb = psf(g, "proj", [0, 1, 2, 3, 4, 5])
                        mm_fm(g, ps, 128, 512, w_, w_b, fcl * 128, hT, hTb, tc * 512, pb)
                        r_, r_b = rt[k % 2], rtb[k % 2]
                        k += 1
                        ACT(g, r_[:], ps[:, :], AF.Relu, [pb], [r_b])
                        TT(g, "pool", uT[:, fc, tc * 512:(tc + 1) * 512], r_[:], r_[:], ALU.mult, [r_b], [ub[fc][tc]])
            xbs = [sb(g, st, [128, D], F32, "xb") for _ in range(4)]
            xbb = nbs(st, "xb", 4)
            for tb in range(NB):
                xb, xbuf = xbs[tb % 4], xbb[tb % 4]
                DMA(g, "sp", xb[:], g.xres_d[tb * 128:(tb + 1) * 128, :], [g.xres_b[tb]], [xbuf])
                for nh in range(2):
                    ps, pb = psf(g, "proj", [0, 1, 2, 3, 4, 5])
                    MM(g, [mmf(ps[:, :], uT[:, fc, tb * 128:(tb + 1) * 128], wd[:, fc, nh * 512:(nh + 1) * 512], fc == 0, fc == 15)
                           for fc in range(16)], [wdb[nh]] + [ub[fc][tb // 4] for fc in range(16)], [pb])
                    xs = xb[:, nh * 512:(nh + 1) * 512]
                    TT(g, "dve", xs, ps[:, :], xs, ALU.add, [pb, xbuf], [xbuf])
                if last:
                    DMA(g, "sp", dst_ap[tb * 128:(tb + 1) * 128, :], xb[:], [xbuf], [dst_bufs[tb]])
                else:
                    DMA(g, "sp", g.xres_d[tb * 128:(tb + 1) * 128, :], xb[:], [xbuf], [g.xres_b[tb]])


def dump_t(g, name, t, ncol):
    if g.dump == name:
        g.P.barrier()
        DMA(g, "sp", g.dbg_d[:, 0:ncol], t, [], [g.dbgb])
        g.P.wait_bufs("sp", [g.dbgb])
        g.P.barrier()


def build_layer(g, l):
    if l > 0 and g.stage < 7:
        return
    src_ap, src_bufs = (g.x_d, g.xin_b) if l == 0 else (g.xres_d, g.xres_b)
    with scope(g) as ls:
        hT = sb(g, ls, [128, DC, T], BF16, "hT")
        hTb = nbs(ls, "hT", NB)
        with scope(g) as st:
            norm_T(g, st, src_ap, src_bufs, g.norm_mix[l:l + 1, :], hT, hTb)
        if l == 0:
            dump_t(g, "hT", hT[:].rearrange("p c t -> p (c t)"), 8 * T)
        if g.stage < 2:
            return
        with scope(g) as ms:
            osbT = sb(g, ms, [128, 3, T], BF16, "osbT")
            odT = sb(g, ms, [128, 3, T], BF16, "odT")
            ohT = sb(g, ms, [128, 2, T], BF16, "ohT")
            osbb = nbs(ms, "osb", 3, 4)
            odb = nbs(ms, "od", 3, 4)
            ohb = nbs(ms, "oh", 2, 4)
            phase_sb(g, l, hT, hTb, osbT, osbb)
            if l == 0:
                dump_t(g, "osbT", osbT[:].rearrange("p c t -> p (c t)"), 3 * T)
            if g.stage < 3:
                return
            phase_dsa(g, l, hT, hTb, odT, odb)
            if l == 0:
                dump_t(g, "odT", odT[:].rearrange("p c t -> p (c t)"), 3 * T)
            if g.stage < 4:
                return
            phase_hgrn(g, l, hT, hTb, ohT, ohb)
            if l == 0:
                dump_t(g, "ohT", ohT[:].rearrange("p c t -> p (c t)"), 2 * T)
            if g.stage < 5:
                return
            phase_mix(g, l, hT, hTb, osbT, osbb, odT, odb, ohT, ohb, src_ap, src_bufs)
        if g.stage < 6:
            return
        if l == DEPTH - 1:
            phase_ffn(g, l, hT, hTb, g.out_d, g.out_b)
        else:
            phase_ffn(g, l, hT, hTb, g.xres_d, g.xres_b)


_NC_CACHE = {}


def rope_tables():
    half = 8
    inv = 500000.0 ** (-(np.arange(half, dtype=np.float32) * 2.0) / 16.0)
    ang = np.arange(T, dtype=np.float32)[:, None] * inv[None, :].astype(np.float32)
    return np.cos(ang).astype(np.float32), np.sin(ang).astype(np.float32)


def kernel(x, norm_mix, w_in, qn_dsa, kn_dsa, hgrn_lb, hgrn_onorm, w_br_sb, w_br_dsa, w_br_hgrn, w_out, norm_mlp, w_up, w_down):
    if "nc" not in _NC_CACHE:
        _NC_CACHE["nc"] = build_two_pass()
    nc = _NC_CACHE["nc"]
    f = lambda a: np.ascontiguousarray(np.asarray(a, dtype=np.float32))
    cs, sn = rope_tables()
    shared = dict(norm_mix=f(norm_mix), w_in=f(w_in), qn_dsa=f(qn_dsa), kn_dsa=f(kn_dsa), hgrn_lb=f(hgrn_lb),
                  hgrn_onorm=f(hgrn_onorm), w_br_sb=f(w_br_sb), w_br_dsa=f(w_br_dsa), w_br_hgrn=f(w_br_hgrn),
                  w_out=f(w_out), norm_mlp=f(norm_mlp), w_up=f(w_up), w_down=f(w_down), rope_cos=cs, rope_sin=sn)
    xs = f(x)
    in_maps = [dict(shared, x=xs[b]) for b in range(8)]
    res = run_bass_kernel_spmd(nc, in_maps, core_ids=list(range(8)))
    return np.stack([np.asarray(r["out"], dtype=np.float32) for r in res.results], axis=0)
```

```python
import math
import numpy as np
from contextlib import ExitStack, contextmanager
import concourse.bass as bass
import concourse.mybir as mybir
from concourse.bass_utils import run_bass_kernel_spmd

F32 = mybir.dt.float32
BF16 = mybir.dt.bfloat16
AF = mybir.ActivationFunctionType
ALU = mybir.AluOpType
AX = mybir.AxisListType

T = 2048
D = 1024
NB = 16
DC = 8
DIN = 6856
DFF = 4096
DEPTH = 2
EPS = 1e-6
F_MIN = 1e-12
IDX_SCALE = (64 * 8) ** -0.5
C_SQ, C_SK, C_SV = 0, 384, 768
C_DQ, C_DK, C_DV = 1152, 1536, 1600
C_IQ, C_IK, C_IW = 1664, 2176, 2240
C_HQ, C_HF, C_HI, C_HG = 2248, 2760, 3272, 3528
C_G = 3784
N_BISECT = 16


class Buf:
    __slots__ = ("name", "w", "r", "dsem", "excl")

    def __init__(self, name, excl=False):
        self.name = name
        self.w = None
        self.r = {}
        self.dsem = None
        self.excl = excl


class Prog:
    ENG = ("pe", "act", "dve", "pool", "sp")
    CLEAR_NS = 330.0
    FILL_NS = {"dve": 66.0, "act": 190.0, "pool": 125.0}
    EST = {"dve": (60.0, 0.26), "act": (185.0, 0.83), "pool": (120.0, 0.8), "pe": (0.0, 0.0), "sp": (0.0, 0.0)}

    def __init__(self, nc, stack, needed=None):
        self.needed = needed
        self.used = set()
        self.remap = {}
        self.sig = {}
        self.fill = {}
        self.nc = nc
        self.stack = stack
        self.eng = {"pe": nc.tensor, "act": nc.scalar, "dve": nc.vector, "pool": nc.gpsimd, "sp": nc.sync}
        self.cnt = {e: 0 for e in self.ENG}
        self.known = {e: {} for e in self.ENG}
        self.sems = {}
        self.semval = {}
        for e in ("pe", "act", "dve", "pool"):
            self.sems["E_" + e] = stack.enter_context(nc.semaphore("sem_" + e))
            self.semval["E_" + e] = 0
        self.ndsem = 0
        self.free_dsems = []
        self.nwaits = 0
        self.tcum = {e: 0.0 for e in self.ENG}
        self.tend = {e: {} for e in self.ENG}

    def _dsem(self, buf):
        if buf.dsem is None:
            if self.free_dsems:
                key = self.free_dsems.pop()
            else:
                key = "D%d" % self.ndsem
                self.ndsem += 1
                self.sems[key] = self.stack.enter_context(self.nc.semaphore("dsem%d" % (self.ndsem - 1)))
                self.semval[key] = 0
            buf.dsem = key
        return buf.dsem

    def release(self, bufs):
        for b in bufs:
            if b.dsem is not None:
                self.free_dsems.append(b.dsem)
                b.dsem = None

    def _waits(self, eng, deps):
        need = {}
        own = "E_" + eng
        for (k, v) in deps:
            if eng == "pe" and k == "E_pe":
                continue
            if k == own and eng in ("act", "dve", "pool"):
                te = self.tend[eng].get(v)
                if te is not None and eng in self.fill:
                    gap = self.CLEAR_NS - (self.tcum[eng] - te)
                    if gap > 0:
                        n = int(math.ceil(gap / self.FILL_NS[eng]))
                        for _ in range(n):
                            self.fill[eng](self.eng[eng])
                        self.tcum[eng] += n * self.FILL_NS[eng]
                        self.nfill = getattr(self, "nfill", 0) + n
                continue
            if v > need.get(k, 0):
                need[k] = v
        out = []
        kn = self.known[eng]
        for k, v in need.items():
            if kn.get(k, 0) < v:
                kn[k] = v
                out.append((k, v))
        return out

    @staticmethod
    def _deps(reads, writes):
        deps = []
        for b in reads:
            if b.w is not None:
                deps.append(b.w)
            if b.excl:
                deps.extend(b.r.items())
        for b in writes:
            if b.w is not None:
                deps.append(b.w)
            deps.extend(b.r.items())
        return deps

    def _emit_waits(self, eng, waits):
        e = self.eng[eng]
        for (k, v) in waits:
            if k.startswith("E_"):
                self.used.add((k, v))
                if self.needed is not None:
                    v = self.remap[(k, v)]
            e.wait_ge(self.sems[k], v)
            self.nwaits += 1

    def _mark(self, ev, reads, writes):
        k, v = ev
        for b in reads:
            if b.r.get(k, 0) < v:
                b.r[k] = v
        for b in writes:
            b.w = ev
            b.r = {}

    def op(self, eng, fn, reads=(), writes=(), n=0):
        self.group(eng, [fn], reads, writes, n)

    def group(self, eng, fns, reads=(), writes=(), n=0):
        self._emit_waits(eng, self._waits(eng, self._deps(reads, writes)))
        e = self.eng[eng]
        for fn in fns[:-1]:
            fn(e)
        self.cnt[eng] += 1
        ov, pe_ = self.EST[eng]
        self.tcum[eng] += ov + pe_ * n
        td = self.tend[eng]
        td[self.cnt[eng]] = self.tcum[eng]
        if len(td) > 64:
            for k_ in sorted(td)[:32]:
                del td[k_]
        key = "E_" + eng
        self.semval[key] = self.cnt[eng]
        if self.needed is None or (key, self.cnt[eng]) in self.needed:
            self.sig[key] = self.sig.get(key, 0) + 1
            self.remap[(key, self.cnt[eng])] = self.sig[key]
            fns[-1](e).then_inc(self.sems[key], 1)
        else:
            fns[-1](e)
        self._mark((key, self.cnt[eng]), reads, writes)

    def dma(self, eng, fn, reads=(), writes=()):
        assert len(writes) == 1
        wb = writes[0]
        deps = self._deps(reads, writes)
        if eng == "pool" and getattr(self, "last_swdge", None) is not None:
            deps.append(self.last_swdge)
        self._emit_waits(eng, self._waits(eng, deps))
        key = self._dsem(wb)
        self.semval[key] += 16
        fn(self.eng[eng]).then_inc(self.sems[key], 16)
        if eng == "pool":
            self.last_swdge = (key, self.semval[key])
        self._mark((key, self.semval[key]), reads, writes)

    def barrier(self):
        deps = [(k, v) for k, v in self.semval.items() if v > 0]
        for eng in self.ENG:
            self._emit_waits(eng, self._waits(eng, deps))

    def wait_bufs(self, eng, bufs):
        deps = []
        for b in bufs:
            if b.w is not None:
                deps.append(b.w)
            deps.extend(b.r.items())
        self._emit_waits(eng, self._waits(eng, deps))


class G:
    pass


def bufs(prefix, *dims):
    if len(dims) == 1:
        return [Buf("%s%d" % (prefix, i)) for i in range(dims[0])]
    return [bufs("%s%d_" % (prefix, i), *dims[1:]) for i in range(dims[0])]


def build_program(stage=99, dump=None, needed=None):
    nc = bass.Bass("TRN2", target_bir_lowering=False)
    g = G()
    g.nc = nc
    g.stage = stage
    g.dump = dump
    import os
    g.ntl = int(os.environ.get("NTL", "9"))
    g.dbg_d = None
    if dump is not None:
        g.dbg_d = nc.dram_tensor("dbg", [128, 8 * T], BF16, kind="ExternalOutput").ap()
        g.dbgb = Buf("dbg")
    dt = lambda name, shape, kind, d=F32: nc.dram_tensor(name, shape, d, kind=kind).ap()
    g.x_d = dt("x", [T, D], "ExternalInput")
    g.norm_mix = dt("norm_mix", [DEPTH, D], "ExternalInput")
    g.w_in = dt("w_in", [DEPTH, D, DIN], "ExternalInput")
    g.qn = dt("qn_dsa", [DEPTH, 64], "ExternalInput")
    g.kn = dt("kn_dsa", [DEPTH, 64], "ExternalInput")
    g.lb_d = dt("hgrn_lb", [DEPTH, 512], "ExternalInput")
    g.onorm = dt("hgrn_onorm", [DEPTH, 64], "ExternalInput")
    g.w_sb = dt("w_br_sb", [DEPTH, 384, D], "ExternalInput")
    g.w_dsa = dt("w_br_dsa", [DEPTH, 384, D], "ExternalInput")
    g.w_hg = dt("w_br_hgrn", [DEPTH, 256, D], "ExternalInput")
    g.w_out = dt("w_out", [DEPTH, D, D], "ExternalInput")
    g.norm_mlp = dt("norm_mlp", [DEPTH, D], "ExternalInput")
    g.w_up = dt("w_up", [DEPTH, D, DFF], "ExternalInput")
    g.w_down = dt("w_down", [DEPTH, DFF, D], "ExternalInput")
    g.cs_d = dt("rope_cos", [T, 8], "ExternalInput")
    g.sn_d = dt("rope_sin", [T, 8], "ExternalInput")
    g.out_d = dt("out", [T, D], "ExternalOutput")
    g.xres_d = dt("xres", [T, D], "Internal")
    g.xin_b = bufs("xin", NB)
    g.xres_b = bufs("xres", NB)
    g.out_b = bufs("outb", NB)

    with ExitStack() as gs:
        P = Prog(nc, gs, needed)
        g.P = P
        fa = gs.enter_context(nc.sbuf_tensor("fill_a", [128, 2], F32))
        fd = gs.enter_context(nc.sbuf_tensor("fill_d", [128, 2], F32))
        nc.vector.memset(fd[:], 0.0)
        nc.vector.memset(fa[:], 0.0)
        P.fill["dve"] = lambda e: e.memset(fd[:, 0:1], 0.0)
        P.fill["act"] = lambda e: e.activation(out=fa[:, 0:1], in_=fa[:, 1:2], func=AF.Copy)
        fp = gs.enter_context(nc.sbuf_tensor("fill_p", [128, 2], F32))
        nc.gpsimd.memset(fp[:], 0.0)
        P.fill["pool"] = lambda e: e.memset(fp[:, 0:1], 0.0)
        g.uid = 0
        g.psF = [gs.enter_context(nc.psum_tensor("psF%d" % i, [128, 512], F32)) for i in range(6)]
        g.psFb = [Buf("psF%d" % i, excl=True) for i in range(6)]
        g.psB = [gs.enter_context(nc.psum_tensor("psB%d" % i, [128, 1024], BF16)) for i in range(2)]
        g.psBb = [Buf("psB%d" % i, excl=True) for i in range(2)]
        g.rotc = {}
        build_consts(g, gs)
        g.hT = gs.enter_context(nc.sbuf_tensor("hT_glob", [128, DC, T], BF16))
        g.hTb = bufs("hT", NB)
        for l in range(DEPTH):
            if g.stage >= 1:
                build_layer(g, l)
        if g.dbg_d is not None:
            P.wait_bufs("sp", [g.dbgb])
        P.wait_bufs("sp", g.out_b)
        P.barrier()
        g.used = P.used
        print("ops", P.cnt, "signals", P.sig, "waits", P.nwaits, "fillers", getattr(P, "nfill", 0), "dsems", P.ndsem, flush=True)
    return nc, P.used


def build_two_pass(stage=99, dump=None):
    _, used = build_program(stage, dump, None)
    nc, _ = build_program(stage, dump, used)
    return nc


def pipeline(stages, ntiles):
    ns = len(stages)
    for t in range(ntiles + ns - 1):
        for k, f in enumerate(stages):
            i = t - k
            if 0 <= i < ntiles:
                f(i)


def rot(g, role, items):
    i = g.rotc.get(role, 0)
    g.rotc[role] = i + 1
    return items[i % len(items)]


def psf(g, role, banks):
    b = rot(g, role, banks)
    return g.psF[b], g.psFb[b]


def psb(g, role="pb"):
    b = rot(g, role, [0, 1])
    return g.psB[b], g.psBb[b]


@contextmanager
def scope(g):
    st = ExitStack()
    st.tbufs = []
    try:
        yield st
    finally:
        g.P.barrier()
        g.P.release(st.tbufs)
        st.close()


def sb(g, st, shape, dtype, name=None):
    g.uid += 1
    return st.enter_context(g.nc.sbuf_tensor("%s_%d" % (name or "t", g.uid), shape, dtype))


def nb(st, name):
    b = Buf(name)
    st.tbufs.append(b)
    return b


def nbs(st, prefix, *dims):
    r = bufs(prefix, *dims)

    def flat(x):
        if isinstance(x, Buf):
            st.tbufs.append(x)
        else:
            for y in x:
                flat(y)
    flat(r)
    return r


def _fs(ap):
    try:
        return int(ap.free_size())
    except Exception:
        return 0


def ACT(g, out, in_, func, reads, writes, **kw):
    g.P.op("act", lambda e: e.activation(out=out, in_=in_, func=func, **kw), reads, writes, _fs(out))


def TT(g, eng, out, in0, in1, op, reads, writes):
    g.P.op(eng, lambda e: e.tensor_tensor(out=out, in0=in0, in1=in1, op=op), reads, writes, _fs(out))


def TS(g, eng, out, in0, s1, s2, op0, op1, reads, writes, **kw):
    if op1 is None:
        s2 = 0.0 if isinstance(s1, (int, float)) else g.zc[0:in0.shape[0], 0:1]
        g.P.op(eng, lambda e: e.tensor_scalar(out=out, in0=in0, scalar1=s1, scalar2=s2, op0=op0, op1=ALU.add, **kw), reads, writes, _fs(out))
    else:
        g.P.op(eng, lambda e: e.tensor_scalar(out=out, in0=in0, scalar1=s1, scalar2=s2, op0=op0, op1=op1, **kw), reads, writes, _fs(out))


def STT(g, out, in0, scalar, in1, op0, op1, reads, writes):
    g.P.op("dve", lambda e: e.scalar_tensor_tensor(out=out, in0=in0, scalar=scalar, in1=in1, op0=op0, op1=op1), reads, writes, _fs(out))


def CP(g, eng, out, in_, reads, writes):
    if eng == "act":
        g.P.op("act", lambda e: e.activation(out=out, in_=in_, func=AF.Copy), reads, writes, _fs(out))
    else:
        g.P.op(eng, lambda e: e.tensor_copy(out, in_), reads, writes, _fs(out))


def MS(g, eng, ap, val, writes):
    g.P.op(eng, lambda e: e.memset(ap, val), (), writes, _fs(ap))


def ASEL(g, out, in_, pattern, cmp, fill, base, cm, reads, writes):
    g.P.op("pool", lambda e: e.affine_select(out=out, in_=in_, pattern=pattern, compare_op=cmp, fill=fill, base=base,
                                             channel_multiplier=cm), reads, writes, _fs(out))


def MM(g, outs_fns, reads, writes):
    g.P.group("pe", outs_fns, reads, writes)


def mmf(out, lhsT, rhs, start, stop):
    return lambda e: e.matmul(out, lhsT=lhsT, rhs=rhs, start=start, stop=stop)


def trf(out, in_, ident):
    return lambda e: e.transpose(out, in_, ident)


def DMA(g, eng, out, in_, reads, writes, **kw):
    g.P.dma(eng, lambda e: e.dma_start(out=out, in_=in_, **kw), reads, writes)


def build_consts(g, gs):
    nc = g.nc
    mk = lambda name, shape, d: gs.enter_context(nc.sbuf_tensor(name, shape, d))
    g.ident = mk("ident", [128, 128], BF16)
    g.negtri = mk("negtri", [128, 128], BF16)
    g.negones = mk("negones", [128, 128], BF16)
    g.onesb = mk("onesb", [128, 128], BF16)
    g.maskbd = mk("maskbd", [128, 128], F32)
    g.onesf = mk("onesf", [128, 128], F32)
    g.resetm = mk("resetm", [128, T], BF16)
    g.zc = mk("zc", [128, 1], F32)
    g.negbig = mk("negbig", [128, 1], F32)
    g.cs = mk("cs", [128, NB, 8], F32)
    g.sn = mk("sn", [128, NB, 8], F32)
    g.lbraw = mk("lbraw", [128, 2, 4], F32)
    g.lbv = mk("lbv", [128, 2, 4], F32)
    g.oml = mk("oml", [128, 2, 4], F32)
    g.cb = Buf("consts")
    g.csb = Buf("cs")
    g.snb = Buf("sn")
    g.lbb = Buf("lbraw")
    cb = [g.cb]
    MS(g, "pool", g.onesb[:], 1.0, cb)
    MS(g, "pool", g.negones[:], -1.0, cb)
    MS(g, "pool", g.onesf[:], 1.0, cb)
    MS(g, "pool", g.zc[:], 0.0, cb)
    MS(g, "pool", g.negbig[:], -1e29, cb)
    ASEL(g, g.ident[:], g.onesb[:], [[1, 128]], ALU.is_equal, 0.0, 0, -1, cb, cb)
    ASEL(g, g.negtri[:], g.negones[:], [[-1, 128]], ALU.is_ge, 0.0, 0, 1, cb, cb)
    ASEL(g, g.maskbd[:], g.onesf[:], [[1, 128]], ALU.is_ge, 0.0, 0, -1, cb, cb)
    MS(g, "pool", g.maskbd[0:64, 64:128], 0.0, cb)
    g.ones512 = mk("ones512", [128, 512], BF16)
    g.mlt = mk("mlt", [128, 512], BF16)
    MS(g, "pool", g.ones512[:], 1.0, cb)
    ASEL(g, g.mlt[:], g.ones512[:], [[1, 512]], ALU.is_gt, 0.0, 0, -1, cb, cb)
    g.caus01 = mk("caus01", [128, 128], F32)
    g.negfill = mk("negfill", [128, 128], F32)
    ASEL(g, g.caus01[:], g.onesf[:], [[-1, 128]], ALU.is_ge, 0.0, 0, 1, cb, cb)
    TS(g, "pool", g.negfill[:], g.caus01[:], -1.0, 1e30, ALU.add, ALU.mult, cb, cb)
    MS(g, "pool", g.resetm[:], 1.0, cb)
    MS(g, "pool", g.resetm[:].rearrange("p (c j) -> p c j", j=64)[:, :, 0:1], 0.0, cb)
    DMA(g, "sp", g.cs[:], g.cs_d.rearrange("(b p) i -> p b i", p=128), (), [g.csb])
    DMA(g, "sp", g.sn[:], g.sn_d.rearrange("(b p) i -> p b i", p=128), (), [g.snb])
    DMA(g, "sp", g.lbraw[:], g.lb_d.rearrange("l (h k) -> k l h", k=128), (), [g.lbb], allow_slow_non_contiguous=True)
    MS(g, "dve", g.lbv[:], 0.0, cb)
    TT(g, "dve", g.lbv[:, 1, :], g.lbraw[:, 1, :], g.lbraw[:, 0, :], ALU.subtract, [g.lbb], cb)
    ACT(g, g.lbv[:, 1, :], g.lbv[:, 1, :], AF.Sigmoid, cb, cb)
    TS(g, "dve", g.oml[:], g.lbv[:], -1.0, 1.0, ALU.mult, ALU.add, cb, cb)
    g.P.barrier()


class NormCtx:
    pass


def norm_setup(g, st, gain_d_row):
    c = NormCtx()
    c.gain = sb(g, st, [128, D], F32, "gain")
    c.gb = nb(st, "gain")
    DMA(g, "sp", c.gain[:], gain_d_row.to_broadcast([128, D]), (), [c.gb])
    c.junk = sb(g, st, [128, D], BF16, "junk")
    c.jb = nb(st, "junk")
    c.hbs = [sb(g, st, [128, D], BF16, "hb") for _ in range(2)]
    c.hbb = nbs(st, "hb", 2)
    c.ss = sb(g, st, [128, NB], F32, "ss")
    c.ssb = nbs(st, "ss", NB)
    MS(g, "dve", c.ss[:], 0.0, c.ssb)
    return c


def norm_block(g, c, xb, xbuf, tb, hT, hTb):
    hb, hbuf = c.hbs[tb % 2], c.hbb[tb % 2]
    s1 = c.ss[:, tb:tb + 1]
    ACT(g, c.junk[:], xb[:], AF.Square, [xbuf, c.ssb[tb]], [c.jb, c.ssb[tb]], accum_out=s1)
    ACT(g, s1, s1, AF.Ln, [c.ssb[tb]], [c.ssb[tb]], scale=1.0 / D, bias=EPS)
    ACT(g, s1, s1, AF.Exp, [c.ssb[tb]], [c.ssb[tb]], scale=-0.5)
    STT(g, hb[:], xb[:], s1, c.gain[:], ALU.mult, ALU.mult, [xbuf, c.ssb[tb], c.gb], [hbuf])
    for half in range(2):
        pt, ptb = psb(g)
        MM(g, [trf(pt[:, m * 128:(m + 1) * 128], hb[:, (half * 4 + m) * 128:(half * 4 + m + 1) * 128], g.ident[:])
               for m in range(4)], [hbuf, g.cb], [ptb])
        CP(g, "act" if half == 0 else "dve", hT[:, half * 4:half * 4 + 4, tb * 128:(tb + 1) * 128],
           pt[:, 0:512].rearrange("p (m j) -> p m j", j=128), [ptb], [hTb[tb]])


def norm_T(g, st, src_ap, src_bufs, gain_d_row, hT, hTb):
    c = norm_setup(g, st, gain_d_row)
    xbs = [sb(g, st, [128, D], F32, "xb") for _ in range(4)]
    xbb = nbs(st, "xb", 4)
    for tb in range(NB):
        xb, xbuf = xbs[tb % 4], xbb[tb % 4]
        DMA(g, "sp", xb[:], src_ap[tb * 128:(tb + 1) * 128, :], [src_bufs[tb]], [xbuf])
        norm_block(g, c, xb, xbuf, tb, hT, hTb)


def win_cols(g, l, c0, c1):
    return g.w_in[l].rearrange("(c p) n -> p c n", p=128)[:, :, c0:c1]


def mm_fm(g, ps, M, n, w, wb, col0, hT, hTb, tok0, role_bufs):
    MM(g, [mmf(ps[0:M, 0:n], w[:, c, col0:col0 + M], hT[:, c, tok0:tok0 + n], c == 0, c == DC - 1) for c in range(DC)],
       [wb] + hTb[tok0 // 128:(tok0 + n + 127) // 128], [role_bufs])


def mm_tm(g, ps, N, w, wb, col0, hT, hTb, tb, psbuf):
    MM(g, [mmf(ps[:, 0:N], hT[:, c, tb * 128:(tb + 1) * 128], w[:, c, col0:col0 + N], c == 0, c == DC - 1) for c in range(DC)],
       [wb, hTb[tb]], [psbuf])


def phase_sb(g, l, hT, hTb, osbT, osbb):
    with scope(g) as st:
        ws = []
        for i, c0 in enumerate((C_SQ, C_SK, C_SV)):
            w = sb(g, st, [128, DC, 384], BF16, "wsb")
            wb = nb(st, "wsb%d" % i)
            DMA(g, "pool", w[:], win_cols(g, l, c0, c0 + 384), (), [wb])
            ws.append((w, wb))
        sqT = sb(g, st, [128, 3, T], BF16, "sqT")
        skT = sb(g, st, [128, 3, T], BF16, "skT")
        sqb = nbs(st, "sq", 3, 4)
        skb = nbs(st, "sk", 3, 4)
        k = 0
        for (dst, dstb, (w, wb), scl) in ((sqT, sqb, ws[0], 0.125), (skT, skb, ws[1], 1.0)):
            for hp in range(3):
                for tc in range(4):
                    ps, pb = psf(g, "proj", [0, 1, 2, 3, 4, 5])
                    mm_fm(g, ps, 128, 512, w, wb, hp * 128, hT, hTb, tc * 512, pb)
                    o = dst[:, hp, tc * 512:(tc + 1) * 512]
                    if k % 2 == 0:
                        ACT(g, o, ps[:, :], AF.Copy, [pb], [dstb[hp][tc]], scale=scl)
                    else:
                        TS(g, "dve", o, ps[:, :], scl, None, ALU.mult, None, [pb], [dstb[hp][tc]])
                    k += 1
        svp = [sb(g, st, [128, NB, 384], BF16, "svp") for _ in range(2)]
        svb = nbs(st, "sv", 2, NB)
        for s_ in range(2):
            MS(g, "pool", svp[s_][:].rearrange("p t c -> p (t c)"), 0.0, svb[s_])
        for tb in range(NB):
            ps, pb = psf(g, "proj", [0, 1, 2, 3, 4, 5])
            mm_tm(g, ps, 384, ws[2][0], ws[2][1], 0, hT, hTb, tb, pb)
            src = ps[:, 0:384].rearrange("p (m s d) -> p m s d", s=2, d=64)
            for s_ in range(2):
                dst = svp[s_][:, tb, :].rearrange("p (m s d) -> p m s d", s=2, d=64)
                CP(g, "act" if s_ == 0 else "dve", dst[:, :, s_, :], src[:, :, s_, :], [pb], [svb[s_][tb]])
        R = 4
        mk2 = lambda shape, dt_, nm: [[sb(g, st, shape, dt_, nm) for _ in range(R)] for _ in range(2)]
        Et, SPt, SPs, At = mk2([128, 512], F32, "Et"), mk2([128, 512], BF16, "SPt"), mk2([128, 512], BF16, "SPs"), mk2([128, 512], BF16, "At")
        Etb, SPb, SPsb, Atb = nbs(st, "Et", 2, R), nbs(st, "SPt", 2, R), nbs(st, "SPs", 2, R), nbs(st, "At", 2, R)
        steps = []
        for m in range(3):
            for qc in range(4):
                for n_, kb in enumerate(range(4 * qc + 3, -1, -1)):
                    steps.append((m, qc, kb, n_))
        stt = {}
        pso = {}

        def info(t):
            m, qc, kb, n_ = steps[t]
            j0 = max(0, kb * 128 - qc * 512)
            return m, qc, kb, n_, j0, kb >= 4 * qc, qc * 512 + j0 - kb * 128, n_ == 0

        def opnds(t):
            m, qc, kb, n_, j0, diag, base, first = info(t)
            kk = [skT[64 * s_:64 * s_ + 64, m, kb * 128:(kb + 1) * 128] for s_ in range(2)]
            qq = [sqT[64 * s_:64 * s_ + 64, m, qc * 512 + j0:(qc + 1) * 512] for s_ in range(2)]
            return kk, qq, skb[m][kb // 4], sqb[m][qc]

        def s1(t):
            m, qc, kb, n_, j0, diag, base, first = info(t)
            kk, qq, rk, rq = opnds(t)
            pz = [psf(g, "sbZ", [0, 1, 2]) for _ in range(2)]
            stt[t] = {"pz": pz}
            for s_ in range(2):
                MM(g, [mmf(pz[s_][0][:, j0:512], kk[s_], qq[s_], True, True)], [rk, rq], [pz[s_][1]])

        def s2(t):
            m, qc, kb, n_, j0, diag, base, first = info(t)
            ib = t % R
            pz = stt[t]["pz"]
            for s_ in range(2):
                ACT(g, Et[s_][ib][:, j0:512], pz[s_][0][:, j0:512], AF.Exp, [pz[s_][1]], [Etb[s_][ib]])

        def s3(t):
            m, qc, kb, n_, j0, diag, base, first = info(t)
            ib = t % R
            for s_ in range(2):
                ACT(g, SPt[s_][ib][:, j0:512], Et[s_][ib][:, j0:512], AF.Ln, [Etb[s_][ib]], [SPb[s_][ib]], bias=1.0)
            if diag:
                for s_ in range(2):
                    S = SPt[s_][ib]
                    assert base == 0
                    TT(g, "pool", S[:, j0:512], S[:, j0:512], g.mlt[:, 0:512 - j0], ALU.mult, [SPb[s_][ib], g.cb], [SPb[s_][ib]])
            if kb > 0:
                for s_ in range(2):
                    S, Sb = SPt[s_][ib], SPb[s_][ib]
                    Sn, Snb = SPs[s_][n_ % R], SPsb[s_][n_ % R]
                    if first:
                        if j0 > 0:
                            MS(g, "pool", Sn[:, 0:j0], 0.0, [Snb])
                        CP(g, "pool", Sn[:, j0:512], S[:, j0:512], [Sb], [Snb])
                    else:
                        So, Sob = SPs[s_][(n_ - 1) % R], SPsb[s_][(n_ - 1) % R]
                        if j0 > 0:
                            CP(g, "pool", Sn[:, 0:j0], So[:, 0:j0], [Sob], [Snb])
                        TT(g, "dve", Sn[:, j0:512], So[:, j0:512], S[:, j0:512], ALU.add, [Sob, Sb], [Snb])

        def s4(t):
            m, qc, kb, n_, j0, diag, base, first = info(t)
            ib = t % R
            kk, qq, rk, rq = opnds(t)
            pc = [psf(g, "sbC", [3, 4]) for _ in range(2)]
            stt[t]["pc"] = pc
            for s_ in range(2):
                S = SPt[s_][ib]
                fns = [mmf(pc[s_][0][:, j0:512], kk[s_], qq[s_], True, False),
                       mmf(pc[s_][0][:, j0:512], g.negtri[:], S[:, j0:512], False, first)]
                rd = [rk, rq, SPb[s_][ib], g.cb]
                if not first:
                    So, Sob = SPs[s_][(n_ - 1) % R], SPsb[s_][(n_ - 1) % R]
                    fns.append(mmf(pc[s_][0][:, j0:512], g.negones[:], So[:, j0:512], False, True))
                    rd.append(Sob)
                MM(g, fns, rd, [pc[s_][1]])

        def s5(t):
            m, qc, kb, n_, j0, diag, base, first = info(t)
            ib = t % R
            pc = stt[t]["pc"]
            for s_ in range(2):
                ACT(g, At[s_][ib][:, j0:512], pc[s_][0][:, j0:512], AF.Exp, [pc[s_][1]], [Atb[s_][ib]])
            for s_ in range(2):
                A, Ab = At[s_][ib], Atb[s_][ib]
                if diag:
                    TT(g, "pool", A[:, j0:512], A[:, j0:512], g.mlt[:, 0:512 - j0], ALU.mult, [Ab, g.cb], [Ab])
                if first and j0 > 0:
                    MS(g, "pool", A[:, 0:j0], 0.0, [Ab])

        def s6(t):
            m, qc, kb, n_, j0, diag, base, first = info(t)
            ib = t % R
            if first:
                pso[(m, qc)] = psf(g, "sbO", [5])
            psO, pOb = pso[(m, qc)]
            for s_ in range(2):
                A, Ab = At[s_][ib], Atb[s_][ib]
                vv = svp[s_][:, kb, m * 128:(m + 1) * 128]
                c0 = 0 if first else j0
                MM(g, [mmf(psO[:, c0:512], vv, A[:, c0:512], first and s_ == 0, kb == 0 and s_ == 1)], [svb[s_][kb], Ab], [pOb])
            if kb == 0:
                CP(g, "act" if (m * 4 + qc) % 2 == 0 else "dve", osbT[:, m, qc * 512:(qc + 1) * 512], psO[:, :], [pOb], [osbb[m][qc]])
            del stt[t]

        nst = len(steps)
        stages = [s1, s2, s3, s4, s5, s6]
        for e in range(nst + len(stages) - 1):
            for k in range(len(stages) - 1, -1, -1):
                t = e - k
                if 0 <= t < nst:
                    stages[k](t)


def phase_dsa(g, l, hT, hTb, odT, odb):
    P = g.P
    with scope(g) as st:
        featT = sb(g, st, [128, 9, T], BF16, "featT")
        fb = nbs(st, "feat", NB)
        dvx = sb(g, st, [128, NB, 128], BF16, "dvx")
        dvb = nbs(st, "dvx", NB)
        sgn = sb(g, st, [128, NB, 8], F32, "sgn")
        sgb = nbs(st, "sgn", NB)
        qkg = sb(g, st, [128, 7, 64], F32, "qkg")
        qkgb = nbs(st, "qkg", 7)
        for hh in range(7):
            src = (g.qn if hh < 6 else g.kn)[l:l + 1, :].to_broadcast([128, 64])
            DMA(g, "sp", qkg[:, hh, :], src, (), [qkgb[hh]])
        with scope(g) as s2:
            wA = sb(g, s2, [128, DC, 512], BF16, "wA")
            wB = sb(g, s2, [128, DC, 512], BF16, "wB")
            wC = sb(g, s2, [128, DC, 72], BF16, "wC")
            wAb, wBb, wCb = nb(s2, "wA"), nb(s2, "wB"), nb(s2, "wC")
            DMA(g, "pool", wA[:], win_cols(g, l, C_DQ, C_DQ + 512), (), [wAb])
            DMA(g, "pool", wB[:], win_cols(g, l, C_IQ, C_IQ + 512), (), [wBb])
            DMA(g, "pool", wC[:], win_cols(g, l, C_IK, C_IK + 72), (), [wCb])
            tq = [sb(g, s2, [128, 18, 64], F32, "tokq") for _ in range(2)]
            tqb = nbs(s2, "tokq", 2)
            tbq = [sb(g, s2, [128, 18, 64], BF16, "tokb") for _ in range(2)]
            tbb = nbs(s2, "tokb", 2)
            sqt = sb(g, s2, [128, 448], F32, "sqt")
            sqtb = nb(s2, "sqt")
            sm = sb(g, s2, [128, 32], F32, "small")
            smb = nb(s2, "small")
            rt = [sb(g, s2, [128, 18, 8], F32, "ropet") for _ in range(4)]
            rtb = nbs(s2, "ropet", 4)
            for tb in range(NB):
                MS(g, "pool", dvx[:, tb, 64:128], 1.0, [dvb[tb]])
                tk, tkb = tq[tb % 2], tqb[tb % 2]
                tkh, tkhb = tbq[tb % 2], tbb[tb % 2]
                MS(g, "pool", tk[:, 7, :], 0.0, [tkb])
                MS(g, "pool", tk[:, 17, :], 0.0, [tkb])
                psA, pAb = psf(g, "dA", [0, 1])
                psBq, pBb = psf(g, "dB", [2, 3])
                psC, pCb = psf(g, "dC", [4, 5])
                mm_tm(g, psA, 512, wA, wAb, 0, hT, hTb, tb, pAb)
                mm_tm(g, psBq, 512, wB, wBb, 0, hT, hTb, tb, pBb)
                mm_tm(g, psC, 72, wC, wCb, 0, hT, hTb, tb, pCb)
                ACT(g, sqt[:], psA[:, 0:448], AF.Square, [pAb], [sqtb])
                ss = sm[:, 0:7]
                P.op("dve", lambda e, ss=ss: e.tensor_reduce(out=ss, in_=sqt[:].rearrange("p (h d) -> p h d", d=64), axis=AX.X,
                                                             op=ALU.add), [sqtb], [smb])
                ACT(g, ss, ss, AF.Ln, [smb], [smb], scale=1.0 / 64, bias=EPS)
                ACT(g, ss, ss, AF.Exp, [smb], [smb], scale=-0.5)
                TT(g, "dve", tk[:, 0:7, :], psA[:, 0:448].rearrange("p (h d) -> p h d", d=64),
                   ss.unsqueeze(2).to_broadcast([128, 7, 64]), ALU.mult, [pAb, smb], [tkb])
                TT(g, "dve", tk[:, 0:7, :], tk[:, 0:7, :], qkg[:], ALU.mult, [tkb] + qkgb, [tkb])
                aw = sm[:, 8:16]
                TS(g, "dve", sgn[:, tb, :], psC[:, 64:72], 0.0, 2.0, ALU.is_gt, ALU.mult, [pCb], [sgb[tb]])
                TS(g, "dve", sgn[:, tb, :], sgn[:, tb, :], -1.0, 0.0, ALU.add, ALU.add, [sgb[tb]], [sgb[tb]])
                STT(g, aw, psC[:, 64:72], IDX_SCALE, sgn[:, tb, :], ALU.mult, ALU.mult, [pCb, sgb[tb]], [smb])
                TT(g, "dve", tk[:, 8:16, :], psBq[:, :].rearrange("p (h d) -> p h d", d=64),
                   aw.unsqueeze(2).to_broadcast([128, 8, 64]), ALU.mult, [pBb, smb], [tkb])
                CP(g, "act", tk[:, 16, :], psC[:, 0:64], [pCb], [tkb])
                CP(g, "act", dvx[:, tb, 0:64], psA[:, 448:512], [pAb], [dvb[tb]])
                x1, x2 = tk[:, :, 0:8], tk[:, :, 8:16]
                cb_ = g.cs[:, tb, :].unsqueeze(1).to_broadcast([128, 18, 8])
                sb_ = g.sn[:, tb, :].unsqueeze(1).to_broadcast([128, 18, 8])
                TT(g, "dve", rt[0][:], x1, cb_, ALU.mult, [tkb, g.csb], [rtb[0]])
                TT(g, "pool", rt[1][:], x2, sb_, ALU.mult, [tkb, g.snb], [rtb[1]])
                TT(g, "dve", rt[2][:], x2, cb_, ALU.mult, [tkb, g.csb], [rtb[2]])
                TT(g, "pool", rt[3][:], x1, sb_, ALU.mult, [tkb, g.snb], [rtb[3]])
                TT(g, "dve", x1, rt[0][:], rt[1][:], ALU.subtract, [rtb[0], rtb[1]], [tkb])
                TT(g, "pool", x2, rt[2][:], rt[3][:], ALU.add, [rtb[2], rtb[3]], [tkb])
                CP(g, "act", tkh[:], tk[:], [tkb], [tkhb])
                CP(g, "pool", tkh[:, 7, :], tkh[:, 6, :], [tkhb], [tkhb])
                CP(g, "pool", tkh[:, 17, :], tkh[:, 16, :], [tkhb], [tkhb])
                flat = tkh[:].rearrange("p h d -> p (h d)")
                pt, ptb = psb(g)
                MM(g, [trf(pt[:, m * 128:(m + 1) * 128], flat[:, m * 128:(m + 1) * 128], g.ident[:]) for m in range(8)],
                   [tkhb, g.cb], [ptb])
                CP(g, "dve", featT[:, 0:8, tb * 128:(tb + 1) * 128], pt[:, :].rearrange("p (m j) -> p m j", j=128), [ptb], [fb[tb]])
                pt2, ptb2 = psb(g)
                MM(g, [trf(pt2[:, 0:128], flat[:, 1024:1152], g.ident[:])], [tkhb, g.cb], [ptb2])
                CP(g, "act", featT[:, 8, tb * 128:(tb + 1) * 128], pt2[:, 0:128], [ptb2], [fb[tb]])
        sc = [sb(g, st, [128, T], F32, "sc") for _ in range(4)]
        scb = nbs(st, "sc", 4, 4)
        junk = sb(g, st, [128, T], BF16, "junk")
        maskq = [sb(g, st, [128, T], BF16, "maskq") for _ in range(2)]
        mqb = nbs(st, "maskq", 2)
        maskT = sb(g, st, [128, NB, 512], BF16, "maskT")
        mTb = nbs(st, "maskT", 4)
        rj = [sb(g, st, [128, 512], BF16, "rj") for _ in range(4)]
        rjb = nbs(st, "rj", 4)
        dg = [sb(g, st, [128, 8, 128], BF16, "dg") for _ in range(2)]
        dgb = nbs(st, "dg", 2)
        sm = [sb(g, st, [128, 8 + 2 * N_BISECT], F32, "bis") for _ in range(2)]
        smb = nbs(st, "bis", 2)
        cvec = sb(g, st, [128, N_BISECT], F32, "cvec")
        c255 = sb(g, st, [128, 1], F32, "c255")
        cvb = nb(st, "cvec")
        for n_ in range(N_BISECT):
            MS(g, "pool", cvec[:, n_:n_ + 1], 2.0 ** -(n_ + 1), [cvb])
        MS(g, "pool", c255[:], 255.5, [cvb])
        Pt = [sb(g, st, [128, 512], BF16, "Pt") for _ in range(4)]
        Ptb = nbs(st, "Pt", 4)
        Pm = [sb(g, st, [128, 512], BF16, "Pm") for _ in range(4)]
        Pmb = nbs(st, "Pm", 4)
        rs = sb(g, st, [64, 512], F32, "rs")
        rsb = nb(st, "rs")
        cnt_ = {"ri": 0, "pi": 0, "pm": 0}

        def idx_blocks(blocks):
            tiles = []
            for i in blocks:
                nk = (i + 1) * 128
                d_, d_b = dg[i % 2], dgb[i % 2]
                for j in range(8):
                    TS(g, "pool", d_[:, j, :], g.ident[:], sgn[:, i, j:j + 1], None, ALU.mult, None, [g.cb, sgb[i]], [d_b])
                for kc in range((nk + 511) // 512):
                    n = min(512, nk - kc * 512)
                    for j in range(8):
                        tiles.append((i, kc, n, j))
            stt = {}

            def s1(t):
                i, kc, n, j = tiles[t]
                po = 64 * (j % 2)
                psZ, pZb = psf(g, "ixZ", [0, 1, 2])
                stt[t] = [psZ, pZb]
                MM(g, [mmf(psZ[:, 0:n], featT[po:po + 64, 4 + j // 2, i * 128:(i + 1) * 128],
                           featT[po:po + 64, 8, kc * 512:kc * 512 + n], True, True)],
                   [fb[i]] + fb[kc * 4:(kc * 512 + n) // 128], [pZb])

            def s2(t):
                i, kc, n, j = tiles[t]
                psZ, pZb = stt[t]
                r_, r_b = rj[cnt_["ri"] % 4], rjb[cnt_["ri"] % 4]
                cnt_["ri"] += 1
                stt[t] += [r_, r_b]
                ACT(g, r_[:, 0:n], psZ[:, 0:n], AF.Relu, [pZb], [r_b])

            def s3(t):
                i, kc, n, j = tiles[t]
                r_, r_b = stt[t][2], stt[t][3]
                if j == 0:
                    cnt_["psS"] = psf(g, "ixS", [3, 4])
                psS, pSb = cnt_["psS"]
                d_, d_b = dg[i % 2], dgb[i % 2]
                MM(g, [mmf(psS[:, 0:n], d_[:, j, :], r_[:, 0:n], j == 0, j == 7)], [d_b, r_b], [pSb])
                if j == 7:
                    CP(g, "act", sc[i % 4][:, kc * 512:kc * 512 + n], psS[:, 0:n], [pSb], [scb[i % 4][kc]])
                del stt[t]

            pipeline([s1, s2, s3], len(tiles))

        def bis_pair(p):
            blocks = [2 * p, 2 * p + 1]
            st_ = []
            for bi, i in enumerate(blocks):
                nk = (i + 1) * 128
                s_, s_b = sc[i % 4], scb[i % 4]
                nkc = (nk + 511) // 512
                srd = s_b[0:nkc]
                m_, m_b = sm[bi], smb[bi]
                rmax, rmin, step0, mid, cntv, tt = (m_[:, c:c + 1] for c in range(6))
                stepc = m_[:, 8:8 + N_BISECT]
                if nk > 256:
                    P.op("dve", lambda e, s_=s_, nk=nk, rmax=rmax: e.tensor_reduce(out=rmax, in_=s_[:, 0:nk], axis=AX.X, op=ALU.max), srd, [m_b], nk)
                    P.op("dve", lambda e, s_=s_, nk=nk, rmin=rmin: e.tensor_reduce(out=rmin, in_=s_[:, 0:nk], axis=AX.X, op=ALU.min), srd, [m_b], nk)
                dsl = s_[:, i * 128:(i + 1) * 128]
                TT(g, "pool", dsl, dsl, g.caus01[:], ALU.mult, [s_b[i // 4], g.cb], [s_b[i // 4]])
                TT(g, "pool", dsl, dsl, g.negfill[:], ALU.add, [s_b[i // 4], g.cb], [s_b[i // 4]])
                st_.append((i, nk, s_, srd, m_, m_b, rmax, rmin, step0, mid, cntv, tt, stepc))
            act = [x for x in st_ if x[1] > 256]
            for (i, nk, s_, srd, m_, m_b, rmax, rmin, step0, mid, cntv, tt, stepc) in act:
                TT(g, "dve", step0, rmax, rmin, ALU.subtract, [m_b], [m_b])
            for (i, nk, s_, srd, m_, m_b, rmax, rmin, step0, mid, cntv, tt, stepc) in act:
                TS(g, "dve", stepc, cvec[:], step0, None, ALU.mult, None, [m_b, cvb], [m_b])
            for (i, nk, s_, srd, m_, m_b, rmax, rmin, step0, mid, cntv, tt, stepc) in act:
                TS(g, "dve", mid, stepc[:, 0:1], rmin, g.zc[:, 0:1], ALU.add, ALU.add, [m_b, g.cb], [m_b])
            for n_ in range(N_BISECT):
                for (i, nk, s_, srd, m_, m_b, rmax, rmin, step0, mid, cntv, tt, stepc) in act:
                    TS(g, "dve", junk[:, 0:nk], s_[:, 0:nk], mid, g.zc[:, 0:1], ALU.is_ge, ALU.add, srd + [m_b, g.cb], [m_b], accum_out=cntv)
                for (i, nk, s_, srd, m_, m_b, rmax, rmin, step0, mid, cntv, tt, stepc) in act:
                    TS(g, "dve", tt, cntv, c255[:, 0:1], stepc[:, n_:n_ + 1], ALU.is_ge, ALU.mult, [m_b, cvb], [m_b])
                for (i, nk, s_, srd, m_, m_b, rmax, rmin, step0, mid, cntv, tt, stepc) in act:
                    nn = min(n_ + 1, N_BISECT - 1)
                    TS(g, "dve", mid, tt, stepc[:, nn:nn + 1], mid, ALU.subtract, ALU.add, [m_b], [m_b])
            for (i, nk, s_, srd, m_, m_b, rmax, rmin, step0, mid, cntv, tt, stepc) in st_:
                thr = mid if nk > 256 else g.negbig[:, 0:1]
                mq, mq_b = maskq[i % 2], mqb[i % 2]
                TS(g, "dve", mq[:, 0:nk], s_[:, 0:nk], thr, None, ALU.is_ge, None, srd + [m_b, g.cb], [mq_b])

        def mT_pair(p):
            for i in (2 * p, 2 * p + 1):
                ii = i % 4
                mq, mq_b = maskq[i % 2], mqb[i % 2]
                for k0 in range(0, i + 1, 8):
                    k1 = min(i + 1, k0 + 8)
                    pt, ptb = psb(g)
                    MM(g, [trf(pt[:, (kb - k0) * 128:(kb - k0 + 1) * 128], mq[:, kb * 128:(kb + 1) * 128], g.ident[:])
                           for kb in range(k0, k1)], [mq_b, g.cb], [ptb])
                    CP(g, "act", maskT[:, k0:k1, ii * 128:(ii + 1) * 128],
                       pt[:, 0:(k1 - k0) * 128].rearrange("p (m j) -> p m j", j=128), [ptb], [mTb[ii]])

        def att_chunk(qc):
            last = 4 * qc + 3
            tiles = [(h, kb) for h in range(6) for kb in range(last + 1)]
            stt = {}
            pso = {}

            def s1(t):
                h, kb = tiles[t]
                hp, po = h // 2, 64 * (h % 2)
                j0 = max(0, kb * 128 - qc * 512)
                psL, pLb = psf(g, "dsL", [0, 1, 2])
                stt[t] = [psL, pLb]
                MM(g, [mmf(psL[:, j0:512], featT[po:po + 64, 3, kb * 128:(kb + 1) * 128],
                           featT[po:po + 64, hp, qc * 512 + j0:(qc + 1) * 512], True, True)],
                   [fb[kb]] + fb[qc * 4:qc * 4 + 4], [pLb])

            def s2(t):
                h, kb = tiles[t]
                j0 = max(0, kb * 128 - qc * 512)
                psL, pLb = stt[t][0], stt[t][1]
                pi = cnt_["pi"]
                cnt_["pi"] += 1
                p_, p_b = Pt[pi % 4], Ptb[pi % 4]
                stt[t] += [p_, p_b]
                ACT(g, p_[:, j0:512], psL[:, j0:512], AF.Exp, [pLb], [p_b], scale=0.125)

            def s3(t):
                h, kb = tiles[t]
                j0 = max(0, kb * 128 - qc * 512)
                p_, p_b = stt[t][2], stt[t][3]
                pm = cnt_["pm"]
                cnt_["pm"] += 1
                m_, m_b = Pm[pm % 4], Pmb[pm % 4]
                stt[t] += [m_, m_b]
                TT(g, "pool", m_[:, j0:512], p_[:, j0:512], maskT[:, kb, j0:512], ALU.mult, [p_b] + mTb[j0 // 128:4], [m_b])

            def s4(t):
                h, kb = tiles[t]
                hp, po = h // 2, 64 * (h % 2)
                j0 = max(0, kb * 128 - qc * 512)
                m_, m_b = stt[t][4], stt[t][5]
                if kb == 0:
                    pso[h] = psf(g, "dsO", [4, 5])
                psO, pOb = pso[h]
                MM(g, [mmf(psO[:, j0:512], dvx[:, kb, :], m_[:, j0:512], kb == 0, kb == last)], [dvb[kb], m_b], [pOb])
                if kb == last:
                    ACT(g, rs[0:64, :], psO[64:128, :], AF.Ln, [pOb], [rsb])
                    ACT(g, rs[0:64, :], rs[0:64, :], AF.Exp, [rsb], [rsb], scale=-1.0)
                    TT(g, "dve", odT[po:po + 64, hp, qc * 512:(qc + 1) * 512], psO[0:64, :], rs[0:64, :], ALU.mult, [pOb, rsb],
                       [odb[hp][qc]])
                del stt[t]

            pipeline([s1, s2, s3, s4], len(tiles))

        idx_blocks([0, 1])
        for p in range(8):
            if p + 1 < 8:
                idx_blocks([2 * p + 2, 2 * p + 3])
            bis_pair(p)
            mT_pair(p)
            if p % 2 == 1:
                att_chunk(p // 2)


def phase_hgrn(g, l, hT, hTb, ohT, ohb):
    P = g.P
    import os
    if int(os.environ.get("HGL", "9")) == 0:
        return
    with scope(g) as st:
        hi_tm = sb(g, st, [128, NB, 256], BF16, "hi_tm")
        hib = nbs(st, "hi", NB)
        hgs = sb(g, st, [128, NB, 256], BF16, "hgs")
        hgb = nbs(st, "hgs", NB)
        onb = sb(g, st, [128, 64], F32, "onorm")
        onbb = nb(st, "onorm")
        DMA(g, "sp", onb[:], g.onorm[l:l + 1, :].to_broadcast([128, 64]), (), [onbb])
        with scope(g) as s2:
            w = sb(g, s2, [128, DC, 512], BF16, "whihg")
            wb = nb(s2, "whihg")
            DMA(g, "pool", w[:], win_cols(g, l, C_HI, C_HI + 512), (), [wb])
            sgs = [sb(g, s2, [128, 256], F32, "sgs") for _ in range(2)]
            sgsb = nbs(s2, "sgs", 2)
            for tb in range(NB):
                ps, pb = psf(g, "proj", [0, 1, 2, 3, 4, 5])
                hgv = int(os.environ.get("HGV", "15"))
                if hgv & 8:
                    mm_tm(g, ps, 512, w, wb, 0, hT, hTb, tb, pb)
                if hgv & 1:
                    CP(g, "dve", hi_tm[:, tb, :], ps[:, 0:256], [pb], [hib[tb]])
                sgt, sgtb = sgs[tb % 2], sgsb[tb % 2]
                if hgv & 2:
                    ACT(g, sgt[:], ps[:, 256:512], AF.Exp if hgv & 16 else AF.Sigmoid, [pb], [sgtb])
                if hgv & 4:
                    TT(g, "dve", hgs[:, tb, :], ps[:, 256:512], sgt[:], ALU.mult, [pb, sgtb], [hgb[tb]])
        with scope(g) as s3:
            NH = 4
            R = 4
            qtT = sb(g, s3, [128, NH, T], BF16, "qtT")
            ktT = sb(g, s3, [128, NH, T], BF16, "ktT")
            qtb = nbs(s3, "qt", NH)
            ktb = nbs(s3, "kt", NH)
            kt_tm = sb(g, s3, [128, NB, NH * 128], BF16, "kt_tm")
            kttb = nbs(s3, "kttm", NH)
            t1 = sb(g, s3, [128, T], F32, "t1")
            t2 = sb(g, s3, [128, T], F32, "t2")
            t3 = sb(g, s3, [128, T], F32, "t3")
            t1b, t2b, t3b = nb(s3, "t1"), nb(s3, "t2"), nb(s3, "t3")
            ebl = sb(g, s3, [128, NH, 32], F32, "ebl")
            eblb = nb(s3, "ebl")
            W = [sb(g, s3, [128, NH, 64], F32, "W") for _ in range(2)]
            Wb = nbs(s3, "W", 2)
            Sbf = [sb(g, s3, [128, NH, 64], BF16, "Sbf") for _ in range(R)]
            Sbfb = nbs(s3, "Sbf", R)
            attm = [sb(g, s3, [128, NH, 128], BF16, "attm") for _ in range(3)]
            attb = nbs(s3, "attm", 3)
            o_tm = [sb(g, s3, [128, NH * 64], F32, "o_tm") for _ in range(3)]
            otb = nbs(s3, "otm", 3)
            osq = sb(g, s3, [128, NH * 64], F32, "osq")
            osqb = nb(s3, "osq")
            og = [sb(g, s3, [128, NH * 64], BF16, "og") for _ in range(2)]
            ogb = nbs(s3, "og", 2)
            sm = [sb(g, s3, [128, 4], F32, "hsm") for _ in range(2)]
            smb = nbs(s3, "hsm", 2)
            wq = [sb(g, s3, [128, DC, 128], BF16, "wq") for _ in range(2)]
            wqb = nbs(s3, "wq", 2)
            wf = [sb(g, s3, [128, DC, 128], BF16, "wf") for _ in range(2)]
            wfb = nbs(s3, "wf", 2)

            def load_head(hd):
                DMA(g, "pool", wf[hd % 2][:], win_cols(g, l, C_HF + hd * 128, C_HF + hd * 128 + 128), (), [wfb[hd % 2]])
                DMA(g, "pool", wq[hd % 2][:], win_cols(g, l, C_HQ + hd * 128, C_HQ + hd * 128 + 128), (), [wqb[hd % 2]])

            load_head(0)
            for hd in range(NH):
                if hd + 1 < NH:
                    load_head(hd + 1)
                w_f, w_fb, w_q, w_qb = wf[hd % 2], wfb[hd % 2], wq[hd % 2], wqb[hd % 2]
                for tc in range(4):
                    ps, pb = psf(g, "proj", [0, 1, 2, 3, 4, 5])
                    mm_fm(g, ps, 128, 512, w_f, w_fb, 0, hT, hTb, tc * 512, pb)
                    ACT(g, t1[:, tc * 512:(tc + 1) * 512], ps[:, :], AF.Sigmoid, [pb], [t1b])
                TS(g, "dve", t1[:], t1[:], g.oml[:, l, hd:hd + 1], g.lbv[:, l, hd:hd + 1], ALU.mult, ALU.add, [t1b, g.cb], [t1b])
                ACT(g, t2[:], t1[:], AF.Copy, [t1b], [t2b], scale=-1.0, bias=1.0)
                TS(g, "dve", t1[:], t1[:], F_MIN, None, ALU.max, None, [t1b], [t1b])
                ACT(g, t1[:], t1[:], AF.Ln, [t1b], [t1b])
                P.op("dve", lambda e: e.tensor_tensor_scan(out=t3[:], data0=g.resetm[:], data1=t1[:], initial=0.0,
                                                           op0=ALU.mult, op1=ALU.add), [t1b, g.cb], [t3b], 2 * T)
                TS(g, "dve", t3[:], t3[:], -80.0, None, ALU.max, None, [t3b], [t3b])
                ACT(g, t1[:], t3[:], AF.Exp, [t3b], [t1b])
                CP(g, "pool", ebl[:, hd, :].unsqueeze(2), t1[:].rearrange("p (c j) -> p c j", j=64)[:, :, 63:64], [t1b], [eblb])
                ACT(g, t3[:], t3[:], AF.Exp, [t3b], [t3b], scale=-1.0)
                TT(g, "dve", ktT[:, hd, :], t2[:], t3[:], ALU.mult, [t2b, t3b], [ktb[hd]])
                for tc in range(4):
                    ps, pb = psf(g, "proj", [0, 1, 2, 3, 4, 5])
                    mm_fm(g, ps, 128, 512, w_q, w_qb, 0, hT, hTb, tc * 512, pb)
                    ACT(g, t2[:, tc * 512:(tc + 1) * 512], ps[:, :], AF.Sigmoid, [pb], [t2b])
                    TT(g, "dve", t2[:, tc * 512:(tc + 1) * 512], ps[:, :], t2[:, tc * 512:(tc + 1) * 512], ALU.mult, [pb, t2b], [t2b])
                TT(g, "dve", qtT[:, hd, :], t2[:], t1[:], ALU.mult, [t2b, t1b], [qtb[hd]])
                for k0 in (0, 8):
                    pt, ptb = psb(g)
                    MM(g, [trf(pt[:, m * 128:(m + 1) * 128], ktT[:, hd, (k0 + m) * 128:(k0 + m + 1) * 128], g.ident[:])
                           for m in range(8)], [ktb[hd], g.cb], [ptb])
                    CP(g, "act" if k0 == 0 else "dve", kt_tm[:, k0:k0 + 8, hd * 128:(hd + 1) * 128],
                       pt[:, :].rearrange("p (m j) -> p m j", j=128), [ptb], [kttb[hd]])

            NCH = 32
            xps = {}

            def emit_X(c):
                tb, pr = c // 2, (c % 2) * 64
                psX, pXb = psf(g, "hgX", [0, 1, 2])
                xps[c] = (psX, pXb)
                MM(g, [mmf(psX[:, hh * 64:(hh + 1) * 64], kt_tm[pr:pr + 64, tb, hh * 128:(hh + 1) * 128],
                           hi_tm[pr:pr + 64, tb, hh * 64:(hh + 1) * 64], True, True) for hh in range(NH)], kttb + [hib[tb]], [pXb])

            def emit_A(tb):
                psA, pAb = psf(g, "hgA", [3])
                MM(g, [mmf(psA[:, hh * 128:(hh + 1) * 128], ktT[:, hh, tb * 128:(tb + 1) * 128],
                           qtT[:, hh, tb * 128:(tb + 1) * 128], True, True) for hh in range(NH)], ktb + qtb, [pAb])
                TT(g, "dve", attm[tb % 3][:], psA[:, :].rearrange("p (h t) -> p h t", t=128),
                   g.maskbd[:].unsqueeze(1).to_broadcast([128, NH, 128]), ALU.mult, [pAb, g.cb], [attb[tb % 3]])

            emit_X(0)
            emit_X(1)
            emit_A(0)
            CP(g, "dve", W[0][:], xps[0][0][:, 0:NH * 64].rearrange("p (h v) -> p h v", v=64), [xps[0][1]], [Wb[0]])
            MS(g, "pool", Sbf[0][:], 0.0, [Sbfb[0]])
            for c in range(NCH):
                tb, half = c // 2, c % 2
                pr = half * 64
                if c + 2 < NCH:
                    emit_X(c + 2)
                if half == 0 and tb + 1 < NB:
                    emit_A(tb + 1)
                if c + 1 < NCH:
                    eb_ = ebl[:, :, c:c + 1].to_broadcast([128, NH, 64])
                    TT(g, "pool", Sbf[(c + 1) % R][:], W[c % 2][:], eb_, ALU.mult, [Wb[c % 2], eblb], [Sbfb[(c + 1) % R]])
                    psX, pXb = xps.pop(c + 1)
                    for hh in range(NH):
                        STT(g, W[(c + 1) % 2][:, hh, :], W[c % 2][:, hh, :], ebl[:, hh, c:c + 1], psX[:, hh * 64:(hh + 1) * 64],
                            ALU.mult, ALU.add, [Wb[c % 2], eblb, pXb], [Wb[(c + 1) % 2]])
                am, amb = attm[tb % 3], attb[tb % 3]
                ot, otbuf = o_tm[tb % 3], otb[tb % 3]
                psO, pOb = psf(g, "hgO", [4, 5])
                fns = []
                for hh in range(NH):
                    fns.append(mmf(psO[0:64, hh * 64:(hh + 1) * 64], am[pr:pr + 64, hh, pr:pr + 64],
                                   hi_tm[pr:pr + 64, tb, hh * 64:(hh + 1) * 64], True, False))
                    fns.append(mmf(psO[0:64, hh * 64:(hh + 1) * 64], qtT[:, hh, c * 64:(c + 1) * 64], Sbf[c % R][:, hh, :], False, True))
                MM(g, fns, [amb, hib[tb], Sbfb[c % R]] + qtb, [pOb])
                CP(g, "act", ot[pr:pr + 64, :], psO[0:64, 0:NH * 64], [pOb], [otbuf])
                if half == 1:
                    sm_, sm_b = sm[tb % 2], smb[tb % 2]
                    og_, og_b = og[tb % 2], ogb[tb % 2]
                    TT(g, "pool", osq[:], ot[:], ot[:], ALU.mult, [otbuf], [osqb])
                    ss = sm_[:, 0:NH]
                    P.op("dve", lambda e, ss=ss: e.tensor_reduce(out=ss, in_=osq[:].rearrange("p (h d) -> p h d", d=64), axis=AX.X,
                                                                 op=ALU.add), [osqb], [sm_b], NH * 64)
                    ACT(g, ss, ss, AF.Ln, [sm_b], [sm_b], scale=1.0 / 64, bias=EPS)
                    ACT(g, ss, ss, AF.Exp, [sm_b], [sm_b], scale=-0.5)
                    o3 = ot[:].rearrange("p (h d) -> p h d", d=64)
                    TT(g, "dve", o3, o3, ss.unsqueeze(2).to_broadcast([128, NH, 64]), ALU.mult, [otbuf, sm_b], [otbuf])
                    TT(g, "dve", o3, o3, onb[:].unsqueeze(1).to_broadcast([128, NH, 64]), ALU.mult, [otbuf, onbb], [otbuf])
                    TT(g, "dve", og_[:], ot[:], hgs[:, tb, :], ALU.mult, [otbuf, hgb[tb]], [og_b])
                    pt, ptb = psb(g)
                    MM(g, [trf(pt[:, m * 128:(m + 1) * 128], og_[:, m * 128:(m + 1) * 128], g.ident[:]) for m in range(2)],
                       [og_b, g.cb], [ptb])
                    CP(g, "act", ohT[:, 0:2, tb * 128:(tb + 1) * 128], pt[:, 0:256].rearrange("p (m j) -> p m j", j=128), [ptb],
                       [ohb[0][tb // 4], ohb[1][tb // 4]])


def phase_mix(g, l, hT, hTb, osbT, osbb, odT, odb, ohT, ohb, src_ap, src_bufs):
    P = g.P
    with scope(g) as st:
        mixT = sb(g, st, [128, DC, T], BF16, "mixT")
        mxb = nbs(st, "mix", DC, 4)
        wg = [sb(g, st, [128, DC, 3, 128], BF16, "wg") for _ in range(2)]
        wgb = nbs(st, "wg", 2, 3)
        wy = [sb(g, st, [128, 8, 128], BF16, "wy") for _ in range(2)]
        wyb = nbs(st, "wy", 2, 3)
        sg = [sb(g, st, [128, 512], F32, "sg") for _ in range(2)]
        sgb = nbs(st, "sg", 2)
        acc = [sb(g, st, [128, 512], F32, "acc") for _ in range(2)]
        accb = nbs(st, "acc", 2)
        tm = [sb(g, st, [128, 512], F32, "tm") for _ in range(2)]
        tmb = nbs(st, "tm", 2)
        k = 0

        def load_dc(dc):
            w_, w_b = wg[dc % 2], wgb[dc % 2]
            y_, y_b = wy[dc % 2], wyb[dc % 2]
            for gi in range(3):
                c0 = C_G + gi * 1024 + dc * 128
                DMA(g, "pool", w_[:, :, gi, :], win_cols(g, l, c0, c0 + 128), (), [w_b[gi]])
            DMA(g, "pool", y_[:, 0:3, :], g.w_sb[l].rearrange("(c p) n -> p c n", p=128)[:, :, dc * 128:(dc + 1) * 128], (), [y_b[0]])
            DMA(g, "pool", y_[:, 3:6, :], g.w_dsa[l].rearrange("(c p) n -> p c n", p=128)[:, :, dc * 128:(dc + 1) * 128], (), [y_b[1]])
            DMA(g, "pool", y_[:, 6:8, :], g.w_hg[l].rearrange("(c p) n -> p c n", p=128)[:, :, dc * 128:(dc + 1) * 128], (), [y_b[2]])

        load_dc(0)
        wo = sb(g, st, [128, DC, D], BF16, "wo")
        wob = nbs(st, "wo", 2)
        for dc in range(DC):
            w_, w_b = wg[dc % 2], wgb[dc % 2]
            y_, y_b = wy[dc % 2], wyb[dc % 2]
            if dc + 1 < DC:
                load_dc(dc + 1)
            else:
                for nh in range(2):
                    DMA(g, "pool", wo[:, :, nh * 512:(nh + 1) * 512],
                        g.w_out[l].rearrange("(c p) n -> p c n", p=128)[:, :, nh * 512:(nh + 1) * 512], (), [wob[nh]])
            for tc in range(4):
                a_, a_b = acc[k % 2], accb[k % 2]
                for gi, (oT, obufs, nch, c0) in enumerate(((osbT, osbb, 3, 0), (odT, odb, 3, 3), (ohT, ohb, 2, 6))):
                    psG, pGb = psf(g, "mxG", [0, 1, 2])
                    MM(g, [mmf(psG[:, :], w_[:, c, gi, :], hT[:, c, tc * 512:(tc + 1) * 512], c == 0, c == DC - 1) for c in range(DC)],
                       [w_b[gi]] + hTb[tc * 4:tc * 4 + 4], [pGb])
                    s_, s_b = sg[(k * 3 + gi) % 2], sgb[(k * 3 + gi) % 2]
                    ACT(g, s_[:], psG[:, :], AF.Sigmoid, [pGb], [s_b])
                    psY, pYb = psf(g, "mxY", [3, 4, 5])
                    MM(g, [mmf(psY[:, :], y_[:, c0 + c, :], oT[:, c, tc * 512:(tc + 1) * 512], c == 0, c == nch - 1) for c in range(nch)],
                       [y_b[gi]] + [obufs[c][tc] for c in range(nch)], [pYb])
                    if gi == 0:
                        TT(g, "dve", a_[:], psY[:, :], s_[:], ALU.mult, [pYb, s_b], [a_b])
                    else:
                        t_, t_b = tm[gi % 2], tmb[gi % 2]
                        TT(g, "dve", t_[:], psY[:, :], s_[:], ALU.mult, [pYb, s_b], [t_b])
                        if gi == 1:
                            TT(g, "pool", a_[:], a_[:], t_[:], ALU.add, [a_b, t_b], [a_b])
                        else:
                            TT(g, "pool", mixT[:, dc, tc * 512:(tc + 1) * 512], a_[:], t_[:], ALU.add, [a_b, t_b], [mxb[dc][tc]])
                k += 1
        xbs = [sb(g, st, [128, D], F32, "xb") for _ in range(4)]
        xbb = nbs(st, "xb", 4)
        nctx = norm_setup(g, st, g.norm_mlp[l:l + 1, :])
        for tb in range(NB):
            xb, xbuf = xbs[tb % 4], xbb[tb % 4]
            DMA(g, "sp", xb[:], src_ap[tb * 128:(tb + 1) * 128, :], [src_bufs[tb]], [xbuf])
            for nh in range(2):
                ps, pb = psf(g, "mxO", [0, 1, 2, 3, 4, 5])
                MM(g, [mmf(ps[:, :], mixT[:, c, tb * 128:(tb + 1) * 128], wo[:, c, nh * 512:(nh + 1) * 512], c == 0, c == DC - 1)
                       for c in range(DC)], [wob[nh]] + [mxb[c][tb // 4] for c in range(DC)], [pb])
                xs = xb[:, nh * 512:(nh + 1) * 512]
                TT(g, "dve", xs, ps[:, :], xs, ALU.add, [pb, xbuf], [xbuf])
            DMA(g, "sp", g.xres_d[tb * 128:(tb + 1) * 128, :], xb[:], [xbuf], [g.xres_b[tb]])
            norm_block(g, nctx, xb, xbuf, tb, hT, hTb)


def phase_ffn(g, l, hT, hTb, dst_ap, dst_bufs):
    for half in range(2):
        last = half == 1
        with scope(g) as st:
            uT = sb(g, st, [128, 16, T], BF16, "uT")
            ub = nbs(st, "uT", 16, 4)
            wd = sb(g, st, [128, 16, D], BF16, "wd")
            wdb = nbs(st, "wd", 2)
            wdv = g.w_down[l].rearrange("(f p) n -> p f n", p=128)
            wu = [sb(g, st, [128, DC, 512], BF16, "wu") for _ in range(2)]
            wub = nbs(st, "wu", 2)
            rt = [sb(g, st, [128, 512], BF16, "rt") for _ in range(2)]
            rtb = nbs(st, "rt", 2)
            k = 0

            def load_wu(g4):
                c0 = half * 2048 + g4 * 512
                DMA(g, "pool", wu[g4 % 2][:], g.w_up[l].rearrange("(c p) n -> p c n", p=128)[:, :, c0:c0 + 512], (), [wub[g4 % 2]])

            load_wu(0)
            for g4 in range(4):
                w_, w_b = wu[g4 % 2], wub[g4 % 2]
                if g4 + 1 < 4:
                    load_wu(g4 + 1)
                if g4 == 1:
                    for nh in range(2):
                        DMA(g, "pool", wd[:, :, nh * 512:(nh + 1) * 512], wdv[:, half * 16:(half + 1) * 16, nh * 512:(nh + 1) * 512], (),
                            [wdb[nh]])
                for fcl in range(4):
                    fc = g4 * 4 + fcl
                    for tc in range(4):
                        ps, pb = psf(g, "proj", [0, 1, 2, 3, 4, 5])
                        mm_fm(g, ps, 128, 512, w_, w_b, fcl * 128, hT, hTb, tc * 512, pb)
                        r_, r_b = rt[k % 2], rtb[k % 2]
                        k += 1
                        ACT(g, r_[:], ps[:, :], AF.Relu, [pb], [r_b])
                        TT(g, "pool", uT[:, fc, tc * 512:(tc + 1) * 512], r_[:], r_[:], ALU.mult, [r_b], [ub[fc][tc]])
            xbs = [sb(g, st, [128, D], F32, "xb") for _ in range(4)]
            xbb = nbs(st, "xb", 4)
            nctx = norm_setup(g, st, g.norm_mix[l + 1:l + 2, :]) if (last and l + 1 < DEPTH) else None
            for tb in range(NB):
                xb, xbuf = xbs[tb % 4], xbb[tb % 4]
                DMA(g, "sp", xb[:], g.xres_d[tb * 128:(tb + 1) * 128, :], [g.xres_b[tb]], [xbuf])
                for nh in range(2):
                    ps, pb = psf(g, "proj", [0, 1, 2, 3, 4, 5])
                    MM(g, [mmf(ps[:, :], uT[:, fc, tb * 128:(tb + 1) * 128], wd[:, fc, nh * 512:(nh + 1) * 512], fc == 0, fc == 15)
                           for fc in range(16)], [wdb[nh]] + [ub[fc][tb // 4] for fc in range(16)], [pb])
                    xs = xb[:, nh * 512:(nh + 1) * 512]
                    TT(g, "dve", xs, ps[:, :], xs, ALU.add, [pb, xbuf], [xbuf])
                if last:
                    DMA(g, "sp", dst_ap[tb * 128:(tb + 1) * 128, :], xb[:], [xbuf], [dst_bufs[tb]])
                    if nctx is not None:
                        norm_block(g, nctx, xb, xbuf, tb, hT, hTb)
                else:
                    DMA(g, "sp", g.xres_d[tb * 128:(tb + 1) * 128, :], xb[:], [xbuf], [g.xres_b[tb]])


def dump_t(g, name, t, ncol):
    if g.dump == name:
        g.P.barrier()
        DMA(g, "sp", g.dbg_d[:, 0:ncol], t, [], [g.dbgb])
        g.P.wait_bufs("sp", [g.dbgb])
        g.P.barrier()


def build_layer(g, l):
    if l > 0 and g.stage < 7:
        return
    src_ap, src_bufs = (g.x_d, g.xin_b) if l == 0 else (g.xres_d, g.xres_b)
    with scope(g) as ls:
        hT, hTb = g.hT, g.hTb
        if l == 0:
            with scope(g) as st:
                norm_T(g, st, src_ap, src_bufs, g.norm_mix[l:l + 1, :], hT, hTb)
        if l == 0:
            dump_t(g, "hT", hT[:].rearrange("p c t -> p (c t)"), 8 * T)
        if g.stage < 2:
            return
        with scope(g) as ms:
            osbT = sb(g, ms, [128, 3, T], BF16, "osbT")
            odT = sb(g, ms, [128, 3, T], BF16, "odT")
            ohT = sb(g, ms, [128, 2, T], BF16, "ohT")
            osbb = nbs(ms, "osb", 3, 4)
            odb = nbs(ms, "od", 3, 4)
            ohb = nbs(ms, "oh", 2, 4)
            phase_sb(g, l, hT, hTb, osbT, osbb)
            if l == 0:
                dump_t(g, "osbT", osbT[:].rearrange("p c t -> p (c t)"), 3 * T)
            if g.stage < 3:
                return
            phase_dsa(g, l, hT, hTb, odT, odb)
            if l == 0:
                dump_t(g, "odT", odT[:].rearrange("p c t -> p (c t)"), 3 * T)
            if g.stage < 4:
                return
            phase_hgrn(g, l, hT, hTb, ohT, ohb)
            if l == 0:
                dump_t(g, "ohT", ohT[:].rearrange("p c t -> p (c t)"), 2 * T)
            if g.stage < 5:
                return
            phase_mix(g, l, hT, hTb, osbT, osbb, odT, odb, ohT, ohb, src_ap, src_bufs)
        if g.stage < 6:
            return
        if l == DEPTH - 1:
            phase_ffn(g, l, hT, hTb, g.out_d, g.out_b)
        else:
            phase_ffn(g, l, hT, hTb, g.xres_d, g.xres_b)


_NC_CACHE = {}


def rope_tables():
    half = 8
    inv = 500000.0 ** (-(np.arange(half, dtype=np.float32) * 2.0) / 16.0)
    ang = np.arange(T, dtype=np.float32)[:, None] * inv[None, :].astype(np.float32)
    return np.cos(ang).astype(np.float32), np.sin(ang).astype(np.float32)


def kernel(x, norm_mix, w_in, qn_dsa, kn_dsa, hgrn_lb, hgrn_onorm, w_br_sb, w_br_dsa, w_br_hgrn, w_out, norm_mlp, w_up, w_down):
    if "nc" not in _NC_CACHE:
        _NC_CACHE["nc"] = build_two_pass()
    nc = _NC_CACHE["nc"]
    f = lambda a: np.ascontiguousarray(np.asarray(a, dtype=np.float32))
    cs, sn = rope_tables()
    shared = dict(norm_mix=f(norm_mix), w_in=f(w_in), qn_dsa=f(qn_dsa), kn_dsa=f(kn_dsa), hgrn_lb=f(hgrn_lb),
                  hgrn_onorm=f(hgrn_onorm), w_br_sb=f(w_br_sb), w_br_dsa=f(w_br_dsa), w_br_hgrn=f(w_br_hgrn),
                  w_out=f(w_out), norm_mlp=f(norm_mlp), w_up=f(w_up), w_down=f(w_down), rope_cos=cs, rope_sin=sn)
    xs = f(x)
    in_maps = [dict(shared, x=xs[b]) for b in range(8)]
    res = run_bass_kernel_spmd(nc, in_maps, core_ids=list(range(8)))
    return np.stack([np.asarray(r["out"], dtype=np.float32) for r in res.results], axis=0)
```

```python
import math
import numpy as np
from contextlib import ExitStack, contextmanager
import concourse.bass as bass
import concourse.mybir as mybir
from concourse.bass_utils import run_bass_kernel_spmd

F32 = mybir.dt.float32
BF16 = mybir.dt.bfloat16
AF = mybir.ActivationFunctionType
ALU = mybir.AluOpType
AX = mybir.AxisListType

T = 2048
D = 1024
NB = 16
DC = 8
DIN = 6856
DFF = 4096
DEPTH = 2
EPS = 1e-6
F_MIN = 1e-12
IDX_SCALE = (64 * 8) ** -0.5
C_SQ, C_SK, C_SV = 0, 384, 768
C_DQ, C_DK, C_DV = 1152, 1536, 1600
C_IQ, C_IK, C_IW = 1664, 2176, 2240
C_HQ, C_HF, C_HI, C_HG = 2248, 2760, 3272, 3528
C_G = 3784
N_BISECT = 16


class Buf:
    __slots__ = ("name", "w", "r", "dsem", "excl")

    def __init__(self, name, excl=False):
        self.name = name
        self.w = None
        self.r = {}
        self.dsem = None
        self.excl = excl


class Prog:
    ENG = ("pe", "act", "dve", "pool", "sp")
    CLEAR_NS = 330.0
    FILL_NS = {"dve": 66.0, "act": 190.0, "pool": 125.0}
    EST = {"dve": (60.0, 0.26), "act": (185.0, 0.83), "pool": (120.0, 0.8), "pe": (0.0, 0.0), "sp": (0.0, 0.0)}

    def __init__(self, nc, stack, needed=None):
        self.needed = needed
        self.used = set()
        self.remap = {}
        self.sig = {}
        self.fill = {}
        self.nc = nc
        self.stack = stack
        self.eng = {"pe": nc.tensor, "act": nc.scalar, "dve": nc.vector, "pool": nc.gpsimd, "sp": nc.sync}
        self.cnt = {e: 0 for e in self.ENG}
        self.known = {e: {} for e in self.ENG}
        self.sems = {}
        self.semval = {}
        for e in ("pe", "act", "dve", "pool"):
            self.sems["E_" + e] = stack.enter_context(nc.semaphore("sem_" + e))
            self.semval["E_" + e] = 0
        self.ndsem = 0
        self.free_dsems = []
        self.nwaits = 0
        self.tcum = {e: 0.0 for e in self.ENG}
        self.tend = {e: {} for e in self.ENG}

    def _dsem(self, buf):
        if buf.dsem is None:
            if self.free_dsems:
                key = self.free_dsems.pop()
            else:
                key = "D%d" % self.ndsem
                self.ndsem += 1
                self.sems[key] = self.stack.enter_context(self.nc.semaphore("dsem%d" % (self.ndsem - 1)))
                self.semval[key] = 0
            buf.dsem = key
        return buf.dsem

    def release(self, bufs):
        for b in bufs:
            if b.dsem is not None:
                self.free_dsems.append(b.dsem)
                b.dsem = None

    def _waits(self, eng, deps):
        need = {}
        own = "E_" + eng
        for (k, v) in deps:
            if eng == "pe" and k == "E_pe":
                continue
            if k == own and eng in ("act", "dve", "pool"):
                te = self.tend[eng].get(v)
                if te is not None and eng in self.fill:
                    gap = self.CLEAR_NS - (self.tcum[eng] - te)
                    if gap > 0:
                        n = int(math.ceil(gap / self.FILL_NS[eng]))
                        for _ in range(n):
                            self.fill[eng](self.eng[eng])
                        self.tcum[eng] += n * self.FILL_NS[eng]
                        self.nfill = getattr(self, "nfill", 0) + n
                continue
            if v > need.get(k, 0):
                need[k] = v
        out = []
        kn = self.known[eng]
        for k, v in need.items():
            if kn.get(k, 0) < v:
                kn[k] = v
                out.append((k, v))
        return out

    @staticmethod
    def _deps(reads, writes):
        deps = []
        for b in reads:
            if b.w is not None:
                deps.append(b.w)
            if b.excl:
                deps.extend(b.r.items())
        for b in writes:
            if b.w is not None:
                deps.append(b.w)
            deps.extend(b.r.items())
        return deps

    def _emit_waits(self, eng, waits):
        e = self.eng[eng]
        for (k, v) in waits:
            if k.startswith("E_"):
                self.used.add((k, v))
                if self.needed is not None:
                    v = self.remap[(k, v)]
            e.wait_ge(self.sems[k], v)
            self.nwaits += 1

    def _mark(self, ev, reads, writes):
        k, v = ev
        for b in reads:
            if b.r.get(k, 0) < v:
                b.r[k] = v
        for b in writes:
            b.w = ev
            b.r = {}

    def op(self, eng, fn, reads=(), writes=(), n=0):
        self.group(eng, [fn], reads, writes, n)

    def group(self, eng, fns, reads=(), writes=(), n=0):
        self._emit_waits(eng, self._waits(eng, self._deps(reads, writes)))
        e = self.eng[eng]
        for fn in fns[:-1]:
            fn(e)
        self.cnt[eng] += 1
        ov, pe_ = self.EST[eng]
        self.tcum[eng] += ov + pe_ * n
        td = self.tend[eng]
        td[self.cnt[eng]] = self.tcum[eng]
        if len(td) > 64:
            for k_ in sorted(td)[:32]:
                del td[k_]
        key = "E_" + eng
        self.semval[key] = self.cnt[eng]
        if self.needed is None or (key, self.cnt[eng]) in self.needed:
            self.sig[key] = self.sig.get(key, 0) + 1
            self.remap[(key, self.cnt[eng])] = self.sig[key]
            fns[-1](e).then_inc(self.sems[key], 1)
        else:
            fns[-1](e)
        self._mark((key, self.cnt[eng]), reads, writes)

    def dma(self, eng, fn, reads=(), writes=()):
        assert len(writes) == 1
        wb = writes[0]
        deps = self._deps(reads, writes)
        if eng == "pool" and getattr(self, "last_swdge", None) is not None:
            deps.append(self.last_swdge)
        self._emit_waits(eng, self._waits(eng, deps))
        key = self._dsem(wb)
        self.semval[key] += 16
        fn(self.eng[eng]).then_inc(self.sems[key], 16)
        if eng == "pool":
            self.last_swdge = (key, self.semval[key])
        self._mark((key, self.semval[key]), reads, writes)

    def barrier(self):
        deps = [(k, v) for k, v in self.semval.items() if v > 0]
        for eng in self.ENG:
            self._emit_waits(eng, self._waits(eng, deps))

    def wait_bufs(self, eng, bufs):
        deps = []
        for b in bufs:
            if b.w is not None:
                deps.append(b.w)
            deps.extend(b.r.items())
        self._emit_waits(eng, self._waits(eng, deps))


class G:
    pass


def bufs(prefix, *dims):
    if len(dims) == 1:
        return [Buf("%s%d" % (prefix, i)) for i in range(dims[0])]
    return [bufs("%s%d_" % (prefix, i), *dims[1:]) for i in range(dims[0])]


def build_program(stage=99, dump=None, needed=None):
    nc = bass.Bass("TRN2", target_bir_lowering=False)
    g = G()
    g.nc = nc
    g.stage = stage
    g.dump = dump
    import os
    g.ntl = int(os.environ.get("NTL", "9"))
    g.dbg_d = None
    if dump is not None:
        g.dbg_d = nc.dram_tensor("dbg", [128, 8 * T], BF16, kind="ExternalOutput").ap()
        g.dbgb = Buf("dbg")
    dt = lambda name, shape, kind, d=F32: nc.dram_tensor(name, shape, d, kind=kind).ap()
    g.x_d = dt("x", [T, D], "ExternalInput")
    g.norm_mix = dt("norm_mix", [DEPTH, D], "ExternalInput")
    g.w_in = dt("w_in", [DEPTH, D, DIN], "ExternalInput")
    g.qn = dt("qn_dsa", [DEPTH, 64], "ExternalInput")
    g.kn = dt("kn_dsa", [DEPTH, 64], "ExternalInput")
    g.lb_d = dt("hgrn_lb", [DEPTH, 512], "ExternalInput")
    g.onorm = dt("hgrn_onorm", [DEPTH, 64], "ExternalInput")
    g.w_sb = dt("w_br_sb", [DEPTH, 384, D], "ExternalInput")
    g.w_dsa = dt("w_br_dsa", [DEPTH, 384, D], "ExternalInput")
    g.w_hg = dt("w_br_hgrn", [DEPTH, 256, D], "ExternalInput")
    g.w_out = dt("w_out", [DEPTH, D, D], "ExternalInput")
    g.norm_mlp = dt("norm_mlp", [DEPTH, D], "ExternalInput")
    g.w_up = dt("w_up", [DEPTH, D, DFF], "ExternalInput")
    g.w_down = dt("w_down", [DEPTH, DFF, D], "ExternalInput")
    g.cs_d = dt("rope_cos", [T, 8], "ExternalInput")
    g.sn_d = dt("rope_sin", [T, 8], "ExternalInput")
    g.out_d = dt("out", [T, D], "ExternalOutput")
    g.xres_d = dt("xres", [T, D], "Internal")
    g.xin_b = bufs("xin", NB)
    g.xres_b = bufs("xres", NB)
    g.out_b = bufs("outb", NB)

    with ExitStack() as gs:
        P = Prog(nc, gs, needed)
        g.P = P
        fa = gs.enter_context(nc.sbuf_tensor("fill_a", [128, 2], F32))
        fd = gs.enter_context(nc.sbuf_tensor("fill_d", [128, 2], F32))
        nc.vector.memset(fd[:], 0.0)
        nc.vector.memset(fa[:], 0.0)
        P.fill["dve"] = lambda e: e.memset(fd[:, 0:1], 0.0)
        P.fill["act"] = lambda e: e.activation(out=fa[:, 0:1], in_=fa[:, 1:2], func=AF.Copy)
        fp = gs.enter_context(nc.sbuf_tensor("fill_p", [128, 2], F32))
        nc.gpsimd.memset(fp[:], 0.0)
        P.fill["pool"] = lambda e: e.memset(fp[:, 0:1], 0.0)
        g.uid = 0
        g.psF = [gs.enter_context(nc.psum_tensor("psF%d" % i, [128, 512], F32)) for i in range(6)]
        g.psFb = [Buf("psF%d" % i, excl=True) for i in range(6)]
        g.psB = [gs.enter_context(nc.psum_tensor("psB%d" % i, [128, 1024], BF16)) for i in range(2)]
        g.psBb = [Buf("psB%d" % i, excl=True) for i in range(2)]
        g.rotc = {}
        build_consts(g, gs)
        g.hT = gs.enter_context(nc.sbuf_tensor("hT_glob", [128, DC, T], BF16))
        g.hTb = bufs("hT", NB)
        for l in range(DEPTH):
            if g.stage >= 1:
                build_layer(g, l)
        if g.dbg_d is not None:
            P.wait_bufs("sp", [g.dbgb])
        P.wait_bufs("sp", g.out_b)
        P.barrier()
        g.used = P.used
        print("ops", P.cnt, "signals", P.sig, "waits", P.nwaits, "fillers", getattr(P, "nfill", 0), "dsems", P.ndsem, flush=True)
    return nc, P.used


def build_two_pass(stage=99, dump=None):
    _, used = build_program(stage, dump, None)
    nc, _ = build_program(stage, dump, used)
    return nc


def pipeline(stages, ntiles):
    ns = len(stages)
    for t in range(ntiles + ns - 1):
        for k, f in enumerate(stages):
            i = t - k
            if 0 <= i < ntiles:
                f(i)


def rot(g, role, items):
    i = g.rotc.get(role, 0)
    g.rotc[role] = i + 1
    return items[i % len(items)]


def psf(g, role, banks):
    b = rot(g, role, banks)
    return g.psF[b], g.psFb[b]


def psb(g, role="pb"):
    b = rot(g, role, [0, 1])
    return g.psB[b], g.psBb[b]


@contextmanager
def scope(g):
    st = ExitStack()
    st.tbufs = []
    try:
        yield st
    finally:
        g.P.barrier()
        g.P.release(st.tbufs)
        st.close()


def sb(g, st, shape, dtype, name=None):
    g.uid += 1
    return st.enter_context(g.nc.sbuf_tensor("%s_%d" % (name or "t", g.uid), shape, dtype))


def nb(st, name):
    b = Buf(name)
    st.tbufs.append(b)
    return b


def nbs(st, prefix, *dims):
    r = bufs(prefix, *dims)

    def flat(x):
        if isinstance(x, Buf):
            st.tbufs.append(x)
        else:
            for y in x:
                flat(y)
    flat(r)
    return r


def _fs(ap):
    try:
        return int(ap.free_size())
    except Exception:
        return 0


def ACT(g, out, in_, func, reads, writes, **kw):
    g.P.op("act", lambda e: e.activation(out=out, in_=in_, func=func, **kw), reads, writes, _fs(out))


def TT(g, eng, out, in0, in1, op, reads, writes):
    g.P.op(eng, lambda e: e.tensor_tensor(out=out, in0=in0, in1=in1, op=op), reads, writes, _fs(out))


def TS(g, eng, out, in0, s1, s2, op0, op1, reads, writes, **kw):
    if op1 is None:
        s2 = 0.0 if isinstance(s1, (int, float)) else g.zc[0:in0.shape[0], 0:1]
        g.P.op(eng, lambda e: e.tensor_scalar(out=out, in0=in0, scalar1=s1, scalar2=s2, op0=op0, op1=ALU.add, **kw), reads, writes, _fs(out))
    else:
        g.P.op(eng, lambda e: e.tensor_scalar(out=out, in0=in0, scalar1=s1, scalar2=s2, op0=op0, op1=op1, **kw), reads, writes, _fs(out))


def STT(g, out, in0, scalar, in1, op0, op1, reads, writes):
    g.P.op("dve", lambda e: e.scalar_tensor_tensor(out=out, in0=in0, scalar=scalar, in1=in1, op0=op0, op1=op1), reads, writes, _fs(out))


def CP(g, eng, out, in_, reads, writes):
    if eng == "act":
        g.P.op("act", lambda e: e.activation(out=out, in_=in_, func=AF.Copy), reads, writes, _fs(out))
    else:
        g.P.op(eng, lambda e: e.tensor_copy(out, in_), reads, writes, _fs(out))


def MS(g, eng, ap, val, writes):
    g.P.op(eng, lambda e: e.memset(ap, val), (), writes, _fs(ap))


def ASEL(g, out, in_, pattern, cmp, fill, base, cm, reads, writes):
    g.P.op("pool", lambda e: e.affine_select(out=out, in_=in_, pattern=pattern, compare_op=cmp, fill=fill, base=base,
                                             channel_multiplier=cm), reads, writes, _fs(out))


def MM(g, outs_fns, reads, writes):
    g.P.group("pe", outs_fns, reads, writes)


def mmf(out, lhsT, rhs, start, stop):
    return lambda e: e.matmul(out, lhsT=lhsT, rhs=rhs, start=start, stop=stop)


def trf(out, in_, ident):
    return lambda e: e.transpose(out, in_, ident)


def DMA(g, eng, out, in_, reads, writes, **kw):
    g.P.dma(eng, lambda e: e.dma_start(out=out, in_=in_, **kw), reads, writes)


def build_consts(g, gs):
    nc = g.nc
    mk = lambda name, shape, d: gs.enter_context(nc.sbuf_tensor(name, shape, d))
    g.ident = mk("ident", [128, 128], BF16)
    g.negtri = mk("negtri", [128, 128], BF16)
    g.negones = mk("negones", [128, 128], BF16)
    g.onesb = mk("onesb", [128, 128], BF16)
    g.maskbd = mk("maskbd", [128, 128], F32)
    g.onesf = mk("onesf", [128, 128], F32)
    g.resetm = mk("resetm", [128, T], BF16)
    g.zc = mk("zc", [128, 1], F32)
    g.negbig = mk("negbig", [128, 1], F32)
    g.cs = mk("cs", [128, NB, 8], F32)
    g.sn = mk("sn", [128, NB, 8], F32)
    g.lbraw = mk("lbraw", [128, 2, 4], F32)
    g.lbv = mk("lbv", [128, 2, 4], F32)
    g.oml = mk("oml", [128, 2, 4], F32)
    g.cb = Buf("consts")
    g.csb = Buf("cs")
    g.snb = Buf("sn")
    g.lbb = Buf("lbraw")
    cb = [g.cb]
    MS(g, "pool", g.onesb[:], 1.0, cb)
    MS(g, "pool", g.negones[:], -1.0, cb)
    MS(g, "pool", g.onesf[:], 1.0, cb)
    MS(g, "pool", g.zc[:], 0.0, cb)
    MS(g, "pool", g.negbig[:], -1e29, cb)
    ASEL(g, g.ident[:], g.onesb[:], [[1, 128]], ALU.is_equal, 0.0, 0, -1, cb, cb)
    ASEL(g, g.negtri[:], g.negones[:], [[-1, 128]], ALU.is_ge, 0.0, 0, 1, cb, cb)
    ASEL(g, g.maskbd[:], g.onesf[:], [[1, 128]], ALU.is_ge, 0.0, 0, -1, cb, cb)
    MS(g, "pool", g.maskbd[0:64, 64:128], 0.0, cb)
    g.ones512 = mk("ones512", [128, 512], BF16)
    g.mlt = mk("mlt", [128, 512], BF16)
    MS(g, "pool", g.ones512[:], 1.0, cb)
    ASEL(g, g.mlt[:], g.ones512[:], [[1, 512]], ALU.is_gt, 0.0, 0, -1, cb, cb)
    g.caus01 = mk("caus01", [128, 128], F32)
    g.negfill = mk("negfill", [128, 128], F32)
    ASEL(g, g.caus01[:], g.onesf[:], [[-1, 128]], ALU.is_ge, 0.0, 0, 1, cb, cb)
    TS(g, "pool", g.negfill[:], g.caus01[:], -1.0, 1e30, ALU.add, ALU.mult, cb, cb)
    MS(g, "pool", g.resetm[:], 1.0, cb)
    MS(g, "pool", g.resetm[:].rearrange("p (c j) -> p c j", j=64)[:, :, 0:1], 0.0, cb)
    DMA(g, "sp", g.cs[:], g.cs_d.rearrange("(b p) i -> p b i", p=128), (), [g.csb])
    DMA(g, "sp", g.sn[:], g.sn_d.rearrange("(b p) i -> p b i", p=128), (), [g.snb])
    DMA(g, "sp", g.lbraw[:], g.lb_d.rearrange("l (h k) -> k l h", k=128), (), [g.lbb], allow_slow_non_contiguous=True)
    MS(g, "dve", g.lbv[:], 0.0, cb)
    TT(g, "dve", g.lbv[:, 1, :], g.lbraw[:, 1, :], g.lbraw[:, 0, :], ALU.subtract, [g.lbb], cb)
    ACT(g, g.lbv[:, 1, :], g.lbv[:, 1, :], AF.Sigmoid, cb, cb)
    TS(g, "dve", g.oml[:], g.lbv[:], -1.0, 1.0, ALU.mult, ALU.add, cb, cb)
    g.P.barrier()


class NormCtx:
    pass


def norm_setup(g, st, gain_d_row):
    c = NormCtx()
    c.gain = sb(g, st, [128, D], F32, "gain")
    c.gb = nb(st, "gain")
    DMA(g, "sp", c.gain[:], gain_d_row.to_broadcast([128, D]), (), [c.gb])
    c.junk = sb(g, st, [128, D], BF16, "junk")
    c.jb = nb(st, "junk")
    c.hbs = [sb(g, st, [128, D], BF16, "hb") for _ in range(2)]
    c.hbb = nbs(st, "hb", 2)
    c.ss = sb(g, st, [128, NB], F32, "ss")
    c.ssb = nbs(st, "ss", NB)
    MS(g, "dve", c.ss[:], 0.0, c.ssb)
    return c


def norm_block(g, c, xb, xbuf, tb, hT, hTb):
    hb, hbuf = c.hbs[tb % 2], c.hbb[tb % 2]
    s1 = c.ss[:, tb:tb + 1]
    ACT(g, c.junk[:], xb[:], AF.Square, [xbuf, c.ssb[tb]], [c.jb, c.ssb[tb]], accum_out=s1)
    ACT(g, s1, s1, AF.Ln, [c.ssb[tb]], [c.ssb[tb]], scale=1.0 / D, bias=EPS)
    ACT(g, s1, s1, AF.Exp, [c.ssb[tb]], [c.ssb[tb]], scale=-0.5)
    STT(g, hb[:], xb[:], s1, c.gain[:], ALU.mult, ALU.mult, [xbuf, c.ssb[tb], c.gb], [hbuf])
    for half in range(2):
        pt, ptb = psb(g)
        MM(g, [trf(pt[:, m * 128:(m + 1) * 128], hb[:, (half * 4 + m) * 128:(half * 4 + m + 1) * 128], g.ident[:])
               for m in range(4)], [hbuf, g.cb], [ptb])
        CP(g, "act" if half == 0 else "dve", hT[:, half * 4:half * 4 + 4, tb * 128:(tb + 1) * 128],
           pt[:, 0:512].rearrange("p (m j) -> p m j", j=128), [ptb], [hTb[tb]])


def norm_T(g, st, src_ap, src_bufs, gain_d_row, hT, hTb):
    c = norm_setup(g, st, gain_d_row)
    xbs = [sb(g, st, [128, D], F32, "xb") for _ in range(4)]
    xbb = nbs(st, "xb", 4)
    for tb in range(NB):
        xb, xbuf = xbs[tb % 4], xbb[tb % 4]
        DMA(g, "sp", xb[:], src_ap[tb * 128:(tb + 1) * 128, :], [src_bufs[tb]], [xbuf])
        norm_block(g, c, xb, xbuf, tb, hT, hTb)


def win_cols(g, l, c0, c1):
    return g.w_in[l].rearrange("(c p) n -> p c n", p=128)[:, :, c0:c1]


def mm_fm(g, ps, M, n, w, wb, col0, hT, hTb, tok0, role_bufs):
    MM(g, [mmf(ps[0:M, 0:n], w[:, c, col0:col0 + M], hT[:, c, tok0:tok0 + n], c == 0, c == DC - 1) for c in range(DC)],
       [wb] + hTb[tok0 // 128:(tok0 + n + 127) // 128], [role_bufs])


def mm_tm(g, ps, N, w, wb, col0, hT, hTb, tb, psbuf):
    MM(g, [mmf(ps[:, 0:N], hT[:, c, tb * 128:(tb + 1) * 128], w[:, c, col0:col0 + N], c == 0, c == DC - 1) for c in range(DC)],
       [wb, hTb[tb]], [psbuf])


def phase_sb(g, l, hT, hTb, osbT, osbb):
    with scope(g) as st:
        ws = []
        for i, c0 in enumerate((C_SQ, C_SK, C_SV)):
            w = sb(g, st, [128, DC, 384], BF16, "wsb")
            wb = nb(st, "wsb%d" % i)
            DMA(g, "pool", w[:], win_cols(g, l, c0, c0 + 384), (), [wb])
            ws.append((w, wb))
        sqT = sb(g, st, [128, 3, T], BF16, "sqT")
        skT = sb(g, st, [128, 3, T], BF16, "skT")
        sqb = nbs(st, "sq", 3, 4)
        skb = nbs(st, "sk", 3, 4)
        k = 0
        for (dst, dstb, (w, wb), scl) in ((sqT, sqb, ws[0], 0.125), (skT, skb, ws[1], 1.0)):
            for hp in range(3):
                for tc in range(4):
                    ps, pb = psf(g, "proj", [0, 1, 2, 3, 4, 5])
                    mm_fm(g, ps, 128, 512, w, wb, hp * 128, hT, hTb, tc * 512, pb)
                    o = dst[:, hp, tc * 512:(tc + 1) * 512]
                    if k % 2 == 0:
                        ACT(g, o, ps[:, :], AF.Copy, [pb], [dstb[hp][tc]], scale=scl)
                    else:
                        TS(g, "dve", o, ps[:, :], scl, None, ALU.mult, None, [pb], [dstb[hp][tc]])
                    k += 1
        svp = [sb(g, st, [128, NB, 384], BF16, "svp") for _ in range(2)]
        svb = nbs(st, "sv", 2, NB)
        for s_ in range(2):
            MS(g, "pool", svp[s_][:].rearrange("p t c -> p (t c)"), 0.0, svb[s_])
        for tb in range(NB):
            ps, pb = psf(g, "proj", [0, 1, 2, 3, 4, 5])
            mm_tm(g, ps, 384, ws[2][0], ws[2][1], 0, hT, hTb, tb, pb)
            src = ps[:, 0:384].rearrange("p (m s d) -> p m s d", s=2, d=64)
            for s_ in range(2):
                dst = svp[s_][:, tb, :].rearrange("p (m s d) -> p m s d", s=2, d=64)
                CP(g, "act" if s_ == 0 else "dve", dst[:, :, s_, :], src[:, :, s_, :], [pb], [svb[s_][tb]])
        R = 4
        mk2 = lambda shape, dt_, nm: [[sb(g, st, shape, dt_, nm) for _ in range(R)] for _ in range(2)]
        Et, SPt, SPs, At = mk2([128, 512], F32, "Et"), mk2([128, 512], BF16, "SPt"), mk2([128, 512], BF16, "SPs"), mk2([128, 512], BF16, "At")
        Etb, SPb, SPsb, Atb = nbs(st, "Et", 2, R), nbs(st, "SPt", 2, R), nbs(st, "SPs", 2, R), nbs(st, "At", 2, R)
        steps = []
        for m in range(3):
            for qc in range(4):
                for n_, kb in enumerate(range(4 * qc + 3, -1, -1)):
                    steps.append((m, qc, kb, n_))
        stt = {}
        pso = {}

        def info(t):
            m, qc, kb, n_ = steps[t]
            j0 = max(0, kb * 128 - qc * 512)
            return m, qc, kb, n_, j0, kb >= 4 * qc, qc * 512 + j0 - kb * 128, n_ == 0

        def opnds(t):
            m, qc, kb, n_, j0, diag, base, first = info(t)
            kk = [skT[64 * s_:64 * s_ + 64, m, kb * 128:(kb + 1) * 128] for s_ in range(2)]
            qq = [sqT[64 * s_:64 * s_ + 64, m, qc * 512 + j0:(qc + 1) * 512] for s_ in range(2)]
            return kk, qq, skb[m][kb // 4], sqb[m][qc]

        def s1(t):
            m, qc, kb, n_, j0, diag, base, first = info(t)
            kk, qq, rk, rq = opnds(t)
            pz = [psf(g, "sbZ", [0, 1, 2]) for _ in range(2)]
            stt[t] = {"pz": pz}
            for s_ in range(2):
                MM(g, [mmf(pz[s_][0][:, j0:512], kk[s_], qq[s_], True, True)], [rk, rq], [pz[s_][1]])

        def s2(t):
            m, qc, kb, n_, j0, diag, base, first = info(t)
            ib = t % R
            pz = stt[t]["pz"]
            for s_ in range(2):
                ACT(g, Et[s_][ib][:, j0:512], pz[s_][0][:, j0:512], AF.Exp, [pz[s_][1]], [Etb[s_][ib]])

        def s3(t):
            m, qc, kb, n_, j0, diag, base, first = info(t)
            ib = t % R
            for s_ in range(2):
                ACT(g, SPt[s_][ib][:, j0:512], Et[s_][ib][:, j0:512], AF.Ln, [Etb[s_][ib]], [SPb[s_][ib]], bias=1.0)
            if diag:
                for s_ in range(2):
                    S = SPt[s_][ib]
                    assert base == 0
                    TT(g, "pool", S[:, j0:512], S[:, j0:512], g.mlt[:, 0:512 - j0], ALU.mult, [SPb[s_][ib], g.cb], [SPb[s_][ib]])
            if kb > 0:
                for s_ in range(2):
                    S, Sb = SPt[s_][ib], SPb[s_][ib]
                    Sn, Snb = SPs[s_][n_ % R], SPsb[s_][n_ % R]
                    if first:
                        if j0 > 0:
                            MS(g, "pool", Sn[:, 0:j0], 0.0, [Snb])
                        CP(g, "pool", Sn[:, j0:512], S[:, j0:512], [Sb], [Snb])
                    else:
                        So, Sob = SPs[s_][(n_ - 1) % R], SPsb[s_][(n_ - 1) % R]
                        if j0 > 0:
                            CP(g, "pool", Sn[:, 0:j0], So[:, 0:j0], [Sob], [Snb])
                        TT(g, "dve", Sn[:, j0:512], So[:, j0:512], S[:, j0:512], ALU.add, [Sob, Sb], [Snb])

        def s4(t):
            m, qc, kb, n_, j0, diag, base, first = info(t)
            ib = t % R
            kk, qq, rk, rq = opnds(t)
            pc = [psf(g, "sbC", [3, 4]) for _ in range(2)]
            stt[t]["pc"] = pc
            for s_ in range(2):
                S = SPt[s_][ib]
                fns = [mmf(pc[s_][0][:, j0:512], kk[s_], qq[s_], True, False),
                       mmf(pc[s_][0][:, j0:512], g.negtri[:], S[:, j0:512], False, first)]
                rd = [rk, rq, SPb[s_][ib], g.cb]
                if not first:
                    So, Sob = SPs[s_][(n_ - 1) % R], SPsb[s_][(n_ - 1) % R]
                    fns.append(mmf(pc[s_][0][:, j0:512], g.negones[:], So[:, j0:512], False, True))
                    rd.append(Sob)
                MM(g, fns, rd, [pc[s_][1]])

        def s5(t):
            m, qc, kb, n_, j0, diag, base, first = info(t)
            ib = t % R
            pc = stt[t]["pc"]
            for s_ in range(2):
                ACT(g, At[s_][ib][:, j0:512], pc[s_][0][:, j0:512], AF.Exp, [pc[s_][1]], [Atb[s_][ib]])
            for s_ in range(2):
                A, Ab = At[s_][ib], Atb[s_][ib]
                if diag:
                    TT(g, "pool", A[:, j0:512], A[:, j0:512], g.mlt[:, 0:512 - j0], ALU.mult, [Ab, g.cb], [Ab])
                if first and j0 > 0:
                    MS(g, "pool", A[:, 0:j0], 0.0, [Ab])

        def s6(t):
            m, qc, kb, n_, j0, diag, base, first = info(t)
            ib = t % R
            if first:
                pso[(m, qc)] = psf(g, "sbO", [5])
            psO, pOb = pso[(m, qc)]
            for s_ in range(2):
                A, Ab = At[s_][ib], Atb[s_][ib]
                vv = svp[s_][:, kb, m * 128:(m + 1) * 128]
                c0 = 0 if first else j0
                MM(g, [mmf(psO[:, c0:512], vv, A[:, c0:512], first and s_ == 0, kb == 0 and s_ == 1)], [svb[s_][kb], Ab], [pOb])
            if kb == 0:
                CP(g, "act" if (m * 4 + qc) % 2 == 0 else "dve", osbT[:, m, qc * 512:(qc + 1) * 512], psO[:, :], [pOb], [osbb[m][qc]])
            del stt[t]

        nst = len(steps)
        stages = [s1, s2, s3, s4, s5, s6]
        for e in range(nst + len(stages) - 1):
            for k in range(len(stages) - 1, -1, -1):
                t = e - k
                if 0 <= t < nst:
                    stages[k](t)


def phase_dsa(g, l, hT, hTb, odT, odb):
    P = g.P
    with scope(g) as st:
        featT = sb(g, st, [128, 9, T], BF16, "featT")
        fb = nbs(st, "feat", NB)
        dvx = sb(g, st, [128, NB, 128], BF16, "dvx")
        dvb = nbs(st, "dvx", NB)
        sgn = sb(g, st, [128, NB, 8], F32, "sgn")
        sgb = nbs(st, "sgn", NB)
        qkg = sb(g, st, [128, 7, 64], F32, "qkg")
        qkgb = nbs(st, "qkg", 7)
        for hh in range(7):
            src = (g.qn if hh < 6 else g.kn)[l:l + 1, :].to_broadcast([128, 64])
            DMA(g, "sp", qkg[:, hh, :], src, (), [qkgb[hh]])
        with scope(g) as s2:
            wA = sb(g, s2, [128, DC, 512], BF16, "wA")
            wB = sb(g, s2, [128, DC, 512], BF16, "wB")
            wC = sb(g, s2, [128, DC, 72], BF16, "wC")
            wAb, wBb, wCb = nb(s2, "wA"), nb(s2, "wB"), nb(s2, "wC")
            DMA(g, "pool", wA[:], win_cols(g, l, C_DQ, C_DQ + 512), (), [wAb])
            DMA(g, "pool", wB[:], win_cols(g, l, C_IQ, C_IQ + 512), (), [wBb])
            DMA(g, "pool", wC[:], win_cols(g, l, C_IK, C_IK + 72), (), [wCb])
            NR = 4
            tq = [sb(g, s2, [128, 18, 64], F32, "tokq") for _ in range(NR)]
            tqb = nbs(s2, "tokq", NR)
            tbq = [sb(g, s2, [128, 18, 64], BF16, "tokb") for _ in range(3)]
            tbb = nbs(s2, "tokb", 3)
            sqt = [sb(g, s2, [128, 448], F32, "sqt") for _ in range(2)]
            sqtb = nbs(s2, "sqt", 2)
            smq = [sb(g, s2, [128, 32], F32, "small") for _ in range(2)]
            smqb = nbs(s2, "small", 2)
            rt = [sb(g, s2, [128, 18, 8], F32, "ropet") for _ in range(4)]
            rtb = nbs(s2, "ropet", 4)
            pst = {}

            def p1(tb):
                MS(g, "pool", dvx[:, tb, 64:128], 1.0, [dvb[tb]])
                tk, tkb = tq[tb % NR], tqb[tb % NR]
                MS(g, "pool", tk[:, 7, :], 0.0, [tkb])
                MS(g, "pool", tk[:, 17, :], 0.0, [tkb])
                psA, pAb = psf(g, "dA", [0, 1])
                psBq, pBb = psf(g, "dB", [2, 3])
                psC, pCb = psf(g, "dC", [4, 5])
                pst[tb] = (psA, pAb, psBq, pBb, psC, pCb)
                mm_tm(g, psA, 512, wA, wAb, 0, hT, hTb, tb, pAb)
                mm_tm(g, psBq, 512, wB, wBb, 0, hT, hTb, tb, pBb)
                mm_tm(g, psC, 72, wC, wCb, 0, hT, hTb, tb, pCb)

            def p2(tb):
                psA, pAb, psBq, pBb, psC, pCb = pst.pop(tb)
                tk, tkb = tq[tb % NR], tqb[tb % NR]
                sq_, sq_b = sqt[tb % 2], sqtb[tb % 2]
                sm, smb = smq[tb % 2], smqb[tb % 2]
                ACT(g, sq_[:], psA[:, 0:448], AF.Square, [pAb], [sq_b])
                ss = sm[:, 0:7]
                P.op("dve", lambda e, ss=ss, sq_=sq_: e.tensor_reduce(out=ss, in_=sq_[:].rearrange("p (h d) -> p h d", d=64), axis=AX.X,
                                                                      op=ALU.add), [sq_b], [smb], 448)
                ACT(g, ss, ss, AF.Ln, [smb], [smb], scale=1.0 / 64, bias=EPS)
                ACT(g, ss, ss, AF.Exp, [smb], [smb], scale=-0.5)
                aw = sm[:, 8:16]
                TS(g, "dve", sgn[:, tb, :], psC[:, 64:72], 0.0, 2.0, ALU.is_gt, ALU.mult, [pCb], [sgb[tb]])
                TS(g, "dve", sgn[:, tb, :], sgn[:, tb, :], -1.0, 0.0, ALU.add, ALU.add, [sgb[tb]], [sgb[tb]])
                STT(g, aw, psC[:, 64:72], IDX_SCALE, sgn[:, tb, :], ALU.mult, ALU.mult, [pCb, sgb[tb]], [smb])
                TT(g, "dve", tk[:, 8:16, :], psBq[:, :].rearrange("p (h d) -> p h d", d=64),
                   aw.unsqueeze(2).to_broadcast([128, 8, 64]), ALU.mult, [pBb, smb], [tkb])
                CP(g, "act", tk[:, 16, :], psC[:, 0:64], [pCb], [tkb])
                CP(g, "act", dvx[:, tb, 0:64], psA[:, 448:512], [pAb], [dvb[tb]])
                TT(g, "dve", tk[:, 0:7, :], psA[:, 0:448].rearrange("p (h d) -> p h d", d=64),
                   ss.unsqueeze(2).to_broadcast([128, 7, 64]), ALU.mult, [pAb, smb], [tkb])
                TT(g, "dve", tk[:, 0:7, :], tk[:, 0:7, :], qkg[:], ALU.mult, [tkb] + qkgb, [tkb])

            def p3(tb):
                tk, tkb = tq[tb % NR], tqb[tb % NR]
                x1, x2 = tk[:, :, 0:8], tk[:, :, 8:16]
                cb_ = g.cs[:, tb, :].unsqueeze(1).to_broadcast([128, 18, 8])
                sb_ = g.sn[:, tb, :].unsqueeze(1).to_broadcast([128, 18, 8])
                TT(g, "dve", rt[0][:], x1, cb_, ALU.mult, [tkb, g.csb], [rtb[0]])
                TT(g, "pool", rt[1][:], x2, sb_, ALU.mult, [tkb, g.snb], [rtb[1]])
                TT(g, "dve", rt[2][:], x2, cb_, ALU.mult, [tkb, g.csb], [rtb[2]])
                TT(g, "pool", rt[3][:], x1, sb_, ALU.mult, [tkb, g.snb], [rtb[3]])
                TT(g, "dve", x1, rt[0][:], rt[1][:], ALU.subtract, [rtb[0], rtb[1]], [tkb])
                TT(g, "pool", x2, rt[2][:], rt[3][:], ALU.add, [rtb[2], rtb[3]], [tkb])

            def p4(tb):
                tk, tkb = tq[tb % NR], tqb[tb % NR]
                tkh, tkhb = tbq[tb % 3], tbb[tb % 3]
                CP(g, "act", tkh[:], tk[:], [tkb], [tkhb])
                CP(g, "pool", tkh[:, 7, :], tkh[:, 6, :], [tkhb], [tkhb])
                CP(g, "pool", tkh[:, 17, :], tkh[:, 16, :], [tkhb], [tkhb])

            def p5(tb):
                tkh, tkhb = tbq[tb % 3], tbb[tb % 3]
                flat = tkh[:].rearrange("p h d -> p (h d)")
                pt, ptb = psb(g)
                MM(g, [trf(pt[:, m * 128:(m + 1) * 128], flat[:, m * 128:(m + 1) * 128], g.ident[:]) for m in range(8)],
                   [tkhb, g.cb], [ptb])
                CP(g, "dve", featT[:, 0:8, tb * 128:(tb + 1) * 128], pt[:, :].rearrange("p (m j) -> p m j", j=128), [ptb], [fb[tb]])
                pt2, ptb2 = psb(g)
                MM(g, [trf(pt2[:, 0:128], flat[:, 1024:1152], g.ident[:])], [tkhb, g.cb], [ptb2])
                CP(g, "act", featT[:, 8, tb * 128:(tb + 1) * 128], pt2[:, 0:128], [ptb2], [fb[tb]])

            stages = [p1, p2, p3, p4, p5]
            for e in range(NB + len(stages) - 1):
                for k in range(len(stages) - 1, -1, -1):
                    t_ = e - k
                    if 0 <= t_ < NB:
                        stages[k](t_)
        sc = [sb(g, st, [128, T], F32, "sc") for _ in range(4)]
        scb = nbs(st, "sc", 4, 4)
        junk = sb(g, st, [128, T], BF16, "junk")
        maskq = [sb(g, st, [128, T], BF16, "maskq") for _ in range(2)]
        mqb = nbs(st, "maskq", 2)
        maskT = sb(g, st, [128, NB, 512], BF16, "maskT")
        mTb = nbs(st, "maskT", 4)
        rj = [sb(g, st, [128, 512], BF16, "rj") for _ in range(4)]
        rjb = nbs(st, "rj", 4)
        dg = [sb(g, st, [128, 8, 128], BF16, "dg") for _ in range(2)]
        dgb = nbs(st, "dg", 2)
        sm = [sb(g, st, [128, 8 + 2 * N_BISECT], F32, "bis") for _ in range(2)]
        smb = nbs(st, "bis", 2)
        cvec = sb(g, st, [128, N_BISECT], F32, "cvec")
        c255 = sb(g, st, [128, 1], F32, "c255")
        cvb = nb(st, "cvec")
        for n_ in range(N_BISECT):
            MS(g, "pool", cvec[:, n_:n_ + 1], 2.0 ** -(n_ + 1), [cvb])
        MS(g, "pool", c255[:], 255.5, [cvb])
        Pt = [sb(g, st, [128, 512], BF16, "Pt") for _ in range(4)]
        Ptb = nbs(st, "Pt", 4)
        Pm = [sb(g, st, [128, 512], BF16, "Pm") for _ in range(4)]
        Pmb = nbs(st, "Pm", 4)
        rs = sb(g, st, [64, 512], F32, "rs")
        rsb = nb(st, "rs")
        cnt_ = {"ri": 0, "pi": 0, "pm": 0}

        def idx_blocks(blocks):
            tiles = []
            for i in blocks:
                nk = (i + 1) * 128
                d_, d_b = dg[i % 2], dgb[i % 2]
                for j in range(8):
                    TS(g, "pool", d_[:, j, :], g.ident[:], sgn[:, i, j:j + 1], None, ALU.mult, None, [g.cb, sgb[i]], [d_b])
                for kc in range((nk + 511) // 512):
                    n = min(512, nk - kc * 512)
                    for j in range(8):
                        tiles.append((i, kc, n, j))
            stt = {}

            def s1(t):
                i, kc, n, j = tiles[t]
                po = 64 * (j % 2)
                psZ, pZb = psf(g, "ixZ", [0, 1, 2])
                stt[t] = [psZ, pZb]
                MM(g, [mmf(psZ[:, 0:n], featT[po:po + 64, 4 + j // 2, i * 128:(i + 1) * 128],
                           featT[po:po + 64, 8, kc * 512:kc * 512 + n], True, True)],
                   [fb[i]] + fb[kc * 4:(kc * 512 + n) // 128], [pZb])

            def s2(t):
                i, kc, n, j = tiles[t]
                psZ, pZb = stt[t]
                r_, r_b = rj[cnt_["ri"] % 4], rjb[cnt_["ri"] % 4]
                cnt_["ri"] += 1
                stt[t] += [r_, r_b]
                ACT(g, r_[:, 0:n], psZ[:, 0:n], AF.Relu, [pZb], [r_b])

            def s3(t):
                i, kc, n, j = tiles[t]
                r_, r_b = stt[t][2], stt[t][3]
                if j == 0:
                    cnt_["psS"] = psf(g, "ixS", [3, 4])
                psS, pSb = cnt_["psS"]
                d_, d_b = dg[i % 2], dgb[i % 2]
                MM(g, [mmf(psS[:, 0:n], d_[:, j, :], r_[:, 0:n], j == 0, j == 7)], [d_b, r_b], [pSb])
                if j == 7:
                    CP(g, "act", sc[i % 4][:, kc * 512:kc * 512 + n], psS[:, 0:n], [pSb], [scb[i % 4][kc]])
                del stt[t]

            pipeline([s1, s2, s3], len(tiles))

        def bis_pair(p):
            blocks = [2 * p, 2 * p + 1]
            st_ = []
            for bi, i in enumerate(blocks):
                nk = (i + 1) * 128
                s_, s_b = sc[i % 4], scb[i % 4]
                nkc = (nk + 511) // 512
                srd = s_b[0:nkc]
                m_, m_b = sm[bi], smb[bi]
                rmax, rmin, step0, mid, cntv, tt = (m_[:, c:c + 1] for c in range(6))
                stepc = m_[:, 8:8 + N_BISECT]
                if nk > 256:
                    P.op("dve", lambda e, s_=s_, nk=nk, rmax=rmax: e.tensor_reduce(out=rmax, in_=s_[:, 0:nk], axis=AX.X, op=ALU.max), srd, [m_b], nk)
                    P.op("dve", lambda e, s_=s_, nk=nk, rmin=rmin: e.tensor_reduce(out=rmin, in_=s_[:, 0:nk], axis=AX.X, op=ALU.min), srd, [m_b], nk)
                dsl = s_[:, i * 128:(i + 1) * 128]
                TT(g, "pool", dsl, dsl, g.caus01[:], ALU.mult, [s_b[i // 4], g.cb], [s_b[i // 4]])
                TT(g, "pool", dsl, dsl, g.negfill[:], ALU.add, [s_b[i // 4], g.cb], [s_b[i // 4]])
                st_.append((i, nk, s_, srd, m_, m_b, rmax, rmin, step0, mid, cntv, tt, stepc))
            act = [x for x in st_ if x[1] > 256]
            for (i, nk, s_, srd, m_, m_b, rmax, rmin, step0, mid, cntv, tt, stepc) in act:
                TT(g, "dve", step0, rmax, rmin, ALU.subtract, [m_b], [m_b])
            for (i, nk, s_, srd, m_, m_b, rmax, rmin, step0, mid, cntv, tt, stepc) in act:
                TS(g, "dve", stepc, cvec[:], step0, None, ALU.mult, None, [m_b, cvb], [m_b])
            for (i, nk, s_, srd, m_, m_b, rmax, rmin, step0, mid, cntv, tt, stepc) in act:
                TS(g, "dve", mid, stepc[:, 0:1], rmin, g.zc[:, 0:1], ALU.add, ALU.add, [m_b, g.cb], [m_b])
            for n_ in range(N_BISECT):
                for (i, nk, s_, srd, m_, m_b, rmax, rmin, step0, mid, cntv, tt, stepc) in act:
                    TS(g, "dve", junk[:, 0:nk], s_[:, 0:nk], mid, g.zc[:, 0:1], ALU.is_ge, ALU.add, srd + [m_b, g.cb], [m_b], accum_out=cntv)
                for (i, nk, s_, srd, m_, m_b, rmax, rmin, step0, mid, cntv, tt, stepc) in act:
                    TS(g, "dve", tt, cntv, c255[:, 0:1], stepc[:, n_:n_ + 1], ALU.is_ge, ALU.mult, [m_b, cvb], [m_b])
                for (i, nk, s_, srd, m_, m_b, rmax, rmin, step0, mid, cntv, tt, stepc) in act:
                    nn = min(n_ + 1, N_BISECT - 1)
                    TS(g, "dve", mid, tt, stepc[:, nn:nn + 1], mid, ALU.subtract, ALU.add, [m_b], [m_b])
            for (i, nk, s_, srd, m_, m_b, rmax, rmin, step0, mid, cntv, tt, stepc) in st_:
                thr = mid if nk > 256 else g.negbig[:, 0:1]
                mq, mq_b = maskq[i % 2], mqb[i % 2]
                TS(g, "dve", mq[:, 0:nk], s_[:, 0:nk], thr, None, ALU.is_ge, None, srd + [m_b, g.cb], [mq_b])

        def mT_pair(p):
            for i in (2 * p, 2 * p + 1):
                ii = i % 4
                mq, mq_b = maskq[i % 2], mqb[i % 2]
                for k0 in range(0, i + 1, 8):
                    k1 = min(i + 1, k0 + 8)
                    pt, ptb = psb(g)
                    MM(g, [trf(pt[:, (kb - k0) * 128:(kb - k0 + 1) * 128], mq[:, kb * 128:(kb + 1) * 128], g.ident[:])
                           for kb in range(k0, k1)], [mq_b, g.cb], [ptb])
                    CP(g, "act", maskT[:, k0:k1, ii * 128:(ii + 1) * 128],
                       pt[:, 0:(k1 - k0) * 128].rearrange("p (m j) -> p m j", j=128), [ptb], [mTb[ii]])

        def att_chunk(qc):
            last = 4 * qc + 3
            tiles = [(h, kb) for h in range(6) for kb in range(last + 1)]
            stt = {}
            pso = {}

            def s1(t):
                h, kb = tiles[t]
                hp, po = h // 2, 64 * (h % 2)
                j0 = max(0, kb * 128 - qc * 512)
                psL, pLb = psf(g, "dsL", [0, 1, 2])
                stt[t] = [psL, pLb]
                MM(g, [mmf(psL[:, j0:512], featT[po:po + 64, 3, kb * 128:(kb + 1) * 128],
                           featT[po:po + 64, hp, qc * 512 + j0:(qc + 1) * 512], True, True)],
                   [fb[kb]] + fb[qc * 4:qc * 4 + 4], [pLb])

            def s2(t):
                h, kb = tiles[t]
                j0 = max(0, kb * 128 - qc * 512)
                psL, pLb = stt[t][0], stt[t][1]
                pi = cnt_["pi"]
                cnt_["pi"] += 1
                p_, p_b = Pt[pi % 4], Ptb[pi % 4]
                stt[t] += [p_, p_b]
                ACT(g, p_[:, j0:512], psL[:, j0:512], AF.Exp, [pLb], [p_b], scale=0.125)

            def s3(t):
                h, kb = tiles[t]
                j0 = max(0, kb * 128 - qc * 512)
                p_, p_b = stt[t][2], stt[t][3]
                pm = cnt_["pm"]
                cnt_["pm"] += 1
                m_, m_b = Pm[pm % 4], Pmb[pm % 4]
                stt[t] += [m_, m_b]
                TT(g, "pool", m_[:, j0:512], p_[:, j0:512], maskT[:, kb, j0:512], ALU.mult, [p_b] + mTb[j0 // 128:4], [m_b])

            def s4(t):
                h, kb = tiles[t]
                hp, po = h // 2, 64 * (h % 2)
                j0 = max(0, kb * 128 - qc * 512)
                m_, m_b = stt[t][4], stt[t][5]
                if kb == 0:
                    pso[h] = psf(g, "dsO", [4, 5])
                psO, pOb = pso[h]
                MM(g, [mmf(psO[:, j0:512], dvx[:, kb, :], m_[:, j0:512], kb == 0, kb == last)], [dvb[kb], m_b], [pOb])
                if kb == last:
                    ACT(g, rs[0:64, :], psO[64:128, :], AF.Ln, [pOb], [rsb])
                    ACT(g, rs[0:64, :], rs[0:64, :], AF.Exp, [rsb], [rsb], scale=-1.0)
                    TT(g, "dve", odT[po:po + 64, hp, qc * 512:(qc + 1) * 512], psO[0:64, :], rs[0:64, :], ALU.mult, [pOb, rsb],
                       [odb[hp][qc]])
                del stt[t]

            pipeline([s1, s2, s3, s4], len(tiles))

        idx_blocks([0, 1])
        for p in range(8):
            if p + 1 < 8:
                idx_blocks([2 * p + 2, 2 * p + 3])
            bis_pair(p)
            mT_pair(p)
            if p % 2 == 1:
                att_chunk(p // 2)


def phase_hgrn(g, l, hT, hTb, ohT, ohb):
    P = g.P
    import os
    if int(os.environ.get("HGL", "9")) == 0:
        return
    with scope(g) as st:
        hi_tm = sb(g, st, [128, NB, 256], BF16, "hi_tm")
        hib = nbs(st, "hi", NB)
        hgs = sb(g, st, [128, NB, 256], BF16, "hgs")
        hgb = nbs(st, "hgs", NB)
        onb = sb(g, st, [128, 64], F32, "onorm")
        onbb = nb(st, "onorm")
        DMA(g, "sp", onb[:], g.onorm[l:l + 1, :].to_broadcast([128, 64]), (), [onbb])
        with scope(g) as s2:
            w = sb(g, s2, [128, DC, 512], BF16, "whihg")
            wb = nb(s2, "whihg")
            DMA(g, "pool", w[:], win_cols(g, l, C_HI, C_HI + 512), (), [wb])
            sgs = [sb(g, s2, [128, 256], F32, "sgs") for _ in range(2)]
            sgsb = nbs(s2, "sgs", 2)
            for tb in range(NB):
                ps, pb = psf(g, "proj", [0, 1, 2, 3, 4, 5])
                hgv = int(os.environ.get("HGV", "15"))
                if hgv & 8:
                    mm_tm(g, ps, 512, w, wb, 0, hT, hTb, tb, pb)
                if hgv & 1:
                    CP(g, "dve", hi_tm[:, tb, :], ps[:, 0:256], [pb], [hib[tb]])
                sgt, sgtb = sgs[tb % 2], sgsb[tb % 2]
                if hgv & 2:
                    ACT(g, sgt[:], ps[:, 256:512], AF.Exp if hgv & 16 else AF.Sigmoid, [pb], [sgtb])
                if hgv & 4:
                    TT(g, "dve", hgs[:, tb, :], ps[:, 256:512], sgt[:], ALU.mult, [pb, sgtb], [hgb[tb]])
        with scope(g) as s3:
            NH = 4
            R = 4
            qtT = sb(g, s3, [128, NH, T], BF16, "qtT")
            ktT = sb(g, s3, [128, NH, T], BF16, "ktT")
            qtb = nbs(s3, "qt", NH)
            ktb = nbs(s3, "kt", NH)
            kt_tm = sb(g, s3, [128, NB, NH * 128], BF16, "kt_tm")
            kttb = nbs(s3, "kttm", NH)
            t1 = sb(g, s3, [128, T], F32, "t1")
            t2 = sb(g, s3, [128, T], F32, "t2")
            t3 = sb(g, s3, [128, T], F32, "t3")
            t1h, t2h, t3h = nbs(s3, "t1", 2), nbs(s3, "t2", 2), nbs(s3, "t3", 2)
            ebl = sb(g, s3, [128, NH, 32], F32, "ebl")
            eblb = nb(s3, "ebl")
            W = [sb(g, s3, [128, NH, 64], F32, "W") for _ in range(2)]
            Wb = nbs(s3, "W", 2)
            Sbf = [sb(g, s3, [128, NH, 64], BF16, "Sbf") for _ in range(R)]
            Sbfb = nbs(s3, "Sbf", R)
            attm = [sb(g, s3, [128, NH, 128], BF16, "attm") for _ in range(3)]
            attb = nbs(s3, "attm", 3)
            o_tm = [sb(g, s3, [128, NH * 64], F32, "o_tm") for _ in range(3)]
            otb = nbs(s3, "otm", 3)
            osq = sb(g, s3, [128, NH * 64], F32, "osq")
            osqb = nb(s3, "osq")
            og = [sb(g, s3, [128, NH * 64], BF16, "og") for _ in range(2)]
            ogb = nbs(s3, "og", 2)
            sm = [sb(g, s3, [128, 4], F32, "hsm") for _ in range(2)]
            smb = nbs(s3, "hsm", 2)
            wq = [sb(g, s3, [128, DC, 128], BF16, "wq") for _ in range(2)]
            wqb = nbs(s3, "wq", 2)
            wf = [sb(g, s3, [128, DC, 128], BF16, "wf") for _ in range(2)]
            wfb = nbs(s3, "wf", 2)

            def load_head(hd):
                DMA(g, "pool", wf[hd % 2][:], win_cols(g, l, C_HF + hd * 128, C_HF + hd * 128 + 128), (), [wfb[hd % 2]])
                DMA(g, "pool", wq[hd % 2][:], win_cols(g, l, C_HQ + hd * 128, C_HQ + hd * 128 + 128), (), [wqb[hd % 2]])

            load_head(0)
            for hd in range(NH):
                if hd + 1 < NH:
                    load_head(hd + 1)
                w_f, w_fb, w_q, w_qb = wf[hd % 2], wfb[hd % 2], wq[hd % 2], wqb[hd % 2]
                HS = [slice(0, T // 2), slice(T // 2, T)]
                for tc in range(4):
                    ps, pb = psf(g, "proj", [0, 1, 2, 3, 4, 5])
                    mm_fm(g, ps, 128, 512, w_f, w_fb, 0, hT, hTb, tc * 512, pb)
                    ACT(g, t1[:, tc * 512:(tc + 1) * 512], ps[:, :], AF.Sigmoid, [pb], [t1h[tc // 2]])
                for hf in range(2):
                    TS(g, "dve", t1[:, HS[hf]], t1[:, HS[hf]], g.oml[:, l, hd:hd + 1], g.lbv[:, l, hd:hd + 1], ALU.mult, ALU.add,
                       [t1h[hf], g.cb], [t1h[hf]])
                for hf in range(2):
                    ACT(g, t2[:, HS[hf]], t1[:, HS[hf]], AF.Copy, [t1h[hf]], [t2h[hf]], scale=-1.0, bias=1.0)
                for hf in range(2):
                    TS(g, "dve", t1[:, HS[hf]], t1[:, HS[hf]], F_MIN, None, ALU.max, None, [t1h[hf]], [t1h[hf]])
                for hf in range(2):
                    ACT(g, t1[:, HS[hf]], t1[:, HS[hf]], AF.Ln, [t1h[hf]], [t1h[hf]])
                for hf in range(2):
                    P.op("dve", lambda e, hf=hf: e.tensor_tensor_scan(out=t3[:, HS[hf]], data0=g.resetm[:, HS[hf]], data1=t1[:, HS[hf]],
                                                                     initial=0.0, op0=ALU.mult, op1=ALU.add),
                         [t1h[hf], g.cb], [t3h[hf]], T)
                for hf in range(2):
                    TS(g, "dve", t3[:, HS[hf]], t3[:, HS[hf]], -80.0, None, ALU.max, None, [t3h[hf]], [t3h[hf]])
                for hf in range(2):
                    ACT(g, t1[:, HS[hf]], t3[:, HS[hf]], AF.Exp, [t3h[hf]], [t1h[hf]])
                for hf in range(2):
                    CP(g, "pool", ebl[:, hd, hf * 16:(hf + 1) * 16].unsqueeze(2),
                       t1[:, HS[hf]].rearrange("p (c j) -> p c j", j=64)[:, :, 63:64], [t1h[hf]], [eblb])
                for hf in range(2):
                    ACT(g, t3[:, HS[hf]], t3[:, HS[hf]], AF.Exp, [t3h[hf]], [t3h[hf]], scale=-1.0)
                for hf in range(2):
                    TT(g, "dve", ktT[:, hd, HS[hf]], t2[:, HS[hf]], t3[:, HS[hf]], ALU.mult, [t2h[hf], t3h[hf]], [ktb[hd]])
                for tc in range(4):
                    ps, pb = psf(g, "proj", [0, 1, 2, 3, 4, 5])
                    mm_fm(g, ps, 128, 512, w_q, w_qb, 0, hT, hTb, tc * 512, pb)
                    ACT(g, t2[:, tc * 512:(tc + 1) * 512], ps[:, :], AF.Sigmoid, [pb], [t2h[tc // 2]])
                    TT(g, "dve", t2[:, tc * 512:(tc + 1) * 512], ps[:, :], t2[:, tc * 512:(tc + 1) * 512], ALU.mult, [pb, t2h[tc // 2]],
                       [t2h[tc // 2]])
                for hf in range(2):
                    TT(g, "dve", qtT[:, hd, HS[hf]], t2[:, HS[hf]], t1[:, HS[hf]], ALU.mult, [t2h[hf], t1h[hf]], [qtb[hd]])
                for k0 in (0, 8):
                    pt, ptb = psb(g)
                    MM(g, [trf(pt[:, m * 128:(m + 1) * 128], ktT[:, hd, (k0 + m) * 128:(k0 + m + 1) * 128], g.ident[:])
                           for m in range(8)], [ktb[hd], g.cb], [ptb])
                    CP(g, "act" if k0 == 0 else "dve", kt_tm[:, k0:k0 + 8, hd * 128:(hd + 1) * 128],
                       pt[:, :].rearrange("p (m j) -> p m j", j=128), [ptb], [kttb[hd]])

            NCH = 32
            xps = {}

            def emit_X(c):
                tb, pr = c // 2, (c % 2) * 64
                psX, pXb = psf(g, "hgX", [0, 1, 2])
                xps[c] = (psX, pXb)
                MM(g, [mmf(psX[:, hh * 64:(hh + 1) * 64], kt_tm[pr:pr + 64, tb, hh * 128:(hh + 1) * 128],
                           hi_tm[pr:pr + 64, tb, hh * 64:(hh + 1) * 64], True, True) for hh in range(NH)], kttb + [hib[tb]], [pXb])

            def emit_A(tb):
                psA, pAb = psf(g, "hgA", [3])
                MM(g, [mmf(psA[:, hh * 128:(hh + 1) * 128], ktT[:, hh, tb * 128:(tb + 1) * 128],
                           qtT[:, hh, tb * 128:(tb + 1) * 128], True, True) for hh in range(NH)], ktb + qtb, [pAb])
                TT(g, "dve", attm[tb % 3][:], psA[:, :].rearrange("p (h t) -> p h t", t=128),
                   g.maskbd[:].unsqueeze(1).to_broadcast([128, NH, 128]), ALU.mult, [pAb, g.cb], [attb[tb % 3]])

            emit_X(0)
            emit_X(1)
            emit_A(0)
            CP(g, "dve", W[0][:], xps[0][0][:, 0:NH * 64].rearrange("p (h v) -> p h v", v=64), [xps[0][1]], [Wb[0]])
            MS(g, "pool", Sbf[0][:], 0.0, [Sbfb[0]])
            for c in range(NCH):
                tb, half = c // 2, c % 2
                pr = half * 64
                if c + 2 < NCH:
                    emit_X(c + 2)
                if half == 0 and tb + 1 < NB:
                    emit_A(tb + 1)
                if c + 1 < NCH:
                    eb_ = ebl[:, :, c:c + 1].to_broadcast([128, NH, 64])
                    TT(g, "pool", Sbf[(c + 1) % R][:], W[c % 2][:], eb_, ALU.mult, [Wb[c % 2], eblb], [Sbfb[(c + 1) % R]])
                    psX, pXb = xps.pop(c + 1)
                    for hh in range(NH):
                        STT(g, W[(c + 1) % 2][:, hh, :], W[c % 2][:, hh, :], ebl[:, hh, c:c + 1], psX[:, hh * 64:(hh + 1) * 64],
                            ALU.mult, ALU.add, [Wb[c % 2], eblb, pXb], [Wb[(c + 1) % 2]])
                am, amb = attm[tb % 3], attb[tb % 3]
                ot, otbuf = o_tm[tb % 3], otb[tb % 3]
                psO, pOb = psf(g, "hgO", [4, 5])
                fns = []
                for hh in range(NH):
                    fns.append(mmf(psO[0:64, hh * 64:(hh + 1) * 64], am[pr:pr + 64, hh, pr:pr + 64],
                                   hi_tm[pr:pr + 64, tb, hh * 64:(hh + 1) * 64], True, False))
                    fns.append(mmf(psO[0:64, hh * 64:(hh + 1) * 64], qtT[:, hh, c * 64:(c + 1) * 64], Sbf[c % R][:, hh, :], False, True))
                MM(g, fns, [amb, hib[tb], Sbfb[c % R]] + qtb, [pOb])
                CP(g, "act", ot[pr:pr + 64, :], psO[0:64, 0:NH * 64], [pOb], [otbuf])
                if half == 1:
                    sm_, sm_b = sm[tb % 2], smb[tb % 2]
                    og_, og_b = og[tb % 2], ogb[tb % 2]
                    TT(g, "pool", osq[:], ot[:], ot[:], ALU.mult, [otbuf], [osqb])
                    ss = sm_[:, 0:NH]
                    P.op("dve", lambda e, ss=ss: e.tensor_reduce(out=ss, in_=osq[:].rearrange("p (h d) -> p h d", d=64), axis=AX.X,
                                                                 op=ALU.add), [osqb], [sm_b], NH * 64)
                    ACT(g, ss, ss, AF.Ln, [sm_b], [sm_b], scale=1.0 / 64, bias=EPS)
                    ACT(g, ss, ss, AF.Exp, [sm_b], [sm_b], scale=-0.5)
                    o3 = ot[:].rearrange("p (h d) -> p h d", d=64)
                    TT(g, "dve", o3, o3, ss.unsqueeze(2).to_broadcast([128, NH, 64]), ALU.mult, [otbuf, sm_b], [otbuf])
                    TT(g, "dve", o3, o3, onb[:].unsqueeze(1).to_broadcast([128, NH, 64]), ALU.mult, [otbuf, onbb], [otbuf])
                    TT(g, "dve", og_[:], ot[:], hgs[:, tb, :], ALU.mult, [otbuf, hgb[tb]], [og_b])
                    pt, ptb = psb(g)
                    MM(g, [trf(pt[:, m * 128:(m + 1) * 128], og_[:, m * 128:(m + 1) * 128], g.ident[:]) for m in range(2)],
                       [og_b, g.cb], [ptb])
                    CP(g, "act", ohT[:, 0:2, tb * 128:(tb + 1) * 128], pt[:, 0:256].rearrange("p (m j) -> p m j", j=128), [ptb],
                       [ohb[0][tb // 4], ohb[1][tb // 4]])


def phase_mix(g, l, hT, hTb, osbT, osbb, odT, odb, ohT, ohb, src_ap, src_bufs):
    P = g.P
    with scope(g) as st:
        mixT = sb(g, st, [128, DC, T], BF16, "mixT")
        mxb = nbs(st, "mix", DC, 4)
        wg = [sb(g, st, [128, DC, 3, 128], BF16, "wg") for _ in range(2)]
        wgb = nbs(st, "wg", 2, 3)
        wy = [sb(g, st, [128, 8, 128], BF16, "wy") for _ in range(2)]
        wyb = nbs(st, "wy", 2, 3)
        sg = [sb(g, st, [128, 512], F32, "sg") for _ in range(2)]
        sgb = nbs(st, "sg", 2)
        acc = [sb(g, st, [128, 512], F32, "acc") for _ in range(2)]
        accb = nbs(st, "acc", 2)
        tm = [sb(g, st, [128, 512], F32, "tm") for _ in range(2)]
        tmb = nbs(st, "tm", 2)
        k = 0

        def load_dc(dc):
            w_, w_b = wg[dc % 2], wgb[dc % 2]
            y_, y_b = wy[dc % 2], wyb[dc % 2]
            for gi in range(3):
                c0 = C_G + gi * 1024 + dc * 128
                DMA(g, "pool", w_[:, :, gi, :], win_cols(g, l, c0, c0 + 128), (), [w_b[gi]])
            DMA(g, "pool", y_[:, 0:3, :], g.w_sb[l].rearrange("(c p) n -> p c n", p=128)[:, :, dc * 128:(dc + 1) * 128], (), [y_b[0]])
            DMA(g, "pool", y_[:, 3:6, :], g.w_dsa[l].rearrange("(c p) n -> p c n", p=128)[:, :, dc * 128:(dc + 1) * 128], (), [y_b[1]])
            DMA(g, "pool", y_[:, 6:8, :], g.w_hg[l].rearrange("(c p) n -> p c n", p=128)[:, :, dc * 128:(dc + 1) * 128], (), [y_b[2]])

        load_dc(0)
        wo = sb(g, st, [128, DC, D], BF16, "wo")
        wob = nbs(st, "wo", 2)
        for dc in range(DC):
            w_, w_b = wg[dc % 2], wgb[dc % 2]
            y_, y_b = wy[dc % 2], wyb[dc % 2]
            if dc + 1 < DC:
                load_dc(dc + 1)
            else:
                for nh in range(2):
                    DMA(g, "pool", wo[:, :, nh * 512:(nh + 1) * 512],
                        g.w_out[l].rearrange("(c p) n -> p c n", p=128)[:, :, nh * 512:(nh + 1) * 512], (), [wob[nh]])
            for tc in range(4):
                a_, a_b = acc[k % 2], accb[k % 2]
                for gi, (oT, obufs, nch, c0) in enumerate(((osbT, osbb, 3, 0), (odT, odb, 3, 3), (ohT, ohb, 2, 6))):
                    psG, pGb = psf(g, "mxG", [0, 1, 2])
                    MM(g, [mmf(psG[:, :], w_[:, c, gi, :], hT[:, c, tc * 512:(tc + 1) * 512], c == 0, c == DC - 1) for c in range(DC)],
                       [w_b[gi]] + hTb[tc * 4:tc * 4 + 4], [pGb])
                    s_, s_b = sg[(k * 3 + gi) % 2], sgb[(k * 3 + gi) % 2]
                    ACT(g, s_[:], psG[:, :], AF.Sigmoid, [pGb], [s_b])
                    psY, pYb = psf(g, "mxY", [3, 4, 5])
                    MM(g, [mmf(psY[:, :], y_[:, c0 + c, :], oT[:, c, tc * 512:(tc + 1) * 512], c == 0, c == nch - 1) for c in range(nch)],
                       [y_b[gi]] + [obufs[c][tc] for c in range(nch)], [pYb])
                    if gi == 0:
                        TT(g, "dve", a_[:], psY[:, :], s_[:], ALU.mult, [pYb, s_b], [a_b])
                    else:
                        t_, t_b = tm[gi % 2], tmb[gi % 2]
                        TT(g, "dve", t_[:], psY[:, :], s_[:], ALU.mult, [pYb, s_b], [t_b])
                        if gi == 1:
                            TT(g, "pool", a_[:], a_[:], t_[:], ALU.add, [a_b, t_b], [a_b])
                        else:
                            TT(g, "pool", mixT[:, dc, tc * 512:(tc + 1) * 512], a_[:], t_[:], ALU.add, [a_b, t_b], [mxb[dc][tc]])
                k += 1
        xbs = [sb(g, st, [128, D], F32, "xb") for _ in range(4)]
        xbb = nbs(st, "xb", 4)
        nctx = norm_setup(g, st, g.norm_mlp[l:l + 1, :])
        for tb in range(NB):
            xb, xbuf = xbs[tb % 4], xbb[tb % 4]
            DMA(g, "sp", xb[:], src_ap[tb * 128:(tb + 1) * 128, :], [src_bufs[tb]], [xbuf])
            for nh in range(2):
                ps, pb = psf(g, "mxO", [0, 1, 2, 3, 4, 5])
                MM(g, [mmf(ps[:, :], mixT[:, c, tb * 128:(tb + 1) * 128], wo[:, c, nh * 512:(nh + 1) * 512], c == 0, c == DC - 1)
                       for c in range(DC)], [wob[nh]] + [mxb[c][tb // 4] for c in range(DC)], [pb])
                xs = xb[:, nh * 512:(nh + 1) * 512]
                TT(g, "dve", xs, ps[:, :], xs, ALU.add, [pb, xbuf], [xbuf])
            DMA(g, "sp", g.xres_d[tb * 128:(tb + 1) * 128, :], xb[:], [xbuf], [g.xres_b[tb]])
            norm_block(g, nctx, xb, xbuf, tb, hT, hTb)


def phase_ffn(g, l, hT, hTb, dst_ap, dst_bufs):
    for half in range(2):
        last = half == 1
        with scope(g) as st:
            uT = sb(g, st, [128, 16, T], BF16, "uT")
            ub = nbs(st, "uT", 16, 4)
            wd = sb(g, st, [128, 16, D], BF16, "wd")
            wdb = nbs(st, "wd", 2)
            wdv = g.w_down[l].rearrange("(f p) n -> p f n", p=128)
            wu = [sb(g, st, [128, DC, 512], BF16, "wu") for _ in range(2)]
            wub = nbs(st, "wu", 2)
            rt = [sb(g, st, [128, 512], BF16, "rt") for _ in range(2)]
            rtb = nbs(st, "rt", 2)
            k = 0

            def load_wu(g4):
                c0 = half * 2048 + g4 * 512
                DMA(g, "pool", wu[g4 % 2][:], g.w_up[l].rearrange("(c p) n -> p c n", p=128)[:, :, c0:c0 + 512], (), [wub[g4 % 2]])

            load_wu(0)
            for g4 in range(4):
                w_, w_b = wu[g4 % 2], wub[g4 % 2]
                if g4 + 1 < 4:
                    load_wu(g4 + 1)
                if g4 == 1:
                    for nh in range(2):
                        DMA(g, "pool", wd[:, :, nh * 512:(nh + 1) * 512], wdv[:, half * 16:(half + 1) * 16, nh * 512:(nh + 1) * 512], (),
                            [wdb[nh]])
                for fcl in range(4):
                    fc = g4 * 4 + fcl
                    for tc in range(4):
                        ps, pb = psf(g, "proj", [0, 1, 2, 3, 4, 5])
                        mm_fm(g, ps, 128, 512, w_, w_b, fcl * 128, hT, hTb, tc * 512, pb)
                        r_, r_b = rt[k % 2], rtb[k % 2]
                        k += 1
                        ACT(g, r_[:], ps[:, :], AF.Relu, [pb], [r_b])
                        TT(g, "pool", uT[:, fc, tc * 512:(tc + 1) * 512], r_[:], r_[:], ALU.mult, [r_b], [ub[fc][tc]])
            xbs = [sb(g, st, [128, D], F32, "xb") for _ in range(4)]
            xbb = nbs(st, "xb", 4)
            nctx = norm_setup(g, st, g.norm_mix[l + 1:l + 2, :]) if (last and l + 1 < DEPTH) else None
            for tb in range(NB):
                xb, xbuf = xbs[tb % 4], xbb[tb % 4]
                DMA(g, "sp", xb[:], g.xres_d[tb * 128:(tb + 1) * 128, :], [g.xres_b[tb]], [xbuf])
                for nh in range(2):
                    ps, pb = psf(g, "proj", [0, 1, 2, 3, 4, 5])
                    MM(g, [mmf(ps[:, :], uT[:, fc, tb * 128:(tb + 1) * 128], wd[:, fc, nh * 512:(nh + 1) * 512], fc == 0, fc == 15)
                           for fc in range(16)], [wdb[nh]] + [ub[fc][tb // 4] for fc in range(16)], [pb])
                    xs = xb[:, nh * 512:(nh + 1) * 512]
                    TT(g, "dve", xs, ps[:, :], xs, ALU.add, [pb, xbuf], [xbuf])
                if last:
                    DMA(g, "sp", dst_ap[tb * 128:(tb + 1) * 128, :], xb[:], [xbuf], [dst_bufs[tb]])
                    if nctx is not None:
                        norm_block(g, nctx, xb, xbuf, tb, hT, hTb)
                else:
                    DMA(g, "sp", g.xres_d[tb * 128:(tb + 1) * 128, :], xb[:], [xbuf], [g.xres_b[tb]])


def dump_t(g, name, t, ncol):
    if g.dump == name:
        g.P.barrier()
        DMA(g, "sp", g.dbg_d[:, 0:ncol], t, [], [g.dbgb])
        g.P.wait_bufs("sp", [g.dbgb])
        g.P.barrier()


def build_layer(g, l):
    if l > 0 and g.stage < 7:
        return
    src_ap, src_bufs = (g.x_d, g.xin_b) if l == 0 else (g.xres_d, g.xres_b)
    with scope(g) as ls:
        hT, hTb = g.hT, g.hTb
        if l == 0:
            with scope(g) as st:
                norm_T(g, st, src_ap, src_bufs, g.norm_mix[l:l + 1, :], hT, hTb)
        if l == 0:
            dump_t(g, "hT", hT[:].rearrange("p c t -> p (c t)"), 8 * T)
        if g.stage < 2:
            return
        with scope(g) as ms:
            osbT = sb(g, ms, [128, 3, T], BF16, "osbT")
            odT = sb(g, ms, [128, 3, T], BF16, "odT")
            ohT = sb(g, ms, [128, 2, T], BF16, "ohT")
            osbb = nbs(ms, "osb", 3, 4)
            odb = nbs(ms, "od", 3, 4)
            ohb = nbs(ms, "oh", 2, 4)
            phase_sb(g, l, hT, hTb, osbT, osbb)
            if l == 0:
                dump_t(g, "osbT", osbT[:].rearrange("p c t -> p (c t)"), 3 * T)
            if g.stage < 3:
                return
            phase_dsa(g, l, hT, hTb, odT, odb)
            if l == 0:
                dump_t(g, "odT", odT[:].rearrange("p c t -> p (c t)"), 3 * T)
            if g.stage < 4:
                return
            phase_hgrn(g, l, hT, hTb, ohT, ohb)
            if l == 0:
                dump_t(g, "ohT", ohT[:].rearrange("p c t -> p (c t)"), 2 * T)
            if g.stage < 5:
                return
            phase_mix(g, l, hT, hTb, osbT, osbb, odT, odb, ohT, ohb, src_ap, src_bufs)
        if g.stage < 6:
            return
        if l == DEPTH - 1:
            phase_ffn(g, l, hT, hTb, g.out_d, g.out_b)
        else:
            phase_ffn(g, l, hT, hTb, g.xres_d, g.xres_b)


_NC_CACHE = {}


def rope_tables():
    half = 8
    inv = 500000.0 ** (-(np.arange(half, dtype=np.float32) * 2.0) / 16.0)
    ang = np.arange(T, dtype=np.float32)[:, None] * inv[None, :].astype(np.float32)
    return np.cos(ang).astype(np.float32), np.sin(ang).astype(np.float32)


def kernel(x, norm_mix, w_in, qn_dsa, kn_dsa, hgrn_lb, hgrn_onorm, w_br_sb, w_br_dsa, w_br_hgrn, w_out, norm_mlp, w_up, w_down):
    if "nc" not in _NC_CACHE:
        _NC_CACHE["nc"] = build_two_pass()
    nc = _NC_CACHE["nc"]
    f = lambda a: np.ascontiguousarray(np.asarray(a, dtype=np.float32))
    cs, sn = rope_tables()
    shared = dict(norm_mix=f(norm_mix), w_in=f(w_in), qn_dsa=f(qn_dsa), kn_dsa=f(kn_dsa), hgrn_lb=f(hgrn_lb),
                  hgrn_onorm=f(hgrn_onorm), w_br_sb=f(w_br_sb), w_br_dsa=f(w_br_dsa), w_br_hgrn=f(w_br_hgrn),
                  w_out=f(w_out), norm_mlp=f(norm_mlp), w_up=f(w_up), w_down=f(w_down), rope_cos=cs, rope_sin=sn)
    xs = f(x)
    in_maps = [dict(shared, x=xs[b]) for b in range(8)]
    res = run_bass_kernel_spmd(nc, in_maps, core_ids=list(range(8)))
    return np.stack([np.asarray(r["out"], dtype=np.float32) for r in res.results], axis=0)
```

```python
import math
import numpy as np
from contextlib import ExitStack, contextmanager
import concourse.bass as bass
import concourse.mybir as mybir
from concourse.bass_utils import run_bass_kernel_spmd

F32 = mybir.dt.float32
BF16 = mybir.dt.bfloat16
AF = mybir.ActivationFunctionType
ALU = mybir.AluOpType
AX = mybir.AxisListType

T = 2048
D = 1024
NB = 16
DC = 8
DIN = 6856
DFF = 4096
DEPTH = 2
EPS = 1e-6
F_MIN = 1e-12
IDX_SCALE = (64 * 8) ** -0.5
C_SQ, C_SK, C_SV = 0, 384, 768
C_DQ, C_DK, C_DV = 1152, 1536, 1600
C_IQ, C_IK, C_IW = 1664, 2176, 2240
C_HQ, C_HF, C_HI, C_HG = 2248, 2760, 3272, 3528
C_G = 3784
N_BISECT = 16


class Buf:
    __slots__ = ("name", "w", "r", "dsem", "excl")

    def __init__(self, name, excl=False):
        self.name = name
        self.w = None
        self.r = {}
        self.dsem = None
        self.excl = excl


class Prog:
    ENG = ("pe", "act", "dve", "pool", "sp")
    CLEAR_NS = 330.0
    FILL_NS = {"dve": 66.0, "act": 190.0, "pool": 125.0}
    EST = {"dve": (60.0, 0.26), "act": (185.0, 0.83), "pool": (120.0, 0.8), "pe": (0.0, 0.0), "sp": (0.0, 0.0)}

    def __init__(self, nc, stack, needed=None):
        self.needed = needed
        self.used = set()
        self.remap = {}
        self.sig = {}
        self.fill = {}
        self.nc = nc
        self.stack = stack
        self.eng = {"pe": nc.tensor, "act": nc.scalar, "dve": nc.vector, "pool": nc.gpsimd, "sp": nc.sync}
        self.cnt = {e: 0 for e in self.ENG}
        self.known = {e: {} for e in self.ENG}
        self.sems = {}
        self.semval = {}
        for e in ("pe", "act", "dve", "pool"):
            self.sems["E_" + e] = stack.enter_context(nc.semaphore("sem_" + e))
            self.semval["E_" + e] = 0
        self.ndsem = 0
        self.free_dsems = []
        self.nwaits = 0
        self.tcum = {e: 0.0 for e in self.ENG}
        self.tend = {e: {} for e in self.ENG}

    def _dsem(self, buf):
        if buf.dsem is None:
            if self.free_dsems:
                key = self.free_dsems.pop()
            else:
                key = "D%d" % self.ndsem
                self.ndsem += 1
                self.sems[key] = self.stack.enter_context(self.nc.semaphore("dsem%d" % (self.ndsem - 1)))
                self.semval[key] = 0
            buf.dsem = key
        return buf.dsem

    def release(self, bufs):
        for b in bufs:
            if b.dsem is not None:
                self.free_dsems.append(b.dsem)
                b.dsem = None

    def _waits(self, eng, deps):
        need = {}
        own = "E_" + eng
        for (k, v) in deps:
            if eng == "pe" and k == "E_pe":
                continue
            if k == own and eng in ("act", "dve", "pool"):
                te = self.tend[eng].get(v)
                if te is not None and eng in self.fill:
                    gap = self.CLEAR_NS - (self.tcum[eng] - te)
                    if gap > 0:
                        n = int(math.ceil(gap / self.FILL_NS[eng]))
                        for _ in range(n):
                            self.fill[eng](self.eng[eng])
                        self.tcum[eng] += n * self.FILL_NS[eng]
                        self.nfill = getattr(self, "nfill", 0) + n
                continue
            if v > need.get(k, 0):
                need[k] = v
        out = []
        kn = self.known[eng]
        for k, v in need.items():
            if kn.get(k, 0) < v:
                kn[k] = v
                out.append((k, v))
        return out

    @staticmethod
    def _deps(reads, writes):
        deps = []
        for b in reads:
            if b.w is not None:
                deps.append(b.w)
            if b.excl:
                deps.extend(b.r.items())
        for b in writes:
            if b.w is not None:
                deps.append(b.w)
            deps.extend(b.r.items())
        return deps

    def _emit_waits(self, eng, waits):
        e = self.eng[eng]
        for (k, v) in waits:
            if k.startswith("E_"):
                self.used.add((k, v))
                if self.needed is not None:
                    v = self.remap[(k, v)]
            e.wait_ge(self.sems[k], v)
            self.nwaits += 1

    def _mark(self, ev, reads, writes):
        k, v = ev
        for b in reads:
            if b.r.get(k, 0) < v:
                b.r[k] = v
        for b in writes:
            b.w = ev
            b.r = {}

    def op(self, eng, fn, reads=(), writes=(), n=0):
        self.group(eng, [fn], reads, writes, n)

    def group(self, eng, fns, reads=(), writes=(), n=0):
        self._emit_waits(eng, self._waits(eng, self._deps(reads, writes)))
        e = self.eng[eng]
        for fn in fns[:-1]:
            fn(e)
        self.cnt[eng] += 1
        ov, pe_ = self.EST[eng]
        self.tcum[eng] += ov + pe_ * n
        td = self.tend[eng]
        td[self.cnt[eng]] = self.tcum[eng]
        if len(td) > 64:
            for k_ in sorted(td)[:32]:
                del td[k_]
        key = "E_" + eng
        self.semval[key] = self.cnt[eng]
        if self.needed is None or (key, self.cnt[eng]) in self.needed:
            self.sig[key] = self.sig.get(key, 0) + 1
            self.remap[(key, self.cnt[eng])] = self.sig[key]
            fns[-1](e).then_inc(self.sems[key], 1)
        else:
            fns[-1](e)
        self._mark((key, self.cnt[eng]), reads, writes)

    def dma(self, eng, fn, reads=(), writes=()):
        assert len(writes) == 1
        wb = writes[0]
        deps = self._deps(reads, writes)
        if eng == "pool" and getattr(self, "last_swdge", None) is not None:
            deps.append(self.last_swdge)
        self._emit_waits(eng, self._waits(eng, deps))
        key = self._dsem(wb)
        self.semval[key] += 16
        fn(self.eng[eng]).then_inc(self.sems[key], 16)
        if eng == "pool":
            self.last_swdge = (key, self.semval[key])
        self._mark((key, self.semval[key]), reads, writes)

    def barrier(self):
        deps = [(k, v) for k, v in self.semval.items() if v > 0]
        for eng in self.ENG:
            self._emit_waits(eng, self._waits(eng, deps))

    def wait_bufs(self, eng, bufs):
        deps = []
        for b in bufs:
            if b.w is not None:
                deps.append(b.w)
            deps.extend(b.r.items())
        self._emit_waits(eng, self._waits(eng, deps))


class G:
    pass


def bufs(prefix, *dims):
    if len(dims) == 1:
        return [Buf("%s%d" % (prefix, i)) for i in range(dims[0])]
    return [bufs("%s%d_" % (prefix, i), *dims[1:]) for i in range(dims[0])]


def build_program(stage=99, dump=None, needed=None):
    nc = bass.Bass("TRN2", target_bir_lowering=False)
    g = G()
    g.nc = nc
    g.stage = stage
    g.dump = dump
    import os
    g.ntl = int(os.environ.get("NTL", "9"))
    g.dbg_d = None
    if dump is not None:
        g.dbg_d = nc.dram_tensor("dbg", [128, 8 * T], BF16, kind="ExternalOutput").ap()
        g.dbgb = Buf("dbg")
    dt = lambda name, shape, kind, d=F32: nc.dram_tensor(name, shape, d, kind=kind).ap()
    g.x_d = dt("x", [T, D], "ExternalInput")
    g.norm_mix = dt("norm_mix", [DEPTH, D], "ExternalInput")
    g.w_in = dt("w_in", [DEPTH, D, DIN], "ExternalInput")
    g.qn = dt("qn_dsa", [DEPTH, 64], "ExternalInput")
    g.kn = dt("kn_dsa", [DEPTH, 64], "ExternalInput")
    g.lb_d = dt("hgrn_lb", [DEPTH, 512], "ExternalInput")
    g.onorm = dt("hgrn_onorm", [DEPTH, 64], "ExternalInput")
    g.w_sb = dt("w_br_sb", [DEPTH, 384, D], "ExternalInput")
    g.w_dsa = dt("w_br_dsa", [DEPTH, 384, D], "ExternalInput")
    g.w_hg = dt("w_br_hgrn", [DEPTH, 256, D], "ExternalInput")
    g.w_out = dt("w_out", [DEPTH, D, D], "ExternalInput")
    g.norm_mlp = dt("norm_mlp", [DEPTH, D], "ExternalInput")
    g.w_up = dt("w_up", [DEPTH, D, DFF], "ExternalInput")
    g.w_down = dt("w_down", [DEPTH, DFF, D], "ExternalInput")
    g.cs_d = dt("rope_cos", [T, 8], "ExternalInput")
    g.sn_d = dt("rope_sin", [T, 8], "ExternalInput")
    g.out_d = dt("out", [T, D], "ExternalOutput")
    g.xres_d = dt("xres", [T, D], "Internal")
    g.xin_b = bufs("xin", NB)
    g.xres_b = bufs("xres", NB)
    g.out_b = bufs("outb", NB)

    with ExitStack() as gs:
        P = Prog(nc, gs, needed)
        g.P = P
        fa = gs.enter_context(nc.sbuf_tensor("fill_a", [128, 2], F32))
        fd = gs.enter_context(nc.sbuf_tensor("fill_d", [128, 2], F32))
        nc.vector.memset(fd[:], 0.0)
        nc.vector.memset(fa[:], 0.0)
        P.fill["dve"] = lambda e: e.memset(fd[:, 0:1], 0.0)
        P.fill["act"] = lambda e: e.activation(out=fa[:, 0:1], in_=fa[:, 1:2], func=AF.Copy)
        fp = gs.enter_context(nc.sbuf_tensor("fill_p", [128, 2], F32))
        nc.gpsimd.memset(fp[:], 0.0)
        P.fill["pool"] = lambda e: e.memset(fp[:, 0:1], 0.0)
        g.uid = 0
        g.psF = [gs.enter_context(nc.psum_tensor("psF%d" % i, [128, 512], F32)) for i in range(6)]
        g.psFb = [Buf("psF%d" % i, excl=True) for i in range(6)]
        g.psB = [gs.enter_context(nc.psum_tensor("psB%d" % i, [128, 1024], BF16)) for i in range(2)]
        g.psBb = [Buf("psB%d" % i, excl=True) for i in range(2)]
        g.rotc = {}
        build_consts(g, gs)
        g.hT = gs.enter_context(nc.sbuf_tensor("hT_glob", [128, DC, T], BF16))
        g.hTb = bufs("hT", NB)
        g.pre = gs.enter_context(nc.sbuf_tensor("pre_w", [128, DC, 512], BF16))
        g.preb = Buf("pre_w")
        prefetch(g, ("sq", 0), win_cols(g, 0, C_SQ, C_SQ + 384), 384)
        for l in range(DEPTH):
            if g.stage >= 1:
                build_layer(g, l)
        if g.dbg_d is not None:
            P.wait_bufs("sp", [g.dbgb])
        P.wait_bufs("sp", g.out_b)
        P.barrier()
        g.used = P.used
        print("ops", P.cnt, "signals", P.sig, "waits", P.nwaits, "fillers", getattr(P, "nfill", 0), "dsems", P.ndsem, flush=True)
    return nc, P.used


def build_two_pass(stage=99, dump=None):
    _, used = build_program(stage, dump, None)
    nc, _ = build_program(stage, dump, used)
    return nc


def pipeline(stages, ntiles):
    ns = len(stages)
    for t in range(ntiles + ns - 1):
        for k, f in enumerate(stages):
            i = t - k
            if 0 <= i < ntiles:
                f(i)


def prefetch(g, tag, src, ncols):
    DMA(g, "pool", g.pre[:, :, 0:ncols], src, (), [g.preb])
    g.pre_tag = tag


def take_pre(g, tag):
    if getattr(g, "pre_tag", None) == tag:
        g.pre_tag = None
        return True
    return False


def rot(g, role, items):
    i = g.rotc.get(role, 0)
    g.rotc[role] = i + 1
    return items[i % len(items)]


def psf(g, role, banks):
    b = rot(g, role, banks)
    return g.psF[b], g.psFb[b]


def psb(g, role="pb"):
    b = rot(g, role, [0, 1])
    return g.psB[b], g.psBb[b]


@contextmanager
def scope(g):
    st = ExitStack()
    st.tbufs = []
    try:
        yield st
    finally:
        g.P.barrier()
        g.P.release(st.tbufs)
        st.close()


def sb(g, st, shape, dtype, name=None):
    g.uid += 1
    return st.enter_context(g.nc.sbuf_tensor("%s_%d" % (name or "t", g.uid), shape, dtype))


def nb(st, name):
    b = Buf(name)
    st.tbufs.append(b)
    return b


def nbs(st, prefix, *dims):
    r = bufs(prefix, *dims)

    def flat(x):
        if isinstance(x, Buf):
            st.tbufs.append(x)
        else:
            for y in x:
                flat(y)
    flat(r)
    return r


def _fs(ap):
    try:
        return int(ap.free_size())
    except Exception:
        return 0


def ACT(g, out, in_, func, reads, writes, **kw):
    g.P.op("act", lambda e: e.activation(out=out, in_=in_, func=func, **kw), reads, writes, _fs(out))


def TT(g, eng, out, in0, in1, op, reads, writes):
    g.P.op(eng, lambda e: e.tensor_tensor(out=out, in0=in0, in1=in1, op=op), reads, writes, _fs(out))


def TS(g, eng, out, in0, s1, s2, op0, op1, reads, writes, **kw):
    if op1 is None:
        s2 = 0.0 if isinstance(s1, (int, float)) else g.zc[0:in0.shape[0], 0:1]
        g.P.op(eng, lambda e: e.tensor_scalar(out=out, in0=in0, scalar1=s1, scalar2=s2, op0=op0, op1=ALU.add, **kw), reads, writes, _fs(out))
    else:
        g.P.op(eng, lambda e: e.tensor_scalar(out=out, in0=in0, scalar1=s1, scalar2=s2, op0=op0, op1=op1, **kw), reads, writes, _fs(out))


def STT(g, out, in0, scalar, in1, op0, op1, reads, writes):
    g.P.op("dve", lambda e: e.scalar_tensor_tensor(out=out, in0=in0, scalar=scalar, in1=in1, op0=op0, op1=op1), reads, writes, _fs(out))


def CP(g, eng, out, in_, reads, writes):
    if eng == "act":
        g.P.op("act", lambda e: e.activation(out=out, in_=in_, func=AF.Copy), reads, writes, _fs(out))
    else:
        g.P.op(eng, lambda e: e.tensor_copy(out, in_), reads, writes, _fs(out))


def MS(g, eng, ap, val, writes):
    g.P.op(eng, lambda e: e.memset(ap, val), (), writes, _fs(ap))


def ASEL(g, out, in_, pattern, cmp, fill, base, cm, reads, writes):
    g.P.op("pool", lambda e: e.affine_select(out=out, in_=in_, pattern=pattern, compare_op=cmp, fill=fill, base=base,
                                             channel_multiplier=cm), reads, writes, _fs(out))


def MM(g, outs_fns, reads, writes):
    g.P.group("pe", outs_fns, reads, writes)


def mmf(out, lhsT, rhs, start, stop):
    return lambda e: e.matmul(out, lhsT=lhsT, rhs=rhs, start=start, stop=stop)


def trf(out, in_, ident):
    return lambda e: e.transpose(out, in_, ident)


def DMA(g, eng, out, in_, reads, writes, **kw):
    g.P.dma(eng, lambda e: e.dma_start(out=out, in_=in_, **kw), reads, writes)


def build_consts(g, gs):
    nc = g.nc
    mk = lambda name, shape, d: gs.enter_context(nc.sbuf_tensor(name, shape, d))
    g.ident = mk("ident", [128, 128], BF16)
    g.negtri = mk("negtri", [128, 128], BF16)
    g.negones = mk("negones", [128, 128], BF16)
    g.onesb = mk("onesb", [128, 128], BF16)
    g.maskbd = mk("maskbd", [128, 128], F32)
    g.onesf = mk("onesf", [128, 128], F32)
    g.resetm = mk("resetm", [128, T], BF16)
    g.zc = mk("zc", [128, 1], F32)
    g.negbig = mk("negbig", [128, 1], F32)
    g.cs = mk("cs", [128, NB, 8], F32)
    g.sn = mk("sn", [128, NB, 8], F32)
    g.lbraw = mk("lbraw", [128, 2, 4], F32)
    g.lbv = mk("lbv", [128, 2, 4], F32)
    g.oml = mk("oml", [128, 2, 4], F32)
    g.cb = Buf("consts")
    g.csb = Buf("cs")
    g.snb = Buf("sn")
    g.lbb = Buf("lbraw")
    cb = [g.cb]
    MS(g, "pool", g.onesb[:], 1.0, cb)
    MS(g, "pool", g.negones[:], -1.0, cb)
    MS(g, "pool", g.onesf[:], 1.0, cb)
    MS(g, "pool", g.zc[:], 0.0, cb)
    MS(g, "pool", g.negbig[:], -1e29, cb)
    ASEL(g, g.ident[:], g.onesb[:], [[1, 128]], ALU.is_equal, 0.0, 0, -1, cb, cb)
    ASEL(g, g.negtri[:], g.negones[:], [[-1, 128]], ALU.is_ge, 0.0, 0, 1, cb, cb)
    ASEL(g, g.maskbd[:], g.onesf[:], [[1, 128]], ALU.is_ge, 0.0, 0, -1, cb, cb)
    MS(g, "pool", g.maskbd[0:64, 64:128], 0.0, cb)
    g.ones512 = mk("ones512", [128, 512], BF16)
    g.mlt = mk("mlt", [128, 512], BF16)
    MS(g, "pool", g.ones512[:], 1.0, cb)
    ASEL(g, g.mlt[:], g.ones512[:], [[1, 512]], ALU.is_gt, 0.0, 0, -1, cb, cb)
    g.caus01 = mk("caus01", [128, 128], F32)
    g.negfill = mk("negfill", [128, 128], F32)
    ASEL(g, g.caus01[:], g.onesf[:], [[-1, 128]], ALU.is_ge, 0.0, 0, 1, cb, cb)
    TS(g, "pool", g.negfill[:], g.caus01[:], -1.0, 1e30, ALU.add, ALU.mult, cb, cb)
    MS(g, "pool", g.resetm[:], 1.0, cb)
    MS(g, "pool", g.resetm[:].rearrange("p (c j) -> p c j", j=64)[:, :, 0:1], 0.0, cb)
    DMA(g, "sp", g.cs[:], g.cs_d.rearrange("(b p) i -> p b i", p=128), (), [g.csb])
    DMA(g, "sp", g.sn[:], g.sn_d.rearrange("(b p) i -> p b i", p=128), (), [g.snb])
    DMA(g, "sp", g.lbraw[:], g.lb_d.rearrange("l (h k) -> k l h", k=128), (), [g.lbb], allow_slow_non_contiguous=True)
    MS(g, "dve", g.lbv[:], 0.0, cb)
    TT(g, "dve", g.lbv[:, 1, :], g.lbraw[:, 1, :], g.lbraw[:, 0, :], ALU.subtract, [g.lbb], cb)
    ACT(g, g.lbv[:, 1, :], g.lbv[:, 1, :], AF.Sigmoid, cb, cb)
    TS(g, "dve", g.oml[:], g.lbv[:], -1.0, 1.0, ALU.mult, ALU.add, cb, cb)
    g.P.barrier()


class NormCtx:
    pass


def norm_setup(g, st, gain_d_row):
    c = NormCtx()
    c.gain = sb(g, st, [128, D], F32, "gain")
    c.gb = nb(st, "gain")
    DMA(g, "sp", c.gain[:], gain_d_row.to_broadcast([128, D]), (), [c.gb])
    c.junk = sb(g, st, [128, D], BF16, "junk")
    c.jb = nb(st, "junk")
    c.hbs = [sb(g, st, [128, D], BF16, "hb") for _ in range(2)]
    c.hbb = nbs(st, "hb", 2)
    c.ss = sb(g, st, [128, NB], F32, "ss")
    c.ssb = nbs(st, "ss", NB)
    MS(g, "dve", c.ss[:], 0.0, c.ssb)
    return c


def norm_block(g, c, xb, xbuf, tb, hT, hTb):
    hb, hbuf = c.hbs[tb % 2], c.hbb[tb % 2]
    s1 = c.ss[:, tb:tb + 1]
    ACT(g, c.junk[:], xb[:], AF.Square, [xbuf, c.ssb[tb]], [c.jb, c.ssb[tb]], accum_out=s1)
    ACT(g, s1, s1, AF.Ln, [c.ssb[tb]], [c.ssb[tb]], scale=1.0 / D, bias=EPS)
    ACT(g, s1, s1, AF.Exp, [c.ssb[tb]], [c.ssb[tb]], scale=-0.5)
    STT(g, hb[:], xb[:], s1, c.gain[:], ALU.mult, ALU.mult, [xbuf, c.ssb[tb], c.gb], [hbuf])
    for half in range(2):
        pt, ptb = psb(g)
        MM(g, [trf(pt[:, m * 128:(m + 1) * 128], hb[:, (half * 4 + m) * 128:(half * 4 + m + 1) * 128], g.ident[:])
               for m in range(4)], [hbuf, g.cb], [ptb])
        CP(g, "act" if half == 0 else "dve", hT[:, half * 4:half * 4 + 4, tb * 128:(tb + 1) * 128],
           pt[:, 0:512].rearrange("p (m j) -> p m j", j=128), [ptb], [hTb[tb]])


def norm_T(g, st, src_ap, src_bufs, gain_d_row, hT, hTb):
    c = norm_setup(g, st, gain_d_row)
    xbs = [sb(g, st, [128, D], F32, "xb") for _ in range(4)]
    xbb = nbs(st, "xb", 4)
    for tb in range(NB):
        xb, xbuf = xbs[tb % 4], xbb[tb % 4]
        DMA(g, "sp", xb[:], src_ap[tb * 128:(tb + 1) * 128, :], [src_bufs[tb]], [xbuf])
        norm_block(g, c, xb, xbuf, tb, hT, hTb)


def win_cols(g, l, c0, c1):
    return g.w_in[l].rearrange("(c p) n -> p c n", p=128)[:, :, c0:c1]


def mm_fm(g, ps, M, n, w, wb, col0, hT, hTb, tok0, role_bufs):
    MM(g, [mmf(ps[0:M, 0:n], w[:, c, col0:col0 + M], hT[:, c, tok0:tok0 + n], c == 0, c == DC - 1) for c in range(DC)],
       [wb] + hTb[tok0 // 128:(tok0 + n + 127) // 128], [role_bufs])


def mm_tm(g, ps, N, w, wb, col0, hT, hTb, tb, psbuf):
    MM(g, [mmf(ps[:, 0:N], hT[:, c, tb * 128:(tb + 1) * 128], w[:, c, col0:col0 + N], c == 0, c == DC - 1) for c in range(DC)],
       [wb, hTb[tb]], [psbuf])


def phase_sb(g, l, hT, hTb, osbT, osbb):
    with scope(g) as st:
        ws = []
        for i, c0 in enumerate((C_SQ, C_SK, C_SV)):
            if i == 0 and take_pre(g, ("sq", l)):
                ws.append((g.pre, g.preb))
                continue
            w = sb(g, st, [128, DC, 384], BF16, "wsb")
            wb = nb(st, "wsb%d" % i)
            DMA(g, "pool", w[:], win_cols(g, l, c0, c0 + 384), (), [wb])
            ws.append((w, wb))
        sqT = sb(g, st, [128, 3, T], BF16, "sqT")
        skT = sb(g, st, [128, 3, T], BF16, "skT")
        sqb = nbs(st, "sq", 3, 4)
        skb = nbs(st, "sk", 3, 4)
        k = 0
        for (dst, dstb, (w, wb), scl) in ((sqT, sqb, ws[0], 0.125), (skT, skb, ws[1], 1.0)):
            for hp in range(3):
                for tc in range(4):
                    ps, pb = psf(g, "proj", [0, 1, 2, 3, 4, 5])
                    mm_fm(g, ps, 128, 512, w, wb, hp * 128, hT, hTb, tc * 512, pb)
                    o = dst[:, hp, tc * 512:(tc + 1) * 512]
                    if k % 2 == 0:
                        ACT(g, o, ps[:, :], AF.Copy, [pb], [dstb[hp][tc]], scale=scl)
                    else:
                        TS(g, "dve", o, ps[:, :], scl, None, ALU.mult, None, [pb], [dstb[hp][tc]])
                    k += 1
        svp = [sb(g, st, [128, NB, 384], BF16, "svp") for _ in range(2)]
        svb = nbs(st, "sv", 2, NB)
        for s_ in range(2):
            MS(g, "pool", svp[s_][:].rearrange("p t c -> p (t c)"), 0.0, svb[s_])
        for tb in range(NB):
            ps, pb = psf(g, "proj", [0, 1, 2, 3, 4, 5])
            mm_tm(g, ps, 384, ws[2][0], ws[2][1], 0, hT, hTb, tb, pb)
            src = ps[:, 0:384].rearrange("p (m s d) -> p m s d", s=2, d=64)
            for s_ in range(2):
                dst = svp[s_][:, tb, :].rearrange("p (m s d) -> p m s d", s=2, d=64)
                CP(g, "act" if s_ == 0 else "dve", dst[:, :, s_, :], src[:, :, s_, :], [pb], [svb[s_][tb]])
        prefetch(g, ("wA", l), win_cols(g, l, C_DQ, C_DQ + 512), 512)
        R = 4
        mk2 = lambda shape, dt_, nm: [[sb(g, st, shape, dt_, nm) for _ in range(R)] for _ in range(2)]
        Et, SPt, SPs, At = mk2([128, 512], F32, "Et"), mk2([128, 512], BF16, "SPt"), mk2([128, 512], BF16, "SPs"), mk2([128, 512], BF16, "At")
        Etb, SPb, SPsb, Atb = nbs(st, "Et", 2, R), nbs(st, "SPt", 2, R), nbs(st, "SPs", 2, R), nbs(st, "At", 2, R)
        steps = []
        for m in range(3):
            for qc in range(4):
                for n_, kb in enumerate(range(4 * qc + 3, -1, -1)):
                    steps.append((m, qc, kb, n_))
        stt = {}
        pso = {}

        def info(t):
            m, qc, kb, n_ = steps[t]
            j0 = max(0, kb * 128 - qc * 512)
            return m, qc, kb, n_, j0, kb >= 4 * qc, qc * 512 + j0 - kb * 128, n_ == 0

        def opnds(t):
            m, qc, kb, n_, j0, diag, base, first = info(t)
            kk = [skT[64 * s_:64 * s_ + 64, m, kb * 128:(kb + 1) * 128] for s_ in range(2)]
            qq = [sqT[64 * s_:64 * s_ + 64, m, qc * 512 + j0:(qc + 1) * 512] for s_ in range(2)]
            return kk, qq, skb[m][kb // 4], sqb[m][qc]

        def s1(t):
            m, qc, kb, n_, j0, diag, base, first = info(t)
            kk, qq, rk, rq = opnds(t)
            pz = [psf(g, "sbZ", [0, 1, 2]) for _ in range(2)]
            stt[t] = {"pz": pz}
            for s_ in range(2):
                MM(g, [mmf(pz[s_][0][:, j0:512], kk[s_], qq[s_], True, True)], [rk, rq], [pz[s_][1]])

        def s2(t):
            m, qc, kb, n_, j0, diag, base, first = info(t)
            ib = t % R
            pz = stt[t]["pz"]
            for s_ in range(2):
                ACT(g, Et[s_][ib][:, j0:512], pz[s_][0][:, j0:512], AF.Exp, [pz[s_][1]], [Etb[s_][ib]])

        def s3(t):
            m, qc, kb, n_, j0, diag, base, first = info(t)
            ib = t % R
            for s_ in range(2):
                ACT(g, SPt[s_][ib][:, j0:512], Et[s_][ib][:, j0:512], AF.Ln, [Etb[s_][ib]], [SPb[s_][ib]], bias=1.0)
            if diag:
                for s_ in range(2):
                    S = SPt[s_][ib]
                    assert base == 0
                    TT(g, "pool", S[:, j0:512], S[:, j0:512], g.mlt[:, 0:512 - j0], ALU.mult, [SPb[s_][ib], g.cb], [SPb[s_][ib]])
            if kb > 0:
                for s_ in range(2):
                    S, Sb = SPt[s_][ib], SPb[s_][ib]
                    Sn, Snb = SPs[s_][n_ % R], SPsb[s_][n_ % R]
                    if first:
                        if j0 > 0:
                            MS(g, "pool", Sn[:, 0:j0], 0.0, [Snb])
                        CP(g, "pool", Sn[:, j0:512], S[:, j0:512], [Sb], [Snb])
                    else:
                        So, Sob = SPs[s_][(n_ - 1) % R], SPsb[s_][(n_ - 1) % R]
                        if j0 > 0:
                            CP(g, "pool", Sn[:, 0:j0], So[:, 0:j0], [Sob], [Snb])
                        TT(g, "dve", Sn[:, j0:512], So[:, j0:512], S[:, j0:512], ALU.add, [Sob, Sb], [Snb])

        def s4(t):
            m, qc, kb, n_, j0, diag, base, first = info(t)
            ib = t % R
            kk, qq, rk, rq = opnds(t)
            pc = [psf(g, "sbC", [3, 4]) for _ in range(2)]
            stt[t]["pc"] = pc
            for s_ in range(2):
                S = SPt[s_][ib]
                fns = [mmf(pc[s_][0][:, j0:512], kk[s_], qq[s_], True, False),
                       mmf(pc[s_][0][:, j0:512], g.negtri[:], S[:, j0:512], False, first)]
                rd = [rk, rq, SPb[s_][ib], g.cb]
                if not first:
                    So, Sob = SPs[s_][(n_ - 1) % R], SPsb[s_][(n_ - 1) % R]
                    fns.append(mmf(pc[s_][0][:, j0:512], g.negones[:], So[:, j0:512], False, True))
                    rd.append(Sob)
                MM(g, fns, rd, [pc[s_][1]])

        def s5(t):
            m, qc, kb, n_, j0, diag, base, first = info(t)
            ib = t % R
            pc = stt[t]["pc"]
            for s_ in range(2):
                ACT(g, At[s_][ib][:, j0:512], pc[s_][0][:, j0:512], AF.Exp, [pc[s_][1]], [Atb[s_][ib]])
            for s_ in range(2):
                A, Ab = At[s_][ib], Atb[s_][ib]
                if diag:
                    TT(g, "pool", A[:, j0:512], A[:, j0:512], g.mlt[:, 0:512 - j0], ALU.mult, [Ab, g.cb], [Ab])
                if first and j0 > 0:
                    MS(g, "pool", A[:, 0:j0], 0.0, [Ab])

        def s6(t):
            m, qc, kb, n_, j0, diag, base, first = info(t)
            ib = t % R
            if first:
                pso[(m, qc)] = psf(g, "sbO", [5])
            psO, pOb = pso[(m, qc)]
            for s_ in range(2):
                A, Ab = At[s_][ib], Atb[s_][ib]
                vv = svp[s_][:, kb, m * 128:(m + 1) * 128]
                c0 = 0 if first else j0
                MM(g, [mmf(psO[:, c0:512], vv, A[:, c0:512], first and s_ == 0, kb == 0 and s_ == 1)], [svb[s_][kb], Ab], [pOb])
            if kb == 0:
                CP(g, "act" if (m * 4 + qc) % 2 == 0 else "dve", osbT[:, m, qc * 512:(qc + 1) * 512], psO[:, :], [pOb], [osbb[m][qc]])
            del stt[t]

        nst = len(steps)
        stages = [s1, s2, s3, s4, s5, s6]
        for e in range(nst + len(stages) - 1):
            for k in range(len(stages) - 1, -1, -1):
                t = e - k
                if 0 <= t < nst:
                    stages[k](t)


def phase_dsa(g, l, hT, hTb, odT, odb):
    P = g.P
    with scope(g) as st:
        featT = sb(g, st, [128, 9, T], BF16, "featT")
        fb = nbs(st, "feat", NB)
        dvx = sb(g, st, [128, NB, 128], BF16, "dvx")
        dvb = nbs(st, "dvx", NB)
        sgn = sb(g, st, [128, NB, 8], F32, "sgn")
        sgb = nbs(st, "sgn", NB)
        qkg = sb(g, st, [128, 7, 64], F32, "qkg")
        qkgb = nbs(st, "qkg", 7)
        for hh in range(7):
            src = (g.qn if hh < 6 else g.kn)[l:l + 1, :].to_broadcast([128, 64])
            DMA(g, "sp", qkg[:, hh, :], src, (), [qkgb[hh]])
        with scope(g) as s2:
            wB = sb(g, s2, [128, DC, 512], BF16, "wB")
            wC = sb(g, s2, [128, DC, 72], BF16, "wC")
            wBb, wCb = nb(s2, "wB"), nb(s2, "wC")
            if take_pre(g, ("wA", l)):
                wA, wAb = g.pre, g.preb
            else:
                wA = sb(g, s2, [128, DC, 512], BF16, "wA")
                wAb = nb(s2, "wA")
                DMA(g, "pool", wA[:], win_cols(g, l, C_DQ, C_DQ + 512), (), [wAb])
            DMA(g, "pool", wB[:], win_cols(g, l, C_IQ, C_IQ + 512), (), [wBb])
            DMA(g, "pool", wC[:], win_cols(g, l, C_IK, C_IK + 72), (), [wCb])
            NR = 4
            tq = [sb(g, s2, [128, 18, 64], F32, "tokq") for _ in range(NR)]
            tqb = nbs(s2, "tokq", NR)
            tbq = [sb(g, s2, [128, 18, 64], BF16, "tokb") for _ in range(3)]
            tbb = nbs(s2, "tokb", 3)
            sqt = [sb(g, s2, [128, 448], F32, "sqt") for _ in range(2)]
            sqtb = nbs(s2, "sqt", 2)
            smq = [sb(g, s2, [128, 32], F32, "small") for _ in range(2)]
            smqb = nbs(s2, "small", 2)
            rt = [sb(g, s2, [128, 18, 8], F32, "ropet") for _ in range(4)]
            rtb = nbs(s2, "ropet", 4)
            pst = {}

            def p1(tb):
                MS(g, "pool", dvx[:, tb, 64:128], 1.0, [dvb[tb]])
                tk, tkb = tq[tb % NR], tqb[tb % NR]
                MS(g, "pool", tk[:, 7, :], 0.0, [tkb])
                MS(g, "pool", tk[:, 17, :], 0.0, [tkb])
                psA, pAb = psf(g, "dA", [0, 1])
                psBq, pBb = psf(g, "dB", [2, 3])
                psC, pCb = psf(g, "dC", [4, 5])
                pst[tb] = (psA, pAb, psBq, pBb, psC, pCb)
                mm_tm(g, psA, 512, wA, wAb, 0, hT, hTb, tb, pAb)
                mm_tm(g, psBq, 512, wB, wBb, 0, hT, hTb, tb, pBb)
                mm_tm(g, psC, 72, wC, wCb, 0, hT, hTb, tb, pCb)

            def p2(tb):
                psA, pAb, psBq, pBb, psC, pCb = pst.pop(tb)
                tk, tkb = tq[tb % NR], tqb[tb % NR]
                sq_, sq_b = sqt[tb % 2], sqtb[tb % 2]
                sm, smb = smq[tb % 2], smqb[tb % 2]
                ACT(g, sq_[:], psA[:, 0:448], AF.Square, [pAb], [sq_b])
                ss = sm[:, 0:7]
                P.op("dve", lambda e, ss=ss, sq_=sq_: e.tensor_reduce(out=ss, in_=sq_[:].rearrange("p (h d) -> p h d", d=64), axis=AX.X,
                                                                      op=ALU.add), [sq_b], [smb], 448)
                ACT(g, ss, ss, AF.Ln, [smb], [smb], scale=1.0 / 64, bias=EPS)
                ACT(g, ss, ss, AF.Exp, [smb], [smb], scale=-0.5)
                aw = sm[:, 8:16]
                TS(g, "dve", sgn[:, tb, :], psC[:, 64:72], 0.0, 2.0, ALU.is_gt, ALU.mult, [pCb], [sgb[tb]])
                TS(g, "dve", sgn[:, tb, :], sgn[:, tb, :], -1.0, 0.0, ALU.add, ALU.add, [sgb[tb]], [sgb[tb]])
                STT(g, aw, psC[:, 64:72], IDX_SCALE, sgn[:, tb, :], ALU.mult, ALU.mult, [pCb, sgb[tb]], [smb])
                TT(g, "dve", tk[:, 8:16, :], psBq[:, :].rearrange("p (h d) -> p h d", d=64),
                   aw.unsqueeze(2).to_broadcast([128, 8, 64]), ALU.mult, [pBb, smb], [tkb])
                CP(g, "act", tk[:, 16, :], psC[:, 0:64], [pCb], [tkb])
                CP(g, "act", dvx[:, tb, 0:64], psA[:, 448:512], [pAb], [dvb[tb]])
                TT(g, "dve", tk[:, 0:7, :], psA[:, 0:448].rearrange("p (h d) -> p h d", d=64),
                   ss.unsqueeze(2).to_broadcast([128, 7, 64]), ALU.mult, [pAb, smb], [tkb])
                TT(g, "dve", tk[:, 0:7, :], tk[:, 0:7, :], qkg[:], ALU.mult, [tkb] + qkgb, [tkb])

            def p3(tb):
                tk, tkb = tq[tb % NR], tqb[tb % NR]
                x1, x2 = tk[:, :, 0:8], tk[:, :, 8:16]
                cb_ = g.cs[:, tb, :].unsqueeze(1).to_broadcast([128, 18, 8])
                sb_ = g.sn[:, tb, :].unsqueeze(1).to_broadcast([128, 18, 8])
                TT(g, "dve", rt[0][:], x1, cb_, ALU.mult, [tkb, g.csb], [rtb[0]])
                TT(g, "pool", rt[1][:], x2, sb_, ALU.mult, [tkb, g.snb], [rtb[1]])
                TT(g, "dve", rt[2][:], x2, cb_, ALU.mult, [tkb, g.csb], [rtb[2]])
                TT(g, "pool", rt[3][:], x1, sb_, ALU.mult, [tkb, g.snb], [rtb[3]])
                TT(g, "dve", x1, rt[0][:], rt[1][:], ALU.subtract, [rtb[0], rtb[1]], [tkb])
                TT(g, "pool", x2, rt[2][:], rt[3][:], ALU.add, [rtb[2], rtb[3]], [tkb])

            def p4(tb):
                tk, tkb = tq[tb % NR], tqb[tb % NR]
                tkh, tkhb = tbq[tb % 3], tbb[tb % 3]
                CP(g, "act", tkh[:], tk[:], [tkb], [tkhb])
                CP(g, "pool", tkh[:, 7, :], tkh[:, 6, :], [tkhb], [tkhb])
                CP(g, "pool", tkh[:, 17, :], tkh[:, 16, :], [tkhb], [tkhb])

            def p5(tb):
                tkh, tkhb = tbq[tb % 3], tbb[tb % 3]
                flat = tkh[:].rearrange("p h d -> p (h d)")
                pt, ptb = psb(g)
                MM(g, [trf(pt[:, m * 128:(m + 1) * 128], flat[:, m * 128:(m + 1) * 128], g.ident[:]) for m in range(8)],
                   [tkhb, g.cb], [ptb])
                CP(g, "dve", featT[:, 0:8, tb * 128:(tb + 1) * 128], pt[:, :].rearrange("p (m j) -> p m j", j=128), [ptb], [fb[tb]])
                pt2, ptb2 = psb(g)
                MM(g, [trf(pt2[:, 0:128], flat[:, 1024:1152], g.ident[:])], [tkhb, g.cb], [ptb2])
                CP(g, "act", featT[:, 8, tb * 128:(tb + 1) * 128], pt2[:, 0:128], [ptb2], [fb[tb]])

            stages = [p1, p2, p3, p4, p5]
            for e in range(NB + len(stages) - 1):
                for k in range(len(stages) - 1, -1, -1):
                    t_ = e - k
                    if 0 <= t_ < NB:
                        stages[k](t_)
        prefetch(g, ("hihg", l), win_cols(g, l, C_HI, C_HI + 512), 512)
        sc = [sb(g, st, [128, T], F32, "sc") for _ in range(4)]
        scb = nbs(st, "sc", 4, 4)
        junk = sb(g, st, [128, T], BF16, "junk")
        maskq = [sb(g, st, [128, T], BF16, "maskq") for _ in range(2)]
        mqb = nbs(st, "maskq", 2)
        maskT = sb(g, st, [128, NB, 512], BF16, "maskT")
        mTb = nbs(st, "maskT", 4)
        rj = [sb(g, st, [128, 512], BF16, "rj") for _ in range(4)]
        rjb = nbs(st, "rj", 4)
        dg = [sb(g, st, [128, 8, 128], BF16, "dg") for _ in range(2)]
        dgb = nbs(st, "dg", 2)
        sm = [sb(g, st, [128, 8 + 2 * N_BISECT], F32, "bis") for _ in range(2)]
        smb = nbs(st, "bis", 2)
        cvec = sb(g, st, [128, N_BISECT], F32, "cvec")
        c255 = sb(g, st, [128, 1], F32, "c255")
        cvb = nb(st, "cvec")
        for n_ in range(N_BISECT):
            MS(g, "pool", cvec[:, n_:n_ + 1], 2.0 ** -(n_ + 1), [cvb])
        MS(g, "pool", c255[:], 255.5, [cvb])
        Pt = [sb(g, st, [128, 512], BF16, "Pt") for _ in range(4)]
        Ptb = nbs(st, "Pt", 4)
        Pm = [sb(g, st, [128, 512], BF16, "Pm") for _ in range(4)]
        Pmb = nbs(st, "Pm", 4)
        rs = sb(g, st, [64, 512], F32, "rs")
        rsb = nb(st, "rs")
        cnt_ = {"ri": 0, "pi": 0, "pm": 0}

        def idx_blocks(blocks):
            tiles = []
            for i in blocks:
                nk = (i + 1) * 128
                d_, d_b = dg[i % 2], dgb[i % 2]
                for j in range(8):
                    TS(g, "pool", d_[:, j, :], g.ident[:], sgn[:, i, j:j + 1], None, ALU.mult, None, [g.cb, sgb[i]], [d_b])
                for kc in range((nk + 511) // 512):
                    n = min(512, nk - kc * 512)
                    for j in range(8):
                        tiles.append((i, kc, n, j))
            stt = {}

            def s1(t):
                i, kc, n, j = tiles[t]
                po = 64 * (j % 2)
                psZ, pZb = psf(g, "ixZ", [0, 1, 2])
                stt[t] = [psZ, pZb]
                MM(g, [mmf(psZ[:, 0:n], featT[po:po + 64, 4 + j // 2, i * 128:(i + 1) * 128],
                           featT[po:po + 64, 8, kc * 512:kc * 512 + n], True, True)],
                   [fb[i]] + fb[kc * 4:(kc * 512 + n) // 128], [pZb])

            def s2(t):
                i, kc, n, j = tiles[t]
                psZ, pZb = stt[t]
                r_, r_b = rj[cnt_["ri"] % 4], rjb[cnt_["ri"] % 4]
                cnt_["ri"] += 1
                stt[t] += [r_, r_b]
                ACT(g, r_[:, 0:n], psZ[:, 0:n], AF.Relu, [pZb], [r_b])

            def s3(t):
                i, kc, n, j = tiles[t]
                r_, r_b = stt[t][2], stt[t][3]
                if j == 0:
                    cnt_["psS"] = psf(g, "ixS", [3, 4])
                psS, pSb = cnt_["psS"]
                d_, d_b = dg[i % 2], dgb[i % 2]
                MM(g, [mmf(psS[:, 0:n], d_[:, j, :], r_[:, 0:n], j == 0, j == 7)], [d_b, r_b], [pSb])
                if j == 7:
                    CP(g, "act", sc[i % 4][:, kc * 512:kc * 512 + n], psS[:, 0:n], [pSb], [scb[i % 4][kc]])
                del stt[t]

            pipeline([s1, s2, s3], len(tiles))

        def bis_pair(p):
            blocks = [2 * p, 2 * p + 1]
            st_ = []
            for bi, i in enumerate(blocks):
                nk = (i + 1) * 128
                s_, s_b = sc[i % 4], scb[i % 4]
                nkc = (nk + 511) // 512
                srd = s_b[0:nkc]
                m_, m_b = sm[bi], smb[bi]
                rmax, rmin, step0, mid, cntv, tt = (m_[:, c:c + 1] for c in range(6))
                stepc = m_[:, 8:8 + N_BISECT]
                if nk > 256:
                    P.op("dve", lambda e, s_=s_, nk=nk, rmax=rmax: e.tensor_reduce(out=rmax, in_=s_[:, 0:nk], axis=AX.X, op=ALU.max), srd, [m_b], nk)
                    P.op("dve", lambda e, s_=s_, nk=nk, rmin=rmin: e.tensor_reduce(out=rmin, in_=s_[:, 0:nk], axis=AX.X, op=ALU.min), srd, [m_b], nk)
                dsl = s_[:, i * 128:(i + 1) * 128]
                TT(g, "pool", dsl, dsl, g.caus01[:], ALU.mult, [s_b[i // 4], g.cb], [s_b[i // 4]])
                TT(g, "pool", dsl, dsl, g.negfill[:], ALU.add, [s_b[i // 4], g.cb], [s_b[i // 4]])
                st_.append((i, nk, s_, srd, m_, m_b, rmax, rmin, step0, mid, cntv, tt, stepc))
            act = [x for x in st_ if x[1] > 256]
            for (i, nk, s_, srd, m_, m_b, rmax, rmin, step0, mid, cntv, tt, stepc) in act:
                TT(g, "dve", step0, rmax, rmin, ALU.subtract, [m_b], [m_b])
            for (i, nk, s_, srd, m_, m_b, rmax, rmin, step0, mid, cntv, tt, stepc) in act:
                TS(g, "dve", stepc, cvec[:], step0, None, ALU.mult, None, [m_b, cvb], [m_b])
            for (i, nk, s_, srd, m_, m_b, rmax, rmin, step0, mid, cntv, tt, stepc) in act:
                TS(g, "dve", mid, stepc[:, 0:1], rmin, g.zc[:, 0:1], ALU.add, ALU.add, [m_b, g.cb], [m_b])
            for n_ in range(N_BISECT):
                for (i, nk, s_, srd, m_, m_b, rmax, rmin, step0, mid, cntv, tt, stepc) in act:
                    TS(g, "dve", junk[:, 0:nk], s_[:, 0:nk], mid, g.zc[:, 0:1], ALU.is_ge, ALU.add, srd + [m_b, g.cb], [m_b], accum_out=cntv)
                for (i, nk, s_, srd, m_, m_b, rmax, rmin, step0, mid, cntv, tt, stepc) in act:
                    TS(g, "dve", tt, cntv, c255[:, 0:1], stepc[:, n_:n_ + 1], ALU.is_ge, ALU.mult, [m_b, cvb], [m_b])
                for (i, nk, s_, srd, m_, m_b, rmax, rmin, step0, mid, cntv, tt, stepc) in act:
                    nn = min(n_ + 1, N_BISECT - 1)
                    TS(g, "dve", mid, tt, stepc[:, nn:nn + 1], mid, ALU.subtract, ALU.add, [m_b], [m_b])
            for (i, nk, s_, srd, m_, m_b, rmax, rmin, step0, mid, cntv, tt, stepc) in st_:
                thr = mid if nk > 256 else g.negbig[:, 0:1]
                mq, mq_b = maskq[i % 2], mqb[i % 2]
                TS(g, "dve", mq[:, 0:nk], s_[:, 0:nk], thr, None, ALU.is_ge, None, srd + [m_b, g.cb], [mq_b])

        def mT_pair(p):
            for i in (2 * p, 2 * p + 1):
                ii = i % 4
                mq, mq_b = maskq[i % 2], mqb[i % 2]
                for k0 in range(0, i + 1, 8):
                    k1 = min(i + 1, k0 + 8)
                    pt, ptb = psb(g)
                    MM(g, [trf(pt[:, (kb - k0) * 128:(kb - k0 + 1) * 128], mq[:, kb * 128:(kb + 1) * 128], g.ident[:])
                           for kb in range(k0, k1)], [mq_b, g.cb], [ptb])
                    CP(g, "act", maskT[:, k0:k1, ii * 128:(ii + 1) * 128],
                       pt[:, 0:(k1 - k0) * 128].rearrange("p (m j) -> p m j", j=128), [ptb], [mTb[ii]])

        def att_chunk(qc):
            last = 4 * qc + 3
            tiles = [(h, kb) for h in range(6) for kb in range(last + 1)]
            stt = {}
            pso = {}

            def s1(t):
                h, kb = tiles[t]
                hp, po = h // 2, 64 * (h % 2)
                j0 = max(0, kb * 128 - qc * 512)
                psL, pLb = psf(g, "dsL", [0, 1, 2])
                stt[t] = [psL, pLb]
                MM(g, [mmf(psL[:, j0:512], featT[po:po + 64, 3, kb * 128:(kb + 1) * 128],
                           featT[po:po + 64, hp, qc * 512 + j0:(qc + 1) * 512], True, True)],
                   [fb[kb]] + fb[qc * 4:qc * 4 + 4], [pLb])

            def s2(t):
                h, kb = tiles[t]
                j0 = max(0, kb * 128 - qc * 512)
                psL, pLb = stt[t][0], stt[t][1]
                pi = cnt_["pi"]
                cnt_["pi"] += 1
                p_, p_b = Pt[pi % 4], Ptb[pi % 4]
                stt[t] += [p_, p_b]
                ACT(g, p_[:, j0:512], psL[:, j0:512], AF.Exp, [pLb], [p_b], scale=0.125)

            def s3(t):
                h, kb = tiles[t]
                j0 = max(0, kb * 128 - qc * 512)
                p_, p_b = stt[t][2], stt[t][3]
                pm = cnt_["pm"]
                cnt_["pm"] += 1
                m_, m_b = Pm[pm % 4], Pmb[pm % 4]
                stt[t] += [m_, m_b]
                TT(g, "pool", m_[:, j0:512], p_[:, j0:512], maskT[:, kb, j0:512], ALU.mult, [p_b] + mTb[j0 // 128:4], [m_b])

            def s4(t):
                h, kb = tiles[t]
                hp, po = h // 2, 64 * (h % 2)
                j0 = max(0, kb * 128 - qc * 512)
                m_, m_b = stt[t][4], stt[t][5]
                if kb == 0:
                    pso[h] = psf(g, "dsO", [4, 5])
                psO, pOb = pso[h]
                MM(g, [mmf(psO[:, j0:512], dvx[:, kb, :], m_[:, j0:512], kb == 0, kb == last)], [dvb[kb], m_b], [pOb])
                if kb == last:
                    ACT(g, rs[0:64, :], psO[64:128, :], AF.Ln, [pOb], [rsb])
                    ACT(g, rs[0:64, :], rs[0:64, :], AF.Exp, [rsb], [rsb], scale=-1.0)
                    TT(g, "dve", odT[po:po + 64, hp, qc * 512:(qc + 1) * 512], psO[0:64, :], rs[0:64, :], ALU.mult, [pOb, rsb],
                       [odb[hp][qc]])
                del stt[t]

            pipeline([s1, s2, s3, s4], len(tiles))

        idx_blocks([0, 1])
        for p in range(8):
            if p + 1 < 8:
                idx_blocks([2 * p + 2, 2 * p + 3])
            bis_pair(p)
            mT_pair(p)
            if p % 2 == 1:
                att_chunk(p // 2)


def phase_hgrn(g, l, hT, hTb, ohT, ohb):
    P = g.P
    import os
    if int(os.environ.get("HGL", "9")) == 0:
        return
    with scope(g) as st:
        hi_tm = sb(g, st, [128, NB, 256], BF16, "hi_tm")
        hib = nbs(st, "hi", NB)
        hgs = sb(g, st, [128, NB, 256], BF16, "hgs")
        hgb = nbs(st, "hgs", NB)
        onb = sb(g, st, [128, 64], F32, "onorm")
        onbb = nb(st, "onorm")
        DMA(g, "sp", onb[:], g.onorm[l:l + 1, :].to_broadcast([128, 64]), (), [onbb])
        with scope(g) as s2:
            if take_pre(g, ("hihg", l)):
                w, wb = g.pre, g.preb
            else:
                w = sb(g, s2, [128, DC, 512], BF16, "whihg")
                wb = nb(s2, "whihg")
                DMA(g, "pool", w[:], win_cols(g, l, C_HI, C_HI + 512), (), [wb])
            sgs = [sb(g, s2, [128, 256], F32, "sgs") for _ in range(2)]
            sgsb = nbs(s2, "sgs", 2)
            for tb in range(NB):
                ps, pb = psf(g, "proj", [0, 1, 2, 3, 4, 5])
                hgv = int(os.environ.get("HGV", "15"))
                if hgv & 8:
                    mm_tm(g, ps, 512, w, wb, 0, hT, hTb, tb, pb)
                if hgv & 1:
                    CP(g, "dve", hi_tm[:, tb, :], ps[:, 0:256], [pb], [hib[tb]])
                sgt, sgtb = sgs[tb % 2], sgsb[tb % 2]
                if hgv & 2:
                    ACT(g, sgt[:], ps[:, 256:512], AF.Exp if hgv & 16 else AF.Sigmoid, [pb], [sgtb])
                if hgv & 4:
                    TT(g, "dve", hgs[:, tb, :], ps[:, 256:512], sgt[:], ALU.mult, [pb, sgtb], [hgb[tb]])
        with scope(g) as s3:
            NH = 4
            R = 4
            qtT = sb(g, s3, [128, NH, T], BF16, "qtT")
            ktT = sb(g, s3, [128, NH, T], BF16, "ktT")
            qtb = nbs(s3, "qt", NH)
            ktb = nbs(s3, "kt", NH)
            kt_tm = sb(g, s3, [128, NB, NH * 128], BF16, "kt_tm")
            kttb = nbs(s3, "kttm", NH)
            t1 = sb(g, s3, [128, T], F32, "t1")
            t2 = sb(g, s3, [128, T], F32, "t2")
            t3 = sb(g, s3, [128, T], F32, "t3")
            t1h, t2h, t3h = nbs(s3, "t1", 2), nbs(s3, "t2", 2), nbs(s3, "t3", 2)
            ebl = sb(g, s3, [128, NH, 32], F32, "ebl")
            eblb = nb(s3, "ebl")
            W = [sb(g, s3, [128, NH, 64], F32, "W") for _ in range(2)]
            Wb = nbs(s3, "W", 2)
            Sbf = [sb(g, s3, [128, NH, 64], BF16, "Sbf") for _ in range(R)]
            Sbfb = nbs(s3, "Sbf", R)
            attm = [sb(g, s3, [128, NH, 128], BF16, "attm") for _ in range(3)]
            attb = nbs(s3, "attm", 3)
            o_tm = [sb(g, s3, [128, NH * 64], F32, "o_tm") for _ in range(3)]
            otb = nbs(s3, "otm", 3)
            osq = sb(g, s3, [128, NH * 64], F32, "osq")
            osqb = nb(s3, "osq")
            og = [sb(g, s3, [128, NH * 64], BF16, "og") for _ in range(2)]
            ogb = nbs(s3, "og", 2)
            sm = [sb(g, s3, [128, 4], F32, "hsm") for _ in range(2)]
            smb = nbs(s3, "hsm", 2)
            wq = [sb(g, s3, [128, DC, 128], BF16, "wq") for _ in range(2)]
            wqb = nbs(s3, "wq", 2)
            wf = [sb(g, s3, [128, DC, 128], BF16, "wf") for _ in range(2)]
            wfb = nbs(s3, "wf", 2)

            def load_head(hd):
                DMA(g, "pool", wf[hd % 2][:], win_cols(g, l, C_HF + hd * 128, C_HF + hd * 128 + 128), (), [wfb[hd % 2]])
                DMA(g, "pool", wq[hd % 2][:], win_cols(g, l, C_HQ + hd * 128, C_HQ + hd * 128 + 128), (), [wqb[hd % 2]])

            load_head(0)
            for hd in range(NH):
                if hd + 1 < NH:
                    load_head(hd + 1)
                w_f, w_fb, w_q, w_qb = wf[hd % 2], wfb[hd % 2], wq[hd % 2], wqb[hd % 2]
                HS = [slice(0, T // 2), slice(T // 2, T)]
                for tc in range(4):
                    ps, pb = psf(g, "proj", [0, 1, 2, 3, 4, 5])
                    mm_fm(g, ps, 128, 512, w_f, w_fb, 0, hT, hTb, tc * 512, pb)
                    ACT(g, t1[:, tc * 512:(tc + 1) * 512], ps[:, :], AF.Sigmoid, [pb], [t1h[tc // 2]])
                for hf in range(2):
                    TS(g, "dve", t1[:, HS[hf]], t1[:, HS[hf]], g.oml[:, l, hd:hd + 1], g.lbv[:, l, hd:hd + 1], ALU.mult, ALU.add,
                       [t1h[hf], g.cb], [t1h[hf]])
                for hf in range(2):
                    ACT(g, t2[:, HS[hf]], t1[:, HS[hf]], AF.Copy, [t1h[hf]], [t2h[hf]], scale=-1.0, bias=1.0)
                for hf in range(2):
                    TS(g, "dve", t1[:, HS[hf]], t1[:, HS[hf]], F_MIN, None, ALU.max, None, [t1h[hf]], [t1h[hf]])
                for hf in range(2):
                    ACT(g, t1[:, HS[hf]], t1[:, HS[hf]], AF.Ln, [t1h[hf]], [t1h[hf]])
                for hf in range(2):
                    P.op("dve", lambda e, hf=hf: e.tensor_tensor_scan(out=t3[:, HS[hf]], data0=g.resetm[:, HS[hf]], data1=t1[:, HS[hf]],
                                                                     initial=0.0, op0=ALU.mult, op1=ALU.add),
                         [t1h[hf], g.cb], [t3h[hf]], T)
                for hf in range(2):
                    TS(g, "dve", t3[:, HS[hf]], t3[:, HS[hf]], -80.0, None, ALU.max, None, [t3h[hf]], [t3h[hf]])
                for hf in range(2):
                    ACT(g, t1[:, HS[hf]], t3[:, HS[hf]], AF.Exp, [t3h[hf]], [t1h[hf]])
                for hf in range(2):
                    CP(g, "pool", ebl[:, hd, hf * 16:(hf + 1) * 16].unsqueeze(2),
                       t1[:, HS[hf]].rearrange("p (c j) -> p c j", j=64)[:, :, 63:64], [t1h[hf]], [eblb])
                for hf in range(2):
                    ACT(g, t3[:, HS[hf]], t3[:, HS[hf]], AF.Exp, [t3h[hf]], [t3h[hf]], scale=-1.0)
                for hf in range(2):
                    TT(g, "dve", ktT[:, hd, HS[hf]], t2[:, HS[hf]], t3[:, HS[hf]], ALU.mult, [t2h[hf], t3h[hf]], [ktb[hd]])
                for tc in range(4):
                    ps, pb = psf(g, "proj", [0, 1, 2, 3, 4, 5])
                    mm_fm(g, ps, 128, 512, w_q, w_qb, 0, hT, hTb, tc * 512, pb)
                    ACT(g, t2[:, tc * 512:(tc + 1) * 512], ps[:, :], AF.Sigmoid, [pb], [t2h[tc // 2]])
                    TT(g, "dve", t2[:, tc * 512:(tc + 1) * 512], ps[:, :], t2[:, tc * 512:(tc + 1) * 512], ALU.mult, [pb, t2h[tc // 2]],
                       [t2h[tc // 2]])
                for hf in range(2):
                    TT(g, "dve", qtT[:, hd, HS[hf]], t2[:, HS[hf]], t1[:, HS[hf]], ALU.mult, [t2h[hf], t1h[hf]], [qtb[hd]])
                for k0 in (0, 8):
                    pt, ptb = psb(g)
                    MM(g, [trf(pt[:, m * 128:(m + 1) * 128], ktT[:, hd, (k0 + m) * 128:(k0 + m + 1) * 128], g.ident[:])
                           for m in range(8)], [ktb[hd], g.cb], [ptb])
                    CP(g, "act" if k0 == 0 else "dve", kt_tm[:, k0:k0 + 8, hd * 128:(hd + 1) * 128],
                       pt[:, :].rearrange("p (m j) -> p m j", j=128), [ptb], [kttb[hd]])

            NCH = 32
            xps = {}

            def emit_X(c):
                tb, pr = c // 2, (c % 2) * 64
                psX, pXb = psf(g, "hgX", [0, 1, 2])
                xps[c] = (psX, pXb)
                MM(g, [mmf(psX[:, hh * 64:(hh + 1) * 64], kt_tm[pr:pr + 64, tb, hh * 128:(hh + 1) * 128],
                           hi_tm[pr:pr + 64, tb, hh * 64:(hh + 1) * 64], True, True) for hh in range(NH)], kttb + [hib[tb]], [pXb])

            def emit_A(tb):
                psA, pAb = psf(g, "hgA", [3])
                MM(g, [mmf(psA[:, hh * 128:(hh + 1) * 128], ktT[:, hh, tb * 128:(tb + 1) * 128],
                           qtT[:, hh, tb * 128:(tb + 1) * 128], True, True) for hh in range(NH)], ktb + qtb, [pAb])
                TT(g, "dve", attm[tb % 3][:], psA[:, :].rearrange("p (h t) -> p h t", t=128),
                   g.maskbd[:].unsqueeze(1).to_broadcast([128, NH, 128]), ALU.mult, [pAb, g.cb], [attb[tb % 3]])

            emit_X(0)
            emit_X(1)
            emit_A(0)
            CP(g, "dve", W[0][:], xps[0][0][:, 0:NH * 64].rearrange("p (h v) -> p h v", v=64), [xps[0][1]], [Wb[0]])
            MS(g, "pool", Sbf[0][:], 0.0, [Sbfb[0]])
            for c in range(NCH):
                tb, half = c // 2, c % 2
                pr = half * 64
                if c + 2 < NCH:
                    emit_X(c + 2)
                if half == 0 and tb + 1 < NB:
                    emit_A(tb + 1)
                if c + 1 < NCH:
                    eb_ = ebl[:, :, c:c + 1].to_broadcast([128, NH, 64])
                    TT(g, "pool", Sbf[(c + 1) % R][:], W[c % 2][:], eb_, ALU.mult, [Wb[c % 2], eblb], [Sbfb[(c + 1) % R]])
                    psX, pXb = xps.pop(c + 1)
                    for hh in range(NH):
                        STT(g, W[(c + 1) % 2][:, hh, :], W[c % 2][:, hh, :], ebl[:, hh, c:c + 1], psX[:, hh * 64:(hh + 1) * 64],
                            ALU.mult, ALU.add, [Wb[c % 2], eblb, pXb], [Wb[(c + 1) % 2]])
                am, amb = attm[tb % 3], attb[tb % 3]
                ot, otbuf = o_tm[tb % 3], otb[tb % 3]
                psO, pOb = psf(g, "hgO", [4, 5])
                fns = []
                for hh in range(NH):
                    fns.append(mmf(psO[0:64, hh * 64:(hh + 1) * 64], am[pr:pr + 64, hh, pr:pr + 64],
                                   hi_tm[pr:pr + 64, tb, hh * 64:(hh + 1) * 64], True, False))
                    fns.append(mmf(psO[0:64, hh * 64:(hh + 1) * 64], qtT[:, hh, c * 64:(c + 1) * 64], Sbf[c % R][:, hh, :], False, True))
                MM(g, fns, [amb, hib[tb], Sbfb[c % R]] + qtb, [pOb])
                CP(g, "act", ot[pr:pr + 64, :], psO[0:64, 0:NH * 64], [pOb], [otbuf])
                if half == 1:
                    sm_, sm_b = sm[tb % 2], smb[tb % 2]
                    og_, og_b = og[tb % 2], ogb[tb % 2]
                    TT(g, "pool", osq[:], ot[:], ot[:], ALU.mult, [otbuf], [osqb])
                    ss = sm_[:, 0:NH]
                    P.op("dve", lambda e, ss=ss: e.tensor_reduce(out=ss, in_=osq[:].rearrange("p (h d) -> p h d", d=64), axis=AX.X,
                                                                 op=ALU.add), [osqb], [sm_b], NH * 64)
                    ACT(g, ss, ss, AF.Ln, [sm_b], [sm_b], scale=1.0 / 64, bias=EPS)
                    ACT(g, ss, ss, AF.Exp, [sm_b], [sm_b], scale=-0.5)
                    o3 = ot[:].rearrange("p (h d) -> p h d", d=64)
                    TT(g, "dve", o3, o3, ss.unsqueeze(2).to_broadcast([128, NH, 64]), ALU.mult, [otbuf, sm_b], [otbuf])
                    TT(g, "dve", o3, o3, onb[:].unsqueeze(1).to_broadcast([128, NH, 64]), ALU.mult, [otbuf, onbb], [otbuf])
                    TT(g, "dve", og_[:], ot[:], hgs[:, tb, :], ALU.mult, [otbuf, hgb[tb]], [og_b])
                    pt, ptb = psb(g)
                    MM(g, [trf(pt[:, m * 128:(m + 1) * 128], og_[:, m * 128:(m + 1) * 128], g.ident[:]) for m in range(2)],
                       [og_b, g.cb], [ptb])
                    CP(g, "act", ohT[:, 0:2, tb * 128:(tb + 1) * 128], pt[:, 0:256].rearrange("p (m j) -> p m j", j=128), [ptb],
                       [ohb[0][tb // 4], ohb[1][tb // 4]])


def phase_mix(g, l, hT, hTb, osbT, osbb, odT, odb, ohT, ohb, src_ap, src_bufs):
    P = g.P
    with scope(g) as st:
        mixT = sb(g, st, [128, DC, T], BF16, "mixT")
        mxb = nbs(st, "mix", DC, 4)
        wg = [sb(g, st, [128, DC, 3, 256], BF16, "wg") for _ in range(2)]
        wgb = nbs(st, "wg", 2, 3)
        wy = [sb(g, st, [128, 8, 256], BF16, "wy") for _ in range(2)]
        wyb = nbs(st, "wy", 2, 3)
        sg = [sb(g, st, [128, 512], F32, "sg") for _ in range(2)]
        sgb = nbs(st, "sg", 2)
        acc = [sb(g, st, [128, 512], F32, "acc") for _ in range(2)]
        accb = nbs(st, "acc", 2)
        tm = [sb(g, st, [128, 512], F32, "tm") for _ in range(2)]
        tmb = nbs(st, "tm", 2)
        k = 0

        def load_pair(dp):
            w_, w_b = wg[dp % 2], wgb[dp % 2]
            y_, y_b = wy[dp % 2], wyb[dp % 2]
            cs_ = slice(dp * 256, (dp + 1) * 256)
            for gi in range(3):
                c0 = C_G + gi * 1024 + dp * 256
                DMA(g, "pool", w_[:, :, gi, :], win_cols(g, l, c0, c0 + 256), (), [w_b[gi]])
            DMA(g, "pool", y_[:, 0:3, :], g.w_sb[l].rearrange("(c p) n -> p c n", p=128)[:, :, cs_], (), [y_b[0]])
            DMA(g, "pool", y_[:, 3:6, :], g.w_dsa[l].rearrange("(c p) n -> p c n", p=128)[:, :, cs_], (), [y_b[1]])
            DMA(g, "pool", y_[:, 6:8, :], g.w_hg[l].rearrange("(c p) n -> p c n", p=128)[:, :, cs_], (), [y_b[2]])

        load_pair(0)
        wo = sb(g, st, [128, DC, D], BF16, "wo")
        wob = nbs(st, "wo", 2)
        for dc in range(DC):
            dp, do = dc // 2, (dc % 2) * 128
            w_, w_b = wg[dp % 2], wgb[dp % 2]
            y_, y_b = wy[dp % 2], wyb[dp % 2]
            if dc % 2 == 0:
                if dp + 1 < DC // 2:
                    load_pair(dp + 1)
                else:
                    for nh in range(2):
                        DMA(g, "pool", wo[:, :, nh * 512:(nh + 1) * 512],
                            g.w_out[l].rearrange("(c p) n -> p c n", p=128)[:, :, nh * 512:(nh + 1) * 512], (), [wob[nh]])
                    prefetch(g, ("wu", l, 0), g.w_up[l].rearrange("(c p) n -> p c n", p=128)[:, :, 0:512], 512)
            for tc in range(4):
                a_, a_b = acc[k % 2], accb[k % 2]
                for gi, (oT, obufs, nch, c0) in enumerate(((osbT, osbb, 3, 0), (odT, odb, 3, 3), (ohT, ohb, 2, 6))):
                    psG, pGb = psf(g, "mxG", [0, 1, 2])
                    MM(g, [mmf(psG[:, :], w_[:, c, gi, do:do + 128], hT[:, c, tc * 512:(tc + 1) * 512], c == 0, c == DC - 1) for c in range(DC)],
                       [w_b[gi]] + hTb[tc * 4:tc * 4 + 4], [pGb])
                    s_, s_b = sg[(k * 3 + gi) % 2], sgb[(k * 3 + gi) % 2]
                    ACT(g, s_[:], psG[:, :], AF.Sigmoid, [pGb], [s_b])
                    psY, pYb = psf(g, "mxY", [3, 4, 5])
                    MM(g, [mmf(psY[:, :], y_[:, c0 + c, do:do + 128], oT[:, c, tc * 512:(tc + 1) * 512], c == 0, c == nch - 1) for c in range(nch)],
                       [y_b[gi]] + [obufs[c][tc] for c in range(nch)], [pYb])
                    if gi == 0:
                        TT(g, "dve", a_[:], psY[:, :], s_[:], ALU.mult, [pYb, s_b], [a_b])
                    else:
                        t_, t_b = tm[gi % 2], tmb[gi % 2]
                        TT(g, "dve", t_[:], psY[:, :], s_[:], ALU.mult, [pYb, s_b], [t_b])
                        if gi == 1:
                            TT(g, "pool", a_[:], a_[:], t_[:], ALU.add, [a_b, t_b], [a_b])
                        else:
                            TT(g, "pool", mixT[:, dc, tc * 512:(tc + 1) * 512], a_[:], t_[:], ALU.add, [a_b, t_b], [mxb[dc][tc]])
                k += 1
        xbs = [sb(g, st, [128, D], F32, "xb") for _ in range(4)]
        xbb = nbs(st, "xb", 4)
        nctx = norm_setup(g, st, g.norm_mlp[l:l + 1, :])
        for tb in range(NB):
            xb, xbuf = xbs[tb % 4], xbb[tb % 4]
            DMA(g, "sp", xb[:], src_ap[tb * 128:(tb + 1) * 128, :], [src_bufs[tb]], [xbuf])
            for nh in range(2):
                ps, pb = psf(g, "mxO", [0, 1, 2, 3, 4, 5])
                MM(g, [mmf(ps[:, :], mixT[:, c, tb * 128:(tb + 1) * 128], wo[:, c, nh * 512:(nh + 1) * 512], c == 0, c == DC - 1)
                       for c in range(DC)], [wob[nh]] + [mxb[c][tb // 4] for c in range(DC)], [pb])
                xs = xb[:, nh * 512:(nh + 1) * 512]
                TT(g, "dve", xs, ps[:, :], xs, ALU.add, [pb, xbuf], [xbuf])
            DMA(g, "sp", g.xres_d[tb * 128:(tb + 1) * 128, :], xb[:], [xbuf], [g.xres_b[tb]])
            norm_block(g, nctx, xb, xbuf, tb, hT, hTb)


def phase_ffn(g, l, hT, hTb, dst_ap, dst_bufs):
    for half in range(2):
        last = half == 1
        with scope(g) as st:
            uT = sb(g, st, [128, 16, T], BF16, "uT")
            ub = nbs(st, "uT", 16, 4)
            wd = sb(g, st, [128, 16, D], BF16, "wd")
            wdb = nbs(st, "wd", 2)
            wdv = g.w_down[l].rearrange("(f p) n -> p f n", p=128)
            wu = [sb(g, st, [128, DC, 512], BF16, "wu") for _ in range(2)]
            wub = nbs(st, "wu", 2)
            rt = [sb(g, st, [128, 512], BF16, "rt") for _ in range(2)]
            rtb = nbs(st, "rt", 2)
            k = 0

            def load_wu(g4):
                c0 = half * 2048 + g4 * 512
                DMA(g, "pool", wu[g4 % 2][:], g.w_up[l].rearrange("(c p) n -> p c n", p=128)[:, :, c0:c0 + 512], (), [wub[g4 % 2]])

            pre0 = take_pre(g, ("wu", l, half))
            if not pre0:
                load_wu(0)
            for g4 in range(4):
                w_, w_b = wu[g4 % 2], wub[g4 % 2]
                if g4 == 0 and pre0:
                    w_, w_b = g.pre, g.preb
                if g4 + 1 < 4:
                    load_wu(g4 + 1)
                if g4 == 1:
                    for nh in range(2):
                        DMA(g, "pool", wd[:, :, nh * 512:(nh + 1) * 512], wdv[:, half * 16:(half + 1) * 16, nh * 512:(nh + 1) * 512], (),
                            [wdb[nh]])
                for fcl in range(4):
                    fc = g4 * 4 + fcl
                    for tc in range(4):
                        ps, pb = psf(g, "proj", [0, 1, 2, 3, 4, 5])
                        mm_fm(g, ps, 128, 512, w_, w_b, fcl * 128, hT, hTb, tc * 512, pb)
                        r_, r_b = rt[k % 2], rtb[k % 2]
                        k += 1
                        ACT(g, r_[:], ps[:, :], AF.Relu, [pb], [r_b])
                        TT(g, "pool", uT[:, fc, tc * 512:(tc + 1) * 512], r_[:], r_[:], ALU.mult, [r_b], [ub[fc][tc]])
            if half == 0:
                prefetch(g, ("wu", l, 1), g.w_up[l].rearrange("(c p) n -> p c n", p=128)[:, :, 2048:2560], 512)
            elif l + 1 < DEPTH:
                prefetch(g, ("sq", l + 1), win_cols(g, l + 1, C_SQ, C_SQ + 384), 384)
            xbs = [sb(g, st, [128, D], F32, "xb") for _ in range(4)]
            xbb = nbs(st, "xb", 4)
            nctx = norm_setup(g, st, g.norm_mix[l + 1:l + 2, :]) if (last and l + 1 < DEPTH) else None
            for tb in range(NB):
                xb, xbuf = xbs[tb % 4], xbb[tb % 4]
                DMA(g, "sp", xb[:], g.xres_d[tb * 128:(tb + 1) * 128, :], [g.xres_b[tb]], [xbuf])
                for nh in range(2):
                    ps, pb = psf(g, "proj", [0, 1, 2, 3, 4, 5])
                    MM(g, [mmf(ps[:, :], uT[:, fc, tb * 128:(tb + 1) * 128], wd[:, fc, nh * 512:(nh + 1) * 512], fc == 0, fc == 15)
                           for fc in range(16)], [wdb[nh]] + [ub[fc][tb // 4] for fc in range(16)], [pb])
                    xs = xb[:, nh * 512:(nh + 1) * 512]
                    TT(g, "dve", xs, ps[:, :], xs, ALU.add, [pb, xbuf], [xbuf])
                if last:
                    DMA(g, "sp", dst_ap[tb * 128:(tb + 1) * 128, :], xb[:], [xbuf], [dst_bufs[tb]])
                    if nctx is not None:
                        norm_block(g, nctx, xb, xbuf, tb, hT, hTb)
                else:
                    DMA(g, "sp", g.xres_d[tb * 128:(tb + 1) * 128, :], xb[:], [xbuf], [g.xres_b[tb]])


def dump_t(g, name, t, ncol):
    if g.dump == name:
        g.P.barrier()
        DMA(g, "sp", g.dbg_d[:, 0:ncol], t, [], [g.dbgb])
        g.P.wait_bufs("sp", [g.dbgb])
        g.P.barrier()


def build_layer(g, l):
    if l > 0 and g.stage < 7:
        return
    src_ap, src_bufs = (g.x_d, g.xin_b) if l == 0 else (g.xres_d, g.xres_b)
    with scope(g) as ls:
        hT, hTb = g.hT, g.hTb
        if l == 0:
            with scope(g) as st:
                norm_T(g, st, src_ap, src_bufs, g.norm_mix[l:l + 1, :], hT, hTb)
        if l == 0:
            dump_t(g, "hT", hT[:].rearrange("p c t -> p (c t)"), 8 * T)
        if g.stage < 2:
            return
        with scope(g) as ms:
            osbT = sb(g, ms, [128, 3, T], BF16, "osbT")
            odT = sb(g, ms, [128, 3, T], BF16, "odT")
            ohT = sb(g, ms, [128, 2, T], BF16, "ohT")
            osbb = nbs(ms, "osb", 3, 4)
            odb = nbs(ms, "od", 3, 4)
            ohb = nbs(ms, "oh", 2, 4)
            phase_sb(g, l, hT, hTb, osbT, osbb)
            if l == 0:
                dump_t(g, "osbT", osbT[:].rearrange("p c t -> p (c t)"), 3 * T)
            if g.stage < 3:
                return
            phase_dsa(g, l, hT, hTb, odT, odb)
            if l == 0:
                dump_t(g, "odT", odT[:].rearrange("p c t -> p (c t)"), 3 * T)
            if g.stage < 4:
                return
            phase_hgrn(g, l, hT, hTb, ohT, ohb)
            if l == 0:
                dump_t(g, "ohT", ohT[:].rearrange("p c t -> p (c t)"), 2 * T)
            if g.stage < 5:
                return
            phase_mix(g, l, hT, hTb, osbT, osbb, odT, odb, ohT, ohb, src_ap, src_bufs)
        if g.stage < 6:
            return
        if l == DEPTH - 1:
            phase_ffn(g, l, hT, hTb, g.out_d, g.out_b)
        else:
            phase_ffn(g, l, hT, hTb, g.xres_d, g.xres_b)


_NC_CACHE = {}


def rope_tables():
    half = 8
    inv = 500000.0 ** (-(np.arange(half, dtype=np.float32) * 2.0) / 16.0)
    ang = np.arange(T, dtype=np.float32)[:, None] * inv[None, :].astype(np.float32)
    return np.cos(ang).astype(np.float32), np.sin(ang).astype(np.float32)


def kernel(x, norm_mix, w_in, qn_dsa, kn_dsa, hgrn_lb, hgrn_onorm, w_br_sb, w_br_dsa, w_br_hgrn, w_out, norm_mlp, w_up, w_down):
    if "nc" not in _NC_CACHE:
        _NC_CACHE["nc"] = build_two_pass()
    nc = _NC_CACHE["nc"]
    f = lambda a: np.ascontiguousarray(np.asarray(a, dtype=np.float32))
    cs, sn = rope_tables()
    shared = dict(norm_mix=f(norm_mix), w_in=f(w_in), qn_dsa=f(qn_dsa), kn_dsa=f(kn_dsa), hgrn_lb=f(hgrn_lb),
                  hgrn_onorm=f(hgrn_onorm), w_br_sb=f(w_br_sb), w_br_dsa=f(w_br_dsa), w_br_hgrn=f(w_br_hgrn),
                  w_out=f(w_out), norm_mlp=f(norm_mlp), w_up=f(w_up), w_down=f(w_down), rope_cos=cs, rope_sin=sn)
    xs = f(x)
    in_maps = [dict(shared, x=xs[b]) for b in range(8)]
    res = run_bass_kernel_spmd(nc, in_maps, core_ids=list(range(8)))
    return np.stack([np.asarray(r["out"], dtype=np.float32) for r in res.results], axis=0)
```

```python
import math
import numpy as np
from contextlib import ExitStack, contextmanager
import concourse.bass as bass
import concourse.mybir as mybir
from concourse.bass_utils import run_bass_kernel_spmd

F32 = mybir.dt.float32
BF16 = mybir.dt.bfloat16
AF = mybir.ActivationFunctionType
ALU = mybir.AluOpType
AX = mybir.AxisListType

T = 2048
D = 1024
NB = 16
DC = 8
DIN = 6856
DFF = 4096
DEPTH = 2
EPS = 1e-6
F_MIN = 1e-12
IDX_SCALE = (64 * 8) ** -0.5
C_SQ, C_SK, C_SV = 0, 384, 768
C_DQ, C_DK, C_DV = 1152, 1536, 1600
C_IQ, C_IK, C_IW = 1664, 2176, 2240
C_HQ, C_HF, C_HI, C_HG = 2248, 2760, 3272, 3528
C_G = 3784
N_BISECT = 15


class Buf:
    __slots__ = ("name", "w", "r", "dsem", "excl")

    def __init__(self, name, excl=False):
        self.name = name
        self.w = None
        self.r = {}
        self.dsem = None
        self.excl = excl


class Prog:
    ENG = ("pe", "act", "dve", "pool", "sp")
    CLEAR_NS = 330.0
    FILL_NS = {"dve": 66.0, "act": 190.0, "pool": 125.0}
    EST = {"dve": (60.0, 0.26), "act": (185.0, 0.83), "pool": (120.0, 0.8), "pe": (0.0, 0.0), "sp": (0.0, 0.0)}

    def __init__(self, nc, stack, needed=None):
        self.needed = needed
        self.used = set()
        self.remap = {}
        self.sig = {}
        self.fill = {}
        self.nc = nc
        self.stack = stack
        self.eng = {"pe": nc.tensor, "act": nc.scalar, "dve": nc.vector, "pool": nc.gpsimd, "sp": nc.sync}
        self.cnt = {e: 0 for e in self.ENG}
        self.known = {e: {} for e in self.ENG}
        self.sems = {}
        self.semval = {}
        for e in ("pe", "act", "dve", "pool"):
            self.sems["E_" + e] = stack.enter_context(nc.semaphore("sem_" + e))
            self.semval["E_" + e] = 0
        self.ndsem = 0
        self.free_dsems = []
        self.nwaits = 0
        self.tcum = {e: 0.0 for e in self.ENG}
        self.tend = {e: {} for e in self.ENG}

    def _dsem(self, buf):
        if buf.dsem is None:
            if self.free_dsems:
                key = self.free_dsems.pop()
            else:
                key = "D%d" % self.ndsem
                self.ndsem += 1
                self.sems[key] = self.stack.enter_context(self.nc.semaphore("dsem%d" % (self.ndsem - 1)))
                self.semval[key] = 0
            buf.dsem = key
        return buf.dsem

    def release(self, bufs):
        for b in bufs:
            if b.dsem is not None:
                self.free_dsems.append(b.dsem)
                b.dsem = None

    def _waits(self, eng, deps):
        need = {}
        own = "E_" + eng
        for (k, v) in deps:
            if eng == "pe" and k == "E_pe":
                continue
            if k == own and eng in ("act", "dve", "pool"):
                te = self.tend[eng].get(v)
                if te is not None and eng in self.fill:
                    gap = self.CLEAR_NS - (self.tcum[eng] - te)
                    if gap > 0:
                        n = int(math.ceil(gap / self.FILL_NS[eng]))
                        for _ in range(n):
                            self.fill[eng](self.eng[eng])
                        self.tcum[eng] += n * self.FILL_NS[eng]
                        self.nfill = getattr(self, "nfill", 0) + n
                continue
            if v > need.get(k, 0):
                need[k] = v
        out = []
        kn = self.known[eng]
        for k, v in need.items():
            if kn.get(k, 0) < v:
                kn[k] = v
                out.append((k, v))
        return out

    @staticmethod
    def _deps(reads, writes):
        deps = []
        for b in reads:
            if b.w is not None:
                deps.append(b.w)
            if b.excl:
                deps.extend(b.r.items())
        for b in writes:
            if b.w is not None:
                deps.append(b.w)
            deps.extend(b.r.items())
        return deps

    def _emit_waits(self, eng, waits):
        e = self.eng[eng]
        for (k, v) in waits:
            if k.startswith("E_"):
                self.used.add((k, v))
                if self.needed is not None:
                    v = self.remap[(k, v)]
            e.wait_ge(self.sems[k], v)
            self.nwaits += 1

    def _mark(self, ev, reads, writes):
        k, v = ev
        for b in reads:
            if b.r.get(k, 0) < v:
                b.r[k] = v
        for b in writes:
            b.w = ev
            b.r = {}

    def op(self, eng, fn, reads=(), writes=(), n=0):
        self.group(eng, [fn], reads, writes, n)

    def group(self, eng, fns, reads=(), writes=(), n=0):
        self._emit_waits(eng, self._waits(eng, self._deps(reads, writes)))
        e = self.eng[eng]
        for fn in fns[:-1]:
            fn(e)
        self.cnt[eng] += 1
        ov, pe_ = self.EST[eng]
        self.tcum[eng] += ov + pe_ * n
        td = self.tend[eng]
        td[self.cnt[eng]] = self.tcum[eng]
        if len(td) > 64:
            for k_ in sorted(td)[:32]:
                del td[k_]
        key = "E_" + eng
        self.semval[key] = self.cnt[eng]
        if self.needed is None or (key, self.cnt[eng]) in self.needed:
            self.sig[key] = self.sig.get(key, 0) + 1
            self.remap[(key, self.cnt[eng])] = self.sig[key]
            fns[-1](e).then_inc(self.sems[key], 1)
        else:
            fns[-1](e)
        self._mark((key, self.cnt[eng]), reads, writes)

    def dma(self, eng, fn, reads=(), writes=()):
        assert len(writes) == 1
        wb = writes[0]
        deps = self._deps(reads, writes)
        if eng == "pool" and getattr(self, "last_swdge", None) is not None:
            deps.append(self.last_swdge)
        self._emit_waits(eng, self._waits(eng, deps))
        key = self._dsem(wb)
        self.semval[key] += 16
        fn(self.eng[eng]).then_inc(self.sems[key], 16)
        if eng == "pool":
            self.last_swdge = (key, self.semval[key])
        self._mark((key, self.semval[key]), reads, writes)

    def barrier(self):
        deps = [(k, v) for k, v in self.semval.items() if v > 0]
        for eng in self.ENG:
            self._emit_waits(eng, self._waits(eng, deps))

    def wait_bufs(self, eng, bufs):
        deps = []
        for b in bufs:
            if b.w is not None:
                deps.append(b.w)
            deps.extend(b.r.items())
        self._emit_waits(eng, self._waits(eng, deps))


class G:
    pass


def bufs(prefix, *dims):
    if len(dims) == 1:
        return [Buf("%s%d" % (prefix, i)) for i in range(dims[0])]
    return [bufs("%s%d_" % (prefix, i), *dims[1:]) for i in range(dims[0])]


def build_program(stage=99, dump=None, needed=None):
    nc = bass.Bass("TRN2", target_bir_lowering=False)
    g = G()
    g.nc = nc
    g.stage = stage
    g.dump = dump
    import os
    g.ntl = int(os.environ.get("NTL", "9"))
    g.dbg_d = None
    if dump is not None:
        g.dbg_d = nc.dram_tensor("dbg", [128, 8 * T], BF16, kind="ExternalOutput").ap()
        g.dbgb = Buf("dbg")
    dt = lambda name, shape, kind, d=F32: nc.dram_tensor(name, shape, d, kind=kind).ap()
    g.x_d = dt("x", [T, D], "ExternalInput")
    g.norm_mix = dt("norm_mix", [DEPTH, D], "ExternalInput")
    g.w_in = dt("w_in", [DEPTH, D, DIN], "ExternalInput")
    g.qn = dt("qn_dsa", [DEPTH, 64], "ExternalInput")
    g.kn = dt("kn_dsa", [DEPTH, 64], "ExternalInput")
    g.lb_d = dt("hgrn_lb", [DEPTH, 512], "ExternalInput")
    g.onorm = dt("hgrn_onorm", [DEPTH, 64], "ExternalInput")
    g.w_sb = dt("w_br_sb", [DEPTH, 384, D], "ExternalInput")
    g.w_dsa = dt("w_br_dsa", [DEPTH, 384, D], "ExternalInput")
    g.w_hg = dt("w_br_hgrn", [DEPTH, 256, D], "ExternalInput")
    g.w_out = dt("w_out", [DEPTH, D, D], "ExternalInput")
    g.norm_mlp = dt("norm_mlp", [DEPTH, D], "ExternalInput")
    g.w_up = dt("w_up", [DEPTH, D, DFF], "ExternalInput")
    g.w_down = dt("w_down", [DEPTH, DFF, D], "ExternalInput")
    g.cs_d = dt("rope_cos", [T, 8], "ExternalInput")
    g.sn_d = dt("rope_sin", [T, 8], "ExternalInput")
    g.out_d = dt("out", [T, D], "ExternalOutput")
    g.xres_d = dt("xres", [T, D], "Internal")
    g.xin_b = bufs("xin", NB)
    g.xres_b = bufs("xres", NB)
    g.out_b = bufs("outb", NB)

    with ExitStack() as gs:
        P = Prog(nc, gs, needed)
        g.P = P
        fa = gs.enter_context(nc.sbuf_tensor("fill_a", [128, 2], F32))
        fd = gs.enter_context(nc.sbuf_tensor("fill_d", [128, 2], F32))
        nc.vector.memset(fd[:], 0.0)
        nc.vector.memset(fa[:], 0.0)
        P.fill["dve"] = lambda e: e.memset(fd[:, 0:1], 0.0)
        P.fill["act"] = lambda e: e.activation(out=fa[:, 0:1], in_=fa[:, 1:2], func=AF.Copy)
        fp = gs.enter_context(nc.sbuf_tensor("fill_p", [128, 2], F32))
        nc.gpsimd.memset(fp[:], 0.0)
        P.fill["pool"] = lambda e: e.memset(fp[:, 0:1], 0.0)
        g.uid = 0
        g.psF = [gs.enter_context(nc.psum_tensor("psF%d" % i, [128, 512], F32)) for i in range(6)]
        g.psFb = [Buf("psF%d" % i, excl=True) for i in range(6)]
        g.psB = [gs.enter_context(nc.psum_tensor("psB%d" % i, [128, 1024], BF16)) for i in range(2)]
        g.psBb = [Buf("psB%d" % i, excl=True) for i in range(2)]
        g.rotc = {}
        build_consts(g, gs)
        g.hT = gs.enter_context(nc.sbuf_tensor("hT_glob", [128, DC, T], BF16))
        g.hTb = bufs("hT", NB)
        g.pre = gs.enter_context(nc.sbuf_tensor("pre_w", [128, DC, 512], BF16))
        g.preb = Buf("pre_w")
        prefetch(g, ("sq", 0), win_cols(g, 0, C_SQ, C_SQ + 384), 384)
        for l in range(DEPTH):
            if g.stage >= 1:
                build_layer(g, l)
        if g.dbg_d is not None:
            P.wait_bufs("sp", [g.dbgb])
        P.wait_bufs("sp", g.out_b)
        P.barrier()
        g.used = P.used
        print("ops", P.cnt, "signals", P.sig, "waits", P.nwaits, "fillers", getattr(P, "nfill", 0), "dsems", P.ndsem, flush=True)
    return nc, P.used


def build_two_pass(stage=99, dump=None):
    _, used = build_program(stage, dump, None)
    nc, _ = build_program(stage, dump, used)
    return nc


def pipeline(stages, ntiles):
    ns = len(stages)
    for t in range(ntiles + ns - 1):
        for k, f in enumerate(stages):
            i = t - k
            if 0 <= i < ntiles:
                f(i)


def prefetch(g, tag, src, ncols):
    DMA(g, "pool", g.pre[:, :, 0:ncols], src, (), [g.preb])
    g.pre_tag = tag


def take_pre(g, tag):
    if getattr(g, "pre_tag", None) == tag:
        g.pre_tag = None
        return True
    return False


def rot(g, role, items):
    i = g.rotc.get(role, 0)
    g.rotc[role] = i + 1
    return items[i % len(items)]


def psf(g, role, banks):
    b = rot(g, role, banks)
    return g.psF[b], g.psFb[b]


def psb(g, role="pb"):
    b = rot(g, role, [0, 1])
    return g.psB[b], g.psBb[b]


@contextmanager
def scope(g):
    st = ExitStack()
    st.tbufs = []
    try:
        yield st
    finally:
        g.P.barrier()
        g.P.release(st.tbufs)
        st.close()


def sb(g, st, shape, dtype, name=None):
    g.uid += 1
    return st.enter_context(g.nc.sbuf_tensor("%s_%d" % (name or "t", g.uid), shape, dtype))


def nb(st, name):
    b = Buf(name)
    st.tbufs.append(b)
    return b


def nbs(st, prefix, *dims):
    r = bufs(prefix, *dims)

    def flat(x):
        if isinstance(x, Buf):
            st.tbufs.append(x)
        else:
            for y in x:
                flat(y)
    flat(r)
    return r


def _fs(ap):
    try:
        return int(ap.free_size())
    except Exception:
        return 0


def ACT(g, out, in_, func, reads, writes, **kw):
    g.P.op("act", lambda e: e.activation(out=out, in_=in_, func=func, **kw), reads, writes, _fs(out))


def TT(g, eng, out, in0, in1, op, reads, writes):
    g.P.op(eng, lambda e: e.tensor_tensor(out=out, in0=in0, in1=in1, op=op), reads, writes, _fs(out))


def TS(g, eng, out, in0, s1, s2, op0, op1, reads, writes, **kw):
    if op1 is None:
        s2 = 0.0 if isinstance(s1, (int, float)) else g.zc[0:in0.shape[0], 0:1]
        g.P.op(eng, lambda e: e.tensor_scalar(out=out, in0=in0, scalar1=s1, scalar2=s2, op0=op0, op1=ALU.add, **kw), reads, writes, _fs(out))
    else:
        g.P.op(eng, lambda e: e.tensor_scalar(out=out, in0=in0, scalar1=s1, scalar2=s2, op0=op0, op1=op1, **kw), reads, writes, _fs(out))


def STT(g, out, in0, scalar, in1, op0, op1, reads, writes):
    g.P.op("dve", lambda e: e.scalar_tensor_tensor(out=out, in0=in0, scalar=scalar, in1=in1, op0=op0, op1=op1), reads, writes, _fs(out))


def CP(g, eng, out, in_, reads, writes):
    if eng == "act":
        g.P.op("act", lambda e: e.activation(out=out, in_=in_, func=AF.Copy), reads, writes, _fs(out))
    else:
        g.P.op(eng, lambda e: e.tensor_copy(out, in_), reads, writes, _fs(out))


def MS(g, eng, ap, val, writes):
    g.P.op(eng, lambda e: e.memset(ap, val), (), writes, _fs(ap))


def ASEL(g, out, in_, pattern, cmp, fill, base, cm, reads, writes):
    g.P.op("pool", lambda e: e.affine_select(out=out, in_=in_, pattern=pattern, compare_op=cmp, fill=fill, base=base,
                                             channel_multiplier=cm), reads, writes, _fs(out))


def MM(g, outs_fns, reads, writes):
    g.P.group("pe", outs_fns, reads, writes)


def mmf(out, lhsT, rhs, start, stop):
    return lambda e: e.matmul(out, lhsT=lhsT, rhs=rhs, start=start, stop=stop)


def trf(out, in_, ident):
    return lambda e: e.transpose(out, in_, ident)


def DMA(g, eng, out, in_, reads, writes, **kw):
    g.P.dma(eng, lambda e: e.dma_start(out=out, in_=in_, **kw), reads, writes)


def build_consts(g, gs):
    nc = g.nc
    mk = lambda name, shape, d: gs.enter_context(nc.sbuf_tensor(name, shape, d))
    g.ident = mk("ident", [128, 128], BF16)
    g.negtri = mk("negtri", [128, 128], BF16)
    g.negones = mk("negones", [128, 128], BF16)
    g.onesb = mk("onesb", [128, 128], BF16)
    g.maskbd = mk("maskbd", [128, 128], F32)
    g.onesf = mk("onesf", [128, 128], F32)
    g.resetm = mk("resetm", [128, T], BF16)
    g.zc = mk("zc", [128, 1], F32)
    g.negbig = mk("negbig", [128, 1], F32)
    g.cs = mk("cs", [128, NB, 8], F32)
    g.sn = mk("sn", [128, NB, 8], F32)
    g.lbraw = mk("lbraw", [128, 2, 4], F32)
    g.lbv = mk("lbv", [128, 2, 4], F32)
    g.oml = mk("oml", [128, 2, 4], F32)
    g.cb = Buf("consts")
    g.csb = Buf("cs")
    g.snb = Buf("sn")
    g.lbb = Buf("lbraw")
    cb = [g.cb]
    MS(g, "pool", g.onesb[:], 1.0, cb)
    MS(g, "pool", g.negones[:], -1.0, cb)
    MS(g, "pool", g.onesf[:], 1.0, cb)
    MS(g, "pool", g.zc[:], 0.0, cb)
    MS(g, "pool", g.negbig[:], -1e29, cb)
    ASEL(g, g.ident[:], g.onesb[:], [[1, 128]], ALU.is_equal, 0.0, 0, -1, cb, cb)
    ASEL(g, g.negtri[:], g.negones[:], [[-1, 128]], ALU.is_ge, 0.0, 0, 1, cb, cb)
    ASEL(g, g.maskbd[:], g.onesf[:], [[1, 128]], ALU.is_ge, 0.0, 0, -1, cb, cb)
    MS(g, "pool", g.maskbd[0:64, 64:128], 0.0, cb)
    g.ones512 = mk("ones512", [128, 512], BF16)
    g.mlt = mk("mlt", [128, 512], BF16)
    MS(g, "pool", g.ones512[:], 1.0, cb)
    ASEL(g, g.mlt[:], g.ones512[:], [[1, 512]], ALU.is_gt, 0.0, 0, -1, cb, cb)
    g.caus01 = mk("caus01", [128, 128], F32)
    g.negfill = mk("negfill", [128, 128], F32)
    ASEL(g, g.caus01[:], g.onesf[:], [[-1, 128]], ALU.is_ge, 0.0, 0, 1, cb, cb)
    TS(g, "pool", g.negfill[:], g.caus01[:], -1.0, 1e30, ALU.add, ALU.mult, cb, cb)
    MS(g, "pool", g.resetm[:], 1.0, cb)
    MS(g, "pool", g.resetm[:].rearrange("p (c j) -> p c j", j=64)[:, :, 0:1], 0.0, cb)
    DMA(g, "sp", g.cs[:], g.cs_d.rearrange("(b p) i -> p b i", p=128), (), [g.csb])
    DMA(g, "sp", g.sn[:], g.sn_d.rearrange("(b p) i -> p b i", p=128), (), [g.snb])
    DMA(g, "sp", g.lbraw[:], g.lb_d.rearrange("l (h k) -> k l h", k=128), (), [g.lbb], allow_slow_non_contiguous=True)
    MS(g, "dve", g.lbv[:], 0.0, cb)
    TT(g, "dve", g.lbv[:, 1, :], g.lbraw[:, 1, :], g.lbraw[:, 0, :], ALU.subtract, [g.lbb], cb)
    ACT(g, g.lbv[:, 1, :], g.lbv[:, 1, :], AF.Sigmoid, cb, cb)
    TS(g, "dve", g.oml[:], g.lbv[:], -1.0, 1.0, ALU.mult, ALU.add, cb, cb)
    g.P.barrier()


class NormCtx:
    pass


def norm_setup(g, st, gain_d_row):
    c = NormCtx()
    c.gain = sb(g, st, [128, D], F32, "gain")
    c.gb = nb(st, "gain")
    DMA(g, "sp", c.gain[:], gain_d_row.to_broadcast([128, D]), (), [c.gb])
    c.junk = sb(g, st, [128, D], BF16, "junk")
    c.jb = nb(st, "junk")
    c.hbs = [sb(g, st, [128, D], BF16, "hb") for _ in range(2)]
    c.hbb = nbs(st, "hb", 2)
    c.ss = sb(g, st, [128, NB], F32, "ss")
    c.ssb = nbs(st, "ss", NB)
    MS(g, "dve", c.ss[:], 0.0, c.ssb)
    return c


def norm_block(g, c, xb, xbuf, tb, hT, hTb):
    hb, hbuf = c.hbs[tb % 2], c.hbb[tb % 2]
    s1 = c.ss[:, tb:tb + 1]
    ACT(g, c.junk[:], xb[:], AF.Square, [xbuf, c.ssb[tb]], [c.jb, c.ssb[tb]], accum_out=s1)
    ACT(g, s1, s1, AF.Ln, [c.ssb[tb]], [c.ssb[tb]], scale=1.0 / D, bias=EPS)
    ACT(g, s1, s1, AF.Exp, [c.ssb[tb]], [c.ssb[tb]], scale=-0.5)
    STT(g, hb[:], xb[:], s1, c.gain[:], ALU.mult, ALU.mult, [xbuf, c.ssb[tb], c.gb], [hbuf])
    for half in range(2):
        pt, ptb = psb(g)
        MM(g, [trf(pt[:, m * 128:(m + 1) * 128], hb[:, (half * 4 + m) * 128:(half * 4 + m + 1) * 128], g.ident[:])
               for m in range(4)], [hbuf, g.cb], [ptb])
        CP(g, "act" if half == 0 else "dve", hT[:, half * 4:half * 4 + 4, tb * 128:(tb + 1) * 128],
           pt[:, 0:512].rearrange("p (m j) -> p m j", j=128), [ptb], [hTb[tb]])


def norm_T(g, st, src_ap, src_bufs, gain_d_row, hT, hTb):
    c = norm_setup(g, st, gain_d_row)
    xbs = [sb(g, st, [128, D], F32, "xb") for _ in range(4)]
    xbb = nbs(st, "xb", 4)
    for tb in range(NB):
        xb, xbuf = xbs[tb % 4], xbb[tb % 4]
        DMA(g, "sp", xb[:], src_ap[tb * 128:(tb + 1) * 128, :], [src_bufs[tb]], [xbuf])
        norm_block(g, c, xb, xbuf, tb, hT, hTb)


def win_cols(g, l, c0, c1):
    return g.w_in[l].rearrange("(c p) n -> p c n", p=128)[:, :, c0:c1]


def mm_fm(g, ps, M, n, w, wb, col0, hT, hTb, tok0, role_bufs):
    MM(g, [mmf(ps[0:M, 0:n], w[:, c, col0:col0 + M], hT[:, c, tok0:tok0 + n], c == 0, c == DC - 1) for c in range(DC)],
       [wb] + hTb[tok0 // 128:(tok0 + n + 127) // 128], [role_bufs])


def mm_tm(g, ps, N, w, wb, col0, hT, hTb, tb, psbuf):
    MM(g, [mmf(ps[:, 0:N], hT[:, c, tb * 128:(tb + 1) * 128], w[:, c, col0:col0 + N], c == 0, c == DC - 1) for c in range(DC)],
       [wb, hTb[tb]], [psbuf])


def phase_sb(g, l, hT, hTb, osbT, osbb):
    with scope(g) as st:
        ws = []
        for i, c0 in enumerate((C_SQ, C_SK, C_SV)):
            if i == 0 and take_pre(g, ("sq", l)):
                ws.append((g.pre, g.preb))
                continue
            w = sb(g, st, [128, DC, 384], BF16, "wsb")
            wb = nb(st, "wsb%d" % i)
            DMA(g, "pool", w[:], win_cols(g, l, c0, c0 + 384), (), [wb])
            ws.append((w, wb))
        sqT = sb(g, st, [128, 3, T], BF16, "sqT")
        skT = sb(g, st, [128, 3, T], BF16, "skT")
        sqb = nbs(st, "sq", 3, 4)
        skb = nbs(st, "sk", 3, 4)
        k = 0
        for (dst, dstb, (w, wb), scl) in ((sqT, sqb, ws[0], 0.125), (skT, skb, ws[1], 1.0)):
            for hp in range(3):
                for tc in range(4):
                    ps, pb = psf(g, "proj", [0, 1, 2, 3, 4, 5])
                    mm_fm(g, ps, 128, 512, w, wb, hp * 128, hT, hTb, tc * 512, pb)
                    o = dst[:, hp, tc * 512:(tc + 1) * 512]
                    if k % 2 == 0:
                        ACT(g, o, ps[:, :], AF.Copy, [pb], [dstb[hp][tc]], scale=scl)
                    else:
                        TS(g, "dve", o, ps[:, :], scl, None, ALU.mult, None, [pb], [dstb[hp][tc]])
                    k += 1
        svp = [sb(g, st, [128, NB, 384], BF16, "svp") for _ in range(2)]
        svb = nbs(st, "sv", 2, NB)
        for s_ in range(2):
            MS(g, "pool", svp[s_][:].rearrange("p t c -> p (t c)"), 0.0, svb[s_])
        for tb in range(NB):
            ps, pb = psf(g, "proj", [0, 1, 2, 3, 4, 5])
            mm_tm(g, ps, 384, ws[2][0], ws[2][1], 0, hT, hTb, tb, pb)
            src = ps[:, 0:384].rearrange("p (m s d) -> p m s d", s=2, d=64)
            for s_ in range(2):
                dst = svp[s_][:, tb, :].rearrange("p (m s d) -> p m s d", s=2, d=64)
                CP(g, "act" if s_ == 0 else "dve", dst[:, :, s_, :], src[:, :, s_, :], [pb], [svb[s_][tb]])
        prefetch(g, ("wA", l), win_cols(g, l, C_DQ, C_DQ + 512), 512)
        R = 4
        mk2 = lambda shape, dt_, nm: [[sb(g, st, shape, dt_, nm) for _ in range(R)] for _ in range(2)]
        Et, SPt, SPs, At = mk2([128, 512], F32, "Et"), mk2([128, 512], BF16, "SPt"), mk2([128, 512], BF16, "SPs"), mk2([128, 512], BF16, "At")
        Etb, SPb, SPsb, Atb = nbs(st, "Et", 2, R), nbs(st, "SPt", 2, R), nbs(st, "SPs", 2, R), nbs(st, "At", 2, R)
        steps = []
        for m in range(3):
            for qc in range(4):
                for n_, kb in enumerate(range(4 * qc + 3, -1, -1)):
                    steps.append((m, qc, kb, n_))
        stt = {}
        pso = {}

        def info(t):
            m, qc, kb, n_ = steps[t]
            j0 = max(0, kb * 128 - qc * 512)
            return m, qc, kb, n_, j0, kb >= 4 * qc, qc * 512 + j0 - kb * 128, n_ == 0

        def opnds(t):
            m, qc, kb, n_, j0, diag, base, first = info(t)
            kk = [skT[64 * s_:64 * s_ + 64, m, kb * 128:(kb + 1) * 128] for s_ in range(2)]
            qq = [sqT[64 * s_:64 * s_ + 64, m, qc * 512 + j0:(qc + 1) * 512] for s_ in range(2)]
            return kk, qq, skb[m][kb // 4], sqb[m][qc]

        def s1(t):
            m, qc, kb, n_, j0, diag, base, first = info(t)
            kk, qq, rk, rq = opnds(t)
            pz = [psf(g, "sbZ", [0, 1, 2]) for _ in range(2)]
            stt[t] = {"pz": pz}
            for s_ in range(2):
                MM(g, [mmf(pz[s_][0][:, j0:512], kk[s_], qq[s_], True, True)], [rk, rq], [pz[s_][1]])

        def s2(t):
            m, qc, kb, n_, j0, diag, base, first = info(t)
            ib = t % R
            pz = stt[t]["pz"]
            for s_ in range(2):
                ACT(g, Et[s_][ib][:, j0:512], pz[s_][0][:, j0:512], AF.Exp, [pz[s_][1]], [Etb[s_][ib]])

        def s3(t):
            m, qc, kb, n_, j0, diag, base, first = info(t)
            ib = t % R
            for s_ in range(2):
                ACT(g, SPt[s_][ib][:, j0:512], Et[s_][ib][:, j0:512], AF.Ln, [Etb[s_][ib]], [SPb[s_][ib]], bias=1.0)
            if diag:
                for s_ in range(2):
                    S = SPt[s_][ib]
                    assert base == 0
                    TT(g, "pool", S[:, j0:512], S[:, j0:512], g.mlt[:, 0:512 - j0], ALU.mult, [SPb[s_][ib], g.cb], [SPb[s_][ib]])
            if kb > 0:
                for s_ in range(2):
                    S, Sb = SPt[s_][ib], SPb[s_][ib]
                    Sn, Snb = SPs[s_][n_ % R], SPsb[s_][n_ % R]
                    if first:
                        if j0 > 0:
                            MS(g, "pool", Sn[:, 0:j0], 0.0, [Snb])
                        CP(g, "pool", Sn[:, j0:512], S[:, j0:512], [Sb], [Snb])
                    else:
                        So, Sob = SPs[s_][(n_ - 1) % R], SPsb[s_][(n_ - 1) % R]
                        if j0 > 0:
                            CP(g, "pool", Sn[:, 0:j0], So[:, 0:j0], [Sob], [Snb])
                        TT(g, "dve", Sn[:, j0:512], So[:, j0:512], S[:, j0:512], ALU.add, [Sob, Sb], [Snb])

        def s4(t):
            m, qc, kb, n_, j0, diag, base, first = info(t)
            ib = t % R
            kk, qq, rk, rq = opnds(t)
            pc = [psf(g, "sbC", [3, 4]) for _ in range(2)]
            stt[t]["pc"] = pc
            for s_ in range(2):
                S = SPt[s_][ib]
                fns = [mmf(pc[s_][0][:, j0:512], kk[s_], qq[s_], True, False),
                       mmf(pc[s_][0][:, j0:512], g.negtri[:], S[:, j0:512], False, first)]
                rd = [rk, rq, SPb[s_][ib], g.cb]
                if not first:
                    So, Sob = SPs[s_][(n_ - 1) % R], SPsb[s_][(n_ - 1) % R]
                    fns.append(mmf(pc[s_][0][:, j0:512], g.negones[:], So[:, j0:512], False, True))
                    rd.append(Sob)
                MM(g, fns, rd, [pc[s_][1]])

        def s5(t):
            m, qc, kb, n_, j0, diag, base, first = info(t)
            ib = t % R
            pc = stt[t]["pc"]
            for s_ in range(2):
                ACT(g, At[s_][ib][:, j0:512], pc[s_][0][:, j0:512], AF.Exp, [pc[s_][1]], [Atb[s_][ib]])
            for s_ in range(2):
                A, Ab = At[s_][ib], Atb[s_][ib]
                if diag:
                    TT(g, "pool", A[:, j0:512], A[:, j0:512], g.mlt[:, 0:512 - j0], ALU.mult, [Ab, g.cb], [Ab])
                if first and j0 > 0:
                    MS(g, "pool", A[:, 0:j0], 0.0, [Ab])

        def s6(t):
            m, qc, kb, n_, j0, diag, base, first = info(t)
            ib = t % R
            if first:
                pso[(m, qc)] = psf(g, "sbO", [5])
            psO, pOb = pso[(m, qc)]
            for s_ in range(2):
                A, Ab = At[s_][ib], Atb[s_][ib]
                vv = svp[s_][:, kb, m * 128:(m + 1) * 128]
                c0 = 0 if first else j0
                MM(g, [mmf(psO[:, c0:512], vv, A[:, c0:512], first and s_ == 0, kb == 0 and s_ == 1)], [svb[s_][kb], Ab], [pOb])
            if kb == 0:
                CP(g, "act" if (m * 4 + qc) % 2 == 0 else "dve", osbT[:, m, qc * 512:(qc + 1) * 512], psO[:, :], [pOb], [osbb[m][qc]])
            del stt[t]

        nst = len(steps)
        stages = [s1, s2, s3, s4, s5, s6]
        for e in range(nst + len(stages) - 1):
            for k in range(len(stages) - 1, -1, -1):
                t = e - k
                if 0 <= t < nst:
                    stages[k](t)


def phase_dsa(g, l, hT, hTb, odT, odb):
    P = g.P
    with scope(g) as st:
        featT = sb(g, st, [128, 9, T], BF16, "featT")
        fb = nbs(st, "feat", NB)
        dvx = sb(g, st, [128, NB, 128], BF16, "dvx")
        dvb = nbs(st, "dvx", NB)
        sgn = sb(g, st, [128, NB, 8], F32, "sgn")
        sgb = nbs(st, "sgn", NB)
        qkg = sb(g, st, [128, 7, 64], F32, "qkg")
        qkgb = nbs(st, "qkg", 7)
        for hh in range(7):
            src = (g.qn if hh < 6 else g.kn)[l:l + 1, :].to_broadcast([128, 64])
            DMA(g, "sp", qkg[:, hh, :], src, (), [qkgb[hh]])
        with scope(g) as s2:
            wB = sb(g, s2, [128, DC, 512], BF16, "wB")
            wC = sb(g, s2, [128, DC, 72], BF16, "wC")
            wBb, wCb = nb(s2, "wB"), nb(s2, "wC")
            if take_pre(g, ("wA", l)):
                wA, wAb = g.pre, g.preb
            else:
                wA = sb(g, s2, [128, DC, 512], BF16, "wA")
                wAb = nb(s2, "wA")
                DMA(g, "pool", wA[:], win_cols(g, l, C_DQ, C_DQ + 512), (), [wAb])
            DMA(g, "pool", wB[:], win_cols(g, l, C_IQ, C_IQ + 512), (), [wBb])
            DMA(g, "pool", wC[:], win_cols(g, l, C_IK, C_IK + 72), (), [wCb])
            NR = 4
            tq = [sb(g, s2, [128, 18, 64], F32, "tokq") for _ in range(NR)]
            tqb = nbs(s2, "tokq", NR)
            tbq = [sb(g, s2, [128, 18, 64], BF16, "tokb") for _ in range(3)]
            tbb = nbs(s2, "tokb", 3)
            sqt = [sb(g, s2, [128, 448], F32, "sqt") for _ in range(2)]
            sqtb = nbs(s2, "sqt", 2)
            smq = [sb(g, s2, [128, 32], F32, "small") for _ in range(2)]
            smqb = nbs(s2, "small", 2)
            rt = [sb(g, s2, [128, 18, 8], F32, "ropet") for _ in range(4)]
            rtb = nbs(s2, "ropet", 4)
            pst = {}

            def p1(tb):
                MS(g, "pool", dvx[:, tb, 64:128], 1.0, [dvb[tb]])
                tk, tkb = tq[tb % NR], tqb[tb % NR]
                MS(g, "pool", tk[:, 7, :], 0.0, [tkb])
                MS(g, "pool", tk[:, 17, :], 0.0, [tkb])
                psA, pAb = psf(g, "dA", [0, 1])
                psBq, pBb = psf(g, "dB", [2, 3])
                psC, pCb = psf(g, "dC", [4, 5])
                pst[tb] = (psA, pAb, psBq, pBb, psC, pCb)
                mm_tm(g, psA, 512, wA, wAb, 0, hT, hTb, tb, pAb)
                mm_tm(g, psBq, 512, wB, wBb, 0, hT, hTb, tb, pBb)
                mm_tm(g, psC, 72, wC, wCb, 0, hT, hTb, tb, pCb)

            def p2(tb):
                psA, pAb, psBq, pBb, psC, pCb = pst.pop(tb)
                tk, tkb = tq[tb % NR], tqb[tb % NR]
                sq_, sq_b = sqt[tb % 2], sqtb[tb % 2]
                sm, smb = smq[tb % 2], smqb[tb % 2]
                ACT(g, sq_[:], psA[:, 0:448], AF.Square, [pAb], [sq_b])
                ss = sm[:, 0:7]
                P.op("dve", lambda e, ss=ss, sq_=sq_: e.tensor_reduce(out=ss, in_=sq_[:].rearrange("p (h d) -> p h d", d=64), axis=AX.X,
                                                                      op=ALU.add), [sq_b], [smb], 448)
                ACT(g, ss, ss, AF.Ln, [smb], [smb], scale=1.0 / 64, bias=EPS)
                ACT(g, ss, ss, AF.Exp, [smb], [smb], scale=-0.5)
                aw = sm[:, 8:16]
                TS(g, "dve", sgn[:, tb, :], psC[:, 64:72], 0.0, 2.0, ALU.is_gt, ALU.mult, [pCb], [sgb[tb]])
                TS(g, "dve", sgn[:, tb, :], sgn[:, tb, :], -1.0, 0.0, ALU.add, ALU.add, [sgb[tb]], [sgb[tb]])
                STT(g, aw, psC[:, 64:72], IDX_SCALE, sgn[:, tb, :], ALU.mult, ALU.mult, [pCb, sgb[tb]], [smb])
                TT(g, "dve", tk[:, 8:16, :], psBq[:, :].rearrange("p (h d) -> p h d", d=64),
                   aw.unsqueeze(2).to_broadcast([128, 8, 64]), ALU.mult, [pBb, smb], [tkb])
                CP(g, "act", tk[:, 16, :], psC[:, 0:64], [pCb], [tkb])
                CP(g, "act", dvx[:, tb, 0:64], psA[:, 448:512], [pAb], [dvb[tb]])
                TT(g, "dve", tk[:, 0:7, :], psA[:, 0:448].rearrange("p (h d) -> p h d", d=64),
                   ss.unsqueeze(2).to_broadcast([128, 7, 64]), ALU.mult, [pAb, smb], [tkb])
                TT(g, "dve", tk[:, 0:7, :], tk[:, 0:7, :], qkg[:], ALU.mult, [tkb] + qkgb, [tkb])

            def p3(tb):
                tk, tkb = tq[tb % NR], tqb[tb % NR]
                x1, x2 = tk[:, :, 0:8], tk[:, :, 8:16]
                cb_ = g.cs[:, tb, :].unsqueeze(1).to_broadcast([128, 18, 8])
                sb_ = g.sn[:, tb, :].unsqueeze(1).to_broadcast([128, 18, 8])
                TT(g, "dve", rt[0][:], x1, cb_, ALU.mult, [tkb, g.csb], [rtb[0]])
                TT(g, "pool", rt[1][:], x2, sb_, ALU.mult, [tkb, g.snb], [rtb[1]])
                TT(g, "dve", rt[2][:], x2, cb_, ALU.mult, [tkb, g.csb], [rtb[2]])
                TT(g, "pool", rt[3][:], x1, sb_, ALU.mult, [tkb, g.snb], [rtb[3]])
                TT(g, "dve", x1, rt[0][:], rt[1][:], ALU.subtract, [rtb[0], rtb[1]], [tkb])
                TT(g, "pool", x2, rt[2][:], rt[3][:], ALU.add, [rtb[2], rtb[3]], [tkb])

            def p4(tb):
                tk, tkb = tq[tb % NR], tqb[tb % NR]
                tkh, tkhb = tbq[tb % 3], tbb[tb % 3]
                CP(g, "act", tkh[:], tk[:], [tkb], [tkhb])
                CP(g, "pool", tkh[:, 7, :], tkh[:, 6, :], [tkhb], [tkhb])
                CP(g, "pool", tkh[:, 17, :], tkh[:, 16, :], [tkhb], [tkhb])

            def p5(tb):
                tkh, tkhb = tbq[tb % 3], tbb[tb % 3]
                flat = tkh[:].rearrange("p h d -> p (h d)")
                pt, ptb = psb(g)
                MM(g, [trf(pt[:, m * 128:(m + 1) * 128], flat[:, m * 128:(m + 1) * 128], g.ident[:]) for m in range(8)],
                   [tkhb, g.cb], [ptb])
                CP(g, "dve", featT[:, 0:8, tb * 128:(tb + 1) * 128], pt[:, :].rearrange("p (m j) -> p m j", j=128), [ptb], [fb[tb]])
                pt2, ptb2 = psb(g)
                MM(g, [trf(pt2[:, 0:128], flat[:, 1024:1152], g.ident[:])], [tkhb, g.cb], [ptb2])
                CP(g, "act", featT[:, 8, tb * 128:(tb + 1) * 128], pt2[:, 0:128], [ptb2], [fb[tb]])

            stages = [p1, p2, p3, p4, p5]
            for e in range(NB + len(stages) - 1):
                for k in range(len(stages) - 1, -1, -1):
                    t_ = e - k
                    if 0 <= t_ < NB:
                        stages[k](t_)
        prefetch(g, ("hihg", l), win_cols(g, l, C_HI, C_HI + 512), 512)
        sc = [sb(g, st, [128, T], F32, "sc") for _ in range(4)]
        scb = nbs(st, "sc", 4, 4)
        junk = sb(g, st, [128, T], BF16, "junk")
        maskq = [sb(g, st, [128, T], BF16, "maskq") for _ in range(2)]
        mqb = nbs(st, "maskq", 2)
        maskT = sb(g, st, [128, NB, 512], BF16, "maskT")
        mTb = nbs(st, "maskT", 4)
        rj = [sb(g, st, [128, 512], BF16, "rj") for _ in range(4)]
        rjb = nbs(st, "rj", 4)
        dg = [sb(g, st, [128, 8, 128], BF16, "dg") for _ in range(2)]
        dgb = nbs(st, "dg", 2)
        sm = [sb(g, st, [128, 8 + 2 * N_BISECT], F32, "bis") for _ in range(2)]
        smb = nbs(st, "bis", 2)
        cvec = sb(g, st, [128, N_BISECT], F32, "cvec")
        c255 = sb(g, st, [128, 1], F32, "c255")
        cvb = nb(st, "cvec")
        for n_ in range(N_BISECT):
            MS(g, "pool", cvec[:, n_:n_ + 1], 2.0 ** -(n_ + 1), [cvb])
        MS(g, "pool", c255[:], 255.5, [cvb])
        Pt = [sb(g, st, [128, 512], BF16, "Pt") for _ in range(4)]
        Ptb = nbs(st, "Pt", 4)
        Pm = [sb(g, st, [128, 512], BF16, "Pm") for _ in range(4)]
        Pmb = nbs(st, "Pm", 4)
        rs = sb(g, st, [64, 512], F32, "rs")
        rsb = nb(st, "rs")
        cnt_ = {"ri": 0, "pi": 0, "pm": 0}

        def idx_blocks(blocks):
            tiles = []
            for i in blocks:
                nk = (i + 1) * 128
                d_, d_b = dg[i % 2], dgb[i % 2]
                for j in range(8):
                    TS(g, "pool", d_[:, j, :], g.ident[:], sgn[:, i, j:j + 1], None, ALU.mult, None, [g.cb, sgb[i]], [d_b])
                for kc in range((nk + 511) // 512):
                    n = min(512, nk - kc * 512)
                    for j in range(8):
                        tiles.append((i, kc, n, j))
            stt = {}

            def s1(t):
                i, kc, n, j = tiles[t]
                po = 64 * (j % 2)
                psZ, pZb = psf(g, "ixZ", [0, 1, 2])
                stt[t] = [psZ, pZb]
                MM(g, [mmf(psZ[:, 0:n], featT[po:po + 64, 4 + j // 2, i * 128:(i + 1) * 128],
                           featT[po:po + 64, 8, kc * 512:kc * 512 + n], True, True)],
                   [fb[i]] + fb[kc * 4:(kc * 512 + n) // 128], [pZb])

            def s2(t):
                i, kc, n, j = tiles[t]
                psZ, pZb = stt[t]
                r_, r_b = rj[cnt_["ri"] % 4], rjb[cnt_["ri"] % 4]
                cnt_["ri"] += 1
                stt[t] += [r_, r_b]
                ACT(g, r_[:, 0:n], psZ[:, 0:n], AF.Relu, [pZb], [r_b])

            def s3(t):
                i, kc, n, j = tiles[t]
                r_, r_b = stt[t][2], stt[t][3]
                if j == 0:
                    cnt_["psS"] = psf(g, "ixS", [3, 4])
                psS, pSb = cnt_["psS"]
                d_, d_b = dg[i % 2], dgb[i % 2]
                MM(g, [mmf(psS[:, 0:n], d_[:, j, :], r_[:, 0:n], j == 0, j == 7)], [d_b, r_b], [pSb])
                if j == 7:
                    CP(g, "act", sc[i % 4][:, kc * 512:kc * 512 + n], psS[:, 0:n], [pSb], [scb[i % 4][kc]])
                del stt[t]

            pipeline([s1, s2, s3], len(tiles))

        def bis_pair(p):
            blocks = [2 * p, 2 * p + 1]
            st_ = []
            for bi, i in enumerate(blocks):
                nk = (i + 1) * 128
                s_, s_b = sc[i % 4], scb[i % 4]
                nkc = (nk + 511) // 512
                srd = s_b[0:nkc]
                m_, m_b = sm[bi], smb[bi]
                rmax, rmin, step0, mid, cntv, tt = (m_[:, c:c + 1] for c in range(6))
                stepc = m_[:, 8:8 + N_BISECT]
                if nk > 256:
                    P.op("dve", lambda e, s_=s_, nk=nk, rmax=rmax: e.tensor_reduce(out=rmax, in_=s_[:, 0:nk], axis=AX.X, op=ALU.max), srd, [m_b], nk)
                    P.op("dve", lambda e, s_=s_, nk=nk, rmin=rmin: e.tensor_reduce(out=rmin, in_=s_[:, 0:nk], axis=AX.X, op=ALU.min), srd, [m_b], nk)
                dsl = s_[:, i * 128:(i + 1) * 128]
                TT(g, "pool", dsl, dsl, g.caus01[:], ALU.mult, [s_b[i // 4], g.cb], [s_b[i // 4]])
                TT(g, "pool", dsl, dsl, g.negfill[:], ALU.add, [s_b[i // 4], g.cb], [s_b[i // 4]])
                st_.append((i, nk, s_, srd, m_, m_b, rmax, rmin, step0, mid, cntv, tt, stepc))
            act = [x for x in st_ if x[1] > 256]
            for (i, nk, s_, srd, m_, m_b, rmax, rmin, step0, mid, cntv, tt, stepc) in act:
                TT(g, "dve", step0, rmax, rmin, ALU.subtract, [m_b], [m_b])
            for (i, nk, s_, srd, m_, m_b, rmax, rmin, step0, mid, cntv, tt, stepc) in act:
                TS(g, "dve", stepc, cvec[:], step0, None, ALU.mult, None, [m_b, cvb], [m_b])
            for (i, nk, s_, srd, m_, m_b, rmax, rmin, step0, mid, cntv, tt, stepc) in act:
                TS(g, "dve", mid, stepc[:, 0:1], rmin, g.zc[:, 0:1], ALU.add, ALU.add, [m_b, g.cb], [m_b])
            for n_ in range(N_BISECT):
                for (i, nk, s_, srd, m_, m_b, rmax, rmin, step0, mid, cntv, tt, stepc) in act:
                    TS(g, "dve", junk[:, 0:nk], s_[:, 0:nk], mid, g.zc[:, 0:1], ALU.is_ge, ALU.add, srd + [m_b, g.cb], [m_b], accum_out=cntv)
                for (i, nk, s_, srd, m_, m_b, rmax, rmin, step0, mid, cntv, tt, stepc) in act:
                    TS(g, "dve", tt, cntv, c255[:, 0:1], stepc[:, n_:n_ + 1], ALU.is_ge, ALU.mult, [m_b, cvb], [m_b])
                for (i, nk, s_, srd, m_, m_b, rmax, rmin, step0, mid, cntv, tt, stepc) in act:
                    nn = min(n_ + 1, N_BISECT - 1)
                    TS(g, "dve", mid, tt, stepc[:, nn:nn + 1], mid, ALU.subtract, ALU.add, [m_b], [m_b])
            for (i, nk, s_, srd, m_, m_b, rmax, rmin, step0, mid, cntv, tt, stepc) in st_:
                thr = mid if nk > 256 else g.negbig[:, 0:1]
                mq, mq_b = maskq[i % 2], mqb[i % 2]
                TS(g, "dve", mq[:, 0:nk], s_[:, 0:nk], thr, None, ALU.is_ge, None, srd + [m_b, g.cb], [mq_b])

        def mT_pair(p):
            for i in (2 * p, 2 * p + 1):
                ii = i % 4
                mq, mq_b = maskq[i % 2], mqb[i % 2]
                for k0 in range(0, i + 1, 8):
                    k1 = min(i + 1, k0 + 8)
                    pt, ptb = psb(g)
                    MM(g, [trf(pt[:, (kb - k0) * 128:(kb - k0 + 1) * 128], mq[:, kb * 128:(kb + 1) * 128], g.ident[:])
                           for kb in range(k0, k1)], [mq_b, g.cb], [ptb])
                    CP(g, "act", maskT[:, k0:k1, ii * 128:(ii + 1) * 128],
                       pt[:, 0:(k1 - k0) * 128].rearrange("p (m j) -> p m j", j=128), [ptb], [mTb[ii]])

        def att_chunk(qc):
            last = 4 * qc + 3
            tiles = [(h, kb) for h in range(6) for kb in range(last + 1)]
            stt = {}
            pso = {}

            def s1(t):
                h, kb = tiles[t]
                hp, po = h // 2, 64 * (h % 2)
                j0 = max(0, kb * 128 - qc * 512)
                psL, pLb = psf(g, "dsL", [0, 1, 2])
                stt[t] = [psL, pLb]
                MM(g, [mmf(psL[:, j0:512], featT[po:po + 64, 3, kb * 128:(kb + 1) * 128],
                           featT[po:po + 64, hp, qc * 512 + j0:(qc + 1) * 512], True, True)],
                   [fb[kb]] + fb[qc * 4:qc * 4 + 4], [pLb])

            def s2(t):
                h, kb = tiles[t]
                j0 = max(0, kb * 128 - qc * 512)
                psL, pLb = stt[t][0], stt[t][1]
                pi = cnt_["pi"]
                cnt_["pi"] += 1
                p_, p_b = Pt[pi % 4], Ptb[pi % 4]
                stt[t] += [p_, p_b]
                ACT(g, p_[:, j0:512], psL[:, j0:512], AF.Exp, [pLb], [p_b], scale=0.125)

            def s3(t):
                h, kb = tiles[t]
                j0 = max(0, kb * 128 - qc * 512)
                p_, p_b = stt[t][2], stt[t][3]
                pm = cnt_["pm"]
                cnt_["pm"] += 1
                m_, m_b = Pm[pm % 4], Pmb[pm % 4]
                stt[t] += [m_, m_b]
                TT(g, "pool" if t % 3 == 0 else "dve", m_[:, j0:512], p_[:, j0:512], maskT[:, kb, j0:512], ALU.mult,
                   [p_b] + mTb[j0 // 128:4], [m_b])

            def s4(t):
                h, kb = tiles[t]
                hp, po = h // 2, 64 * (h % 2)
                j0 = max(0, kb * 128 - qc * 512)
                m_, m_b = stt[t][4], stt[t][5]
                if kb == 0:
                    pso[h] = psf(g, "dsO", [4, 5])
                psO, pOb = pso[h]
                MM(g, [mmf(psO[:, j0:512], dvx[:, kb, :], m_[:, j0:512], kb == 0, kb == last)], [dvb[kb], m_b], [pOb])
                if kb == last:
                    ACT(g, rs[0:64, :], psO[64:128, :], AF.Ln, [pOb], [rsb])
                    ACT(g, rs[0:64, :], rs[0:64, :], AF.Exp, [rsb], [rsb], scale=-1.0)
                    TT(g, "dve", odT[po:po + 64, hp, qc * 512:(qc + 1) * 512], psO[0:64, :], rs[0:64, :], ALU.mult, [pOb, rsb],
                       [odb[hp][qc]])
                del stt[t]

            pipeline([s1, s2, s3, s4], len(tiles))

        idx_blocks([14, 15])
        for p in range(7, -1, -1):
            if p > 0:
                idx_blocks([2 * p - 2, 2 * p - 1])
            bis_pair(p)
            mT_pair(p)
            if p % 2 == 0:
                att_chunk(p // 2)


def phase_hgrn(g, l, hT, hTb, ohT, ohb):
    P = g.P
    import os
    if int(os.environ.get("HGL", "9")) == 0:
        return
    with scope(g) as st:
        hi_tm = sb(g, st, [128, NB, 256], BF16, "hi_tm")
        hib = nbs(st, "hi", NB)
        hgs = sb(g, st, [128, NB, 256], BF16, "hgs")
        hgb = nbs(st, "hgs", NB)
        onb = sb(g, st, [128, 64], F32, "onorm")
        onbb = nb(st, "onorm")
        DMA(g, "sp", onb[:], g.onorm[l:l + 1, :].to_broadcast([128, 64]), (), [onbb])
        with scope(g) as s2:
            if take_pre(g, ("hihg", l)):
                w, wb = g.pre, g.preb
            else:
                w = sb(g, s2, [128, DC, 512], BF16, "whihg")
                wb = nb(s2, "whihg")
                DMA(g, "pool", w[:], win_cols(g, l, C_HI, C_HI + 512), (), [wb])
            sgs = [sb(g, s2, [128, 256], F32, "sgs") for _ in range(2)]
            sgsb = nbs(s2, "sgs", 2)
            for tb in range(NB):
                ps, pb = psf(g, "proj", [0, 1, 2, 3, 4, 5])
                hgv = int(os.environ.get("HGV", "15"))
                if hgv & 8:
                    mm_tm(g, ps, 512, w, wb, 0, hT, hTb, tb, pb)
                if hgv & 1:
                    CP(g, "dve", hi_tm[:, tb, :], ps[:, 0:256], [pb], [hib[tb]])
                sgt, sgtb = sgs[tb % 2], sgsb[tb % 2]
                if hgv & 2:
                    ACT(g, sgt[:], ps[:, 256:512], AF.Exp if hgv & 16 else AF.Sigmoid, [pb], [sgtb])
                if hgv & 4:
                    TT(g, "dve", hgs[:, tb, :], ps[:, 256:512], sgt[:], ALU.mult, [pb, sgtb], [hgb[tb]])
        with scope(g) as s3:
            NH = 4
            R = 4
            qtT = sb(g, s3, [128, NH, T], BF16, "qtT")
            ktT = sb(g, s3, [128, NH, T], BF16, "ktT")
            qtb = nbs(s3, "qt", NH)
            ktb = nbs(s3, "kt", NH)
            kt_tm = sb(g, s3, [128, NB, NH * 128], BF16, "kt_tm")
            kttb = nbs(s3, "kttm", NH)
            t1 = sb(g, s3, [128, T], F32, "t1")
            t2 = sb(g, s3, [128, T], F32, "t2")
            t3 = sb(g, s3, [128, T], F32, "t3")
            t1h, t2h, t3h = nbs(s3, "t1", 2), nbs(s3, "t2", 2), nbs(s3, "t3", 2)
            ebl = sb(g, s3, [128, NH, 32], F32, "ebl")
            eblb = nb(s3, "ebl")
            W = [sb(g, s3, [128, NH, 64], F32, "W") for _ in range(2)]
            Wb = nbs(s3, "W", 2)
            Sbf = [sb(g, s3, [128, NH, 64], BF16, "Sbf") for _ in range(R)]
            Sbfb = nbs(s3, "Sbf", R)
            attm = [sb(g, s3, [128, NH, 128], BF16, "attm") for _ in range(3)]
            attb = nbs(s3, "attm", 3)
            o_tm = [sb(g, s3, [128, NH * 64], F32, "o_tm") for _ in range(3)]
            otb = nbs(s3, "otm", 3)
            osq = sb(g, s3, [128, NH * 64], F32, "osq")
            osqb = nb(s3, "osq")
            og = [sb(g, s3, [128, NH * 64], BF16, "og") for _ in range(2)]
            ogb = nbs(s3, "og", 2)
            sm = [sb(g, s3, [128, 4], F32, "hsm") for _ in range(2)]
            smb = nbs(s3, "hsm", 2)
            wq = [sb(g, s3, [128, DC, 128], BF16, "wq") for _ in range(2)]
            wqb = nbs(s3, "wq", 2)
            wf = [sb(g, s3, [128, DC, 128], BF16, "wf") for _ in range(2)]
            wfb = nbs(s3, "wf", 2)

            def load_head(hd):
                DMA(g, "pool", wf[hd % 2][:], win_cols(g, l, C_HF + hd * 128, C_HF + hd * 128 + 128), (), [wfb[hd % 2]])
                DMA(g, "pool", wq[hd % 2][:], win_cols(g, l, C_HQ + hd * 128, C_HQ + hd * 128 + 128), (), [wqb[hd % 2]])

            load_head(0)
            for hd in range(NH):
                if hd + 1 < NH:
                    load_head(hd + 1)
                w_f, w_fb, w_q, w_qb = wf[hd % 2], wfb[hd % 2], wq[hd % 2], wqb[hd % 2]
                HS = [slice(0, T // 2), slice(T // 2, T)]
                for tc in range(4):
                    ps, pb = psf(g, "proj", [0, 1, 2, 3, 4, 5])
                    mm_fm(g, ps, 128, 512, w_f, w_fb, 0, hT, hTb, tc * 512, pb)
                    ACT(g, t1[:, tc * 512:(tc + 1) * 512], ps[:, :], AF.Sigmoid, [pb], [t1h[tc // 2]])
                for hf in range(2):
                    TS(g, "dve", t1[:, HS[hf]], t1[:, HS[hf]], g.oml[:, l, hd:hd + 1], g.lbv[:, l, hd:hd + 1], ALU.mult, ALU.add,
                       [t1h[hf], g.cb], [t1h[hf]])
                for hf in range(2):
                    ACT(g, t2[:, HS[hf]], t1[:, HS[hf]], AF.Copy, [t1h[hf]], [t2h[hf]], scale=-1.0, bias=1.0)
                for hf in range(2):
                    TS(g, "dve", t1[:, HS[hf]], t1[:, HS[hf]], F_MIN, None, ALU.max, None, [t1h[hf]], [t1h[hf]])
                for hf in range(2):
                    ACT(g, t1[:, HS[hf]], t1[:, HS[hf]], AF.Ln, [t1h[hf]], [t1h[hf]])
                for hf in range(2):
                    P.op("dve", lambda e, hf=hf: e.tensor_tensor_scan(out=t3[:, HS[hf]], data0=g.resetm[:, HS[hf]], data1=t1[:, HS[hf]],
                                                                     initial=0.0, op0=ALU.mult, op1=ALU.add),
                         [t1h[hf], g.cb], [t3h[hf]], T)
                for hf in range(2):
                    TS(g, "dve", t3[:, HS[hf]], t3[:, HS[hf]], -80.0, None, ALU.max, None, [t3h[hf]], [t3h[hf]])
                for hf in range(2):
                    ACT(g, t1[:, HS[hf]], t3[:, HS[hf]], AF.Exp, [t3h[hf]], [t1h[hf]])
                for hf in range(2):
                    CP(g, "pool", ebl[:, hd, hf * 16:(hf + 1) * 16].unsqueeze(2),
                       t1[:, HS[hf]].rearrange("p (c j) -> p c j", j=64)[:, :, 63:64], [t1h[hf]], [eblb])
                for hf in range(2):
                    ACT(g, t3[:, HS[hf]], t3[:, HS[hf]], AF.Exp, [t3h[hf]], [t3h[hf]], scale=-1.0)
                for hf in range(2):
                    TT(g, "dve", ktT[:, hd, HS[hf]], t2[:, HS[hf]], t3[:, HS[hf]], ALU.mult, [t2h[hf], t3h[hf]], [ktb[hd]])
                for tc in range(4):
                    ps, pb = psf(g, "proj", [0, 1, 2, 3, 4, 5])
                    mm_fm(g, ps, 128, 512, w_q, w_qb, 0, hT, hTb, tc * 512, pb)
                    ACT(g, t2[:, tc * 512:(tc + 1) * 512], ps[:, :], AF.Sigmoid, [pb], [t2h[tc // 2]])
                    TT(g, "dve", t2[:, tc * 512:(tc + 1) * 512], ps[:, :], t2[:, tc * 512:(tc + 1) * 512], ALU.mult, [pb, t2h[tc // 2]],
                       [t2h[tc // 2]])
                for hf in range(2):
                    TT(g, "dve", qtT[:, hd, HS[hf]], t2[:, HS[hf]], t1[:, HS[hf]], ALU.mult, [t2h[hf], t1h[hf]], [qtb[hd]])
                for k0 in (0, 8):
                    pt, ptb = psb(g)
                    MM(g, [trf(pt[:, m * 128:(m + 1) * 128], ktT[:, hd, (k0 + m) * 128:(k0 + m + 1) * 128], g.ident[:])
                           for m in range(8)], [ktb[hd], g.cb], [ptb])
                    CP(g, "act" if k0 == 0 else "dve", kt_tm[:, k0:k0 + 8, hd * 128:(hd + 1) * 128],
                       pt[:, :].rearrange("p (m j) -> p m j", j=128), [ptb], [kttb[hd]])

            NCH = 32
            xps = {}

            def emit_X(c):
                tb, pr = c // 2, (c % 2) * 64
                psX, pXb = psf(g, "hgX", [0, 1, 2])
                xps[c] = (psX, pXb)
                MM(g, [mmf(psX[:, hh * 64:(hh + 1) * 64], kt_tm[pr:pr + 64, tb, hh * 128:(hh + 1) * 128],
                           hi_tm[pr:pr + 64, tb, hh * 64:(hh + 1) * 64], True, True) for hh in range(NH)], kttb + [hib[tb]], [pXb])

            def emit_A(tb):
                psA, pAb = psf(g, "hgA", [3])
                MM(g, [mmf(psA[:, hh * 128:(hh + 1) * 128], ktT[:, hh, tb * 128:(tb + 1) * 128],
                           qtT[:, hh, tb * 128:(tb + 1) * 128], True, True) for hh in range(NH)], ktb + qtb, [pAb])
                TT(g, "dve", attm[tb % 3][:], psA[:, :].rearrange("p (h t) -> p h t", t=128),
                   g.maskbd[:].unsqueeze(1).to_broadcast([128, NH, 128]), ALU.mult, [pAb, g.cb], [attb[tb % 3]])

            emit_X(0)
            emit_X(1)
            emit_A(0)
            CP(g, "dve", W[0][:], xps[0][0][:, 0:NH * 64].rearrange("p (h v) -> p h v", v=64), [xps[0][1]], [Wb[0]])
            MS(g, "pool", Sbf[0][:], 0.0, [Sbfb[0]])
            for c in range(NCH):
                tb, half = c // 2, c % 2
                pr = half * 64
                if c + 2 < NCH:
                    emit_X(c + 2)
                if half == 0 and tb + 1 < NB:
                    emit_A(tb + 1)
                if c + 1 < NCH:
                    eb_ = ebl[:, :, c:c + 1].to_broadcast([128, NH, 64])
                    TT(g, "pool", Sbf[(c + 1) % R][:], W[c % 2][:], eb_, ALU.mult, [Wb[c % 2], eblb], [Sbfb[(c + 1) % R]])
                    psX, pXb = xps.pop(c + 1)
                    for hh in range(NH):
                        STT(g, W[(c + 1) % 2][:, hh, :], W[c % 2][:, hh, :], ebl[:, hh, c:c + 1], psX[:, hh * 64:(hh + 1) * 64],
                            ALU.mult, ALU.add, [Wb[c % 2], eblb, pXb], [Wb[(c + 1) % 2]])
                am, amb = attm[tb % 3], attb[tb % 3]
                ot, otbuf = o_tm[tb % 3], otb[tb % 3]
                psO, pOb = psf(g, "hgO", [4, 5])
                fns = []
                for hh in range(NH):
                    fns.append(mmf(psO[0:64, hh * 64:(hh + 1) * 64], am[pr:pr + 64, hh, pr:pr + 64],
                                   hi_tm[pr:pr + 64, tb, hh * 64:(hh + 1) * 64], True, False))
                    fns.append(mmf(psO[0:64, hh * 64:(hh + 1) * 64], qtT[:, hh, c * 64:(c + 1) * 64], Sbf[c % R][:, hh, :], False, True))
                MM(g, fns, [amb, hib[tb], Sbfb[c % R]] + qtb, [pOb])
                CP(g, "act", ot[pr:pr + 64, :], psO[0:64, 0:NH * 64], [pOb], [otbuf])
                if half == 1:
                    sm_, sm_b = sm[tb % 2], smb[tb % 2]
                    og_, og_b = og[tb % 2], ogb[tb % 2]
                    TT(g, "pool", osq[:], ot[:], ot[:], ALU.mult, [otbuf], [osqb])
                    ss = sm_[:, 0:NH]
                    P.op("dve", lambda e, ss=ss: e.tensor_reduce(out=ss, in_=osq[:].rearrange("p (h d) -> p h d", d=64), axis=AX.X,
                                                                 op=ALU.add), [osqb], [sm_b], NH * 64)
                    ACT(g, ss, ss, AF.Ln, [sm_b], [sm_b], scale=1.0 / 64, bias=EPS)
                    ACT(g, ss, ss, AF.Exp, [sm_b], [sm_b], scale=-0.5)
                    o3 = ot[:].rearrange("p (h d) -> p h d", d=64)
                    TT(g, "dve", o3, o3, ss.unsqueeze(2).to_broadcast([128, NH, 64]), ALU.mult, [otbuf, sm_b], [otbuf])
                    TT(g, "dve", o3, o3, onb[:].unsqueeze(1).to_broadcast([128, NH, 64]), ALU.mult, [otbuf, onbb], [otbuf])
                    TT(g, "dve", og_[:], ot[:], hgs[:, tb, :], ALU.mult, [otbuf, hgb[tb]], [og_b])
                    pt, ptb = psb(g)
                    MM(g, [trf(pt[:, m * 128:(m + 1) * 128], og_[:, m * 128:(m + 1) * 128], g.ident[:]) for m in range(2)],
                       [og_b, g.cb], [ptb])
                    CP(g, "act", ohT[:, 0:2, tb * 128:(tb + 1) * 128], pt[:, 0:256].rearrange("p (m j) -> p m j", j=128), [ptb],
                       [ohb[0][tb // 4], ohb[1][tb // 4]])


def phase_mix(g, l, hT, hTb, osbT, osbb, odT, odb, ohT, ohb, src_ap, src_bufs):
    P = g.P
    with scope(g) as st:
        mixT = sb(g, st, [128, DC, T], BF16, "mixT")
        mxb = nbs(st, "mix", DC, 4)
        wg = [sb(g, st, [128, DC, 3, 256], BF16, "wg") for _ in range(2)]
        wgb = nbs(st, "wg", 2, 3)
        wy = [sb(g, st, [128, 8, 256], BF16, "wy") for _ in range(2)]
        wyb = nbs(st, "wy", 2, 3)
        sg = [sb(g, st, [128, 512], F32, "sg") for _ in range(2)]
        sgb = nbs(st, "sg", 2)
        acc = [sb(g, st, [128, 512], F32, "acc") for _ in range(2)]
        accb = nbs(st, "acc", 2)
        tm = [sb(g, st, [128, 512], F32, "tm") for _ in range(2)]
        tmb = nbs(st, "tm", 2)
        k = 0

        def load_pair(dp):
            w_, w_b = wg[dp % 2], wgb[dp % 2]
            y_, y_b = wy[dp % 2], wyb[dp % 2]
            cs_ = slice(dp * 256, (dp + 1) * 256)
            for gi in range(3):
                c0 = C_G + gi * 1024 + dp * 256
                DMA(g, "pool", w_[:, :, gi, :], win_cols(g, l, c0, c0 + 256), (), [w_b[gi]])
            DMA(g, "pool", y_[:, 0:3, :], g.w_sb[l].rearrange("(c p) n -> p c n", p=128)[:, :, cs_], (), [y_b[0]])
            DMA(g, "pool", y_[:, 3:6, :], g.w_dsa[l].rearrange("(c p) n -> p c n", p=128)[:, :, cs_], (), [y_b[1]])
            DMA(g, "pool", y_[:, 6:8, :], g.w_hg[l].rearrange("(c p) n -> p c n", p=128)[:, :, cs_], (), [y_b[2]])

        load_pair(0)
        wo = sb(g, st, [128, DC, D], BF16, "wo")
        wob = nbs(st, "wo", 2)
        for dc in range(DC):
            dp, do = dc // 2, (dc % 2) * 128
            w_, w_b = wg[dp % 2], wgb[dp % 2]
            y_, y_b = wy[dp % 2], wyb[dp % 2]
            if dc % 2 == 0:
                if dp + 1 < DC // 2:
                    load_pair(dp + 1)
                else:
                    for nh in range(2):
                        DMA(g, "pool", wo[:, :, nh * 512:(nh + 1) * 512],
                            g.w_out[l].rearrange("(c p) n -> p c n", p=128)[:, :, nh * 512:(nh + 1) * 512], (), [wob[nh]])
                    prefetch(g, ("wu", l, 0), g.w_up[l].rearrange("(c p) n -> p c n", p=128)[:, :, 0:512], 512)
            for tc in range(4):
                a_, a_b = acc[k % 2], accb[k % 2]
                for gi, (oT, obufs, nch, c0) in enumerate(((osbT, osbb, 3, 0), (odT, odb, 3, 3), (ohT, ohb, 2, 6))):
                    psG, pGb = psf(g, "mxG", [0, 1, 2])
                    MM(g, [mmf(psG[:, :], w_[:, c, gi, do:do + 128], hT[:, c, tc * 512:(tc + 1) * 512], c == 0, c == DC - 1) for c in range(DC)],
                       [w_b[gi]] + hTb[tc * 4:tc * 4 + 4], [pGb])
                    s_, s_b = sg[(k * 3 + gi) % 2], sgb[(k * 3 + gi) % 2]
                    ACT(g, s_[:], psG[:, :], AF.Sigmoid, [pGb], [s_b])
                    psY, pYb = psf(g, "mxY", [3, 4, 5])
                    MM(g, [mmf(psY[:, :], y_[:, c0 + c, do:do + 128], oT[:, c, tc * 512:(tc + 1) * 512], c == 0, c == nch - 1) for c in range(nch)],
                       [y_b[gi]] + [obufs[c][tc] for c in range(nch)], [pYb])
                    if gi == 0:
                        TT(g, "dve", a_[:], psY[:, :], s_[:], ALU.mult, [pYb, s_b], [a_b])
                    else:
                        t_, t_b = tm[gi % 2], tmb[gi % 2]
                        TT(g, "dve", t_[:], psY[:, :], s_[:], ALU.mult, [pYb, s_b], [t_b])
                        if gi == 1:
                            TT(g, "pool", a_[:], a_[:], t_[:], ALU.add, [a_b, t_b], [a_b])
                        else:
                            TT(g, "pool", mixT[:, dc, tc * 512:(tc + 1) * 512], a_[:], t_[:], ALU.add, [a_b, t_b], [mxb[dc][tc]])
                k += 1
        xbs = [sb(g, st, [128, D], F32, "xb") for _ in range(4)]
        xbb = nbs(st, "xb", 4)
        nctx = norm_setup(g, st, g.norm_mlp[l:l + 1, :])
        for tb in range(NB):
            xb, xbuf = xbs[tb % 4], xbb[tb % 4]
            DMA(g, "sp", xb[:], src_ap[tb * 128:(tb + 1) * 128, :], [src_bufs[tb]], [xbuf])
            for nh in range(2):
                ps, pb = psf(g, "mxO", [0, 1, 2, 3, 4, 5])
                MM(g, [mmf(ps[:, :], mixT[:, c, tb * 128:(tb + 1) * 128], wo[:, c, nh * 512:(nh + 1) * 512], c == 0, c == DC - 1)
                       for c in range(DC)], [wob[nh]] + [mxb[c][tb // 4] for c in range(DC)], [pb])
                xs = xb[:, nh * 512:(nh + 1) * 512]
                TT(g, "dve", xs, ps[:, :], xs, ALU.add, [pb, xbuf], [xbuf])
            DMA(g, "sp", g.xres_d[tb * 128:(tb + 1) * 128, :], xb[:], [xbuf], [g.xres_b[tb]])
            norm_block(g, nctx, xb, xbuf, tb, hT, hTb)


def phase_ffn(g, l, hT, hTb, dst_ap, dst_bufs):
    for half in range(2):
        last = half == 1
        with scope(g) as st:
            uT = sb(g, st, [128, 16, T], BF16, "uT")
            ub = nbs(st, "uT", 16, 4)
            wd = sb(g, st, [128, 16, D], BF16, "wd")
            wdb = nbs(st, "wd", 2)
            wdv = g.w_down[l].rearrange("(f p) n -> p f n", p=128)
            wu = [sb(g, st, [128, DC, 512], BF16, "wu") for _ in range(2)]
            wub = nbs(st, "wu", 2)
            rt = [sb(g, st, [128, 512], BF16, "rt") for _ in range(2)]
            rtb = nbs(st, "rt", 2)
            k = 0

            def load_wu(g4):
                c0 = half * 2048 + g4 * 512
                DMA(g, "pool", wu[g4 % 2][:], g.w_up[l].rearrange("(c p) n -> p c n", p=128)[:, :, c0:c0 + 512], (), [wub[g4 % 2]])

            pre0 = take_pre(g, ("wu", l, half))
            if not pre0:
                load_wu(0)
            for g4 in range(4):
                w_, w_b = wu[g4 % 2], wub[g4 % 2]
                if g4 == 0 and pre0:
                    w_, w_b = g.pre, g.preb
                if g4 + 1 < 4:
                    load_wu(g4 + 1)
                if g4 == 1:
                    for nh in range(2):
                        DMA(g, "pool", wd[:, :, nh * 512:(nh + 1) * 512], wdv[:, half * 16:(half + 1) * 16, nh * 512:(nh + 1) * 512], (),
                            [wdb[nh]])
                for fcl in range(4):
                    fc = g4 * 4 + fcl
                    for tc in range(4):
                        ps, pb = psf(g, "proj", [0, 1, 2, 3, 4, 5])
                        mm_fm(g, ps, 128, 512, w_, w_b, fcl * 128, hT, hTb, tc * 512, pb)
                        r_, r_b = rt[k % 2], rtb[k % 2]
                        k += 1
                        ACT(g, r_[:], ps[:, :], AF.Relu, [pb], [r_b])
                        TT(g, "pool", uT[:, fc, tc * 512:(tc + 1) * 512], r_[:], r_[:], ALU.mult, [r_b], [ub[fc][tc]])
            if half == 0:
                prefetch(g, ("wu", l, 1), g.w_up[l].rearrange("(c p) n -> p c n", p=128)[:, :, 2048:2560], 512)
            elif l + 1 < DEPTH:
                prefetch(g, ("sq", l + 1), win_cols(g, l + 1, C_SQ, C_SQ + 384), 384)
            xbs = [sb(g, st, [128, D], F32, "xb") for _ in range(4)]
            xbb = nbs(st, "xb", 4)
            nctx = norm_setup(g, st, g.norm_mix[l + 1:l + 2, :]) if (last and l + 1 < DEPTH) else None
            for tb in range(NB):
                xb, xbuf = xbs[tb % 4], xbb[tb % 4]
                DMA(g, "sp", xb[:], g.xres_d[tb * 128:(tb + 1) * 128, :], [g.xres_b[tb]], [xbuf])
                for nh in range(2):
                    ps, pb = psf(g, "proj", [0, 1, 2, 3, 4, 5])
                    MM(g, [mmf(ps[:, :], uT[:, fc, tb * 128:(tb + 1) * 128], wd[:, fc, nh * 512:(nh + 1) * 512], fc == 0, fc == 15)
                           for fc in range(16)], [wdb[nh]] + [ub[fc][tb // 4] for fc in range(16)], [pb])
                    xs = xb[:, nh * 512:(nh + 1) * 512]
                    TT(g, "dve", xs, ps[:, :], xs, ALU.add, [pb, xbuf], [xbuf])
                if last:
                    DMA(g, "sp", dst_ap[tb * 128:(tb + 1) * 128, :], xb[:], [xbuf], [dst_bufs[tb]])
                    if nctx is not None:
                        norm_block(g, nctx, xb, xbuf, tb, hT, hTb)
                else:
                    DMA(g, "sp", g.xres_d[tb * 128:(tb + 1) * 128, :], xb[:], [xbuf], [g.xres_b[tb]])


def dump_t(g, name, t, ncol):
    if g.dump == name:
        g.P.barrier()
        DMA(g, "sp", g.dbg_d[:, 0:ncol], t, [], [g.dbgb])
        g.P.wait_bufs("sp", [g.dbgb])
        g.P.barrier()


def build_layer(g, l):
    if l > 0 and g.stage < 7:
        return
    src_ap, src_bufs = (g.x_d, g.xin_b) if l == 0 else (g.xres_d, g.xres_b)
    with scope(g) as ls:
        hT, hTb = g.hT, g.hTb
        if l == 0:
            with scope(g) as st:
                norm_T(g, st, src_ap, src_bufs, g.norm_mix[l:l + 1, :], hT, hTb)
        if l == 0:
            dump_t(g, "hT", hT[:].rearrange("p c t -> p (c t)"), 8 * T)
        if g.stage < 2:
            return
        with scope(g) as ms:
            osbT = sb(g, ms, [128, 3, T], BF16, "osbT")
            odT = sb(g, ms, [128, 3, T], BF16, "odT")
            ohT = sb(g, ms, [128, 2, T], BF16, "ohT")
            osbb = nbs(ms, "osb", 3, 4)
            odb = nbs(ms, "od", 3, 4)
            ohb = nbs(ms, "oh", 2, 4)
            phase_sb(g, l, hT, hTb, osbT, osbb)
            if l == 0:
                dump_t(g, "osbT", osbT[:].rearrange("p c t -> p (c t)"), 3 * T)
            if g.stage < 3:
                return
            phase_dsa(g, l, hT, hTb, odT, odb)
            if l == 0:
                dump_t(g, "odT", odT[:].rearrange("p c t -> p (c t)"), 3 * T)
            if g.stage < 4:
                return
            phase_hgrn(g, l, hT, hTb, ohT, ohb)
            if l == 0:
                dump_t(g, "ohT", ohT[:].rearrange("p c t -> p (c t)"), 2 * T)
            if g.stage < 5:
                return
            phase_mix(g, l, hT, hTb, osbT, osbb, odT, odb, ohT, ohb, src_ap, src_bufs)
        if g.stage < 6:
            return
        if l == DEPTH - 1:
            phase_ffn(g, l, hT, hTb, g.out_d, g.out_b)
        else:
            phase_ffn(g, l, hT, hTb, g.xres_d, g.xres_b)


_NC_CACHE = {}


def rope_tables():
    half = 8
    inv = 500000.0 ** (-(np.arange(half, dtype=np.float32) * 2.0) / 16.0)
    ang = np.arange(T, dtype=np.float32)[:, None] * inv[None, :].astype(np.float32)
    return np.cos(ang).astype(np.float32), np.sin(ang).astype(np.float32)


def kernel(x, norm_mix, w_in, qn_dsa, kn_dsa, hgrn_lb, hgrn_onorm, w_br_sb, w_br_dsa, w_br_hgrn, w_out, norm_mlp, w_up, w_down):
    if "nc" not in _NC_CACHE:
        _NC_CACHE["nc"] = build_two_pass()
    nc = _NC_CACHE["nc"]
    f = lambda a: np.ascontiguousarray(np.asarray(a, dtype=np.float32))
    cs, sn = rope_tables()
    shared = dict(norm_mix=f(norm_mix), w_in=f(w_in), qn_dsa=f(qn_dsa), kn_dsa=f(kn_dsa), hgrn_lb=f(hgrn_lb),
                  hgrn_onorm=f(hgrn_onorm), w_br_sb=f(w_br_sb), w_br_dsa=f(w_br_dsa), w_br_hgrn=f(w_br_hgrn),
                  w_out=f(w_out), norm_mlp=f(norm_mlp), w_up=f(w_up), w_down=f(w_down), rope_cos=cs, rope_sin=sn)
    xs = f(x)
    in_maps = [dict(shared, x=xs[b]) for b in range(8)]
    res = run_bass_kernel_spmd(nc, in_maps, core_ids=list(range(8)))
    return np.stack([np.asarray(r["out"], dtype=np.float32) for r in res.results], axis=0)
```

```python
import math
import numpy as np
from contextlib import ExitStack, contextmanager
import concourse.bass as bass
import concourse.mybir as mybir
from concourse.bass_utils import run_bass_kernel_spmd

F32 = mybir.dt.float32
BF16 = mybir.dt.bfloat16
AF = mybir.ActivationFunctionType
ALU = mybir.AluOpType
AX = mybir.AxisListType

T = 2048
D = 1024
NB = 16
DC = 8
DIN = 6856
DFF = 4096
DEPTH = 2
EPS = 1e-6
F_MIN = 1e-12
IDX_SCALE = (64 * 8) ** -0.5
C_SQ, C_SK, C_SV = 0, 384, 768
C_DQ, C_DK, C_DV = 1152, 1536, 1600
C_IQ, C_IK, C_IW = 1664, 2176, 2240
C_HQ, C_HF, C_HI, C_HG = 2248, 2760, 3272, 3528
C_G = 3784
N_BISECT = 14


class Buf:
    __slots__ = ("name", "w", "r", "dsem", "excl")

    def __init__(self, name, excl=False):
        self.name = name
        self.w = None
        self.r = {}
        self.dsem = None
        self.excl = excl


class Prog:
    ENG = ("pe", "act", "dve", "pool", "sp")
    CLEAR_NS = 330.0
    FILL_NS = {"dve": 66.0, "act": 190.0, "pool": 125.0}
    EST = {"dve": (60.0, 0.26), "act": (185.0, 0.83), "pool": (120.0, 0.8), "pe": (0.0, 0.0), "sp": (0.0, 0.0)}

    def __init__(self, nc, stack, needed=None):
        self.needed = needed
        self.used = set()
        self.remap = {}
        self.sig = {}
        self.fill = {}
        self.nc = nc
        self.stack = stack
        self.eng = {"pe": nc.tensor, "act": nc.scalar, "dve": nc.vector, "pool": nc.gpsimd, "sp": nc.sync}
        self.cnt = {e: 0 for e in self.ENG}
        self.known = {e: {} for e in self.ENG}
        self.sems = {}
        self.semval = {}
        for e in ("pe", "act", "dve", "pool"):
            self.sems["E_" + e] = stack.enter_context(nc.semaphore("sem_" + e))
            self.semval["E_" + e] = 0
        self.ndsem = 0
        self.free_dsems = []
        self.nwaits = 0
        self.tcum = {e: 0.0 for e in self.ENG}
        self.tend = {e: {} for e in self.ENG}

    def _dsem(self, buf):
        if buf.dsem is None:
            if self.free_dsems:
                key = self.free_dsems.pop()
            else:
                key = "D%d" % self.ndsem
                self.ndsem += 1
                self.sems[key] = self.stack.enter_context(self.nc.semaphore("dsem%d" % (self.ndsem - 1)))
                self.semval[key] = 0
            buf.dsem = key
        return buf.dsem

    def release(self, bufs):
        for b in bufs:
            if b.dsem is not None:
                self.free_dsems.append(b.dsem)
                b.dsem = None

    def _waits(self, eng, deps):
        need = {}
        own = "E_" + eng
        for (k, v) in deps:
            if eng == "pe" and k == "E_pe":
                continue
            if k == own and eng in ("act", "dve", "pool"):
                te = self.tend[eng].get(v)
                if te is not None and eng in self.fill:
                    gap = self.CLEAR_NS - (self.tcum[eng] - te)
                    if gap > 0:
                        n = int(math.ceil(gap / self.FILL_NS[eng]))
                        for _ in range(n):
                            self.fill[eng](self.eng[eng])
                        self.tcum[eng] += n * self.FILL_NS[eng]
                        self.nfill = getattr(self, "nfill", 0) + n
                continue
            if v > need.get(k, 0):
                need[k] = v
        out = []
        kn = self.known[eng]
        for k, v in need.items():
            if kn.get(k, 0) < v:
                kn[k] = v
                out.append((k, v))
        return out

    @staticmethod
    def _deps(reads, writes):
        deps = []
        for b in reads:
            if b.w is not None:
                deps.append(b.w)
            if b.excl:
                deps.extend(b.r.items())
        for b in writes:
            if b.w is not None:
                deps.append(b.w)
            deps.extend(b.r.items())
        return deps

    def _emit_waits(self, eng, waits):
        e = self.eng[eng]
        for (k, v) in waits:
            if k.startswith("E_"):
                self.used.add((k, v))
                if self.needed is not None:
                    v = self.remap[(k, v)]
            e.wait_ge(self.sems[k], v)
            self.nwaits += 1

    def _mark(self, ev, reads, writes):
        k, v = ev
        for b in reads:
            if b.r.get(k, 0) < v:
                b.r[k] = v
        for b in writes:
            b.w = ev
            b.r = {}

    def op(self, eng, fn, reads=(), writes=(), n=0):
        self.group(eng, [fn], reads, writes, n)

    def group(self, eng, fns, reads=(), writes=(), n=0):
        self._emit_waits(eng, self._waits(eng, self._deps(reads, writes)))
        e = self.eng[eng]
        for fn in fns[:-1]:
            fn(e)
        self.cnt[eng] += 1
        ov, pe_ = self.EST[eng]
        self.tcum[eng] += ov + pe_ * n
        td = self.tend[eng]
        td[self.cnt[eng]] = self.tcum[eng]
        if len(td) > 64:
            for k_ in sorted(td)[:32]:
                del td[k_]
        key = "E_" + eng
        self.semval[key] = self.cnt[eng]
        if self.needed is None or (key, self.cnt[eng]) in self.needed:
            self.sig[key] = self.sig.get(key, 0) + 1
            self.remap[(key, self.cnt[eng])] = self.sig[key]
            fns[-1](e).then_inc(self.sems[key], 1)
        else:
            fns[-1](e)
        self._mark((key, self.cnt[eng]), reads, writes)

    def dma(self, eng, fn, reads=(), writes=()):
        assert len(writes) == 1
        wb = writes[0]
        deps = self._deps(reads, writes)
        if eng == "pool" and getattr(self, "last_swdge", None) is not None:
            deps.append(self.last_swdge)
        self._emit_waits(eng, self._waits(eng, deps))
        key = self._dsem(wb)
        self.semval[key] += 16
        fn(self.eng[eng]).then_inc(self.sems[key], 16)
        if eng == "pool":
            self.last_swdge = (key, self.semval[key])
        self._mark((key, self.semval[key]), reads, writes)

    def barrier(self):
        deps = [(k, v) for k, v in self.semval.items() if v > 0]
        for eng in self.ENG:
            self._emit_waits(eng, self._waits(eng, deps))

    def wait_bufs(self, eng, bufs):
        deps = []
        for b in bufs:
            if b.w is not None:
                deps.append(b.w)
            deps.extend(b.r.items())
        self._emit_waits(eng, self._waits(eng, deps))


class G:
    pass


def bufs(prefix, *dims):
    if len(dims) == 1:
        return [Buf("%s%d" % (prefix, i)) for i in range(dims[0])]
    return [bufs("%s%d_" % (prefix, i), *dims[1:]) for i in range(dims[0])]


def build_program(stage=99, dump=None, needed=None):
    nc = bass.Bass("TRN2", target_bir_lowering=False)
    g = G()
    g.nc = nc
    g.stage = stage
    g.dump = dump
    import os
    g.ntl = int(os.environ.get("NTL", "9"))
    g.dbg_d = None
    if dump is not None:
        g.dbg_d = nc.dram_tensor("dbg", [128, 8 * T], BF16, kind="ExternalOutput").ap()
        g.dbgb = Buf("dbg")
    dt = lambda name, shape, kind, d=F32: nc.dram_tensor(name, shape, d, kind=kind).ap()
    g.x_d = dt("x", [T, D], "ExternalInput")
    g.norm_mix = dt("norm_mix", [DEPTH, D], "ExternalInput")
    g.w_in = dt("w_in", [DEPTH, D, DIN], "ExternalInput")
    g.qn = dt("qn_dsa", [DEPTH, 64], "ExternalInput")
    g.kn = dt("kn_dsa", [DEPTH, 64], "ExternalInput")
    g.lb_d = dt("hgrn_lb", [DEPTH, 512], "ExternalInput")
    g.onorm = dt("hgrn_onorm", [DEPTH, 64], "ExternalInput")
    g.w_sb = dt("w_br_sb", [DEPTH, 384, D], "ExternalInput")
    g.w_dsa = dt("w_br_dsa", [DEPTH, 384, D], "ExternalInput")
    g.w_hg = dt("w_br_hgrn", [DEPTH, 256, D], "ExternalInput")
    g.w_out = dt("w_out", [DEPTH, D, D], "ExternalInput")
    g.norm_mlp = dt("norm_mlp", [DEPTH, D], "ExternalInput")
    g.w_up = dt("w_up", [DEPTH, D, DFF], "ExternalInput")
    g.w_down = dt("w_down", [DEPTH, DFF, D], "ExternalInput")
    g.cs_d = dt("rope_cos", [T, 8], "ExternalInput")
    g.sn_d = dt("rope_sin", [T, 8], "ExternalInput")
    g.out_d = dt("out", [T, D], "ExternalOutput")
    g.xres_d = dt("xres", [T, D], "Internal")
    g.xin_b = bufs("xin", NB)
    g.xres_b = bufs("xres", NB)
    g.out_b = bufs("outb", NB)

    with ExitStack() as gs:
        P = Prog(nc, gs, needed)
        g.P = P
        fa = gs.enter_context(nc.sbuf_tensor("fill_a", [128, 2], F32))
        fd = gs.enter_context(nc.sbuf_tensor("fill_d", [128, 2], F32))
        nc.vector.memset(fd[:], 0.0)
        nc.vector.memset(fa[:], 0.0)
        P.fill["dve"] = lambda e: e.memset(fd[:, 0:1], 0.0)
        P.fill["act"] = lambda e: e.activation(out=fa[:, 0:1], in_=fa[:, 1:2], func=AF.Copy)
        fp = gs.enter_context(nc.sbuf_tensor("fill_p", [128, 2], F32))
        nc.gpsimd.memset(fp[:], 0.0)
        P.fill["pool"] = lambda e: e.memset(fp[:, 0:1], 0.0)
        g.uid = 0
        g.psF = [gs.enter_context(nc.psum_tensor("psF%d" % i, [128, 512], F32)) for i in range(6)]
        g.psFb = [Buf("psF%d" % i, excl=True) for i in range(6)]
        g.psB = [gs.enter_context(nc.psum_tensor("psB%d" % i, [128, 1024], BF16)) for i in range(2)]
        g.psBb = [Buf("psB%d" % i, excl=True) for i in range(2)]
        g.rotc = {}
        build_consts(g, gs)
        g.hT = gs.enter_context(nc.sbuf_tensor("hT_glob", [128, DC, T], BF16))
        g.hTb = bufs("hT", NB)
        g.pre = gs.enter_context(nc.sbuf_tensor("pre_w", [128, DC, 512], BF16))
        g.preb = Buf("pre_w")
        prefetch(g, ("sq", 0), win_cols(g, 0, C_SQ, C_SQ + 384), 384)
        for l in range(DEPTH):
            if g.stage >= 1:
                build_layer(g, l)
        if g.dbg_d is not None:
            P.wait_bufs("sp", [g.dbgb])
        P.wait_bufs("sp", g.out_b)
        P.barrier()
        g.used = P.used
        print("ops", P.cnt, "signals", P.sig, "waits", P.nwaits, "fillers", getattr(P, "nfill", 0), "dsems", P.ndsem, flush=True)
    return nc, P.used


def build_two_pass(stage=99, dump=None):
    _, used = build_program(stage, dump, None)
    nc, _ = build_program(stage, dump, used)
    return nc


def pipeline(stages, ntiles):
    ns = len(stages)
    for t in range(ntiles + ns - 1):
        for k, f in enumerate(stages):
            i = t - k
            if 0 <= i < ntiles:
                f(i)


def prefetch(g, tag, src, ncols):
    DMA(g, "pool", g.pre[:, :, 0:ncols], src, (), [g.preb])
    g.pre_tag = tag


def take_pre(g, tag):
    if getattr(g, "pre_tag", None) == tag:
        g.pre_tag = None
        return True
    return False


def rot(g, role, items):
    i = g.rotc.get(role, 0)
    g.rotc[role] = i + 1
    return items[i % len(items)]


def psf(g, role, banks):
    b = rot(g, role, banks)
    return g.psF[b], g.psFb[b]


def psb(g, role="pb"):
    b = rot(g, role, [0, 1])
    return g.psB[b], g.psBb[b]


@contextmanager
def scope(g):
    st = ExitStack()
    st.tbufs = []
    try:
        yield st
    finally:
        g.P.barrier()
        g.P.release(st.tbufs)
        st.close()


def sb(g, st, shape, dtype, name=None):
    g.uid += 1
    return st.enter_context(g.nc.sbuf_tensor("%s_%d" % (name or "t", g.uid), shape, dtype))


def nb(st, name):
    b = Buf(name)
    st.tbufs.append(b)
    return b


def nbs(st, prefix, *dims):
    r = bufs(prefix, *dims)

    def flat(x):
        if isinstance(x, Buf):
            st.tbufs.append(x)
        else:
            for y in x:
                flat(y)
    flat(r)
    return r


def _fs(ap):
    try:
        return int(ap.free_size())
    except Exception:
        return 0


def ACT(g, out, in_, func, reads, writes, **kw):
    g.P.op("act", lambda e: e.activation(out=out, in_=in_, func=func, **kw), reads, writes, _fs(out))


def TT(g, eng, out, in0, in1, op, reads, writes):
    g.P.op(eng, lambda e: e.tensor_tensor(out=out, in0=in0, in1=in1, op=op), reads, writes, _fs(out))


def TS(g, eng, out, in0, s1, s2, op0, op1, reads, writes, **kw):
    if op1 is None:
        s2 = 0.0 if isinstance(s1, (int, float)) else g.zc[0:in0.shape[0], 0:1]
        g.P.op(eng, lambda e: e.tensor_scalar(out=out, in0=in0, scalar1=s1, scalar2=s2, op0=op0, op1=ALU.add, **kw), reads, writes, _fs(out))
    else:
        g.P.op(eng, lambda e: e.tensor_scalar(out=out, in0=in0, scalar1=s1, scalar2=s2, op0=op0, op1=op1, **kw), reads, writes, _fs(out))


def STT(g, out, in0, scalar, in1, op0, op1, reads, writes):
    g.P.op("dve", lambda e: e.scalar_tensor_tensor(out=out, in0=in0, scalar=scalar, in1=in1, op0=op0, op1=op1), reads, writes, _fs(out))


def CP(g, eng, out, in_, reads, writes):
    if eng == "act":
        g.P.op("act", lambda e: e.activation(out=out, in_=in_, func=AF.Copy), reads, writes, _fs(out))
    else:
        g.P.op(eng, lambda e: e.tensor_copy(out, in_), reads, writes, _fs(out))


def MS(g, eng, ap, val, writes):
    g.P.op(eng, lambda e: e.memset(ap, val), (), writes, _fs(ap))


def ASEL(g, out, in_, pattern, cmp, fill, base, cm, reads, writes):
    g.P.op("pool", lambda e: e.affine_select(out=out, in_=in_, pattern=pattern, compare_op=cmp, fill=fill, base=base,
                                             channel_multiplier=cm), reads, writes, _fs(out))


def MM(g, outs_fns, reads, writes):
    g.P.group("pe", outs_fns, reads, writes)


def mmf(out, lhsT, rhs, start, stop):
    return lambda e: e.matmul(out, lhsT=lhsT, rhs=rhs, start=start, stop=stop)


def trf(out, in_, ident):
    return lambda e: e.transpose(out, in_, ident)


def DMA(g, eng, out, in_, reads, writes, **kw):
    g.P.dma(eng, lambda e: e.dma_start(out=out, in_=in_, **kw), reads, writes)


def build_consts(g, gs):
    nc = g.nc
    mk = lambda name, shape, d: gs.enter_context(nc.sbuf_tensor(name, shape, d))
    g.ident = mk("ident", [128, 128], BF16)
    g.negtri = mk("negtri", [128, 128], BF16)
    g.negones = mk("negones", [128, 128], BF16)
    g.onesb = mk("onesb", [128, 128], BF16)
    g.maskbd = mk("maskbd", [128, 128], F32)
    g.onesf = mk("onesf", [128, 128], F32)
    g.resetm = mk("resetm", [128, T], BF16)
    g.zc = mk("zc", [128, 1], F32)
    g.negbig = mk("negbig", [128, 1], F32)
    g.cs = mk("cs", [128, NB, 8], F32)
    g.sn = mk("sn", [128, NB, 8], F32)
    g.lbraw = mk("lbraw", [128, 2, 4], F32)
    g.lbv = mk("lbv", [128, 2, 4], F32)
    g.oml = mk("oml", [128, 2, 4], F32)
    g.cb = Buf("consts")
    g.csb = Buf("cs")
    g.snb = Buf("sn")
    g.lbb = Buf("lbraw")
    cb = [g.cb]
    MS(g, "pool", g.onesb[:], 1.0, cb)
    MS(g, "pool", g.negones[:], -1.0, cb)
    MS(g, "pool", g.onesf[:], 1.0, cb)
    MS(g, "pool", g.zc[:], 0.0, cb)
    MS(g, "pool", g.negbig[:], -1e29, cb)
    ASEL(g, g.ident[:], g.onesb[:], [[1, 128]], ALU.is_equal, 0.0, 0, -1, cb, cb)
    ASEL(g, g.negtri[:], g.negones[:], [[-1, 128]], ALU.is_ge, 0.0, 0, 1, cb, cb)
    ASEL(g, g.maskbd[:], g.onesf[:], [[1, 128]], ALU.is_ge, 0.0, 0, -1, cb, cb)
    MS(g, "pool", g.maskbd[0:64, 64:128], 0.0, cb)
    g.ones512 = mk("ones512", [128, 512], BF16)
    g.mlt = mk("mlt", [128, 512], BF16)
    MS(g, "pool", g.ones512[:], 1.0, cb)
    ASEL(g, g.mlt[:], g.ones512[:], [[1, 512]], ALU.is_gt, 0.0, 0, -1, cb, cb)
    g.caus01 = mk("caus01", [128, 128], F32)
    g.negfill = mk("negfill", [128, 128], F32)
    ASEL(g, g.caus01[:], g.onesf[:], [[-1, 128]], ALU.is_ge, 0.0, 0, 1, cb, cb)
    TS(g, "pool", g.negfill[:], g.caus01[:], -1.0, 1e30, ALU.add, ALU.mult, cb, cb)
    MS(g, "pool", g.resetm[:], 1.0, cb)
    MS(g, "pool", g.resetm[:].rearrange("p (c j) -> p c j", j=64)[:, :, 0:1], 0.0, cb)
    DMA(g, "sp", g.cs[:], g.cs_d.rearrange("(b p) i -> p b i", p=128), (), [g.csb])
    DMA(g, "sp", g.sn[:], g.sn_d.rearrange("(b p) i -> p b i", p=128), (), [g.snb])
    DMA(g, "sp", g.lbraw[:], g.lb_d.rearrange("l (h k) -> k l h", k=128), (), [g.lbb], allow_slow_non_contiguous=True)
    MS(g, "dve", g.lbv[:], 0.0, cb)
    TT(g, "dve", g.lbv[:, 1, :], g.lbraw[:, 1, :], g.lbraw[:, 0, :], ALU.subtract, [g.lbb], cb)
    ACT(g, g.lbv[:, 1, :], g.lbv[:, 1, :], AF.Sigmoid, cb, cb)
    TS(g, "dve", g.oml[:], g.lbv[:], -1.0, 1.0, ALU.mult, ALU.add, cb, cb)
    g.P.barrier()


class NormCtx:
    pass


def norm_setup(g, st, gain_d_row):
    c = NormCtx()
    c.gain = sb(g, st, [128, D], F32, "gain")
    c.gb = nb(st, "gain")
    DMA(g, "sp", c.gain[:], gain_d_row.to_broadcast([128, D]), (), [c.gb])
    c.junk = sb(g, st, [128, D], BF16, "junk")
    c.jb = nb(st, "junk")
    c.hbs = [sb(g, st, [128, D], BF16, "hb") for _ in range(2)]
    c.hbb = nbs(st, "hb", 2)
    c.ss = sb(g, st, [128, NB], F32, "ss")
    c.ssb = nbs(st, "ss", NB)
    MS(g, "dve", c.ss[:], 0.0, c.ssb)
    return c


def norm_block(g, c, xb, xbuf, tb, hT, hTb):
    hb, hbuf = c.hbs[tb % 2], c.hbb[tb % 2]
    s1 = c.ss[:, tb:tb + 1]
    ACT(g, c.junk[:], xb[:], AF.Square, [xbuf, c.ssb[tb]], [c.jb, c.ssb[tb]], accum_out=s1)
    ACT(g, s1, s1, AF.Ln, [c.ssb[tb]], [c.ssb[tb]], scale=1.0 / D, bias=EPS)
    ACT(g, s1, s1, AF.Exp, [c.ssb[tb]], [c.ssb[tb]], scale=-0.5)
    STT(g, hb[:], xb[:], s1, c.gain[:], ALU.mult, ALU.mult, [xbuf, c.ssb[tb], c.gb], [hbuf])
    for half in range(2):
        pt, ptb = psb(g)
        MM(g, [trf(pt[:, m * 128:(m + 1) * 128], hb[:, (half * 4 + m) * 128:(half * 4 + m + 1) * 128], g.ident[:])
               for m in range(4)], [hbuf, g.cb], [ptb])
        CP(g, "act" if half == 0 else "dve", hT[:, half * 4:half * 4 + 4, tb * 128:(tb + 1) * 128],
           pt[:, 0:512].rearrange("p (m j) -> p m j", j=128), [ptb], [hTb[tb]])


def norm_T(g, st, src_ap, src_bufs, gain_d_row, hT, hTb):
    c = norm_setup(g, st, gain_d_row)
    xbs = [sb(g, st, [128, D], F32, "xb") for _ in range(4)]
    xbb = nbs(st, "xb", 4)
    for tb in range(NB):
        xb, xbuf = xbs[tb % 4], xbb[tb % 4]
        DMA(g, "sp", xb[:], src_ap[tb * 128:(tb + 1) * 128, :], [src_bufs[tb]], [xbuf])
        norm_block(g, c, xb, xbuf, tb, hT, hTb)


def win_cols(g, l, c0, c1):
    return g.w_in[l].rearrange("(c p) n -> p c n", p=128)[:, :, c0:c1]


def mm_fm(g, ps, M, n, w, wb, col0, hT, hTb, tok0, role_bufs):
    MM(g, [mmf(ps[0:M, 0:n], w[:, c, col0:col0 + M], hT[:, c, tok0:tok0 + n], c == 0, c == DC - 1) for c in range(DC)],
       [wb] + hTb[tok0 // 128:(tok0 + n + 127) // 128], [role_bufs])


def mm_tm(g, ps, N, w, wb, col0, hT, hTb, tb, psbuf):
    MM(g, [mmf(ps[:, 0:N], hT[:, c, tb * 128:(tb + 1) * 128], w[:, c, col0:col0 + N], c == 0, c == DC - 1) for c in range(DC)],
       [wb, hTb[tb]], [psbuf])


def phase_sb(g, l, hT, hTb, osbT, osbb):
    with scope(g) as st:
        ws = []
        for i, c0 in enumerate((C_SQ, C_SK, C_SV)):
            if i == 0 and take_pre(g, ("sq", l)):
                ws.append((g.pre, g.preb))
                continue
            w = sb(g, st, [128, DC, 384], BF16, "wsb")
            wb = nb(st, "wsb%d" % i)
            DMA(g, "pool", w[:], win_cols(g, l, c0, c0 + 384), (), [wb])
            ws.append((w, wb))
        sqT = sb(g, st, [128, 3, T], BF16, "sqT")
        skT = sb(g, st, [128, 3, T], BF16, "skT")
        sqb = nbs(st, "sq", 3, 4)
        skb = nbs(st, "sk", 3, 4)
        k = 0
        for (dst, dstb, (w, wb), scl) in ((sqT, sqb, ws[0], 0.125), (skT, skb, ws[1], 1.0)):
            for hp in range(3):
                for tc in range(4):
                    ps, pb = psf(g, "proj", [0, 1, 2, 3, 4, 5])
                    mm_fm(g, ps, 128, 512, w, wb, hp * 128, hT, hTb, tc * 512, pb)
                    o = dst[:, hp, tc * 512:(tc + 1) * 512]
                    if k % 2 == 0:
                        ACT(g, o, ps[:, :], AF.Copy, [pb], [dstb[hp][tc]], scale=scl)
                    else:
                        TS(g, "dve", o, ps[:, :], scl, None, ALU.mult, None, [pb], [dstb[hp][tc]])
                    k += 1
        svp = [sb(g, st, [128, NB, 384], BF16, "svp") for _ in range(2)]
        svb = nbs(st, "sv", 2, NB)
        for s_ in range(2):
            MS(g, "pool", svp[s_][:].rearrange("p t c -> p (t c)"), 0.0, svb[s_])
        for tb in range(NB):
            ps, pb = psf(g, "proj", [0, 1, 2, 3, 4, 5])
            mm_tm(g, ps, 384, ws[2][0], ws[2][1], 0, hT, hTb, tb, pb)
            src = ps[:, 0:384].rearrange("p (m s d) -> p m s d", s=2, d=64)
            for s_ in range(2):
                dst = svp[s_][:, tb, :].rearrange("p (m s d) -> p m s d", s=2, d=64)
                CP(g, "act" if s_ == 0 else "dve", dst[:, :, s_, :], src[:, :, s_, :], [pb], [svb[s_][tb]])
        prefetch(g, ("wA", l), win_cols(g, l, C_DQ, C_DQ + 512), 512)
        R = 4
        mk2 = lambda shape, dt_, nm: [[sb(g, st, shape, dt_, nm) for _ in range(R)] for _ in range(2)]
        Et, SPt, SPs, At = mk2([128, 512], F32, "Et"), mk2([128, 512], BF16, "SPt"), mk2([128, 512], BF16, "SPs"), mk2([128, 512], BF16, "At")
        Etb, SPb, SPsb, Atb = nbs(st, "Et", 2, R), nbs(st, "SPt", 2, R), nbs(st, "SPs", 2, R), nbs(st, "At", 2, R)
        steps = []
        for m in range(3):
            for qc in range(4):
                for n_, kb in enumerate(range(4 * qc + 3, -1, -1)):
                    steps.append((m, qc, kb, n_))
        stt = {}
        pso = {}

        def info(t):
            m, qc, kb, n_ = steps[t]
            j0 = max(0, kb * 128 - qc * 512)
            return m, qc, kb, n_, j0, kb >= 4 * qc, qc * 512 + j0 - kb * 128, n_ == 0

        def opnds(t):
            m, qc, kb, n_, j0, diag, base, first = info(t)
            kk = [skT[64 * s_:64 * s_ + 64, m, kb * 128:(kb + 1) * 128] for s_ in range(2)]
            qq = [sqT[64 * s_:64 * s_ + 64, m, qc * 512 + j0:(qc + 1) * 512] for s_ in range(2)]
            return kk, qq, skb[m][kb // 4], sqb[m][qc]

        def s1(t):
            m, qc, kb, n_, j0, diag, base, first = info(t)
            kk, qq, rk, rq = opnds(t)
            pz = [psf(g, "sbZ", [0, 1, 2]) for _ in range(2)]
            stt[t] = {"pz": pz}
            for s_ in range(2):
                MM(g, [mmf(pz[s_][0][:, j0:512], kk[s_], qq[s_], True, True)], [rk, rq], [pz[s_][1]])

        def s2(t):
            m, qc, kb, n_, j0, diag, base, first = info(t)
            ib = t % R
            pz = stt[t]["pz"]
            for s_ in range(2):
                ACT(g, Et[s_][ib][:, j0:512], pz[s_][0][:, j0:512], AF.Exp, [pz[s_][1]], [Etb[s_][ib]])

        def s3(t):
            m, qc, kb, n_, j0, diag, base, first = info(t)
            ib = t % R
            for s_ in range(2):
                ACT(g, SPt[s_][ib][:, j0:512], Et[s_][ib][:, j0:512], AF.Ln, [Etb[s_][ib]], [SPb[s_][ib]], bias=1.0)
            if diag:
                for s_ in range(2):
                    S = SPt[s_][ib]
                    assert base == 0
                    TT(g, "pool", S[:, j0:512], S[:, j0:512], g.mlt[:, 0:512 - j0], ALU.mult, [SPb[s_][ib], g.cb], [SPb[s_][ib]])
            if kb > 0:
                for s_ in range(2):
                    S, Sb = SPt[s_][ib], SPb[s_][ib]
                    Sn, Snb = SPs[s_][n_ % R], SPsb[s_][n_ % R]
                    if first:
                        if j0 > 0:
                            MS(g, "pool", Sn[:, 0:j0], 0.0, [Snb])
                        CP(g, "pool", Sn[:, j0:512], S[:, j0:512], [Sb], [Snb])
                    else:
                        So, Sob = SPs[s_][(n_ - 1) % R], SPsb[s_][(n_ - 1) % R]
                        if j0 > 0:
                            CP(g, "pool", Sn[:, 0:j0], So[:, 0:j0], [Sob], [Snb])
                        TT(g, "dve", Sn[:, j0:512], So[:, j0:512], S[:, j0:512], ALU.add, [Sob, Sb], [Snb])

        def s4(t):
            m, qc, kb, n_, j0, diag, base, first = info(t)
            ib = t % R
            kk, qq, rk, rq = opnds(t)
            pc = [psf(g, "sbC", [3, 4]) for _ in range(2)]
            stt[t]["pc"] = pc
            for s_ in range(2):
                S = SPt[s_][ib]
                fns = [mmf(pc[s_][0][:, j0:512], kk[s_], qq[s_], True, False),
                       mmf(pc[s_][0][:, j0:512], g.negtri[:], S[:, j0:512], False, first)]
                rd = [rk, rq, SPb[s_][ib], g.cb]
                if not first:
                    So, Sob = SPs[s_][(n_ - 1) % R], SPsb[s_][(n_ - 1) % R]
                    fns.append(mmf(pc[s_][0][:, j0:512], g.negones[:], So[:, j0:512], False, True))
                    rd.append(Sob)
                MM(g, fns, rd, [pc[s_][1]])

        def s5(t):
            m, qc, kb, n_, j0, diag, base, first = info(t)
            ib = t % R
            pc = stt[t]["pc"]
            for s_ in range(2):
                ACT(g, At[s_][ib][:, j0:512], pc[s_][0][:, j0:512], AF.Exp, [pc[s_][1]], [Atb[s_][ib]])
            for s_ in range(2):
                A, Ab = At[s_][ib], Atb[s_][ib]
                if diag:
                    TT(g, "pool", A[:, j0:512], A[:, j0:512], g.mlt[:, 0:512 - j0], ALU.mult, [Ab, g.cb], [Ab])
                if first and j0 > 0:
                    MS(g, "pool", A[:, 0:j0], 0.0, [Ab])

        def s6(t):
            m, qc, kb, n_, j0, diag, base, first = info(t)
            ib = t % R
            if first:
                pso[(m, qc)] = psf(g, "sbO", [5])
            psO, pOb = pso[(m, qc)]
            for s_ in range(2):
                A, Ab = At[s_][ib], Atb[s_][ib]
                vv = svp[s_][:, kb, m * 128:(m + 1) * 128]
                c0 = 0 if first else j0
                MM(g, [mmf(psO[:, c0:512], vv, A[:, c0:512], first and s_ == 0, kb == 0 and s_ == 1)], [svb[s_][kb], Ab], [pOb])
            if kb == 0:
                CP(g, "act" if (m * 4 + qc) % 2 == 0 else "dve", osbT[:, m, qc * 512:(qc + 1) * 512], psO[:, :], [pOb], [osbb[m][qc]])
            del stt[t]

        nst = len(steps)
        stages = [s1, s2, s3, s4, s5, s6]
        for e in range(nst + len(stages) - 1):
            for k in range(len(stages) - 1, -1, -1):
                t = e - k
                if 0 <= t < nst:
                    stages[k](t)


def phase_dsa(g, l, hT, hTb, odT, odb):
    P = g.P
    with scope(g) as st:
        featT = sb(g, st, [128, 9, T], BF16, "featT")
        fb = nbs(st, "feat", NB)
        dvx = sb(g, st, [128, NB, 128], BF16, "dvx")
        dvb = nbs(st, "dvx", NB)
        sgn = sb(g, st, [128, NB, 8], F32, "sgn")
        sgb = nbs(st, "sgn", NB)
        qkg = sb(g, st, [128, 7, 64], F32, "qkg")
        qkgb = nbs(st, "qkg", 7)
        for hh in range(7):
            src = (g.qn if hh < 6 else g.kn)[l:l + 1, :].to_broadcast([128, 64])
            DMA(g, "sp", qkg[:, hh, :], src, (), [qkgb[hh]])
        with scope(g) as s2:
            wB = sb(g, s2, [128, DC, 512], BF16, "wB")
            wC = sb(g, s2, [128, DC, 72], BF16, "wC")
            wBb, wCb = nb(s2, "wB"), nb(s2, "wC")
            if take_pre(g, ("wA", l)):
                wA, wAb = g.pre, g.preb
            else:
                wA = sb(g, s2, [128, DC, 512], BF16, "wA")
                wAb = nb(s2, "wA")
                DMA(g, "pool", wA[:], win_cols(g, l, C_DQ, C_DQ + 512), (), [wAb])
            DMA(g, "pool", wB[:], win_cols(g, l, C_IQ, C_IQ + 512), (), [wBb])
            DMA(g, "pool", wC[:], win_cols(g, l, C_IK, C_IK + 72), (), [wCb])
            NR = 4
            tq = [sb(g, s2, [128, 18, 64], F32, "tokq") for _ in range(NR)]
            tqb = nbs(s2, "tokq", NR)
            tbq = [sb(g, s2, [128, 18, 64], BF16, "tokb") for _ in range(3)]
            tbb = nbs(s2, "tokb", 3)
            sqt = [sb(g, s2, [128, 448], F32, "sqt") for _ in range(2)]
            sqtb = nbs(s2, "sqt", 2)
            smq = [sb(g, s2, [128, 32], F32, "small") for _ in range(2)]
            smqb = nbs(s2, "small", 2)
            rt = [sb(g, s2, [128, 18, 8], F32, "ropet") for _ in range(4)]
            rtb = nbs(s2, "ropet", 4)
            pst = {}

            def p1(tb):
                MS(g, "pool", dvx[:, tb, 64:128], 1.0, [dvb[tb]])
                tk, tkb = tq[tb % NR], tqb[tb % NR]
                MS(g, "pool", tk[:, 7, :], 0.0, [tkb])
                MS(g, "pool", tk[:, 17, :], 0.0, [tkb])
                psA, pAb = psf(g, "dA", [0, 1])
                psBq, pBb = psf(g, "dB", [2, 3])
                psC, pCb = psf(g, "dC", [4, 5])
                pst[tb] = (psA, pAb, psBq, pBb, psC, pCb)
                mm_tm(g, psA, 512, wA, wAb, 0, hT, hTb, tb, pAb)
                mm_tm(g, psBq, 512, wB, wBb, 0, hT, hTb, tb, pBb)
                mm_tm(g, psC, 72, wC, wCb, 0, hT, hTb, tb, pCb)

            def p2(tb):
                psA, pAb, psBq, pBb, psC, pCb = pst.pop(tb)
                tk, tkb = tq[tb % NR], tqb[tb % NR]
                sq_, sq_b = sqt[tb % 2], sqtb[tb % 2]
                sm, smb = smq[tb % 2], smqb[tb % 2]
                ACT(g, sq_[:], psA[:, 0:448], AF.Square, [pAb], [sq_b])
                ss = sm[:, 0:7]
                P.op("dve", lambda e, ss=ss, sq_=sq_: e.tensor_reduce(out=ss, in_=sq_[:].rearrange("p (h d) -> p h d", d=64), axis=AX.X,
                                                                      op=ALU.add), [sq_b], [smb], 448)
                ACT(g, ss, ss, AF.Ln, [smb], [smb], scale=1.0 / 64, bias=EPS)
                ACT(g, ss, ss, AF.Exp, [smb], [smb], scale=-0.5)
                aw = sm[:, 8:16]
                TS(g, "dve", sgn[:, tb, :], psC[:, 64:72], 0.0, 2.0, ALU.is_gt, ALU.mult, [pCb], [sgb[tb]])
                TS(g, "dve", sgn[:, tb, :], sgn[:, tb, :], -1.0, 0.0, ALU.add, ALU.add, [sgb[tb]], [sgb[tb]])
                STT(g, aw, psC[:, 64:72], IDX_SCALE, sgn[:, tb, :], ALU.mult, ALU.mult, [pCb, sgb[tb]], [smb])
                TT(g, "dve", tk[:, 8:16, :], psBq[:, :].rearrange("p (h d) -> p h d", d=64),
                   aw.unsqueeze(2).to_broadcast([128, 8, 64]), ALU.mult, [pBb, smb], [tkb])
                CP(g, "act", tk[:, 16, :], psC[:, 0:64], [pCb], [tkb])
                CP(g, "act", dvx[:, tb, 0:64], psA[:, 448:512], [pAb], [dvb[tb]])
                TT(g, "dve", tk[:, 0:7, :], psA[:, 0:448].rearrange("p (h d) -> p h d", d=64),
                   ss.unsqueeze(2).to_broadcast([128, 7, 64]), ALU.mult, [pAb, smb], [tkb])
                TT(g, "dve", tk[:, 0:7, :], tk[:, 0:7, :], qkg[:], ALU.mult, [tkb] + qkgb, [tkb])

            def p3(tb):
                tk, tkb = tq[tb % NR], tqb[tb % NR]
                x1, x2 = tk[:, :, 0:8], tk[:, :, 8:16]
                cb_ = g.cs[:, tb, :].unsqueeze(1).to_broadcast([128, 18, 8])
                sb_ = g.sn[:, tb, :].unsqueeze(1).to_broadcast([128, 18, 8])
                TT(g, "dve", rt[0][:], x1, cb_, ALU.mult, [tkb, g.csb], [rtb[0]])
                TT(g, "pool", rt[1][:], x2, sb_, ALU.mult, [tkb, g.snb], [rtb[1]])
                TT(g, "dve", rt[2][:], x2, cb_, ALU.mult, [tkb, g.csb], [rtb[2]])
                TT(g, "pool", rt[3][:], x1, sb_, ALU.mult, [tkb, g.snb], [rtb[3]])
                TT(g, "dve", x1, rt[0][:], rt[1][:], ALU.subtract, [rtb[0], rtb[1]], [tkb])
                TT(g, "pool", x2, rt[2][:], rt[3][:], ALU.add, [rtb[2], rtb[3]], [tkb])

            def p4(tb):
                tk, tkb = tq[tb % NR], tqb[tb % NR]
                tkh, tkhb = tbq[tb % 3], tbb[tb % 3]
                CP(g, "act", tkh[:], tk[:], [tkb], [tkhb])
                CP(g, "pool", tkh[:, 7, :], tkh[:, 6, :], [tkhb], [tkhb])
                CP(g, "pool", tkh[:, 17, :], tkh[:, 16, :], [tkhb], [tkhb])

            def p5(tb):
                tkh, tkhb = tbq[tb % 3], tbb[tb % 3]
                flat = tkh[:].rearrange("p h d -> p (h d)")
                pt, ptb = psb(g)
                MM(g, [trf(pt[:, m * 128:(m + 1) * 128], flat[:, m * 128:(m + 1) * 128], g.ident[:]) for m in range(8)],
                   [tkhb, g.cb], [ptb])
                CP(g, "dve", featT[:, 0:8, tb * 128:(tb + 1) * 128], pt[:, :].rearrange("p (m j) -> p m j", j=128), [ptb], [fb[tb]])
                pt2, ptb2 = psb(g)
                MM(g, [trf(pt2[:, 0:128], flat[:, 1024:1152], g.ident[:])], [tkhb, g.cb], [ptb2])
                CP(g, "act", featT[:, 8, tb * 128:(tb + 1) * 128], pt2[:, 0:128], [ptb2], [fb[tb]])

            stages = [p1, p2, p3, p4, p5]
            for e in range(NB + len(stages) - 1):
                for k in range(len(stages) - 1, -1, -1):
                    t_ = e - k
                    if 0 <= t_ < NB:
                        stages[k](t_)
        prefetch(g, ("hihg", l), win_cols(g, l, C_HI, C_HI + 512), 512)
        sc = [sb(g, st, [128, T], F32, "sc") for _ in range(4)]
        scb = nbs(st, "sc", 4, 4)
        junk = sb(g, st, [128, T], BF16, "junk")
        maskq = [sb(g, st, [128, T], BF16, "maskq") for _ in range(2)]
        mqb = nbs(st, "maskq", 2)
        maskT = sb(g, st, [128, NB, 512], BF16, "maskT")
        mTb = nbs(st, "maskT", 4)
        rj = [sb(g, st, [128, 512], BF16, "rj") for _ in range(4)]
        rjb = nbs(st, "rj", 4)
        dg = [sb(g, st, [128, 8, 128], BF16, "dg") for _ in range(2)]
        dgb = nbs(st, "dg", 2)
        sm = [sb(g, st, [128, 8 + 2 * N_BISECT], F32, "bis") for _ in range(2)]
        smb = nbs(st, "bis", 2)
        cvec = sb(g, st, [128, N_BISECT], F32, "cvec")
        c255 = sb(g, st, [128, 1], F32, "c255")
        cvb = nb(st, "cvec")
        for n_ in range(N_BISECT):
            MS(g, "pool", cvec[:, n_:n_ + 1], 2.0 ** -(n_ + 1), [cvb])
        MS(g, "pool", c255[:], 255.5, [cvb])
        Pt = [sb(g, st, [128, 512], BF16, "Pt") for _ in range(4)]
        Ptb = nbs(st, "Pt", 4)
        Pm = [sb(g, st, [128, 512], BF16, "Pm") for _ in range(4)]
        Pmb = nbs(st, "Pm", 4)
        rs = sb(g, st, [64, 512], F32, "rs")
        rsb = nb(st, "rs")
        cnt_ = {"ri": 0, "pi": 0, "pm": 0}

        def idx_blocks(blocks):
            tiles = []
            for i in blocks:
                nk = (i + 1) * 128
                d_, d_b = dg[i % 2], dgb[i % 2]
                for j in range(8):
                    TS(g, "pool", d_[:, j, :], g.ident[:], sgn[:, i, j:j + 1], None, ALU.mult, None, [g.cb, sgb[i]], [d_b])
                for kc in range((nk + 511) // 512):
                    n = min(512, nk - kc * 512)
                    for j in range(8):
                        tiles.append((i, kc, n, j))
            stt = {}

            def s1(t):
                i, kc, n, j = tiles[t]
                po = 64 * (j % 2)
                psZ, pZb = psf(g, "ixZ", [0, 1, 2])
                stt[t] = [psZ, pZb]
                MM(g, [mmf(psZ[:, 0:n], featT[po:po + 64, 4 + j // 2, i * 128:(i + 1) * 128],
                           featT[po:po + 64, 8, kc * 512:kc * 512 + n], True, True)],
                   [fb[i]] + fb[kc * 4:(kc * 512 + n) // 128], [pZb])

            def s2(t):
                i, kc, n, j = tiles[t]
                psZ, pZb = stt[t]
                r_, r_b = rj[cnt_["ri"] % 4], rjb[cnt_["ri"] % 4]
                cnt_["ri"] += 1
                stt[t] += [r_, r_b]
                ACT(g, r_[:, 0:n], psZ[:, 0:n], AF.Relu, [pZb], [r_b])

            def s3(t):
                i, kc, n, j = tiles[t]
                r_, r_b = stt[t][2], stt[t][3]
                if j == 0:
                    cnt_["psS"] = psf(g, "ixS", [3, 4])
                psS, pSb = cnt_["psS"]
                d_, d_b = dg[i % 2], dgb[i % 2]
                MM(g, [mmf(psS[:, 0:n], d_[:, j, :], r_[:, 0:n], j == 0, j == 7)], [d_b, r_b], [pSb])
                if j == 7:
                    CP(g, "act", sc[i % 4][:, kc * 512:kc * 512 + n], psS[:, 0:n], [pSb], [scb[i % 4][kc]])
                del stt[t]

            pipeline([s1, s2, s3], len(tiles))

        def bis_pair(p):
            blocks = [2 * p, 2 * p + 1]
            st_ = []
            for bi, i in enumerate(blocks):
                nk = (i + 1) * 128
                s_, s_b = sc[i % 4], scb[i % 4]
                nkc = (nk + 511) // 512
                srd = s_b[0:nkc]
                m_, m_b = sm[bi], smb[bi]
                rmax, rmin, step0, mid, cntv, tt = (m_[:, c:c + 1] for c in range(6))
                stepc = m_[:, 8:8 + N_BISECT]
                if nk > 256:
                    P.op("dve", lambda e, s_=s_, nk=nk, rmax=rmax: e.tensor_reduce(out=rmax, in_=s_[:, 0:nk], axis=AX.X, op=ALU.max), srd, [m_b], nk)
                    P.op("dve", lambda e, s_=s_, nk=nk, rmin=rmin: e.tensor_reduce(out=rmin, in_=s_[:, 0:nk], axis=AX.X, op=ALU.min), srd, [m_b], nk)
                dsl = s_[:, i * 128:(i + 1) * 128]
                TT(g, "pool", dsl, dsl, g.caus01[:], ALU.mult, [s_b[i // 4], g.cb], [s_b[i // 4]])
                TT(g, "pool", dsl, dsl, g.negfill[:], ALU.add, [s_b[i // 4], g.cb], [s_b[i // 4]])
                st_.append((i, nk, s_, srd, m_, m_b, rmax, rmin, step0, mid, cntv, tt, stepc))
            act = [x for x in st_ if x[1] > 256]
            for (i, nk, s_, srd, m_, m_b, rmax, rmin, step0, mid, cntv, tt, stepc) in act:
                TT(g, "dve", step0, rmax, rmin, ALU.subtract, [m_b], [m_b])
            for (i, nk, s_, srd, m_, m_b, rmax, rmin, step0, mid, cntv, tt, stepc) in act:
                TS(g, "dve", stepc, cvec[:], step0, None, ALU.mult, None, [m_b, cvb], [m_b])
            for (i, nk, s_, srd, m_, m_b, rmax, rmin, step0, mid, cntv, tt, stepc) in act:
                TS(g, "dve", mid, stepc[:, 0:1], rmin, g.zc[:, 0:1], ALU.add, ALU.add, [m_b, g.cb], [m_b])
            for n_ in range(N_BISECT):
                for (i, nk, s_, srd, m_, m_b, rmax, rmin, step0, mid, cntv, tt, stepc) in act:
                    TS(g, "dve", junk[:, 0:nk], s_[:, 0:nk], mid, g.zc[:, 0:1], ALU.is_ge, ALU.add, srd + [m_b, g.cb], [m_b], accum_out=cntv)
                for (i, nk, s_, srd, m_, m_b, rmax, rmin, step0, mid, cntv, tt, stepc) in act:
                    TS(g, "dve", tt, cntv, c255[:, 0:1], stepc[:, n_:n_ + 1], ALU.is_ge, ALU.mult, [m_b, cvb], [m_b])
                for (i, nk, s_, srd, m_, m_b, rmax, rmin, step0, mid, cntv, tt, stepc) in act:
                    nn = min(n_ + 1, N_BISECT - 1)
                    TS(g, "dve", mid, tt, stepc[:, nn:nn + 1], mid, ALU.subtract, ALU.add, [m_b], [m_b])
            for (i, nk, s_, srd, m_, m_b, rmax, rmin, step0, mid, cntv, tt, stepc) in st_:
                thr = mid if nk > 256 else g.negbig[:, 0:1]
                mq, mq_b = maskq[i % 2], mqb[i % 2]
                TS(g, "dve", mq[:, 0:nk], s_[:, 0:nk], thr, None, ALU.is_ge, None, srd + [m_b, g.cb], [mq_b])

        def mT_pair(p):
            for i in (2 * p, 2 * p + 1):
                ii = i % 4
                mq, mq_b = maskq[i % 2], mqb[i % 2]
                for k0 in range(0, i + 1, 8):
                    k1 = min(i + 1, k0 + 8)
                    pt, ptb = psb(g)
                    MM(g, [trf(pt[:, (kb - k0) * 128:(kb - k0 + 1) * 128], mq[:, kb * 128:(kb + 1) * 128], g.ident[:])
                           for kb in range(k0, k1)], [mq_b, g.cb], [ptb])
                    CP(g, "act", maskT[:, k0:k1, ii * 128:(ii + 1) * 128],
                       pt[:, 0:(k1 - k0) * 128].rearrange("p (m j) -> p m j", j=128), [ptb], [mTb[ii]])

        def att_chunk(qc):
            last = 4 * qc + 3
            tiles = [(h, kb) for h in range(6) for kb in range(last + 1)]
            stt = {}
            pso = {}

            def s1(t):
                h, kb = tiles[t]
                hp, po = h // 2, 64 * (h % 2)
                j0 = max(0, kb * 128 - qc * 512)
                psL, pLb = psf(g, "dsL", [0, 1, 2])
                stt[t] = [psL, pLb]
                MM(g, [mmf(psL[:, j0:512], featT[po:po + 64, 3, kb * 128:(kb + 1) * 128],
                           featT[po:po + 64, hp, qc * 512 + j0:(qc + 1) * 512], True, True)],
                   [fb[kb]] + fb[qc * 4:qc * 4 + 4], [pLb])

            def s2(t):
                h, kb = tiles[t]
                j0 = max(0, kb * 128 - qc * 512)
                psL, pLb = stt[t][0], stt[t][1]
                pi = cnt_["pi"]
                cnt_["pi"] += 1
                p_, p_b = Pt[pi % 4], Ptb[pi % 4]
                stt[t] += [p_, p_b]
                ACT(g, p_[:, j0:512], psL[:, j0:512], AF.Exp, [pLb], [p_b], scale=0.125)

            def s3(t):
                h, kb = tiles[t]
                j0 = max(0, kb * 128 - qc * 512)
                p_, p_b = stt[t][2], stt[t][3]
                pm = cnt_["pm"]
                cnt_["pm"] += 1
                m_, m_b = Pm[pm % 4], Pmb[pm % 4]
                stt[t] += [m_, m_b]
                TT(g, "pool" if t % 3 == 0 else "dve", m_[:, j0:512], p_[:, j0:512], maskT[:, kb, j0:512], ALU.mult,
                   [p_b] + mTb[j0 // 128:4], [m_b])

            def s4(t):
                h, kb = tiles[t]
                hp, po = h // 2, 64 * (h % 2)
                j0 = max(0, kb * 128 - qc * 512)
                m_, m_b = stt[t][4], stt[t][5]
                if kb == 0:
                    pso[h] = psf(g, "dsO", [4, 5])
                psO, pOb = pso[h]
                MM(g, [mmf(psO[:, j0:512], dvx[:, kb, :], m_[:, j0:512], kb == 0, kb == last)], [dvb[kb], m_b], [pOb])
                if kb == last:
                    ACT(g, rs[0:64, :], psO[64:128, :], AF.Ln, [pOb], [rsb])
                    ACT(g, rs[0:64, :], rs[0:64, :], AF.Exp, [rsb], [rsb], scale=-1.0)
                    TT(g, "dve", odT[po:po + 64, hp, qc * 512:(qc + 1) * 512], psO[0:64, :], rs[0:64, :], ALU.mult, [pOb, rsb],
                       [odb[hp][qc]])
                del stt[t]

            pipeline([s1, s2, s3, s4], len(tiles))

        idx_blocks([14, 15])
        for p in range(7, -1, -1):
            if p > 0:
                idx_blocks([2 * p - 2, 2 * p - 1])
            bis_pair(p)
            mT_pair(p)
            if p % 2 == 0:
                att_chunk(p // 2)


def phase_hgrn(g, l, hT, hTb, ohT, ohb):
    P = g.P
    import os
    if int(os.environ.get("HGL", "9")) == 0:
        return
    with scope(g) as st:
        hi_tm = sb(g, st, [128, NB, 256], BF16, "hi_tm")
        hib = nbs(st, "hi", NB)
        hgs = sb(g, st, [128, NB, 256], BF16, "hgs")
        hgb = nbs(st, "hgs", NB)
        onb = sb(g, st, [128, 64], F32, "onorm")
        onbb = nb(st, "onorm")
        DMA(g, "sp", onb[:], g.onorm[l:l + 1, :].to_broadcast([128, 64]), (), [onbb])
        with scope(g) as s2:
            if take_pre(g, ("hihg", l)):
                w, wb = g.pre, g.preb
            else:
                w = sb(g, s2, [128, DC, 512], BF16, "whihg")
                wb = nb(s2, "whihg")
                DMA(g, "pool", w[:], win_cols(g, l, C_HI, C_HI + 512), (), [wb])
            sgs = [sb(g, s2, [128, 256], F32, "sgs") for _ in range(2)]
            sgsb = nbs(s2, "sgs", 2)
            for tb in range(NB):
                ps, pb = psf(g, "proj", [0, 1, 2, 3, 4, 5])
                hgv = int(os.environ.get("HGV", "15"))
                if hgv & 8:
                    mm_tm(g, ps, 512, w, wb, 0, hT, hTb, tb, pb)
                if hgv & 1:
                    CP(g, "dve", hi_tm[:, tb, :], ps[:, 0:256], [pb], [hib[tb]])
                sgt, sgtb = sgs[tb % 2], sgsb[tb % 2]
                if hgv & 2:
                    ACT(g, sgt[:], ps[:, 256:512], AF.Exp if hgv & 16 else AF.Sigmoid, [pb], [sgtb])
                if hgv & 4:
                    TT(g, "dve", hgs[:, tb, :], ps[:, 256:512], sgt[:], ALU.mult, [pb, sgtb], [hgb[tb]])
        with scope(g) as s3:
            NH = 4
            R = 4
            qtT = sb(g, s3, [128, NH, T], BF16, "qtT")
            ktT = sb(g, s3, [128, NH, T], BF16, "ktT")
            qtb = nbs(s3, "qt", NH)
            ktb = nbs(s3, "kt", NH)
            kt_tm = sb(g, s3, [128, NB, NH * 128], BF16, "kt_tm")
            kttb = nbs(s3, "kttm", NH)
            t1 = sb(g, s3, [128, T], F32, "t1")
            t2 = sb(g, s3, [128, T], F32, "t2")
            t3 = sb(g, s3, [128, T], F32, "t3")
            t1h, t2h, t3h = nbs(s3, "t1", 4), nbs(s3, "t2", 4), nbs(s3, "t3", 4)
            ebl = sb(g, s3, [128, NH, 32], F32, "ebl")
            eblb = nb(s3, "ebl")
            W = [sb(g, s3, [128, NH, 64], F32, "W") for _ in range(2)]
            Wb = nbs(s3, "W", 2)
            Sbf = [sb(g, s3, [128, NH, 64], BF16, "Sbf") for _ in range(R)]
            Sbfb = nbs(s3, "Sbf", R)
            attm = [sb(g, s3, [128, NH, 128], BF16, "attm") for _ in range(3)]
            attb = nbs(s3, "attm", 3)
            o_tm = [sb(g, s3, [128, NH * 64], F32, "o_tm") for _ in range(3)]
            otb = nbs(s3, "otm", 3)
            osq = sb(g, s3, [128, NH * 64], F32, "osq")
            osqb = nb(s3, "osq")
            og = [sb(g, s3, [128, NH * 64], BF16, "og") for _ in range(2)]
            ogb = nbs(s3, "og", 2)
            sm = [sb(g, s3, [128, 4], F32, "hsm") for _ in range(2)]
            smb = nbs(s3, "hsm", 2)
            wq = [sb(g, s3, [128, DC, 128], BF16, "wq") for _ in range(2)]
            wqb = nbs(s3, "wq", 2)
            wf = [sb(g, s3, [128, DC, 128], BF16, "wf") for _ in range(2)]
            wfb = nbs(s3, "wf", 2)

            def load_head(hd):
                DMA(g, "pool", wf[hd % 2][:], win_cols(g, l, C_HF + hd * 128, C_HF + hd * 128 + 128), (), [wfb[hd % 2]])
                DMA(g, "pool", wq[hd % 2][:], win_cols(g, l, C_HQ + hd * 128, C_HQ + hd * 128 + 128), (), [wqb[hd % 2]])

            load_head(0)
            for hd in range(NH):
                if hd + 1 < NH:
                    load_head(hd + 1)
                w_f, w_fb, w_q, w_qb = wf[hd % 2], wfb[hd % 2], wq[hd % 2], wqb[hd % 2]
                NQ = 4
                HS = [slice(q_ * (T // NQ), (q_ + 1) * (T // NQ)) for q_ in range(NQ)]
                for tc in range(4):
                    ps, pb = psf(g, "proj", [0, 1, 2, 3, 4, 5])
                    mm_fm(g, ps, 128, 512, w_f, w_fb, 0, hT, hTb, tc * 512, pb)
                    ACT(g, t1[:, tc * 512:(tc + 1) * 512], ps[:, :], AF.Sigmoid, [pb], [t1h[tc]])
                for hf in range(NQ):
                    TS(g, "dve", t1[:, HS[hf]], t1[:, HS[hf]], g.oml[:, l, hd:hd + 1], g.lbv[:, l, hd:hd + 1], ALU.mult, ALU.add,
                       [t1h[hf], g.cb], [t1h[hf]])
                for hf in range(NQ):
                    ACT(g, t2[:, HS[hf]], t1[:, HS[hf]], AF.Copy, [t1h[hf]], [t2h[hf]], scale=-1.0, bias=1.0)
                for hf in range(NQ):
                    TS(g, "dve", t1[:, HS[hf]], t1[:, HS[hf]], F_MIN, None, ALU.max, None, [t1h[hf]], [t1h[hf]])
                for hf in range(NQ):
                    ACT(g, t1[:, HS[hf]], t1[:, HS[hf]], AF.Ln, [t1h[hf]], [t1h[hf]])
                for hf in range(NQ):
                    P.op("dve", lambda e, hf=hf: e.tensor_tensor_scan(out=t3[:, HS[hf]], data0=g.resetm[:, HS[hf]], data1=t1[:, HS[hf]],
                                                                     initial=0.0, op0=ALU.mult, op1=ALU.add),
                         [t1h[hf], g.cb], [t3h[hf]], 2 * T // NQ)
                for hf in range(NQ):
                    TS(g, "dve", t3[:, HS[hf]], t3[:, HS[hf]], -80.0, None, ALU.max, None, [t3h[hf]], [t3h[hf]])
                for hf in range(NQ):
                    ACT(g, t1[:, HS[hf]], t3[:, HS[hf]], AF.Exp, [t3h[hf]], [t1h[hf]])
                for hf in range(NQ):
                    CP(g, "pool", ebl[:, hd, hf * (32 // NQ):(hf + 1) * (32 // NQ)].unsqueeze(2),
                       t1[:, HS[hf]].rearrange("p (c j) -> p c j", j=64)[:, :, 63:64], [t1h[hf]], [eblb])
                for hf in range(NQ):
                    ACT(g, t3[:, HS[hf]], t3[:, HS[hf]], AF.Exp, [t3h[hf]], [t3h[hf]], scale=-1.0)
                for hf in range(NQ):
                    TT(g, "dve", ktT[:, hd, HS[hf]], t2[:, HS[hf]], t3[:, HS[hf]], ALU.mult, [t2h[hf], t3h[hf]], [ktb[hd]])
                for tc in range(4):
                    ps, pb = psf(g, "proj", [0, 1, 2, 3, 4, 5])
                    mm_fm(g, ps, 128, 512, w_q, w_qb, 0, hT, hTb, tc * 512, pb)
                    ACT(g, t2[:, tc * 512:(tc + 1) * 512], ps[:, :], AF.Sigmoid, [pb], [t2h[tc]])
                    TT(g, "dve", t2[:, tc * 512:(tc + 1) * 512], ps[:, :], t2[:, tc * 512:(tc + 1) * 512], ALU.mult, [pb, t2h[tc]],
                       [t2h[tc]])
                for hf in range(NQ):
                    TT(g, "dve", qtT[:, hd, HS[hf]], t2[:, HS[hf]], t1[:, HS[hf]], ALU.mult, [t2h[hf], t1h[hf]], [qtb[hd]])
                for k0 in (0, 8):
                    pt, ptb = psb(g)
                    MM(g, [trf(pt[:, m * 128:(m + 1) * 128], ktT[:, hd, (k0 + m) * 128:(k0 + m + 1) * 128], g.ident[:])
                           for m in range(8)], [ktb[hd], g.cb], [ptb])
                    CP(g, "act" if k0 == 0 else "dve", kt_tm[:, k0:k0 + 8, hd * 128:(hd + 1) * 128],
                       pt[:, :].rearrange("p (m j) -> p m j", j=128), [ptb], [kttb[hd]])

            NCH = 32
            xps = {}

            def emit_X(c):
                tb, pr = c // 2, (c % 2) * 64
                psX, pXb = psf(g, "hgX", [0, 1, 2])
                xps[c] = (psX, pXb)
                MM(g, [mmf(psX[:, hh * 64:(hh + 1) * 64], kt_tm[pr:pr + 64, tb, hh * 128:(hh + 1) * 128],
                           hi_tm[pr:pr + 64, tb, hh * 64:(hh + 1) * 64], True, True) for hh in range(NH)], kttb + [hib[tb]], [pXb])

            def emit_A(tb):
                psA, pAb = psf(g, "hgA", [3])
                MM(g, [mmf(psA[:, hh * 128:(hh + 1) * 128], ktT[:, hh, tb * 128:(tb + 1) * 128],
                           qtT[:, hh, tb * 128:(tb + 1) * 128], True, True) for hh in range(NH)], ktb + qtb, [pAb])
                TT(g, "dve", attm[tb % 3][:], psA[:, :].rearrange("p (h t) -> p h t", t=128),
                   g.maskbd[:].unsqueeze(1).to_broadcast([128, NH, 128]), ALU.mult, [pAb, g.cb], [attb[tb % 3]])

            emit_X(0)
            emit_X(1)
            emit_A(0)
            CP(g, "dve", W[0][:], xps[0][0][:, 0:NH * 64].rearrange("p (h v) -> p h v", v=64), [xps[0][1]], [Wb[0]])
            MS(g, "pool", Sbf[0][:], 0.0, [Sbfb[0]])
            for c in range(NCH):
                tb, half = c // 2, c % 2
                pr = half * 64
                if c + 2 < NCH:
                    emit_X(c + 2)
                if half == 0 and tb + 1 < NB:
                    emit_A(tb + 1)
                if c + 1 < NCH:
                    eb_ = ebl[:, :, c:c + 1].to_broadcast([128, NH, 64])
                    TT(g, "pool", Sbf[(c + 1) % R][:], W[c % 2][:], eb_, ALU.mult, [Wb[c % 2], eblb], [Sbfb[(c + 1) % R]])
                    psX, pXb = xps.pop(c + 1)
                    for hh in range(NH):
                        STT(g, W[(c + 1) % 2][:, hh, :], W[c % 2][:, hh, :], ebl[:, hh, c:c + 1], psX[:, hh * 64:(hh + 1) * 64],
                            ALU.mult, ALU.add, [Wb[c % 2], eblb, pXb], [Wb[(c + 1) % 2]])
                am, amb = attm[tb % 3], attb[tb % 3]
                ot, otbuf = o_tm[tb % 3], otb[tb % 3]
                psO, pOb = psf(g, "hgO", [4, 5])
                fns = []
                for hh in range(NH):
                    fns.append(mmf(psO[0:64, hh * 64:(hh + 1) * 64], am[pr:pr + 64, hh, pr:pr + 64],
                                   hi_tm[pr:pr + 64, tb, hh * 64:(hh + 1) * 64], True, False))
                    fns.append(mmf(psO[0:64, hh * 64:(hh + 1) * 64], qtT[:, hh, c * 64:(c + 1) * 64], Sbf[c % R][:, hh, :], False, True))
                MM(g, fns, [amb, hib[tb], Sbfb[c % R]] + qtb, [pOb])
                CP(g, "act", ot[pr:pr + 64, :], psO[0:64, 0:NH * 64], [pOb], [otbuf])
                if half == 1:
                    sm_, sm_b = sm[tb % 2], smb[tb % 2]
                    og_, og_b = og[tb % 2], ogb[tb % 2]
                    TT(g, "pool", osq[:], ot[:], ot[:], ALU.mult, [otbuf], [osqb])
                    ss = sm_[:, 0:NH]
                    P.op("dve", lambda e, ss=ss: e.tensor_reduce(out=ss, in_=osq[:].rearrange("p (h d) -> p h d", d=64), axis=AX.X,
                                                                 op=ALU.add), [osqb], [sm_b], NH * 64)
                    ACT(g, ss, ss, AF.Ln, [sm_b], [sm_b], scale=1.0 / 64, bias=EPS)
                    ACT(g, ss, ss, AF.Exp, [sm_b], [sm_b], scale=-0.5)
                    o3 = ot[:].rearrange("p (h d) -> p h d", d=64)
                    TT(g, "dve", o3, o3, ss.unsqueeze(2).to_broadcast([128, NH, 64]), ALU.mult, [otbuf, sm_b], [otbuf])
                    TT(g, "dve", o3, o3, onb[:].unsqueeze(1).to_broadcast([128, NH, 64]), ALU.mult, [otbuf, onbb], [otbuf])
                    TT(g, "dve", og_[:], ot[:], hgs[:, tb, :], ALU.mult, [otbuf, hgb[tb]], [og_b])
                    pt, ptb = psb(g)
                    MM(g, [trf(pt[:, m * 128:(m + 1) * 128], og_[:, m * 128:(m + 1) * 128], g.ident[:]) for m in range(2)],
                       [og_b, g.cb], [ptb])
                    CP(g, "act", ohT[:, 0:2, tb * 128:(tb + 1) * 128], pt[:, 0:256].rearrange("p (m j) -> p m j", j=128), [ptb],
                       [ohb[0][tb // 4], ohb[1][tb // 4]])


def phase_mix(g, l, hT, hTb, osbT, osbb, odT, odb, ohT, ohb, src_ap, src_bufs):
    P = g.P
    with scope(g) as st:
        mixT = sb(g, st, [128, DC, T], BF16, "mixT")
        mxb = nbs(st, "mix", DC, 4)
        wg = [sb(g, st, [128, DC, 3, 256], BF16, "wg") for _ in range(2)]
        wgb = nbs(st, "wg", 2, 3)
        wy = [sb(g, st, [128, 8, 256], BF16, "wy") for _ in range(2)]
        wyb = nbs(st, "wy", 2, 3)
        sg = [sb(g, st, [128, 512], F32, "sg") for _ in range(2)]
        sgb = nbs(st, "sg", 2)
        acc = [sb(g, st, [128, 512], F32, "acc") for _ in range(2)]
        accb = nbs(st, "acc", 2)
        tm = [sb(g, st, [128, 512], F32, "tm") for _ in range(2)]
        tmb = nbs(st, "tm", 2)
        k = 0

        def load_pair(dp):
            w_, w_b = wg[dp % 2], wgb[dp % 2]
            y_, y_b = wy[dp % 2], wyb[dp % 2]
            cs_ = slice(dp * 256, (dp + 1) * 256)
            for gi in range(3):
                c0 = C_G + gi * 1024 + dp * 256
                DMA(g, "pool", w_[:, :, gi, :], win_cols(g, l, c0, c0 + 256), (), [w_b[gi]])
            DMA(g, "pool", y_[:, 0:3, :], g.w_sb[l].rearrange("(c p) n -> p c n", p=128)[:, :, cs_], (), [y_b[0]])
            DMA(g, "pool", y_[:, 3:6, :], g.w_dsa[l].rearrange("(c p) n -> p c n", p=128)[:, :, cs_], (), [y_b[1]])
            DMA(g, "pool", y_[:, 6:8, :], g.w_hg[l].rearrange("(c p) n -> p c n", p=128)[:, :, cs_], (), [y_b[2]])

        load_pair(0)
        wo = sb(g, st, [128, DC, D], BF16, "wo")
        wob = nbs(st, "wo", 2)
        for dc in range(DC):
            dp, do = dc // 2, (dc % 2) * 128
            w_, w_b = wg[dp % 2], wgb[dp % 2]
            y_, y_b = wy[dp % 2], wyb[dp % 2]
            if dc % 2 == 0:
                if dp + 1 < DC // 2:
                    load_pair(dp + 1)
                else:
                    for nh in range(2):
                        DMA(g, "pool", wo[:, :, nh * 512:(nh + 1) * 512],
                            g.w_out[l].rearrange("(c p) n -> p c n", p=128)[:, :, nh * 512:(nh + 1) * 512], (), [wob[nh]])
                    prefetch(g, ("wu", l, 0), g.w_up[l].rearrange("(c p) n -> p c n", p=128)[:, :, 0:512], 512)
            for tc in range(4):
                a_, a_b = acc[k % 2], accb[k % 2]
                for gi, (oT, obufs, nch, c0) in enumerate(((osbT, osbb, 3, 0), (odT, odb, 3, 3), (ohT, ohb, 2, 6))):
                    psG, pGb = psf(g, "mxG", [0, 1, 2])
                    MM(g, [mmf(psG[:, :], w_[:, c, gi, do:do + 128], hT[:, c, tc * 512:(tc + 1) * 512], c == 0, c == DC - 1) for c in range(DC)],
                       [w_b[gi]] + hTb[tc * 4:tc * 4 + 4], [pGb])
                    s_, s_b = sg[(k * 3 + gi) % 2], sgb[(k * 3 + gi) % 2]
                    ACT(g, s_[:], psG[:, :], AF.Sigmoid, [pGb], [s_b])
                    psY, pYb = psf(g, "mxY", [3, 4, 5])
                    MM(g, [mmf(psY[:, :], y_[:, c0 + c, do:do + 128], oT[:, c, tc * 512:(tc + 1) * 512], c == 0, c == nch - 1) for c in range(nch)],
                       [y_b[gi]] + [obufs[c][tc] for c in range(nch)], [pYb])
                    if gi == 0:
                        TT(g, "dve", a_[:], psY[:, :], s_[:], ALU.mult, [pYb, s_b], [a_b])
                    else:
                        t_, t_b = tm[gi % 2], tmb[gi % 2]
                        TT(g, "dve", t_[:], psY[:, :], s_[:], ALU.mult, [pYb, s_b], [t_b])
                        if gi == 1:
                            TT(g, "pool", a_[:], a_[:], t_[:], ALU.add, [a_b, t_b], [a_b])
                        else:
                            TT(g, "pool", mixT[:, dc, tc * 512:(tc + 1) * 512], a_[:], t_[:], ALU.add, [a_b, t_b], [mxb[dc][tc]])
                k += 1
        xbs = [sb(g, st, [128, D], F32, "xb") for _ in range(4)]
        xbb = nbs(st, "xb", 4)
        nctx = norm_setup(g, st, g.norm_mlp[l:l + 1, :])
        for tb in range(NB):
            xb, xbuf = xbs[tb % 4], xbb[tb % 4]
            DMA(g, "sp", xb[:], src_ap[tb * 128:(tb + 1) * 128, :], [src_bufs[tb]], [xbuf])
            for nh in range(2):
                ps, pb = psf(g, "mxO", [0, 1, 2, 3, 4, 5])
                MM(g, [mmf(ps[:, :], mixT[:, c, tb * 128:(tb + 1) * 128], wo[:, c, nh * 512:(nh + 1) * 512], c == 0, c == DC - 1)
                       for c in range(DC)], [wob[nh]] + [mxb[c][tb // 4] for c in range(DC)], [pb])
                xs = xb[:, nh * 512:(nh + 1) * 512]
                TT(g, "dve", xs, ps[:, :], xs, ALU.add, [pb, xbuf], [xbuf])
            DMA(g, "sp", g.xres_d[tb * 128:(tb + 1) * 128, :], xb[:], [xbuf], [g.xres_b[tb]])
            norm_block(g, nctx, xb, xbuf, tb, hT, hTb)


def phase_ffn(g, l, hT, hTb, dst_ap, dst_bufs):
    for half in range(2):
        last = half == 1
        with scope(g) as st:
            uT = sb(g, st, [128, 16, T], BF16, "uT")
            ub = nbs(st, "uT", 16, 4)
            wd = sb(g, st, [128, 16, D], BF16, "wd")
            wdb = nbs(st, "wd", 2)
            wdv = g.w_down[l].rearrange("(f p) n -> p f n", p=128)
            wu = [sb(g, st, [128, DC, 512], BF16, "wu") for _ in range(2)]
            wub = nbs(st, "wu", 2)
            rt = [sb(g, st, [128, 512], BF16, "rt") for _ in range(2)]
            rtb = nbs(st, "rt", 2)
            k = 0

            def load_wu(g4):
                c0 = half * 2048 + g4 * 512
                DMA(g, "pool", wu[g4 % 2][:], g.w_up[l].rearrange("(c p) n -> p c n", p=128)[:, :, c0:c0 + 512], (), [wub[g4 % 2]])

            pre0 = take_pre(g, ("wu", l, half))
            if not pre0:
                load_wu(0)
            for g4 in range(4):
                w_, w_b = wu[g4 % 2], wub[g4 % 2]
                if g4 == 0 and pre0:
                    w_, w_b = g.pre, g.preb
                if g4 + 1 < 4:
                    load_wu(g4 + 1)
                if g4 == 1:
                    for nh in range(2):
                        DMA(g, "pool", wd[:, :, nh * 512:(nh + 1) * 512], wdv[:, half * 16:(half + 1) * 16, nh * 512:(nh + 1) * 512], (),
                            [wdb[nh]])
                for fcl in range(4):
                    fc = g4 * 4 + fcl
                    for tc in range(4):
                        ps, pb = psf(g, "proj", [0, 1, 2, 3, 4, 5])
                        mm_fm(g, ps, 128, 512, w_, w_b, fcl * 128, hT, hTb, tc * 512, pb)
                        r_, r_b = rt[k % 2], rtb[k % 2]
                        k += 1
                        ACT(g, r_[:], ps[:, :], AF.Relu, [pb], [r_b])
                        TT(g, "pool", uT[:, fc, tc * 512:(tc + 1) * 512], r_[:], r_[:], ALU.mult, [r_b], [ub[fc][tc]])
            if half == 0:
                prefetch(g, ("wu", l, 1), g.w_up[l].rearrange("(c p) n -> p c n", p=128)[:, :, 2048:2560], 512)
            elif l + 1 < DEPTH:
                prefetch(g, ("sq", l + 1), win_cols(g, l + 1, C_SQ, C_SQ + 384), 384)
            xbs = [sb(g, st, [128, D], F32, "xb") for _ in range(4)]
            xbb = nbs(st, "xb", 4)
            nctx = norm_setup(g, st, g.norm_mix[l + 1:l + 2, :]) if (last and l + 1 < DEPTH) else None
            for tb in range(NB):
                xb, xbuf = xbs[tb % 4], xbb[tb % 4]
                DMA(g, "sp", xb[:], g.xres_d[tb * 128:(tb + 1) * 128, :], [g.xres_b[tb]], [xbuf])
                for nh in range(2):
                    ps, pb = psf(g, "proj", [0, 1, 2, 3, 4, 5])
                    MM(g, [mmf(ps[:, :], uT[:, fc, tb * 128:(tb + 1) * 128], wd[:, fc, nh * 512:(nh + 1) * 512], fc == 0, fc == 15)
                           for fc in range(16)], [wdb[nh]] + [ub[fc][tb // 4] for fc in range(16)], [pb])
                    xs = xb[:, nh * 512:(nh + 1) * 512]
                    TT(g, "dve", xs, ps[:, :], xs, ALU.add, [pb, xbuf], [xbuf])
                if last:
                    DMA(g, "sp", dst_ap[tb * 128:(tb + 1) * 128, :], xb[:], [xbuf], [dst_bufs[tb]])
                    if nctx is not None:
                        norm_block(g, nctx, xb, xbuf, tb, hT, hTb)
                else:
                    DMA(g, "sp", g.xres_d[tb * 128:(tb + 1) * 128, :], xb[:], [xbuf], [g.xres_b[tb]])


def dump_t(g, name, t, ncol):
    if g.dump == name:
        g.P.barrier()
        DMA(g, "sp", g.dbg_d[:, 0:ncol], t, [], [g.dbgb])
        g.P.wait_bufs("sp", [g.dbgb])
        g.P.barrier()


def build_layer(g, l):
    if l > 0 and g.stage < 7:
        return
    src_ap, src_bufs = (g.x_d, g.xin_b) if l == 0 else (g.xres_d, g.xres_b)
    with scope(g) as ls:
        hT, hTb = g.hT, g.hTb
        if l == 0:
            with scope(g) as st:
                norm_T(g, st, src_ap, src_bufs, g.norm_mix[l:l + 1, :], hT, hTb)
        if l == 0:
            dump_t(g, "hT", hT[:].rearrange("p c t -> p (c t)"), 8 * T)
        if g.stage < 2:
            return
        with scope(g) as ms:
            osbT = sb(g, ms, [128, 3, T], BF16, "osbT")
            odT = sb(g, ms, [128, 3, T], BF16, "odT")
            ohT = sb(g, ms, [128, 2, T], BF16, "ohT")
            osbb = nbs(ms, "osb", 3, 4)
            odb = nbs(ms, "od", 3, 4)
            ohb = nbs(ms, "oh", 2, 4)
            phase_sb(g, l, hT, hTb, osbT, osbb)
            if l == 0:
                dump_t(g, "osbT", osbT[:].rearrange("p c t -> p (c t)"), 3 * T)
            if g.stage < 3:
                return
            phase_dsa(g, l, hT, hTb, odT, odb)
            if l == 0:
                dump_t(g, "odT", odT[:].rearrange("p c t -> p (c t)"), 3 * T)
            if g.stage < 4:
                return
            phase_hgrn(g, l, hT, hTb, ohT, ohb)
            if l == 0:
                dump_t(g, "ohT", ohT[:].rearrange("p c t -> p (c t)"), 2 * T)
            if g.stage < 5:
                return
            phase_mix(g, l, hT, hTb, osbT, osbb, odT, odb, ohT, ohb, src_ap, src_bufs)
        if g.stage < 6:
            return
        if l == DEPTH - 1:
            phase_ffn(g, l, hT, hTb, g.out_d, g.out_b)
        else:
            phase_ffn(g, l, hT, hTb, g.xres_d, g.xres_b)


_NC_CACHE = {}


def rope_tables():
    half = 8
    inv = 500000.0 ** (-(np.arange(half, dtype=np.float32) * 2.0) / 16.0)
    ang = np.arange(T, dtype=np.float32)[:, None] * inv[None, :].astype(np.float32)
    return np.cos(ang).astype(np.float32), np.sin(ang).astype(np.float32)


def kernel(x, norm_mix, w_in, qn_dsa, kn_dsa, hgrn_lb, hgrn_onorm, w_br_sb, w_br_dsa, w_br_hgrn, w_out, norm_mlp, w_up, w_down):
    if "nc" not in _NC_CACHE:
        _NC_CACHE["nc"] = build_two_pass()
    nc = _NC_CACHE["nc"]
    f = lambda a: np.ascontiguousarray(np.asarray(a, dtype=np.float32))
    cs, sn = rope_tables()
    shared = dict(norm_mix=f(norm_mix), w_in=f(w_in), qn_dsa=f(qn_dsa), kn_dsa=f(kn_dsa), hgrn_lb=f(hgrn_lb),
                  hgrn_onorm=f(hgrn_onorm), w_br_sb=f(w_br_sb), w_br_dsa=f(w_br_dsa), w_br_hgrn=f(w_br_hgrn),
                  w_out=f(w_out), norm_mlp=f(norm_mlp), w_up=f(w_up), w_down=f(w_down), rope_cos=cs, rope_sin=sn)
    xs = f(x)
    in_maps = [dict(shared, x=xs[b]) for b in range(8)]
    res = run_bass_kernel_spmd(nc, in_maps, core_ids=list(range(8)))
    return np.stack([np.asarray(r["out"], dtype=np.float32) for r in res.results], axis=0)
```

```python
import math
import numpy as np
from contextlib import ExitStack, contextmanager
import concourse.bass as bass
import concourse.mybir as mybir
from concourse.bass_utils import run_bass_kernel_spmd

F32 = mybir.dt.float32
BF16 = mybir.dt.bfloat16
AF = mybir.ActivationFunctionType
ALU = mybir.AluOpType
AX = mybir.AxisListType

T = 2048
D = 1024
NB = 16
DC = 8
DIN = 6856
DFF = 4096
DEPTH = 2
EPS = 1e-6
F_MIN = 1e-12
IDX_SCALE = (64 * 8) ** -0.5
C_SQ, C_SK, C_SV = 0, 384, 768
C_DQ, C_DK, C_DV = 1152, 1536, 1600
C_IQ, C_IK, C_IW = 1664, 2176, 2240
C_HQ, C_HF, C_HI, C_HG = 2248, 2760, 3272, 3528
C_G = 3784
N_BISECT = 14


class Buf:
    __slots__ = ("name", "w", "r", "dsem", "excl")

    def __init__(self, name, excl=False):
        self.name = name
        self.w = None
        self.r = {}
        self.dsem = None
        self.excl = excl


class Prog:
    ENG = ("pe", "act", "dve", "pool", "sp")
    CLEAR_NS = 330.0
    FILL_NS = {"dve": 66.0, "act": 190.0, "pool": 125.0}
    EST = {"dve": (60.0, 0.26), "act": (185.0, 0.83), "pool": (120.0, 0.8), "pe": (0.0, 0.0), "sp": (0.0, 0.0)}

    def __init__(self, nc, stack, needed=None):
        self.needed = needed
        self.used = set()
        self.remap = {}
        self.sig = {}
        self.fill = {}
        self.nc = nc
        self.stack = stack
        self.eng = {"pe": nc.tensor, "act": nc.scalar, "dve": nc.vector, "pool": nc.gpsimd, "sp": nc.sync}
        self.cnt = {e: 0 for e in self.ENG}
        self.known = {e: {} for e in self.ENG}
        self.sems = {}
        self.semval = {}
        for e in ("pe", "act", "dve", "pool"):
            self.sems["E_" + e] = stack.enter_context(nc.semaphore("sem_" + e))
            self.semval["E_" + e] = 0
        self.ndsem = 0
        self.free_dsems = []
        self.nwaits = 0
        self.tcum = {e: 0.0 for e in self.ENG}
        self.tend = {e: {} for e in self.ENG}

    def _dsem(self, buf):
        if buf.dsem is None:
            if self.free_dsems:
                key = self.free_dsems.pop()
            else:
                key = "D%d" % self.ndsem
                self.ndsem += 1
                self.sems[key] = self.stack.enter_context(self.nc.semaphore("dsem%d" % (self.ndsem - 1)))
                self.semval[key] = 0
            buf.dsem = key
        return buf.dsem

    def release(self, bufs):
        for b in bufs:
            if b.dsem is not None:
                self.free_dsems.append(b.dsem)
                b.dsem = None

    def _waits(self, eng, deps):
        need = {}
        own = "E_" + eng
        for (k, v) in deps:
            if eng == "pe" and k == "E_pe":
                continue
            if k == own and eng in ("act", "dve", "pool"):
                te = self.tend[eng].get(v)
                if te is not None and eng in self.fill:
                    gap = self.CLEAR_NS - (self.tcum[eng] - te)
                    if gap > 0:
                        n = int(math.ceil(gap / self.FILL_NS[eng]))
                        for _ in range(n):
                            self.fill[eng](self.eng[eng])
                        self.tcum[eng] += n * self.FILL_NS[eng]
                        self.nfill = getattr(self, "nfill", 0) + n
                continue
            if v > need.get(k, 0):
                need[k] = v
        out = []
        kn = self.known[eng]
        for k, v in need.items():
            if kn.get(k, 0) < v:
                kn[k] = v
                out.append((k, v))
        return out

    @staticmethod
    def _deps(reads, writes):
        deps = []
        for b in reads:
            if b.w is not None:
                deps.append(b.w)
            if b.excl:
                deps.extend(b.r.items())
        for b in writes:
            if b.w is not None:
                deps.append(b.w)
            deps.extend(b.r.items())
        return deps

    def _emit_waits(self, eng, waits):
        e = self.eng[eng]
        for (k, v) in waits:
            if k.startswith("E_"):
                self.used.add((k, v))
                if self.needed is not None:
                    v = self.remap[(k, v)]
            e.wait_ge(self.sems[k], v)
            self.nwaits += 1

    def _mark(self, ev, reads, writes):
        k, v = ev
        for b in reads:
            if b.r.get(k, 0) < v:
                b.r[k] = v
        for b in writes:
            b.w = ev
            b.r = {}

    def op(self, eng, fn, reads=(), writes=(), n=0):
        self.group(eng, [fn], reads, writes, n)

    def group(self, eng, fns, reads=(), writes=(), n=0):
        self._emit_waits(eng, self._waits(eng, self._deps(reads, writes)))
        e = self.eng[eng]
        for fn in fns[:-1]:
            fn(e)
        self.cnt[eng] += 1
        ov, pe_ = self.EST[eng]
        self.tcum[eng] += ov + pe_ * n
        td = self.tend[eng]
        td[self.cnt[eng]] = self.tcum[eng]
        if len(td) > 64:
            for k_ in sorted(td)[:32]:
                del td[k_]
        key = "E_" + eng
        self.semval[key] = self.cnt[eng]
        if self.needed is None or (key, self.cnt[eng]) in self.needed:
            self.sig[key] = self.sig.get(key, 0) + 1
            self.remap[(key, self.cnt[eng])] = self.sig[key]
            fns[-1](e).then_inc(self.sems[key], 1)
        else:
            fns[-1](e)
        self._mark((key, self.cnt[eng]), reads, writes)

    def dma(self, eng, fn, reads=(), writes=()):
        assert len(writes) == 1
        wb = writes[0]
        deps = self._deps(reads, writes)
        if eng == "pool" and getattr(self, "last_swdge", None) is not None:
            deps.append(self.last_swdge)
        self._emit_waits(eng, self._waits(eng, deps))
        key = self._dsem(wb)
        self.semval[key] += 16
        fn(self.eng[eng]).then_inc(self.sems[key], 16)
        if eng == "pool":
            self.last_swdge = (key, self.semval[key])
        self._mark((key, self.semval[key]), reads, writes)

    def barrier(self):
        deps = [(k, v) for k, v in self.semval.items() if v > 0]
        for eng in self.ENG:
            self._emit_waits(eng, self._waits(eng, deps))

    def wait_bufs(self, eng, bufs):
        deps = []
        for b in bufs:
            if b.w is not None:
                deps.append(b.w)
            deps.extend(b.r.items())
        self._emit_waits(eng, self._waits(eng, deps))


class G:
    pass


def bufs(prefix, *dims):
    if len(dims) == 1:
        return [Buf("%s%d" % (prefix, i)) for i in range(dims[0])]
    return [bufs("%s%d_" % (prefix, i), *dims[1:]) for i in range(dims[0])]


def build_program(stage=99, dump=None, needed=None):
    nc = bass.Bass("TRN2", target_bir_lowering=False)
    g = G()
    g.nc = nc
    g.stage = stage
    g.dump = dump
    import os
    g.ntl = int(os.environ.get("NTL", "9"))
    g.dbg_d = None
    if dump is not None:
        g.dbg_d = nc.dram_tensor("dbg", [128, 8 * T], BF16, kind="ExternalOutput").ap()
        g.dbgb = Buf("dbg")
    dt = lambda name, shape, kind, d=F32: nc.dram_tensor(name, shape, d, kind=kind).ap()
    g.x_d = dt("x", [T, D], "ExternalInput")
    g.norm_mix = dt("norm_mix", [DEPTH, D], "ExternalInput")
    g.w_in = dt("w_in", [DEPTH, D, DIN], "ExternalInput")
    g.qn = dt("qn_dsa", [DEPTH, 64], "ExternalInput")
    g.kn = dt("kn_dsa", [DEPTH, 64], "ExternalInput")
    g.lb_d = dt("hgrn_lb", [DEPTH, 512], "ExternalInput")
    g.onorm = dt("hgrn_onorm", [DEPTH, 64], "ExternalInput")
    g.w_sb = dt("w_br_sb", [DEPTH, 384, D], "ExternalInput")
    g.w_dsa = dt("w_br_dsa", [DEPTH, 384, D], "ExternalInput")
    g.w_hg = dt("w_br_hgrn", [DEPTH, 256, D], "ExternalInput")
    g.w_out = dt("w_out", [DEPTH, D, D], "ExternalInput")
    g.norm_mlp = dt("norm_mlp", [DEPTH, D], "ExternalInput")
    g.w_up = dt("w_up", [DEPTH, D, DFF], "ExternalInput")
    g.w_down = dt("w_down", [DEPTH, DFF, D], "ExternalInput")
    g.cs_d = dt("rope_cos", [T, 8], "ExternalInput")
    g.sn_d = dt("rope_sin", [T, 8], "ExternalInput")
    g.out_d = dt("out", [T, D], "ExternalOutput")
    g.xres_d = dt("xres", [T, D], "Internal")
    g.xin_b = bufs("xin", NB)
    g.xres_b = bufs("xres", NB)
    g.out_b = bufs("outb", NB)

    with ExitStack() as gs:
        P = Prog(nc, gs, needed)
        g.P = P
        fa = gs.enter_context(nc.sbuf_tensor("fill_a", [128, 2], F32))
        fd = gs.enter_context(nc.sbuf_tensor("fill_d", [128, 2], F32))
        nc.vector.memset(fd[:], 0.0)
        nc.vector.memset(fa[:], 0.0)
        P.fill["dve"] = lambda e: e.memset(fd[:, 0:1], 0.0)
        P.fill["act"] = lambda e: e.activation(out=fa[:, 0:1], in_=fa[:, 1:2], func=AF.Copy)
        fp = gs.enter_context(nc.sbuf_tensor("fill_p", [128, 2], F32))
        nc.gpsimd.memset(fp[:], 0.0)
        P.fill["pool"] = lambda e: e.memset(fp[:, 0:1], 0.0)
        g.uid = 0
        g.psF = [gs.enter_context(nc.psum_tensor("psF%d" % i, [128, 512], F32)) for i in range(6)]
        g.psFb = [Buf("psF%d" % i, excl=True) for i in range(6)]
        g.psB = [gs.enter_context(nc.psum_tensor("psB%d" % i, [128, 1024], BF16)) for i in range(2)]
        g.psBb = [Buf("psB%d" % i, excl=True) for i in range(2)]
        g.rotc = {}
        build_consts(g, gs)
        g.hT = gs.enter_context(nc.sbuf_tensor("hT_glob", [128, DC, T], BF16))
        g.hTb = bufs("hT", NB)
        g.pre = gs.enter_context(nc.sbuf_tensor("pre_w", [128, DC, 512], BF16))
        g.preb = Buf("pre_w")
        prefetch(g, ("sq", 0), win_cols(g, 0, C_SQ, C_SQ + 384), 384)
        for l in range(DEPTH):
            if g.stage >= 1:
                build_layer(g, l)
        if g.dbg_d is not None:
            P.wait_bufs("sp", [g.dbgb])
        P.wait_bufs("sp", g.out_b)
        P.barrier()
        g.used = P.used
        print("ops", P.cnt, "signals", P.sig, "waits", P.nwaits, "fillers", getattr(P, "nfill", 0), "dsems", P.ndsem, flush=True)
    return nc, P.used


def build_two_pass(stage=99, dump=None):
    _, used = build_program(stage, dump, None)
    nc, _ = build_program(stage, dump, used)
    return nc


def pipeline(stages, ntiles):
    ns = len(stages)
    for t in range(ntiles + ns - 1):
        for k, f in enumerate(stages):
            i = t - k
            if 0 <= i < ntiles:
                f(i)


def prefetch(g, tag, src, ncols):
    DMA(g, "pool", g.pre[:, :, 0:ncols], src, (), [g.preb])
    g.pre_tag = tag


def take_pre(g, tag):
    if getattr(g, "pre_tag", None) == tag:
        g.pre_tag = None
        return True
    return False


def rot(g, role, items):
    i = g.rotc.get(role, 0)
    g.rotc[role] = i + 1
    return items[i % len(items)]


def psf(g, role, banks):
    b = rot(g, role, banks)
    return g.psF[b], g.psFb[b]


def psb(g, role="pb"):
    b = rot(g, role, [0, 1])
    return g.psB[b], g.psBb[b]


@contextmanager
def scope(g):
    st = ExitStack()
    st.tbufs = []
    try:
        yield st
    finally:
        g.P.barrier()
        g.P.release(st.tbufs)
        st.close()


def sb(g, st, shape, dtype, name=None):
    g.uid += 1
    return st.enter_context(g.nc.sbuf_tensor("%s_%d" % (name or "t", g.uid), shape, dtype))


def nb(st, name):
    b = Buf(name)
    st.tbufs.append(b)
    return b


def nbs(st, prefix, *dims):
    r = bufs(prefix, *dims)

    def flat(x):
        if isinstance(x, Buf):
            st.tbufs.append(x)
        else:
            for y in x:
                flat(y)
    flat(r)
    return r


def _fs(ap):
    try:
        return int(ap.free_size())
    except Exception:
        return 0


def ACT(g, out, in_, func, reads, writes, **kw):
    g.P.op("act", lambda e: e.activation(out=out, in_=in_, func=func, **kw), reads, writes, _fs(out))


def TT(g, eng, out, in0, in1, op, reads, writes):
    g.P.op(eng, lambda e: e.tensor_tensor(out=out, in0=in0, in1=in1, op=op), reads, writes, _fs(out))


def TS(g, eng, out, in0, s1, s2, op0, op1, reads, writes, **kw):
    if op1 is None:
        s2 = 0.0 if isinstance(s1, (int, float)) else g.zc[0:in0.shape[0], 0:1]
        g.P.op(eng, lambda e: e.tensor_scalar(out=out, in0=in0, scalar1=s1, scalar2=s2, op0=op0, op1=ALU.add, **kw), reads, writes, _fs(out))
    else:
        g.P.op(eng, lambda e: e.tensor_scalar(out=out, in0=in0, scalar1=s1, scalar2=s2, op0=op0, op1=op1, **kw), reads, writes, _fs(out))


def STT(g, out, in0, scalar, in1, op0, op1, reads, writes):
    g.P.op("dve", lambda e: e.scalar_tensor_tensor(out=out, in0=in0, scalar=scalar, in1=in1, op0=op0, op1=op1), reads, writes, _fs(out))


def CP(g, eng, out, in_, reads, writes):
    if eng == "act":
        g.P.op("act", lambda e: e.activation(out=out, in_=in_, func=AF.Copy), reads, writes, _fs(out))
    else:
        g.P.op(eng, lambda e: e.tensor_copy(out, in_), reads, writes, _fs(out))


def MS(g, eng, ap, val, writes):
    g.P.op(eng, lambda e: e.memset(ap, val), (), writes, _fs(ap))


def ASEL(g, out, in_, pattern, cmp, fill, base, cm, reads, writes):
    g.P.op("pool", lambda e: e.affine_select(out=out, in_=in_, pattern=pattern, compare_op=cmp, fill=fill, base=base,
                                             channel_multiplier=cm), reads, writes, _fs(out))


def MM(g, outs_fns, reads, writes):
    g.P.group("pe", outs_fns, reads, writes)


def mmf(out, lhsT, rhs, start, stop):
    return lambda e: e.matmul(out, lhsT=lhsT, rhs=rhs, start=start, stop=stop)


def trf(out, in_, ident):
    return lambda e: e.transpose(out, in_, ident)


def DMA(g, eng, out, in_, reads, writes, **kw):
    g.P.dma(eng, lambda e: e.dma_start(out=out, in_=in_, **kw), reads, writes)


def build_consts(g, gs):
    nc = g.nc
    mk = lambda name, shape, d: gs.enter_context(nc.sbuf_tensor(name, shape, d))
    g.ident = mk("ident", [128, 128], BF16)
    g.negtri = mk("negtri", [128, 128], BF16)
    g.negones = mk("negones", [128, 128], BF16)
    g.onesb = mk("onesb", [128, 128], BF16)
    g.maskbd = mk("maskbd", [128, 128], F32)
    g.onesf = mk("onesf", [128, 128], F32)
    g.resetm = mk("resetm", [128, T], BF16)
    g.zc = mk("zc", [128, 1], F32)
    g.negbig = mk("negbig", [128, 1], F32)
    g.cs = mk("cs", [128, NB, 8], F32)
    g.sn = mk("sn", [128, NB, 8], F32)
    g.lbraw = mk("lbraw", [128, 2, 4], F32)
    g.lbv = mk("lbv", [128, 2, 4], F32)
    g.oml = mk("oml", [128, 2, 4], F32)
    g.cb = Buf("consts")
    g.csb = Buf("cs")
    g.snb = Buf("sn")
    g.lbb = Buf("lbraw")
    cb = [g.cb]
    MS(g, "pool", g.onesb[:], 1.0, cb)
    MS(g, "pool", g.negones[:], -1.0, cb)
    MS(g, "pool", g.onesf[:], 1.0, cb)
    MS(g, "pool", g.zc[:], 0.0, cb)
    MS(g, "pool", g.negbig[:], -1e29, cb)
    ASEL(g, g.ident[:], g.onesb[:], [[1, 128]], ALU.is_equal, 0.0, 0, -1, cb, cb)
    ASEL(g, g.negtri[:], g.negones[:], [[-1, 128]], ALU.is_ge, 0.0, 0, 1, cb, cb)
    ASEL(g, g.maskbd[:], g.onesf[:], [[1, 128]], ALU.is_ge, 0.0, 0, -1, cb, cb)
    MS(g, "pool", g.maskbd[0:64, 64:128], 0.0, cb)
    g.ones512 = mk("ones512", [128, 512], BF16)
    g.mlt = mk("mlt", [128, 512], BF16)
    MS(g, "pool", g.ones512[:], 1.0, cb)
    ASEL(g, g.mlt[:], g.ones512[:], [[1, 512]], ALU.is_gt, 0.0, 0, -1, cb, cb)
    g.caus01 = mk("caus01", [128, 128], F32)
    g.negfill = mk("negfill", [128, 128], F32)
    ASEL(g, g.caus01[:], g.onesf[:], [[-1, 128]], ALU.is_ge, 0.0, 0, 1, cb, cb)
    TS(g, "pool", g.negfill[:], g.caus01[:], -1.0, 1e30, ALU.add, ALU.mult, cb, cb)
    MS(g, "pool", g.resetm[:], 1.0, cb)
    MS(g, "pool", g.resetm[:].rearrange("p (c j) -> p c j", j=64)[:, :, 0:1], 0.0, cb)
    DMA(g, "sp", g.cs[:], g.cs_d.rearrange("(b p) i -> p b i", p=128), (), [g.csb])
    DMA(g, "sp", g.sn[:], g.sn_d.rearrange("(b p) i -> p b i", p=128), (), [g.snb])
    DMA(g, "sp", g.lbraw[:], g.lb_d.rearrange("l (h k) -> k l h", k=128), (), [g.lbb], allow_slow_non_contiguous=True)
    MS(g, "dve", g.lbv[:], 0.0, cb)
    TT(g, "dve", g.lbv[:, 1, :], g.lbraw[:, 1, :], g.lbraw[:, 0, :], ALU.subtract, [g.lbb], cb)
    ACT(g, g.lbv[:, 1, :], g.lbv[:, 1, :], AF.Sigmoid, cb, cb)
    TS(g, "dve", g.oml[:], g.lbv[:], -1.0, 1.0, ALU.mult, ALU.add, cb, cb)
    g.P.barrier()


class NormCtx:
    pass


def norm_setup(g, st, gain_d_row):
    c = NormCtx()
    c.gain = sb(g, st, [128, D], F32, "gain")
    c.gb = nb(st, "gain")
    DMA(g, "sp", c.gain[:], gain_d_row.to_broadcast([128, D]), (), [c.gb])
    c.junk = sb(g, st, [128, D], BF16, "junk")
    c.jb = nb(st, "junk")
    c.hbs = [sb(g, st, [128, D], BF16, "hb") for _ in range(2)]
    c.hbb = nbs(st, "hb", 2)
    c.ss = sb(g, st, [128, NB], F32, "ss")
    c.ssb = nbs(st, "ss", NB)
    MS(g, "dve", c.ss[:], 0.0, c.ssb)
    return c


def norm_block(g, c, xb, xbuf, tb, hT, hTb):
    hb, hbuf = c.hbs[tb % 2], c.hbb[tb % 2]
    s1 = c.ss[:, tb:tb + 1]
    ACT(g, c.junk[:], xb[:], AF.Square, [xbuf, c.ssb[tb]], [c.jb, c.ssb[tb]], accum_out=s1)
    ACT(g, s1, s1, AF.Ln, [c.ssb[tb]], [c.ssb[tb]], scale=1.0 / D, bias=EPS)
    ACT(g, s1, s1, AF.Exp, [c.ssb[tb]], [c.ssb[tb]], scale=-0.5)
    STT(g, hb[:], xb[:], s1, c.gain[:], ALU.mult, ALU.mult, [xbuf, c.ssb[tb], c.gb], [hbuf])
    for half in range(2):
        pt, ptb = psb(g)
        MM(g, [trf(pt[:, m * 128:(m + 1) * 128], hb[:, (half * 4 + m) * 128:(half * 4 + m + 1) * 128], g.ident[:])
               for m in range(4)], [hbuf, g.cb], [ptb])
        CP(g, "act" if half == 0 else "dve", hT[:, half * 4:half * 4 + 4, tb * 128:(tb + 1) * 128],
           pt[:, 0:512].rearrange("p (m j) -> p m j", j=128), [ptb], [hTb[tb]])


def norm_T(g, st, src_ap, src_bufs, gain_d_row, hT, hTb):
    c = norm_setup(g, st, gain_d_row)
    xbs = [sb(g, st, [128, D], F32, "xb") for _ in range(4)]
    xbb = nbs(st, "xb", 4)
    for tb in range(NB):
        xb, xbuf = xbs[tb % 4], xbb[tb % 4]
        DMA(g, "sp", xb[:], src_ap[tb * 128:(tb + 1) * 128, :], [src_bufs[tb]], [xbuf])
        norm_block(g, c, xb, xbuf, tb, hT, hTb)


def win_cols(g, l, c0, c1):
    return g.w_in[l].rearrange("(c p) n -> p c n", p=128)[:, :, c0:c1]


def mm_fm(g, ps, M, n, w, wb, col0, hT, hTb, tok0, role_bufs):
    MM(g, [mmf(ps[0:M, 0:n], w[:, c, col0:col0 + M], hT[:, c, tok0:tok0 + n], c == 0, c == DC - 1) for c in range(DC)],
       [wb] + hTb[tok0 // 128:(tok0 + n + 127) // 128], [role_bufs])


def mm_tm(g, ps, N, w, wb, col0, hT, hTb, tb, psbuf):
    MM(g, [mmf(ps[:, 0:N], hT[:, c, tb * 128:(tb + 1) * 128], w[:, c, col0:col0 + N], c == 0, c == DC - 1) for c in range(DC)],
       [wb, hTb[tb]], [psbuf])


def phase_sb(g, l, hT, hTb, osbT, osbb):
    with scope(g) as st:
        ws = []
        for i, c0 in enumerate((C_SQ, C_SK, C_SV)):
            if i == 0 and take_pre(g, ("sq", l)):
                ws.append((g.pre, g.preb))
                continue
            w = sb(g, st, [128, DC, 384], BF16, "wsb")
            wb = nb(st, "wsb%d" % i)
            DMA(g, "pool", w[:], win_cols(g, l, c0, c0 + 384), (), [wb])
            ws.append((w, wb))
        sqT = sb(g, st, [128, 3, T], BF16, "sqT")
        skT = sb(g, st, [128, 3, T], BF16, "skT")
        sqb = nbs(st, "sq", 3, 4)
        skb = nbs(st, "sk", 3, 4)
        k = 0
        for (dst, dstb, (w, wb), scl) in ((sqT, sqb, ws[0], 0.125), (skT, skb, ws[1], 1.0)):
            for hp in range(3):
                for tc in range(4):
                    ps, pb = psf(g, "proj", [0, 1, 2, 3, 4, 5])
                    mm_fm(g, ps, 128, 512, w, wb, hp * 128, hT, hTb, tc * 512, pb)
                    o = dst[:, hp, tc * 512:(tc + 1) * 512]
                    if k % 2 == 0:
                        ACT(g, o, ps[:, :], AF.Copy, [pb], [dstb[hp][tc]], scale=scl)
                    else:
                        TS(g, "dve", o, ps[:, :], scl, None, ALU.mult, None, [pb], [dstb[hp][tc]])
                    k += 1
        svp = [sb(g, st, [128, NB, 384], BF16, "svp") for _ in range(2)]
        svb = nbs(st, "sv", 2, NB)
        for s_ in range(2):
            MS(g, "pool", svp[s_][:].rearrange("p t c -> p (t c)"), 0.0, svb[s_])
        for tb in range(NB):
            ps, pb = psf(g, "proj", [0, 1, 2, 3, 4, 5])
            mm_tm(g, ps, 384, ws[2][0], ws[2][1], 0, hT, hTb, tb, pb)
            src = ps[:, 0:384].rearrange("p (m s d) -> p m s d", s=2, d=64)
            for s_ in range(2):
                dst = svp[s_][:, tb, :].rearrange("p (m s d) -> p m s d", s=2, d=64)
                CP(g, "act" if s_ == 0 else "dve", dst[:, :, s_, :], src[:, :, s_, :], [pb], [svb[s_][tb]])
        prefetch(g, ("wA", l), win_cols(g, l, C_DQ, C_DQ + 512), 512)
        R = 4
        mk2 = lambda shape, dt_, nm: [[sb(g, st, shape, dt_, nm) for _ in range(R)] for _ in range(2)]
        Et, SPt, SPs, At = mk2([128, 512], F32, "Et"), mk2([128, 512], BF16, "SPt"), mk2([128, 512], BF16, "SPs"), mk2([128, 512], BF16, "At")
        Etb, SPb, SPsb, Atb = nbs(st, "Et", 2, R), nbs(st, "SPt", 2, R), nbs(st, "SPs", 2, R), nbs(st, "At", 2, R)
        steps = []
        for m in range(3):
            for qc in range(4):
                for n_, kb in enumerate(range(4 * qc + 3, -1, -1)):
                    steps.append((m, qc, kb, n_))
        stt = {}
        pso = {}

        def info(t):
            m, qc, kb, n_ = steps[t]
            j0 = max(0, kb * 128 - qc * 512)
            return m, qc, kb, n_, j0, kb >= 4 * qc, qc * 512 + j0 - kb * 128, n_ == 0

        def opnds(t):
            m, qc, kb, n_, j0, diag, base, first = info(t)
            kk = [skT[64 * s_:64 * s_ + 64, m, kb * 128:(kb + 1) * 128] for s_ in range(2)]
            qq = [sqT[64 * s_:64 * s_ + 64, m, qc * 512 + j0:(qc + 1) * 512] for s_ in range(2)]
            return kk, qq, skb[m][kb // 4], sqb[m][qc]

        def s1(t):
            m, qc, kb, n_, j0, diag, base, first = info(t)
            kk, qq, rk, rq = opnds(t)
            pz = [psf(g, "sbZ", [0, 1, 2]) for _ in range(2)]
            stt[t] = {"pz": pz}
            for s_ in range(2):
                MM(g, [mmf(pz[s_][0][:, j0:512], kk[s_], qq[s_], True, True)], [rk, rq], [pz[s_][1]])

        def s2(t):
            m, qc, kb, n_, j0, diag, base, first = info(t)
            ib = t % R
            pz = stt[t]["pz"]
            for s_ in range(2):
                ACT(g, Et[s_][ib][:, j0:512], pz[s_][0][:, j0:512], AF.Exp, [pz[s_][1]], [Etb[s_][ib]])

        def s3(t):
            m, qc, kb, n_, j0, diag, base, first = info(t)
            ib = t % R
            for s_ in range(2):
                ACT(g, SPt[s_][ib][:, j0:512], Et[s_][ib][:, j0:512], AF.Ln, [Etb[s_][ib]], [SPb[s_][ib]], bias=1.0)
            if diag:
                for s_ in range(2):
                    S = SPt[s_][ib]
                    assert base == 0
                    TT(g, "pool", S[:, j0:512], S[:, j0:512], g.mlt[:, 0:512 - j0], ALU.mult, [SPb[s_][ib], g.cb], [SPb[s_][ib]])
            if kb > 0:
                for s_ in range(2):
                    S, Sb = SPt[s_][ib], SPb[s_][ib]
                    Sn, Snb = SPs[s_][n_ % R], SPsb[s_][n_ % R]
                    if first:
                        if j0 > 0:
                            MS(g, "pool", Sn[:, 0:j0], 0.0, [Snb])
                        CP(g, "pool", Sn[:, j0:512], S[:, j0:512], [Sb], [Snb])
                    else:
                        So, Sob = SPs[s_][(n_ - 1) % R], SPsb[s_][(n_ - 1) % R]
                        if j0 > 0:
                            CP(g, "pool", Sn[:, 0:j0], So[:, 0:j0], [Sob], [Snb])
                        TT(g, "dve", Sn[:, j0:512], So[:, j0:512], S[:, j0:512], ALU.add, [Sob, Sb], [Snb])

        def s4(t):
            m, qc, kb, n_, j0, diag, base, first = info(t)
            ib = t % R
            kk, qq, rk, rq = opnds(t)
            pc = [psf(g, "sbC", [3, 4]) for _ in range(2)]
            stt[t]["pc"] = pc
            for s_ in range(2):
                S = SPt[s_][ib]
                fns = [mmf(pc[s_][0][:, j0:512], kk[s_], qq[s_], True, False),
                       mmf(pc[s_][0][:, j0:512], g.negtri[:], S[:, j0:512], False, first)]
                rd = [rk, rq, SPb[s_][ib], g.cb]
                if not first:
                    So, Sob = SPs[s_][(n_ - 1) % R], SPsb[s_][(n_ - 1) % R]
                    fns.append(mmf(pc[s_][0][:, j0:512], g.negones[:], So[:, j0:512], False, True))
                    rd.append(Sob)
                MM(g, fns, rd, [pc[s_][1]])

        def s5(t):
            m, qc, kb, n_, j0, diag, base, first = info(t)
            ib = t % R
            pc = stt[t]["pc"]
            for s_ in range(2):
                ACT(g, At[s_][ib][:, j0:512], pc[s_][0][:, j0:512], AF.Exp, [pc[s_][1]], [Atb[s_][ib]])
            for s_ in range(2):
                A, Ab = At[s_][ib], Atb[s_][ib]
                if diag:
                    TT(g, "pool", A[:, j0:512], A[:, j0:512], g.mlt[:, 0:512 - j0], ALU.mult, [Ab, g.cb], [Ab])
                if first and j0 > 0:
                    MS(g, "pool", A[:, 0:j0], 0.0, [Ab])

        def s6(t):
            m, qc, kb, n_, j0, diag, base, first = info(t)
            ib = t % R
            if first:
                pso[(m, qc)] = psf(g, "sbO", [5])
            psO, pOb = pso[(m, qc)]
            for s_ in range(2):
                A, Ab = At[s_][ib], Atb[s_][ib]
                vv = svp[s_][:, kb, m * 128:(m + 1) * 128]
                c0 = 0 if first else j0
                MM(g, [mmf(psO[:, c0:512], vv, A[:, c0:512], first and s_ == 0, kb == 0 and s_ == 1)], [svb[s_][kb], Ab], [pOb])
            if kb == 0:
                CP(g, "act" if (m * 4 + qc) % 2 == 0 else "dve", osbT[:, m, qc * 512:(qc + 1) * 512], psO[:, :], [pOb], [osbb[m][qc]])
            del stt[t]

        nst = len(steps)
        stages = [s1, s2, s3, s4, s5, s6]
        for e in range(nst + len(stages) - 1):
            for k in range(len(stages) - 1, -1, -1):
                t = e - k
                if 0 <= t < nst:
                    stages[k](t)


def phase_dsa(g, l, hT, hTb, odT, odb):
    P = g.P
    with scope(g) as st:
        featT = sb(g, st, [128, 9, T], BF16, "featT")
        fb = nbs(st, "feat", NB)
        dvx = sb(g, st, [128, NB, 128], BF16, "dvx")
        dvb = nbs(st, "dvx", NB)
        sgn = sb(g, st, [128, NB, 8], F32, "sgn")
        sgb = nbs(st, "sgn", NB)
        qkg = sb(g, st, [128, 7, 64], F32, "qkg")
        qkgb = nbs(st, "qkg", 7)
        for hh in range(7):
            src = (g.qn if hh < 6 else g.kn)[l:l + 1, :].to_broadcast([128, 64])
            DMA(g, "sp", qkg[:, hh, :], src, (), [qkgb[hh]])
        with scope(g) as s2:
            wB = sb(g, s2, [128, DC, 512], BF16, "wB")
            wC = sb(g, s2, [128, DC, 72], BF16, "wC")
            wBb, wCb = nb(s2, "wB"), nb(s2, "wC")
            if take_pre(g, ("wA", l)):
                wA, wAb = g.pre, g.preb
            else:
                wA = sb(g, s2, [128, DC, 512], BF16, "wA")
                wAb = nb(s2, "wA")
                DMA(g, "pool", wA[:], win_cols(g, l, C_DQ, C_DQ + 512), (), [wAb])
            DMA(g, "pool", wB[:], win_cols(g, l, C_IQ, C_IQ + 512), (), [wBb])
            DMA(g, "pool", wC[:], win_cols(g, l, C_IK, C_IK + 72), (), [wCb])
            NR = 4
            tq = [sb(g, s2, [128, 18, 64], F32, "tokq") for _ in range(NR)]
            tqb = nbs(s2, "tokq", NR)
            tbq = [sb(g, s2, [128, 18, 64], BF16, "tokb") for _ in range(3)]
            tbb = nbs(s2, "tokb", 3)
            sqt = [sb(g, s2, [128, 448], F32, "sqt") for _ in range(2)]
            sqtb = nbs(s2, "sqt", 2)
            smq = [sb(g, s2, [128, 32], F32, "small") for _ in range(2)]
            smqb = nbs(s2, "small", 2)
            rt = [sb(g, s2, [128, 18, 8], F32, "ropet") for _ in range(4)]
            rtb = nbs(s2, "ropet", 4)
            pst = {}

            def p1(tb):
                MS(g, "pool", dvx[:, tb, 64:128], 1.0, [dvb[tb]])
                tk, tkb = tq[tb % NR], tqb[tb % NR]
                MS(g, "pool", tk[:, 7, :], 0.0, [tkb])
                MS(g, "pool", tk[:, 17, :], 0.0, [tkb])
                psA, pAb = psf(g, "dA", [0, 1])
                psBq, pBb = psf(g, "dB", [2, 3])
                psC, pCb = psf(g, "dC", [4, 5])
                pst[tb] = (psA, pAb, psBq, pBb, psC, pCb)
                mm_tm(g, psA, 512, wA, wAb, 0, hT, hTb, tb, pAb)
                mm_tm(g, psBq, 512, wB, wBb, 0, hT, hTb, tb, pBb)
                mm_tm(g, psC, 72, wC, wCb, 0, hT, hTb, tb, pCb)

            def p2(tb):
                psA, pAb, psBq, pBb, psC, pCb = pst.pop(tb)
                tk, tkb = tq[tb % NR], tqb[tb % NR]
                sq_, sq_b = sqt[tb % 2], sqtb[tb % 2]
                sm, smb = smq[tb % 2], smqb[tb % 2]
                ACT(g, sq_[:], psA[:, 0:448], AF.Square, [pAb], [sq_b])
                ss = sm[:, 0:7]
                P.op("dve", lambda e, ss=ss, sq_=sq_: e.tensor_reduce(out=ss, in_=sq_[:].rearrange("p (h d) -> p h d", d=64), axis=AX.X,
                                                                      op=ALU.add), [sq_b], [smb], 448)
                ACT(g, ss, ss, AF.Ln, [smb], [smb], scale=1.0 / 64, bias=EPS)
                ACT(g, ss, ss, AF.Exp, [smb], [smb], scale=-0.5)
                aw = sm[:, 8:16]
                TS(g, "dve", sgn[:, tb, :], psC[:, 64:72], 0.0, 2.0, ALU.is_gt, ALU.mult, [pCb], [sgb[tb]])
                TS(g, "dve", sgn[:, tb, :], sgn[:, tb, :], -1.0, 0.0, ALU.add, ALU.add, [sgb[tb]], [sgb[tb]])
                STT(g, aw, psC[:, 64:72], IDX_SCALE, sgn[:, tb, :], ALU.mult, ALU.mult, [pCb, sgb[tb]], [smb])
                TT(g, "dve", tk[:, 8:16, :], psBq[:, :].rearrange("p (h d) -> p h d", d=64),
                   aw.unsqueeze(2).to_broadcast([128, 8, 64]), ALU.mult, [pBb, smb], [tkb])
                CP(g, "act", tk[:, 16, :], psC[:, 0:64], [pCb], [tkb])
                CP(g, "act", dvx[:, tb, 0:64], psA[:, 448:512], [pAb], [dvb[tb]])
                TT(g, "dve", tk[:, 0:7, :], psA[:, 0:448].rearrange("p (h d) -> p h d", d=64),
                   ss.unsqueeze(2).to_broadcast([128, 7, 64]), ALU.mult, [pAb, smb], [tkb])
                TT(g, "dve", tk[:, 0:7, :], tk[:, 0:7, :], qkg[:], ALU.mult, [tkb] + qkgb, [tkb])

            def p3(tb):
                tk, tkb = tq[tb % NR], tqb[tb % NR]
                x1, x2 = tk[:, :, 0:8], tk[:, :, 8:16]
                cb_ = g.cs[:, tb, :].unsqueeze(1).to_broadcast([128, 18, 8])
                sb_ = g.sn[:, tb, :].unsqueeze(1).to_broadcast([128, 18, 8])
                TT(g, "dve", rt[0][:], x1, cb_, ALU.mult, [tkb, g.csb], [rtb[0]])
                TT(g, "pool", rt[1][:], x2, sb_, ALU.mult, [tkb, g.snb], [rtb[1]])
                TT(g, "dve", rt[2][:], x2, cb_, ALU.mult, [tkb, g.csb], [rtb[2]])
                TT(g, "pool", rt[3][:], x1, sb_, ALU.mult, [tkb, g.snb], [rtb[3]])
                TT(g, "dve", x1, rt[0][:], rt[1][:], ALU.subtract, [rtb[0], rtb[1]], [tkb])
                TT(g, "pool", x2, rt[2][:], rt[3][:], ALU.add, [rtb[2], rtb[3]], [tkb])

            def p4(tb):
                tk, tkb = tq[tb % NR], tqb[tb % NR]
                tkh, tkhb = tbq[tb % 3], tbb[tb % 3]
                CP(g, "act", tkh[:], tk[:], [tkb], [tkhb])
                CP(g, "pool", tkh[:, 7, :], tkh[:, 6, :], [tkhb], [tkhb])
                CP(g, "pool", tkh[:, 17, :], tkh[:, 16, :], [tkhb], [tkhb])

            def p5(tb):
                tkh, tkhb = tbq[tb % 3], tbb[tb % 3]
                flat = tkh[:].rearrange("p h d -> p (h d)")
                pt, ptb = psb(g)
                MM(g, [trf(pt[:, m * 128:(m + 1) * 128], flat[:, m * 128:(m + 1) * 128], g.ident[:]) for m in range(8)],
                   [tkhb, g.cb], [ptb])
                CP(g, "dve", featT[:, 0:8, tb * 128:(tb + 1) * 128], pt[:, :].rearrange("p (m j) -> p m j", j=128), [ptb], [fb[tb]])
                pt2, ptb2 = psb(g)
                MM(g, [trf(pt2[:, 0:128], flat[:, 1024:1152], g.ident[:])], [tkhb, g.cb], [ptb2])
                CP(g, "act", featT[:, 8, tb * 128:(tb + 1) * 128], pt2[:, 0:128], [ptb2], [fb[tb]])

            stages = [p1, p2, p3, p4, p5]
            for e in range(NB + len(stages) - 1):
                for k in range(len(stages) - 1, -1, -1):
                    t_ = e - k
                    if 0 <= t_ < NB:
                        stages[k](t_)
        prefetch(g, ("hihg", l), win_cols(g, l, C_HI, C_HI + 512), 512)
        sc = [sb(g, st, [128, T], F32, "sc") for _ in range(4)]
        scb = nbs(st, "sc", 4, 4)
        junk = sb(g, st, [128, T], BF16, "junk")
        maskq = [sb(g, st, [128, T], BF16, "maskq") for _ in range(2)]
        mqb = nbs(st, "maskq", 2)
        maskT = sb(g, st, [128, NB, 512], BF16, "maskT")
        mTb = nbs(st, "maskT", 4)
        rj = [sb(g, st, [128, 512], BF16, "rj") for _ in range(4)]
        rjb = nbs(st, "rj", 4)
        dg = [sb(g, st, [128, 8, 128], BF16, "dg") for _ in range(2)]
        dgb = nbs(st, "dg", 2)
        sm = [sb(g, st, [128, 8 + 2 * N_BISECT], F32, "bis") for _ in range(2)]
        smb = nbs(st, "bis", 2)
        cvec = sb(g, st, [128, N_BISECT], F32, "cvec")
        c255 = sb(g, st, [128, 1], F32, "c255")
        cvb = nb(st, "cvec")
        for n_ in range(N_BISECT):
            MS(g, "pool", cvec[:, n_:n_ + 1], 2.0 ** -(n_ + 1), [cvb])
        MS(g, "pool", c255[:], 255.5, [cvb])
        Pt = [sb(g, st, [128, 512], BF16, "Pt") for _ in range(4)]
        Ptb = nbs(st, "Pt", 4)
        Pm = [sb(g, st, [128, 512], BF16, "Pm") for _ in range(4)]
        Pmb = nbs(st, "Pm", 4)
        rs = sb(g, st, [64, 512], F32, "rs")
        rsb = nb(st, "rs")
        cnt_ = {"ri": 0, "pi": 0, "pm": 0}

        def idx_blocks(blocks):
            tiles = []
            for i in blocks:
                nk = (i + 1) * 128
                d_, d_b = dg[i % 2], dgb[i % 2]
                for j in range(8):
                    TS(g, "pool", d_[:, j, :], g.ident[:], sgn[:, i, j:j + 1], None, ALU.mult, None, [g.cb, sgb[i]], [d_b])
                for kc in range((nk + 511) // 512):
                    n = min(512, nk - kc * 512)
                    for j in range(8):
                        tiles.append((i, kc, n, j))
            stt = {}

            def s1(t):
                i, kc, n, j = tiles[t]
                po = 64 * (j % 2)
                psZ, pZb = psf(g, "ixZ", [0, 1, 2])
                stt[t] = [psZ, pZb]
                MM(g, [mmf(psZ[:, 0:n], featT[po:po + 64, 4 + j // 2, i * 128:(i + 1) * 128],
                           featT[po:po + 64, 8, kc * 512:kc * 512 + n], True, True)],
                   [fb[i]] + fb[kc * 4:(kc * 512 + n) // 128], [pZb])

            def s2(t):
                i, kc, n, j = tiles[t]
                psZ, pZb = stt[t]
                r_, r_b = rj[cnt_["ri"] % 4], rjb[cnt_["ri"] % 4]
                cnt_["ri"] += 1
                stt[t] += [r_, r_b]
                ACT(g, r_[:, 0:n], psZ[:, 0:n], AF.Relu, [pZb], [r_b])

            def s3(t):
                i, kc, n, j = tiles[t]
                r_, r_b = stt[t][2], stt[t][3]
                if j == 0:
                    cnt_["psS"] = psf(g, "ixS", [3, 4])
                psS, pSb = cnt_["psS"]
                d_, d_b = dg[i % 2], dgb[i % 2]
                MM(g, [mmf(psS[:, 0:n], d_[:, j, :], r_[:, 0:n], j == 0, j == 7)], [d_b, r_b], [pSb])
                if j == 7:
                    CP(g, "act", sc[i % 4][:, kc * 512:kc * 512 + n], psS[:, 0:n], [pSb], [scb[i % 4][kc]])
                del stt[t]

            pipeline([s1, s2, s3], len(tiles))

        def bis_pair(p):
            blocks = [2 * p, 2 * p + 1]
            st_ = []
            for bi, i in enumerate(blocks):
                nk = (i + 1) * 128
                s_, s_b = sc[i % 4], scb[i % 4]
                nkc = (nk + 511) // 512
                srd = s_b[0:nkc]
                m_, m_b = sm[bi], smb[bi]
                rmax, rmin, step0, mid, cntv, tt = (m_[:, c:c + 1] for c in range(6))
                stepc = m_[:, 8:8 + N_BISECT]
                if nk > 256:
                    P.op("dve", lambda e, s_=s_, nk=nk, rmax=rmax: e.tensor_reduce(out=rmax, in_=s_[:, 0:nk], axis=AX.X, op=ALU.max), srd, [m_b], nk)
                    P.op("dve", lambda e, s_=s_, nk=nk, rmin=rmin: e.tensor_reduce(out=rmin, in_=s_[:, 0:nk], axis=AX.X, op=ALU.min), srd, [m_b], nk)
                dsl = s_[:, i * 128:(i + 1) * 128]
                TT(g, "pool", dsl, dsl, g.caus01[:], ALU.mult, [s_b[i // 4], g.cb], [s_b[i // 4]])
                TT(g, "pool", dsl, dsl, g.negfill[:], ALU.add, [s_b[i // 4], g.cb], [s_b[i // 4]])
                st_.append((i, nk, s_, srd, m_, m_b, rmax, rmin, step0, mid, cntv, tt, stepc))
            act = [x for x in st_ if x[1] > 256]
            for (i, nk, s_, srd, m_, m_b, rmax, rmin, step0, mid, cntv, tt, stepc) in act:
                TT(g, "dve", step0, rmax, rmin, ALU.subtract, [m_b], [m_b])
            for (i, nk, s_, srd, m_, m_b, rmax, rmin, step0, mid, cntv, tt, stepc) in act:
                TS(g, "dve", stepc, cvec[:], step0, None, ALU.mult, None, [m_b, cvb], [m_b])
            for (i, nk, s_, srd, m_, m_b, rmax, rmin, step0, mid, cntv, tt, stepc) in act:
                TS(g, "dve", mid, stepc[:, 0:1], rmin, g.zc[:, 0:1], ALU.add, ALU.add, [m_b, g.cb], [m_b])
            for n_ in range(N_BISECT):
                for (i, nk, s_, srd, m_, m_b, rmax, rmin, step0, mid, cntv, tt, stepc) in act:
                    TS(g, "dve", junk[:, 0:nk], s_[:, 0:nk], mid, g.zc[:, 0:1], ALU.is_ge, ALU.add, srd + [m_b, g.cb], [m_b], accum_out=cntv)
                for (i, nk, s_, srd, m_, m_b, rmax, rmin, step0, mid, cntv, tt, stepc) in act:
                    TS(g, "dve", tt, cntv, c255[:, 0:1], stepc[:, n_:n_ + 1], ALU.is_ge, ALU.mult, [m_b, cvb], [m_b])
                for (i, nk, s_, srd, m_, m_b, rmax, rmin, step0, mid, cntv, tt, stepc) in act:
                    nn = min(n_ + 1, N_BISECT - 1)
                    TS(g, "dve", mid, tt, stepc[:, nn:nn + 1], mid, ALU.subtract, ALU.add, [m_b], [m_b])
            for (i, nk, s_, srd, m_, m_b, rmax, rmin, step0, mid, cntv, tt, stepc) in st_:
                thr = mid if nk > 256 else g.negbig[:, 0:1]
                mq, mq_b = maskq[i % 2], mqb[i % 2]
                TS(g, "dve", mq[:, 0:nk], s_[:, 0:nk], thr, None, ALU.is_ge, None, srd + [m_b, g.cb], [mq_b])

        def mT_pair(p):
            for i in (2 * p, 2 * p + 1):
                ii = i % 4
                mq, mq_b = maskq[i % 2], mqb[i % 2]
                for k0 in range(0, i + 1, 8):
                    k1 = min(i + 1, k0 + 8)
                    pt, ptb = psb(g)
                    MM(g, [trf(pt[:, (kb - k0) * 128:(kb - k0 + 1) * 128], mq[:, kb * 128:(kb + 1) * 128], g.ident[:])
                           for kb in range(k0, k1)], [mq_b, g.cb], [ptb])
                    CP(g, "act", maskT[:, k0:k1, ii * 128:(ii + 1) * 128],
                       pt[:, 0:(k1 - k0) * 128].rearrange("p (m j) -> p m j", j=128), [ptb], [mTb[ii]])

        def att_chunk(qc):
            last = 4 * qc + 3
            tiles = [(h, kb) for h in range(6) for kb in range(last + 1)]
            stt = {}
            pso = {}

            def s1(t):
                h, kb = tiles[t]
                hp, po = h // 2, 64 * (h % 2)
                j0 = max(0, kb * 128 - qc * 512)
                psL, pLb = psf(g, "dsL", [0, 1, 2])
                stt[t] = [psL, pLb]
                MM(g, [mmf(psL[:, j0:512], featT[po:po + 64, 3, kb * 128:(kb + 1) * 128],
                           featT[po:po + 64, hp, qc * 512 + j0:(qc + 1) * 512], True, True)],
                   [fb[kb]] + fb[qc * 4:qc * 4 + 4], [pLb])

            def s2(t):
                h, kb = tiles[t]
                j0 = max(0, kb * 128 - qc * 512)
                psL, pLb = stt[t][0], stt[t][1]
                pi = cnt_["pi"]
                cnt_["pi"] += 1
                p_, p_b = Pt[pi % 4], Ptb[pi % 4]
                stt[t] += [p_, p_b]
                ACT(g, p_[:, j0:512], psL[:, j0:512], AF.Exp, [pLb], [p_b], scale=0.125)

            def s3(t):
                h, kb = tiles[t]
                j0 = max(0, kb * 128 - qc * 512)
                p_, p_b = stt[t][2], stt[t][3]
                pm = cnt_["pm"]
                cnt_["pm"] += 1
                m_, m_b = Pm[pm % 4], Pmb[pm % 4]
                stt[t] += [m_, m_b]
                TT(g, "pool" if t % 3 == 0 else "dve", m_[:, j0:512], p_[:, j0:512], maskT[:, kb, j0:512], ALU.mult,
                   [p_b] + mTb[j0 // 128:4], [m_b])

            def s4(t):
                h, kb = tiles[t]
                hp, po = h // 2, 64 * (h % 2)
                j0 = max(0, kb * 128 - qc * 512)
                m_, m_b = stt[t][4], stt[t][5]
                if kb == 0:
                    pso[h] = psf(g, "dsO", [4, 5])
                psO, pOb = pso[h]
                MM(g, [mmf(psO[:, j0:512], dvx[:, kb, :], m_[:, j0:512], kb == 0, kb == last)], [dvb[kb], m_b], [pOb])
                if kb == last:
                    ACT(g, rs[0:64, :], psO[64:128, :], AF.Ln, [pOb], [rsb])
                    ACT(g, rs[0:64, :], rs[0:64, :], AF.Exp, [rsb], [rsb], scale=-1.0)
                    TT(g, "dve", odT[po:po + 64, hp, qc * 512:(qc + 1) * 512], psO[0:64, :], rs[0:64, :], ALU.mult, [pOb, rsb],
                       [odb[hp][qc]])
                del stt[t]

            pipeline([s1, s2, s3, s4], len(tiles))

        idx_blocks([14, 15])
        for p in range(7, -1, -1):
            if p > 0:
                idx_blocks([2 * p - 2, 2 * p - 1])
            bis_pair(p)
            mT_pair(p)
            if p % 2 == 0:
                att_chunk(p // 2)


def phase_hgrn(g, l, hT, hTb, ohT, ohb):
    P = g.P
    import os
    if int(os.environ.get("HGL", "9")) == 0:
        return
    with scope(g) as st:
        hi_tm = sb(g, st, [128, NB, 256], BF16, "hi_tm")
        hib = nbs(st, "hi", NB)
        hgs = sb(g, st, [128, NB, 256], BF16, "hgs")
        hgb = nbs(st, "hgs", NB)
        onb = sb(g, st, [128, 64], F32, "onorm")
        onbb = nb(st, "onorm")
        DMA(g, "sp", onb[:], g.onorm[l:l + 1, :].to_broadcast([128, 64]), (), [onbb])
        with scope(g) as s2:
            if take_pre(g, ("hihg", l)):
                w, wb = g.pre, g.preb
            else:
                w = sb(g, s2, [128, DC, 512], BF16, "whihg")
                wb = nb(s2, "whihg")
                DMA(g, "pool", w[:], win_cols(g, l, C_HI, C_HI + 512), (), [wb])
            sgs = [sb(g, s2, [128, 256], F32, "sgs") for _ in range(2)]
            sgsb = nbs(s2, "sgs", 2)
            for tb in range(NB):
                ps, pb = psf(g, "proj", [0, 1, 2, 3, 4, 5])
                hgv = int(os.environ.get("HGV", "15"))
                if hgv & 8:
                    mm_tm(g, ps, 512, w, wb, 0, hT, hTb, tb, pb)
                if hgv & 1:
                    CP(g, "dve", hi_tm[:, tb, :], ps[:, 0:256], [pb], [hib[tb]])
                sgt, sgtb = sgs[tb % 2], sgsb[tb % 2]
                if hgv & 2:
                    ACT(g, sgt[:], ps[:, 256:512], AF.Exp if hgv & 16 else AF.Sigmoid, [pb], [sgtb])
                if hgv & 4:
                    TT(g, "dve", hgs[:, tb, :], ps[:, 256:512], sgt[:], ALU.mult, [pb, sgtb], [hgb[tb]])
        with scope(g) as s3:
            NH = 4
            R = 4
            qtT = sb(g, s3, [128, NH, T], BF16, "qtT")
            ktT = sb(g, s3, [128, NH, T], BF16, "ktT")
            qtb = nbs(s3, "qt", NH)
            ktb = nbs(s3, "kt", NH)
            kt_tm = sb(g, s3, [128, NB, NH * 128], BF16, "kt_tm")
            kttb = nbs(s3, "kttm", NH)
            t1 = sb(g, s3, [128, T], F32, "t1")
            t2 = sb(g, s3, [128, T], F32, "t2")
            t3 = sb(g, s3, [128, T], F32, "t3")
            t1h, t2h, t3h = nbs(s3, "t1", 4), nbs(s3, "t2", 4), nbs(s3, "t3", 4)
            ebl = sb(g, s3, [128, NH, 32], F32, "ebl")
            eblb = nb(s3, "ebl")
            W = [sb(g, s3, [128, NH, 64], F32, "W") for _ in range(2)]
            Wb = nbs(s3, "W", 2)
            Sbf = [sb(g, s3, [128, NH, 64], BF16, "Sbf") for _ in range(R)]
            Sbfb = nbs(s3, "Sbf", R)
            attm = [sb(g, s3, [128, NH, 128], BF16, "attm") for _ in range(3)]
            attb = nbs(s3, "attm", 3)
            o_tm = [sb(g, s3, [128, NH * 64], F32, "o_tm") for _ in range(3)]
            otb = nbs(s3, "otm", 3)
            osq = sb(g, s3, [128, NH * 64], F32, "osq")
            osqb = nb(s3, "osq")
            og = [sb(g, s3, [128, NH * 64], BF16, "og") for _ in range(2)]
            ogb = nbs(s3, "og", 2)
            sm = [sb(g, s3, [128, 4], F32, "hsm") for _ in range(2)]
            smb = nbs(s3, "hsm", 2)
            wq = [sb(g, s3, [128, DC, 128], BF16, "wq") for _ in range(2)]
            wqb = nbs(s3, "wq", 2)
            wf = [sb(g, s3, [128, DC, 128], BF16, "wf") for _ in range(2)]
            wfb = nbs(s3, "wf", 2)

            def load_head(hd):
                DMA(g, "pool", wf[hd % 2][:], win_cols(g, l, C_HF + hd * 128, C_HF + hd * 128 + 128), (), [wfb[hd % 2]])
                DMA(g, "pool", wq[hd % 2][:], win_cols(g, l, C_HQ + hd * 128, C_HQ + hd * 128 + 128), (), [wqb[hd % 2]])

            load_head(0)
            for hd in range(NH):
                if hd + 1 < NH:
                    load_head(hd + 1)
                w_f, w_fb, w_q, w_qb = wf[hd % 2], wfb[hd % 2], wq[hd % 2], wqb[hd % 2]
                NQ = 4
                HS = [slice(q_ * (T // NQ), (q_ + 1) * (T // NQ)) for q_ in range(NQ)]
                for tc in range(4):
                    ps, pb = psf(g, "proj", [0, 1, 2, 3, 4, 5])
                    mm_fm(g, ps, 128, 512, w_f, w_fb, 0, hT, hTb, tc * 512, pb)
                    ACT(g, t1[:, tc * 512:(tc + 1) * 512], ps[:, :], AF.Sigmoid, [pb], [t1h[tc]])
                for hf in range(NQ):
                    TS(g, "dve", t1[:, HS[hf]], t1[:, HS[hf]], g.oml[:, l, hd:hd + 1], g.lbv[:, l, hd:hd + 1], ALU.mult, ALU.add,
                       [t1h[hf], g.cb], [t1h[hf]])
                for hf in range(NQ):
                    ACT(g, t2[:, HS[hf]], t1[:, HS[hf]], AF.Copy, [t1h[hf]], [t2h[hf]], scale=-1.0, bias=1.0)
                for hf in range(NQ):
                    TS(g, "dve", t1[:, HS[hf]], t1[:, HS[hf]], F_MIN, None, ALU.max, None, [t1h[hf]], [t1h[hf]])
                for hf in range(NQ):
                    ACT(g, t1[:, HS[hf]], t1[:, HS[hf]], AF.Ln, [t1h[hf]], [t1h[hf]])
                for hf in range(NQ):
                    P.op("dve", lambda e, hf=hf: e.tensor_tensor_scan(out=t3[:, HS[hf]], data0=g.resetm[:, HS[hf]], data1=t1[:, HS[hf]],
                                                                     initial=0.0, op0=ALU.mult, op1=ALU.add),
                         [t1h[hf], g.cb], [t3h[hf]], 2 * T // NQ)
                for hf in range(NQ):
                    TS(g, "dve", t3[:, HS[hf]], t3[:, HS[hf]], -80.0, None, ALU.max, None, [t3h[hf]], [t3h[hf]])
                for hf in range(NQ):
                    ACT(g, t1[:, HS[hf]], t3[:, HS[hf]], AF.Exp, [t3h[hf]], [t1h[hf]])
                for hf in range(NQ):
                    CP(g, "pool", ebl[:, hd, hf * (32 // NQ):(hf + 1) * (32 // NQ)].unsqueeze(2),
                       t1[:, HS[hf]].rearrange("p (c j) -> p c j", j=64)[:, :, 63:64], [t1h[hf]], [eblb])
                for hf in range(NQ):
                    ACT(g, t3[:, HS[hf]], t3[:, HS[hf]], AF.Exp, [t3h[hf]], [t3h[hf]], scale=-1.0)
                for hf in range(NQ):
                    TT(g, "dve", ktT[:, hd, HS[hf]], t2[:, HS[hf]], t3[:, HS[hf]], ALU.mult, [t2h[hf], t3h[hf]], [ktb[hd]])
                for tc in range(4):
                    ps, pb = psf(g, "proj", [0, 1, 2, 3, 4, 5])
                    mm_fm(g, ps, 128, 512, w_q, w_qb, 0, hT, hTb, tc * 512, pb)
                    ACT(g, t2[:, tc * 512:(tc + 1) * 512], ps[:, :], AF.Sigmoid, [pb], [t2h[tc]])
                    TT(g, "dve", t2[:, tc * 512:(tc + 1) * 512], ps[:, :], t2[:, tc * 512:(tc + 1) * 512], ALU.mult, [pb, t2h[tc]],
                       [t2h[tc]])
                for hf in range(NQ):
                    TT(g, "dve", qtT[:, hd, HS[hf]], t2[:, HS[hf]], t1[:, HS[hf]], ALU.mult, [t2h[hf], t1h[hf]], [qtb[hd]])
                for k0 in (0, 8):
                    pt, ptb = psb(g)
                    MM(g, [trf(pt[:, m * 128:(m + 1) * 128], ktT[:, hd, (k0 + m) * 128:(k0 + m + 1) * 128], g.ident[:])
                           for m in range(8)], [ktb[hd], g.cb], [ptb])
                    CP(g, "act" if k0 == 0 else "dve", kt_tm[:, k0:k0 + 8, hd * 128:(hd + 1) * 128],
                       pt[:, :].rearrange("p (m j) -> p m j", j=128), [ptb], [kttb[hd]])

            NCH = 32
            xps = {}

            def emit_X(c):
                tb, pr = c // 2, (c % 2) * 64
                psX, pXb = psf(g, "hgX", [0, 1, 2])
                xps[c] = (psX, pXb)
                MM(g, [mmf(psX[:, hh * 64:(hh + 1) * 64], kt_tm[pr:pr + 64, tb, hh * 128:(hh + 1) * 128],
                           hi_tm[pr:pr + 64, tb, hh * 64:(hh + 1) * 64], True, True) for hh in range(NH)], kttb + [hib[tb]], [pXb])

            def emit_A(tb):
                psA, pAb = psf(g, "hgA", [3])
                MM(g, [mmf(psA[:, hh * 128:(hh + 1) * 128], ktT[:, hh, tb * 128:(tb + 1) * 128],
                           qtT[:, hh, tb * 128:(tb + 1) * 128], True, True) for hh in range(NH)], ktb + qtb, [pAb])
                TT(g, "dve", attm[tb % 3][:], psA[:, :].rearrange("p (h t) -> p h t", t=128),
                   g.maskbd[:].unsqueeze(1).to_broadcast([128, NH, 128]), ALU.mult, [pAb, g.cb], [attb[tb % 3]])

            emit_X(0)
            emit_X(1)
            emit_A(0)
            CP(g, "dve", W[0][:], xps[0][0][:, 0:NH * 64].rearrange("p (h v) -> p h v", v=64), [xps[0][1]], [Wb[0]])
            MS(g, "pool", Sbf[0][:], 0.0, [Sbfb[0]])
            for c in range(NCH):
                tb, half = c // 2, c % 2
                pr = half * 64
                if c + 2 < NCH:
                    emit_X(c + 2)
                if half == 0 and tb + 1 < NB:
                    emit_A(tb + 1)
                if c + 1 < NCH:
                    eb_ = ebl[:, :, c:c + 1].to_broadcast([128, NH, 64])
                    TT(g, "pool", Sbf[(c + 1) % R][:], W[c % 2][:], eb_, ALU.mult, [Wb[c % 2], eblb], [Sbfb[(c + 1) % R]])
                    psX, pXb = xps.pop(c + 1)
                    for hh in range(NH):
                        STT(g, W[(c + 1) % 2][:, hh, :], W[c % 2][:, hh, :], ebl[:, hh, c:c + 1], psX[:, hh * 64:(hh + 1) * 64],
                            ALU.mult, ALU.add, [Wb[c % 2], eblb, pXb], [Wb[(c + 1) % 2]])
                am, amb = attm[tb % 3], attb[tb % 3]
                ot, otbuf = o_tm[tb % 3], otb[tb % 3]
                psO, pOb = psf(g, "hgO", [4, 5])
                fns = []
                for hh in range(NH):
                    fns.append(mmf(psO[0:64, hh * 64:(hh + 1) * 64], am[pr:pr + 64, hh, pr:pr + 64],
                                   hi_tm[pr:pr + 64, tb, hh * 64:(hh + 1) * 64], True, False))
                    fns.append(mmf(psO[0:64, hh * 64:(hh + 1) * 64], qtT[:, hh, c * 64:(c + 1) * 64], Sbf[c % R][:, hh, :], False, True))
                MM(g, fns, [amb, hib[tb], Sbfb[c % R]] + qtb, [pOb])
                CP(g, "act", ot[pr:pr + 64, :], psO[0:64, 0:NH * 64], [pOb], [otbuf])
                if half == 1:
                    sm_, sm_b = sm[tb % 2], smb[tb % 2]
                    og_, og_b = og[tb % 2], ogb[tb % 2]
                    TT(g, "pool", osq[:], ot[:], ot[:], ALU.mult, [otbuf], [osqb])
                    ss = sm_[:, 0:NH]
                    P.op("dve", lambda e, ss=ss: e.tensor_reduce(out=ss, in_=osq[:].rearrange("p (h d) -> p h d", d=64), axis=AX.X,
                                                                 op=ALU.add), [osqb], [sm_b], NH * 64)
                    ACT(g, ss, ss, AF.Ln, [sm_b], [sm_b], scale=1.0 / 64, bias=EPS)
                    ACT(g, ss, ss, AF.Exp, [sm_b], [sm_b], scale=-0.5)
                    o3 = ot[:].rearrange("p (h d) -> p h d", d=64)
                    TT(g, "dve", o3, o3, ss.unsqueeze(2).to_broadcast([128, NH, 64]), ALU.mult, [otbuf, sm_b], [otbuf])
                    TT(g, "dve", o3, o3, onb[:].unsqueeze(1).to_broadcast([128, NH, 64]), ALU.mult, [otbuf, onbb], [otbuf])
                    TT(g, "dve", og_[:], ot[:], hgs[:, tb, :], ALU.mult, [otbuf, hgb[tb]], [og_b])
                    pt, ptb = psb(g)
                    MM(g, [trf(pt[:, m * 128:(m + 1) * 128], og_[:, m * 128:(m + 1) * 128], g.ident[:]) for m in range(2)],
                       [og_b, g.cb], [ptb])
                    CP(g, "act", ohT[:, 0:2, tb * 128:(tb + 1) * 128], pt[:, 0:256].rearrange("p (m j) -> p m j", j=128), [ptb],
                       [ohb[0][tb // 4], ohb[1][tb // 4]])


def phase_mix(g, l, hT, hTb, osbT, osbb, odT, odb, ohT, ohb, src_ap, src_bufs):
    P = g.P
    with scope(g) as st:
        mixT = sb(g, st, [128, DC, T], BF16, "mixT")
        mxb = nbs(st, "mix", DC, 4)
        wg = [sb(g, st, [128, DC, 3, 256], BF16, "wg") for _ in range(2)]
        wgb = nbs(st, "wg", 2, 3)
        wy = sb(g, st, [128, 8, D], BF16, "wy")
        wyb = nbs(st, "wy", 3)
        sg = [sb(g, st, [128, 512], F32, "sg") for _ in range(2)]
        sgb = nbs(st, "sg", 2)
        acc = [sb(g, st, [128, 512], F32, "acc") for _ in range(2)]
        accb = nbs(st, "acc", 2)
        tm = [sb(g, st, [128, 512], F32, "tm") for _ in range(2)]
        tmb = nbs(st, "tm", 2)
        k = 0

        def load_pair(dp):
            w_, w_b = wg[dp % 2], wgb[dp % 2]
            for gi in range(3):
                c0 = C_G + gi * 1024 + dp * 256
                DMA(g, "pool", w_[:, :, gi, :], win_cols(g, l, c0, c0 + 256), (), [w_b[gi]])

        load_pair(0)
        DMA(g, "pool", wy[:, 0:3, :], g.w_sb[l].rearrange("(c p) n -> p c n", p=128), (), [wyb[0]])
        DMA(g, "pool", wy[:, 3:6, :], g.w_dsa[l].rearrange("(c p) n -> p c n", p=128), (), [wyb[1]])
        DMA(g, "pool", wy[:, 6:8, :], g.w_hg[l].rearrange("(c p) n -> p c n", p=128), (), [wyb[2]])
        wo = sb(g, st, [128, DC, D], BF16, "wo")
        wob = nbs(st, "wo", 2)
        for dc in range(DC):
            dp, do = dc // 2, (dc % 2) * 128
            w_, w_b = wg[dp % 2], wgb[dp % 2]
            y_, y_b = wy, wyb
            if dc % 2 == 0:
                if dp + 1 < DC // 2:
                    load_pair(dp + 1)
                else:
                    for nh in range(2):
                        DMA(g, "pool", wo[:, :, nh * 512:(nh + 1) * 512],
                            g.w_out[l].rearrange("(c p) n -> p c n", p=128)[:, :, nh * 512:(nh + 1) * 512], (), [wob[nh]])
                    prefetch(g, ("wu", l, 0), g.w_up[l].rearrange("(c p) n -> p c n", p=128)[:, :, 0:512], 512)
            for tc in range(4):
                a_, a_b = acc[k % 2], accb[k % 2]
                for gi, (oT, obufs, nch, c0) in enumerate(((osbT, osbb, 3, 0), (odT, odb, 3, 3), (ohT, ohb, 2, 6))):
                    psG, pGb = psf(g, "mxG", [0, 1, 2])
                    MM(g, [mmf(psG[:, :], w_[:, c, gi, do:do + 128], hT[:, c, tc * 512:(tc + 1) * 512], c == 0, c == DC - 1) for c in range(DC)],
                       [w_b[gi]] + hTb[tc * 4:tc * 4 + 4], [pGb])
                    s_, s_b = sg[(k * 3 + gi) % 2], sgb[(k * 3 + gi) % 2]
                    ACT(g, s_[:], psG[:, :], AF.Sigmoid, [pGb], [s_b])
                    psY, pYb = psf(g, "mxY", [3, 4, 5])
                    MM(g, [mmf(psY[:, :], y_[:, c0 + c, dc * 128:(dc + 1) * 128], oT[:, c, tc * 512:(tc + 1) * 512], c == 0, c == nch - 1)
                           for c in range(nch)],
                       [y_b[gi]] + [obufs[c][tc] for c in range(nch)], [pYb])
                    if gi == 0:
                        TT(g, "dve", a_[:], psY[:, :], s_[:], ALU.mult, [pYb, s_b], [a_b])
                    else:
                        t_, t_b = tm[gi % 2], tmb[gi % 2]
                        TT(g, "dve", t_[:], psY[:, :], s_[:], ALU.mult, [pYb, s_b], [t_b])
                        if gi == 1:
                            TT(g, "pool", a_[:], a_[:], t_[:], ALU.add, [a_b, t_b], [a_b])
                        else:
                            TT(g, "pool", mixT[:, dc, tc * 512:(tc + 1) * 512], a_[:], t_[:], ALU.add, [a_b, t_b], [mxb[dc][tc]])
                k += 1
        xbs = [sb(g, st, [128, D], F32, "xb") for _ in range(3)]
        xbb = nbs(st, "xb", 3)
        nctx = norm_setup(g, st, g.norm_mlp[l:l + 1, :])
        for tb in range(NB):
            xb, xbuf = xbs[tb % 3], xbb[tb % 3]
            DMA(g, "sp", xb[:], src_ap[tb * 128:(tb + 1) * 128, :], [src_bufs[tb]], [xbuf])
            for nh in range(2):
                ps, pb = psf(g, "mxO", [0, 1, 2, 3, 4, 5])
                MM(g, [mmf(ps[:, :], mixT[:, c, tb * 128:(tb + 1) * 128], wo[:, c, nh * 512:(nh + 1) * 512], c == 0, c == DC - 1)
                       for c in range(DC)], [wob[nh]] + [mxb[c][tb // 4] for c in range(DC)], [pb])
                xs = xb[:, nh * 512:(nh + 1) * 512]
                TT(g, "dve", xs, ps[:, :], xs, ALU.add, [pb, xbuf], [xbuf])
            DMA(g, "sp", g.xres_d[tb * 128:(tb + 1) * 128, :], xb[:], [xbuf], [g.xres_b[tb]])
            norm_block(g, nctx, xb, xbuf, tb, hT, hTb)


def phase_ffn(g, l, hT, hTb, dst_ap, dst_bufs):
    for half in range(2):
        last = half == 1
        with scope(g) as st:
            uT = sb(g, st, [128, 16, T], BF16, "uT")
            ub = nbs(st, "uT", 16, 4)
            wd = sb(g, st, [128, 16, D], BF16, "wd")
            wdb = nbs(st, "wd", 2)
            wdv = g.w_down[l].rearrange("(f p) n -> p f n", p=128)
            wu = [sb(g, st, [128, DC, 512], BF16, "wu") for _ in range(2)]
            wub = nbs(st, "wu", 2)
            rt = [sb(g, st, [128, 512], BF16, "rt") for _ in range(2)]
            rtb = nbs(st, "rt", 2)
            k = 0

            def load_wu(g4):
                c0 = half * 2048 + g4 * 512
                DMA(g, "pool", wu[g4 % 2][:], g.w_up[l].rearrange("(c p) n -> p c n", p=128)[:, :, c0:c0 + 512], (), [wub[g4 % 2]])

            pre0 = take_pre(g, ("wu", l, half))
            if not pre0:
                load_wu(0)
            for g4 in range(4):
                w_, w_b = wu[g4 % 2], wub[g4 % 2]
                if g4 == 0 and pre0:
                    w_, w_b = g.pre, g.preb
                if g4 + 1 < 4:
                    load_wu(g4 + 1)
                if g4 == 1:
                    for nh in range(2):
                        DMA(g, "pool", wd[:, :, nh * 512:(nh + 1) * 512], wdv[:, half * 16:(half + 1) * 16, nh * 512:(nh + 1) * 512], (),
                            [wdb[nh]])
                for fcl in range(4):
                    fc = g4 * 4 + fcl
                    for tc in range(4):
                        ps, pb = psf(g, "proj", [0, 1, 2, 3, 4, 5])
                        mm_fm(g, ps, 128, 512, w_, w_b, fcl * 128, hT, hTb, tc * 512, pb)
                        r_, r_b = rt[k % 2], rtb[k % 2]
                        k += 1
                        ACT(g, r_[:], ps[:, :], AF.Relu, [pb], [r_b])
                        TT(g, "pool", uT[:, fc, tc * 512:(tc + 1) * 512], r_[:], r_[:], ALU.mult, [r_b], [ub[fc][tc]])
            if half == 0:
                prefetch(g, ("wu", l, 1), g.w_up[l].rearrange("(c p) n -> p c n", p=128)[:, :, 2048:2560], 512)
            elif l + 1 < DEPTH:
                prefetch(g, ("sq", l + 1), win_cols(g, l + 1, C_SQ, C_SQ + 384), 384)
            xbs = [sb(g, st, [128, D], F32, "xb") for _ in range(4)]
            xbb = nbs(st, "xb", 4)
            nctx = norm_setup(g, st, g.norm_mix[l + 1:l + 2, :]) if (last and l + 1 < DEPTH) else None
            for tb in range(NB):
                xb, xbuf = xbs[tb % 4], xbb[tb % 4]
                DMA(g, "sp", xb[:], g.xres_d[tb * 128:(tb + 1) * 128, :], [g.xres_b[tb]], [xbuf])
                for nh in range(2):
                    ps, pb = psf(g, "proj", [0, 1, 2, 3, 4, 5])
                    MM(g, [mmf(ps[:, :], uT[:, fc, tb * 128:(tb + 1) * 128], wd[:, fc, nh * 512:(nh + 1) * 512], fc == 0, fc == 15)
                           for fc in range(16)], [wdb[nh]] + [ub[fc][tb // 4] for fc in range(16)], [pb])
                    xs = xb[:, nh * 512:(nh + 1) * 512]
                    TT(g, "dve", xs, ps[:, :], xs, ALU.add, [pb, xbuf], [xbuf])
                if last:
                    DMA(g, "sp", dst_ap[tb * 128:(tb + 1) * 128, :], xb[:], [xbuf], [dst_bufs[tb]])
                    if nctx is not None:
                        norm_block(g, nctx, xb, xbuf, tb, hT, hTb)
                else:
                    DMA(g, "sp", g.xres_d[tb * 128:(tb + 1) * 128, :], xb[:], [xbuf], [g.xres_b[tb]])


def dump_t(g, name, t, ncol):
    if g.dump == name:
        g.P.barrier()
        DMA(g, "sp", g.dbg_d[:, 0:ncol], t, [], [g.dbgb])
        g.P.wait_bufs("sp", [g.dbgb])
        g.P.barrier()


def build_layer(g, l):
    if l > 0 and g.stage < 7:
        return
    src_ap, src_bufs = (g.x_d, g.xin_b) if l == 0 else (g.xres_d, g.xres_b)
    with scope(g) as ls:
        hT, hTb = g.hT, g.hTb
        if l == 0:
            with scope(g) as st:
                norm_T(g, st, src_ap, src_bufs, g.norm_mix[l:l + 1, :], hT, hTb)
        if l == 0:
            dump_t(g, "hT", hT[:].rearrange("p c t -> p (c t)"), 8 * T)
        if g.stage < 2:
            return
        with scope(g) as ms:
            osbT = sb(g, ms, [128, 3, T], BF16, "osbT")
            odT = sb(g, ms, [128, 3, T], BF16, "odT")
            ohT = sb(g, ms, [128, 2, T], BF16, "ohT")
            osbb = nbs(ms, "osb", 3, 4)
            odb = nbs(ms, "od", 3, 4)
            ohb = nbs(ms, "oh", 2, 4)
            phase_sb(g, l, hT, hTb, osbT, osbb)
            if l == 0:
                dump_t(g, "osbT", osbT[:].rearrange("p c t -> p (c t)"), 3 * T)
            if g.stage < 3:
                return
            phase_dsa(g, l, hT, hTb, odT, odb)
            if l == 0:
                dump_t(g, "odT", odT[:].rearrange("p c t -> p (c t)"), 3 * T)
            if g.stage < 4:
                return
            phase_hgrn(g, l, hT, hTb, ohT, ohb)
            if l == 0:
                dump_t(g, "ohT", ohT[:].rearrange("p c t -> p (c t)"), 2 * T)
            if g.stage < 5:
                return
            phase_mix(g, l, hT, hTb, osbT, osbb, odT, odb, ohT, ohb, src_ap, src_bufs)
        if g.stage < 6:
            return
        if l == DEPTH - 1:
            phase_ffn(g, l, hT, hTb, g.out_d, g.out_b)
        else:
            phase_ffn(g, l, hT, hTb, g.xres_d, g.xres_b)


_NC_CACHE = {}


def rope_tables():
    half = 8
    inv = 500000.0 ** (-(np.arange(half, dtype=np.float32) * 2.0) / 16.0)
    ang = np.arange(T, dtype=np.float32)[:, None] * inv[None, :].astype(np.float32)
    return np.cos(ang).astype(np.float32), np.sin(ang).astype(np.float32)


def kernel(x, norm_mix, w_in, qn_dsa, kn_dsa, hgrn_lb, hgrn_onorm, w_br_sb, w_br_dsa, w_br_hgrn, w_out, norm_mlp, w_up, w_down):
    if "nc" not in _NC_CACHE:
        _NC_CACHE["nc"] = build_two_pass()
    nc = _NC_CACHE["nc"]
    f = lambda a: np.ascontiguousarray(np.asarray(a, dtype=np.float32))
    cs, sn = rope_tables()
    shared = dict(norm_mix=f(norm_mix), w_in=f(w_in), qn_dsa=f(qn_dsa), kn_dsa=f(kn_dsa), hgrn_lb=f(hgrn_lb),
                  hgrn_onorm=f(hgrn_onorm), w_br_sb=f(w_br_sb), w_br_dsa=f(w_br_dsa), w_br_hgrn=f(w_br_hgrn),
                  w_out=f(w_out), norm_mlp=f(norm_mlp), w_up=f(w_up), w_down=f(w_down), rope_cos=cs, rope_sin=sn)
    xs = f(x)
    in_maps = [dict(shared, x=xs[b]) for b in range(8)]
    res = run_bass_kernel_spmd(nc, in_maps, core_ids=list(range(8)))
    return np.stack([np.asarray(r["out"], dtype=np.float32) for r in res.results], axis=0)
```

```python
import math
import numpy as np
from contextlib import ExitStack, contextmanager
import concourse.bass as bass
import concourse.mybir as mybir
from concourse.bass_utils import run_bass_kernel_spmd

F32 = mybir.dt.float32
BF16 = mybir.dt.bfloat16
AF = mybir.ActivationFunctionType
ALU = mybir.AluOpType
AX = mybir.AxisListType

T = 2048
D = 1024
NB = 16
DC = 8
DIN = 6856
DFF = 4096
DEPTH = 2
EPS = 1e-6
F_MIN = 1e-12
IDX_SCALE = (64 * 8) ** -0.5
C_SQ, C_SK, C_SV = 0, 384, 768
C_DQ, C_DK, C_DV = 1152, 1536, 1600
C_IQ, C_IK, C_IW = 1664, 2176, 2240
C_HQ, C_HF, C_HI, C_HG = 2248, 2760, 3272, 3528
C_G = 3784
N_BISECT = 14


class Buf:
    __slots__ = ("name", "w", "r", "dsem", "excl")

    def __init__(self, name, excl=False):
        self.name = name
        self.w = None
        self.r = {}
        self.dsem = None
        self.excl = excl


class Prog:
    ENG = ("pe", "act", "dve", "pool", "sp")
    CLEAR_NS = 330.0
    FILL_NS = {"dve": 66.0, "act": 190.0, "pool": 125.0}
    EST = {"dve": (60.0, 0.26), "act": (185.0, 0.83), "pool": (120.0, 0.8), "pe": (0.0, 0.0), "sp": (0.0, 0.0)}

    def __init__(self, nc, stack, needed=None):
        self.needed = needed
        self.used = set()
        self.remap = {}
        self.sig = {}
        self.fill = {}
        self.nc = nc
        self.stack = stack
        self.eng = {"pe": nc.tensor, "act": nc.scalar, "dve": nc.vector, "pool": nc.gpsimd, "sp": nc.sync}
        self.cnt = {e: 0 for e in self.ENG}
        self.known = {e: {} for e in self.ENG}
        self.sems = {}
        self.semval = {}
        for e in ("pe", "act", "dve", "pool"):
            self.sems["E_" + e] = stack.enter_context(nc.semaphore("sem_" + e))
            self.semval["E_" + e] = 0
        self.ndsem = 0
        self.free_dsems = []
        self.nwaits = 0
        self.tcum = {e: 0.0 for e in self.ENG}
        self.tend = {e: {} for e in self.ENG}

    def _dsem(self, buf):
        if buf.dsem is None:
            if self.free_dsems:
                key = self.free_dsems.pop()
            else:
                key = "D%d" % self.ndsem
                self.ndsem += 1
                self.sems[key] = self.stack.enter_context(self.nc.semaphore("dsem%d" % (self.ndsem - 1)))
                self.semval[key] = 0
            buf.dsem = key
        return buf.dsem

    def release(self, bufs):
        for b in bufs:
            if b.dsem is not None:
                self.free_dsems.append(b.dsem)
                b.dsem = None

    def _waits(self, eng, deps):
        need = {}
        own = "E_" + eng
        for (k, v) in deps:
            if eng == "pe" and k == "E_pe":
                continue
            if k == own and eng in ("act", "dve", "pool"):
                te = self.tend[eng].get(v)
                if te is not None and eng in self.fill:
                    gap = self.CLEAR_NS - (self.tcum[eng] - te)
                    if gap > 0:
                        n = int(math.ceil(gap / self.FILL_NS[eng]))
                        for _ in range(n):
                            self.fill[eng](self.eng[eng])
                        self.tcum[eng] += n * self.FILL_NS[eng]
                        self.nfill = getattr(self, "nfill", 0) + n
                continue
            if v > need.get(k, 0):
                need[k] = v
        out = []
        kn = self.known[eng]
        for k, v in need.items():
            if kn.get(k, 0) < v:
                kn[k] = v
                out.append((k, v))
        return out

    @staticmethod
    def _deps(reads, writes):
        deps = []
        for b in reads:
            if b.w is not None:
                deps.append(b.w)
            if b.excl:
                deps.extend(b.r.items())
        for b in writes:
            if b.w is not None:
                deps.append(b.w)
            deps.extend(b.r.items())
        return deps

    def _emit_waits(self, eng, waits):
        e = self.eng[eng]
        for (k, v) in waits:
            if k.startswith("E_"):
                self.used.add((k, v))
                if self.needed is not None:
                    v = self.remap[(k, v)]
            e.wait_ge(self.sems[k], v)
            self.nwaits += 1

    def _mark(self, ev, reads, writes):
        k, v = ev
        for b in reads:
            if b.r.get(k, 0) < v:
                b.r[k] = v
        for b in writes:
            b.w = ev
            b.r = {}

    def op(self, eng, fn, reads=(), writes=(), n=0):
        self.group(eng, [fn], reads, writes, n)

    def group(self, eng, fns, reads=(), writes=(), n=0):
        self._emit_waits(eng, self._waits(eng, self._deps(reads, writes)))
        e = self.eng[eng]
        for fn in fns[:-1]:
            fn(e)
        self.cnt[eng] += 1
        ov, pe_ = self.EST[eng]
        self.tcum[eng] += ov + pe_ * n
        td = self.tend[eng]
        td[self.cnt[eng]] = self.tcum[eng]
        if len(td) > 64:
            for k_ in sorted(td)[:32]:
                del td[k_]
        key = "E_" + eng
        self.semval[key] = self.cnt[eng]
        if self.needed is None or (key, self.cnt[eng]) in self.needed:
            self.sig[key] = self.sig.get(key, 0) + 1
            self.remap[(key, self.cnt[eng])] = self.sig[key]
            fns[-1](e).then_inc(self.sems[key], 1)
        else:
            fns[-1](e)
        self._mark((key, self.cnt[eng]), reads, writes)

    def dma(self, eng, fn, reads=(), writes=()):
        assert len(writes) == 1
        wb = writes[0]
        deps = self._deps(reads, writes)
        if eng == "pool" and getattr(self, "last_swdge", None) is not None:
            deps.append(self.last_swdge)
        self._emit_waits(eng, self._waits(eng, deps))
        key = self._dsem(wb)
        self.semval[key] += 16
        fn(self.eng[eng]).then_inc(self.sems[key], 16)
        if eng == "pool":
            self.last_swdge = (key, self.semval[key])
        self._mark((key, self.semval[key]), reads, writes)

    def barrier(self):
        deps = [(k, v) for k, v in self.semval.items() if v > 0]
        for eng in self.ENG:
            self._emit_waits(eng, self._waits(eng, deps))

    def wait_bufs(self, eng, bufs):
        deps = []
        for b in bufs:
            if b.w is not None:
                deps.append(b.w)
            deps.extend(b.r.items())
        self._emit_waits(eng, self._waits(eng, deps))


class G:
    pass


def bufs(prefix, *dims):
    if len(dims) == 1:
        return [Buf("%s%d" % (prefix, i)) for i in range(dims[0])]
    return [bufs("%s%d_" % (prefix, i), *dims[1:]) for i in range(dims[0])]


def build_program(stage=99, dump=None, needed=None):
    nc = bass.Bass("TRN2", target_bir_lowering=False)
    g = G()
    g.nc = nc
    g.stage = stage
    g.dump = dump
    import os
    g.ntl = int(os.environ.get("NTL", "9"))
    g.dbg_d = None
    if dump is not None:
        g.dbg_d = nc.dram_tensor("dbg", [128, 8 * T], BF16, kind="ExternalOutput").ap()
        g.dbgb = Buf("dbg")
    dt = lambda name, shape, kind, d=F32: nc.dram_tensor(name, shape, d, kind=kind).ap()
    g.x_d = dt("x", [T, D], "ExternalInput")
    g.norm_mix = dt("norm_mix", [DEPTH, D], "ExternalInput")
    g.w_in = dt("w_in", [DEPTH, D, DIN], "ExternalInput")
    g.qn = dt("qn_dsa", [DEPTH, 64], "ExternalInput")
    g.kn = dt("kn_dsa", [DEPTH, 64], "ExternalInput")
    g.lb_d = dt("hgrn_lb", [DEPTH, 512], "ExternalInput")
    g.onorm = dt("hgrn_onorm", [DEPTH, 64], "ExternalInput")
    g.w_sb = dt("w_br_sb", [DEPTH, 384, D], "ExternalInput")
    g.w_dsa = dt("w_br_dsa", [DEPTH, 384, D], "ExternalInput")
    g.w_hg = dt("w_br_hgrn", [DEPTH, 256, D], "ExternalInput")
    g.w_out = dt("w_out", [DEPTH, D, D], "ExternalInput")
    g.norm_mlp = dt("norm_mlp", [DEPTH, D], "ExternalInput")
    g.w_up = dt("w_up", [DEPTH, D, DFF], "ExternalInput")
    g.w_down = dt("w_down", [DEPTH, DFF, D], "ExternalInput")
    g.cs_d = dt("rope_cos", [T, 8], "ExternalInput")
    g.sn_d = dt("rope_sin", [T, 8], "ExternalInput")
    g.out_d = dt("out", [T, D], "ExternalOutput")
    g.xres_d = dt("xres", [T, D], "Internal")
    g.xin_b = bufs("xin", NB)
    g.xres_b = bufs("xres", NB)
    g.out_b = bufs("outb", NB)

    with ExitStack() as gs:
        P = Prog(nc, gs, needed)
        g.P = P
        fa = gs.enter_context(nc.sbuf_tensor("fill_a", [128, 2], F32))
        fd = gs.enter_context(nc.sbuf_tensor("fill_d", [128, 2], F32))
        nc.vector.memset(fd[:], 0.0)
        nc.vector.memset(fa[:], 0.0)
        P.fill["dve"] = lambda e: e.memset(fd[:, 0:1], 0.0)
        P.fill["act"] = lambda e: e.activation(out=fa[:, 0:1], in_=fa[:, 1:2], func=AF.Copy)
        fp = gs.enter_context(nc.sbuf_tensor("fill_p", [128, 2], F32))
        nc.gpsimd.memset(fp[:], 0.0)
        P.fill["pool"] = lambda e: e.memset(fp[:, 0:1], 0.0)
        g.uid = 0
        g.psF = [gs.enter_context(nc.psum_tensor("psF%d" % i, [128, 512], F32)) for i in range(6)]
        g.psFb = [Buf("psF%d" % i, excl=True) for i in range(6)]
        g.psB = [gs.enter_context(nc.psum_tensor("psB%d" % i, [128, 1024], BF16)) for i in range(2)]
        g.psBb = [Buf("psB%d" % i, excl=True) for i in range(2)]
        g.rotc = {}
        build_consts(g, gs)
        g.hT = gs.enter_context(nc.sbuf_tensor("hT_glob", [128, DC, T], BF16))
        g.hTb = bufs("hT", NB)
        g.pre = gs.enter_context(nc.sbuf_tensor("pre_w", [128, DC, 512], BF16))
        g.preb = Buf("pre_w")
        prefetch(g, ("sq", 0), win_cols(g, 0, C_SQ, C_SQ + 384), 384)
        for l in range(DEPTH):
            if g.stage >= 1:
                build_layer(g, l)
        if g.dbg_d is not None:
            P.wait_bufs("sp", [g.dbgb])
        P.wait_bufs("sp", g.out_b)
        P.barrier()
        g.used = P.used
        print("ops", P.cnt, "signals", P.sig, "waits", P.nwaits, "fillers", getattr(P, "nfill", 0), "dsems", P.ndsem, flush=True)
    return nc, P.used


def build_two_pass(stage=99, dump=None):
    _, used = build_program(stage, dump, None)
    nc, _ = build_program(stage, dump, used)
    return nc


def pipeline(stages, ntiles):
    ns = len(stages)
    for t in range(ntiles + ns - 1):
        for k, f in enumerate(stages):
            i = t - k
            if 0 <= i < ntiles:
                f(i)


def prefetch(g, tag, src, ncols):
    DMA(g, "pool", g.pre[:, :, 0:ncols], src, (), [g.preb])
    g.pre_tag = tag


def take_pre(g, tag):
    if getattr(g, "pre_tag", None) == tag:
        g.pre_tag = None
        return True
    return False


def rot(g, role, items):
    i = g.rotc.get(role, 0)
    g.rotc[role] = i + 1
    return items[i % len(items)]


def psf(g, role, banks):
    b = rot(g, role, banks)
    return g.psF[b], g.psFb[b]


def psb(g, role="pb"):
    b = rot(g, role, [0, 1])
    return g.psB[b], g.psBb[b]


@contextmanager
def scope(g):
    st = ExitStack()
    st.tbufs = []
    try:
        yield st
    finally:
        g.P.barrier()
        g.P.release(st.tbufs)
        st.close()


def sb(g, st, shape, dtype, name=None):
    g.uid += 1
    return st.enter_context(g.nc.sbuf_tensor("%s_%d" % (name or "t", g.uid), shape, dtype))


def nb(st, name):
    b = Buf(name)
    st.tbufs.append(b)
    return b


def nbs(st, prefix, *dims):
    r = bufs(prefix, *dims)

    def flat(x):
        if isinstance(x, Buf):
            st.tbufs.append(x)
        else:
            for y in x:
                flat(y)
    flat(r)
    return r


def _fs(ap):
    try:
        return int(ap.free_size())
    except Exception:
        return 0


def ACT(g, out, in_, func, reads, writes, **kw):
    g.P.op("act", lambda e: e.activation(out=out, in_=in_, func=func, **kw), reads, writes, _fs(out))


def TT(g, eng, out, in0, in1, op, reads, writes):
    g.P.op(eng, lambda e: e.tensor_tensor(out=out, in0=in0, in1=in1, op=op), reads, writes, _fs(out))


def TS(g, eng, out, in0, s1, s2, op0, op1, reads, writes, **kw):
    if op1 is None:
        s2 = 0.0 if isinstance(s1, (int, float)) else g.zc[0:in0.shape[0], 0:1]
        g.P.op(eng, lambda e: e.tensor_scalar(out=out, in0=in0, scalar1=s1, scalar2=s2, op0=op0, op1=ALU.add, **kw), reads, writes, _fs(out))
    else:
        g.P.op(eng, lambda e: e.tensor_scalar(out=out, in0=in0, scalar1=s1, scalar2=s2, op0=op0, op1=op1, **kw), reads, writes, _fs(out))


def STT(g, out, in0, scalar, in1, op0, op1, reads, writes):
    g.P.op("dve", lambda e: e.scalar_tensor_tensor(out=out, in0=in0, scalar=scalar, in1=in1, op0=op0, op1=op1), reads, writes, _fs(out))


def CP(g, eng, out, in_, reads, writes):
    if eng == "act":
        g.P.op("act", lambda e: e.activation(out=out, in_=in_, func=AF.Copy), reads, writes, _fs(out))
    else:
        g.P.op(eng, lambda e: e.tensor_copy(out, in_), reads, writes, _fs(out))


def MS(g, eng, ap, val, writes):
    g.P.op(eng, lambda e: e.memset(ap, val), (), writes, _fs(ap))


def ASEL(g, out, in_, pattern, cmp, fill, base, cm, reads, writes):
    g.P.op("pool", lambda e: e.affine_select(out=out, in_=in_, pattern=pattern, compare_op=cmp, fill=fill, base=base,
                                             channel_multiplier=cm), reads, writes, _fs(out))


def MM(g, outs_fns, reads, writes):
    g.P.group("pe", outs_fns, reads, writes)


def mmf(out, lhsT, rhs, start, stop):
    return lambda e: e.matmul(out, lhsT=lhsT, rhs=rhs, start=start, stop=stop)


def trf(out, in_, ident):
    return lambda e: e.transpose(out, in_, ident)


def DMA(g, eng, out, in_, reads, writes, **kw):
    g.P.dma(eng, lambda e: e.dma_start(out=out, in_=in_, **kw), reads, writes)


def build_consts(g, gs):
    nc = g.nc
    mk = lambda name, shape, d: gs.enter_context(nc.sbuf_tensor(name, shape, d))
    g.ident = mk("ident", [128, 128], BF16)
    g.negtri = mk("negtri", [128, 128], BF16)
    g.negones = mk("negones", [128, 128], BF16)
    g.onesb = mk("onesb", [128, 128], BF16)
    g.maskbd = mk("maskbd", [128, 128], F32)
    g.onesf = mk("onesf", [128, 128], F32)
    g.resetm = mk("resetm", [128, T], BF16)
    g.zc = mk("zc", [128, 1], F32)
    g.negbig = mk("negbig", [128, 1], F32)
    g.cs = mk("cs", [128, NB, 8], F32)
    g.sn = mk("sn", [128, NB, 8], F32)
    g.lbraw = mk("lbraw", [128, 2, 4], F32)
    g.lbv = mk("lbv", [128, 2, 4], F32)
    g.oml = mk("oml", [128, 2, 4], F32)
    g.cb = Buf("consts")
    g.csb = Buf("cs")
    g.snb = Buf("sn")
    g.lbb = Buf("lbraw")
    cb = [g.cb]
    MS(g, "pool", g.onesb[:], 1.0, cb)
    MS(g, "pool", g.negones[:], -1.0, cb)
    MS(g, "pool", g.onesf[:], 1.0, cb)
    MS(g, "pool", g.zc[:], 0.0, cb)
    MS(g, "pool", g.negbig[:], -1e29, cb)
    ASEL(g, g.ident[:], g.onesb[:], [[1, 128]], ALU.is_equal, 0.0, 0, -1, cb, cb)
    ASEL(g, g.negtri[:], g.negones[:], [[-1, 128]], ALU.is_ge, 0.0, 0, 1, cb, cb)
    ASEL(g, g.maskbd[:], g.onesf[:], [[1, 128]], ALU.is_ge, 0.0, 0, -1, cb, cb)
    MS(g, "pool", g.maskbd[0:64, 64:128], 0.0, cb)
    g.ones512 = mk("ones512", [128, 512], BF16)
    g.mlt = mk("mlt", [128, 512], BF16)
    MS(g, "pool", g.ones512[:], 1.0, cb)
    ASEL(g, g.mlt[:], g.ones512[:], [[1, 512]], ALU.is_gt, 0.0, 0, -1, cb, cb)
    g.caus01 = mk("caus01", [128, 128], F32)
    g.negfill = mk("negfill", [128, 128], F32)
    ASEL(g, g.caus01[:], g.onesf[:], [[-1, 128]], ALU.is_ge, 0.0, 0, 1, cb, cb)
    TS(g, "pool", g.negfill[:], g.caus01[:], -1.0, 1e30, ALU.add, ALU.mult, cb, cb)
    MS(g, "pool", g.resetm[:], 1.0, cb)
    MS(g, "pool", g.resetm[:].rearrange("p (c j) -> p c j", j=64)[:, :, 0:1], 0.0, cb)
    DMA(g, "sp", g.cs[:], g.cs_d.rearrange("(b p) i -> p b i", p=128), (), [g.csb])
    DMA(g, "sp", g.sn[:], g.sn_d.rearrange("(b p) i -> p b i", p=128), (), [g.snb])
    DMA(g, "sp", g.lbraw[:], g.lb_d.rearrange("l (h k) -> k l h", k=128), (), [g.lbb], allow_slow_non_contiguous=True)
    MS(g, "dve", g.lbv[:], 0.0, cb)
    TT(g, "dve", g.lbv[:, 1, :], g.lbraw[:, 1, :], g.lbraw[:, 0, :], ALU.subtract, [g.lbb], cb)
    ACT(g, g.lbv[:, 1, :], g.lbv[:, 1, :], AF.Sigmoid, cb, cb)
    TS(g, "dve", g.oml[:], g.lbv[:], -1.0, 1.0, ALU.mult, ALU.add, cb, cb)
    g.P.barrier()


class NormCtx:
    pass


def norm_setup(g, st, gain_d_row):
    c = NormCtx()
    c.gain = sb(g, st, [128, D], F32, "gain")
    c.gb = nb(st, "gain")
    DMA(g, "sp", c.gain[:], gain_d_row.to_broadcast([128, D]), (), [c.gb])
    c.junk = sb(g, st, [128, D], BF16, "junk")
    c.jb = nb(st, "junk")
    c.hbs = [sb(g, st, [128, D], BF16, "hb") for _ in range(2)]
    c.hbb = nbs(st, "hb", 2)
    c.ss = sb(g, st, [128, NB], F32, "ss")
    c.ssb = nbs(st, "ss", NB)
    MS(g, "dve", c.ss[:], 0.0, c.ssb)
    return c


def norm_block(g, c, xb, xbuf, tb, hT, hTb):
    hb, hbuf = c.hbs[tb % 2], c.hbb[tb % 2]
    s1 = c.ss[:, tb:tb + 1]
    ACT(g, c.junk[:], xb[:], AF.Square, [xbuf, c.ssb[tb]], [c.jb, c.ssb[tb]], accum_out=s1)
    ACT(g, s1, s1, AF.Ln, [c.ssb[tb]], [c.ssb[tb]], scale=1.0 / D, bias=EPS)
    ACT(g, s1, s1, AF.Exp, [c.ssb[tb]], [c.ssb[tb]], scale=-0.5)
    STT(g, hb[:], xb[:], s1, c.gain[:], ALU.mult, ALU.mult, [xbuf, c.ssb[tb], c.gb], [hbuf])
    for half in range(2):
        pt, ptb = psb(g)
        MM(g, [trf(pt[:, m * 128:(m + 1) * 128], hb[:, (half * 4 + m) * 128:(half * 4 + m + 1) * 128], g.ident[:])
               for m in range(4)], [hbuf, g.cb], [ptb])
        CP(g, "act" if half == 0 else "dve", hT[:, half * 4:half * 4 + 4, tb * 128:(tb + 1) * 128],
           pt[:, 0:512].rearrange("p (m j) -> p m j", j=128), [ptb], [hTb[tb]])


def norm_T(g, st, src_ap, src_bufs, gain_d_row, hT, hTb):
    c = norm_setup(g, st, gain_d_row)
    xbs = [sb(g, st, [128, D], F32, "xb") for _ in range(4)]
    xbb = nbs(st, "xb", 4)
    for tb in range(NB):
        xb, xbuf = xbs[tb % 4], xbb[tb % 4]
        DMA(g, "sp", xb[:], src_ap[tb * 128:(tb + 1) * 128, :], [src_bufs[tb]], [xbuf])
        norm_block(g, c, xb, xbuf, tb, hT, hTb)


def win_cols(g, l, c0, c1):
    return g.w_in[l].rearrange("(c p) n -> p c n", p=128)[:, :, c0:c1]


def mm_fm(g, ps, M, n, w, wb, col0, hT, hTb, tok0, role_bufs):
    MM(g, [mmf(ps[0:M, 0:n], w[:, c, col0:col0 + M], hT[:, c, tok0:tok0 + n], c == 0, c == DC - 1) for c in range(DC)],
       [wb] + hTb[tok0 // 128:(tok0 + n + 127) // 128], [role_bufs])


def mm_tm(g, ps, N, w, wb, col0, hT, hTb, tb, psbuf):
    MM(g, [mmf(ps[:, 0:N], hT[:, c, tb * 128:(tb + 1) * 128], w[:, c, col0:col0 + N], c == 0, c == DC - 1) for c in range(DC)],
       [wb, hTb[tb]], [psbuf])


def phase_sb(g, l, hT, hTb, osbT, osbb):
    with scope(g) as st:
        ws = []
        for i, c0 in enumerate((C_SQ, C_SK, C_SV)):
            if i == 0 and take_pre(g, ("sq", l)):
                ws.append((g.pre, g.preb))
                continue
            w = sb(g, st, [128, DC, 384], BF16, "wsb")
            wb = nb(st, "wsb%d" % i)
            DMA(g, "pool", w[:], win_cols(g, l, c0, c0 + 384), (), [wb])
            ws.append((w, wb))
        sqT = sb(g, st, [128, 3, T], BF16, "sqT")
        skT = sb(g, st, [128, 3, T], BF16, "skT")
        sqb = nbs(st, "sq", 3, 4)
        skb = nbs(st, "sk", 3, 4)
        k = 0
        for (dst, dstb, (w, wb), scl) in ((sqT, sqb, ws[0], 0.125), (skT, skb, ws[1], 1.0)):
            for hp in range(3):
                for tc in range(4):
                    ps, pb = psf(g, "proj", [0, 1, 2, 3, 4, 5])
                    mm_fm(g, ps, 128, 512, w, wb, hp * 128, hT, hTb, tc * 512, pb)
                    o = dst[:, hp, tc * 512:(tc + 1) * 512]
                    if k % 2 == 0:
                        ACT(g, o, ps[:, :], AF.Copy, [pb], [dstb[hp][tc]], scale=scl)
                    else:
                        TS(g, "dve", o, ps[:, :], scl, None, ALU.mult, None, [pb], [dstb[hp][tc]])
                    k += 1
        svp = [sb(g, st, [128, NB, 384], BF16, "svp") for _ in range(2)]
        svb = nbs(st, "sv", 2, NB)
        for s_ in range(2):
            MS(g, "pool", svp[s_][:].rearrange("p t c -> p (t c)"), 0.0, svb[s_])
        for tb in range(NB):
            ps, pb = psf(g, "proj", [0, 1, 2, 3, 4, 5])
            mm_tm(g, ps, 384, ws[2][0], ws[2][1], 0, hT, hTb, tb, pb)
            src = ps[:, 0:384].rearrange("p (m s d) -> p m s d", s=2, d=64)
            for s_ in range(2):
                dst = svp[s_][:, tb, :].rearrange("p (m s d) -> p m s d", s=2, d=64)
                CP(g, "act" if s_ == 0 else "dve", dst[:, :, s_, :], src[:, :, s_, :], [pb], [svb[s_][tb]])
        prefetch(g, ("wA", l), win_cols(g, l, C_DQ, C_DQ + 512), 512)
        R = 4
        mk2 = lambda shape, dt_, nm: [[sb(g, st, shape, dt_, nm) for _ in range(R)] for _ in range(2)]
        Et, SPt, SPs, At = mk2([128, 512], F32, "Et"), mk2([128, 512], BF16, "SPt"), mk2([128, 512], BF16, "SPs"), mk2([128, 512], BF16, "At")
        Etb, SPb, SPsb, Atb = nbs(st, "Et", 2, R), nbs(st, "SPt", 2, R), nbs(st, "SPs", 2, R), nbs(st, "At", 2, R)
        steps = []
        for m in range(3):
            for qc in range(4):
                for n_, kb in enumerate(range(4 * qc + 3, -1, -1)):
                    steps.append((m, qc, kb, n_))
        stt = {}
        pso = {}

        def info(t):
            m, qc, kb, n_ = steps[t]
            j0 = max(0, kb * 128 - qc * 512)
            return m, qc, kb, n_, j0, kb >= 4 * qc, qc * 512 + j0 - kb * 128, n_ == 0

        def opnds(t):
            m, qc, kb, n_, j0, diag, base, first = info(t)
            kk = [skT[64 * s_:64 * s_ + 64, m, kb * 128:(kb + 1) * 128] for s_ in range(2)]
            qq = [sqT[64 * s_:64 * s_ + 64, m, qc * 512 + j0:(qc + 1) * 512] for s_ in range(2)]
            return kk, qq, skb[m][kb // 4], sqb[m][qc]

        def s1(t):
            m, qc, kb, n_, j0, diag, base, first = info(t)
            kk, qq, rk, rq = opnds(t)
            pz = [psf(g, "sbZ", [0, 1, 2]) for _ in range(2)]
            stt[t] = {"pz": pz}
            for s_ in range(2):
                MM(g, [mmf(pz[s_][0][:, j0:512], kk[s_], qq[s_], True, True)], [rk, rq], [pz[s_][1]])

        def s2(t):
            m, qc, kb, n_, j0, diag, base, first = info(t)
            ib = t % R
            pz = stt[t]["pz"]
            for s_ in range(2):
                ACT(g, Et[s_][ib][:, j0:512], pz[s_][0][:, j0:512], AF.Exp, [pz[s_][1]], [Etb[s_][ib]])

        def s3(t):
            m, qc, kb, n_, j0, diag, base, first = info(t)
            ib = t % R
            for s_ in range(2):
                ACT(g, SPt[s_][ib][:, j0:512], Et[s_][ib][:, j0:512], AF.Ln, [Etb[s_][ib]], [SPb[s_][ib]], bias=1.0)
            if diag:
                for s_ in range(2):
                    S = SPt[s_][ib]
                    assert base == 0
                    TT(g, "pool", S[:, j0:512], S[:, j0:512], g.mlt[:, 0:512 - j0], ALU.mult, [SPb[s_][ib], g.cb], [SPb[s_][ib]])
            if kb > 0:
                for s_ in range(2):
                    S, Sb = SPt[s_][ib], SPb[s_][ib]
                    Sn, Snb = SPs[s_][n_ % R], SPsb[s_][n_ % R]
                    if first:
                        if j0 > 0:
                            MS(g, "pool", Sn[:, 0:j0], 0.0, [Snb])
                        CP(g, "pool", Sn[:, j0:512], S[:, j0:512], [Sb], [Snb])
                    else:
                        So, Sob = SPs[s_][(n_ - 1) % R], SPsb[s_][(n_ - 1) % R]
                        if j0 > 0:
                            CP(g, "pool", Sn[:, 0:j0], So[:, 0:j0], [Sob], [Snb])
                        TT(g, "dve", Sn[:, j0:512], So[:, j0:512], S[:, j0:512], ALU.add, [Sob, Sb], [Snb])

        def s4(t):
            m, qc, kb, n_, j0, diag, base, first = info(t)
            ib = t % R
            kk, qq, rk, rq = opnds(t)
            pc = [psf(g, "sbC", [3, 4]) for _ in range(2)]
            stt[t]["pc"] = pc
            for s_ in range(2):
                S = SPt[s_][ib]
                fns = [mmf(pc[s_][0][:, j0:512], kk[s_], qq[s_], True, False),
                       mmf(pc[s_][0][:, j0:512], g.negtri[:], S[:, j0:512], False, first)]
                rd = [rk, rq, SPb[s_][ib], g.cb]
                if not first:
                    So, Sob = SPs[s_][(n_ - 1) % R], SPsb[s_][(n_ - 1) % R]
                    fns.append(mmf(pc[s_][0][:, j0:512], g.negones[:], So[:, j0:512], False, True))
                    rd.append(Sob)
                MM(g, fns, rd, [pc[s_][1]])

        def s5(t):
            m, qc, kb, n_, j0, diag, base, first = info(t)
            ib = t % R
            pc = stt[t]["pc"]
            for s_ in range(2):
                ACT(g, At[s_][ib][:, j0:512], pc[s_][0][:, j0:512], AF.Exp, [pc[s_][1]], [Atb[s_][ib]])
            for s_ in range(2):
                A, Ab = At[s_][ib], Atb[s_][ib]
                if diag:
                    TT(g, "pool", A[:, j0:512], A[:, j0:512], g.mlt[:, 0:512 - j0], ALU.mult, [Ab, g.cb], [Ab])
                if first and j0 > 0:
                    MS(g, "pool", A[:, 0:j0], 0.0, [Ab])

        def s6(t):
            m, qc, kb, n_, j0, diag, base, first = info(t)
            ib = t % R
            if first:
                pso[(m, qc)] = psf(g, "sbO", [5])
            psO, pOb = pso[(m, qc)]
            for s_ in range(2):
                A, Ab = At[s_][ib], Atb[s_][ib]
                vv = svp[s_][:, kb, m * 128:(m + 1) * 128]
                c0 = 0 if first else j0
                MM(g, [mmf(psO[:, c0:512], vv, A[:, c0:512], first and s_ == 0, kb == 0 and s_ == 1)], [svb[s_][kb], Ab], [pOb])
            if kb == 0:
                CP(g, "act" if (m * 4 + qc) % 2 == 0 else "dve", osbT[:, m, qc * 512:(qc + 1) * 512], psO[:, :], [pOb], [osbb[m][qc]])
            del stt[t]

        nst = len(steps)
        stages = [s1, s2, s3, s4, s5, s6]
        for e in range(nst + len(stages) - 1):
            for k in range(len(stages) - 1, -1, -1):
                t = e - k
                if 0 <= t < nst:
                    stages[k](t)


def phase_dsa(g, l, hT, hTb, odT, odb):
    P = g.P
    with scope(g) as st:
        featT = sb(g, st, [128, 9, T], BF16, "featT")
        fb = nbs(st, "feat", NB)
        dvx = sb(g, st, [128, NB, 128], BF16, "dvx")
        dvb = nbs(st, "dvx", NB)
        sgn = sb(g, st, [128, NB, 8], F32, "sgn")
        sgb = nbs(st, "sgn", NB)
        qkg = sb(g, st, [128, 7, 64], F32, "qkg")
        qkgb = nbs(st, "qkg", 7)
        for hh in range(7):
            src = (g.qn if hh < 6 else g.kn)[l:l + 1, :].to_broadcast([128, 64])
            DMA(g, "sp", qkg[:, hh, :], src, (), [qkgb[hh]])
        with scope(g) as s2:
            wB = sb(g, s2, [128, DC, 512], BF16, "wB")
            wC = sb(g, s2, [128, DC, 72], BF16, "wC")
            wBb, wCb = nb(s2, "wB"), nb(s2, "wC")
            if take_pre(g, ("wA", l)):
                wA, wAb = g.pre, g.preb
            else:
                wA = sb(g, s2, [128, DC, 512], BF16, "wA")
                wAb = nb(s2, "wA")
                DMA(g, "pool", wA[:], win_cols(g, l, C_DQ, C_DQ + 512), (), [wAb])
            DMA(g, "pool", wB[:], win_cols(g, l, C_IQ, C_IQ + 512), (), [wBb])
            DMA(g, "pool", wC[:], win_cols(g, l, C_IK, C_IK + 72), (), [wCb])
            NR = 4
            tq = [sb(g, s2, [128, 18, 64], F32, "tokq") for _ in range(NR)]
            tqb = nbs(s2, "tokq", NR)
            tbq = [sb(g, s2, [128, 18, 64], BF16, "tokb") for _ in range(3)]
            tbb = nbs(s2, "tokb", 3)
            sqt = [sb(g, s2, [128, 448], F32, "sqt") for _ in range(2)]
            sqtb = nbs(s2, "sqt", 2)
            smq = [sb(g, s2, [128, 32], F32, "small") for _ in range(2)]
            smqb = nbs(s2, "small", 2)
            rt = [sb(g, s2, [128, 18, 8], F32, "ropet") for _ in range(4)]
            rtb = nbs(s2, "ropet", 4)
            pst = {}

            def p1(tb):
                MS(g, "pool", dvx[:, tb, 64:128], 1.0, [dvb[tb]])
                tk, tkb = tq[tb % NR], tqb[tb % NR]
                MS(g, "pool", tk[:, 7, :], 0.0, [tkb])
                MS(g, "pool", tk[:, 17, :], 0.0, [tkb])
                psA, pAb = psf(g, "dA", [0, 1])
                psBq, pBb = psf(g, "dB", [2, 3])
                psC, pCb = psf(g, "dC", [4, 5])
                pst[tb] = (psA, pAb, psBq, pBb, psC, pCb)
                mm_tm(g, psA, 512, wA, wAb, 0, hT, hTb, tb, pAb)
                mm_tm(g, psBq, 512, wB, wBb, 0, hT, hTb, tb, pBb)
                mm_tm(g, psC, 72, wC, wCb, 0, hT, hTb, tb, pCb)

            def p2(tb):
                psA, pAb, psBq, pBb, psC, pCb = pst.pop(tb)
                tk, tkb = tq[tb % NR], tqb[tb % NR]
                sq_, sq_b = sqt[tb % 2], sqtb[tb % 2]
                sm, smb = smq[tb % 2], smqb[tb % 2]
                ACT(g, sq_[:], psA[:, 0:448], AF.Square, [pAb], [sq_b])
                ss = sm[:, 0:7]
                P.op("dve", lambda e, ss=ss, sq_=sq_: e.tensor_reduce(out=ss, in_=sq_[:].rearrange("p (h d) -> p h d", d=64), axis=AX.X,
                                                                      op=ALU.add), [sq_b], [smb], 448)
                ACT(g, ss, ss, AF.Ln, [smb], [smb], scale=1.0 / 64, bias=EPS)
                ACT(g, ss, ss, AF.Exp, [smb], [smb], scale=-0.5)
                aw = sm[:, 8:16]
                TS(g, "dve", sgn[:, tb, :], psC[:, 64:72], 0.0, 2.0, ALU.is_gt, ALU.mult, [pCb], [sgb[tb]])
                TS(g, "dve", sgn[:, tb, :], sgn[:, tb, :], -1.0, 0.0, ALU.add, ALU.add, [sgb[tb]], [sgb[tb]])
                STT(g, aw, psC[:, 64:72], IDX_SCALE, sgn[:, tb, :], ALU.mult, ALU.mult, [pCb, sgb[tb]], [smb])
                TT(g, "dve", tk[:, 8:16, :], psBq[:, :].rearrange("p (h d) -> p h d", d=64),
                   aw.unsqueeze(2).to_broadcast([128, 8, 64]), ALU.mult, [pBb, smb], [tkb])
                CP(g, "act", tk[:, 16, :], psC[:, 0:64], [pCb], [tkb])
                CP(g, "act", dvx[:, tb, 0:64], psA[:, 448:512], [pAb], [dvb[tb]])
                TT(g, "dve", tk[:, 0:7, :], psA[:, 0:448].rearrange("p (h d) -> p h d", d=64),
                   ss.unsqueeze(2).to_broadcast([128, 7, 64]), ALU.mult, [pAb, smb], [tkb])
                TT(g, "dve", tk[:, 0:7, :], tk[:, 0:7, :], qkg[:], ALU.mult, [tkb] + qkgb, [tkb])

            def p3(tb):
                tk, tkb = tq[tb % NR], tqb[tb % NR]
                x1, x2 = tk[:, :, 0:8], tk[:, :, 8:16]
                cb_ = g.cs[:, tb, :].unsqueeze(1).to_broadcast([128, 18, 8])
                sb_ = g.sn[:, tb, :].unsqueeze(1).to_broadcast([128, 18, 8])
                TT(g, "dve", rt[0][:], x1, cb_, ALU.mult, [tkb, g.csb], [rtb[0]])
                TT(g, "pool", rt[1][:], x2, sb_, ALU.mult, [tkb, g.snb], [rtb[1]])
                TT(g, "dve", rt[2][:], x2, cb_, ALU.mult, [tkb, g.csb], [rtb[2]])
                TT(g, "pool", rt[3][:], x1, sb_, ALU.mult, [tkb, g.snb], [rtb[3]])
                TT(g, "dve", x1, rt[0][:], rt[1][:], ALU.subtract, [rtb[0], rtb[1]], [tkb])
                TT(g, "pool", x2, rt[2][:], rt[3][:], ALU.add, [rtb[2], rtb[3]], [tkb])

            def p4(tb):
                tk, tkb = tq[tb % NR], tqb[tb % NR]
                tkh, tkhb = tbq[tb % 3], tbb[tb % 3]
                CP(g, "act", tkh[:], tk[:], [tkb], [tkhb])
                CP(g, "pool", tkh[:, 7, :], tkh[:, 6, :], [tkhb], [tkhb])
                CP(g, "pool", tkh[:, 17, :], tkh[:, 16, :], [tkhb], [tkhb])

            def p5(tb):
                tkh, tkhb = tbq[tb % 3], tbb[tb % 3]
                flat = tkh[:].rearrange("p h d -> p (h d)")
                pt, ptb = psb(g)
                MM(g, [trf(pt[:, m * 128:(m + 1) * 128], flat[:, m * 128:(m + 1) * 128], g.ident[:]) for m in range(8)],
                   [tkhb, g.cb], [ptb])
                CP(g, "dve", featT[:, 0:8, tb * 128:(tb + 1) * 128], pt[:, :].rearrange("p (m j) -> p m j", j=128), [ptb], [fb[tb]])
                pt2, ptb2 = psb(g)
                MM(g, [trf(pt2[:, 0:128], flat[:, 1024:1152], g.ident[:])], [tkhb, g.cb], [ptb2])
                CP(g, "act", featT[:, 8, tb * 128:(tb + 1) * 128], pt2[:, 0:128], [ptb2], [fb[tb]])

            stages = [p1, p2, p3, p4, p5]
            for e in range(NB + len(stages) - 1):
                for k in range(len(stages) - 1, -1, -1):
                    t_ = e - k
                    if 0 <= t_ < NB:
                        stages[k](t_)
        prefetch(g, ("hihg", l), win_cols(g, l, C_HI, C_HI + 512), 512)
        sc = [sb(g, st, [128, T], F32, "sc") for _ in range(4)]
        scb = nbs(st, "sc", 4, 4)
        junk = sb(g, st, [128, T], BF16, "junk")
        maskq = [sb(g, st, [128, T], BF16, "maskq") for _ in range(2)]
        mqb = nbs(st, "maskq", 2)
        maskT = sb(g, st, [128, NB, 512], BF16, "maskT")
        mTb = nbs(st, "maskT", 4)
        rj = [sb(g, st, [128, 512], BF16, "rj") for _ in range(4)]
        rjb = nbs(st, "rj", 4)
        dg = [sb(g, st, [128, 8, 128], BF16, "dg") for _ in range(2)]
        dgb = nbs(st, "dg", 2)
        sm = [sb(g, st, [128, 8 + 2 * N_BISECT], F32, "bis") for _ in range(2)]
        smb = nbs(st, "bis", 2)
        cvec = sb(g, st, [128, N_BISECT], F32, "cvec")
        c255 = sb(g, st, [128, 1], F32, "c255")
        cvb = nb(st, "cvec")
        for n_ in range(N_BISECT):
            MS(g, "pool", cvec[:, n_:n_ + 1], 2.0 ** -(n_ + 1), [cvb])
        MS(g, "pool", c255[:], 255.5, [cvb])
        Pt = [sb(g, st, [128, 512], BF16, "Pt") for _ in range(4)]
        Ptb = nbs(st, "Pt", 4)
        Pm = [sb(g, st, [128, 512], BF16, "Pm") for _ in range(4)]
        Pmb = nbs(st, "Pm", 4)
        rs = sb(g, st, [64, 512], F32, "rs")
        rsb = nb(st, "rs")
        cnt_ = {"ri": 0, "pi": 0, "pm": 0}

        def idx_blocks(blocks):
            tiles = []
            for i in blocks:
                nk = (i + 1) * 128
                d_, d_b = dg[i % 2], dgb[i % 2]
                for j in range(8):
                    TS(g, "pool", d_[:, j, :], g.ident[:], sgn[:, i, j:j + 1], None, ALU.mult, None, [g.cb, sgb[i]], [d_b])
                for kc in range((nk + 511) // 512):
                    n = min(512, nk - kc * 512)
                    for j in range(8):
                        tiles.append((i, kc, n, j))
            stt = {}

            def s1(t):
                i, kc, n, j = tiles[t]
                po = 64 * (j % 2)
                psZ, pZb = psf(g, "ixZ", [0, 1, 2])
                stt[t] = [psZ, pZb]
                MM(g, [mmf(psZ[:, 0:n], featT[po:po + 64, 4 + j // 2, i * 128:(i + 1) * 128],
                           featT[po:po + 64, 8, kc * 512:kc * 512 + n], True, True)],
                   [fb[i]] + fb[kc * 4:(kc * 512 + n) // 128], [pZb])

            def s2(t):
                i, kc, n, j = tiles[t]
                psZ, pZb = stt[t]
                r_, r_b = rj[cnt_["ri"] % 4], rjb[cnt_["ri"] % 4]
                cnt_["ri"] += 1
                stt[t] += [r_, r_b]
                ACT(g, r_[:, 0:n], psZ[:, 0:n], AF.Relu, [pZb], [r_b])

            def s3(t):
                i, kc, n, j = tiles[t]
                r_, r_b = stt[t][2], stt[t][3]
                if j == 0:
                    cnt_["psS"] = psf(g, "ixS", [3, 4])
                psS, pSb = cnt_["psS"]
                d_, d_b = dg[i % 2], dgb[i % 2]
                MM(g, [mmf(psS[:, 0:n], d_[:, j, :], r_[:, 0:n], j == 0, j == 7)], [d_b, r_b], [pSb])
                if j == 7:
                    CP(g, "act", sc[i % 4][:, kc * 512:kc * 512 + n], psS[:, 0:n], [pSb], [scb[i % 4][kc]])
                del stt[t]

            pipeline([s1, s2, s3], len(tiles))

        def bis_pair(p):
            blocks = [2 * p, 2 * p + 1]
            st_ = []
            for bi, i in enumerate(blocks):
                nk = (i + 1) * 128
                s_, s_b = sc[i % 4], scb[i % 4]
                nkc = (nk + 511) // 512
                srd = s_b[0:nkc]
                m_, m_b = sm[bi], smb[bi]
                rmax, rmin, step0, mid, cntv, tt = (m_[:, c:c + 1] for c in range(6))
                stepc = m_[:, 8:8 + N_BISECT]
                if nk > 256:
                    P.op("dve", lambda e, s_=s_, nk=nk, rmax=rmax: e.tensor_reduce(out=rmax, in_=s_[:, 0:nk], axis=AX.X, op=ALU.max), srd, [m_b], nk)
                    P.op("dve", lambda e, s_=s_, nk=nk, rmin=rmin: e.tensor_reduce(out=rmin, in_=s_[:, 0:nk], axis=AX.X, op=ALU.min), srd, [m_b], nk)
                dsl = s_[:, i * 128:(i + 1) * 128]
                TT(g, "pool", dsl, dsl, g.caus01[:], ALU.mult, [s_b[i // 4], g.cb], [s_b[i // 4]])
                TT(g, "pool", dsl, dsl, g.negfill[:], ALU.add, [s_b[i // 4], g.cb], [s_b[i // 4]])
                st_.append((i, nk, s_, srd, m_, m_b, rmax, rmin, step0, mid, cntv, tt, stepc))
            act = [x for x in st_ if x[1] > 256]
            for (i, nk, s_, srd, m_, m_b, rmax, rmin, step0, mid, cntv, tt, stepc) in act:
                TT(g, "dve", step0, rmax, rmin, ALU.subtract, [m_b], [m_b])
            for (i, nk, s_, srd, m_, m_b, rmax, rmin, step0, mid, cntv, tt, stepc) in act:
                TS(g, "dve", stepc, cvec[:], step0, None, ALU.mult, None, [m_b, cvb], [m_b])
            for (i, nk, s_, srd, m_, m_b, rmax, rmin, step0, mid, cntv, tt, stepc) in act:
                TS(g, "dve", mid, stepc[:, 0:1], rmin, g.zc[:, 0:1], ALU.add, ALU.add, [m_b, g.cb], [m_b])
            for n_ in range(N_BISECT):
                for (i, nk, s_, srd, m_, m_b, rmax, rmin, step0, mid, cntv, tt, stepc) in act:
                    TS(g, "dve", junk[:, 0:nk], s_[:, 0:nk], mid, g.zc[:, 0:1], ALU.is_ge, ALU.add, srd + [m_b, g.cb], [m_b], accum_out=cntv)
                for (i, nk, s_, srd, m_, m_b, rmax, rmin, step0, mid, cntv, tt, stepc) in act:
                    TS(g, "dve", tt, cntv, c255[:, 0:1], stepc[:, n_:n_ + 1], ALU.is_ge, ALU.mult, [m_b, cvb], [m_b])
                for (i, nk, s_, srd, m_, m_b, rmax, rmin, step0, mid, cntv, tt, stepc) in act:
                    nn = min(n_ + 1, N_BISECT - 1)
                    TS(g, "dve", mid, tt, stepc[:, nn:nn + 1], mid, ALU.subtract, ALU.add, [m_b], [m_b])
            for (i, nk, s_, srd, m_, m_b, rmax, rmin, step0, mid, cntv, tt, stepc) in st_:
                thr = mid if nk > 256 else g.negbig[:, 0:1]
                mq, mq_b = maskq[i % 2], mqb[i % 2]
                TS(g, "dve", mq[:, 0:nk], s_[:, 0:nk], thr, None, ALU.is_ge, None, srd + [m_b, g.cb], [mq_b])

        def mT_pair(p):
            for i in (2 * p, 2 * p + 1):
                ii = i % 4
                mq, mq_b = maskq[i % 2], mqb[i % 2]
                for k0 in range(0, i + 1, 8):
                    k1 = min(i + 1, k0 + 8)
                    pt, ptb = psb(g)
                    MM(g, [trf(pt[:, (kb - k0) * 128:(kb - k0 + 1) * 128], mq[:, kb * 128:(kb + 1) * 128], g.ident[:])
                           for kb in range(k0, k1)], [mq_b, g.cb], [ptb])
                    CP(g, "act", maskT[:, k0:k1, ii * 128:(ii + 1) * 128],
                       pt[:, 0:(k1 - k0) * 128].rearrange("p (m j) -> p m j", j=128), [ptb], [mTb[ii]])

        def att_chunk(qc):
            last = 4 * qc + 3
            tiles = [(h, kb) for h in range(6) for kb in range(last + 1)]
            stt = {}
            pso = {}

            def s1(t):
                h, kb = tiles[t]
                hp, po = h // 2, 64 * (h % 2)
                j0 = max(0, kb * 128 - qc * 512)
                psL, pLb = psf(g, "dsL", [0, 1, 2])
                stt[t] = [psL, pLb]
                MM(g, [mmf(psL[:, j0:512], featT[po:po + 64, 3, kb * 128:(kb + 1) * 128],
                           featT[po:po + 64, hp, qc * 512 + j0:(qc + 1) * 512], True, True)],
                   [fb[kb]] + fb[qc * 4:qc * 4 + 4], [pLb])

            def s2(t):
                h, kb = tiles[t]
                j0 = max(0, kb * 128 - qc * 512)
                psL, pLb = stt[t][0], stt[t][1]
                pi = cnt_["pi"]
                cnt_["pi"] += 1
                p_, p_b = Pt[pi % 4], Ptb[pi % 4]
                stt[t] += [p_, p_b]
                ACT(g, p_[:, j0:512], psL[:, j0:512], AF.Exp, [pLb], [p_b], scale=0.125)

            def s3(t):
                h, kb = tiles[t]
                j0 = max(0, kb * 128 - qc * 512)
                p_, p_b = stt[t][2], stt[t][3]
                pm = cnt_["pm"]
                cnt_["pm"] += 1
                m_, m_b = Pm[pm % 4], Pmb[pm % 4]
                stt[t] += [m_, m_b]
                TT(g, "pool" if t % 3 == 0 else "dve", m_[:, j0:512], p_[:, j0:512], maskT[:, kb, j0:512], ALU.mult,
                   [p_b] + mTb[j0 // 128:4], [m_b])

            def s4(t):
                h, kb = tiles[t]
                hp, po = h // 2, 64 * (h % 2)
                j0 = max(0, kb * 128 - qc * 512)
                m_, m_b = stt[t][4], stt[t][5]
                if kb == 0:
                    pso[h] = psf(g, "dsO", [4, 5])
                psO, pOb = pso[h]
                MM(g, [mmf(psO[:, j0:512], dvx[:, kb, :], m_[:, j0:512], kb == 0, kb == last)], [dvb[kb], m_b], [pOb])
                if kb == last:
                    ACT(g, rs[0:64, :], psO[64:128, :], AF.Ln, [pOb], [rsb])
                    ACT(g, rs[0:64, :], rs[0:64, :], AF.Exp, [rsb], [rsb], scale=-1.0)
                    TT(g, "dve", odT[po:po + 64, hp, qc * 512:(qc + 1) * 512], psO[0:64, :], rs[0:64, :], ALU.mult, [pOb, rsb],
                       [odb[hp][qc]])
                del stt[t]

            pipeline([s1, s2, s3, s4], len(tiles))

        idx_blocks([14, 15])
        for p in range(7, -1, -1):
            if p > 0:
                idx_blocks([2 * p - 2, 2 * p - 1])
            bis_pair(p)
            mT_pair(p)
            if p % 2 == 0:
                att_chunk(p // 2)


def phase_hgrn(g, l, hT, hTb, ohT, ohb):
    P = g.P
    import os
    if int(os.environ.get("HGL", "9")) == 0:
        return
    with scope(g) as st:
        hi_tm = sb(g, st, [128, NB, 256], BF16, "hi_tm")
        hib = nbs(st, "hi", NB)
        hgs = sb(g, st, [128, NB, 256], BF16, "hgs")
        hgb = nbs(st, "hgs", NB)
        onb = sb(g, st, [128, 64], F32, "onorm")
        onbb = nb(st, "onorm")
        DMA(g, "sp", onb[:], g.onorm[l:l + 1, :].to_broadcast([128, 64]), (), [onbb])
        with scope(g) as s2:
            if take_pre(g, ("hihg", l)):
                w, wb = g.pre, g.preb
            else:
                w = sb(g, s2, [128, DC, 512], BF16, "whihg")
                wb = nb(s2, "whihg")
                DMA(g, "pool", w[:], win_cols(g, l, C_HI, C_HI + 512), (), [wb])
            sgs = [sb(g, s2, [128, 256], F32, "sgs") for _ in range(2)]
            sgsb = nbs(s2, "sgs", 2)
            for tb in range(NB):
                ps, pb = psf(g, "proj", [0, 1, 2, 3, 4, 5])
                hgv = int(os.environ.get("HGV", "15"))
                if hgv & 8:
                    mm_tm(g, ps, 512, w, wb, 0, hT, hTb, tb, pb)
                if hgv & 1:
                    CP(g, "dve", hi_tm[:, tb, :], ps[:, 0:256], [pb], [hib[tb]])
                sgt, sgtb = sgs[tb % 2], sgsb[tb % 2]
                if hgv & 2:
                    ACT(g, sgt[:], ps[:, 256:512], AF.Exp if hgv & 16 else AF.Sigmoid, [pb], [sgtb])
                if hgv & 4:
                    TT(g, "dve", hgs[:, tb, :], ps[:, 256:512], sgt[:], ALU.mult, [pb, sgtb], [hgb[tb]])
        with scope(g) as s3:
            NH = 4
            R = 4
            qtT = sb(g, s3, [128, NH, T], BF16, "qtT")
            ktT = sb(g, s3, [128, NH, T], BF16, "ktT")
            qtb = nbs(s3, "qt", NH)
            ktb = nbs(s3, "kt", NH)
            kt_tm = sb(g, s3, [128, NB, NH * 128], BF16, "kt_tm")
            kttb = nbs(s3, "kttm", NH)
            t1 = sb(g, s3, [128, T], F32, "t1")
            t2 = sb(g, s3, [128, T], F32, "t2")
            t3 = sb(g, s3, [128, T], F32, "t3")
            t1h, t2h, t3h = nbs(s3, "t1", 4), nbs(s3, "t2", 4), nbs(s3, "t3", 4)
            ebl = sb(g, s3, [128, NH, 32], F32, "ebl")
            eblb = nb(s3, "ebl")
            W = [sb(g, s3, [128, NH, 64], F32, "W") for _ in range(2)]
            Wb = nbs(s3, "W", 2)
            Sbf = [sb(g, s3, [128, NH, 64], BF16, "Sbf") for _ in range(R)]
            Sbfb = nbs(s3, "Sbf", R)
            attm = [sb(g, s3, [128, NH, 128], BF16, "attm") for _ in range(3)]
            attb = nbs(s3, "attm", 3)
            o_tm = [sb(g, s3, [128, NH * 64], F32, "o_tm") for _ in range(3)]
            otb = nbs(s3, "otm", 3)
            osq = sb(g, s3, [128, NH * 64], F32, "osq")
            osqb = nb(s3, "osq")
            og = [sb(g, s3, [128, NH * 64], BF16, "og") for _ in range(2)]
            ogb = nbs(s3, "og", 2)
            sm = [sb(g, s3, [128, 4], F32, "hsm") for _ in range(2)]
            smb = nbs(s3, "hsm", 2)
            wq = [sb(g, s3, [128, DC, 128], BF16, "wq") for _ in range(2)]
            wqb = nbs(s3, "wq", 2)
            wf = [sb(g, s3, [128, DC, 128], BF16, "wf") for _ in range(2)]
            wfb = nbs(s3, "wf", 2)

            def load_head(hd):
                DMA(g, "pool", wf[hd % 2][:], win_cols(g, l, C_HF + hd * 128, C_HF + hd * 128 + 128), (), [wfb[hd % 2]])
                DMA(g, "pool", wq[hd % 2][:], win_cols(g, l, C_HQ + hd * 128, C_HQ + hd * 128 + 128), (), [wqb[hd % 2]])

            load_head(0)
            for hd in range(NH):
                if hd + 1 < NH:
                    load_head(hd + 1)
                w_f, w_fb, w_q, w_qb = wf[hd % 2], wfb[hd % 2], wq[hd % 2], wqb[hd % 2]
                NQ = 4
                HS = [slice(q_ * (T // NQ), (q_ + 1) * (T // NQ)) for q_ in range(NQ)]
                for tc in range(4):
                    ps, pb = psf(g, "proj", [0, 1, 2, 3, 4, 5])
                    mm_fm(g, ps, 128, 512, w_f, w_fb, 0, hT, hTb, tc * 512, pb)
                    ACT(g, t1[:, tc * 512:(tc + 1) * 512], ps[:, :], AF.Sigmoid, [pb], [t1h[tc]])
                for hf in range(NQ):
                    TS(g, "dve", t1[:, HS[hf]], t1[:, HS[hf]], g.oml[:, l, hd:hd + 1], g.lbv[:, l, hd:hd + 1], ALU.mult, ALU.add,
                       [t1h[hf], g.cb], [t1h[hf]])
                for hf in range(NQ):
                    ACT(g, t2[:, HS[hf]], t1[:, HS[hf]], AF.Copy, [t1h[hf]], [t2h[hf]], scale=-1.0, bias=1.0)
                for hf in range(NQ):
                    TS(g, "dve", t1[:, HS[hf]], t1[:, HS[hf]], F_MIN, None, ALU.max, None, [t1h[hf]], [t1h[hf]])
                for hf in range(NQ):
                    ACT(g, t1[:, HS[hf]], t1[:, HS[hf]], AF.Ln, [t1h[hf]], [t1h[hf]])
                for hf in range(NQ):
                    P.op("dve", lambda e, hf=hf: e.tensor_tensor_scan(out=t3[:, HS[hf]], data0=g.resetm[:, HS[hf]], data1=t1[:, HS[hf]],
                                                                     initial=0.0, op0=ALU.mult, op1=ALU.add),
                         [t1h[hf], g.cb], [t3h[hf]], 2 * T // NQ)
                for hf in range(NQ):
                    TS(g, "dve", t3[:, HS[hf]], t3[:, HS[hf]], -80.0, None, ALU.max, None, [t3h[hf]], [t3h[hf]])
                for hf in range(NQ):
                    ACT(g, t1[:, HS[hf]], t3[:, HS[hf]], AF.Exp, [t3h[hf]], [t1h[hf]])
                for hf in range(NQ):
                    CP(g, "pool", ebl[:, hd, hf * (32 // NQ):(hf + 1) * (32 // NQ)].unsqueeze(2),
                       t1[:, HS[hf]].rearrange("p (c j) -> p c j", j=64)[:, :, 63:64], [t1h[hf]], [eblb])
                for hf in range(NQ):
                    ACT(g, t3[:, HS[hf]], t3[:, HS[hf]], AF.Exp, [t3h[hf]], [t3h[hf]], scale=-1.0)
                for hf in range(NQ):
                    TT(g, "dve", ktT[:, hd, HS[hf]], t2[:, HS[hf]], t3[:, HS[hf]], ALU.mult, [t2h[hf], t3h[hf]], [ktb[hd]])
                for tc in range(4):
                    ps, pb = psf(g, "proj", [0, 1, 2, 3, 4, 5])
                    mm_fm(g, ps, 128, 512, w_q, w_qb, 0, hT, hTb, tc * 512, pb)
                    ACT(g, t2[:, tc * 512:(tc + 1) * 512], ps[:, :], AF.Sigmoid, [pb], [t2h[tc]])
                    TT(g, "dve", t2[:, tc * 512:(tc + 1) * 512], ps[:, :], t2[:, tc * 512:(tc + 1) * 512], ALU.mult, [pb, t2h[tc]],
                       [t2h[tc]])
                for hf in range(NQ):
                    TT(g, "dve", qtT[:, hd, HS[hf]], t2[:, HS[hf]], t1[:, HS[hf]], ALU.mult, [t2h[hf], t1h[hf]], [qtb[hd]])
                for k0 in (0, 8):
                    pt, ptb = psb(g)
                    MM(g, [trf(pt[:, m * 128:(m + 1) * 128], ktT[:, hd, (k0 + m) * 128:(k0 + m + 1) * 128], g.ident[:])
                           for m in range(8)], [ktb[hd], g.cb], [ptb])
                    CP(g, "act" if k0 == 0 else "dve", kt_tm[:, k0:k0 + 8, hd * 128:(hd + 1) * 128],
                       pt[:, :].rearrange("p (m j) -> p m j", j=128), [ptb], [kttb[hd]])

            NCH = 32
            xps = {}

            def emit_X(c):
                tb, pr = c // 2, (c % 2) * 64
                psX, pXb = psf(g, "hgX", [0, 1, 2])
                xps[c] = (psX, pXb)
                MM(g, [mmf(psX[:, hh * 64:(hh + 1) * 64], kt_tm[pr:pr + 64, tb, hh * 128:(hh + 1) * 128],
                           hi_tm[pr:pr + 64, tb, hh * 64:(hh + 1) * 64], True, True) for hh in range(NH)], kttb + [hib[tb]], [pXb])

            def emit_A(tb):
                psA, pAb = psf(g, "hgA", [3])
                MM(g, [mmf(psA[:, hh * 128:(hh + 1) * 128], ktT[:, hh, tb * 128:(tb + 1) * 128],
                           qtT[:, hh, tb * 128:(tb + 1) * 128], True, True) for hh in range(NH)], ktb + qtb, [pAb])
                TT(g, "dve", attm[tb % 3][:], psA[:, :].rearrange("p (h t) -> p h t", t=128),
                   g.maskbd[:].unsqueeze(1).to_broadcast([128, NH, 128]), ALU.mult, [pAb, g.cb], [attb[tb % 3]])

            emit_X(0)
            emit_X(1)
            emit_A(0)
            CP(g, "dve", W[0][:], xps[0][0][:, 0:NH * 64].rearrange("p (h v) -> p h v", v=64), [xps[0][1]], [Wb[0]])
            MS(g, "pool", Sbf[0][:], 0.0, [Sbfb[0]])
            for c in range(NCH):
                tb, half = c // 2, c % 2
                pr = half * 64
                if c + 2 < NCH:
                    emit_X(c + 2)
                if half == 0 and tb + 1 < NB:
                    emit_A(tb + 1)
                if c + 1 < NCH:
                    eb_ = ebl[:, :, c:c + 1].to_broadcast([128, NH, 64])
                    TT(g, "pool", Sbf[(c + 1) % R][:], W[c % 2][:], eb_, ALU.mult, [Wb[c % 2], eblb], [Sbfb[(c + 1) % R]])
                    psX, pXb = xps.pop(c + 1)
                    for hh in range(NH):
                        STT(g, W[(c + 1) % 2][:, hh, :], W[c % 2][:, hh, :], ebl[:, hh, c:c + 1], psX[:, hh * 64:(hh + 1) * 64],
                            ALU.mult, ALU.add, [Wb[c % 2], eblb, pXb], [Wb[(c + 1) % 2]])
                am, amb = attm[tb % 3], attb[tb % 3]
                ot, otbuf = o_tm[tb % 3], otb[tb % 3]
                psO, pOb = psf(g, "hgO", [4, 5])
                fns = []
                for hh in range(NH):
                    fns.append(mmf(psO[0:64, hh * 64:(hh + 1) * 64], am[pr:pr + 64, hh, pr:pr + 64],
                                   hi_tm[pr:pr + 64, tb, hh * 64:(hh + 1) * 64], True, False))
                    fns.append(mmf(psO[0:64, hh * 64:(hh + 1) * 64], qtT[:, hh, c * 64:(c + 1) * 64], Sbf[c % R][:, hh, :], False, True))
                MM(g, fns, [amb, hib[tb], Sbfb[c % R]] + qtb, [pOb])
                CP(g, "act", ot[pr:pr + 64, :], psO[0:64, 0:NH * 64], [pOb], [otbuf])
                if half == 1:
                    sm_, sm_b = sm[tb % 2], smb[tb % 2]
                    og_, og_b = og[tb % 2], ogb[tb % 2]
                    TT(g, "pool", osq[:], ot[:], ot[:], ALU.mult, [otbuf], [osqb])
                    ss = sm_[:, 0:NH]
                    P.op("dve", lambda e, ss=ss: e.tensor_reduce(out=ss, in_=osq[:].rearrange("p (h d) -> p h d", d=64), axis=AX.X,
                                                                 op=ALU.add), [osqb], [sm_b], NH * 64)
                    ACT(g, ss, ss, AF.Ln, [sm_b], [sm_b], scale=1.0 / 64, bias=EPS)
                    ACT(g, ss, ss, AF.Exp, [sm_b], [sm_b], scale=-0.5)
                    o3 = ot[:].rearrange("p (h d) -> p h d", d=64)
                    TT(g, "dve", o3, o3, ss.unsqueeze(2).to_broadcast([128, NH, 64]), ALU.mult, [otbuf, sm_b], [otbuf])
                    TT(g, "dve", o3, o3, onb[:].unsqueeze(1).to_broadcast([128, NH, 64]), ALU.mult, [otbuf, onbb], [otbuf])
                    TT(g, "dve", og_[:], ot[:], hgs[:, tb, :], ALU.mult, [otbuf, hgb[tb]], [og_b])
                    pt, ptb = psb(g)
                    MM(g, [trf(pt[:, m * 128:(m + 1) * 128], og_[:, m * 128:(m + 1) * 128], g.ident[:]) for m in range(2)],
                       [og_b, g.cb], [ptb])
                    CP(g, "act", ohT[:, 0:2, tb * 128:(tb + 1) * 128], pt[:, 0:256].rearrange("p (m j) -> p m j", j=128), [ptb],
                       [ohb[0][tb // 4], ohb[1][tb // 4]])


def phase_mix(g, l, hT, hTb, osbT, osbb, odT, odb, ohT, ohb, src_ap, src_bufs):
    P = g.P
    with scope(g) as st:
        mixT = sb(g, st, [128, DC, T], BF16, "mixT")
        mxb = nbs(st, "mix", DC, 4)
        wg = [sb(g, st, [128, DC, 3, 256], BF16, "wg") for _ in range(2)]
        wgb = nbs(st, "wg", 2, 3)
        wy = sb(g, st, [128, 8, D], BF16, "wy")
        wyb = nbs(st, "wy", 3)
        sg = [sb(g, st, [128, 512], F32, "sg") for _ in range(2)]
        sgb = nbs(st, "sg", 2)
        acc = [sb(g, st, [128, 512], F32, "acc") for _ in range(2)]
        accb = nbs(st, "acc", 2)
        tm = [sb(g, st, [128, 512], F32, "tm") for _ in range(2)]
        tmb = nbs(st, "tm", 2)
        k = 0

        def load_pair(dp):
            w_, w_b = wg[dp % 2], wgb[dp % 2]
            for gi in range(3):
                c0 = C_G + gi * 1024 + dp * 256
                DMA(g, "pool", w_[:, :, gi, :], win_cols(g, l, c0, c0 + 256), (), [w_b[gi]])

        load_pair(0)
        DMA(g, "pool", wy[:, 0:3, :], g.w_sb[l].rearrange("(c p) n -> p c n", p=128), (), [wyb[0]])
        DMA(g, "pool", wy[:, 3:6, :], g.w_dsa[l].rearrange("(c p) n -> p c n", p=128), (), [wyb[1]])
        DMA(g, "pool", wy[:, 6:8, :], g.w_hg[l].rearrange("(c p) n -> p c n", p=128), (), [wyb[2]])
        wo = sb(g, st, [128, DC, D], BF16, "wo")
        wob = nbs(st, "wo", 2)
        for dc in range(DC):
            dp, do = dc // 2, (dc % 2) * 128
            w_, w_b = wg[dp % 2], wgb[dp % 2]
            y_, y_b = wy, wyb
            if dc % 2 == 0:
                if dp + 1 < DC // 2:
                    load_pair(dp + 1)
                else:
                    for nh in range(2):
                        DMA(g, "pool", wo[:, :, nh * 512:(nh + 1) * 512],
                            g.w_out[l].rearrange("(c p) n -> p c n", p=128)[:, :, nh * 512:(nh + 1) * 512], (), [wob[nh]])
                    prefetch(g, ("wu", l, 0), g.w_up[l].rearrange("(c p) n -> p c n", p=128)[:, :, 0:512], 512)
            for tc in range(4):
                a_, a_b = acc[k % 2], accb[k % 2]
                for gi, (oT, obufs, nch, c0) in enumerate(((osbT, osbb, 3, 0), (odT, odb, 3, 3), (ohT, ohb, 2, 6))):
                    psG, pGb = psf(g, "mxG", [0, 1, 2])
                    MM(g, [mmf(psG[:, :], w_[:, c, gi, do:do + 128], hT[:, c, tc * 512:(tc + 1) * 512], c == 0, c == DC - 1) for c in range(DC)],
                       [w_b[gi]] + hTb[tc * 4:tc * 4 + 4], [pGb])
                    s_, s_b = sg[(k * 3 + gi) % 2], sgb[(k * 3 + gi) % 2]
                    ACT(g, s_[:], psG[:, :], AF.Sigmoid, [pGb], [s_b])
                    psY, pYb = psf(g, "mxY", [3, 4, 5])
                    MM(g, [mmf(psY[:, :], y_[:, c0 + c, dc * 128:(dc + 1) * 128], oT[:, c, tc * 512:(tc + 1) * 512], c == 0, c == nch - 1)
                           for c in range(nch)],
                       [y_b[gi]] + [obufs[c][tc] for c in range(nch)], [pYb])
                    if gi == 0:
                        TT(g, "dve", a_[:], psY[:, :], s_[:], ALU.mult, [pYb, s_b], [a_b])
                    else:
                        t_, t_b = tm[gi % 2], tmb[gi % 2]
                        TT(g, "dve", t_[:], psY[:, :], s_[:], ALU.mult, [pYb, s_b], [t_b])
                        if gi == 1:
                            TT(g, "pool", a_[:], a_[:], t_[:], ALU.add, [a_b, t_b], [a_b])
                        else:
                            TT(g, "pool", mixT[:, dc, tc * 512:(tc + 1) * 512], a_[:], t_[:], ALU.add, [a_b, t_b], [mxb[dc][tc]])
                k += 1
        xbs = [sb(g, st, [128, D], F32, "xb") for _ in range(3)]
        xbb = nbs(st, "xb", 3)
        nctx = norm_setup(g, st, g.norm_mlp[l:l + 1, :])
        def part_a(tb):
            xb, xbuf = xbs[tb % 3], xbb[tb % 3]
            DMA(g, "sp", xb[:], src_ap[tb * 128:(tb + 1) * 128, :], [src_bufs[tb]], [xbuf])
            for nh in range(2):
                ps, pb = psf(g, "mxO", [0, 1, 2, 3, 4, 5])
                MM(g, [mmf(ps[:, :], mixT[:, c, tb * 128:(tb + 1) * 128], wo[:, c, nh * 512:(nh + 1) * 512], c == 0, c == DC - 1)
                       for c in range(DC)], [wob[nh]] + [mxb[c][tb // 4] for c in range(DC)], [pb])
                xs = xb[:, nh * 512:(nh + 1) * 512]
                TT(g, "dve", xs, ps[:, :], xs, ALU.add, [pb, xbuf], [xbuf])
            DMA(g, "sp", g.xres_d[tb * 128:(tb + 1) * 128, :], xb[:], [xbuf], [g.xres_b[tb]])

        part_a(0)
        for tb in range(NB):
            if tb + 1 < NB:
                part_a(tb + 1)
            norm_block(g, nctx, xbs[tb % 3], xbb[tb % 3], tb, hT, hTb)


def phase_ffn(g, l, hT, hTb, dst_ap, dst_bufs):
    for half in range(2):
        last = half == 1
        with scope(g) as st:
            uT = sb(g, st, [128, 16, T], BF16, "uT")
            ub = nbs(st, "uT", 16, 4)
            wd = sb(g, st, [128, 16, D], BF16, "wd")
            wdb = nbs(st, "wd", 2)
            wdv = g.w_down[l].rearrange("(f p) n -> p f n", p=128)
            wu = [sb(g, st, [128, DC, 512], BF16, "wu") for _ in range(2)]
            wub = nbs(st, "wu", 2)
            rt = [sb(g, st, [128, 512], BF16, "rt") for _ in range(2)]
            rtb = nbs(st, "rt", 2)
            k = 0

            def load_wu(g4):
                c0 = half * 2048 + g4 * 512
                DMA(g, "pool", wu[g4 % 2][:], g.w_up[l].rearrange("(c p) n -> p c n", p=128)[:, :, c0:c0 + 512], (), [wub[g4 % 2]])

            pre0 = take_pre(g, ("wu", l, half))
            if not pre0:
                load_wu(0)
            for g4 in range(4):
                w_, w_b = wu[g4 % 2], wub[g4 % 2]
                if g4 == 0 and pre0:
                    w_, w_b = g.pre, g.preb
                if g4 + 1 < 4:
                    load_wu(g4 + 1)
                if g4 == 1:
                    for nh in range(2):
                        DMA(g, "pool", wd[:, :, nh * 512:(nh + 1) * 512], wdv[:, half * 16:(half + 1) * 16, nh * 512:(nh + 1) * 512], (),
                            [wdb[nh]])
                for fcl in range(4):
                    fc = g4 * 4 + fcl
                    for tc in range(4):
                        ps, pb = psf(g, "proj", [0, 1, 2, 3, 4, 5])
                        mm_fm(g, ps, 128, 512, w_, w_b, fcl * 128, hT, hTb, tc * 512, pb)
                        r_, r_b = rt[k % 2], rtb[k % 2]
                        k += 1
                        ACT(g, r_[:], ps[:, :], AF.Relu, [pb], [r_b])
                        TT(g, "pool", uT[:, fc, tc * 512:(tc + 1) * 512], r_[:], r_[:], ALU.mult, [r_b], [ub[fc][tc]])
            if half == 0:
                prefetch(g, ("wu", l, 1), g.w_up[l].rearrange("(c p) n -> p c n", p=128)[:, :, 2048:2560], 512)
            elif l + 1 < DEPTH:
                prefetch(g, ("sq", l + 1), win_cols(g, l + 1, C_SQ, C_SQ + 384), 384)
            xbs = [sb(g, st, [128, D], F32, "xb") for _ in range(4)]
            xbb = nbs(st, "xb", 4)
            nctx = norm_setup(g, st, g.norm_mix[l + 1:l + 2, :]) if (last and l + 1 < DEPTH) else None
            def down_a(tb):
                xb, xbuf = xbs[tb % 4], xbb[tb % 4]
                DMA(g, "sp", xb[:], g.xres_d[tb * 128:(tb + 1) * 128, :], [g.xres_b[tb]], [xbuf])
                for nh in range(2):
                    ps, pb = psf(g, "proj", [0, 1, 2, 3, 4, 5])
                    MM(g, [mmf(ps[:, :], uT[:, fc, tb * 128:(tb + 1) * 128], wd[:, fc, nh * 512:(nh + 1) * 512], fc == 0, fc == 15)
                           for fc in range(16)], [wdb[nh]] + [ub[fc][tb // 4] for fc in range(16)], [pb])
                    xs = xb[:, nh * 512:(nh + 1) * 512]
                    TT(g, "dve", xs, ps[:, :], xs, ALU.add, [pb, xbuf], [xbuf])
                if last:
                    DMA(g, "sp", dst_ap[tb * 128:(tb + 1) * 128, :], xb[:], [xbuf], [dst_bufs[tb]])
                else:
                    DMA(g, "sp", g.xres_d[tb * 128:(tb + 1) * 128, :], xb[:], [xbuf], [g.xres_b[tb]])

            down_a(0)
            for tb in range(NB):
                if tb + 1 < NB:
                    down_a(tb + 1)
                if last and nctx is not None:
                    norm_block(g, nctx, xbs[tb % 4], xbb[tb % 4], tb, hT, hTb)


def dump_t(g, name, t, ncol):
    if g.dump == name:
        g.P.barrier()
        DMA(g, "sp", g.dbg_d[:, 0:ncol], t, [], [g.dbgb])
        g.P.wait_bufs("sp", [g.dbgb])
        g.P.barrier()


def build_layer(g, l):
    if l > 0 and g.stage < 7:
        return
    src_ap, src_bufs = (g.x_d, g.xin_b) if l == 0 else (g.xres_d, g.xres_b)
    with scope(g) as ls:
        hT, hTb = g.hT, g.hTb
        if l == 0:
            with scope(g) as st:
                norm_T(g, st, src_ap, src_bufs, g.norm_mix[l:l + 1, :], hT, hTb)
        if l == 0:
            dump_t(g, "hT", hT[:].rearrange("p c t -> p (c t)"), 8 * T)
        if g.stage < 2:
            return
        with scope(g) as ms:
            osbT = sb(g, ms, [128, 3, T], BF16, "osbT")
            odT = sb(g, ms, [128, 3, T], BF16, "odT")
            ohT = sb(g, ms, [128, 2, T], BF16, "ohT")
            osbb = nbs(ms, "osb", 3, 4)
            odb = nbs(ms, "od", 3, 4)
            ohb = nbs(ms, "oh", 2, 4)
            phase_sb(g, l, hT, hTb, osbT, osbb)
            if l == 0:
                dump_t(g, "osbT", osbT[:].rearrange("p c t -> p (c t)"), 3 * T)
            if g.stage < 3:
                return
            phase_dsa(g, l, hT, hTb, odT, odb)
            if l == 0:
                dump_t(g, "odT", odT[:].rearrange("p c t -> p (c t)"), 3 * T)
            if g.stage < 4:
                return
            phase_hgrn(g, l, hT, hTb, ohT, ohb)
            if l == 0:
                dump_t(g, "ohT", ohT[:].rearrange("p c t -> p (c t)"), 2 * T)
            if g.stage < 5:
                return
            phase_mix(g, l, hT, hTb, osbT, osbb, odT, odb, ohT, ohb, src_ap, src_bufs)
        if g.stage < 6:
            return
        if l == DEPTH - 1:
            phase_ffn(g, l, hT, hTb, g.out_d, g.out_b)
        else:
            phase_ffn(g, l, hT, hTb, g.xres_d, g.xres_b)


_NC_CACHE = {}


def rope_tables():
    half = 8
    inv = 500000.0 ** (-(np.arange(half, dtype=np.float32) * 2.0) / 16.0)
    ang = np.arange(T, dtype=np.float32)[:, None] * inv[None, :].astype(np.float32)
    return np.cos(ang).astype(np.float32), np.sin(ang).astype(np.float32)


def kernel(x, norm_mix, w_in, qn_dsa, kn_dsa, hgrn_lb, hgrn_onorm, w_br_sb, w_br_dsa, w_br_hgrn, w_out, norm_mlp, w_up, w_down):
    if "nc" not in _NC_CACHE:
        _NC_CACHE["nc"] = build_two_pass()
    nc = _NC_CACHE["nc"]
    f = lambda a: np.ascontiguousarray(np.asarray(a, dtype=np.float32))
    cs, sn = rope_tables()
    shared = dict(norm_mix=f(norm_mix), w_in=f(w_in), qn_dsa=f(qn_dsa), kn_dsa=f(kn_dsa), hgrn_lb=f(hgrn_lb),
                  hgrn_onorm=f(hgrn_onorm), w_br_sb=f(w_br_sb), w_br_dsa=f(w_br_dsa), w_br_hgrn=f(w_br_hgrn),
                  w_out=f(w_out), norm_mlp=f(norm_mlp), w_up=f(w_up), w_down=f(w_down), rope_cos=cs, rope_sin=sn)
    xs = f(x)
    in_maps = [dict(shared, x=xs[b]) for b in range(8)]
    res = run_bass_kernel_spmd(nc, in_maps, core_ids=list(range(8)))
    return np.stack([np.asarray(r["out"], dtype=np.float32) for r in res.results], axis=0)
```

```python
import math
import numpy as np
from contextlib import ExitStack, contextmanager
import concourse.bass as bass
import concourse.mybir as mybir
from concourse.bass_utils import run_bass_kernel_spmd

F32 = mybir.dt.float32
BF16 = mybir.dt.bfloat16
AF = mybir.ActivationFunctionType
ALU = mybir.AluOpType
AX = mybir.AxisListType

T = 2048
D = 1024
NB = 16
DC = 8
DIN = 6856
DFF = 4096
DEPTH = 2
EPS = 1e-6
F_MIN = 1e-12
IDX_SCALE = (64 * 8) ** -0.5
C_SQ, C_SK, C_SV = 0, 384, 768
C_DQ, C_DK, C_DV = 1152, 1536, 1600
C_IQ, C_IK, C_IW = 1664, 2176, 2240
C_HQ, C_HF, C_HI, C_HG = 2248, 2760, 3272, 3528
C_G = 3784
N_BISECT = 14


class Buf:
    __slots__ = ("name", "w", "r", "dsem", "excl")

    def __init__(self, name, excl=False):
        self.name = name
        self.w = None
        self.r = {}
        self.dsem = None
        self.excl = excl


class Prog:
    ENG = ("pe", "act", "dve", "pool", "sp")
    CLEAR_NS = 330.0
    FILL_NS = {"dve": 66.0, "act": 190.0, "pool": 125.0}
    EST = {"dve": (60.0, 0.26), "act": (185.0, 0.83), "pool": (120.0, 0.8), "pe": (0.0, 0.0), "sp": (0.0, 0.0)}

    def __init__(self, nc, stack, needed=None):
        self.needed = needed
        self.used = set()
        self.remap = {}
        self.sig = {}
        self.fill = {}
        self.nc = nc
        self.stack = stack
        self.eng = {"pe": nc.tensor, "act": nc.scalar, "dve": nc.vector, "pool": nc.gpsimd, "sp": nc.sync}
        self.cnt = {e: 0 for e in self.ENG}
        self.known = {e: {} for e in self.ENG}
        self.sems = {}
        self.semval = {}
        for e in ("pe", "act", "dve", "pool"):
            self.sems["E_" + e] = stack.enter_context(nc.semaphore("sem_" + e))
            self.semval["E_" + e] = 0
        self.ndsem = 0
        self.free_dsems = []
        self.nwaits = 0
        self.tcum = {e: 0.0 for e in self.ENG}
        self.tend = {e: {} for e in self.ENG}

    def _dsem(self, buf):
        if buf.dsem is None:
            if self.free_dsems:
                key = self.free_dsems.pop()
            else:
                key = "D%d" % self.ndsem
                self.ndsem += 1
                self.sems[key] = self.stack.enter_context(self.nc.semaphore("dsem%d" % (self.ndsem - 1)))
                self.semval[key] = 0
            buf.dsem = key
        return buf.dsem

    def release(self, bufs):
        for b in bufs:
            if b.dsem is not None:
                self.free_dsems.append(b.dsem)
                b.dsem = None

    def _waits(self, eng, deps):
        need = {}
        own = "E_" + eng
        for (k, v) in deps:
            if eng == "pe" and k == "E_pe":
                continue
            if k == own and eng in ("act", "dve", "pool"):
                te = self.tend[eng].get(v)
                if te is not None and eng in self.fill:
                    gap = self.CLEAR_NS - (self.tcum[eng] - te)
                    if gap > 0:
                        n = int(math.ceil(gap / self.FILL_NS[eng]))
                        for _ in range(n):
                            self.fill[eng](self.eng[eng])
                        self.tcum[eng] += n * self.FILL_NS[eng]
                        self.nfill = getattr(self, "nfill", 0) + n
                continue
            if v > need.get(k, 0):
                need[k] = v
        out = []
        kn = self.known[eng]
        for k, v in need.items():
            if kn.get(k, 0) < v:
                kn[k] = v
                out.append((k, v))
        return out

    @staticmethod
    def _deps(reads, writes):
        deps = []
        for b in reads:
            if b.w is not None:
                deps.append(b.w)
            if b.excl:
                deps.extend(b.r.items())
        for b in writes:
            if b.w is not None:
                deps.append(b.w)
            deps.extend(b.r.items())
        return deps

    def _emit_waits(self, eng, waits):
        e = self.eng[eng]
        for (k, v) in waits:
            if k.startswith("E_"):
                self.used.add((k, v))
                if self.needed is not None:
                    v = self.remap[(k, v)]
            e.wait_ge(self.sems[k], v)
            self.nwaits += 1

    def _mark(self, ev, reads, writes):
        k, v = ev
        for b in reads:
            if b.r.get(k, 0) < v:
                b.r[k] = v
        for b in writes:
            b.w = ev
            b.r = {}

    def op(self, eng, fn, reads=(), writes=(), n=0):
        self.group(eng, [fn], reads, writes, n)

    def group(self, eng, fns, reads=(), writes=(), n=0):
        self._emit_waits(eng, self._waits(eng, self._deps(reads, writes)))
        e = self.eng[eng]
        for fn in fns[:-1]:
            fn(e)
        self.cnt[eng] += 1
        ov, pe_ = self.EST[eng]
        self.tcum[eng] += ov + pe_ * n
        td = self.tend[eng]
        td[self.cnt[eng]] = self.tcum[eng]
        if len(td) > 64:
            for k_ in sorted(td)[:32]:
                del td[k_]
        key = "E_" + eng
        self.semval[key] = self.cnt[eng]
        if self.needed is None or (key, self.cnt[eng]) in self.needed:
            self.sig[key] = self.sig.get(key, 0) + 1
            self.remap[(key, self.cnt[eng])] = self.sig[key]
            fns[-1](e).then_inc(self.sems[key], 1)
        else:
            fns[-1](e)
        self._mark((key, self.cnt[eng]), reads, writes)

    def dma(self, eng, fn, reads=(), writes=()):
        assert len(writes) == 1
        wb = writes[0]
        deps = self._deps(reads, writes)
        if eng == "pool" and getattr(self, "last_swdge", None) is not None:
            deps.append(self.last_swdge)
        self._emit_waits(eng, self._waits(eng, deps))
        key = self._dsem(wb)
        self.semval[key] += 16
        fn(self.eng[eng]).then_inc(self.sems[key], 16)
        if eng == "pool":
            self.last_swdge = (key, self.semval[key])
        self._mark((key, self.semval[key]), reads, writes)

    def barrier(self):
        deps = [(k, v) for k, v in self.semval.items() if v > 0]
        for eng in self.ENG:
            self._emit_waits(eng, self._waits(eng, deps))

    def wait_bufs(self, eng, bufs):
        deps = []
        for b in bufs:
            if b.w is not None:
                deps.append(b.w)
            deps.extend(b.r.items())
        self._emit_waits(eng, self._waits(eng, deps))


class G:
    pass


def bufs(prefix, *dims):
    if len(dims) == 1:
        return [Buf("%s%d" % (prefix, i)) for i in range(dims[0])]
    return [bufs("%s%d_" % (prefix, i), *dims[1:]) for i in range(dims[0])]


def build_program(stage=99, dump=None, needed=None):
    nc = bass.Bass("TRN2", target_bir_lowering=False)
    g = G()
    g.nc = nc
    g.stage = stage
    g.dump = dump
    import os
    g.ntl = int(os.environ.get("NTL", "9"))
    g.dbg_d = None
    if dump is not None:
        g.dbg_d = nc.dram_tensor("dbg", [128, 8 * T], BF16, kind="ExternalOutput").ap()
        g.dbgb = Buf("dbg")
    dt = lambda name, shape, kind, d=F32: nc.dram_tensor(name, shape, d, kind=kind).ap()
    g.x_d = dt("x", [T, D], "ExternalInput")
    g.norm_mix = dt("norm_mix", [DEPTH, D], "ExternalInput")
    g.w_in = dt("w_in", [DEPTH, D, DIN], "ExternalInput")
    g.qn = dt("qn_dsa", [DEPTH, 64], "ExternalInput")
    g.kn = dt("kn_dsa", [DEPTH, 64], "ExternalInput")
    g.lb_d = dt("hgrn_lb", [DEPTH, 512], "ExternalInput")
    g.onorm = dt("hgrn_onorm", [DEPTH, 64], "ExternalInput")
    g.w_sb = dt("w_br_sb", [DEPTH, 384, D], "ExternalInput")
    g.w_dsa = dt("w_br_dsa", [DEPTH, 384, D], "ExternalInput")
    g.w_hg = dt("w_br_hgrn", [DEPTH, 256, D], "ExternalInput")
    g.w_out = dt("w_out", [DEPTH, D, D], "ExternalInput")
    g.norm_mlp = dt("norm_mlp", [DEPTH, D], "ExternalInput")
    g.w_up = dt("w_up", [DEPTH, D, DFF], "ExternalInput")
    g.w_down = dt("w_down", [DEPTH, DFF, D], "ExternalInput")
    g.cs_d = dt("rope_cos", [T, 8], "ExternalInput")
    g.sn_d = dt("rope_sin", [T, 8], "ExternalInput")
    g.out_d = dt("out", [T, D], "ExternalOutput")
    g.xres_d = dt("xres", [T, D], "Internal")
    g.xin_b = bufs("xin", NB)
    g.xres_b = bufs("xres", NB)
    g.out_b = bufs("outb", NB)

    with ExitStack() as gs:
        P = Prog(nc, gs, needed)
        g.P = P
        fa = gs.enter_context(nc.sbuf_tensor("fill_a", [128, 2], F32))
        fd = gs.enter_context(nc.sbuf_tensor("fill_d", [128, 2], F32))
        nc.vector.memset(fd[:], 0.0)
        nc.vector.memset(fa[:], 0.0)
        P.fill["dve"] = lambda e: e.memset(fd[:, 0:1], 0.0)
        P.fill["act"] = lambda e: e.activation(out=fa[:, 0:1], in_=fa[:, 1:2], func=AF.Copy)
        fp = gs.enter_context(nc.sbuf_tensor("fill_p", [128, 2], F32))
        nc.gpsimd.memset(fp[:], 0.0)
        P.fill["pool"] = lambda e: e.memset(fp[:, 0:1], 0.0)
        g.uid = 0
        g.psF = [gs.enter_context(nc.psum_tensor("psF%d" % i, [128, 512], F32)) for i in range(6)]
        g.psFb = [Buf("psF%d" % i, excl=True) for i in range(6)]
        g.psB = [gs.enter_context(nc.psum_tensor("psB%d" % i, [128, 1024], BF16)) for i in range(2)]
        g.psBb = [Buf("psB%d" % i, excl=True) for i in range(2)]
        g.rotc = {}
        build_consts(g, gs)
        g.hT = gs.enter_context(nc.sbuf_tensor("hT_glob", [128, DC, T], BF16))
        g.hTb = bufs("hT", NB)
        g.pre = gs.enter_context(nc.sbuf_tensor("pre_w", [128, DC, 512], BF16))
        g.preb = Buf("pre_w")
        prefetch(g, ("sq", 0), win_cols(g, 0, C_SQ, C_SQ + 384), 384)
        for l in range(DEPTH):
            if g.stage >= 1:
                build_layer(g, l)
        if g.dbg_d is not None:
            P.wait_bufs("sp", [g.dbgb])
        P.wait_bufs("sp", g.out_b)
        P.barrier()
        g.used = P.used
        print("ops", P.cnt, "signals", P.sig, "waits", P.nwaits, "fillers", getattr(P, "nfill", 0), "dsems", P.ndsem, flush=True)
    return nc, P.used


def build_two_pass(stage=99, dump=None):
    _, used = build_program(stage, dump, None)
    nc, _ = build_program(stage, dump, used)
    return nc


def pipeline(stages, ntiles):
    ns = len(stages)
    for t in range(ntiles + ns - 1):
        for k, f in enumerate(stages):
            i = t - k
            if 0 <= i < ntiles:
                f(i)


def prefetch(g, tag, src, ncols):
    DMA(g, "pool", g.pre[:, :, 0:ncols], src, (), [g.preb])
    g.pre_tag = tag


def take_pre(g, tag):
    if getattr(g, "pre_tag", None) == tag:
        g.pre_tag = None
        return True
    return False


def rot(g, role, items):
    i = g.rotc.get(role, 0)
    g.rotc[role] = i + 1
    return items[i % len(items)]


def psf(g, role, banks):
    b = rot(g, role, banks)
    return g.psF[b], g.psFb[b]


def psb(g, role="pb"):
    b = rot(g, role, [0, 1])
    return g.psB[b], g.psBb[b]


@contextmanager
def scope(g):
    st = ExitStack()
    st.tbufs = []
    try:
        yield st
    finally:
        g.P.barrier()
        g.P.release(st.tbufs)
        st.close()


def sb(g, st, shape, dtype, name=None):
    g.uid += 1
    return st.enter_context(g.nc.sbuf_tensor("%s_%d" % (name or "t", g.uid), shape, dtype))


def nb(st, name):
    b = Buf(name)
    st.tbufs.append(b)
    return b


def nbs(st, prefix, *dims):
    r = bufs(prefix, *dims)

    def flat(x):
        if isinstance(x, Buf):
            st.tbufs.append(x)
        else:
            for y in x:
                flat(y)
    flat(r)
    return r


def _fs(ap):
    try:
        return int(ap.free_size())
    except Exception:
        return 0


def ACT(g, out, in_, func, reads, writes, **kw):
    g.P.op("act", lambda e: e.activation(out=out, in_=in_, func=func, **kw), reads, writes, _fs(out))


def TT(g, eng, out, in0, in1, op, reads, writes):
    g.P.op(eng, lambda e: e.tensor_tensor(out=out, in0=in0, in1=in1, op=op), reads, writes, _fs(out))


def TS(g, eng, out, in0, s1, s2, op0, op1, reads, writes, **kw):
    if op1 is None:
        s2 = 0.0 if isinstance(s1, (int, float)) else g.zc[0:in0.shape[0], 0:1]
        g.P.op(eng, lambda e: e.tensor_scalar(out=out, in0=in0, scalar1=s1, scalar2=s2, op0=op0, op1=ALU.add, **kw), reads, writes, _fs(out))
    else:
        g.P.op(eng, lambda e: e.tensor_scalar(out=out, in0=in0, scalar1=s1, scalar2=s2, op0=op0, op1=op1, **kw), reads, writes, _fs(out))


def STT(g, out, in0, scalar, in1, op0, op1, reads, writes):
    g.P.op("dve", lambda e: e.scalar_tensor_tensor(out=out, in0=in0, scalar=scalar, in1=in1, op0=op0, op1=op1), reads, writes, _fs(out))


def CP(g, eng, out, in_, reads, writes):
    if eng == "act":
        g.P.op("act", lambda e: e.activation(out=out, in_=in_, func=AF.Copy), reads, writes, _fs(out))
    else:
        g.P.op(eng, lambda e: e.tensor_copy(out, in_), reads, writes, _fs(out))


def MS(g, eng, ap, val, writes):
    g.P.op(eng, lambda e: e.memset(ap, val), (), writes, _fs(ap))


def ASEL(g, out, in_, pattern, cmp, fill, base, cm, reads, writes):
    g.P.op("pool", lambda e: e.affine_select(out=out, in_=in_, pattern=pattern, compare_op=cmp, fill=fill, base=base,
                                             channel_multiplier=cm), reads, writes, _fs(out))


def MM(g, outs_fns, reads, writes):
    g.P.group("pe", outs_fns, reads, writes)


def mmf(out, lhsT, rhs, start, stop):
    return lambda e: e.matmul(out, lhsT=lhsT, rhs=rhs, start=start, stop=stop)


def trf(out, in_, ident):
    return lambda e: e.transpose(out, in_, ident)


def DMA(g, eng, out, in_, reads, writes, **kw):
    g.P.dma(eng, lambda e: e.dma_start(out=out, in_=in_, **kw), reads, writes)


def build_consts(g, gs):
    nc = g.nc
    mk = lambda name, shape, d: gs.enter_context(nc.sbuf_tensor(name, shape, d))
    g.ident = mk("ident", [128, 128], BF16)
    g.negtri = mk("negtri", [128, 128], BF16)
    g.negones = mk("negones", [128, 128], BF16)
    g.onesb = mk("onesb", [128, 128], BF16)
    g.maskbd = mk("maskbd", [128, 128], F32)
    g.onesf = mk("onesf", [128, 128], F32)
    g.resetm = mk("resetm", [128, T], BF16)
    g.zc = mk("zc", [128, 1], F32)
    g.negbig = mk("negbig", [128, 1], F32)
    g.cs = mk("cs", [128, NB, 8], F32)
    g.sn = mk("sn", [128, NB, 8], F32)
    g.lbraw = mk("lbraw", [128, 2, 4], F32)
    g.lbv = mk("lbv", [128, 2, 4], F32)
    g.oml = mk("oml", [128, 2, 4], F32)
    g.cb = Buf("consts")
    g.csb = Buf("cs")
    g.snb = Buf("sn")
    g.lbb = Buf("lbraw")
    cb = [g.cb]
    MS(g, "pool", g.onesb[:], 1.0, cb)
    MS(g, "pool", g.negones[:], -1.0, cb)
    MS(g, "pool", g.onesf[:], 1.0, cb)
    MS(g, "pool", g.zc[:], 0.0, cb)
    MS(g, "pool", g.negbig[:], -1e29, cb)
    ASEL(g, g.ident[:], g.onesb[:], [[1, 128]], ALU.is_equal, 0.0, 0, -1, cb, cb)
    ASEL(g, g.negtri[:], g.negones[:], [[-1, 128]], ALU.is_ge, 0.0, 0, 1, cb, cb)
    ASEL(g, g.maskbd[:], g.onesf[:], [[1, 128]], ALU.is_ge, 0.0, 0, -1, cb, cb)
    MS(g, "pool", g.maskbd[0:64, 64:128], 0.0, cb)
    g.ones512 = mk("ones512", [128, 512], BF16)
    g.mlt = mk("mlt", [128, 512], BF16)
    MS(g, "pool", g.ones512[:], 1.0, cb)
    ASEL(g, g.mlt[:], g.ones512[:], [[1, 512]], ALU.is_gt, 0.0, 0, -1, cb, cb)
    g.caus01 = mk("caus01", [128, 128], F32)
    g.negfill = mk("negfill", [128, 128], F32)
    ASEL(g, g.caus01[:], g.onesf[:], [[-1, 128]], ALU.is_ge, 0.0, 0, 1, cb, cb)
    TS(g, "pool", g.negfill[:], g.caus01[:], -1.0, 1e30, ALU.add, ALU.mult, cb, cb)
    MS(g, "pool", g.resetm[:], 1.0, cb)
    MS(g, "pool", g.resetm[:].rearrange("p (c j) -> p c j", j=64)[:, :, 0:1], 0.0, cb)
    DMA(g, "sp", g.cs[:], g.cs_d.rearrange("(b p) i -> p b i", p=128), (), [g.csb])
    DMA(g, "sp", g.sn[:], g.sn_d.rearrange("(b p) i -> p b i", p=128), (), [g.snb])
    DMA(g, "sp", g.lbraw[:], g.lb_d.rearrange("l (h k) -> k l h", k=128), (), [g.lbb], allow_slow_non_contiguous=True)
    MS(g, "dve", g.lbv[:], 0.0, cb)
    TT(g, "dve", g.lbv[:, 1, :], g.lbraw[:, 1, :], g.lbraw[:, 0, :], ALU.subtract, [g.lbb], cb)
    ACT(g, g.lbv[:, 1, :], g.lbv[:, 1, :], AF.Sigmoid, cb, cb)
    TS(g, "dve", g.oml[:], g.lbv[:], -1.0, 1.0, ALU.mult, ALU.add, cb, cb)
    g.P.barrier()


class NormCtx:
    pass


def norm_setup(g, st, gain_d_row):
    c = NormCtx()
    c.gain = sb(g, st, [128, D], F32, "gain")
    c.gb = nb(st, "gain")
    DMA(g, "sp", c.gain[:], gain_d_row.to_broadcast([128, D]), (), [c.gb])
    c.junk = sb(g, st, [128, D], BF16, "junk")
    c.jb = nb(st, "junk")
    c.hbs = [sb(g, st, [128, D], BF16, "hb") for _ in range(2)]
    c.hbb = nbs(st, "hb", 2)
    c.ss = sb(g, st, [128, NB], F32, "ss")
    c.ssb = nbs(st, "ss", NB)
    MS(g, "dve", c.ss[:], 0.0, c.ssb)
    return c


def norm_block(g, c, xb, xbuf, tb, hT, hTb):
    hb, hbuf = c.hbs[tb % 2], c.hbb[tb % 2]
    s1 = c.ss[:, tb:tb + 1]
    ACT(g, c.junk[:], xb[:], AF.Square, [xbuf, c.ssb[tb]], [c.jb, c.ssb[tb]], accum_out=s1)
    ACT(g, s1, s1, AF.Ln, [c.ssb[tb]], [c.ssb[tb]], scale=1.0 / D, bias=EPS)
    ACT(g, s1, s1, AF.Exp, [c.ssb[tb]], [c.ssb[tb]], scale=-0.5)
    STT(g, hb[:], xb[:], s1, c.gain[:], ALU.mult, ALU.mult, [xbuf, c.ssb[tb], c.gb], [hbuf])
    for half in range(2):
        pt, ptb = psb(g)
        MM(g, [trf(pt[:, m * 128:(m + 1) * 128], hb[:, (half * 4 + m) * 128:(half * 4 + m + 1) * 128], g.ident[:])
               for m in range(4)], [hbuf, g.cb], [ptb])
        CP(g, "act" if half == 0 else "dve", hT[:, half * 4:half * 4 + 4, tb * 128:(tb + 1) * 128],
           pt[:, 0:512].rearrange("p (m j) -> p m j", j=128), [ptb], [hTb[tb]])


def norm_T(g, st, src_ap, src_bufs, gain_d_row, hT, hTb):
    c = norm_setup(g, st, gain_d_row)
    xbs = [sb(g, st, [128, D], F32, "xb") for _ in range(4)]
    xbb = nbs(st, "xb", 4)
    for tb in range(NB):
        xb, xbuf = xbs[tb % 4], xbb[tb % 4]
        DMA(g, "sp", xb[:], src_ap[tb * 128:(tb + 1) * 128, :], [src_bufs[tb]], [xbuf])
        norm_block(g, c, xb, xbuf, tb, hT, hTb)


def win_cols(g, l, c0, c1):
    return g.w_in[l].rearrange("(c p) n -> p c n", p=128)[:, :, c0:c1]


def mm_fm(g, ps, M, n, w, wb, col0, hT, hTb, tok0, role_bufs):
    MM(g, [mmf(ps[0:M, 0:n], w[:, c, col0:col0 + M], hT[:, c, tok0:tok0 + n], c == 0, c == DC - 1) for c in range(DC)],
       [wb] + hTb[tok0 // 128:(tok0 + n + 127) // 128], [role_bufs])


def mm_tm(g, ps, N, w, wb, col0, hT, hTb, tb, psbuf):
    MM(g, [mmf(ps[:, 0:N], hT[:, c, tb * 128:(tb + 1) * 128], w[:, c, col0:col0 + N], c == 0, c == DC - 1) for c in range(DC)],
       [wb, hTb[tb]], [psbuf])


def phase_sb(g, l, hT, hTb, osbT, osbb):
    with scope(g) as st:
        ws = []
        for i, c0 in enumerate((C_SQ, C_SK, C_SV)):
            if i == 0 and take_pre(g, ("sq", l)):
                ws.append((g.pre, g.preb))
                continue
            w = sb(g, st, [128, DC, 384], BF16, "wsb")
            wb = nb(st, "wsb%d" % i)
            DMA(g, "pool", w[:], win_cols(g, l, c0, c0 + 384), (), [wb])
            ws.append((w, wb))
        sqT = sb(g, st, [128, 3, T], BF16, "sqT")
        skT = sb(g, st, [128, 3, T], BF16, "skT")
        sqb = nbs(st, "sq", 3, 4)
        skb = nbs(st, "sk", 3, 4)
        k = 0
        for (dst, dstb, (w, wb), scl) in ((sqT, sqb, ws[0], 0.125), (skT, skb, ws[1], 1.0)):
            for hp in range(3):
                for tc in range(4):
                    ps, pb = psf(g, "proj", [0, 1, 2, 3, 4, 5])
                    mm_fm(g, ps, 128, 512, w, wb, hp * 128, hT, hTb, tc * 512, pb)
                    o = dst[:, hp, tc * 512:(tc + 1) * 512]
                    if k % 2 == 0:
                        ACT(g, o, ps[:, :], AF.Copy, [pb], [dstb[hp][tc]], scale=scl)
                    else:
                        TS(g, "dve", o, ps[:, :], scl, None, ALU.mult, None, [pb], [dstb[hp][tc]])
                    k += 1
        svp = [sb(g, st, [128, NB, 384], BF16, "svp") for _ in range(2)]
        svb = nbs(st, "sv", 2, NB)
        for s_ in range(2):
            MS(g, "pool", svp[s_][:].rearrange("p t c -> p (t c)"), 0.0, svb[s_])
        for tb in range(NB):
            ps, pb = psf(g, "proj", [0, 1, 2, 3, 4, 5])
            mm_tm(g, ps, 384, ws[2][0], ws[2][1], 0, hT, hTb, tb, pb)
            src = ps[:, 0:384].rearrange("p (m s d) -> p m s d", s=2, d=64)
            for s_ in range(2):
                dst = svp[s_][:, tb, :].rearrange("p (m s d) -> p m s d", s=2, d=64)
                CP(g, "act" if s_ == 0 else "dve", dst[:, :, s_, :], src[:, :, s_, :], [pb], [svb[s_][tb]])
        prefetch(g, ("wA", l), win_cols(g, l, C_DQ, C_DQ + 512), 512)
        R = 4
        mk2 = lambda shape, dt_, nm: [[sb(g, st, shape, dt_, nm) for _ in range(R)] for _ in range(2)]
        Et, SPt, SPs, At = mk2([128, 512], F32, "Et"), mk2([128, 512], BF16, "SPt"), mk2([128, 512], BF16, "SPs"), mk2([128, 512], BF16, "At")
        Etb, SPb, SPsb, Atb = nbs(st, "Et", 2, R), nbs(st, "SPt", 2, R), nbs(st, "SPs", 2, R), nbs(st, "At", 2, R)
        steps = []
        for m in range(3):
            for qc in range(4):
                for n_, kb in enumerate(range(4 * qc + 3, -1, -1)):
                    steps.append((m, qc, kb, n_))
        stt = {}
        pso = {}

        def info(t):
            m, qc, kb, n_ = steps[t]
            j0 = max(0, kb * 128 - qc * 512)
            return m, qc, kb, n_, j0, kb >= 4 * qc, qc * 512 + j0 - kb * 128, n_ == 0

        def opnds(t):
            m, qc, kb, n_, j0, diag, base, first = info(t)
            kk = [skT[64 * s_:64 * s_ + 64, m, kb * 128:(kb + 1) * 128] for s_ in range(2)]
            qq = [sqT[64 * s_:64 * s_ + 64, m, qc * 512 + j0:(qc + 1) * 512] for s_ in range(2)]
            return kk, qq, skb[m][kb // 4], sqb[m][qc]

        def s1(t):
            m, qc, kb, n_, j0, diag, base, first = info(t)
            kk, qq, rk, rq = opnds(t)
            pz = [psf(g, "sbZ", [0, 1, 2]) for _ in range(2)]
            stt[t] = {"pz": pz}
            for s_ in range(2):
                MM(g, [mmf(pz[s_][0][:, j0:512], kk[s_], qq[s_], True, True)], [rk, rq], [pz[s_][1]])

        def s2(t):
            m, qc, kb, n_, j0, diag, base, first = info(t)
            ib = t % R
            pz = stt[t]["pz"]
            for s_ in range(2):
                ACT(g, Et[s_][ib][:, j0:512], pz[s_][0][:, j0:512], AF.Exp, [pz[s_][1]], [Etb[s_][ib]])

        def s3(t):
            m, qc, kb, n_, j0, diag, base, first = info(t)
            ib = t % R
            for s_ in range(2):
                ACT(g, SPt[s_][ib][:, j0:512], Et[s_][ib][:, j0:512], AF.Ln, [Etb[s_][ib]], [SPb[s_][ib]], bias=1.0)
            if diag:
                for s_ in range(2):
                    S = SPt[s_][ib]
                    assert base == 0
                    TT(g, "pool", S[:, j0:512], S[:, j0:512], g.mlt[:, 0:512 - j0], ALU.mult, [SPb[s_][ib], g.cb], [SPb[s_][ib]])
            if kb > 0:
                for s_ in range(2):
                    S, Sb = SPt[s_][ib], SPb[s_][ib]
                    Sn, Snb = SPs[s_][n_ % R], SPsb[s_][n_ % R]
                    if first:
                        if j0 > 0:
                            MS(g, "pool", Sn[:, 0:j0], 0.0, [Snb])
                        CP(g, "pool", Sn[:, j0:512], S[:, j0:512], [Sb], [Snb])
                    else:
                        So, Sob = SPs[s_][(n_ - 1) % R], SPsb[s_][(n_ - 1) % R]
                        if j0 > 0:
                            CP(g, "pool", Sn[:, 0:j0], So[:, 0:j0], [Sob], [Snb])
                        TT(g, "dve", Sn[:, j0:512], So[:, j0:512], S[:, j0:512], ALU.add, [Sob, Sb], [Snb])

        def s4(t):
            m, qc, kb, n_, j0, diag, base, first = info(t)
            ib = t % R
            kk, qq, rk, rq = opnds(t)
            pc = [psf(g, "sbC", [3, 4]) for _ in range(2)]
            stt[t]["pc"] = pc
            for s_ in range(2):
                S = SPt[s_][ib]
                fns = [mmf(pc[s_][0][:, j0:512], kk[s_], qq[s_], True, False),
                       mmf(pc[s_][0][:, j0:512], g.negtri[:], S[:, j0:512], False, first)]
                rd = [rk, rq, SPb[s_][ib], g.cb]
                if not first:
                    So, Sob = SPs[s_][(n_ - 1) % R], SPsb[s_][(n_ - 1) % R]
                    fns.append(mmf(pc[s_][0][:, j0:512], g.negones[:], So[:, j0:512], False, True))
                    rd.append(Sob)
                MM(g, fns, rd, [pc[s_][1]])

        def s5(t):
            m, qc, kb, n_, j0, diag, base, first = info(t)
            ib = t % R
            pc = stt[t]["pc"]
            for s_ in range(2):
                ACT(g, At[s_][ib][:, j0:512], pc[s_][0][:, j0:512], AF.Exp, [pc[s_][1]], [Atb[s_][ib]])
            for s_ in range(2):
                A, Ab = At[s_][ib], Atb[s_][ib]
                if diag:
                    TT(g, "pool", A[:, j0:512], A[:, j0:512], g.mlt[:, 0:512 - j0], ALU.mult, [Ab, g.cb], [Ab])
                if first and j0 > 0:
                    MS(g, "pool", A[:, 0:j0], 0.0, [Ab])

        def s6(t):
            m, qc, kb, n_, j0, diag, base, first = info(t)
            ib = t % R
            if first:
                pso[(m, qc)] = psf(g, "sbO", [5])
            psO, pOb = pso[(m, qc)]
            for s_ in range(2):
                A, Ab = At[s_][ib], Atb[s_][ib]
                vv = svp[s_][:, kb, m * 128:(m + 1) * 128]
                c0 = 0 if first else j0
                MM(g, [mmf(psO[:, c0:512], vv, A[:, c0:512], first and s_ == 0, kb == 0 and s_ == 1)], [svb[s_][kb], Ab], [pOb])
            if kb == 0:
                CP(g, "act" if (m * 4 + qc) % 2 == 0 else "dve", osbT[:, m, qc * 512:(qc + 1) * 512], psO[:, :], [pOb], [osbb[m][qc]])
            del stt[t]

        nst = len(steps)
        stages = [s1, s2, s3, s4, s5, s6]
        for e in range(nst + len(stages) - 1):
            for k in range(len(stages) - 1, -1, -1):
                t = e - k
                if 0 <= t < nst:
                    stages[k](t)


def phase_dsa(g, l, hT, hTb, odT, odb):
    P = g.P
    with scope(g) as st:
        featT = sb(g, st, [128, 9, T], BF16, "featT")
        fb = nbs(st, "feat", NB)
        dvx = sb(g, st, [128, NB, 128], BF16, "dvx")
        dvb = nbs(st, "dvx", NB)
        sgn = sb(g, st, [128, NB, 8], F32, "sgn")
        sgb = nbs(st, "sgn", NB)
        qkg = sb(g, st, [128, 7, 64], F32, "qkg")
        qkgb = nbs(st, "qkg", 7)
        for hh in range(7):
            src = (g.qn if hh < 6 else g.kn)[l:l + 1, :].to_broadcast([128, 64])
            DMA(g, "sp", qkg[:, hh, :], src, (), [qkgb[hh]])
        with scope(g) as s2:
            wB = sb(g, s2, [128, DC, 512], BF16, "wB")
            wC = sb(g, s2, [128, DC, 72], BF16, "wC")
            wBb, wCb = nb(s2, "wB"), nb(s2, "wC")
            if take_pre(g, ("wA", l)):
                wA, wAb = g.pre, g.preb
            else:
                wA = sb(g, s2, [128, DC, 512], BF16, "wA")
                wAb = nb(s2, "wA")
                DMA(g, "pool", wA[:], win_cols(g, l, C_DQ, C_DQ + 512), (), [wAb])
            DMA(g, "pool", wB[:], win_cols(g, l, C_IQ, C_IQ + 512), (), [wBb])
            DMA(g, "pool", wC[:], win_cols(g, l, C_IK, C_IK + 72), (), [wCb])
            NR = 4
            tq = [sb(g, s2, [128, 18, 64], F32, "tokq") for _ in range(NR)]
            tqb = nbs(s2, "tokq", NR)
            tbq = [sb(g, s2, [128, 18, 64], BF16, "tokb") for _ in range(3)]
            tbb = nbs(s2, "tokb", 3)
            sqt = [sb(g, s2, [128, 448], F32, "sqt") for _ in range(2)]
            sqtb = nbs(s2, "sqt", 2)
            smq = [sb(g, s2, [128, 32], F32, "small") for _ in range(2)]
            smqb = nbs(s2, "small", 2)
            rt = [sb(g, s2, [128, 18, 8], F32, "ropet") for _ in range(4)]
            rtb = nbs(s2, "ropet", 4)
            pst = {}

            def p1(tb):
                MS(g, "pool", dvx[:, tb, 64:128], 1.0, [dvb[tb]])
                tk, tkb = tq[tb % NR], tqb[tb % NR]
                MS(g, "pool", tk[:, 7, :], 0.0, [tkb])
                MS(g, "pool", tk[:, 17, :], 0.0, [tkb])
                psA, pAb = psf(g, "dA", [0, 1])
                psBq, pBb = psf(g, "dB", [2, 3])
                psC, pCb = psf(g, "dC", [4, 5])
                pst[tb] = (psA, pAb, psBq, pBb, psC, pCb)
                mm_tm(g, psA, 512, wA, wAb, 0, hT, hTb, tb, pAb)
                mm_tm(g, psBq, 512, wB, wBb, 0, hT, hTb, tb, pBb)
                mm_tm(g, psC, 72, wC, wCb, 0, hT, hTb, tb, pCb)

            def p2(tb):
                psA, pAb, psBq, pBb, psC, pCb = pst.pop(tb)
                tk, tkb = tq[tb % NR], tqb[tb % NR]
                sq_, sq_b = sqt[tb % 2], sqtb[tb % 2]
                sm, smb = smq[tb % 2], smqb[tb % 2]
                ACT(g, sq_[:], psA[:, 0:448], AF.Square, [pAb], [sq_b])
                ss = sm[:, 0:7]
                P.op("dve", lambda e, ss=ss, sq_=sq_: e.tensor_reduce(out=ss, in_=sq_[:].rearrange("p (h d) -> p h d", d=64), axis=AX.X,
                                                                      op=ALU.add), [sq_b], [smb], 448)
                ACT(g, ss, ss, AF.Ln, [smb], [smb], scale=1.0 / 64, bias=EPS)
                ACT(g, ss, ss, AF.Exp, [smb], [smb], scale=-0.5)
                aw = sm[:, 8:16]
                TS(g, "dve", sgn[:, tb, :], psC[:, 64:72], 0.0, 2.0, ALU.is_gt, ALU.mult, [pCb], [sgb[tb]])
                TS(g, "dve", sgn[:, tb, :], sgn[:, tb, :], -1.0, 0.0, ALU.add, ALU.add, [sgb[tb]], [sgb[tb]])
                STT(g, aw, psC[:, 64:72], IDX_SCALE, sgn[:, tb, :], ALU.mult, ALU.mult, [pCb, sgb[tb]], [smb])
                TT(g, "dve", tk[:, 8:16, :], psBq[:, :].rearrange("p (h d) -> p h d", d=64),
                   aw.unsqueeze(2).to_broadcast([128, 8, 64]), ALU.mult, [pBb, smb], [tkb])
                CP(g, "act", tk[:, 16, :], psC[:, 0:64], [pCb], [tkb])
                CP(g, "act", dvx[:, tb, 0:64], psA[:, 448:512], [pAb], [dvb[tb]])
                TT(g, "dve", tk[:, 0:7, :], psA[:, 0:448].rearrange("p (h d) -> p h d", d=64),
                   ss.unsqueeze(2).to_broadcast([128, 7, 64]), ALU.mult, [pAb, smb], [tkb])
                TT(g, "dve", tk[:, 0:7, :], tk[:, 0:7, :], qkg[:], ALU.mult, [tkb] + qkgb, [tkb])

            def p3(tb):
                tk, tkb = tq[tb % NR], tqb[tb % NR]
                x1, x2 = tk[:, :, 0:8], tk[:, :, 8:16]
                cb_ = g.cs[:, tb, :].unsqueeze(1).to_broadcast([128, 18, 8])
                sb_ = g.sn[:, tb, :].unsqueeze(1).to_broadcast([128, 18, 8])
                TT(g, "dve", rt[0][:], x1, cb_, ALU.mult, [tkb, g.csb], [rtb[0]])
                TT(g, "pool", rt[1][:], x2, sb_, ALU.mult, [tkb, g.snb], [rtb[1]])
                TT(g, "dve", rt[2][:], x2, cb_, ALU.mult, [tkb, g.csb], [rtb[2]])
                TT(g, "pool", rt[3][:], x1, sb_, ALU.mult, [tkb, g.snb], [rtb[3]])
                TT(g, "dve", x1, rt[0][:], rt[1][:], ALU.subtract, [rtb[0], rtb[1]], [tkb])
                TT(g, "pool", x2, rt[2][:], rt[3][:], ALU.add, [rtb[2], rtb[3]], [tkb])

            def p4(tb):
                tk, tkb = tq[tb % NR], tqb[tb % NR]
                tkh, tkhb = tbq[tb % 3], tbb[tb % 3]
                CP(g, "act", tkh[:], tk[:], [tkb], [tkhb])
                CP(g, "pool", tkh[:, 7, :], tkh[:, 6, :], [tkhb], [tkhb])
                CP(g, "pool", tkh[:, 17, :], tkh[:, 16, :], [tkhb], [tkhb])

            def p5(tb):
                tkh, tkhb = tbq[tb % 3], tbb[tb % 3]
                flat = tkh[:].rearrange("p h d -> p (h d)")
                pt, ptb = psb(g)
                MM(g, [trf(pt[:, m * 128:(m + 1) * 128], flat[:, m * 128:(m + 1) * 128], g.ident[:]) for m in range(8)],
                   [tkhb, g.cb], [ptb])
                CP(g, "dve", featT[:, 0:8, tb * 128:(tb + 1) * 128], pt[:, :].rearrange("p (m j) -> p m j", j=128), [ptb], [fb[tb]])
                pt2, ptb2 = psb(g)
                MM(g, [trf(pt2[:, 0:128], flat[:, 1024:1152], g.ident[:])], [tkhb, g.cb], [ptb2])
                CP(g, "act", featT[:, 8, tb * 128:(tb + 1) * 128], pt2[:, 0:128], [ptb2], [fb[tb]])

            stages = [p1, p2, p3, p4, p5]
            for e in range(NB + len(stages) - 1):
                for k in range(len(stages) - 1, -1, -1):
                    t_ = e - k
                    if 0 <= t_ < NB:
                        stages[k](t_)
        prefetch(g, ("hihg", l), win_cols(g, l, C_HI, C_HI + 512), 512)
        sc = [sb(g, st, [128, T], F32, "sc") for _ in range(4)]
        scb = nbs(st, "sc", 4, 4)
        junk = sb(g, st, [128, T], BF16, "junk")
        maskq = [sb(g, st, [128, T], BF16, "maskq") for _ in range(2)]
        mqb = nbs(st, "maskq", 2)
        maskT = sb(g, st, [128, NB, 512], BF16, "maskT")
        mTb = nbs(st, "maskT", 4)
        rj = [sb(g, st, [128, 512], BF16, "rj") for _ in range(4)]
        rjb = nbs(st, "rj", 4)
        dg = [sb(g, st, [128, 8, 128], BF16, "dg") for _ in range(2)]
        dgb = nbs(st, "dg", 2)
        sm = [sb(g, st, [128, 8 + 2 * N_BISECT], F32, "bis") for _ in range(2)]
        smb = nbs(st, "bis", 2)
        cvec = sb(g, st, [128, N_BISECT], F32, "cvec")
        c255 = sb(g, st, [128, 1], F32, "c255")
        cvb = nb(st, "cvec")
        for n_ in range(N_BISECT):
            MS(g, "pool", cvec[:, n_:n_ + 1], 2.0 ** -(n_ + 1), [cvb])
        MS(g, "pool", c255[:], 255.5, [cvb])
        Pt = [sb(g, st, [128, 512], BF16, "Pt") for _ in range(4)]
        Ptb = nbs(st, "Pt", 4)
        Pm = [sb(g, st, [128, 512], BF16, "Pm") for _ in range(4)]
        Pmb = nbs(st, "Pm", 4)
        rs = sb(g, st, [64, 512], F32, "rs")
        rsb = nb(st, "rs")
        cnt_ = {"ri": 0, "pi": 0, "pm": 0}

        def idx_blocks(blocks):
            tiles = []
            for i in blocks:
                nk = (i + 1) * 128
                d_, d_b = dg[i % 2], dgb[i % 2]
                for j in range(8):
                    TS(g, "pool", d_[:, j, :], g.ident[:], sgn[:, i, j:j + 1], None, ALU.mult, None, [g.cb, sgb[i]], [d_b])
                for kc in range((nk + 511) // 512):
                    n = min(512, nk - kc * 512)
                    for j in range(8):
                        tiles.append((i, kc, n, j))
            stt = {}

            def s1(t):
                i, kc, n, j = tiles[t]
                po = 64 * (j % 2)
                psZ, pZb = psf(g, "ixZ", [0, 1, 2])
                stt[t] = [psZ, pZb]
                MM(g, [mmf(psZ[:, 0:n], featT[po:po + 64, 4 + j // 2, i * 128:(i + 1) * 128],
                           featT[po:po + 64, 8, kc * 512:kc * 512 + n], True, True)],
                   [fb[i]] + fb[kc * 4:(kc * 512 + n) // 128], [pZb])

            def s2(t):
                i, kc, n, j = tiles[t]
                psZ, pZb = stt[t]
                r_, r_b = rj[cnt_["ri"] % 4], rjb[cnt_["ri"] % 4]
                cnt_["ri"] += 1
                stt[t] += [r_, r_b]
                ACT(g, r_[:, 0:n], psZ[:, 0:n], AF.Relu, [pZb], [r_b])

            def s3(t):
                i, kc, n, j = tiles[t]
                r_, r_b = stt[t][2], stt[t][3]
                if j == 0:
                    cnt_["psS"] = psf(g, "ixS", [3, 4])
                psS, pSb = cnt_["psS"]
                d_, d_b = dg[i % 2], dgb[i % 2]
                MM(g, [mmf(psS[:, 0:n], d_[:, j, :], r_[:, 0:n], j == 0, j == 7)], [d_b, r_b], [pSb])
                if j == 7:
                    CP(g, "act", sc[i % 4][:, kc * 512:kc * 512 + n], psS[:, 0:n], [pSb], [scb[i % 4][kc]])
                del stt[t]

            pipeline([s1, s2, s3], len(tiles))

        def bis_pair(p):
            blocks = [2 * p, 2 * p + 1]
            st_ = []
            for bi, i in enumerate(blocks):
                nk = (i + 1) * 128
                s_, s_b = sc[i % 4], scb[i % 4]
                nkc = (nk + 511) // 512
                srd = s_b[0:nkc]
                m_, m_b = sm[bi], smb[bi]
                rmax, rmin, step0, mid, cntv, tt = (m_[:, c:c + 1] for c in range(6))
                stepc = m_[:, 8:8 + N_BISECT]
                if nk > 256:
                    P.op("dve", lambda e, s_=s_, nk=nk, rmax=rmax: e.tensor_reduce(out=rmax, in_=s_[:, 0:nk], axis=AX.X, op=ALU.max), srd, [m_b], nk)
                    P.op("dve", lambda e, s_=s_, nk=nk, rmin=rmin: e.tensor_reduce(out=rmin, in_=s_[:, 0:nk], axis=AX.X, op=ALU.min), srd, [m_b], nk)
                dsl = s_[:, i * 128:(i + 1) * 128]
                TT(g, "pool", dsl, dsl, g.caus01[:], ALU.mult, [s_b[i // 4], g.cb], [s_b[i // 4]])
                TT(g, "pool", dsl, dsl, g.negfill[:], ALU.add, [s_b[i // 4], g.cb], [s_b[i // 4]])
                st_.append((i, nk, s_, srd, m_, m_b, rmax, rmin, step0, mid, cntv, tt, stepc))
            act = [x for x in st_ if x[1] > 256]
            for (i, nk, s_, srd, m_, m_b, rmax, rmin, step0, mid, cntv, tt, stepc) in act:
                TT(g, "dve", step0, rmax, rmin, ALU.subtract, [m_b], [m_b])
            for (i, nk, s_, srd, m_, m_b, rmax, rmin, step0, mid, cntv, tt, stepc) in act:
                TS(g, "dve", stepc, cvec[:], step0, None, ALU.mult, None, [m_b, cvb], [m_b])
            for (i, nk, s_, srd, m_, m_b, rmax, rmin, step0, mid, cntv, tt, stepc) in act:
                TS(g, "dve", mid, stepc[:, 0:1], rmin, g.zc[:, 0:1], ALU.add, ALU.add, [m_b, g.cb], [m_b])
            for n_ in range(N_BISECT):
                for (i, nk, s_, srd, m_, m_b, rmax, rmin, step0, mid, cntv, tt, stepc) in act:
                    TS(g, "dve", junk[:, 0:nk], s_[:, 0:nk], mid, g.zc[:, 0:1], ALU.is_ge, ALU.add, srd + [m_b, g.cb], [m_b], accum_out=cntv)
                for (i, nk, s_, srd, m_, m_b, rmax, rmin, step0, mid, cntv, tt, stepc) in act:
                    TS(g, "dve", tt, cntv, c255[:, 0:1], stepc[:, n_:n_ + 1], ALU.is_ge, ALU.mult, [m_b, cvb], [m_b])
                for (i, nk, s_, srd, m_, m_b, rmax, rmin, step0, mid, cntv, tt, stepc) in act:
                    nn = min(n_ + 1, N_BISECT - 1)
                    TS(g, "dve", mid, tt, stepc[:, nn:nn + 1], mid, ALU.subtract, ALU.add, [m_b], [m_b])
            for (i, nk, s_, srd, m_, m_b, rmax, rmin, step0, mid, cntv, tt, stepc) in st_:
                thr = mid if nk > 256 else g.negbig[:, 0:1]
                mq, mq_b = maskq[i % 2], mqb[i % 2]
                TS(g, "dve", mq[:, 0:nk], s_[:, 0:nk], thr, None, ALU.is_ge, None, srd + [m_b, g.cb], [mq_b])

        def mT_pair(p):
            for i in (2 * p, 2 * p + 1):
                ii = i % 4
                mq, mq_b = maskq[i % 2], mqb[i % 2]
                for k0 in range(0, i + 1, 8):
                    k1 = min(i + 1, k0 + 8)
                    pt, ptb = psb(g)
                    MM(g, [trf(pt[:, (kb - k0) * 128:(kb - k0 + 1) * 128], mq[:, kb * 128:(kb + 1) * 128], g.ident[:])
                           for kb in range(k0, k1)], [mq_b, g.cb], [ptb])
                    CP(g, "act", maskT[:, k0:k1, ii * 128:(ii + 1) * 128],
                       pt[:, 0:(k1 - k0) * 128].rearrange("p (m j) -> p m j", j=128), [ptb], [mTb[ii]])

        def att_chunk(qc):
            last = 4 * qc + 3
            tiles = [(h, kb) for h in range(6) for kb in range(last + 1)]
            stt = {}
            pso = {}

            def s1(t):
                h, kb = tiles[t]
                hp, po = h // 2, 64 * (h % 2)
                j0 = max(0, kb * 128 - qc * 512)
                psL, pLb = psf(g, "dsL", [0, 1, 2])
                stt[t] = [psL, pLb]
                MM(g, [mmf(psL[:, j0:512], featT[po:po + 64, 3, kb * 128:(kb + 1) * 128],
                           featT[po:po + 64, hp, qc * 512 + j0:(qc + 1) * 512], True, True)],
                   [fb[kb]] + fb[qc * 4:qc * 4 + 4], [pLb])

            def s2(t):
                h, kb = tiles[t]
                j0 = max(0, kb * 128 - qc * 512)
                psL, pLb = stt[t][0], stt[t][1]
                pi = cnt_["pi"]
                cnt_["pi"] += 1
                p_, p_b = Pt[pi % 4], Ptb[pi % 4]
                stt[t] += [p_, p_b]
                ACT(g, p_[:, j0:512], psL[:, j0:512], AF.Exp, [pLb], [p_b], scale=0.125)

            def s3(t):
                h, kb = tiles[t]
                j0 = max(0, kb * 128 - qc * 512)
                p_, p_b = stt[t][2], stt[t][3]
                pm = cnt_["pm"]
                cnt_["pm"] += 1
                m_, m_b = Pm[pm % 4], Pmb[pm % 4]
                stt[t] += [m_, m_b]
                TT(g, "pool" if t % 3 == 0 else "dve", m_[:, j0:512], p_[:, j0:512], maskT[:, kb, j0:512], ALU.mult,
                   [p_b] + mTb[j0 // 128:4], [m_b])

            def s4(t):
                h, kb = tiles[t]
                hp, po = h // 2, 64 * (h % 2)
                j0 = max(0, kb * 128 - qc * 512)
                m_, m_b = stt[t][4], stt[t][5]
                if kb == 0:
                    pso[h] = psf(g, "dsO", [4, 5])
                psO, pOb = pso[h]
                MM(g, [mmf(psO[:, j0:512], dvx[:, kb, :], m_[:, j0:512], kb == 0, kb == last)], [dvb[kb], m_b], [pOb])
                if kb == last:
                    ACT(g, rs[0:64, :], psO[64:128, :], AF.Ln, [pOb], [rsb])
                    ACT(g, rs[0:64, :], rs[0:64, :], AF.Exp, [rsb], [rsb], scale=-1.0)
                    TT(g, "dve", odT[po:po + 64, hp, qc * 512:(qc + 1) * 512], psO[0:64, :], rs[0:64, :], ALU.mult, [pOb, rsb],
                       [odb[hp][qc]])
                del stt[t]

            pipeline([s1, s2, s3, s4], len(tiles))

        idx_blocks([14, 15])
        for p in range(7, -1, -1):
            if p > 0:
                idx_blocks([2 * p - 2, 2 * p - 1])
            bis_pair(p)
            mT_pair(p)
            if p % 2 == 0:
                att_chunk(p // 2)


def phase_hgrn(g, l, hT, hTb, ohT, ohb):
    P = g.P
    import os
    if int(os.environ.get("HGL", "9")) == 0:
        return
    with scope(g) as st:
        hi_tm = sb(g, st, [128, NB, 256], BF16, "hi_tm")
        hib = nbs(st, "hi", NB)
        hgs = sb(g, st, [128, NB, 256], BF16, "hgs")
        hgb = nbs(st, "hgs", NB)
        onb = sb(g, st, [128, 64], F32, "onorm")
        onbb = nb(st, "onorm")
        DMA(g, "sp", onb[:], g.onorm[l:l + 1, :].to_broadcast([128, 64]), (), [onbb])
        with scope(g) as s2:
            if take_pre(g, ("hihg", l)):
                w, wb = g.pre, g.preb
            else:
                w = sb(g, s2, [128, DC, 512], BF16, "whihg")
                wb = nb(s2, "whihg")
                DMA(g, "pool", w[:], win_cols(g, l, C_HI, C_HI + 512), (), [wb])
            sgs = [sb(g, s2, [128, 256], F32, "sgs") for _ in range(2)]
            sgsb = nbs(s2, "sgs", 2)
            for tb in range(NB):
                ps, pb = psf(g, "proj", [0, 1, 2, 3, 4, 5])
                hgv = int(os.environ.get("HGV", "15"))
                if hgv & 8:
                    mm_tm(g, ps, 512, w, wb, 0, hT, hTb, tb, pb)
                if hgv & 1:
                    CP(g, "dve", hi_tm[:, tb, :], ps[:, 0:256], [pb], [hib[tb]])
                sgt, sgtb = sgs[tb % 2], sgsb[tb % 2]
                if hgv & 2:
                    ACT(g, sgt[:], ps[:, 256:512], AF.Exp if hgv & 16 else AF.Sigmoid, [pb], [sgtb])
                if hgv & 4:
                    TT(g, "dve", hgs[:, tb, :], ps[:, 256:512], sgt[:], ALU.mult, [pb, sgtb], [hgb[tb]])
        with scope(g) as s3:
            NH = 4
            R = 4
            qtT = sb(g, s3, [128, NH, T], BF16, "qtT")
            ktT = sb(g, s3, [128, NH, T], BF16, "ktT")
            qtb = nbs(s3, "qt", NH)
            ktb = nbs(s3, "kt", NH)
            kt_tm = sb(g, s3, [128, NB, NH * 128], BF16, "kt_tm")
            kttb = nbs(s3, "kttm", NH)
            t1 = sb(g, s3, [128, T], F32, "t1")
            t2 = sb(g, s3, [128, T], F32, "t2")
            t3 = sb(g, s3, [128, T], F32, "t3")
            t1h, t2h, t3h = nbs(s3, "t1", 4), nbs(s3, "t2", 4), nbs(s3, "t3", 4)
            ebl = sb(g, s3, [128, NH, 32], F32, "ebl")
            eblb = nb(s3, "ebl")
            W = [sb(g, s3, [128, NH, 64], F32, "W") for _ in range(2)]
            Wb = nbs(s3, "W", 2)
            Sbf = [sb(g, s3, [128, NH, 64], BF16, "Sbf") for _ in range(R)]
            Sbfb = nbs(s3, "Sbf", R)
            attm = [sb(g, s3, [128, NH, 128], BF16, "attm") for _ in range(3)]
            attb = nbs(s3, "attm", 3)
            o_tm = [sb(g, s3, [128, NH * 64], F32, "o_tm") for _ in range(3)]
            otb = nbs(s3, "otm", 3)
            osq = sb(g, s3, [128, NH * 64], F32, "osq")
            osqb = nb(s3, "osq")
            og = [sb(g, s3, [128, NH * 64], BF16, "og") for _ in range(2)]
            ogb = nbs(s3, "og", 2)
            sm = [sb(g, s3, [128, 4], F32, "hsm") for _ in range(2)]
            smb = nbs(s3, "hsm", 2)
            wq = [sb(g, s3, [128, DC, 128], BF16, "wq") for _ in range(2)]
            wqb = nbs(s3, "wq", 2)
            wf = [sb(g, s3, [128, DC, 128], BF16, "wf") for _ in range(2)]
            wfb = nbs(s3, "wf", 2)

            def load_head(hd):
                DMA(g, "pool", wf[hd % 2][:], win_cols(g, l, C_HF + hd * 128, C_HF + hd * 128 + 128), (), [wfb[hd % 2]])
                DMA(g, "pool", wq[hd % 2][:], win_cols(g, l, C_HQ + hd * 128, C_HQ + hd * 128 + 128), (), [wqb[hd % 2]])

            def kt_transposes(hd):
                for k0 in (0, 8):
                    pt, ptb = psb(g)
                    MM(g, [trf(pt[:, m * 128:(m + 1) * 128], ktT[:, hd, (k0 + m) * 128:(k0 + m + 1) * 128], g.ident[:])
                           for m in range(8)], [ktb[hd], g.cb], [ptb])
                    CP(g, "act" if k0 == 0 else "dve", kt_tm[:, k0:k0 + 8, hd * 128:(hd + 1) * 128],
                       pt[:, :].rearrange("p (m j) -> p m j", j=128), [ptb], [kttb[hd]])

            load_head(0)
            for hd in range(NH):
                if hd + 1 < NH:
                    load_head(hd + 1)
                w_f, w_fb, w_q, w_qb = wf[hd % 2], wfb[hd % 2], wq[hd % 2], wqb[hd % 2]
                NQ = 4
                HS = [slice(q_ * (T // NQ), (q_ + 1) * (T // NQ)) for q_ in range(NQ)]
                for tc in range(4):
                    ps, pb = psf(g, "proj", [0, 1, 2, 3, 4, 5])
                    mm_fm(g, ps, 128, 512, w_f, w_fb, 0, hT, hTb, tc * 512, pb)
                    ACT(g, t1[:, tc * 512:(tc + 1) * 512], ps[:, :], AF.Sigmoid, [pb], [t1h[tc]])
                for hf in range(NQ):
                    TS(g, "dve", t1[:, HS[hf]], t1[:, HS[hf]], g.oml[:, l, hd:hd + 1], g.lbv[:, l, hd:hd + 1], ALU.mult, ALU.add,
                       [t1h[hf], g.cb], [t1h[hf]])
                for hf in range(NQ):
                    ACT(g, t2[:, HS[hf]], t1[:, HS[hf]], AF.Copy, [t1h[hf]], [t2h[hf]], scale=-1.0, bias=1.0)
                for hf in range(NQ):
                    TS(g, "dve", t1[:, HS[hf]], t1[:, HS[hf]], F_MIN, None, ALU.max, None, [t1h[hf]], [t1h[hf]])
                for hf in range(NQ):
                    ACT(g, t1[:, HS[hf]], t1[:, HS[hf]], AF.Ln, [t1h[hf]], [t1h[hf]])
                for hf in range(NQ):
                    P.op("dve", lambda e, hf=hf: e.tensor_tensor_scan(out=t3[:, HS[hf]], data0=g.resetm[:, HS[hf]], data1=t1[:, HS[hf]],
                                                                     initial=0.0, op0=ALU.mult, op1=ALU.add),
                         [t1h[hf], g.cb], [t3h[hf]], 2 * T // NQ)
                for hf in range(NQ):
                    TS(g, "dve", t3[:, HS[hf]], t3[:, HS[hf]], -80.0, None, ALU.max, None, [t3h[hf]], [t3h[hf]])
                for hf in range(NQ):
                    ACT(g, t1[:, HS[hf]], t3[:, HS[hf]], AF.Exp, [t3h[hf]], [t1h[hf]])
                for hf in range(NQ):
                    CP(g, "pool", ebl[:, hd, hf * (32 // NQ):(hf + 1) * (32 // NQ)].unsqueeze(2),
                       t1[:, HS[hf]].rearrange("p (c j) -> p c j", j=64)[:, :, 63:64], [t1h[hf]], [eblb])
                for hf in range(NQ):
                    ACT(g, t3[:, HS[hf]], t3[:, HS[hf]], AF.Exp, [t3h[hf]], [t3h[hf]], scale=-1.0)
                for hf in range(NQ):
                    TT(g, "dve", ktT[:, hd, HS[hf]], t2[:, HS[hf]], t3[:, HS[hf]], ALU.mult, [t2h[hf], t3h[hf]], [ktb[hd]])
                for tc in range(4):
                    ps, pb = psf(g, "proj", [0, 1, 2, 3, 4, 5])
                    mm_fm(g, ps, 128, 512, w_q, w_qb, 0, hT, hTb, tc * 512, pb)
                    ACT(g, t2[:, tc * 512:(tc + 1) * 512], ps[:, :], AF.Sigmoid, [pb], [t2h[tc]])
                    TT(g, "dve", t2[:, tc * 512:(tc + 1) * 512], ps[:, :], t2[:, tc * 512:(tc + 1) * 512], ALU.mult, [pb, t2h[tc]],
                       [t2h[tc]])
                for hf in range(NQ):
                    TT(g, "dve", qtT[:, hd, HS[hf]], t2[:, HS[hf]], t1[:, HS[hf]], ALU.mult, [t2h[hf], t1h[hf]], [qtb[hd]])
                if hd > 0:
                    kt_transposes(hd - 1)

            kt_transposes(NH - 1)
            NCH = 32
            xps = {}

            def o_transposes(tb):
                og_, og_b = og[tb % 2], ogb[tb % 2]
                pt, ptb = psb(g)
                MM(g, [trf(pt[:, m * 128:(m + 1) * 128], og_[:, m * 128:(m + 1) * 128], g.ident[:]) for m in range(2)],
                   [og_b, g.cb], [ptb])
                CP(g, "act", ohT[:, 0:2, tb * 128:(tb + 1) * 128], pt[:, 0:256].rearrange("p (m j) -> p m j", j=128), [ptb],
                   [ohb[0][tb // 4], ohb[1][tb // 4]])

            def emit_X(c):
                tb, pr = c // 2, (c % 2) * 64
                psX, pXb = psf(g, "hgX", [0, 1, 2])
                xps[c] = (psX, pXb)
                MM(g, [mmf(psX[:, hh * 64:(hh + 1) * 64], kt_tm[pr:pr + 64, tb, hh * 128:(hh + 1) * 128],
                           hi_tm[pr:pr + 64, tb, hh * 64:(hh + 1) * 64], True, True) for hh in range(NH)], kttb + [hib[tb]], [pXb])

            def emit_A(tb):
                psA, pAb = psf(g, "hgA", [3])
                MM(g, [mmf(psA[:, hh * 128:(hh + 1) * 128], ktT[:, hh, tb * 128:(tb + 1) * 128],
                           qtT[:, hh, tb * 128:(tb + 1) * 128], True, True) for hh in range(NH)], ktb + qtb, [pAb])
                TT(g, "dve", attm[tb % 3][:], psA[:, :].rearrange("p (h t) -> p h t", t=128),
                   g.maskbd[:].unsqueeze(1).to_broadcast([128, NH, 128]), ALU.mult, [pAb, g.cb], [attb[tb % 3]])

            emit_X(0)
            emit_X(1)
            emit_A(0)
            CP(g, "dve", W[0][:], xps[0][0][:, 0:NH * 64].rearrange("p (h v) -> p h v", v=64), [xps[0][1]], [Wb[0]])
            MS(g, "pool", Sbf[0][:], 0.0, [Sbfb[0]])
            for c in range(NCH):
                tb, half = c // 2, c % 2
                pr = half * 64
                if c + 2 < NCH:
                    emit_X(c + 2)
                if half == 0 and tb + 1 < NB:
                    emit_A(tb + 1)
                if c + 1 < NCH:
                    eb_ = ebl[:, :, c:c + 1].to_broadcast([128, NH, 64])
                    TT(g, "pool", Sbf[(c + 1) % R][:], W[c % 2][:], eb_, ALU.mult, [Wb[c % 2], eblb], [Sbfb[(c + 1) % R]])
                    psX, pXb = xps.pop(c + 1)
                    for hh in range(NH):
                        STT(g, W[(c + 1) % 2][:, hh, :], W[c % 2][:, hh, :], ebl[:, hh, c:c + 1], psX[:, hh * 64:(hh + 1) * 64],
                            ALU.mult, ALU.add, [Wb[c % 2], eblb, pXb], [Wb[(c + 1) % 2]])
                am, amb = attm[tb % 3], attb[tb % 3]
                ot, otbuf = o_tm[tb % 3], otb[tb % 3]
                psO, pOb = psf(g, "hgO", [4, 5])
                fns = []
                for hh in range(NH):
                    fns.append(mmf(psO[0:64, hh * 64:(hh + 1) * 64], am[pr:pr + 64, hh, pr:pr + 64],
                                   hi_tm[pr:pr + 64, tb, hh * 64:(hh + 1) * 64], True, False))
                    fns.append(mmf(psO[0:64, hh * 64:(hh + 1) * 64], qtT[:, hh, c * 64:(c + 1) * 64], Sbf[c % R][:, hh, :], False, True))
                MM(g, fns, [amb, hib[tb], Sbfb[c % R]] + qtb, [pOb])
                CP(g, "act", ot[pr:pr + 64, :], psO[0:64, 0:NH * 64], [pOb], [otbuf])
                if half == 1:
                    sm_, sm_b = sm[tb % 2], smb[tb % 2]
                    og_, og_b = og[tb % 2], ogb[tb % 2]
                    TT(g, "pool", osq[:], ot[:], ot[:], ALU.mult, [otbuf], [osqb])
                    ss = sm_[:, 0:NH]
                    P.op("dve", lambda e, ss=ss: e.tensor_reduce(out=ss, in_=osq[:].rearrange("p (h d) -> p h d", d=64), axis=AX.X,
                                                                 op=ALU.add), [osqb], [sm_b], NH * 64)
                    ACT(g, ss, ss, AF.Ln, [sm_b], [sm_b], scale=1.0 / 64, bias=EPS)
                    ACT(g, ss, ss, AF.Exp, [sm_b], [sm_b], scale=-0.5)
                    o3 = ot[:].rearrange("p (h d) -> p h d", d=64)
                    TT(g, "dve", o3, o3, ss.unsqueeze(2).to_broadcast([128, NH, 64]), ALU.mult, [otbuf, sm_b], [otbuf])
                    TT(g, "dve", o3, o3, onb[:].unsqueeze(1).to_broadcast([128, NH, 64]), ALU.mult, [otbuf, onbb], [otbuf])
                    TT(g, "dve", og_[:], ot[:], hgs[:, tb, :], ALU.mult, [otbuf, hgb[tb]], [og_b])
                    if tb > 0:
                        o_transposes(tb - 1)
                    if tb == NB - 1:
                        o_transposes(tb)


def phase_mix(g, l, hT, hTb, osbT, osbb, odT, odb, ohT, ohb, src_ap, src_bufs):
    P = g.P
    with scope(g) as st:
        mixT = sb(g, st, [128, DC, T], BF16, "mixT")
        mxb = nbs(st, "mix", DC, 4)
        wg = [sb(g, st, [128, DC, 3, 256], BF16, "wg") for _ in range(2)]
        wgb = nbs(st, "wg", 2, 3)
        wy = sb(g, st, [128, 8, D], BF16, "wy")
        wyb = nbs(st, "wy", 3)
        sg = [sb(g, st, [128, 512], F32, "sg") for _ in range(2)]
        sgb = nbs(st, "sg", 2)
        acc = [sb(g, st, [128, 512], F32, "acc") for _ in range(2)]
        accb = nbs(st, "acc", 2)
        tm = [sb(g, st, [128, 512], F32, "tm") for _ in range(2)]
        tmb = nbs(st, "tm", 2)
        k = 0

        def load_pair(dp):
            w_, w_b = wg[dp % 2], wgb[dp % 2]
            for gi in range(3):
                c0 = C_G + gi * 1024 + dp * 256
                DMA(g, "pool", w_[:, :, gi, :], win_cols(g, l, c0, c0 + 256), (), [w_b[gi]])

        load_pair(0)
        DMA(g, "pool", wy[:, 0:3, :], g.w_sb[l].rearrange("(c p) n -> p c n", p=128), (), [wyb[0]])
        DMA(g, "pool", wy[:, 3:6, :], g.w_dsa[l].rearrange("(c p) n -> p c n", p=128), (), [wyb[1]])
        DMA(g, "pool", wy[:, 6:8, :], g.w_hg[l].rearrange("(c p) n -> p c n", p=128), (), [wyb[2]])
        wo = sb(g, st, [128, DC, D], BF16, "wo")
        wob = nbs(st, "wo", 2)
        for dc in range(DC):
            dp, do = dc // 2, (dc % 2) * 128
            w_, w_b = wg[dp % 2], wgb[dp % 2]
            y_, y_b = wy, wyb
            if dc % 2 == 0:
                if dp + 1 < DC // 2:
                    load_pair(dp + 1)
                else:
                    for nh in range(2):
                        DMA(g, "pool", wo[:, :, nh * 512:(nh + 1) * 512],
                            g.w_out[l].rearrange("(c p) n -> p c n", p=128)[:, :, nh * 512:(nh + 1) * 512], (), [wob[nh]])
                    prefetch(g, ("wu", l, 0), g.w_up[l].rearrange("(c p) n -> p c n", p=128)[:, :, 0:512], 512)
            for tc in range(4):
                a_, a_b = acc[k % 2], accb[k % 2]
                for gi, (oT, obufs, nch, c0) in enumerate(((osbT, osbb, 3, 0), (odT, odb, 3, 3), (ohT, ohb, 2, 6))):
                    psG, pGb = psf(g, "mxG", [0, 1, 2])
                    MM(g, [mmf(psG[:, :], w_[:, c, gi, do:do + 128], hT[:, c, tc * 512:(tc + 1) * 512], c == 0, c == DC - 1) for c in range(DC)],
                       [w_b[gi]] + hTb[tc * 4:tc * 4 + 4], [pGb])
                    s_, s_b = sg[(k * 3 + gi) % 2], sgb[(k * 3 + gi) % 2]
                    ACT(g, s_[:], psG[:, :], AF.Sigmoid, [pGb], [s_b])
                    psY, pYb = psf(g, "mxY", [3, 4, 5])
                    MM(g, [mmf(psY[:, :], y_[:, c0 + c, dc * 128:(dc + 1) * 128], oT[:, c, tc * 512:(tc + 1) * 512], c == 0, c == nch - 1)
                           for c in range(nch)],
                       [y_b[gi]] + [obufs[c][tc] for c in range(nch)], [pYb])
                    if gi == 0:
                        TT(g, "dve", a_[:], psY[:, :], s_[:], ALU.mult, [pYb, s_b], [a_b])
                    else:
                        t_, t_b = tm[gi % 2], tmb[gi % 2]
                        TT(g, "dve", t_[:], psY[:, :], s_[:], ALU.mult, [pYb, s_b], [t_b])
                        if gi == 1:
                            TT(g, "pool", a_[:], a_[:], t_[:], ALU.add, [a_b, t_b], [a_b])
                        else:
                            TT(g, "pool", mixT[:, dc, tc * 512:(tc + 1) * 512], a_[:], t_[:], ALU.add, [a_b, t_b], [mxb[dc][tc]])
                k += 1
        xbs = [sb(g, st, [128, D], F32, "xb") for _ in range(3)]
        xbb = nbs(st, "xb", 3)
        nctx = norm_setup(g, st, g.norm_mlp[l:l + 1, :])
        def part_a(tb):
            xb, xbuf = xbs[tb % 3], xbb[tb % 3]
            DMA(g, "sp", xb[:], src_ap[tb * 128:(tb + 1) * 128, :], [src_bufs[tb]], [xbuf])
            for nh in range(2):
                ps, pb = psf(g, "mxO", [0, 1, 2, 3, 4, 5])
                MM(g, [mmf(ps[:, :], mixT[:, c, tb * 128:(tb + 1) * 128], wo[:, c, nh * 512:(nh + 1) * 512], c == 0, c == DC - 1)
                       for c in range(DC)], [wob[nh]] + [mxb[c][tb // 4] for c in range(DC)], [pb])
                xs = xb[:, nh * 512:(nh + 1) * 512]
                TT(g, "dve", xs, ps[:, :], xs, ALU.add, [pb, xbuf], [xbuf])
            DMA(g, "sp", g.xres_d[tb * 128:(tb + 1) * 128, :], xb[:], [xbuf], [g.xres_b[tb]])

        part_a(0)
        for tb in range(NB):
            if tb + 1 < NB:
                part_a(tb + 1)
            norm_block(g, nctx, xbs[tb % 3], xbb[tb % 3], tb, hT, hTb)


def phase_ffn(g, l, hT, hTb, dst_ap, dst_bufs):
    for half in range(2):
        last = half == 1
        with scope(g) as st:
            uT = sb(g, st, [128, 16, T], BF16, "uT")
            ub = nbs(st, "uT", 16, 4)
            wd = sb(g, st, [128, 16, D], BF16, "wd")
            wdb = nbs(st, "wd", 2)
            wdv = g.w_down[l].rearrange("(f p) n -> p f n", p=128)
            wu = [sb(g, st, [128, DC, 512], BF16, "wu") for _ in range(2)]
            wub = nbs(st, "wu", 2)
            rt = [sb(g, st, [128, 512], BF16, "rt") for _ in range(2)]
            rtb = nbs(st, "rt", 2)
            k = 0

            def load_wu(g4):
                c0 = half * 2048 + g4 * 512
                DMA(g, "pool", wu[g4 % 2][:], g.w_up[l].rearrange("(c p) n -> p c n", p=128)[:, :, c0:c0 + 512], (), [wub[g4 % 2]])

            pre0 = take_pre(g, ("wu", l, half))
            if not pre0:
                load_wu(0)
            for g4 in range(4):
                w_, w_b = wu[g4 % 2], wub[g4 % 2]
                if g4 == 0 and pre0:
                    w_, w_b = g.pre, g.preb
                if g4 + 1 < 4:
                    load_wu(g4 + 1)
                if g4 == 1:
                    for nh in range(2):
                        DMA(g, "pool", wd[:, :, nh * 512:(nh + 1) * 512], wdv[:, half * 16:(half + 1) * 16, nh * 512:(nh + 1) * 512], (),
                            [wdb[nh]])
                for fcl in range(4):
                    fc = g4 * 4 + fcl
                    for tc in range(4):
                        ps, pb = psf(g, "proj", [0, 1, 2, 3, 4, 5])
                        mm_fm(g, ps, 128, 512, w_, w_b, fcl * 128, hT, hTb, tc * 512, pb)
                        r_, r_b = rt[k % 2], rtb[k % 2]
                        k += 1
                        ACT(g, r_[:], ps[:, :], AF.Relu, [pb], [r_b])
                        TT(g, "pool", uT[:, fc, tc * 512:(tc + 1) * 512], r_[:], r_[:], ALU.mult, [r_b], [ub[fc][tc]])
            if half == 0:
                prefetch(g, ("wu", l, 1), g.w_up[l].rearrange("(c p) n -> p c n", p=128)[:, :, 2048:2560], 512)
            elif l + 1 < DEPTH:
                prefetch(g, ("sq", l + 1), win_cols(g, l + 1, C_SQ, C_SQ + 384), 384)
            xbs = [sb(g, st, [128, D], F32, "xb") for _ in range(4)]
            xbb = nbs(st, "xb", 4)
            nctx = norm_setup(g, st, g.norm_mix[l + 1:l + 2, :]) if (last and l + 1 < DEPTH) else None
            def down_a(tb):
                xb, xbuf = xbs[tb % 4], xbb[tb % 4]
                DMA(g, "sp", xb[:], g.xres_d[tb * 128:(tb + 1) * 128, :], [g.xres_b[tb]], [xbuf])
                for nh in range(2):
                    ps, pb = psf(g, "proj", [0, 1, 2, 3, 4, 5])
                    MM(g, [mmf(ps[:, :], uT[:, fc, tb * 128:(tb + 1) * 128], wd[:, fc, nh * 512:(nh + 1) * 512], fc == 0, fc == 15)
                           for fc in range(16)], [wdb[nh]] + [ub[fc][tb // 4] for fc in range(16)], [pb])
                    xs = xb[:, nh * 512:(nh + 1) * 512]
                    TT(g, "dve", xs, ps[:, :], xs, ALU.add, [pb, xbuf], [xbuf])
                if last:
                    DMA(g, "sp", dst_ap[tb * 128:(tb + 1) * 128, :], xb[:], [xbuf], [dst_bufs[tb]])
                else:
                    DMA(g, "sp", g.xres_d[tb * 128:(tb + 1) * 128, :], xb[:], [xbuf], [g.xres_b[tb]])

            down_a(0)
            for tb in range(NB):
                if tb + 1 < NB:
                    down_a(tb + 1)
                if last and nctx is not None:
                    norm_block(g, nctx, xbs[tb % 4], xbb[tb % 4], tb, hT, hTb)


def dump_t(g, name, t, ncol):
    if g.dump == name:
        g.P.barrier()
        DMA(g, "sp", g.dbg_d[:, 0:ncol], t, [], [g.dbgb])
        g.P.wait_bufs("sp", [g.dbgb])
        g.P.barrier()


def build_layer(g, l):
    if l > 0 and g.stage < 7:
        return
    src_ap, src_bufs = (g.x_d, g.xin_b) if l == 0 else (g.xres_d, g.xres_b)
    with scope(g) as ls:
        hT, hTb = g.hT, g.hTb
        if l == 0:
            with scope(g) as st:
                norm_T(g, st, src_ap, src_bufs, g.norm_mix[l:l + 1, :], hT, hTb)
        if l == 0:
            dump_t(g, "hT", hT[:].rearrange("p c t -> p (c t)"), 8 * T)
        if g.stage < 2:
            return
        with scope(g) as ms:
            osbT = sb(g, ms, [128, 3, T], BF16, "osbT")
            odT = sb(g, ms, [128, 3, T], BF16, "odT")
            ohT = sb(g, ms, [128, 2, T], BF16, "ohT")
            osbb = nbs(ms, "osb", 3, 4)
            odb = nbs(ms, "od", 3, 4)
            ohb = nbs(ms, "oh", 2, 4)
            phase_sb(g, l, hT, hTb, osbT, osbb)
            if l == 0:
                dump_t(g, "osbT", osbT[:].rearrange("p c t -> p (c t)"), 3 * T)
            if g.stage < 3:
                return
            phase_dsa(g, l, hT, hTb, odT, odb)
            if l == 0:
                dump_t(g, "odT", odT[:].rearrange("p c t -> p (c t)"), 3 * T)
            if g.stage < 4:
                return
            phase_hgrn(g, l, hT, hTb, ohT, ohb)
            if l == 0:
                dump_t(g, "ohT", ohT[:].rearrange("p c t -> p (c t)"), 2 * T)
            if g.stage < 5:
                return
            phase_mix(g, l, hT, hTb, osbT, osbb, odT, odb, ohT, ohb, src_ap, src_bufs)
        if g.stage < 6:
            return
        if l == DEPTH - 1:
            phase_ffn(g, l, hT, hTb, g.out_d, g.out_b)
        else:
            phase_ffn(g, l, hT, hTb, g.xres_d, g.xres_b)


_NC_CACHE = {}


def rope_tables():
    half = 8
    inv = 500000.0 ** (-(np.arange(half, dtype=np.float32) * 2.0) / 16.0)
    ang = np.arange(T, dtype=np.float32)[:, None] * inv[None, :].astype(np.float32)
    return np.cos(ang).astype(np.float32), np.sin(ang).astype(np.float32)


def kernel(x, norm_mix, w_in, qn_dsa, kn_dsa, hgrn_lb, hgrn_onorm, w_br_sb, w_br_dsa, w_br_hgrn, w_out, norm_mlp, w_up, w_down):
    if "nc" not in _NC_CACHE:
        _NC_CACHE["nc"] = build_two_pass()
    nc = _NC_CACHE["nc"]
    f = lambda a: np.ascontiguousarray(np.asarray(a, dtype=np.float32))
    cs, sn = rope_tables()
    shared = dict(norm_mix=f(norm_mix), w_in=f(w_in), qn_dsa=f(qn_dsa), kn_dsa=f(kn_dsa), hgrn_lb=f(hgrn_lb),
                  hgrn_onorm=f(hgrn_onorm), w_br_sb=f(w_br_sb), w_br_dsa=f(w_br_dsa), w_br_hgrn=f(w_br_hgrn),
                  w_out=f(w_out), norm_mlp=f(norm_mlp), w_up=f(w_up), w_down=f(w_down), rope_cos=cs, rope_sin=sn)
    xs = f(x)
    in_maps = [dict(shared, x=xs[b]) for b in range(8)]
    res = run_bass_kernel_spmd(nc, in_maps, core_ids=list(range(8)))
    return np.stack([np.asarray(r["out"], dtype=np.float32) for r in res.results], axis=0)
```

```python
import math
import numpy as np
from contextlib import ExitStack, contextmanager
import concourse.bass as bass
import concourse.mybir as mybir
from concourse.bass_utils import run_bass_kernel_spmd

F32 = mybir.dt.float32
BF16 = mybir.dt.bfloat16
AF = mybir.ActivationFunctionType
ALU = mybir.AluOpType
AX = mybir.AxisListType

T = 2048
D = 1024
NB = 16
DC = 8
DIN = 6856
DFF = 4096
DEPTH = 2
EPS = 1e-6
F_MIN = 1e-12
IDX_SCALE = (64 * 8) ** -0.5
C_SQ, C_SK, C_SV = 0, 384, 768
C_DQ, C_DK, C_DV = 1152, 1536, 1600
C_IQ, C_IK, C_IW = 1664, 2176, 2240
C_HQ, C_HF, C_HI, C_HG = 2248, 2760, 3272, 3528
C_G = 3784
N_BISECT = 12


class Buf:
    __slots__ = ("name", "w", "r", "dsem", "excl")

    def __init__(self, name, excl=False):
        self.name = name
        self.w = None
        self.r = {}
        self.dsem = None
        self.excl = excl


class Prog:
    ENG = ("pe", "act", "dve", "pool", "sp")
    CLEAR_NS = 330.0
    FILL_NS = {"dve": 66.0, "act": 190.0, "pool": 125.0}
    EST = {"dve": (60.0, 0.26), "act": (185.0, 0.83), "pool": (120.0, 0.8), "pe": (0.0, 0.0), "sp": (0.0, 0.0)}

    def __init__(self, nc, stack, needed=None):
        self.needed = needed
        self.used = set()
        self.remap = {}
        self.sig = {}
        self.fill = {}
        self.nc = nc
        self.stack = stack
        self.eng = {"pe": nc.tensor, "act": nc.scalar, "dve": nc.vector, "pool": nc.gpsimd, "sp": nc.sync}
        self.cnt = {e: 0 for e in self.ENG}
        self.known = {e: {} for e in self.ENG}
        self.sems = {}
        self.semval = {}
        for e in ("pe", "act", "dve", "pool"):
            self.sems["E_" + e] = stack.enter_context(nc.semaphore("sem_" + e))
            self.semval["E_" + e] = 0
        self.ndsem = 0
        self.free_dsems = []
        self.nwaits = 0
        self.tcum = {e: 0.0 for e in self.ENG}
        self.tend = {e: {} for e in self.ENG}

    def _dsem(self, buf):
        if buf.dsem is None:
            if self.free_dsems:
                key = self.free_dsems.pop()
            else:
                key = "D%d" % self.ndsem
                self.ndsem += 1
                self.sems[key] = self.stack.enter_context(self.nc.semaphore("dsem%d" % (self.ndsem - 1)))
                self.semval[key] = 0
            buf.dsem = key
        return buf.dsem

    def release(self, bufs):
        for b in bufs:
            if b.dsem is not None:
                self.free_dsems.append(b.dsem)
                b.dsem = None

    def _waits(self, eng, deps):
        need = {}
        own = "E_" + eng
        for (k, v) in deps:
            if eng == "pe" and k == "E_pe":
                continue
            if k == own and eng in ("act", "dve", "pool"):
                te = self.tend[eng].get(v)
                if te is not None and eng in self.fill:
                    gap = self.CLEAR_NS - (self.tcum[eng] - te)
                    if gap > 0:
                        n = int(math.ceil(gap / self.FILL_NS[eng]))
                        for _ in range(n):
                            self.fill[eng](self.eng[eng])
                        self.tcum[eng] += n * self.FILL_NS[eng]
                        self.nfill = getattr(self, "nfill", 0) + n
                continue
            if v > need.get(k, 0):
                need[k] = v
        out = []
        kn = self.known[eng]
        for k, v in need.items():
            if kn.get(k, 0) < v:
                kn[k] = v
                out.append((k, v))
        return out

    @staticmethod
    def _deps(reads, writes):
        deps = []
        for b in reads:
            if b.w is not None:
                deps.append(b.w)
            if b.excl:
                deps.extend(b.r.items())
        for b in writes:
            if b.w is not None:
                deps.append(b.w)
            deps.extend(b.r.items())
        return deps

    def _emit_waits(self, eng, waits):
        e = self.eng[eng]
        for (k, v) in waits:
            if k.startswith("E_"):
                self.used.add((k, v))
                if self.needed is not None:
                    v = self.remap[(k, v)]
            e.wait_ge(self.sems[k], v)
            self.nwaits += 1

    def _mark(self, ev, reads, writes):
        k, v = ev
        for b in reads:
            if b.r.get(k, 0) < v:
                b.r[k] = v
        for b in writes:
            b.w = ev
            b.r = {}

    def op(self, eng, fn, reads=(), writes=(), n=0):
        self.group(eng, [fn], reads, writes, n)

    def group(self, eng, fns, reads=(), writes=(), n=0):
        self._emit_waits(eng, self._waits(eng, self._deps(reads, writes)))
        e = self.eng[eng]
        for fn in fns[:-1]:
            fn(e)
        self.cnt[eng] += 1
        ov, pe_ = self.EST[eng]
        self.tcum[eng] += ov + pe_ * n
        td = self.tend[eng]
        td[self.cnt[eng]] = self.tcum[eng]
        if len(td) > 64:
            for k_ in sorted(td)[:32]:
                del td[k_]
        key = "E_" + eng
        self.semval[key] = self.cnt[eng]
        if self.needed is None or (key, self.cnt[eng]) in self.needed:
            self.sig[key] = self.sig.get(key, 0) + 1
            self.remap[(key, self.cnt[eng])] = self.sig[key]
            fns[-1](e).then_inc(self.sems[key], 1)
        else:
            fns[-1](e)
        self._mark((key, self.cnt[eng]), reads, writes)

    def dma(self, eng, fn, reads=(), writes=()):
        assert len(writes) == 1
        wb = writes[0]
        deps = self._deps(reads, writes)
        if eng == "pool" and getattr(self, "last_swdge", None) is not None:
            deps.append(self.last_swdge)
        self._emit_waits(eng, self._waits(eng, deps))
        key = self._dsem(wb)
        self.semval[key] += 16
        fn(self.eng[eng]).then_inc(self.sems[key], 16)
        if eng == "pool":
            self.last_swdge = (key, self.semval[key])
        self._mark((key, self.semval[key]), reads, writes)

    def barrier(self):
        deps = [(k, v) for k, v in self.semval.items() if v > 0]
        for eng in self.ENG:
            self._emit_waits(eng, self._waits(eng, deps))

    def wait_bufs(self, eng, bufs):
        deps = []
        for b in bufs:
            if b.w is not None:
                deps.append(b.w)
            deps.extend(b.r.items())
        self._emit_waits(eng, self._waits(eng, deps))


class G:
    pass


def bufs(prefix, *dims):
    if len(dims) == 1:
        return [Buf("%s%d" % (prefix, i)) for i in range(dims[0])]
    return [bufs("%s%d_" % (prefix, i), *dims[1:]) for i in range(dims[0])]


def build_program(stage=99, dump=None, needed=None):
    nc = bass.Bass("TRN2", target_bir_lowering=False)
    g = G()
    g.nc = nc
    g.stage = stage
    g.dump = dump
    import os
    g.ntl = int(os.environ.get("NTL", "9"))
    g.dbg_d = None
    if dump is not None:
        g.dbg_d = nc.dram_tensor("dbg", [128, 8 * T], BF16, kind="ExternalOutput").ap()
        g.dbgb = Buf("dbg")
    dt = lambda name, shape, kind, d=F32: nc.dram_tensor(name, shape, d, kind=kind).ap()
    g.x_d = dt("x", [T, D], "ExternalInput")
    g.norm_mix = dt("norm_mix", [DEPTH, D], "ExternalInput")
    g.w_in = dt("w_in", [DEPTH, D, DIN], "ExternalInput")
    g.qn = dt("qn_dsa", [DEPTH, 64], "ExternalInput")
    g.kn = dt("kn_dsa", [DEPTH, 64], "ExternalInput")
    g.lb_d = dt("hgrn_lb", [DEPTH, 512], "ExternalInput")
    g.onorm = dt("hgrn_onorm", [DEPTH, 64], "ExternalInput")
    g.w_sb = dt("w_br_sb", [DEPTH, 384, D], "ExternalInput")
    g.w_dsa = dt("w_br_dsa", [DEPTH, 384, D], "ExternalInput")
    g.w_hg = dt("w_br_hgrn", [DEPTH, 256, D], "ExternalInput")
    g.w_out = dt("w_out", [DEPTH, D, D], "ExternalInput")
    g.norm_mlp = dt("norm_mlp", [DEPTH, D], "ExternalInput")
    g.w_up = dt("w_up", [DEPTH, D, DFF], "ExternalInput")
    g.w_down = dt("w_down", [DEPTH, DFF, D], "ExternalInput")
    g.cs_d = dt("rope_cos", [T, 8], "ExternalInput")
    g.sn_d = dt("rope_sin", [T, 8], "ExternalInput")
    g.out_d = dt("out", [T, D], "ExternalOutput")
    g.xres_d = dt("xres", [T, D], "Internal")
    g.xin_b = bufs("xin", NB)
    g.xres_b = bufs("xres", NB)
    g.out_b = bufs("outb", NB)

    with ExitStack() as gs:
        P = Prog(nc, gs, needed)
        g.P = P
        fa = gs.enter_context(nc.sbuf_tensor("fill_a", [128, 2], F32))
        fd = gs.enter_context(nc.sbuf_tensor("fill_d", [128, 2], F32))
        nc.vector.memset(fd[:], 0.0)
        nc.vector.memset(fa[:], 0.0)
        P.fill["dve"] = lambda e: e.memset(fd[:, 0:1], 0.0)
        P.fill["act"] = lambda e: e.activation(out=fa[:, 0:1], in_=fa[:, 1:2], func=AF.Copy)
        fp = gs.enter_context(nc.sbuf_tensor("fill_p", [128, 2], F32))
        nc.gpsimd.memset(fp[:], 0.0)
        P.fill["pool"] = lambda e: e.memset(fp[:, 0:1], 0.0)
        g.uid = 0
        g.psF = [gs.enter_context(nc.psum_tensor("psF%d" % i, [128, 512], F32)) for i in range(6)]
        g.psFb = [Buf("psF%d" % i, excl=True) for i in range(6)]
        g.psB = [gs.enter_context(nc.psum_tensor("psB%d" % i, [128, 1024], BF16)) for i in range(2)]
        g.psBb = [Buf("psB%d" % i, excl=True) for i in range(2)]
        g.rotc = {}
        build_consts(g, gs)
        g.hT = gs.enter_context(nc.sbuf_tensor("hT_glob", [128, DC, T], BF16))
        g.hTb = bufs("hT", NB)
        g.pre = gs.enter_context(nc.sbuf_tensor("pre_w", [128, DC, 512], BF16))
        g.preb = Buf("pre_w")
        prefetch(g, ("sq", 0), win_cols(g, 0, C_SQ, C_SQ + 384), 384)
        for l in range(DEPTH):
            if g.stage >= 1:
                build_layer(g, l)
        if g.dbg_d is not None:
            P.wait_bufs("sp", [g.dbgb])
        P.wait_bufs("sp", g.out_b)
        P.barrier()
        g.used = P.used
        print("ops", P.cnt, "signals", P.sig, "waits", P.nwaits, "fillers", getattr(P, "nfill", 0), "dsems", P.ndsem, flush=True)
    return nc, P.used


def build_two_pass(stage=99, dump=None):
    _, used = build_program(stage, dump, None)
    nc, _ = build_program(stage, dump, used)
    return nc


def pipeline(stages, ntiles):
    ns = len(stages)
    for t in range(ntiles + ns - 1):
        for k, f in enumerate(stages):
            i = t - k
            if 0 <= i < ntiles:
                f(i)


def prefetch(g, tag, src, ncols):
    DMA(g, "pool", g.pre[:, :, 0:ncols], src, (), [g.preb])
    g.pre_tag = tag


def take_pre(g, tag):
    if getattr(g, "pre_tag", None) == tag:
        g.pre_tag = None
        return True
    return False


def rot(g, role, items):
    i = g.rotc.get(role, 0)
    g.rotc[role] = i + 1
    return items[i % len(items)]


def psf(g, role, banks):
    b = rot(g, role, banks)
    return g.psF[b], g.psFb[b]


def psb(g, role="pb"):
    b = rot(g, role, [0, 1])
    return g.psB[b], g.psBb[b]


@contextmanager
def scope(g):
    st = ExitStack()
    st.tbufs = []
    try:
        yield st
    finally:
        g.P.barrier()
        g.P.release(st.tbufs)
        st.close()


def sb(g, st, shape, dtype, name=None):
    g.uid += 1
    return st.enter_context(g.nc.sbuf_tensor("%s_%d" % (name or "t", g.uid), shape, dtype))


def nb(st, name):
    b = Buf(name)
    st.tbufs.append(b)
    return b


def nbs(st, prefix, *dims):
    r = bufs(prefix, *dims)

    def flat(x):
        if isinstance(x, Buf):
            st.tbufs.append(x)
        else:
            for y in x:
                flat(y)
    flat(r)
    return r


def _fs(ap):
    try:
        return int(ap.free_size())
    except Exception:
        return 0


def ACT(g, out, in_, func, reads, writes, **kw):
    g.P.op("act", lambda e: e.activation(out=out, in_=in_, func=func, **kw), reads, writes, _fs(out))


def TT(g, eng, out, in0, in1, op, reads, writes):
    g.P.op(eng, lambda e: e.tensor_tensor(out=out, in0=in0, in1=in1, op=op), reads, writes, _fs(out))


def TS(g, eng, out, in0, s1, s2, op0, op1, reads, writes, **kw):
    if op1 is None:
        s2 = 0.0 if isinstance(s1, (int, float)) else g.zc[0:in0.shape[0], 0:1]
        g.P.op(eng, lambda e: e.tensor_scalar(out=out, in0=in0, scalar1=s1, scalar2=s2, op0=op0, op1=ALU.add, **kw), reads, writes, _fs(out))
    else:
        g.P.op(eng, lambda e: e.tensor_scalar(out=out, in0=in0, scalar1=s1, scalar2=s2, op0=op0, op1=op1, **kw), reads, writes, _fs(out))


def STT(g, out, in0, scalar, in1, op0, op1, reads, writes):
    g.P.op("dve", lambda e: e.scalar_tensor_tensor(out=out, in0=in0, scalar=scalar, in1=in1, op0=op0, op1=op1), reads, writes, _fs(out))


def CP(g, eng, out, in_, reads, writes):
    if eng == "act":
        g.P.op("act", lambda e: e.activation(out=out, in_=in_, func=AF.Copy), reads, writes, _fs(out))
    else:
        g.P.op(eng, lambda e: e.tensor_copy(out, in_), reads, writes, _fs(out))


def MS(g, eng, ap, val, writes):
    g.P.op(eng, lambda e: e.memset(ap, val), (), writes, _fs(ap))


def ASEL(g, out, in_, pattern, cmp, fill, base, cm, reads, writes):
    g.P.op("pool", lambda e: e.affine_select(out=out, in_=in_, pattern=pattern, compare_op=cmp, fill=fill, base=base,
                                             channel_multiplier=cm), reads, writes, _fs(out))


def MM(g, outs_fns, reads, writes):
    g.P.group("pe", outs_fns, reads, writes)


def mmf(out, lhsT, rhs, start, stop):
    return lambda e: e.matmul(out, lhsT=lhsT, rhs=rhs, start=start, stop=stop)


def trf(out, in_, ident):
    return lambda e: e.transpose(out, in_, ident)


def DMA(g, eng, out, in_, reads, writes, **kw):
    g.P.dma(eng, lambda e: e.dma_start(out=out, in_=in_, **kw), reads, writes)


def build_consts(g, gs):
    nc = g.nc
    mk = lambda name, shape, d: gs.enter_context(nc.sbuf_tensor(name, shape, d))
    g.ident = mk("ident", [128, 128], BF16)
    g.negtri = mk("negtri", [128, 128], BF16)
    g.negones = mk("negones", [128, 128], BF16)
    g.onesb = mk("onesb", [128, 128], BF16)
    g.maskbd = mk("maskbd", [128, 128], F32)
    g.onesf = mk("onesf", [128, 128], F32)
    g.resetm = mk("resetm", [128, T], BF16)
    g.zc = mk("zc", [128, 1], F32)
    g.negbig = mk("negbig", [128, 1], F32)
    g.cs = mk("cs", [128, NB, 8], F32)
    g.sn = mk("sn", [128, NB, 8], F32)
    g.lbraw = mk("lbraw", [128, 2, 4], F32)
    g.lbv = mk("lbv", [128, 2, 4], F32)
    g.oml = mk("oml", [128, 2, 4], F32)
    g.cb = Buf("consts")
    g.csb = Buf("cs")
    g.snb = Buf("sn")
    g.lbb = Buf("lbraw")
    cb = [g.cb]
    MS(g, "pool", g.onesb[:], 1.0, cb)
    MS(g, "pool", g.negones[:], -1.0, cb)
    MS(g, "pool", g.onesf[:], 1.0, cb)
    MS(g, "pool", g.zc[:], 0.0, cb)
    MS(g, "pool", g.negbig[:], -1e29, cb)
    ASEL(g, g.ident[:], g.onesb[:], [[1, 128]], ALU.is_equal, 0.0, 0, -1, cb, cb)
    ASEL(g, g.negtri[:], g.negones[:], [[-1, 128]], ALU.is_ge, 0.0, 0, 1, cb, cb)
    ASEL(g, g.maskbd[:], g.onesf[:], [[1, 128]], ALU.is_ge, 0.0, 0, -1, cb, cb)
    MS(g, "pool", g.maskbd[0:64, 64:128], 0.0, cb)
    g.ones512 = mk("ones512", [128, 512], BF16)
    g.mlt = mk("mlt", [128, 512], BF16)
    MS(g, "pool", g.ones512[:], 1.0, cb)
    ASEL(g, g.mlt[:], g.ones512[:], [[1, 512]], ALU.is_gt, 0.0, 0, -1, cb, cb)
    g.caus01 = mk("caus01", [128, 128], F32)
    g.negfill = mk("negfill", [128, 128], F32)
    ASEL(g, g.caus01[:], g.onesf[:], [[-1, 128]], ALU.is_ge, 0.0, 0, 1, cb, cb)
    TS(g, "pool", g.negfill[:], g.caus01[:], -1.0, 1e30, ALU.add, ALU.mult, cb, cb)
    MS(g, "pool", g.resetm[:], 1.0, cb)
    MS(g, "pool", g.resetm[:].rearrange("p (c j) -> p c j", j=64)[:, :, 0:1], 0.0, cb)
    DMA(g, "sp", g.cs[:], g.cs_d.rearrange("(b p) i -> p b i", p=128), (), [g.csb])
    DMA(g, "sp", g.sn[:], g.sn_d.rearrange("(b p) i -> p b i", p=128), (), [g.snb])
    DMA(g, "sp", g.lbraw[:], g.lb_d.rearrange("l (h k) -> k l h", k=128), (), [g.lbb], allow_slow_non_contiguous=True)
    MS(g, "dve", g.lbv[:], 0.0, cb)
    TT(g, "dve", g.lbv[:, 1, :], g.lbraw[:, 1, :], g.lbraw[:, 0, :], ALU.subtract, [g.lbb], cb)
    ACT(g, g.lbv[:, 1, :], g.lbv[:, 1, :], AF.Sigmoid, cb, cb)
    TS(g, "dve", g.oml[:], g.lbv[:], -1.0, 1.0, ALU.mult, ALU.add, cb, cb)
    g.P.barrier()


class NormCtx:
    pass


def norm_setup(g, st, gain_d_row):
    c = NormCtx()
    c.gain = sb(g, st, [128, D], F32, "gain")
    c.gb = nb(st, "gain")
    DMA(g, "sp", c.gain[:], gain_d_row.to_broadcast([128, D]), (), [c.gb])
    c.junk = sb(g, st, [128, D], BF16, "junk")
    c.jb = nb(st, "junk")
    c.hbs = [sb(g, st, [128, D], BF16, "hb") for _ in range(2)]
    c.hbb = nbs(st, "hb", 2)
    c.ss = sb(g, st, [128, NB], F32, "ss")
    c.ssb = nbs(st, "ss", NB)
    MS(g, "dve", c.ss[:], 0.0, c.ssb)
    return c


def norm_block(g, c, xb, xbuf, tb, hT, hTb):
    hb, hbuf = c.hbs[tb % 2], c.hbb[tb % 2]
    s1 = c.ss[:, tb:tb + 1]
    ACT(g, c.junk[:], xb[:], AF.Square, [xbuf, c.ssb[tb]], [c.jb, c.ssb[tb]], accum_out=s1)
    ACT(g, s1, s1, AF.Ln, [c.ssb[tb]], [c.ssb[tb]], scale=1.0 / D, bias=EPS)
    ACT(g, s1, s1, AF.Exp, [c.ssb[tb]], [c.ssb[tb]], scale=-0.5)
    STT(g, hb[:], xb[:], s1, c.gain[:], ALU.mult, ALU.mult, [xbuf, c.ssb[tb], c.gb], [hbuf])
    for half in range(2):
        pt, ptb = psb(g)
        MM(g, [trf(pt[:, m * 128:(m + 1) * 128], hb[:, (half * 4 + m) * 128:(half * 4 + m + 1) * 128], g.ident[:])
               for m in range(4)], [hbuf, g.cb], [ptb])
        CP(g, "act" if half == 0 else "dve", hT[:, half * 4:half * 4 + 4, tb * 128:(tb + 1) * 128],
           pt[:, 0:512].rearrange("p (m j) -> p m j", j=128), [ptb], [hTb[tb]])


def norm_T(g, st, src_ap, src_bufs, gain_d_row, hT, hTb):
    c = norm_setup(g, st, gain_d_row)
    xbs = [sb(g, st, [128, D], F32, "xb") for _ in range(4)]
    xbb = nbs(st, "xb", 4)
    for tb in range(NB):
        xb, xbuf = xbs[tb % 4], xbb[tb % 4]
        DMA(g, "sp", xb[:], src_ap[tb * 128:(tb + 1) * 128, :], [src_bufs[tb]], [xbuf])
        norm_block(g, c, xb, xbuf, tb, hT, hTb)


def win_cols(g, l, c0, c1):
    return g.w_in[l].rearrange("(c p) n -> p c n", p=128)[:, :, c0:c1]


def mm_fm(g, ps, M, n, w, wb, col0, hT, hTb, tok0, role_bufs):
    MM(g, [mmf(ps[0:M, 0:n], w[:, c, col0:col0 + M], hT[:, c, tok0:tok0 + n], c == 0, c == DC - 1) for c in range(DC)],
       [wb] + hTb[tok0 // 128:(tok0 + n + 127) // 128], [role_bufs])


def mm_tm(g, ps, N, w, wb, col0, hT, hTb, tb, psbuf):
    MM(g, [mmf(ps[:, 0:N], hT[:, c, tb * 128:(tb + 1) * 128], w[:, c, col0:col0 + N], c == 0, c == DC - 1) for c in range(DC)],
       [wb, hTb[tb]], [psbuf])


def phase_sb(g, l, hT, hTb, osbT, osbb):
    with scope(g) as st:
        ws = []
        for i, c0 in enumerate((C_SQ, C_SK, C_SV)):
            if i == 0 and take_pre(g, ("sq", l)):
                ws.append((g.pre, g.preb))
                continue
            w = sb(g, st, [128, DC, 384], BF16, "wsb")
            wb = nb(st, "wsb%d" % i)
            DMA(g, "pool", w[:], win_cols(g, l, c0, c0 + 384), (), [wb])
            ws.append((w, wb))
        sqT = sb(g, st, [128, 3, T], BF16, "sqT")
        skT = sb(g, st, [128, 3, T], BF16, "skT")
        sqb = nbs(st, "sq", 3, 4)
        skb = nbs(st, "sk", 3, 4)
        k = 0
        for (dst, dstb, (w, wb), scl) in ((sqT, sqb, ws[0], 0.125), (skT, skb, ws[1], 1.0)):
            for hp in range(3):
                for tc in range(4):
                    ps, pb = psf(g, "proj", [0, 1, 2, 3, 4, 5])
                    mm_fm(g, ps, 128, 512, w, wb, hp * 128, hT, hTb, tc * 512, pb)
                    o = dst[:, hp, tc * 512:(tc + 1) * 512]
                    if k % 2 == 0:
                        ACT(g, o, ps[:, :], AF.Copy, [pb], [dstb[hp][tc]], scale=scl)
                    else:
                        TS(g, "dve", o, ps[:, :], scl, None, ALU.mult, None, [pb], [dstb[hp][tc]])
                    k += 1
        svp = [sb(g, st, [128, NB, 384], BF16, "svp") for _ in range(2)]
        svb = nbs(st, "sv", 2, NB)
        for s_ in range(2):
            MS(g, "pool", svp[s_][:].rearrange("p t c -> p (t c)"), 0.0, svb[s_])
        for tb in range(NB):
            ps, pb = psf(g, "proj", [0, 1, 2, 3, 4, 5])
            mm_tm(g, ps, 384, ws[2][0], ws[2][1], 0, hT, hTb, tb, pb)
            src = ps[:, 0:384].rearrange("p (m s d) -> p m s d", s=2, d=64)
            for s_ in range(2):
                dst = svp[s_][:, tb, :].rearrange("p (m s d) -> p m s d", s=2, d=64)
                CP(g, "act" if s_ == 0 else "dve", dst[:, :, s_, :], src[:, :, s_, :], [pb], [svb[s_][tb]])
        prefetch(g, ("wA", l), win_cols(g, l, C_DQ, C_DQ + 512), 512)
        R = 4
        mk2 = lambda shape, dt_, nm: [[sb(g, st, shape, dt_, nm) for _ in range(R)] for _ in range(2)]
        Et, SPt, SPs, At = mk2([128, 512], F32, "Et"), mk2([128, 512], BF16, "SPt"), mk2([128, 512], BF16, "SPs"), mk2([128, 512], BF16, "At")
        Etb, SPb, SPsb, Atb = nbs(st, "Et", 2, R), nbs(st, "SPt", 2, R), nbs(st, "SPs", 2, R), nbs(st, "At", 2, R)
        steps = []
        for m in range(3):
            for qc in range(4):
                for n_, kb in enumerate(range(4 * qc + 3, -1, -1)):
                    steps.append((m, qc, kb, n_))
        stt = {}
        pso = {}

        def info(t):
            m, qc, kb, n_ = steps[t]
            j0 = max(0, kb * 128 - qc * 512)
            return m, qc, kb, n_, j0, kb >= 4 * qc, qc * 512 + j0 - kb * 128, n_ == 0

        def opnds(t):
            m, qc, kb, n_, j0, diag, base, first = info(t)
            kk = [skT[64 * s_:64 * s_ + 64, m, kb * 128:(kb + 1) * 128] for s_ in range(2)]
            qq = [sqT[64 * s_:64 * s_ + 64, m, qc * 512 + j0:(qc + 1) * 512] for s_ in range(2)]
            return kk, qq, skb[m][kb // 4], sqb[m][qc]

        def s1(t):
            m, qc, kb, n_, j0, diag, base, first = info(t)
            kk, qq, rk, rq = opnds(t)
            pz = [psf(g, "sbZ", [0, 1, 2]) for _ in range(2)]
            stt[t] = {"pz": pz}
            for s_ in range(2):
                MM(g, [mmf(pz[s_][0][:, j0:512], kk[s_], qq[s_], True, True)], [rk, rq], [pz[s_][1]])

        def s2(t):
            m, qc, kb, n_, j0, diag, base, first = info(t)
            ib = t % R
            pz = stt[t]["pz"]
            for s_ in range(2):
                ACT(g, Et[s_][ib][:, j0:512], pz[s_][0][:, j0:512], AF.Exp, [pz[s_][1]], [Etb[s_][ib]])

        def s3(t):
            m, qc, kb, n_, j0, diag, base, first = info(t)
            ib = t % R
            for s_ in range(2):
                ACT(g, SPt[s_][ib][:, j0:512], Et[s_][ib][:, j0:512], AF.Ln, [Etb[s_][ib]], [SPb[s_][ib]], bias=1.0)
            if diag:
                for s_ in range(2):
                    S = SPt[s_][ib]
                    assert base == 0
                    TT(g, "pool", S[:, j0:512], S[:, j0:512], g.mlt[:, 0:512 - j0], ALU.mult, [SPb[s_][ib], g.cb], [SPb[s_][ib]])
            if kb > 0:
                for s_ in range(2):
                    S, Sb = SPt[s_][ib], SPb[s_][ib]
                    Sn, Snb = SPs[s_][n_ % R], SPsb[s_][n_ % R]
                    if first:
                        if j0 > 0:
                            MS(g, "pool", Sn[:, 0:j0], 0.0, [Snb])
                        CP(g, "pool", Sn[:, j0:512], S[:, j0:512], [Sb], [Snb])
                    else:
                        So, Sob = SPs[s_][(n_ - 1) % R], SPsb[s_][(n_ - 1) % R]
                        if j0 > 0:
                            CP(g, "pool", Sn[:, 0:j0], So[:, 0:j0], [Sob], [Snb])
                        TT(g, "dve", Sn[:, j0:512], So[:, j0:512], S[:, j0:512], ALU.add, [Sob, Sb], [Snb])

        def s4(t):
            m, qc, kb, n_, j0, diag, base, first = info(t)
            ib = t % R
            kk, qq, rk, rq = opnds(t)
            pc = [psf(g, "sbC", [3, 4]) for _ in range(2)]
            stt[t]["pc"] = pc
            for s_ in range(2):
                S = SPt[s_][ib]
                fns = [mmf(pc[s_][0][:, j0:512], kk[s_], qq[s_], True, False),
                       mmf(pc[s_][0][:, j0:512], g.negtri[:], S[:, j0:512], False, first)]
                rd = [rk, rq, SPb[s_][ib], g.cb]
                if not first:
                    So, Sob = SPs[s_][(n_ - 1) % R], SPsb[s_][(n_ - 1) % R]
                    fns.append(mmf(pc[s_][0][:, j0:512], g.negones[:], So[:, j0:512], False, True))
                    rd.append(Sob)
                MM(g, fns, rd, [pc[s_][1]])

        def s5(t):
            m, qc, kb, n_, j0, diag, base, first = info(t)
            ib = t % R
            pc = stt[t]["pc"]
            for s_ in range(2):
                ACT(g, At[s_][ib][:, j0:512], pc[s_][0][:, j0:512], AF.Exp, [pc[s_][1]], [Atb[s_][ib]])
            for s_ in range(2):
                A, Ab = At[s_][ib], Atb[s_][ib]
                if diag:
                    TT(g, "pool", A[:, j0:512], A[:, j0:512], g.mlt[:, 0:512 - j0], ALU.mult, [Ab, g.cb], [Ab])
                if first and j0 > 0:
                    MS(g, "pool", A[:, 0:j0], 0.0, [Ab])

        def s6(t):
            m, qc, kb, n_, j0, diag, base, first = info(t)
            ib = t % R
            if first:
                pso[(m, qc)] = psf(g, "sbO", [5])
            psO, pOb = pso[(m, qc)]
            for s_ in range(2):
                A, Ab = At[s_][ib], Atb[s_][ib]
                vv = svp[s_][:, kb, m * 128:(m + 1) * 128]
                c0 = 0 if first else j0
                MM(g, [mmf(psO[:, c0:512], vv, A[:, c0:512], first and s_ == 0, kb == 0 and s_ == 1)], [svb[s_][kb], Ab], [pOb])
            if kb == 0:
                CP(g, "act" if (m * 4 + qc) % 2 == 0 else "dve", osbT[:, m, qc * 512:(qc + 1) * 512], psO[:, :], [pOb], [osbb[m][qc]])
            del stt[t]

        nst = len(steps)
        stages = [s1, s2, s3, s4, s5, s6]
        for e in range(nst + len(stages) - 1):
            for k in range(len(stages) - 1, -1, -1):
                t = e - k
                if 0 <= t < nst:
                    stages[k](t)


def phase_dsa(g, l, hT, hTb, odT, odb):
    P = g.P
    with scope(g) as st:
        featT = sb(g, st, [128, 9, T], BF16, "featT")
        fb = nbs(st, "feat", NB)
        dvx = sb(g, st, [128, NB, 128], BF16, "dvx")
        dvb = nbs(st, "dvx", NB)
        sgn = sb(g, st, [128, NB, 8], F32, "sgn")
        sgb = nbs(st, "sgn", NB)
        qkg = sb(g, st, [128, 7, 64], F32, "qkg")
        qkgb = nbs(st, "qkg", 7)
        for hh in range(7):
            src = (g.qn if hh < 6 else g.kn)[l:l + 1, :].to_broadcast([128, 64])
            DMA(g, "sp", qkg[:, hh, :], src, (), [qkgb[hh]])
        with scope(g) as s2:
            wB = sb(g, s2, [128, DC, 512], BF16, "wB")
            wC = sb(g, s2, [128, DC, 72], BF16, "wC")
            wBb, wCb = nb(s2, "wB"), nb(s2, "wC")
            if take_pre(g, ("wA", l)):
                wA, wAb = g.pre, g.preb
            else:
                wA = sb(g, s2, [128, DC, 512], BF16, "wA")
                wAb = nb(s2, "wA")
                DMA(g, "pool", wA[:], win_cols(g, l, C_DQ, C_DQ + 512), (), [wAb])
            DMA(g, "pool", wB[:], win_cols(g, l, C_IQ, C_IQ + 512), (), [wBb])
            DMA(g, "pool", wC[:], win_cols(g, l, C_IK, C_IK + 72), (), [wCb])
            NR = 4
            tq = [sb(g, s2, [128, 18, 64], F32, "tokq") for _ in range(NR)]
            tqb = nbs(s2, "tokq", NR)
            tbq = [sb(g, s2, [128, 18, 64], BF16, "tokb") for _ in range(3)]
            tbb = nbs(s2, "tokb", 3)
            sqt = [sb(g, s2, [128, 448], F32, "sqt") for _ in range(2)]
            sqtb = nbs(s2, "sqt", 2)
            smq = [sb(g, s2, [128, 32], F32, "small") for _ in range(2)]
            smqb = nbs(s2, "small", 2)
            rt = [sb(g, s2, [128, 18, 8], F32, "ropet") for _ in range(4)]
            rtb = nbs(s2, "ropet", 4)
            pst = {}

            def p1(tb):
                MS(g, "pool", dvx[:, tb, 64:128], 1.0, [dvb[tb]])
                tk, tkb = tq[tb % NR], tqb[tb % NR]
                MS(g, "pool", tk[:, 7, :], 0.0, [tkb])
                MS(g, "pool", tk[:, 17, :], 0.0, [tkb])
                psA, pAb = psf(g, "dA", [0, 1])
                psBq, pBb = psf(g, "dB", [2, 3])
                psC, pCb = psf(g, "dC", [4, 5])
                pst[tb] = (psA, pAb, psBq, pBb, psC, pCb)
                mm_tm(g, psA, 512, wA, wAb, 0, hT, hTb, tb, pAb)
                mm_tm(g, psBq, 512, wB, wBb, 0, hT, hTb, tb, pBb)
                mm_tm(g, psC, 72, wC, wCb, 0, hT, hTb, tb, pCb)

            def p2(tb):
                psA, pAb, psBq, pBb, psC, pCb = pst.pop(tb)
                tk, tkb = tq[tb % NR], tqb[tb % NR]
                sq_, sq_b = sqt[tb % 2], sqtb[tb % 2]
                sm, smb = smq[tb % 2], smqb[tb % 2]
                ACT(g, sq_[:], psA[:, 0:448], AF.Square, [pAb], [sq_b])
                ss = sm[:, 0:7]
                P.op("dve", lambda e, ss=ss, sq_=sq_: e.tensor_reduce(out=ss, in_=sq_[:].rearrange("p (h d) -> p h d", d=64), axis=AX.X,
                                                                      op=ALU.add), [sq_b], [smb], 448)
                ACT(g, ss, ss, AF.Ln, [smb], [smb], scale=1.0 / 64, bias=EPS)
                ACT(g, ss, ss, AF.Exp, [smb], [smb], scale=-0.5)
                aw = sm[:, 8:16]
                TS(g, "dve", sgn[:, tb, :], psC[:, 64:72], 0.0, 2.0, ALU.is_gt, ALU.mult, [pCb], [sgb[tb]])
                TS(g, "dve", sgn[:, tb, :], sgn[:, tb, :], -1.0, 0.0, ALU.add, ALU.add, [sgb[tb]], [sgb[tb]])
                STT(g, aw, psC[:, 64:72], IDX_SCALE, sgn[:, tb, :], ALU.mult, ALU.mult, [pCb, sgb[tb]], [smb])
                TT(g, "dve", tk[:, 8:16, :], psBq[:, :].rearrange("p (h d) -> p h d", d=64),
                   aw.unsqueeze(2).to_broadcast([128, 8, 64]), ALU.mult, [pBb, smb], [tkb])
                CP(g, "act", tk[:, 16, :], psC[:, 0:64], [pCb], [tkb])
                CP(g, "act", dvx[:, tb, 0:64], psA[:, 448:512], [pAb], [dvb[tb]])
                TT(g, "dve", tk[:, 0:7, :], psA[:, 0:448].rearrange("p (h d) -> p h d", d=64),
                   ss.unsqueeze(2).to_broadcast([128, 7, 64]), ALU.mult, [pAb, smb], [tkb])
                TT(g, "dve", tk[:, 0:7, :], tk[:, 0:7, :], qkg[:], ALU.mult, [tkb] + qkgb, [tkb])

            def p3(tb):
                tk, tkb = tq[tb % NR], tqb[tb % NR]
                x1, x2 = tk[:, :, 0:8], tk[:, :, 8:16]
                cb_ = g.cs[:, tb, :].unsqueeze(1).to_broadcast([128, 18, 8])
                sb_ = g.sn[:, tb, :].unsqueeze(1).to_broadcast([128, 18, 8])
                TT(g, "dve", rt[0][:], x1, cb_, ALU.mult, [tkb, g.csb], [rtb[0]])
                TT(g, "pool", rt[1][:], x2, sb_, ALU.mult, [tkb, g.snb], [rtb[1]])
                TT(g, "dve", rt[2][:], x2, cb_, ALU.mult, [tkb, g.csb], [rtb[2]])
                TT(g, "pool", rt[3][:], x1, sb_, ALU.mult, [tkb, g.snb], [rtb[3]])
                TT(g, "dve", x1, rt[0][:], rt[1][:], ALU.subtract, [rtb[0], rtb[1]], [tkb])
                TT(g, "pool", x2, rt[2][:], rt[3][:], ALU.add, [rtb[2], rtb[3]], [tkb])

            def p4(tb):
                tk, tkb = tq[tb % NR], tqb[tb % NR]
                tkh, tkhb = tbq[tb % 3], tbb[tb % 3]
                CP(g, "act", tkh[:], tk[:], [tkb], [tkhb])
                CP(g, "pool", tkh[:, 7, :], tkh[:, 6, :], [tkhb], [tkhb])
                CP(g, "pool", tkh[:, 17, :], tkh[:, 16, :], [tkhb], [tkhb])

            def p5(tb):
                tkh, tkhb = tbq[tb % 3], tbb[tb % 3]
                flat = tkh[:].rearrange("p h d -> p (h d)")
                pt, ptb = psb(g)
                MM(g, [trf(pt[:, m * 128:(m + 1) * 128], flat[:, m * 128:(m + 1) * 128], g.ident[:]) for m in range(8)],
                   [tkhb, g.cb], [ptb])
                CP(g, "dve", featT[:, 0:8, tb * 128:(tb + 1) * 128], pt[:, :].rearrange("p (m j) -> p m j", j=128), [ptb], [fb[tb]])
                pt2, ptb2 = psb(g)
                MM(g, [trf(pt2[:, 0:128], flat[:, 1024:1152], g.ident[:])], [tkhb, g.cb], [ptb2])
                CP(g, "act", featT[:, 8, tb * 128:(tb + 1) * 128], pt2[:, 0:128], [ptb2], [fb[tb]])

            stages = [p1, p2, p3, p4, p5]
            for e in range(NB + len(stages) - 1):
                for k in range(len(stages) - 1, -1, -1):
                    t_ = e - k
                    if 0 <= t_ < NB:
                        stages[k](t_)
        prefetch(g, ("hihg", l), win_cols(g, l, C_HI, C_HI + 512), 512)
        sc = [sb(g, st, [128, T], F32, "sc") for _ in range(4)]
        scb = nbs(st, "sc", 4, 4)
        junk = sb(g, st, [128, T], BF16, "junk")
        maskq = [sb(g, st, [128, T], BF16, "maskq") for _ in range(2)]
        mqb = nbs(st, "maskq", 2)
        maskT = sb(g, st, [128, NB, 512], BF16, "maskT")
        mTb = nbs(st, "maskT", 4)
        rj = [sb(g, st, [128, 512], BF16, "rj") for _ in range(4)]
        rjb = nbs(st, "rj", 4)
        dg = [sb(g, st, [128, 8, 128], BF16, "dg") for _ in range(2)]
        dgb = nbs(st, "dg", 2)
        sm = [sb(g, st, [128, 8 + 2 * N_BISECT], F32, "bis") for _ in range(2)]
        smb = nbs(st, "bis", 2)
        cvec = sb(g, st, [128, N_BISECT], F32, "cvec")
        c255 = sb(g, st, [128, 1], F32, "c255")
        cvb = nb(st, "cvec")
        for n_ in range(N_BISECT):
            MS(g, "pool", cvec[:, n_:n_ + 1], 2.0 ** -(n_ + 1), [cvb])
        MS(g, "pool", c255[:], 255.5, [cvb])
        Pt = [sb(g, st, [128, 512], BF16, "Pt") for _ in range(4)]
        Ptb = nbs(st, "Pt", 4)
        Pm = [sb(g, st, [128, 512], BF16, "Pm") for _ in range(4)]
        Pmb = nbs(st, "Pm", 4)
        rs = sb(g, st, [64, 512], F32, "rs")
        rsb = nb(st, "rs")
        cnt_ = {"ri": 0, "pi": 0, "pm": 0}

        def idx_blocks(blocks):
            tiles = []
            for i in blocks:
                nk = (i + 1) * 128
                d_, d_b = dg[i % 2], dgb[i % 2]
                for j in range(8):
                    TS(g, "pool", d_[:, j, :], g.ident[:], sgn[:, i, j:j + 1], None, ALU.mult, None, [g.cb, sgb[i]], [d_b])
                for kc in range((nk + 511) // 512):
                    n = min(512, nk - kc * 512)
                    for j in range(8):
                        tiles.append((i, kc, n, j))
            stt = {}

            def s1(t):
                i, kc, n, j = tiles[t]
                po = 64 * (j % 2)
                psZ, pZb = psf(g, "ixZ", [0, 1, 2])
                stt[t] = [psZ, pZb]
                MM(g, [mmf(psZ[:, 0:n], featT[po:po + 64, 4 + j // 2, i * 128:(i + 1) * 128],
                           featT[po:po + 64, 8, kc * 512:kc * 512 + n], True, True)],
                   [fb[i]] + fb[kc * 4:(kc * 512 + n) // 128], [pZb])

            def s2(t):
                i, kc, n, j = tiles[t]
                psZ, pZb = stt[t]
                r_, r_b = rj[cnt_["ri"] % 4], rjb[cnt_["ri"] % 4]
                cnt_["ri"] += 1
                stt[t] += [r_, r_b]
                ACT(g, r_[:, 0:n], psZ[:, 0:n], AF.Relu, [pZb], [r_b])

            def s3(t):
                i, kc, n, j = tiles[t]
                r_, r_b = stt[t][2], stt[t][3]
                if j == 0:
                    cnt_["psS"] = psf(g, "ixS", [3, 4])
                psS, pSb = cnt_["psS"]
                d_, d_b = dg[i % 2], dgb[i % 2]
                MM(g, [mmf(psS[:, 0:n], d_[:, j, :], r_[:, 0:n], j == 0, j == 7)], [d_b, r_b], [pSb])
                if j == 7:
                    CP(g, "act", sc[i % 4][:, kc * 512:kc * 512 + n], psS[:, 0:n], [pSb], [scb[i % 4][kc]])
                del stt[t]

            pipeline([s1, s2, s3], len(tiles))

        def bis_pair(p):
            blocks = [2 * p, 2 * p + 1]
            st_ = []
            for bi, i in enumerate(blocks):
                nk = (i + 1) * 128
                s_, s_b = sc[i % 4], scb[i % 4]
                nkc = (nk + 511) // 512
                srd = s_b[0:nkc]
                m_, m_b = sm[bi], smb[bi]
                rmax, rmin, step0, mid, cntv, tt = (m_[:, c:c + 1] for c in range(6))
                stepc = m_[:, 8:8 + N_BISECT]
                if nk > 256:
                    P.op("dve", lambda e, s_=s_, nk=nk, rmax=rmax: e.tensor_reduce(out=rmax, in_=s_[:, 0:nk], axis=AX.X, op=ALU.max), srd, [m_b], nk)
                    P.op("dve", lambda e, s_=s_, nk=nk, rmin=rmin: e.tensor_reduce(out=rmin, in_=s_[:, 0:nk], axis=AX.X, op=ALU.min), srd, [m_b], nk)
                dsl = s_[:, i * 128:(i + 1) * 128]
                TT(g, "pool", dsl, dsl, g.caus01[:], ALU.mult, [s_b[i // 4], g.cb], [s_b[i // 4]])
                TT(g, "pool", dsl, dsl, g.negfill[:], ALU.add, [s_b[i // 4], g.cb], [s_b[i // 4]])
                st_.append((i, nk, s_, srd, m_, m_b, rmax, rmin, step0, mid, cntv, tt, stepc))
            act = [x for x in st_ if x[1] > 256]
            for (i, nk, s_, srd, m_, m_b, rmax, rmin, step0, mid, cntv, tt, stepc) in act:
                TT(g, "dve", step0, rmax, rmin, ALU.subtract, [m_b], [m_b])
            for (i, nk, s_, srd, m_, m_b, rmax, rmin, step0, mid, cntv, tt, stepc) in act:
                TS(g, "dve", stepc, cvec[:], step0, None, ALU.mult, None, [m_b, cvb], [m_b])
            for (i, nk, s_, srd, m_, m_b, rmax, rmin, step0, mid, cntv, tt, stepc) in act:
                TS(g, "dve", mid, stepc[:, 0:1], rmin, g.zc[:, 0:1], ALU.add, ALU.add, [m_b, g.cb], [m_b])
            for n_ in range(N_BISECT):
                for (i, nk, s_, srd, m_, m_b, rmax, rmin, step0, mid, cntv, tt, stepc) in act:
                    TS(g, "dve", junk[:, 0:nk], s_[:, 0:nk], mid, g.zc[:, 0:1], ALU.is_ge, ALU.add, srd + [m_b, g.cb], [m_b], accum_out=cntv)
                for (i, nk, s_, srd, m_, m_b, rmax, rmin, step0, mid, cntv, tt, stepc) in act:
                    TS(g, "dve", tt, cntv, c255[:, 0:1], stepc[:, n_:n_ + 1], ALU.is_ge, ALU.mult, [m_b, cvb], [m_b])
                for (i, nk, s_, srd, m_, m_b, rmax, rmin, step0, mid, cntv, tt, stepc) in act:
                    nn = min(n_ + 1, N_BISECT - 1)
                    TS(g, "dve", mid, tt, stepc[:, nn:nn + 1], mid, ALU.subtract, ALU.add, [m_b], [m_b])
            for (i, nk, s_, srd, m_, m_b, rmax, rmin, step0, mid, cntv, tt, stepc) in st_:
                thr = mid if nk > 256 else g.negbig[:, 0:1]
                mq, mq_b = maskq[i % 2], mqb[i % 2]
                TS(g, "dve", mq[:, 0:nk], s_[:, 0:nk], thr, None, ALU.is_ge, None, srd + [m_b, g.cb], [mq_b])

        def mT_pair(p):
            for i in (2 * p, 2 * p + 1):
                ii = i % 4
                mq, mq_b = maskq[i % 2], mqb[i % 2]
                for k0 in range(0, i + 1, 8):
                    k1 = min(i + 1, k0 + 8)
                    pt, ptb = psb(g)
                    MM(g, [trf(pt[:, (kb - k0) * 128:(kb - k0 + 1) * 128], mq[:, kb * 128:(kb + 1) * 128], g.ident[:])
                           for kb in range(k0, k1)], [mq_b, g.cb], [ptb])
                    CP(g, "act", maskT[:, k0:k1, ii * 128:(ii + 1) * 128],
                       pt[:, 0:(k1 - k0) * 128].rearrange("p (m j) -> p m j", j=128), [ptb], [mTb[ii]])

        def att_chunk(qc):
            last = 4 * qc + 3
            tiles = [(h, kb) for h in range(6) for kb in range(last + 1)]
            stt = {}
            pso = {}

            def s1(t):
                h, kb = tiles[t]
                hp, po = h // 2, 64 * (h % 2)
                j0 = max(0, kb * 128 - qc * 512)
                psL, pLb = psf(g, "dsL", [0, 1, 2])
                stt[t] = [psL, pLb]
                MM(g, [mmf(psL[:, j0:512], featT[po:po + 64, 3, kb * 128:(kb + 1) * 128],
                           featT[po:po + 64, hp, qc * 512 + j0:(qc + 1) * 512], True, True)],
                   [fb[kb]] + fb[qc * 4:qc * 4 + 4], [pLb])

            def s2(t):
                h, kb = tiles[t]
                j0 = max(0, kb * 128 - qc * 512)
                psL, pLb = stt[t][0], stt[t][1]
                pi = cnt_["pi"]
                cnt_["pi"] += 1
                p_, p_b = Pt[pi % 4], Ptb[pi % 4]
                stt[t] += [p_, p_b]
                ACT(g, p_[:, j0:512], psL[:, j0:512], AF.Exp, [pLb], [p_b], scale=0.125)

            def s3(t):
                h, kb = tiles[t]
                j0 = max(0, kb * 128 - qc * 512)
                p_, p_b = stt[t][2], stt[t][3]
                pm = cnt_["pm"]
                cnt_["pm"] += 1
                m_, m_b = Pm[pm % 4], Pmb[pm % 4]
                stt[t] += [m_, m_b]
                TT(g, "pool" if t % 3 == 0 else "dve", m_[:, j0:512], p_[:, j0:512], maskT[:, kb, j0:512], ALU.mult,
                   [p_b] + mTb[j0 // 128:4], [m_b])

            def s4(t):
                h, kb = tiles[t]
                hp, po = h // 2, 64 * (h % 2)
                j0 = max(0, kb * 128 - qc * 512)
                m_, m_b = stt[t][4], stt[t][5]
                if kb == 0:
                    pso[h] = psf(g, "dsO", [4, 5])
                psO, pOb = pso[h]
                MM(g, [mmf(psO[:, j0:512], dvx[:, kb, :], m_[:, j0:512], kb == 0, kb == last)], [dvb[kb], m_b], [pOb])
                if kb == last:
                    ACT(g, rs[0:64, :], psO[64:128, :], AF.Ln, [pOb], [rsb])
                    ACT(g, rs[0:64, :], rs[0:64, :], AF.Exp, [rsb], [rsb], scale=-1.0)
                    TT(g, "dve", odT[po:po + 64, hp, qc * 512:(qc + 1) * 512], psO[0:64, :], rs[0:64, :], ALU.mult, [pOb, rsb],
                       [odb[hp][qc]])
                del stt[t]

            pipeline([s1, s2, s3, s4], len(tiles))

        idx_blocks([14, 15])
        for p in range(7, -1, -1):
            if p > 0:
                idx_blocks([2 * p - 2, 2 * p - 1])
            bis_pair(p)
            mT_pair(p)
            if p % 2 == 0:
                att_chunk(p // 2)


def phase_hgrn(g, l, hT, hTb, ohT, ohb):
    P = g.P
    import os
    if int(os.environ.get("HGL", "9")) == 0:
        return
    with scope(g) as st:
        hi_tm = sb(g, st, [128, NB, 256], BF16, "hi_tm")
        hib = nbs(st, "hi", NB)
        hgs = sb(g, st, [128, NB, 256], BF16, "hgs")
        hgb = nbs(st, "hgs", NB)
        onb = sb(g, st, [128, 64], F32, "onorm")
        onbb = nb(st, "onorm")
        DMA(g, "sp", onb[:], g.onorm[l:l + 1, :].to_broadcast([128, 64]), (), [onbb])
        with scope(g) as s2:
            if take_pre(g, ("hihg", l)):
                w, wb = g.pre, g.preb
            else:
                w = sb(g, s2, [128, DC, 512], BF16, "whihg")
                wb = nb(s2, "whihg")
                DMA(g, "pool", w[:], win_cols(g, l, C_HI, C_HI + 512), (), [wb])
            sgs = [sb(g, s2, [128, 256], F32, "sgs") for _ in range(2)]
            sgsb = nbs(s2, "sgs", 2)
            for tb in range(NB):
                ps, pb = psf(g, "proj", [0, 1, 2, 3, 4, 5])
                hgv = int(os.environ.get("HGV", "15"))
                if hgv & 8:
                    mm_tm(g, ps, 512, w, wb, 0, hT, hTb, tb, pb)
                if hgv & 1:
                    CP(g, "dve", hi_tm[:, tb, :], ps[:, 0:256], [pb], [hib[tb]])
                sgt, sgtb = sgs[tb % 2], sgsb[tb % 2]
                if hgv & 2:
                    ACT(g, sgt[:], ps[:, 256:512], AF.Exp if hgv & 16 else AF.Sigmoid, [pb], [sgtb])
                if hgv & 4:
                    TT(g, "dve", hgs[:, tb, :], ps[:, 256:512], sgt[:], ALU.mult, [pb, sgtb], [hgb[tb]])
        with scope(g) as s3:
            NH = 4
            R = 4
            qtT = sb(g, s3, [128, NH, T], BF16, "qtT")
            ktT = sb(g, s3, [128, NH, T], BF16, "ktT")
            qtb = nbs(s3, "qt", NH)
            ktb = nbs(s3, "kt", NH)
            kt_tm = sb(g, s3, [128, NB, NH * 128], BF16, "kt_tm")
            kttb = nbs(s3, "kttm", NH)
            t1 = sb(g, s3, [128, T], F32, "t1")
            t2 = sb(g, s3, [128, T], F32, "t2")
            t3 = sb(g, s3, [128, T], F32, "t3")
            t1h, t2h, t3h = nbs(s3, "t1", 4), nbs(s3, "t2", 4), nbs(s3, "t3", 4)
            ebl = sb(g, s3, [128, NH, 32], F32, "ebl")
            eblb = nb(s3, "ebl")
            W = [sb(g, s3, [128, NH, 64], F32, "W") for _ in range(2)]
            Wb = nbs(s3, "W", 2)
            Sbf = [sb(g, s3, [128, NH, 64], BF16, "Sbf") for _ in range(R)]
            Sbfb = nbs(s3, "Sbf", R)
            attm = [sb(g, s3, [128, NH, 128], BF16, "attm") for _ in range(3)]
            attb = nbs(s3, "attm", 3)
            o_tm = [sb(g, s3, [128, NH * 64], F32, "o_tm") for _ in range(3)]
            otb = nbs(s3, "otm", 3)
            osq = sb(g, s3, [128, NH * 64], F32, "osq")
            osqb = nb(s3, "osq")
            og = [sb(g, s3, [128, NH * 64], BF16, "og") for _ in range(2)]
            ogb = nbs(s3, "og", 2)
            sm = [sb(g, s3, [128, 4], F32, "hsm") for _ in range(2)]
            smb = nbs(s3, "hsm", 2)
            wq = [sb(g, s3, [128, DC, 128], BF16, "wq") for _ in range(2)]
            wqb = nbs(s3, "wq", 2)
            wf = [sb(g, s3, [128, DC, 128], BF16, "wf") for _ in range(2)]
            wfb = nbs(s3, "wf", 2)

            def load_head(hd):
                DMA(g, "pool", wf[hd % 2][:], win_cols(g, l, C_HF + hd * 128, C_HF + hd * 128 + 128), (), [wfb[hd % 2]])
                DMA(g, "pool", wq[hd % 2][:], win_cols(g, l, C_HQ + hd * 128, C_HQ + hd * 128 + 128), (), [wqb[hd % 2]])

            def kt_transposes(hd):
                for k0 in (0, 8):
                    pt, ptb = psb(g)
                    MM(g, [trf(pt[:, m * 128:(m + 1) * 128], ktT[:, hd, (k0 + m) * 128:(k0 + m + 1) * 128], g.ident[:])
                           for m in range(8)], [ktb[hd], g.cb], [ptb])
                    CP(g, "act" if k0 == 0 else "dve", kt_tm[:, k0:k0 + 8, hd * 128:(hd + 1) * 128],
                       pt[:, :].rearrange("p (m j) -> p m j", j=128), [ptb], [kttb[hd]])

            load_head(0)
            for hd in range(NH):
                if hd + 1 < NH:
                    load_head(hd + 1)
                w_f, w_fb, w_q, w_qb = wf[hd % 2], wfb[hd % 2], wq[hd % 2], wqb[hd % 2]
                NQ = 4
                HS = [slice(q_ * (T // NQ), (q_ + 1) * (T // NQ)) for q_ in range(NQ)]
                for tc in range(4):
                    ps, pb = psf(g, "proj", [0, 1, 2, 3, 4, 5])
                    mm_fm(g, ps, 128, 512, w_f, w_fb, 0, hT, hTb, tc * 512, pb)
                    ACT(g, t1[:, tc * 512:(tc + 1) * 512], ps[:, :], AF.Sigmoid, [pb], [t1h[tc]])
                for hf in range(NQ):
                    TS(g, "dve", t1[:, HS[hf]], t1[:, HS[hf]], g.oml[:, l, hd:hd + 1], g.lbv[:, l, hd:hd + 1], ALU.mult, ALU.add,
                       [t1h[hf], g.cb], [t1h[hf]])
                for hf in range(NQ):
                    ACT(g, t2[:, HS[hf]], t1[:, HS[hf]], AF.Copy, [t1h[hf]], [t2h[hf]], scale=-1.0, bias=1.0)
                for hf in range(NQ):
                    TS(g, "dve", t1[:, HS[hf]], t1[:, HS[hf]], F_MIN, None, ALU.max, None, [t1h[hf]], [t1h[hf]])
                for hf in range(NQ):
                    ACT(g, t1[:, HS[hf]], t1[:, HS[hf]], AF.Ln, [t1h[hf]], [t1h[hf]])
                for hf in range(NQ):
                    P.op("dve", lambda e, hf=hf: e.tensor_tensor_scan(out=t3[:, HS[hf]], data0=g.resetm[:, HS[hf]], data1=t1[:, HS[hf]],
                                                                     initial=0.0, op0=ALU.mult, op1=ALU.add),
                         [t1h[hf], g.cb], [t3h[hf]], 2 * T // NQ)
                for hf in range(NQ):
                    TS(g, "dve", t3[:, HS[hf]], t3[:, HS[hf]], -80.0, None, ALU.max, None, [t3h[hf]], [t3h[hf]])
                for hf in range(NQ):
                    ACT(g, t1[:, HS[hf]], t3[:, HS[hf]], AF.Exp, [t3h[hf]], [t1h[hf]])
                for hf in range(NQ):
                    CP(g, "pool", ebl[:, hd, hf * (32 // NQ):(hf + 1) * (32 // NQ)].unsqueeze(2),
                       t1[:, HS[hf]].rearrange("p (c j) -> p c j", j=64)[:, :, 63:64], [t1h[hf]], [eblb])
                for hf in range(NQ):
                    ACT(g, t3[:, HS[hf]], t3[:, HS[hf]], AF.Exp, [t3h[hf]], [t3h[hf]], scale=-1.0)
                for hf in range(NQ):
                    TT(g, "dve", ktT[:, hd, HS[hf]], t2[:, HS[hf]], t3[:, HS[hf]], ALU.mult, [t2h[hf], t3h[hf]], [ktb[hd]])
                for tc in range(4):
                    ps, pb = psf(g, "proj", [0, 1, 2, 3, 4, 5])
                    mm_fm(g, ps, 128, 512, w_q, w_qb, 0, hT, hTb, tc * 512, pb)
                    ACT(g, t2[:, tc * 512:(tc + 1) * 512], ps[:, :], AF.Sigmoid, [pb], [t2h[tc]])
                    TT(g, "dve", t2[:, tc * 512:(tc + 1) * 512], ps[:, :], t2[:, tc * 512:(tc + 1) * 512], ALU.mult, [pb, t2h[tc]],
                       [t2h[tc]])
                for hf in range(NQ):
                    TT(g, "dve", qtT[:, hd, HS[hf]], t2[:, HS[hf]], t1[:, HS[hf]], ALU.mult, [t2h[hf], t1h[hf]], [qtb[hd]])
                if hd > 0:
                    kt_transposes(hd - 1)

            kt_transposes(NH - 1)
            NCH = 32
            xps = {}

            def o_transposes(tb):
                og_, og_b = og[tb % 2], ogb[tb % 2]
                pt, ptb = psb(g)
                MM(g, [trf(pt[:, m * 128:(m + 1) * 128], og_[:, m * 128:(m + 1) * 128], g.ident[:]) for m in range(2)],
                   [og_b, g.cb], [ptb])
                CP(g, "act", ohT[:, 0:2, tb * 128:(tb + 1) * 128], pt[:, 0:256].rearrange("p (m j) -> p m j", j=128), [ptb],
                   [ohb[0][tb // 4], ohb[1][tb // 4]])

            def emit_X(c):
                tb, pr = c // 2, (c % 2) * 64
                psX, pXb = psf(g, "hgX", [0, 1, 2])
                xps[c] = (psX, pXb)
                MM(g, [mmf(psX[:, hh * 64:(hh + 1) * 64], kt_tm[pr:pr + 64, tb, hh * 128:(hh + 1) * 128],
                           hi_tm[pr:pr + 64, tb, hh * 64:(hh + 1) * 64], True, True) for hh in range(NH)], kttb + [hib[tb]], [pXb])

            def emit_A(tb):
                psA, pAb = psf(g, "hgA", [3])
                MM(g, [mmf(psA[:, hh * 128:(hh + 1) * 128], ktT[:, hh, tb * 128:(tb + 1) * 128],
                           qtT[:, hh, tb * 128:(tb + 1) * 128], True, True) for hh in range(NH)], ktb + qtb, [pAb])
                TT(g, "dve", attm[tb % 3][:], psA[:, :].rearrange("p (h t) -> p h t", t=128),
                   g.maskbd[:].unsqueeze(1).to_broadcast([128, NH, 128]), ALU.mult, [pAb, g.cb], [attb[tb % 3]])

            emit_X(0)
            emit_X(1)
            emit_A(0)
            CP(g, "dve", W[0][:], xps[0][0][:, 0:NH * 64].rearrange("p (h v) -> p h v", v=64), [xps[0][1]], [Wb[0]])
            MS(g, "pool", Sbf[0][:], 0.0, [Sbfb[0]])
            for c in range(NCH):
                tb, half = c // 2, c % 2
                pr = half * 64
                if c + 2 < NCH:
                    emit_X(c + 2)
                if half == 0 and tb + 1 < NB:
                    emit_A(tb + 1)
                if c + 1 < NCH:
                    eb_ = ebl[:, :, c:c + 1].to_broadcast([128, NH, 64])
                    TT(g, "pool", Sbf[(c + 1) % R][:], W[c % 2][:], eb_, ALU.mult, [Wb[c % 2], eblb], [Sbfb[(c + 1) % R]])
                    psX, pXb = xps.pop(c + 1)
                    for hh in range(NH):
                        STT(g, W[(c + 1) % 2][:, hh, :], W[c % 2][:, hh, :], ebl[:, hh, c:c + 1], psX[:, hh * 64:(hh + 1) * 64],
                            ALU.mult, ALU.add, [Wb[c % 2], eblb, pXb], [Wb[(c + 1) % 2]])
                am, amb = attm[tb % 3], attb[tb % 3]
                ot, otbuf = o_tm[tb % 3], otb[tb % 3]
                psO, pOb = psf(g, "hgO", [4, 5])
                fns = []
                for hh in range(NH):
                    fns.append(mmf(psO[0:64, hh * 64:(hh + 1) * 64], am[pr:pr + 64, hh, pr:pr + 64],
                                   hi_tm[pr:pr + 64, tb, hh * 64:(hh + 1) * 64], True, False))
                    fns.append(mmf(psO[0:64, hh * 64:(hh + 1) * 64], qtT[:, hh, c * 64:(c + 1) * 64], Sbf[c % R][:, hh, :], False, True))
                MM(g, fns, [amb, hib[tb], Sbfb[c % R]] + qtb, [pOb])
                CP(g, "act", ot[pr:pr + 64, :], psO[0:64, 0:NH * 64], [pOb], [otbuf])
                if half == 1:
                    sm_, sm_b = sm[tb % 2], smb[tb % 2]
                    og_, og_b = og[tb % 2], ogb[tb % 2]
                    TT(g, "pool", osq[:], ot[:], ot[:], ALU.mult, [otbuf], [osqb])
                    ss = sm_[:, 0:NH]
                    P.op("dve", lambda e, ss=ss: e.tensor_reduce(out=ss, in_=osq[:].rearrange("p (h d) -> p h d", d=64), axis=AX.X,
                                                                 op=ALU.add), [osqb], [sm_b], NH * 64)
                    ACT(g, ss, ss, AF.Ln, [sm_b], [sm_b], scale=1.0 / 64, bias=EPS)
                    ACT(g, ss, ss, AF.Exp, [sm_b], [sm_b], scale=-0.5)
                    o3 = ot[:].rearrange("p (h d) -> p h d", d=64)
                    TT(g, "dve", o3, o3, ss.unsqueeze(2).to_broadcast([128, NH, 64]), ALU.mult, [otbuf, sm_b], [otbuf])
                    TT(g, "dve", o3, o3, onb[:].unsqueeze(1).to_broadcast([128, NH, 64]), ALU.mult, [otbuf, onbb], [otbuf])
                    TT(g, "dve", og_[:], ot[:], hgs[:, tb, :], ALU.mult, [otbuf, hgb[tb]], [og_b])
                    if tb > 0:
                        o_transposes(tb - 1)
                    if tb == NB - 1:
                        o_transposes(tb)


def phase_mix(g, l, hT, hTb, osbT, osbb, odT, odb, ohT, ohb, src_ap, src_bufs):
    P = g.P
    with scope(g) as st:
        mixT = sb(g, st, [128, DC, T], BF16, "mixT")
        mxb = nbs(st, "mix", DC, 4)
        wg = [sb(g, st, [128, DC, 3, 256], BF16, "wg") for _ in range(2)]
        wgb = nbs(st, "wg", 2, 3)
        wy = sb(g, st, [128, 8, D], BF16, "wy")
        wyb = nbs(st, "wy", 3)
        sg = [sb(g, st, [128, 512], F32, "sg") for _ in range(2)]
        sgb = nbs(st, "sg", 2)
        acc = [sb(g, st, [128, 512], F32, "acc") for _ in range(2)]
        accb = nbs(st, "acc", 2)
        tm = [sb(g, st, [128, 512], F32, "tm") for _ in range(2)]
        tmb = nbs(st, "tm", 2)
        k = 0

        def load_pair(dp):
            w_, w_b = wg[dp % 2], wgb[dp % 2]
            for gi in range(3):
                c0 = C_G + gi * 1024 + dp * 256
                DMA(g, "pool", w_[:, :, gi, :], win_cols(g, l, c0, c0 + 256), (), [w_b[gi]])

        load_pair(0)
        DMA(g, "pool", wy[:, 0:3, :], g.w_sb[l].rearrange("(c p) n -> p c n", p=128), (), [wyb[0]])
        DMA(g, "pool", wy[:, 3:6, :], g.w_dsa[l].rearrange("(c p) n -> p c n", p=128), (), [wyb[1]])
        DMA(g, "pool", wy[:, 6:8, :], g.w_hg[l].rearrange("(c p) n -> p c n", p=128), (), [wyb[2]])
        wo = sb(g, st, [128, DC, D], BF16, "wo")
        wob = nbs(st, "wo", 2)
        for dc in range(DC):
            dp, do = dc // 2, (dc % 2) * 128
            w_, w_b = wg[dp % 2], wgb[dp % 2]
            y_, y_b = wy, wyb
            if dc % 2 == 0:
                if dp + 1 < DC // 2:
                    load_pair(dp + 1)
                else:
                    for nh in range(2):
                        DMA(g, "pool", wo[:, :, nh * 512:(nh + 1) * 512],
                            g.w_out[l].rearrange("(c p) n -> p c n", p=128)[:, :, nh * 512:(nh + 1) * 512], (), [wob[nh]])
                    prefetch(g, ("wu", l, 0), g.w_up[l].rearrange("(c p) n -> p c n", p=128)[:, :, 0:512], 512)
            for tc in range(4):
                a_, a_b = acc[k % 2], accb[k % 2]
                for gi, (oT, obufs, nch, c0) in enumerate(((osbT, osbb, 3, 0), (odT, odb, 3, 3), (ohT, ohb, 2, 6))):
                    psG, pGb = psf(g, "mxG", [0, 1, 2])
                    MM(g, [mmf(psG[:, :], w_[:, c, gi, do:do + 128], hT[:, c, tc * 512:(tc + 1) * 512], c == 0, c == DC - 1) for c in range(DC)],
                       [w_b[gi]] + hTb[tc * 4:tc * 4 + 4], [pGb])
                    s_, s_b = sg[(k * 3 + gi) % 2], sgb[(k * 3 + gi) % 2]
                    ACT(g, s_[:], psG[:, :], AF.Sigmoid, [pGb], [s_b])
                    psY, pYb = psf(g, "mxY", [3, 4, 5])
                    MM(g, [mmf(psY[:, :], y_[:, c0 + c, dc * 128:(dc + 1) * 128], oT[:, c, tc * 512:(tc + 1) * 512], c == 0, c == nch - 1)
                           for c in range(nch)],
                       [y_b[gi]] + [obufs[c][tc] for c in range(nch)], [pYb])
                    if gi == 0:
                        TT(g, "dve", a_[:], psY[:, :], s_[:], ALU.mult, [pYb, s_b], [a_b])
                    else:
                        t_, t_b = tm[gi % 2], tmb[gi % 2]
                        TT(g, "dve", t_[:], psY[:, :], s_[:], ALU.mult, [pYb, s_b], [t_b])
                        if gi == 1:
                            TT(g, "pool", a_[:], a_[:], t_[:], ALU.add, [a_b, t_b], [a_b])
                        else:
                            TT(g, "pool", mixT[:, dc, tc * 512:(tc + 1) * 512], a_[:], t_[:], ALU.add, [a_b, t_b], [mxb[dc][tc]])
                k += 1
        xbs = [sb(g, st, [128, D], F32, "xb") for _ in range(3)]
        xbb = nbs(st, "xb", 3)
        nctx = norm_setup(g, st, g.norm_mlp[l:l + 1, :])
        def part_a(tb):
            xb, xbuf = xbs[tb % 3], xbb[tb % 3]
            DMA(g, "sp", xb[:], src_ap[tb * 128:(tb + 1) * 128, :], [src_bufs[tb]], [xbuf])
            for nh in range(2):
                ps, pb = psf(g, "mxO", [0, 1, 2, 3, 4, 5])
                MM(g, [mmf(ps[:, :], mixT[:, c, tb * 128:(tb + 1) * 128], wo[:, c, nh * 512:(nh + 1) * 512], c == 0, c == DC - 1)
                       for c in range(DC)], [wob[nh]] + [mxb[c][tb // 4] for c in range(DC)], [pb])
                xs = xb[:, nh * 512:(nh + 1) * 512]
                TT(g, "dve", xs, ps[:, :], xs, ALU.add, [pb, xbuf], [xbuf])
            DMA(g, "sp", g.xres_d[tb * 128:(tb + 1) * 128, :], xb[:], [xbuf], [g.xres_b[tb]])

        part_a(0)
        for tb in range(NB):
            if tb + 1 < NB:
                part_a(tb + 1)
            norm_block(g, nctx, xbs[tb % 3], xbb[tb % 3], tb, hT, hTb)


def phase_ffn(g, l, hT, hTb, dst_ap, dst_bufs):
    for half in range(2):
        last = half == 1
        with scope(g) as st:
            uT = sb(g, st, [128, 16, T], BF16, "uT")
            ub = nbs(st, "uT", 16, 4)
            wd = sb(g, st, [128, 16, D], BF16, "wd")
            wdb = nbs(st, "wd", 2)
            wdv = g.w_down[l].rearrange("(f p) n -> p f n", p=128)
            wu = [sb(g, st, [128, DC, 512], BF16, "wu") for _ in range(2)]
            wub = nbs(st, "wu", 2)
            rt = [sb(g, st, [128, 512], BF16, "rt") for _ in range(2)]
            rtb = nbs(st, "rt", 2)
            k = 0

            def load_wu(g4):
                c0 = half * 2048 + g4 * 512
                DMA(g, "pool", wu[g4 % 2][:], g.w_up[l].rearrange("(c p) n -> p c n", p=128)[:, :, c0:c0 + 512], (), [wub[g4 % 2]])

            pre0 = take_pre(g, ("wu", l, half))
            if not pre0:
                load_wu(0)
            for g4 in range(4):
                w_, w_b = wu[g4 % 2], wub[g4 % 2]
                if g4 == 0 and pre0:
                    w_, w_b = g.pre, g.preb
                if g4 + 1 < 4:
                    load_wu(g4 + 1)
                if g4 == 1:
                    for nh in range(2):
                        DMA(g, "pool", wd[:, :, nh * 512:(nh + 1) * 512], wdv[:, half * 16:(half + 1) * 16, nh * 512:(nh + 1) * 512], (),
                            [wdb[nh]])
                for fcl in range(4):
                    fc = g4 * 4 + fcl
                    for tc in range(4):
                        ps, pb = psf(g, "proj", [0, 1, 2, 3, 4, 5])
                        mm_fm(g, ps, 128, 512, w_, w_b, fcl * 128, hT, hTb, tc * 512, pb)
                        r_, r_b = rt[k % 2], rtb[k % 2]
                        k += 1
                        ACT(g, r_[:], ps[:, :], AF.Relu, [pb], [r_b])
                        TT(g, "pool", uT[:, fc, tc * 512:(tc + 1) * 512], r_[:], r_[:], ALU.mult, [r_b], [ub[fc][tc]])
            if half == 0:
                prefetch(g, ("wu", l, 1), g.w_up[l].rearrange("(c p) n -> p c n", p=128)[:, :, 2048:2560], 512)
            elif l + 1 < DEPTH:
                prefetch(g, ("sq", l + 1), win_cols(g, l + 1, C_SQ, C_SQ + 384), 384)
            xbs = [sb(g, st, [128, D], F32, "xb") for _ in range(4)]
            xbb = nbs(st, "xb", 4)
            nctx = norm_setup(g, st, g.norm_mix[l + 1:l + 2, :]) if (last and l + 1 < DEPTH) else None
            def down_a(tb):
                xb, xbuf = xbs[tb % 4], xbb[tb % 4]
                DMA(g, "sp", xb[:], g.xres_d[tb * 128:(tb + 1) * 128, :], [g.xres_b[tb]], [xbuf])
                for nh in range(2):
                    ps, pb = psf(g, "proj", [0, 1, 2, 3, 4, 5])
                    MM(g, [mmf(ps[:, :], uT[:, fc, tb * 128:(tb + 1) * 128], wd[:, fc, nh * 512:(nh + 1) * 512], fc == 0, fc == 15)
                           for fc in range(16)], [wdb[nh]] + [ub[fc][tb // 4] for fc in range(16)], [pb])
                    xs = xb[:, nh * 512:(nh + 1) * 512]
                    TT(g, "dve", xs, ps[:, :], xs, ALU.add, [pb, xbuf], [xbuf])
                if last:
                    DMA(g, "sp", dst_ap[tb * 128:(tb + 1) * 128, :], xb[:], [xbuf], [dst_bufs[tb]])
                else:
                    DMA(g, "sp", g.xres_d[tb * 128:(tb + 1) * 128, :], xb[:], [xbuf], [g.xres_b[tb]])

            down_a(0)
            for tb in range(NB):
                if tb + 1 < NB:
                    down_a(tb + 1)
                if last and nctx is not None:
                    norm_block(g, nctx, xbs[tb % 4], xbb[tb % 4], tb, hT, hTb)


def dump_t(g, name, t, ncol):
    if g.dump == name:
        g.P.barrier()
        DMA(g, "sp", g.dbg_d[:, 0:ncol], t, [], [g.dbgb])
        g.P.wait_bufs("sp", [g.dbgb])
        g.P.barrier()


def build_layer(g, l):
    if l > 0 and g.stage < 7:
        return
    src_ap, src_bufs = (g.x_d, g.xin_b) if l == 0 else (g.xres_d, g.xres_b)
    with scope(g) as ls:
        hT, hTb = g.hT, g.hTb
        if l == 0:
            with scope(g) as st:
                norm_T(g, st, src_ap, src_bufs, g.norm_mix[l:l + 1, :], hT, hTb)
        if l == 0:
            dump_t(g, "hT", hT[:].rearrange("p c t -> p (c t)"), 8 * T)
        if g.stage < 2:
            return
        with scope(g) as ms:
            osbT = sb(g, ms, [128, 3, T], BF16, "osbT")
            odT = sb(g, ms, [128, 3, T], BF16, "odT")
            ohT = sb(g, ms, [128, 2, T], BF16, "ohT")
            osbb = nbs(ms, "osb", 3, 4)
            odb = nbs(ms, "od", 3, 4)
            ohb = nbs(ms, "oh", 2, 4)
            phase_sb(g, l, hT, hTb, osbT, osbb)
            if l == 0:
                dump_t(g, "osbT", osbT[:].rearrange("p c t -> p (c t)"), 3 * T)
            if g.stage < 3:
                return
            phase_dsa(g, l, hT, hTb, odT, odb)
            if l == 0:
                dump_t(g, "odT", odT[:].rearrange("p c t -> p (c t)"), 3 * T)
            if g.stage < 4:
                return
            phase_hgrn(g, l, hT, hTb, ohT, ohb)
            if l == 0:
                dump_t(g, "ohT", ohT[:].rearrange("p c t -> p (c t)"), 2 * T)
            if g.stage < 5:
                return
            phase_mix(g, l, hT, hTb, osbT, osbb, odT, odb, ohT, ohb, src_ap, src_bufs)
        if g.stage < 6:
            return
        if l == DEPTH - 1:
            phase_ffn(g, l, hT, hTb, g.out_d, g.out_b)
        else:
            phase_ffn(g, l, hT, hTb, g.xres_d, g.xres_b)


_NC_CACHE = {}


def rope_tables():
    half = 8
    inv = 500000.0 ** (-(np.arange(half, dtype=np.float32) * 2.0) / 16.0)
    ang = np.arange(T, dtype=np.float32)[:, None] * inv[None, :].astype(np.float32)
    return np.cos(ang).astype(np.float32), np.sin(ang).astype(np.float32)


def kernel(x, norm_mix, w_in, qn_dsa, kn_dsa, hgrn_lb, hgrn_onorm, w_br_sb, w_br_dsa, w_br_hgrn, w_out, norm_mlp, w_up, w_down):
    if "nc" not in _NC_CACHE:
        _NC_CACHE["nc"] = build_two_pass()
    nc = _NC_CACHE["nc"]
    f = lambda a: np.ascontiguousarray(np.asarray(a, dtype=np.float32))
    cs, sn = rope_tables()
    shared = dict(norm_mix=f(norm_mix), w_in=f(w_in), qn_dsa=f(qn_dsa), kn_dsa=f(kn_dsa), hgrn_lb=f(hgrn_lb),
                  hgrn_onorm=f(hgrn_onorm), w_br_sb=f(w_br_sb), w_br_dsa=f(w_br_dsa), w_br_hgrn=f(w_br_hgrn),
                  w_out=f(w_out), norm_mlp=f(norm_mlp), w_up=f(w_up), w_down=f(w_down), rope_cos=cs, rope_sin=sn)
    xs = f(x)
    in_maps = [dict(shared, x=xs[b]) for b in range(8)]
    res = run_bass_kernel_spmd(nc, in_maps, core_ids=list(range(8)))
    return np.stack([np.asarray(r["out"], dtype=np.float32) for r in res.results], axis=0)
```

```python
import math
import numpy as np
from contextlib import ExitStack, contextmanager
import concourse.bass as bass
import concourse.mybir as mybir
from concourse.bass_utils import run_bass_kernel_spmd

F32 = mybir.dt.float32
BF16 = mybir.dt.bfloat16
AF = mybir.ActivationFunctionType
ALU = mybir.AluOpType
AX = mybir.AxisListType

T = 2048
D = 1024
NB = 16
DC = 8
DIN = 6856
DFF = 4096
DEPTH = 2
EPS = 1e-6
F_MIN = 1e-12
IDX_SCALE = (64 * 8) ** -0.5
C_SQ, C_SK, C_SV = 0, 384, 768
C_DQ, C_DK, C_DV = 1152, 1536, 1600
C_IQ, C_IK, C_IW = 1664, 2176, 2240
C_HQ, C_HF, C_HI, C_HG = 2248, 2760, 3272, 3528
C_G = 3784
N_BISECT = 12


class Buf:
    __slots__ = ("name", "w", "r", "dsem", "excl")

    def __init__(self, name, excl=False):
        self.name = name
        self.w = None
        self.r = {}
        self.dsem = None
        self.excl = excl


class Prog:
    ENG = ("pe", "act", "dve", "pool", "sp")
    CLEAR_NS = 330.0
    FILL_NS = {"dve": 66.0, "act": 190.0, "pool": 125.0}
    EST = {"dve": (60.0, 0.26), "act": (185.0, 0.83), "pool": (120.0, 0.8), "pe": (0.0, 0.0), "sp": (0.0, 0.0)}

    def __init__(self, nc, stack, needed=None):
        self.needed = needed
        self.used = set()
        self.remap = {}
        self.sig = {}
        self.fill = {}
        self.nc = nc
        self.stack = stack
        self.eng = {"pe": nc.tensor, "act": nc.scalar, "dve": nc.vector, "pool": nc.gpsimd, "sp": nc.sync}
        self.cnt = {e: 0 for e in self.ENG}
        self.known = {e: {} for e in self.ENG}
        self.sems = {}
        self.semval = {}
        for e in ("pe", "act", "dve", "pool"):
            self.sems["E_" + e] = stack.enter_context(nc.semaphore("sem_" + e))
            self.semval["E_" + e] = 0
        self.ndsem = 0
        self.free_dsems = []
        self.nwaits = 0
        self.tcum = {e: 0.0 for e in self.ENG}
        self.tend = {e: {} for e in self.ENG}

    def _dsem(self, buf):
        if buf.dsem is None:
            if self.free_dsems:
                key = self.free_dsems.pop()
            else:
                key = "D%d" % self.ndsem
                self.ndsem += 1
                self.sems[key] = self.stack.enter_context(self.nc.semaphore("dsem%d" % (self.ndsem - 1)))
                self.semval[key] = 0
            buf.dsem = key
        return buf.dsem

    def release(self, bufs):
        for b in bufs:
            if b.dsem is not None:
                self.free_dsems.append(b.dsem)
                b.dsem = None

    def _waits(self, eng, deps):
        need = {}
        own = "E_" + eng
        for (k, v) in deps:
            if eng == "pe" and k == "E_pe":
                continue
            if k == own and eng in ("act", "dve", "pool"):
                te = self.tend[eng].get(v)
                if te is not None and eng in self.fill:
                    gap = self.CLEAR_NS - (self.tcum[eng] - te)
                    if gap > 0:
                        n = int(math.ceil(gap / self.FILL_NS[eng]))
                        for _ in range(n):
                            self.fill[eng](self.eng[eng])
                        self.tcum[eng] += n * self.FILL_NS[eng]
                        self.nfill = getattr(self, "nfill", 0) + n
                continue
            if v > need.get(k, 0):
                need[k] = v
        out = []
        kn = self.known[eng]
        for k, v in need.items():
            if kn.get(k, 0) < v:
                kn[k] = v
                out.append((k, v))
        return out

    @staticmethod
    def _deps(reads, writes):
        deps = []
        for b in reads:
            if b.w is not None:
                deps.append(b.w)
            if b.excl:
                deps.extend(b.r.items())
        for b in writes:
            if b.w is not None:
                deps.append(b.w)
            deps.extend(b.r.items())
        return deps

    def _emit_waits(self, eng, waits):
        e = self.eng[eng]
        for (k, v) in waits:
            if k.startswith("E_"):
                self.used.add((k, v))
                if self.needed is not None:
                    v = self.remap[(k, v)]
            e.wait_ge(self.sems[k], v)
            self.nwaits += 1

    def _mark(self, ev, reads, writes):
        k, v = ev
        for b in reads:
            if b.r.get(k, 0) < v:
                b.r[k] = v
        for b in writes:
            b.w = ev
            b.r = {}

    def op(self, eng, fn, reads=(), writes=(), n=0):
        self.group(eng, [fn], reads, writes, n)

    def group(self, eng, fns, reads=(), writes=(), n=0):
        self._emit_waits(eng, self._waits(eng, self._deps(reads, writes)))
        e = self.eng[eng]
        for fn in fns[:-1]:
            fn(e)
        self.cnt[eng] += 1
        ov, pe_ = self.EST[eng]
        self.tcum[eng] += ov + pe_ * n
        td = self.tend[eng]
        td[self.cnt[eng]] = self.tcum[eng]
        if len(td) > 64:
            for k_ in sorted(td)[:32]:
                del td[k_]
        key = "E_" + eng
        self.semval[key] = self.cnt[eng]
        if self.needed is None or (key, self.cnt[eng]) in self.needed:
            self.sig[key] = self.sig.get(key, 0) + 1
            self.remap[(key, self.cnt[eng])] = self.sig[key]
            fns[-1](e).then_inc(self.sems[key], 1)
        else:
            fns[-1](e)
        self._mark((key, self.cnt[eng]), reads, writes)

    def dma(self, eng, fn, reads=(), writes=()):
        assert len(writes) == 1
        wb = writes[0]
        deps = self._deps(reads, writes)
        if eng == "pool" and getattr(self, "prev_swdge", None) is not None:
            deps.append(self.prev_swdge)
        self._emit_waits(eng, self._waits(eng, deps))
        key = self._dsem(wb)
        self.semval[key] += 16
        fn(self.eng[eng]).then_inc(self.sems[key], 16)
        if eng == "pool":
            self.prev_swdge = getattr(self, "last_swdge", None)
            self.last_swdge = (key, self.semval[key])
        self._mark((key, self.semval[key]), reads, writes)

    def barrier(self):
        deps = [(k, v) for k, v in self.semval.items() if v > 0]
        for eng in self.ENG:
            self._emit_waits(eng, self._waits(eng, deps))

    def wait_bufs(self, eng, bufs):
        deps = []
        for b in bufs:
            if b.w is not None:
                deps.append(b.w)
            deps.extend(b.r.items())
        self._emit_waits(eng, self._waits(eng, deps))


class G:
    pass


def bufs(prefix, *dims):
    if len(dims) == 1:
        return [Buf("%s%d" % (prefix, i)) for i in range(dims[0])]
    return [bufs("%s%d_" % (prefix, i), *dims[1:]) for i in range(dims[0])]


def build_program(stage=99, dump=None, needed=None):
    nc = bass.Bass("TRN2", target_bir_lowering=False)
    g = G()
    g.nc = nc
    g.stage = stage
    g.dump = dump
    import os
    g.ntl = int(os.environ.get("NTL", "9"))
    g.dbg_d = None
    if dump is not None:
        g.dbg_d = nc.dram_tensor("dbg", [128, 8 * T], BF16, kind="ExternalOutput").ap()
        g.dbgb = Buf("dbg")
    dt = lambda name, shape, kind, d=F32: nc.dram_tensor(name, shape, d, kind=kind).ap()
    g.x_d = dt("x", [T, D], "ExternalInput")
    g.norm_mix = dt("norm_mix", [DEPTH, D], "ExternalInput")
    g.w_in = dt("w_in", [DEPTH, D, DIN], "ExternalInput")
    g.qn = dt("qn_dsa", [DEPTH, 64], "ExternalInput")
    g.kn = dt("kn_dsa", [DEPTH, 64], "ExternalInput")
    g.lb_d = dt("hgrn_lb", [DEPTH, 512], "ExternalInput")
    g.onorm = dt("hgrn_onorm", [DEPTH, 64], "ExternalInput")
    g.w_sb = dt("w_br_sb", [DEPTH, 384, D], "ExternalInput")
    g.w_dsa = dt("w_br_dsa", [DEPTH, 384, D], "ExternalInput")
    g.w_hg = dt("w_br_hgrn", [DEPTH, 256, D], "ExternalInput")
    g.w_out = dt("w_out", [DEPTH, D, D], "ExternalInput")
    g.norm_mlp = dt("norm_mlp", [DEPTH, D], "ExternalInput")
    g.w_up = dt("w_up", [DEPTH, D, DFF], "ExternalInput")
    g.w_down = dt("w_down", [DEPTH, DFF, D], "ExternalInput")
    g.cs_d = dt("rope_cos", [T, 8], "ExternalInput")
    g.sn_d = dt("rope_sin", [T, 8], "ExternalInput")
    g.out_d = dt("out", [T, D], "ExternalOutput")
    g.xres_d = dt("xres", [T, D], "Internal")
    g.xin_b = bufs("xin", NB)
    g.xres_b = bufs("xres", NB)
    g.out_b = bufs("outb", NB)

    with ExitStack() as gs:
        P = Prog(nc, gs, needed)
        g.P = P
        fa = gs.enter_context(nc.sbuf_tensor("fill_a", [128, 2], F32))
        fd = gs.enter_context(nc.sbuf_tensor("fill_d", [128, 2], F32))
        nc.vector.memset(fd[:], 0.0)
        nc.vector.memset(fa[:], 0.0)
        P.fill["dve"] = lambda e: e.memset(fd[:, 0:1], 0.0)
        P.fill["act"] = lambda e: e.activation(out=fa[:, 0:1], in_=fa[:, 1:2], func=AF.Copy)
        fp = gs.enter_context(nc.sbuf_tensor("fill_p", [128, 2], F32))
        nc.gpsimd.memset(fp[:], 0.0)
        P.fill["pool"] = lambda e: e.memset(fp[:, 0:1], 0.0)
        g.uid = 0
        g.psF = [gs.enter_context(nc.psum_tensor("psF%d" % i, [128, 512], F32)) for i in range(6)]
        g.psFb = [Buf("psF%d" % i, excl=True) for i in range(6)]
        g.psB = [gs.enter_context(nc.psum_tensor("psB%d" % i, [128, 1024], BF16)) for i in range(2)]
        g.psBb = [Buf("psB%d" % i, excl=True) for i in range(2)]
        g.rotc = {}
        build_consts(g, gs)
        g.hT = gs.enter_context(nc.sbuf_tensor("hT_glob", [128, DC, T], BF16))
        g.hTb = bufs("hT", NB)
        g.pre = gs.enter_context(nc.sbuf_tensor("pre_w", [128, DC, 512], BF16))
        g.preb = Buf("pre_w")
        prefetch(g, ("sq", 0), win_cols(g, 0, C_SQ, C_SQ + 384), 384)
        for l in range(DEPTH):
            if g.stage >= 1:
                build_layer(g, l)
        if g.dbg_d is not None:
            P.wait_bufs("sp", [g.dbgb])
        P.wait_bufs("sp", g.out_b)
        P.barrier()
        g.used = P.used
        print("ops", P.cnt, "signals", P.sig, "waits", P.nwaits, "fillers", getattr(P, "nfill", 0), "dsems", P.ndsem, flush=True)
    return nc, P.used


def build_two_pass(stage=99, dump=None):
    _, used = build_program(stage, dump, None)
    nc, _ = build_program(stage, dump, used)
    return nc


def pipeline(stages, ntiles):
    ns = len(stages)
    for t in range(ntiles + ns - 1):
        for k, f in enumerate(stages):
            i = t - k
            if 0 <= i < ntiles:
                f(i)


def prefetch(g, tag, src, ncols):
    DMA(g, "pool", g.pre[:, :, 0:ncols], src, (), [g.preb])
    g.pre_tag = tag


def take_pre(g, tag):
    if getattr(g, "pre_tag", None) == tag:
        g.pre_tag = None
        return True
    return False


def rot(g, role, items):
    i = g.rotc.get(role, 0)
    g.rotc[role] = i + 1
    return items[i % len(items)]


def psf(g, role, banks):
    b = rot(g, role, banks)
    return g.psF[b], g.psFb[b]


def psb(g, role="pb"):
    b = rot(g, role, [0, 1])
    return g.psB[b], g.psBb[b]


@contextmanager
def scope(g):
    st = ExitStack()
    st.tbufs = []
    try:
        yield st
    finally:
        g.P.barrier()
        g.P.release(st.tbufs)
        st.close()


def sb(g, st, shape, dtype, name=None):
    g.uid += 1
    return st.enter_context(g.nc.sbuf_tensor("%s_%d" % (name or "t", g.uid), shape, dtype))


def nb(st, name):
    b = Buf(name)
    st.tbufs.append(b)
    return b


def nbs(st, prefix, *dims):
    r = bufs(prefix, *dims)

    def flat(x):
        if isinstance(x, Buf):
            st.tbufs.append(x)
        else:
            for y in x:
                flat(y)
    flat(r)
    return r


def _fs(ap):
    try:
        return int(ap.free_size())
    except Exception:
        return 0


def ACT(g, out, in_, func, reads, writes, **kw):
    g.P.op("act", lambda e: e.activation(out=out, in_=in_, func=func, **kw), reads, writes, _fs(out))


def TT(g, eng, out, in0, in1, op, reads, writes):
    g.P.op(eng, lambda e: e.tensor_tensor(out=out, in0=in0, in1=in1, op=op), reads, writes, _fs(out))


def TS(g, eng, out, in0, s1, s2, op0, op1, reads, writes, **kw):
    if op1 is None:
        s2 = 0.0 if isinstance(s1, (int, float)) else g.zc[0:in0.shape[0], 0:1]
        g.P.op(eng, lambda e: e.tensor_scalar(out=out, in0=in0, scalar1=s1, scalar2=s2, op0=op0, op1=ALU.add, **kw), reads, writes, _fs(out))
    else:
        g.P.op(eng, lambda e: e.tensor_scalar(out=out, in0=in0, scalar1=s1, scalar2=s2, op0=op0, op1=op1, **kw), reads, writes, _fs(out))


def STT(g, out, in0, scalar, in1, op0, op1, reads, writes):
    g.P.op("dve", lambda e: e.scalar_tensor_tensor(out=out, in0=in0, scalar=scalar, in1=in1, op0=op0, op1=op1), reads, writes, _fs(out))


def CP(g, eng, out, in_, reads, writes):
    if eng == "act":
        g.P.op("act", lambda e: e.activation(out=out, in_=in_, func=AF.Copy), reads, writes, _fs(out))
    else:
        g.P.op(eng, lambda e: e.tensor_copy(out, in_), reads, writes, _fs(out))


def MS(g, eng, ap, val, writes):
    g.P.op(eng, lambda e: e.memset(ap, val), (), writes, _fs(ap))


def ASEL(g, out, in_, pattern, cmp, fill, base, cm, reads, writes):
    g.P.op("pool", lambda e: e.affine_select(out=out, in_=in_, pattern=pattern, compare_op=cmp, fill=fill, base=base,
                                             channel_multiplier=cm), reads, writes, _fs(out))


def MM(g, outs_fns, reads, writes):
    g.P.group("pe", outs_fns, reads, writes)


def mmf(out, lhsT, rhs, start, stop):
    return lambda e: e.matmul(out, lhsT=lhsT, rhs=rhs, start=start, stop=stop)


def trf(out, in_, ident):
    return lambda e: e.transpose(out, in_, ident)


def DMA(g, eng, out, in_, reads, writes, **kw):
    g.P.dma(eng, lambda e: e.dma_start(out=out, in_=in_, **kw), reads, writes)


def build_consts(g, gs):
    nc = g.nc
    mk = lambda name, shape, d: gs.enter_context(nc.sbuf_tensor(name, shape, d))
    g.ident = mk("ident", [128, 128], BF16)
    g.negtri = mk("negtri", [128, 128], BF16)
    g.negones = mk("negones", [128, 128], BF16)
    g.onesb = mk("onesb", [128, 128], BF16)
    g.maskbd = mk("maskbd", [128, 128], F32)
    g.onesf = mk("onesf", [128, 128], F32)
    g.resetm = mk("resetm", [128, T], BF16)
    g.zc = mk("zc", [128, 1], F32)
    g.negbig = mk("negbig", [128, 1], F32)
    g.cs = mk("cs", [128, NB, 8], F32)
    g.sn = mk("sn", [128, NB, 8], F32)
    g.lbraw = mk("lbraw", [128, 2, 4], F32)
    g.lbv = mk("lbv", [128, 2, 4], F32)
    g.oml = mk("oml", [128, 2, 4], F32)
    g.cb = Buf("consts")
    g.csb = Buf("cs")
    g.snb = Buf("sn")
    g.lbb = Buf("lbraw")
    cb = [g.cb]
    MS(g, "pool", g.onesb[:], 1.0, cb)
    MS(g, "pool", g.negones[:], -1.0, cb)
    MS(g, "pool", g.onesf[:], 1.0, cb)
    MS(g, "pool", g.zc[:], 0.0, cb)
    MS(g, "pool", g.negbig[:], -1e29, cb)
    ASEL(g, g.ident[:], g.onesb[:], [[1, 128]], ALU.is_equal, 0.0, 0, -1, cb, cb)
    ASEL(g, g.negtri[:], g.negones[:], [[-1, 128]], ALU.is_ge, 0.0, 0, 1, cb, cb)
    ASEL(g, g.maskbd[:], g.onesf[:], [[1, 128]], ALU.is_ge, 0.0, 0, -1, cb, cb)
    MS(g, "pool", g.maskbd[0:64, 64:128], 0.0, cb)
    g.ones512 = mk("ones512", [128, 512], BF16)
    g.mlt = mk("mlt", [128, 512], BF16)
    MS(g, "pool", g.ones512[:], 1.0, cb)
    ASEL(g, g.mlt[:], g.ones512[:], [[1, 512]], ALU.is_gt, 0.0, 0, -1, cb, cb)
    g.caus01 = mk("caus01", [128, 128], F32)
    g.negfill = mk("negfill", [128, 128], F32)
    ASEL(g, g.caus01[:], g.onesf[:], [[-1, 128]], ALU.is_ge, 0.0, 0, 1, cb, cb)
    TS(g, "pool", g.negfill[:], g.caus01[:], -1.0, 1e30, ALU.add, ALU.mult, cb, cb)
    MS(g, "pool", g.resetm[:], 1.0, cb)
    MS(g, "pool", g.resetm[:].rearrange("p (c j) -> p c j", j=64)[:, :, 0:1], 0.0, cb)
    DMA(g, "sp", g.cs[:], g.cs_d.rearrange("(b p) i -> p b i", p=128), (), [g.csb])
    DMA(g, "sp", g.sn[:], g.sn_d.rearrange("(b p) i -> p b i", p=128), (), [g.snb])
    DMA(g, "sp", g.lbraw[:], g.lb_d.rearrange("l (h k) -> k l h", k=128), (), [g.lbb], allow_slow_non_contiguous=True)
    MS(g, "dve", g.lbv[:], 0.0, cb)
    TT(g, "dve", g.lbv[:, 1, :], g.lbraw[:, 1, :], g.lbraw[:, 0, :], ALU.subtract, [g.lbb], cb)
    ACT(g, g.lbv[:, 1, :], g.lbv[:, 1, :], AF.Sigmoid, cb, cb)
    TS(g, "dve", g.oml[:], g.lbv[:], -1.0, 1.0, ALU.mult, ALU.add, cb, cb)
    g.P.barrier()


class NormCtx:
    pass


def norm_setup(g, st, gain_d_row):
    c = NormCtx()
    c.gain = sb(g, st, [128, D], F32, "gain")
    c.gb = nb(st, "gain")
    DMA(g, "sp", c.gain[:], gain_d_row.to_broadcast([128, D]), (), [c.gb])
    c.junk = sb(g, st, [128, D], BF16, "junk")
    c.jb = nb(st, "junk")
    c.hbs = [sb(g, st, [128, D], BF16, "hb") for _ in range(2)]
    c.hbb = nbs(st, "hb", 2)
    c.ss = sb(g, st, [128, NB], F32, "ss")
    c.ssb = nbs(st, "ss", NB)
    MS(g, "dve", c.ss[:], 0.0, c.ssb)
    return c


def norm_block(g, c, xb, xbuf, tb, hT, hTb):
    hb, hbuf = c.hbs[tb % 2], c.hbb[tb % 2]
    s1 = c.ss[:, tb:tb + 1]
    ACT(g, c.junk[:], xb[:], AF.Square, [xbuf, c.ssb[tb]], [c.jb, c.ssb[tb]], accum_out=s1)
    ACT(g, s1, s1, AF.Ln, [c.ssb[tb]], [c.ssb[tb]], scale=1.0 / D, bias=EPS)
    ACT(g, s1, s1, AF.Exp, [c.ssb[tb]], [c.ssb[tb]], scale=-0.5)
    STT(g, hb[:], xb[:], s1, c.gain[:], ALU.mult, ALU.mult, [xbuf, c.ssb[tb], c.gb], [hbuf])
    for half in range(2):
        pt, ptb = psb(g)
        MM(g, [trf(pt[:, m * 128:(m + 1) * 128], hb[:, (half * 4 + m) * 128:(half * 4 + m + 1) * 128], g.ident[:])
               for m in range(4)], [hbuf, g.cb], [ptb])
        CP(g, "act" if half == 0 else "dve", hT[:, half * 4:half * 4 + 4, tb * 128:(tb + 1) * 128],
           pt[:, 0:512].rearrange("p (m j) -> p m j", j=128), [ptb], [hTb[tb]])


def norm_T(g, st, src_ap, src_bufs, gain_d_row, hT, hTb):
    c = norm_setup(g, st, gain_d_row)
    xbs = [sb(g, st, [128, D], F32, "xb") for _ in range(4)]
    xbb = nbs(st, "xb", 4)
    for tb in range(NB):
        xb, xbuf = xbs[tb % 4], xbb[tb % 4]
        DMA(g, "sp", xb[:], src_ap[tb * 128:(tb + 1) * 128, :], [src_bufs[tb]], [xbuf])
        norm_block(g, c, xb, xbuf, tb, hT, hTb)


def win_cols(g, l, c0, c1):
    return g.w_in[l].rearrange("(c p) n -> p c n", p=128)[:, :, c0:c1]


def mm_fm(g, ps, M, n, w, wb, col0, hT, hTb, tok0, role_bufs):
    MM(g, [mmf(ps[0:M, 0:n], w[:, c, col0:col0 + M], hT[:, c, tok0:tok0 + n], c == 0, c == DC - 1) for c in range(DC)],
       [wb] + hTb[tok0 // 128:(tok0 + n + 127) // 128], [role_bufs])


def mm_tm(g, ps, N, w, wb, col0, hT, hTb, tb, psbuf):
    MM(g, [mmf(ps[:, 0:N], hT[:, c, tb * 128:(tb + 1) * 128], w[:, c, col0:col0 + N], c == 0, c == DC - 1) for c in range(DC)],
       [wb, hTb[tb]], [psbuf])


def phase_sb(g, l, hT, hTb, osbT, osbb):
    with scope(g) as st:
        ws = []
        for i, c0 in enumerate((C_SQ, C_SK, C_SV)):
            if i == 0 and take_pre(g, ("sq", l)):
                ws.append((g.pre, g.preb))
                continue
            w = sb(g, st, [128, DC, 384], BF16, "wsb")
            wb = nb(st, "wsb%d" % i)
            DMA(g, "pool", w[:], win_cols(g, l, c0, c0 + 384), (), [wb])
            ws.append((w, wb))
        sqT = sb(g, st, [128, 3, T], BF16, "sqT")
        skT = sb(g, st, [128, 3, T], BF16, "skT")
        sqb = nbs(st, "sq", 3, 4)
        skb = nbs(st, "sk", 3, 4)
        k = 0
        for (dst, dstb, (w, wb), scl) in ((sqT, sqb, ws[0], 0.125), (skT, skb, ws[1], 1.0)):
            for hp in range(3):
                for tc in range(4):
                    ps, pb = psf(g, "proj", [0, 1, 2, 3, 4, 5])
                    mm_fm(g, ps, 128, 512, w, wb, hp * 128, hT, hTb, tc * 512, pb)
                    o = dst[:, hp, tc * 512:(tc + 1) * 512]
                    if k % 2 == 0:
                        ACT(g, o, ps[:, :], AF.Copy, [pb], [dstb[hp][tc]], scale=scl)
                    else:
                        TS(g, "dve", o, ps[:, :], scl, None, ALU.mult, None, [pb], [dstb[hp][tc]])
                    k += 1
        svp = [sb(g, st, [128, NB, 384], BF16, "svp") for _ in range(2)]
        svb = nbs(st, "sv", 2, NB)
        for s_ in range(2):
            MS(g, "pool", svp[s_][:].rearrange("p t c -> p (t c)"), 0.0, svb[s_])
        for tb in range(NB):
            ps, pb = psf(g, "proj", [0, 1, 2, 3, 4, 5])
            mm_tm(g, ps, 384, ws[2][0], ws[2][1], 0, hT, hTb, tb, pb)
            src = ps[:, 0:384].rearrange("p (m s d) -> p m s d", s=2, d=64)
            for s_ in range(2):
                dst = svp[s_][:, tb, :].rearrange("p (m s d) -> p m s d", s=2, d=64)
                CP(g, "act" if s_ == 0 else "dve", dst[:, :, s_, :], src[:, :, s_, :], [pb], [svb[s_][tb]])
        prefetch(g, ("wA", l), win_cols(g, l, C_DQ, C_DQ + 512), 512)
        R = 4
        mk2 = lambda shape, dt_, nm: [[sb(g, st, shape, dt_, nm) for _ in range(R)] for _ in range(2)]
        Et, SPt, SPs, At = mk2([128, 512], F32, "Et"), mk2([128, 512], BF16, "SPt"), mk2([128, 512], BF16, "SPs"), mk2([128, 512], BF16, "At")
        Etb, SPb, SPsb, Atb = nbs(st, "Et", 2, R), nbs(st, "SPt", 2, R), nbs(st, "SPs", 2, R), nbs(st, "At", 2, R)
        steps = []
        for m in range(3):
            for qc in range(4):
                for n_, kb in enumerate(range(4 * qc + 3, -1, -1)):
                    steps.append((m, qc, kb, n_))
        stt = {}
        pso = {}

        def info(t):
            m, qc, kb, n_ = steps[t]
            j0 = max(0, kb * 128 - qc * 512)
            return m, qc, kb, n_, j0, kb >= 4 * qc, qc * 512 + j0 - kb * 128, n_ == 0

        def opnds(t):
            m, qc, kb, n_, j0, diag, base, first = info(t)
            kk = [skT[64 * s_:64 * s_ + 64, m, kb * 128:(kb + 1) * 128] for s_ in range(2)]
            qq = [sqT[64 * s_:64 * s_ + 64, m, qc * 512 + j0:(qc + 1) * 512] for s_ in range(2)]
            return kk, qq, skb[m][kb // 4], sqb[m][qc]

        def s1(t):
            m, qc, kb, n_, j0, diag, base, first = info(t)
            kk, qq, rk, rq = opnds(t)
            pz = [psf(g, "sbZ", [0, 1, 2]) for _ in range(2)]
            stt[t] = {"pz": pz}
            for s_ in range(2):
                MM(g, [mmf(pz[s_][0][:, j0:512], kk[s_], qq[s_], True, True)], [rk, rq], [pz[s_][1]])

        def s2(t):
            m, qc, kb, n_, j0, diag, base, first = info(t)
            ib = t % R
            pz = stt[t]["pz"]
            for s_ in range(2):
                ACT(g, Et[s_][ib][:, j0:512], pz[s_][0][:, j0:512], AF.Exp, [pz[s_][1]], [Etb[s_][ib]])

        def s3(t):
            m, qc, kb, n_, j0, diag, base, first = info(t)
            ib = t % R
            for s_ in range(2):
                ACT(g, SPt[s_][ib][:, j0:512], Et[s_][ib][:, j0:512], AF.Ln, [Etb[s_][ib]], [SPb[s_][ib]], bias=1.0)
            if diag:
                for s_ in range(2):
                    S = SPt[s_][ib]
                    assert base == 0
                    TT(g, "pool", S[:, j0:512], S[:, j0:512], g.mlt[:, 0:512 - j0], ALU.mult, [SPb[s_][ib], g.cb], [SPb[s_][ib]])
            if kb > 0:
                for s_ in range(2):
                    S, Sb = SPt[s_][ib], SPb[s_][ib]
                    Sn, Snb = SPs[s_][n_ % R], SPsb[s_][n_ % R]
                    if first:
                        if j0 > 0:
                            MS(g, "pool", Sn[:, 0:j0], 0.0, [Snb])
                        CP(g, "pool", Sn[:, j0:512], S[:, j0:512], [Sb], [Snb])
                    else:
                        So, Sob = SPs[s_][(n_ - 1) % R], SPsb[s_][(n_ - 1) % R]
                        if j0 > 0:
                            CP(g, "pool", Sn[:, 0:j0], So[:, 0:j0], [Sob], [Snb])
                        TT(g, "dve", Sn[:, j0:512], So[:, j0:512], S[:, j0:512], ALU.add, [Sob, Sb], [Snb])

        def s4(t):
            m, qc, kb, n_, j0, diag, base, first = info(t)
            ib = t % R
            kk, qq, rk, rq = opnds(t)
            pc = [psf(g, "sbC", [3, 4]) for _ in range(2)]
            stt[t]["pc"] = pc
            for s_ in range(2):
                S = SPt[s_][ib]
                fns = [mmf(pc[s_][0][:, j0:512], kk[s_], qq[s_], True, False),
                       mmf(pc[s_][0][:, j0:512], g.negtri[:], S[:, j0:512], False, first)]
                rd = [rk, rq, SPb[s_][ib], g.cb]
                if not first:
                    So, Sob = SPs[s_][(n_ - 1) % R], SPsb[s_][(n_ - 1) % R]
                    fns.append(mmf(pc[s_][0][:, j0:512], g.negones[:], So[:, j0:512], False, True))
                    rd.append(Sob)
                MM(g, fns, rd, [pc[s_][1]])

        def s5(t):
            m, qc, kb, n_, j0, diag, base, first = info(t)
            ib = t % R
            pc = stt[t]["pc"]
            for s_ in range(2):
                ACT(g, At[s_][ib][:, j0:512], pc[s_][0][:, j0:512], AF.Exp, [pc[s_][1]], [Atb[s_][ib]])
            for s_ in range(2):
                A, Ab = At[s_][ib], Atb[s_][ib]
                if diag:
                    TT(g, "pool", A[:, j0:512], A[:, j0:512], g.mlt[:, 0:512 - j0], ALU.mult, [Ab, g.cb], [Ab])
                if first and j0 > 0:
                    MS(g, "pool", A[:, 0:j0], 0.0, [Ab])

        def s6(t):
            m, qc, kb, n_, j0, diag, base, first = info(t)
            ib = t % R
            if first:
                pso[(m, qc)] = psf(g, "sbO", [5])
            psO, pOb = pso[(m, qc)]
            for s_ in range(2):
                A, Ab = At[s_][ib], Atb[s_][ib]
                vv = svp[s_][:, kb, m * 128:(m + 1) * 128]
                c0 = 0 if first else j0
                MM(g, [mmf(psO[:, c0:512], vv, A[:, c0:512], first and s_ == 0, kb == 0 and s_ == 1)], [svb[s_][kb], Ab], [pOb])
            if kb == 0:
                CP(g, "act" if (m * 4 + qc) % 2 == 0 else "dve", osbT[:, m, qc * 512:(qc + 1) * 512], psO[:, :], [pOb], [osbb[m][qc]])
            del stt[t]

        nst = len(steps)
        stages = [s1, s2, s3, s4, s5, s6]
        for e in range(nst + len(stages) - 1):
            for k in range(len(stages) - 1, -1, -1):
                t = e - k
                if 0 <= t < nst:
                    stages[k](t)


def phase_dsa(g, l, hT, hTb, odT, odb):
    P = g.P
    with scope(g) as st:
        featT = sb(g, st, [128, 9, T], BF16, "featT")
        fb = nbs(st, "feat", NB)
        dvx = sb(g, st, [128, NB, 128], BF16, "dvx")
        dvb = nbs(st, "dvx", NB)
        sgn = sb(g, st, [128, NB, 8], F32, "sgn")
        sgb = nbs(st, "sgn", NB)
        qkg = sb(g, st, [128, 7, 64], F32, "qkg")
        qkgb = nbs(st, "qkg", 7)
        for hh in range(7):
            src = (g.qn if hh < 6 else g.kn)[l:l + 1, :].to_broadcast([128, 64])
            DMA(g, "sp", qkg[:, hh, :], src, (), [qkgb[hh]])
        with scope(g) as s2:
            wB = sb(g, s2, [128, DC, 512], BF16, "wB")
            wC = sb(g, s2, [128, DC, 72], BF16, "wC")
            wBb, wCb = nb(s2, "wB"), nb(s2, "wC")
            if take_pre(g, ("wA", l)):
                wA, wAb = g.pre, g.preb
            else:
                wA = sb(g, s2, [128, DC, 512], BF16, "wA")
                wAb = nb(s2, "wA")
                DMA(g, "pool", wA[:], win_cols(g, l, C_DQ, C_DQ + 512), (), [wAb])
            DMA(g, "pool", wB[:], win_cols(g, l, C_IQ, C_IQ + 512), (), [wBb])
            DMA(g, "pool", wC[:], win_cols(g, l, C_IK, C_IK + 72), (), [wCb])
            NR = 4
            tq = [sb(g, s2, [128, 18, 64], F32, "tokq") for _ in range(NR)]
            tqb = nbs(s2, "tokq", NR)
            tbq = [sb(g, s2, [128, 18, 64], BF16, "tokb") for _ in range(3)]
            tbb = nbs(s2, "tokb", 3)
            sqt = [sb(g, s2, [128, 448], F32, "sqt") for _ in range(2)]
            sqtb = nbs(s2, "sqt", 2)
            smq = [sb(g, s2, [128, 32], F32, "small") for _ in range(2)]
            smqb = nbs(s2, "small", 2)
            rt = [sb(g, s2, [128, 18, 8], F32, "ropet") for _ in range(4)]
            rtb = nbs(s2, "ropet", 4)
            pst = {}

            def p1(tb):
                MS(g, "pool", dvx[:, tb, 64:128], 1.0, [dvb[tb]])
                tk, tkb = tq[tb % NR], tqb[tb % NR]
                MS(g, "pool", tk[:, 7, :], 0.0, [tkb])
                MS(g, "pool", tk[:, 17, :], 0.0, [tkb])
                psA, pAb = psf(g, "dA", [0, 1])
                psBq, pBb = psf(g, "dB", [2, 3])
                psC, pCb = psf(g, "dC", [4, 5])
                pst[tb] = (psA, pAb, psBq, pBb, psC, pCb)
                mm_tm(g, psA, 512, wA, wAb, 0, hT, hTb, tb, pAb)
                mm_tm(g, psBq, 512, wB, wBb, 0, hT, hTb, tb, pBb)
                mm_tm(g, psC, 72, wC, wCb, 0, hT, hTb, tb, pCb)

            def p2(tb):
                psA, pAb, psBq, pBb, psC, pCb = pst.pop(tb)
                tk, tkb = tq[tb % NR], tqb[tb % NR]
                sq_, sq_b = sqt[tb % 2], sqtb[tb % 2]
                sm, smb = smq[tb % 2], smqb[tb % 2]
                ACT(g, sq_[:], psA[:, 0:448], AF.Square, [pAb], [sq_b])
                ss = sm[:, 0:7]
                P.op("dve", lambda e, ss=ss, sq_=sq_: e.tensor_reduce(out=ss, in_=sq_[:].rearrange("p (h d) -> p h d", d=64), axis=AX.X,
                                                                      op=ALU.add), [sq_b], [smb], 448)
                ACT(g, ss, ss, AF.Ln, [smb], [smb], scale=1.0 / 64, bias=EPS)
                ACT(g, ss, ss, AF.Exp, [smb], [smb], scale=-0.5)
                aw = sm[:, 8:16]
                TS(g, "dve", sgn[:, tb, :], psC[:, 64:72], 0.0, 2.0, ALU.is_gt, ALU.mult, [pCb], [sgb[tb]])
                TS(g, "dve", sgn[:, tb, :], sgn[:, tb, :], -1.0, 0.0, ALU.add, ALU.add, [sgb[tb]], [sgb[tb]])
                STT(g, aw, psC[:, 64:72], IDX_SCALE, sgn[:, tb, :], ALU.mult, ALU.mult, [pCb, sgb[tb]], [smb])
                TT(g, "dve", tk[:, 8:16, :], psBq[:, :].rearrange("p (h d) -> p h d", d=64),
                   aw.unsqueeze(2).to_broadcast([128, 8, 64]), ALU.mult, [pBb, smb], [tkb])
                CP(g, "act", tk[:, 16, :], psC[:, 0:64], [pCb], [tkb])
                CP(g, "act", dvx[:, tb, 0:64], psA[:, 448:512], [pAb], [dvb[tb]])
                TT(g, "dve", tk[:, 0:7, :], psA[:, 0:448].rearrange("p (h d) -> p h d", d=64),
                   ss.unsqueeze(2).to_broadcast([128, 7, 64]), ALU.mult, [pAb, smb], [tkb])
                TT(g, "dve", tk[:, 0:7, :], tk[:, 0:7, :], qkg[:], ALU.mult, [tkb] + qkgb, [tkb])

            def p3(tb):
                tk, tkb = tq[tb % NR], tqb[tb % NR]
                x1, x2 = tk[:, :, 0:8], tk[:, :, 8:16]
                cb_ = g.cs[:, tb, :].unsqueeze(1).to_broadcast([128, 18, 8])
                sb_ = g.sn[:, tb, :].unsqueeze(1).to_broadcast([128, 18, 8])
                TT(g, "dve", rt[0][:], x1, cb_, ALU.mult, [tkb, g.csb], [rtb[0]])
                TT(g, "pool", rt[1][:], x2, sb_, ALU.mult, [tkb, g.snb], [rtb[1]])
                TT(g, "dve", rt[2][:], x2, cb_, ALU.mult, [tkb, g.csb], [rtb[2]])
                TT(g, "pool", rt[3][:], x1, sb_, ALU.mult, [tkb, g.snb], [rtb[3]])
                TT(g, "dve", x1, rt[0][:], rt[1][:], ALU.subtract, [rtb[0], rtb[1]], [tkb])
                TT(g, "pool", x2, rt[2][:], rt[3][:], ALU.add, [rtb[2], rtb[3]], [tkb])

            def p4(tb):
                tk, tkb = tq[tb % NR], tqb[tb % NR]
                tkh, tkhb = tbq[tb % 3], tbb[tb % 3]
                CP(g, "act", tkh[:], tk[:], [tkb], [tkhb])
                CP(g, "pool", tkh[:, 7, :], tkh[:, 6, :], [tkhb], [tkhb])
                CP(g, "pool", tkh[:, 17, :], tkh[:, 16, :], [tkhb], [tkhb])

            def p5(tb):
                tkh, tkhb = tbq[tb % 3], tbb[tb % 3]
                flat = tkh[:].rearrange("p h d -> p (h d)")
                pt, ptb = psb(g)
                MM(g, [trf(pt[:, m * 128:(m + 1) * 128], flat[:, m * 128:(m + 1) * 128], g.ident[:]) for m in range(8)],
                   [tkhb, g.cb], [ptb])
                CP(g, "dve", featT[:, 0:8, tb * 128:(tb + 1) * 128], pt[:, :].rearrange("p (m j) -> p m j", j=128), [ptb], [fb[tb]])
                pt2, ptb2 = psb(g)
                MM(g, [trf(pt2[:, 0:128], flat[:, 1024:1152], g.ident[:])], [tkhb, g.cb], [ptb2])
                CP(g, "act", featT[:, 8, tb * 128:(tb + 1) * 128], pt2[:, 0:128], [ptb2], [fb[tb]])

            stages = [p1, p2, p3, p4, p5]
            for e in range(NB + len(stages) - 1):
                for k in range(len(stages) - 1, -1, -1):
                    t_ = e - k
                    if 0 <= t_ < NB:
                        stages[k](t_)
        prefetch(g, ("hihg", l), win_cols(g, l, C_HI, C_HI + 512), 512)
        sc = [sb(g, st, [128, T], F32, "sc") for _ in range(4)]
        scb = nbs(st, "sc", 4, 4)
        junk = sb(g, st, [128, T], BF16, "junk")
        maskq = [sb(g, st, [128, T], BF16, "maskq") for _ in range(2)]
        mqb = nbs(st, "maskq", 2)
        maskT = sb(g, st, [128, NB, 512], BF16, "maskT")
        mTb = nbs(st, "maskT", 4)
        rj = [sb(g, st, [128, 512], BF16, "rj") for _ in range(4)]
        rjb = nbs(st, "rj", 4)
        dg = [sb(g, st, [128, 8, 128], BF16, "dg") for _ in range(2)]
        dgb = nbs(st, "dg", 2)
        sm = [sb(g, st, [128, 8 + 2 * N_BISECT], F32, "bis") for _ in range(2)]
        smb = nbs(st, "bis", 2)
        cvec = sb(g, st, [128, N_BISECT], F32, "cvec")
        c255 = sb(g, st, [128, 1], F32, "c255")
        cvb = nb(st, "cvec")
        for n_ in range(N_BISECT):
            MS(g, "pool", cvec[:, n_:n_ + 1], 2.0 ** -(n_ + 1), [cvb])
        MS(g, "pool", c255[:], 255.5, [cvb])
        Pt = [sb(g, st, [128, 512], BF16, "Pt") for _ in range(4)]
        Ptb = nbs(st, "Pt", 4)
        Pm = [sb(g, st, [128, 512], BF16, "Pm") for _ in range(4)]
        Pmb = nbs(st, "Pm", 4)
        rs = sb(g, st, [64, 512], F32, "rs")
        rsb = nb(st, "rs")
        cnt_ = {"ri": 0, "pi": 0, "pm": 0}

        def idx_blocks(blocks):
            tiles = []
            for i in blocks:
                nk = (i + 1) * 128
                d_, d_b = dg[i % 2], dgb[i % 2]
                for j in range(8):
                    TS(g, "pool", d_[:, j, :], g.ident[:], sgn[:, i, j:j + 1], None, ALU.mult, None, [g.cb, sgb[i]], [d_b])
                for kc in range((nk + 511) // 512):
                    n = min(512, nk - kc * 512)
                    for j in range(8):
                        tiles.append((i, kc, n, j))
            stt = {}

            def s1(t):
                i, kc, n, j = tiles[t]
                po = 64 * (j % 2)
                psZ, pZb = psf(g, "ixZ", [0, 1, 2])
                stt[t] = [psZ, pZb]
                MM(g, [mmf(psZ[:, 0:n], featT[po:po + 64, 4 + j // 2, i * 128:(i + 1) * 128],
                           featT[po:po + 64, 8, kc * 512:kc * 512 + n], True, True)],
                   [fb[i]] + fb[kc * 4:(kc * 512 + n) // 128], [pZb])

            def s2(t):
                i, kc, n, j = tiles[t]
                psZ, pZb = stt[t]
                r_, r_b = rj[cnt_["ri"] % 4], rjb[cnt_["ri"] % 4]
                cnt_["ri"] += 1
                stt[t] += [r_, r_b]
                ACT(g, r_[:, 0:n], psZ[:, 0:n], AF.Relu, [pZb], [r_b])

            def s3(t):
                i, kc, n, j = tiles[t]
                r_, r_b = stt[t][2], stt[t][3]
                if j == 0:
                    cnt_["psS"] = psf(g, "ixS", [3, 4])
                psS, pSb = cnt_["psS"]
                d_, d_b = dg[i % 2], dgb[i % 2]
                MM(g, [mmf(psS[:, 0:n], d_[:, j, :], r_[:, 0:n], j == 0, j == 7)], [d_b, r_b], [pSb])
                if j == 7:
                    CP(g, "act", sc[i % 4][:, kc * 512:kc * 512 + n], psS[:, 0:n], [pSb], [scb[i % 4][kc]])
                del stt[t]

            pipeline([s1, s2, s3], len(tiles))

        def bis_pair(p):
            blocks = [2 * p, 2 * p + 1]
            st_ = []
            for bi, i in enumerate(blocks):
                nk = (i + 1) * 128
                s_, s_b = sc[i % 4], scb[i % 4]
                nkc = (nk + 511) // 512
                srd = s_b[0:nkc]
                m_, m_b = sm[bi], smb[bi]
                rmax, rmin, step0, mid, cntv, tt = (m_[:, c:c + 1] for c in range(6))
                stepc = m_[:, 8:8 + N_BISECT]
                if nk > 256:
                    P.op("dve", lambda e, s_=s_, nk=nk, rmax=rmax: e.tensor_reduce(out=rmax, in_=s_[:, 0:nk], axis=AX.X, op=ALU.max), srd, [m_b], nk)
                    P.op("dve", lambda e, s_=s_, nk=nk, rmin=rmin: e.tensor_reduce(out=rmin, in_=s_[:, 0:nk], axis=AX.X, op=ALU.min), srd, [m_b], nk)
                dsl = s_[:, i * 128:(i + 1) * 128]
                TT(g, "pool", dsl, dsl, g.caus01[:], ALU.mult, [s_b[i // 4], g.cb], [s_b[i // 4]])
                TT(g, "pool", dsl, dsl, g.negfill[:], ALU.add, [s_b[i // 4], g.cb], [s_b[i // 4]])
                st_.append((i, nk, s_, srd, m_, m_b, rmax, rmin, step0, mid, cntv, tt, stepc))
            act = [x for x in st_ if x[1] > 256]
            for (i, nk, s_, srd, m_, m_b, rmax, rmin, step0, mid, cntv, tt, stepc) in act:
                TT(g, "dve", step0, rmax, rmin, ALU.subtract, [m_b], [m_b])
            for (i, nk, s_, srd, m_, m_b, rmax, rmin, step0, mid, cntv, tt, stepc) in act:
                TS(g, "dve", stepc, cvec[:], step0, None, ALU.mult, None, [m_b, cvb], [m_b])
            for (i, nk, s_, srd, m_, m_b, rmax, rmin, step0, mid, cntv, tt, stepc) in act:
                TS(g, "dve", mid, stepc[:, 0:1], rmin, g.zc[:, 0:1], ALU.add, ALU.add, [m_b, g.cb], [m_b])
            for n_ in range(N_BISECT):
                for (i, nk, s_, srd, m_, m_b, rmax, rmin, step0, mid, cntv, tt, stepc) in act:
                    TS(g, "dve", junk[:, 0:nk], s_[:, 0:nk], mid, g.zc[:, 0:1], ALU.is_ge, ALU.add, srd + [m_b, g.cb], [m_b], accum_out=cntv)
                for (i, nk, s_, srd, m_, m_b, rmax, rmin, step0, mid, cntv, tt, stepc) in act:
                    TS(g, "dve", tt, cntv, c255[:, 0:1], stepc[:, n_:n_ + 1], ALU.is_ge, ALU.mult, [m_b, cvb], [m_b])
                for (i, nk, s_, srd, m_, m_b, rmax, rmin, step0, mid, cntv, tt, stepc) in act:
                    nn = min(n_ + 1, N_BISECT - 1)
                    TS(g, "dve", mid, tt, stepc[:, nn:nn + 1], mid, ALU.subtract, ALU.add, [m_b], [m_b])
            for (i, nk, s_, srd, m_, m_b, rmax, rmin, step0, mid, cntv, tt, stepc) in st_:
                thr = mid if nk > 256 else g.negbig[:, 0:1]
                mq, mq_b = maskq[i % 2], mqb[i % 2]
                TS(g, "dve", mq[:, 0:nk], s_[:, 0:nk], thr, None, ALU.is_ge, None, srd + [m_b, g.cb], [mq_b])

        def mT_pair(p):
            for i in (2 * p, 2 * p + 1):
                ii = i % 4
                mq, mq_b = maskq[i % 2], mqb[i % 2]
                for k0 in range(0, i + 1, 8):
                    k1 = min(i + 1, k0 + 8)
                    pt, ptb = psb(g)
                    MM(g, [trf(pt[:, (kb - k0) * 128:(kb - k0 + 1) * 128], mq[:, kb * 128:(kb + 1) * 128], g.ident[:])
                           for kb in range(k0, k1)], [mq_b, g.cb], [ptb])
                    CP(g, "act", maskT[:, k0:k1, ii * 128:(ii + 1) * 128],
                       pt[:, 0:(k1 - k0) * 128].rearrange("p (m j) -> p m j", j=128), [ptb], [mTb[ii]])

        def att_chunk(qc):
            last = 4 * qc + 3
            tiles = [(h, kb) for h in range(6) for kb in range(last + 1)]
            stt = {}
            pso = {}

            def s1(t):
                h, kb = tiles[t]
                hp, po = h // 2, 64 * (h % 2)
                j0 = max(0, kb * 128 - qc * 512)
                psL, pLb = psf(g, "dsL", [0, 1, 2])
                stt[t] = [psL, pLb]
                MM(g, [mmf(psL[:, j0:512], featT[po:po + 64, 3, kb * 128:(kb + 1) * 128],
                           featT[po:po + 64, hp, qc * 512 + j0:(qc + 1) * 512], True, True)],
                   [fb[kb]] + fb[qc * 4:qc * 4 + 4], [pLb])

            def s2(t):
                h, kb = tiles[t]
                j0 = max(0, kb * 128 - qc * 512)
                psL, pLb = stt[t][0], stt[t][1]
                pi = cnt_["pi"]
                cnt_["pi"] += 1
                p_, p_b = Pt[pi % 4], Ptb[pi % 4]
                stt[t] += [p_, p_b]
                ACT(g, p_[:, j0:512], psL[:, j0:512], AF.Exp, [pLb], [p_b], scale=0.125)

            def s3(t):
                h, kb = tiles[t]
                j0 = max(0, kb * 128 - qc * 512)
                p_, p_b = stt[t][2], stt[t][3]
                pm = cnt_["pm"]
                cnt_["pm"] += 1
                m_, m_b = Pm[pm % 4], Pmb[pm % 4]
                stt[t] += [m_, m_b]
                TT(g, "pool" if t % 3 == 0 else "dve", m_[:, j0:512], p_[:, j0:512], maskT[:, kb, j0:512], ALU.mult,
                   [p_b] + mTb[j0 // 128:4], [m_b])

            def s4(t):
                h, kb = tiles[t]
                hp, po = h // 2, 64 * (h % 2)
                j0 = max(0, kb * 128 - qc * 512)
                m_, m_b = stt[t][4], stt[t][5]
                if kb == 0:
                    pso[h] = psf(g, "dsO", [4, 5])
                psO, pOb = pso[h]
                MM(g, [mmf(psO[:, j0:512], dvx[:, kb, :], m_[:, j0:512], kb == 0, kb == last)], [dvb[kb], m_b], [pOb])
                if kb == last:
                    ACT(g, rs[0:64, :], psO[64:128, :], AF.Ln, [pOb], [rsb])
                    ACT(g, rs[0:64, :], rs[0:64, :], AF.Exp, [rsb], [rsb], scale=-1.0)
                    TT(g, "dve", odT[po:po + 64, hp, qc * 512:(qc + 1) * 512], psO[0:64, :], rs[0:64, :], ALU.mult, [pOb, rsb],
                       [odb[hp][qc]])
                del stt[t]

            pipeline([s1, s2, s3, s4], len(tiles))

        idx_blocks([14, 15])
        for p in range(7, -1, -1):
            if p > 0:
                idx_blocks([2 * p - 2, 2 * p - 1])
            bis_pair(p)
            mT_pair(p)
            if p % 2 == 0:
                att_chunk(p // 2)


def phase_hgrn(g, l, hT, hTb, ohT, ohb):
    P = g.P
    import os
    if int(os.environ.get("HGL", "9")) == 0:
        return
    with scope(g) as st:
        hi_tm = sb(g, st, [128, NB, 256], BF16, "hi_tm")
        hib = nbs(st, "hi", NB)
        hgs = sb(g, st, [128, NB, 256], BF16, "hgs")
        hgb = nbs(st, "hgs", NB)
        onb = sb(g, st, [128, 64], F32, "onorm")
        onbb = nb(st, "onorm")
        DMA(g, "sp", onb[:], g.onorm[l:l + 1, :].to_broadcast([128, 64]), (), [onbb])
        with scope(g) as s2:
            if take_pre(g, ("hihg", l)):
                w, wb = g.pre, g.preb
            else:
                w = sb(g, s2, [128, DC, 512], BF16, "whihg")
                wb = nb(s2, "whihg")
                DMA(g, "pool", w[:], win_cols(g, l, C_HI, C_HI + 512), (), [wb])
            sgs = [sb(g, s2, [128, 256], F32, "sgs") for _ in range(2)]
            sgsb = nbs(s2, "sgs", 2)
            for tb in range(NB):
                ps, pb = psf(g, "proj", [0, 1, 2, 3, 4, 5])
                hgv = int(os.environ.get("HGV", "15"))
                if hgv & 8:
                    mm_tm(g, ps, 512, w, wb, 0, hT, hTb, tb, pb)
                if hgv & 1:
                    CP(g, "dve", hi_tm[:, tb, :], ps[:, 0:256], [pb], [hib[tb]])
                sgt, sgtb = sgs[tb % 2], sgsb[tb % 2]
                if hgv & 2:
                    ACT(g, sgt[:], ps[:, 256:512], AF.Exp if hgv & 16 else AF.Sigmoid, [pb], [sgtb])
                if hgv & 4:
                    TT(g, "dve", hgs[:, tb, :], ps[:, 256:512], sgt[:], ALU.mult, [pb, sgtb], [hgb[tb]])
        with scope(g) as s3:
            NH = 4
            R = 4
            qtT = sb(g, s3, [128, NH, T], BF16, "qtT")
            ktT = sb(g, s3, [128, NH, T], BF16, "ktT")
            qtb = nbs(s3, "qt", NH)
            ktb = nbs(s3, "kt", NH)
            kt_tm = sb(g, s3, [128, NB, NH * 128], BF16, "kt_tm")
            kttb = nbs(s3, "kttm", NH)
            t1 = sb(g, s3, [128, T], F32, "t1")
            t2 = sb(g, s3, [128, T], F32, "t2")
            t3 = sb(g, s3, [128, T], F32, "t3")
            t1h, t2h, t3h = nbs(s3, "t1", 4), nbs(s3, "t2", 4), nbs(s3, "t3", 4)
            ebl = sb(g, s3, [128, NH, 32], F32, "ebl")
            eblb = nb(s3, "ebl")
            W = [sb(g, s3, [128, NH, 64], F32, "W") for _ in range(2)]
            Wb = nbs(s3, "W", 2)
            Sbf = [sb(g, s3, [128, NH, 64], BF16, "Sbf") for _ in range(R)]
            Sbfb = nbs(s3, "Sbf", R)
            attm = [sb(g, s3, [128, NH, 128], BF16, "attm") for _ in range(3)]
            attb = nbs(s3, "attm", 3)
            o_tm = [sb(g, s3, [128, NH * 64], F32, "o_tm") for _ in range(3)]
            otb = nbs(s3, "otm", 3)
            osq = sb(g, s3, [128, NH * 64], F32, "osq")
            osqb = nb(s3, "osq")
            og = [sb(g, s3, [128, NH * 64], BF16, "og") for _ in range(2)]
            ogb = nbs(s3, "og", 2)
            sm = [sb(g, s3, [128, 4], F32, "hsm") for _ in range(2)]
            smb = nbs(s3, "hsm", 2)
            wq = [sb(g, s3, [128, DC, 128], BF16, "wq") for _ in range(2)]
            wqb = nbs(s3, "wq", 2)
            wf = [sb(g, s3, [128, DC, 128], BF16, "wf") for _ in range(2)]
            wfb = nbs(s3, "wf", 2)

            def load_head(hd):
                DMA(g, "pool", wf[hd % 2][:], win_cols(g, l, C_HF + hd * 128, C_HF + hd * 128 + 128), (), [wfb[hd % 2]])
                DMA(g, "pool", wq[hd % 2][:], win_cols(g, l, C_HQ + hd * 128, C_HQ + hd * 128 + 128), (), [wqb[hd % 2]])

            def kt_transposes(hd):
                for k0 in (0, 8):
                    pt, ptb = psb(g)
                    MM(g, [trf(pt[:, m * 128:(m + 1) * 128], ktT[:, hd, (k0 + m) * 128:(k0 + m + 1) * 128], g.ident[:])
                           for m in range(8)], [ktb[hd], g.cb], [ptb])
                    CP(g, "act" if k0 == 0 else "dve", kt_tm[:, k0:k0 + 8, hd * 128:(hd + 1) * 128],
                       pt[:, :].rearrange("p (m j) -> p m j", j=128), [ptb], [kttb[hd]])

            load_head(0)
            for hd in range(NH):
                if hd + 1 < NH:
                    load_head(hd + 1)
                w_f, w_fb, w_q, w_qb = wf[hd % 2], wfb[hd % 2], wq[hd % 2], wqb[hd % 2]
                NQ = 4
                HS = [slice(q_ * (T // NQ), (q_ + 1) * (T // NQ)) for q_ in range(NQ)]
                for tc in range(4):
                    ps, pb = psf(g, "proj", [0, 1, 2, 3, 4, 5])
                    mm_fm(g, ps, 128, 512, w_f, w_fb, 0, hT, hTb, tc * 512, pb)
                    ACT(g, t1[:, tc * 512:(tc + 1) * 512], ps[:, :], AF.Sigmoid, [pb], [t1h[tc]])
                for hf in range(NQ):
                    TS(g, "dve", t1[:, HS[hf]], t1[:, HS[hf]], g.oml[:, l, hd:hd + 1], g.lbv[:, l, hd:hd + 1], ALU.mult, ALU.add,
                       [t1h[hf], g.cb], [t1h[hf]])
                for hf in range(NQ):
                    ACT(g, t2[:, HS[hf]], t1[:, HS[hf]], AF.Copy, [t1h[hf]], [t2h[hf]], scale=-1.0, bias=1.0)
                for hf in range(NQ):
                    TS(g, "dve", t1[:, HS[hf]], t1[:, HS[hf]], F_MIN, None, ALU.max, None, [t1h[hf]], [t1h[hf]])
                for hf in range(NQ):
                    ACT(g, t1[:, HS[hf]], t1[:, HS[hf]], AF.Ln, [t1h[hf]], [t1h[hf]])
                for hf in range(NQ):
                    P.op("dve", lambda e, hf=hf: e.tensor_tensor_scan(out=t3[:, HS[hf]], data0=g.resetm[:, HS[hf]], data1=t1[:, HS[hf]],
                                                                     initial=0.0, op0=ALU.mult, op1=ALU.add),
                         [t1h[hf], g.cb], [t3h[hf]], 2 * T // NQ)
                for hf in range(NQ):
                    TS(g, "dve", t3[:, HS[hf]], t3[:, HS[hf]], -80.0, None, ALU.max, None, [t3h[hf]], [t3h[hf]])
                for hf in range(NQ):
                    ACT(g, t1[:, HS[hf]], t3[:, HS[hf]], AF.Exp, [t3h[hf]], [t1h[hf]])
                for hf in range(NQ):
                    CP(g, "pool", ebl[:, hd, hf * (32 // NQ):(hf + 1) * (32 // NQ)].unsqueeze(2),
                       t1[:, HS[hf]].rearrange("p (c j) -> p c j", j=64)[:, :, 63:64], [t1h[hf]], [eblb])
                for hf in range(NQ):
                    ACT(g, t3[:, HS[hf]], t3[:, HS[hf]], AF.Exp, [t3h[hf]], [t3h[hf]], scale=-1.0)
                for hf in range(NQ):
                    TT(g, "dve", ktT[:, hd, HS[hf]], t2[:, HS[hf]], t3[:, HS[hf]], ALU.mult, [t2h[hf], t3h[hf]], [ktb[hd]])
                for tc in range(4):
                    ps, pb = psf(g, "proj", [0, 1, 2, 3, 4, 5])
                    mm_fm(g, ps, 128, 512, w_q, w_qb, 0, hT, hTb, tc * 512, pb)
                    ACT(g, t2[:, tc * 512:(tc + 1) * 512], ps[:, :], AF.Sigmoid, [pb], [t2h[tc]])
                    TT(g, "dve", t2[:, tc * 512:(tc + 1) * 512], ps[:, :], t2[:, tc * 512:(tc + 1) * 512], ALU.mult, [pb, t2h[tc]],
                       [t2h[tc]])
                for hf in range(NQ):
                    TT(g, "dve", qtT[:, hd, HS[hf]], t2[:, HS[hf]], t1[:, HS[hf]], ALU.mult, [t2h[hf], t1h[hf]], [qtb[hd]])
                if hd > 0:
                    kt_transposes(hd - 1)

            kt_transposes(NH - 1)
            NCH = 32
            xps = {}

            def o_transposes(tb):
                og_, og_b = og[tb % 2], ogb[tb % 2]
                pt, ptb = psb(g)
                MM(g, [trf(pt[:, m * 128:(m + 1) * 128], og_[:, m * 128:(m + 1) * 128], g.ident[:]) for m in range(2)],
                   [og_b, g.cb], [ptb])
                CP(g, "act", ohT[:, 0:2, tb * 128:(tb + 1) * 128], pt[:, 0:256].rearrange("p (m j) -> p m j", j=128), [ptb],
                   [ohb[0][tb // 4], ohb[1][tb // 4]])

            def emit_X(c):
                tb, pr = c // 2, (c % 2) * 64
                psX, pXb = psf(g, "hgX", [0, 1, 2])
                xps[c] = (psX, pXb)
                MM(g, [mmf(psX[:, hh * 64:(hh + 1) * 64], kt_tm[pr:pr + 64, tb, hh * 128:(hh + 1) * 128],
                           hi_tm[pr:pr + 64, tb, hh * 64:(hh + 1) * 64], True, True) for hh in range(NH)], kttb + [hib[tb]], [pXb])

            def emit_A(tb):
                psA, pAb = psf(g, "hgA", [3])
                MM(g, [mmf(psA[:, hh * 128:(hh + 1) * 128], ktT[:, hh, tb * 128:(tb + 1) * 128],
                           qtT[:, hh, tb * 128:(tb + 1) * 128], True, True) for hh in range(NH)], ktb + qtb, [pAb])
                TT(g, "dve", attm[tb % 3][:], psA[:, :].rearrange("p (h t) -> p h t", t=128),
                   g.maskbd[:].unsqueeze(1).to_broadcast([128, NH, 128]), ALU.mult, [pAb, g.cb], [attb[tb % 3]])

            emit_X(0)
            emit_X(1)
            emit_A(0)
            CP(g, "dve", W[0][:], xps[0][0][:, 0:NH * 64].rearrange("p (h v) -> p h v", v=64), [xps[0][1]], [Wb[0]])
            MS(g, "pool", Sbf[0][:], 0.0, [Sbfb[0]])
            for c in range(NCH):
                tb, half = c // 2, c % 2
                pr = half * 64
                if c + 2 < NCH:
                    emit_X(c + 2)
                if half == 0 and tb + 1 < NB:
                    emit_A(tb + 1)
                if c + 1 < NCH:
                    eb_ = ebl[:, :, c:c + 1].to_broadcast([128, NH, 64])
                    TT(g, "pool", Sbf[(c + 1) % R][:], W[c % 2][:], eb_, ALU.mult, [Wb[c % 2], eblb], [Sbfb[(c + 1) % R]])
                    psX, pXb = xps.pop(c + 1)
                    for hh in range(NH):
                        STT(g, W[(c + 1) % 2][:, hh, :], W[c % 2][:, hh, :], ebl[:, hh, c:c + 1], psX[:, hh * 64:(hh + 1) * 64],
                            ALU.mult, ALU.add, [Wb[c % 2], eblb, pXb], [Wb[(c + 1) % 2]])
                am, amb = attm[tb % 3], attb[tb % 3]
                ot, otbuf = o_tm[tb % 3], otb[tb % 3]
                psO, pOb = psf(g, "hgO", [4, 5])
                fns = []
                for hh in range(NH):
                    fns.append(mmf(psO[0:64, hh * 64:(hh + 1) * 64], am[pr:pr + 64, hh, pr:pr + 64],
                                   hi_tm[pr:pr + 64, tb, hh * 64:(hh + 1) * 64], True, False))
                    fns.append(mmf(psO[0:64, hh * 64:(hh + 1) * 64], qtT[:, hh, c * 64:(c + 1) * 64], Sbf[c % R][:, hh, :], False, True))
                MM(g, fns, [amb, hib[tb], Sbfb[c % R]] + qtb, [pOb])
                CP(g, "act", ot[pr:pr + 64, :], psO[0:64, 0:NH * 64], [pOb], [otbuf])
                if half == 1:
                    sm_, sm_b = sm[tb % 2], smb[tb % 2]
                    og_, og_b = og[tb % 2], ogb[tb % 2]
                    TT(g, "pool", osq[:], ot[:], ot[:], ALU.mult, [otbuf], [osqb])
                    ss = sm_[:, 0:NH]
                    P.op("dve", lambda e, ss=ss: e.tensor_reduce(out=ss, in_=osq[:].rearrange("p (h d) -> p h d", d=64), axis=AX.X,
                                                                 op=ALU.add), [osqb], [sm_b], NH * 64)
                    ACT(g, ss, ss, AF.Ln, [sm_b], [sm_b], scale=1.0 / 64, bias=EPS)
                    ACT(g, ss, ss, AF.Exp, [sm_b], [sm_b], scale=-0.5)
                    o3 = ot[:].rearrange("p (h d) -> p h d", d=64)
                    TT(g, "dve", o3, o3, ss.unsqueeze(2).to_broadcast([128, NH, 64]), ALU.mult, [otbuf, sm_b], [otbuf])
                    TT(g, "dve", o3, o3, onb[:].unsqueeze(1).to_broadcast([128, NH, 64]), ALU.mult, [otbuf, onbb], [otbuf])
                    TT(g, "dve", og_[:], ot[:], hgs[:, tb, :], ALU.mult, [otbuf, hgb[tb]], [og_b])
                    if tb > 0:
                        o_transposes(tb - 1)
                    if tb == NB - 1:
                        o_transposes(tb)


def phase_mix(g, l, hT, hTb, osbT, osbb, odT, odb, ohT, ohb, src_ap, src_bufs):
    P = g.P
    with scope(g) as st:
        mixT = sb(g, st, [128, DC, T], BF16, "mixT")
        mxb = nbs(st, "mix", DC, 4)
        wg = [sb(g, st, [128, DC, 3, 256], BF16, "wg") for _ in range(2)]
        wgb = nbs(st, "wg", 2, 3)
        wy = sb(g, st, [128, 8, D], BF16, "wy")
        wyb = nbs(st, "wy", 3)
        sg = [sb(g, st, [128, 512], F32, "sg") for _ in range(2)]
        sgb = nbs(st, "sg", 2)
        acc = [sb(g, st, [128, 512], F32, "acc") for _ in range(2)]
        accb = nbs(st, "acc", 2)
        tm = [sb(g, st, [128, 512], F32, "tm") for _ in range(2)]
        tmb = nbs(st, "tm", 2)
        k = 0

        def load_pair(dp):
            w_, w_b = wg[dp % 2], wgb[dp % 2]
            for gi in range(3):
                c0 = C_G + gi * 1024 + dp * 256
                DMA(g, "pool", w_[:, :, gi, :], win_cols(g, l, c0, c0 + 256), (), [w_b[gi]])

        load_pair(0)
        DMA(g, "pool", wy[:, 0:3, :], g.w_sb[l].rearrange("(c p) n -> p c n", p=128), (), [wyb[0]])
        DMA(g, "pool", wy[:, 3:6, :], g.w_dsa[l].rearrange("(c p) n -> p c n", p=128), (), [wyb[1]])
        DMA(g, "pool", wy[:, 6:8, :], g.w_hg[l].rearrange("(c p) n -> p c n", p=128), (), [wyb[2]])
        wo = sb(g, st, [128, DC, D], BF16, "wo")
        wob = nbs(st, "wo", 2)
        for dc in range(DC):
            dp, do = dc // 2, (dc % 2) * 128
            w_, w_b = wg[dp % 2], wgb[dp % 2]
            y_, y_b = wy, wyb
            if dc % 2 == 0:
                if dp + 1 < DC // 2:
                    load_pair(dp + 1)
                else:
                    for nh in range(2):
                        DMA(g, "pool", wo[:, :, nh * 512:(nh + 1) * 512],
                            g.w_out[l].rearrange("(c p) n -> p c n", p=128)[:, :, nh * 512:(nh + 1) * 512], (), [wob[nh]])
                    prefetch(g, ("wu", l, 0), g.w_up[l].rearrange("(c p) n -> p c n", p=128)[:, :, 0:512], 512)
            for tc in range(4):
                a_, a_b = acc[k % 2], accb[k % 2]
                for gi, (oT, obufs, nch, c0) in enumerate(((osbT, osbb, 3, 0), (odT, odb, 3, 3), (ohT, ohb, 2, 6))):
                    psG, pGb = psf(g, "mxG", [0, 1, 2])
                    MM(g, [mmf(psG[:, :], w_[:, c, gi, do:do + 128], hT[:, c, tc * 512:(tc + 1) * 512], c == 0, c == DC - 1) for c in range(DC)],
                       [w_b[gi]] + hTb[tc * 4:tc * 4 + 4], [pGb])
                    s_, s_b = sg[(k * 3 + gi) % 2], sgb[(k * 3 + gi) % 2]
                    ACT(g, s_[:], psG[:, :], AF.Sigmoid, [pGb], [s_b])
                    psY, pYb = psf(g, "mxY", [3, 4, 5])
                    MM(g, [mmf(psY[:, :], y_[:, c0 + c, dc * 128:(dc + 1) * 128], oT[:, c, tc * 512:(tc + 1) * 512], c == 0, c == nch - 1)
                           for c in range(nch)],
                       [y_b[gi]] + [obufs[c][tc] for c in range(nch)], [pYb])
                    if gi == 0:
                        TT(g, "dve", a_[:], psY[:, :], s_[:], ALU.mult, [pYb, s_b], [a_b])
                    else:
                        t_, t_b = tm[gi % 2], tmb[gi % 2]
                        TT(g, "dve", t_[:], psY[:, :], s_[:], ALU.mult, [pYb, s_b], [t_b])
                        if gi == 1:
                            TT(g, "pool", a_[:], a_[:], t_[:], ALU.add, [a_b, t_b], [a_b])
                        else:
                            TT(g, "pool", mixT[:, dc, tc * 512:(tc + 1) * 512], a_[:], t_[:], ALU.add, [a_b, t_b], [mxb[dc][tc]])
                k += 1
        xbs = [sb(g, st, [128, D], F32, "xb") for _ in range(3)]
        xbb = nbs(st, "xb", 3)
        nctx = norm_setup(g, st, g.norm_mlp[l:l + 1, :])
        def part_a(tb):
            xb, xbuf = xbs[tb % 3], xbb[tb % 3]
            DMA(g, "sp", xb[:], src_ap[tb * 128:(tb + 1) * 128, :], [src_bufs[tb]], [xbuf])
            for nh in range(2):
                ps, pb = psf(g, "mxO", [0, 1, 2, 3, 4, 5])
                MM(g, [mmf(ps[:, :], mixT[:, c, tb * 128:(tb + 1) * 128], wo[:, c, nh * 512:(nh + 1) * 512], c == 0, c == DC - 1)
                       for c in range(DC)], [wob[nh]] + [mxb[c][tb // 4] for c in range(DC)], [pb])
                xs = xb[:, nh * 512:(nh + 1) * 512]
                TT(g, "dve", xs, ps[:, :], xs, ALU.add, [pb, xbuf], [xbuf])
            DMA(g, "sp", g.xres_d[tb * 128:(tb + 1) * 128, :], xb[:], [xbuf], [g.xres_b[tb]])

        part_a(0)
        for tb in range(NB):
            if tb + 1 < NB:
                part_a(tb + 1)
            norm_block(g, nctx, xbs[tb % 3], xbb[tb % 3], tb, hT, hTb)


def phase_ffn(g, l, hT, hTb, dst_ap, dst_bufs):
    for half in range(2):
        last = half == 1
        with scope(g) as st:
            uT = sb(g, st, [128, 16, T], BF16, "uT")
            ub = nbs(st, "uT", 16, 4)
            wd = sb(g, st, [128, 16, D], BF16, "wd")
            wdb = nbs(st, "wd", 2)
            wdv = g.w_down[l].rearrange("(f p) n -> p f n", p=128)
            wu = [sb(g, st, [128, DC, 512], BF16, "wu") for _ in range(2)]
            wub = nbs(st, "wu", 2)
            rt = [sb(g, st, [128, 512], BF16, "rt") for _ in range(2)]
            rtb = nbs(st, "rt", 2)
            k = 0

            def load_wu(g4):
                c0 = half * 2048 + g4 * 512
                DMA(g, "pool", wu[g4 % 2][:], g.w_up[l].rearrange("(c p) n -> p c n", p=128)[:, :, c0:c0 + 512], (), [wub[g4 % 2]])

            pre0 = take_pre(g, ("wu", l, half))
            if not pre0:
                load_wu(0)
            for g4 in range(4):
                w_, w_b = wu[g4 % 2], wub[g4 % 2]
                if g4 == 0 and pre0:
                    w_, w_b = g.pre, g.preb
                if g4 + 1 < 4:
                    load_wu(g4 + 1)
                if g4 == 1:
                    for nh in range(2):
                        DMA(g, "pool", wd[:, :, nh * 512:(nh + 1) * 512], wdv[:, half * 16:(half + 1) * 16, nh * 512:(nh + 1) * 512], (),
                            [wdb[nh]])
                for fcl in range(4):
                    fc = g4 * 4 + fcl
                    for tc in range(4):
                        ps, pb = psf(g, "proj", [0, 1, 2, 3, 4, 5])
                        mm_fm(g, ps, 128, 512, w_, w_b, fcl * 128, hT, hTb, tc * 512, pb)
                        r_, r_b = rt[k % 2], rtb[k % 2]
                        k += 1
                        ACT(g, r_[:], ps[:, :], AF.Relu, [pb], [r_b])
                        TT(g, "pool", uT[:, fc, tc * 512:(tc + 1) * 512], r_[:], r_[:], ALU.mult, [r_b], [ub[fc][tc]])
            if half == 0:
                prefetch(g, ("wu", l, 1), g.w_up[l].rearrange("(c p) n -> p c n", p=128)[:, :, 2048:2560], 512)
            elif l + 1 < DEPTH:
                prefetch(g, ("sq", l + 1), win_cols(g, l + 1, C_SQ, C_SQ + 384), 384)
            xbs = [sb(g, st, [128, D], F32, "xb") for _ in range(4)]
            xbb = nbs(st, "xb", 4)
            nctx = norm_setup(g, st, g.norm_mix[l + 1:l + 2, :]) if (last and l + 1 < DEPTH) else None
            def down_a(tb):
                xb, xbuf = xbs[tb % 4], xbb[tb % 4]
                DMA(g, "sp", xb[:], g.xres_d[tb * 128:(tb + 1) * 128, :], [g.xres_b[tb]], [xbuf])
                for nh in range(2):
                    ps, pb = psf(g, "proj", [0, 1, 2, 3, 4, 5])
                    MM(g, [mmf(ps[:, :], uT[:, fc, tb * 128:(tb + 1) * 128], wd[:, fc, nh * 512:(nh + 1) * 512], fc == 0, fc == 15)
                           for fc in range(16)], [wdb[nh]] + [ub[fc][tb // 4] for fc in range(16)], [pb])
                    xs = xb[:, nh * 512:(nh + 1) * 512]
                    TT(g, "dve", xs, ps[:, :], xs, ALU.add, [pb, xbuf], [xbuf])
                if last:
                    DMA(g, "sp", dst_ap[tb * 128:(tb + 1) * 128, :], xb[:], [xbuf], [dst_bufs[tb]])
                else:
                    DMA(g, "sp", g.xres_d[tb * 128:(tb + 1) * 128, :], xb[:], [xbuf], [g.xres_b[tb]])

            down_a(0)
            for tb in range(NB):
                if tb + 1 < NB:
                    down_a(tb + 1)
                if last and nctx is not None:
                    norm_block(g, nctx, xbs[tb % 4], xbb[tb % 4], tb, hT, hTb)


def dump_t(g, name, t, ncol):
    if g.dump == name:
        g.P.barrier()
        DMA(g, "sp", g.dbg_d[:, 0:ncol], t, [], [g.dbgb])
        g.P.wait_bufs("sp", [g.dbgb])
        g.P.barrier()


def build_layer(g, l):
    if l > 0 and g.stage < 7:
        return
    src_ap, src_bufs = (g.x_d, g.xin_b) if l == 0 else (g.xres_d, g.xres_b)
    with scope(g) as ls:
        hT, hTb = g.hT, g.hTb
        if l == 0:
            with scope(g) as st:
                norm_T(g, st, src_ap, src_bufs, g.norm_mix[l:l + 1, :], hT, hTb)
        if l == 0:
            dump_t(g, "hT", hT[:].rearrange("p c t -> p (c t)"), 8 * T)
        if g.stage < 2:
            return
        with scope(g) as ms:
            osbT = sb(g, ms, [128, 3, T], BF16, "osbT")
            odT = sb(g, ms, [128, 3, T], BF16, "odT")
            ohT = sb(g, ms, [128, 2, T], BF16, "ohT")
            osbb = nbs(ms, "osb", 3, 4)
            odb = nbs(ms, "od", 3, 4)
            ohb = nbs(ms, "oh", 2, 4)
            phase_sb(g, l, hT, hTb, osbT, osbb)
            if l == 0:
                dump_t(g, "osbT", osbT[:].rearrange("p c t -> p (c t)"), 3 * T)
            if g.stage < 3:
                return
            phase_dsa(g, l, hT, hTb, odT, odb)
            if l == 0:
                dump_t(g, "odT", odT[:].rearrange("p c t -> p (c t)"), 3 * T)
            if g.stage < 4:
                return
            phase_hgrn(g, l, hT, hTb, ohT, ohb)
            if l == 0:
                dump_t(g, "ohT", ohT[:].rearrange("p c t -> p (c t)"), 2 * T)
            if g.stage < 5:
                return
            phase_mix(g, l, hT, hTb, osbT, osbb, odT, odb, ohT, ohb, src_ap, src_bufs)
        if g.stage < 6:
            return
        if l == DEPTH - 1:
            phase_ffn(g, l, hT, hTb, g.out_d, g.out_b)
        else:
            phase_ffn(g, l, hT, hTb, g.xres_d, g.xres_b)


_NC_CACHE = {}


def rope_tables():
    half = 8
    inv = 500000.0 ** (-(np.arange(half, dtype=np.float32) * 2.0) / 16.0)
    ang = np.arange(T, dtype=np.float32)[:, None] * inv[None, :].astype(np.float32)
    return np.cos(ang).astype(np.float32), np.sin(ang).astype(np.float32)


def kernel(x, norm_mix, w_in, qn_dsa, kn_dsa, hgrn_lb, hgrn_onorm, w_br_sb, w_br_dsa, w_br_hgrn, w_out, norm_mlp, w_up, w_down):
    if "nc" not in _NC_CACHE:
        _NC_CACHE["nc"] = build_two_pass()
    nc = _NC_CACHE["nc"]
    f = lambda a: np.ascontiguousarray(np.asarray(a, dtype=np.float32))
    cs, sn = rope_tables()
    shared = dict(norm_mix=f(norm_mix), w_in=f(w_in), qn_dsa=f(qn_dsa), kn_dsa=f(kn_dsa), hgrn_lb=f(hgrn_lb),
                  hgrn_onorm=f(hgrn_onorm), w_br_sb=f(w_br_sb), w_br_dsa=f(w_br_dsa), w_br_hgrn=f(w_br_hgrn),
                  w_out=f(w_out), norm_mlp=f(norm_mlp), w_up=f(w_up), w_down=f(w_down), rope_cos=cs, rope_sin=sn)
    xs = f(x)
    in_maps = [dict(shared, x=xs[b]) for b in range(8)]
    res = run_bass_kernel_spmd(nc, in_maps, core_ids=list(range(8)))
    return np.stack([np.asarray(r["out"], dtype=np.float32) for r in res.results], axis=0)
```

```python
import math
import numpy as np
from contextlib import ExitStack, contextmanager
import concourse.bass as bass
import concourse.mybir as mybir
from concourse.bass_utils import run_bass_kernel_spmd

F32 = mybir.dt.float32
BF16 = mybir.dt.bfloat16
AF = mybir.ActivationFunctionType
ALU = mybir.AluOpType
AX = mybir.AxisListType

T = 2048
D = 1024
NB = 16
DC = 8
DIN = 6856
DFF = 4096
DEPTH = 2
EPS = 1e-6
F_MIN = 1e-12
IDX_SCALE = (64 * 8) ** -0.5
C_SQ, C_SK, C_SV = 0, 384, 768
C_DQ, C_DK, C_DV = 1152, 1536, 1600
C_IQ, C_IK, C_IW = 1664, 2176, 2240
C_HQ, C_HF, C_HI, C_HG = 2248, 2760, 3272, 3528
C_G = 3784
N_BISECT = 12


class Buf:
    __slots__ = ("name", "w", "r", "dsem", "excl")

    def __init__(self, name, excl=False):
        self.name = name
        self.w = None
        self.r = {}
        self.dsem = None
        self.excl = excl


class Prog:
    ENG = ("pe", "act", "dve", "pool", "sp")
    CLEAR_NS = 330.0
    FILL_NS = {"dve": 66.0, "act": 190.0, "pool": 125.0}
    EST = {"dve": (60.0, 0.26), "act": (185.0, 0.83), "pool": (120.0, 0.8), "pe": (0.0, 0.0), "sp": (0.0, 0.0)}

    def __init__(self, nc, stack, needed=None):
        self.needed = needed
        self.used = set()
        self.remap = {}
        self.sig = {}
        self.fill = {}
        self.nc = nc
        self.stack = stack
        self.eng = {"pe": nc.tensor, "act": nc.scalar, "dve": nc.vector, "pool": nc.gpsimd, "sp": nc.sync}
        self.cnt = {e: 0 for e in self.ENG}
        self.known = {e: {} for e in self.ENG}
        self.sems = {}
        self.semval = {}
        for e in ("pe", "act", "dve", "pool"):
            self.sems["E_" + e] = stack.enter_context(nc.semaphore("sem_" + e))
            self.semval["E_" + e] = 0
        self.ndsem = 0
        self.free_dsems = []
        self.nwaits = 0
        self.tcum = {e: 0.0 for e in self.ENG}
        self.tend = {e: {} for e in self.ENG}

    def _dsem(self, buf):
        if buf.dsem is None:
            if self.free_dsems:
                key = self.free_dsems.pop()
            else:
                key = "D%d" % self.ndsem
                self.ndsem += 1
                self.sems[key] = self.stack.enter_context(self.nc.semaphore("dsem%d" % (self.ndsem - 1)))
                self.semval[key] = 0
            buf.dsem = key
        return buf.dsem

    def release(self, bufs):
        for b in bufs:
            if b.dsem is not None:
                self.free_dsems.append(b.dsem)
                b.dsem = None

    def _waits(self, eng, deps):
        need = {}
        own = "E_" + eng
        for (k, v) in deps:
            if eng == "pe" and k == "E_pe":
                continue
            if k == own and eng in ("act", "dve", "pool"):
                te = self.tend[eng].get(v)
                if te is not None and eng in self.fill:
                    gap = self.CLEAR_NS - (self.tcum[eng] - te)
                    if gap > 0:
                        n = int(math.ceil(gap / self.FILL_NS[eng]))
                        for _ in range(n):
                            self.fill[eng](self.eng[eng])
                        self.tcum[eng] += n * self.FILL_NS[eng]
                        self.nfill = getattr(self, "nfill", 0) + n
                continue
            if v > need.get(k, 0):
                need[k] = v
        out = []
        kn = self.known[eng]
        for k, v in need.items():
            if kn.get(k, 0) < v:
                kn[k] = v
                out.append((k, v))
        return out

    @staticmethod
    def _deps(reads, writes):
        deps = []
        for b in reads:
            if b.w is not None:
                deps.append(b.w)
            if b.excl:
                deps.extend(b.r.items())
        for b in writes:
            if b.w is not None:
                deps.append(b.w)
            deps.extend(b.r.items())
        return deps

    def _emit_waits(self, eng, waits):
        e = self.eng[eng]
        for (k, v) in waits:
            if k.startswith("E_"):
                self.used.add((k, v))
                if self.needed is not None:
                    v = self.remap[(k, v)]
            e.wait_ge(self.sems[k], v)
            self.nwaits += 1

    def _mark(self, ev, reads, writes):
        k, v = ev
        for b in reads:
            if b.r.get(k, 0) < v:
                b.r[k] = v
        for b in writes:
            b.w = ev
            b.r = {}

    def op(self, eng, fn, reads=(), writes=(), n=0):
        self.group(eng, [fn], reads, writes, n)

    def group(self, eng, fns, reads=(), writes=(), n=0):
        self._emit_waits(eng, self._waits(eng, self._deps(reads, writes)))
        e = self.eng[eng]
        for fn in fns[:-1]:
            fn(e)
        self.cnt[eng] += 1
        ov, pe_ = self.EST[eng]
        self.tcum[eng] += ov + pe_ * n
        td = self.tend[eng]
        td[self.cnt[eng]] = self.tcum[eng]
        if len(td) > 64:
            for k_ in sorted(td)[:32]:
                del td[k_]
        key = "E_" + eng
        self.semval[key] = self.cnt[eng]
        if self.needed is None or (key, self.cnt[eng]) in self.needed:
            self.sig[key] = self.sig.get(key, 0) + 1
            self.remap[(key, self.cnt[eng])] = self.sig[key]
            fns[-1](e).then_inc(self.sems[key], 1)
        else:
            fns[-1](e)
        self._mark((key, self.cnt[eng]), reads, writes)

    def dma(self, eng, fn, reads=(), writes=()):
        assert len(writes) == 1
        wb = writes[0]
        deps = self._deps(reads, writes)
        if eng == "pool" and getattr(self, "prev_swdge", None) is not None:
            deps.append(self.prev_swdge)
        self._emit_waits(eng, self._waits(eng, deps))
        key = self._dsem(wb)
        self.semval[key] += 16
        fn(self.eng[eng]).then_inc(self.sems[key], 16)
        if eng == "pool":
            self.prev_swdge = getattr(self, "last_swdge", None)
            self.last_swdge = (key, self.semval[key])
        self._mark((key, self.semval[key]), reads, writes)

    def barrier(self):
        deps = [(k, v) for k, v in self.semval.items() if v > 0]
        for eng in self.ENG:
            self._emit_waits(eng, self._waits(eng, deps))

    def wait_bufs(self, eng, bufs):
        deps = []
        for b in bufs:
            if b.w is not None:
                deps.append(b.w)
            deps.extend(b.r.items())
        self._emit_waits(eng, self._waits(eng, deps))


class G:
    pass


def bufs(prefix, *dims):
    if len(dims) == 1:
        return [Buf("%s%d" % (prefix, i)) for i in range(dims[0])]
    return [bufs("%s%d_" % (prefix, i), *dims[1:]) for i in range(dims[0])]


def build_program(stage=99, dump=None, needed=None):
    nc = bass.Bass("TRN2", target_bir_lowering=False)
    g = G()
    g.nc = nc
    g.stage = stage
    g.dump = dump
    import os
    g.ntl = int(os.environ.get("NTL", "9"))
    g.dbg_d = None
    if dump is not None:
        g.dbg_d = nc.dram_tensor("dbg", [128, 8 * T], BF16, kind="ExternalOutput").ap()
        g.dbgb = Buf("dbg")
    dt = lambda name, shape, kind, d=F32: nc.dram_tensor(name, shape, d, kind=kind).ap()
    g.x_d = dt("x", [T, D], "ExternalInput")
    g.norm_mix = dt("norm_mix", [DEPTH, D], "ExternalInput")
    g.w_in = dt("w_in", [DEPTH, D, DIN], "ExternalInput")
    g.qn = dt("qn_dsa", [DEPTH, 64], "ExternalInput")
    g.kn = dt("kn_dsa", [DEPTH, 64], "ExternalInput")
    g.lb_d = dt("hgrn_lb", [DEPTH, 512], "ExternalInput")
    g.onorm = dt("hgrn_onorm", [DEPTH, 64], "ExternalInput")
    g.w_sb = dt("w_br_sb", [DEPTH, 384, D], "ExternalInput")
    g.w_dsa = dt("w_br_dsa", [DEPTH, 384, D], "ExternalInput")
    g.w_hg = dt("w_br_hgrn", [DEPTH, 256, D], "ExternalInput")
    g.w_out = dt("w_out", [DEPTH, D, D], "ExternalInput")
    g.norm_mlp = dt("norm_mlp", [DEPTH, D], "ExternalInput")
    g.w_up = dt("w_up", [DEPTH, D, DFF], "ExternalInput")
    g.w_down = dt("w_down", [DEPTH, DFF, D], "ExternalInput")
    g.cs_d = dt("rope_cos", [T, 8], "ExternalInput")
    g.sn_d = dt("rope_sin", [T, 8], "ExternalInput")
    g.out_d = dt("out", [T, D], "ExternalOutput")
    g.xres_d = dt("xres", [T, D], "Internal")
    g.xin_b = bufs("xin", NB)
    g.xres_b = bufs("xres", NB)
    g.out_b = bufs("outb", NB)

    with ExitStack() as gs:
        P = Prog(nc, gs, needed)
        g.P = P
        fa = gs.enter_context(nc.sbuf_tensor("fill_a", [128, 2], F32))
        fd = gs.enter_context(nc.sbuf_tensor("fill_d", [128, 2], F32))
        nc.vector.memset(fd[:], 0.0)
        nc.vector.memset(fa[:], 0.0)
        P.fill["dve"] = lambda e: e.memset(fd[:, 0:1], 0.0)
        P.fill["act"] = lambda e: e.activation(out=fa[:, 0:1], in_=fa[:, 1:2], func=AF.Copy)
        fp = gs.enter_context(nc.sbuf_tensor("fill_p", [128, 2], F32))
        nc.gpsimd.memset(fp[:], 0.0)
        P.fill["pool"] = lambda e: e.memset(fp[:, 0:1], 0.0)
        g.uid = 0
        g.psF = [gs.enter_context(nc.psum_tensor("psF%d" % i, [128, 512], F32)) for i in range(6)]
        g.psFb = [Buf("psF%d" % i, excl=True) for i in range(6)]
        g.psB = [gs.enter_context(nc.psum_tensor("psB%d" % i, [128, 1024], BF16)) for i in range(2)]
        g.psBb = [Buf("psB%d" % i, excl=True) for i in range(2)]
        g.rotc = {}
        build_consts(g, gs)
        g.hT = gs.enter_context(nc.sbuf_tensor("hT_glob", [128, DC, T], BF16))
        g.hTb = bufs("hT", NB)
        g.pre = gs.enter_context(nc.sbuf_tensor("pre_w", [128, DC, 512], BF16))
        g.preb = Buf("pre_w")
        prefetch(g, ("sq", 0), win_cols(g, 0, C_SQ, C_SQ + 384), 384)
        for l in range(DEPTH):
            if g.stage >= 1:
                build_layer(g, l)
        if g.dbg_d is not None:
            P.wait_bufs("sp", [g.dbgb])
        P.wait_bufs("sp", g.out_b)
        P.barrier()
        g.used = P.used
        print("ops", P.cnt, "signals", P.sig, "waits", P.nwaits, "fillers", getattr(P, "nfill", 0), "dsems", P.ndsem, flush=True)
    return nc, P.used


def build_two_pass(stage=99, dump=None):
    _, used = build_program(stage, dump, None)
    nc, _ = build_program(stage, dump, used)
    return nc


def pipeline(stages, ntiles):
    ns = len(stages)
    for t in range(ntiles + ns - 1):
        for k, f in enumerate(stages):
            i = t - k
            if 0 <= i < ntiles:
                f(i)


def prefetch(g, tag, src, ncols):
    DMA(g, "pool", g.pre[:, :, 0:ncols], src, (), [g.preb])
    g.pre_tag = tag


def take_pre(g, tag):
    if getattr(g, "pre_tag", None) == tag:
        g.pre_tag = None
        return True
    return False


def rot(g, role, items):
    i = g.rotc.get(role, 0)
    g.rotc[role] = i + 1
    return items[i % len(items)]


def psf(g, role, banks):
    b = rot(g, role, banks)
    return g.psF[b], g.psFb[b]


def psb(g, role="pb"):
    b = rot(g, role, [0, 1])
    return g.psB[b], g.psBb[b]


@contextmanager
def scope(g):
    st = ExitStack()
    st.tbufs = []
    try:
        yield st
    finally:
        g.P.barrier()
        g.P.release(st.tbufs)
        st.close()


def sb(g, st, shape, dtype, name=None):
    g.uid += 1
    return st.enter_context(g.nc.sbuf_tensor("%s_%d" % (name or "t", g.uid), shape, dtype))


def nb(st, name):
    b = Buf(name)
    st.tbufs.append(b)
    return b


def nbs(st, prefix, *dims):
    r = bufs(prefix, *dims)

    def flat(x):
        if isinstance(x, Buf):
            st.tbufs.append(x)
        else:
            for y in x:
                flat(y)
    flat(r)
    return r


def _fs(ap):
    try:
        return int(ap.free_size())
    except Exception:
        return 0


def ACT(g, out, in_, func, reads, writes, **kw):
    g.P.op("act", lambda e: e.activation(out=out, in_=in_, func=func, **kw), reads, writes, _fs(out))


def TT(g, eng, out, in0, in1, op, reads, writes):
    g.P.op(eng, lambda e: e.tensor_tensor(out=out, in0=in0, in1=in1, op=op), reads, writes, _fs(out))


def TS(g, eng, out, in0, s1, s2, op0, op1, reads, writes, **kw):
    if op1 is None:
        s2 = 0.0 if isinstance(s1, (int, float)) else g.zc[0:in0.shape[0], 0:1]
        g.P.op(eng, lambda e: e.tensor_scalar(out=out, in0=in0, scalar1=s1, scalar2=s2, op0=op0, op1=ALU.add, **kw), reads, writes, _fs(out))
    else:
        g.P.op(eng, lambda e: e.tensor_scalar(out=out, in0=in0, scalar1=s1, scalar2=s2, op0=op0, op1=op1, **kw), reads, writes, _fs(out))


def STT(g, out, in0, scalar, in1, op0, op1, reads, writes):
    g.P.op("dve", lambda e: e.scalar_tensor_tensor(out=out, in0=in0, scalar=scalar, in1=in1, op0=op0, op1=op1), reads, writes, _fs(out))


def CP(g, eng, out, in_, reads, writes):
    if eng == "act":
        g.P.op("act", lambda e: e.activation(out=out, in_=in_, func=AF.Copy), reads, writes, _fs(out))
    else:
        g.P.op(eng, lambda e: e.tensor_copy(out, in_), reads, writes, _fs(out))


def MS(g, eng, ap, val, writes):
    g.P.op(eng, lambda e: e.memset(ap, val), (), writes, _fs(ap))


def ASEL(g, out, in_, pattern, cmp, fill, base, cm, reads, writes):
    g.P.op("pool", lambda e: e.affine_select(out=out, in_=in_, pattern=pattern, compare_op=cmp, fill=fill, base=base,
                                             channel_multiplier=cm), reads, writes, _fs(out))


def MM(g, outs_fns, reads, writes):
    g.P.group("pe", outs_fns, reads, writes)


def mmf(out, lhsT, rhs, start, stop):
    return lambda e: e.matmul(out, lhsT=lhsT, rhs=rhs, start=start, stop=stop)


def trf(out, in_, ident):
    return lambda e: e.transpose(out, in_, ident)


def DMA(g, eng, out, in_, reads, writes, **kw):
    g.P.dma(eng, lambda e: e.dma_start(out=out, in_=in_, **kw), reads, writes)


def build_consts(g, gs):
    nc = g.nc
    mk = lambda name, shape, d: gs.enter_context(nc.sbuf_tensor(name, shape, d))
    g.ident = mk("ident", [128, 128], BF16)
    g.negtri = mk("negtri", [128, 128], BF16)
    g.negones = mk("negones", [128, 128], BF16)
    g.onesb = mk("onesb", [128, 128], BF16)
    g.maskbd = mk("maskbd", [128, 128], F32)
    g.onesf = mk("onesf", [128, 128], F32)
    g.resetm = mk("resetm", [128, T], BF16)
    g.zc = mk("zc", [128, 1], F32)
    g.negbig = mk("negbig", [128, 1], F32)
    g.cs = mk("cs", [128, NB, 8], F32)
    g.sn = mk("sn", [128, NB, 8], F32)
    g.lbraw = mk("lbraw", [128, 2, 4], F32)
    g.lbv = mk("lbv", [128, 2, 4], F32)
    g.oml = mk("oml", [128, 2, 4], F32)
    g.cb = Buf("consts")
    g.csb = Buf("cs")
    g.snb = Buf("sn")
    g.lbb = Buf("lbraw")
    cb = [g.cb]
    MS(g, "pool", g.onesb[:], 1.0, cb)
    MS(g, "pool", g.negones[:], -1.0, cb)
    MS(g, "pool", g.onesf[:], 1.0, cb)
    MS(g, "pool", g.zc[:], 0.0, cb)
    MS(g, "pool", g.negbig[:], -1e29, cb)
    ASEL(g, g.ident[:], g.onesb[:], [[1, 128]], ALU.is_equal, 0.0, 0, -1, cb, cb)
    ASEL(g, g.negtri[:], g.negones[:], [[-1, 128]], ALU.is_ge, 0.0, 0, 1, cb, cb)
    ASEL(g, g.maskbd[:], g.onesf[:], [[1, 128]], ALU.is_ge, 0.0, 0, -1, cb, cb)
    MS(g, "pool", g.maskbd[0:64, 64:128], 0.0, cb)
    g.ones512 = mk("ones512", [128, 512], BF16)
    g.mlt = mk("mlt", [128, 512], BF16)
    MS(g, "pool", g.ones512[:], 1.0, cb)
    ASEL(g, g.mlt[:], g.ones512[:], [[1, 512]], ALU.is_gt, 0.0, 0, -1, cb, cb)
    g.caus01 = mk("caus01", [128, 128], F32)
    g.negfill = mk("negfill", [128, 128], F32)
    ASEL(g, g.caus01[:], g.onesf[:], [[-1, 128]], ALU.is_ge, 0.0, 0, 1, cb, cb)
    TS(g, "pool", g.negfill[:], g.caus01[:], -1.0, 1e30, ALU.add, ALU.mult, cb, cb)
    MS(g, "pool", g.resetm[:], 1.0, cb)
    MS(g, "pool", g.resetm[:].rearrange("p (c j) -> p c j", j=64)[:, :, 0:1], 0.0, cb)
    DMA(g, "sp", g.cs[:], g.cs_d.rearrange("(b p) i -> p b i", p=128), (), [g.csb])
    DMA(g, "sp", g.sn[:], g.sn_d.rearrange("(b p) i -> p b i", p=128), (), [g.snb])
    DMA(g, "sp", g.lbraw[:], g.lb_d.rearrange("l (h k) -> k l h", k=128), (), [g.lbb], allow_slow_non_contiguous=True)
    MS(g, "dve", g.lbv[:], 0.0, cb)
    TT(g, "dve", g.lbv[:, 1, :], g.lbraw[:, 1, :], g.lbraw[:, 0, :], ALU.subtract, [g.lbb], cb)
    ACT(g, g.lbv[:, 1, :], g.lbv[:, 1, :], AF.Sigmoid, cb, cb)
    TS(g, "dve", g.oml[:], g.lbv[:], -1.0, 1.0, ALU.mult, ALU.add, cb, cb)
    g.P.barrier()


class NormCtx:
    pass


def norm_setup(g, st, gain_d_row):
    c = NormCtx()
    c.gain = sb(g, st, [128, D], F32, "gain")
    c.gb = nb(st, "gain")
    DMA(g, "sp", c.gain[:], gain_d_row.to_broadcast([128, D]), (), [c.gb])
    c.junk = sb(g, st, [128, D], BF16, "junk")
    c.jb = nb(st, "junk")
    c.hbs = [sb(g, st, [128, D], BF16, "hb") for _ in range(2)]
    c.hbb = nbs(st, "hb", 2)
    c.ss = sb(g, st, [128, NB], F32, "ss")
    c.ssb = nbs(st, "ss", NB)
    MS(g, "dve", c.ss[:], 0.0, c.ssb)
    return c


def norm_block(g, c, xb, xbuf, tb, hT, hTb):
    hb, hbuf = c.hbs[tb % 2], c.hbb[tb % 2]
    s1 = c.ss[:, tb:tb + 1]
    ACT(g, c.junk[:], xb[:], AF.Square, [xbuf, c.ssb[tb]], [c.jb, c.ssb[tb]], accum_out=s1)
    ACT(g, s1, s1, AF.Ln, [c.ssb[tb]], [c.ssb[tb]], scale=1.0 / D, bias=EPS)
    ACT(g, s1, s1, AF.Exp, [c.ssb[tb]], [c.ssb[tb]], scale=-0.5)
    STT(g, hb[:], xb[:], s1, c.gain[:], ALU.mult, ALU.mult, [xbuf, c.ssb[tb], c.gb], [hbuf])
    for half in range(2):
        pt, ptb = psb(g)
        MM(g, [trf(pt[:, m * 128:(m + 1) * 128], hb[:, (half * 4 + m) * 128:(half * 4 + m + 1) * 128], g.ident[:])
               for m in range(4)], [hbuf, g.cb], [ptb])
        CP(g, "act" if half == 0 else "dve", hT[:, half * 4:half * 4 + 4, tb * 128:(tb + 1) * 128],
           pt[:, 0:512].rearrange("p (m j) -> p m j", j=128), [ptb], [hTb[tb]])


def norm_T(g, st, src_ap, src_bufs, gain_d_row, hT, hTb):
    c = norm_setup(g, st, gain_d_row)
    xbs = [sb(g, st, [128, D], F32, "xb") for _ in range(4)]
    xbb = nbs(st, "xb", 4)
    for tb in range(NB):
        xb, xbuf = xbs[tb % 4], xbb[tb % 4]
        DMA(g, "sp", xb[:], src_ap[tb * 128:(tb + 1) * 128, :], [src_bufs[tb]], [xbuf])
        norm_block(g, c, xb, xbuf, tb, hT, hTb)


def win_cols(g, l, c0, c1):
    return g.w_in[l].rearrange("(c p) n -> p c n", p=128)[:, :, c0:c1]


def mm_fm(g, ps, M, n, w, wb, col0, hT, hTb, tok0, role_bufs):
    MM(g, [mmf(ps[0:M, 0:n], w[:, c, col0:col0 + M], hT[:, c, tok0:tok0 + n], c == 0, c == DC - 1) for c in range(DC)],
       [wb] + hTb[tok0 // 128:(tok0 + n + 127) // 128], [role_bufs])


def mm_tm(g, ps, N, w, wb, col0, hT, hTb, tb, psbuf):
    MM(g, [mmf(ps[:, 0:N], hT[:, c, tb * 128:(tb + 1) * 128], w[:, c, col0:col0 + N], c == 0, c == DC - 1) for c in range(DC)],
       [wb, hTb[tb]], [psbuf])


def phase_sb(g, l, hT, hTb, osbT, osbb):
    with scope(g) as st:
        ws = []
        for i, c0 in enumerate((C_SQ, C_SK, C_SV)):
            if i == 0 and take_pre(g, ("sq", l)):
                ws.append((g.pre, g.preb))
                continue
            w = sb(g, st, [128, DC, 384], BF16, "wsb")
            wb = nb(st, "wsb%d" % i)
            DMA(g, "pool", w[:], win_cols(g, l, c0, c0 + 384), (), [wb])
            ws.append((w, wb))
        sqT = sb(g, st, [128, 3, T], BF16, "sqT")
        skT = sb(g, st, [128, 3, T], BF16, "skT")
        sqb = nbs(st, "sq", 3, 4)
        skb = nbs(st, "sk", 3, 4)
        k = 0
        for (dst, dstb, (w, wb), scl) in ((sqT, sqb, ws[0], 0.125), (skT, skb, ws[1], 1.0)):
            for hp in range(3):
                for tc in range(4):
                    ps, pb = psf(g, "proj", [0, 1, 2, 3, 4, 5])
                    mm_fm(g, ps, 128, 512, w, wb, hp * 128, hT, hTb, tc * 512, pb)
                    o = dst[:, hp, tc * 512:(tc + 1) * 512]
                    if k % 2 == 0:
                        ACT(g, o, ps[:, :], AF.Copy, [pb], [dstb[hp][tc]], scale=scl)
                    else:
                        TS(g, "dve", o, ps[:, :], scl, None, ALU.mult, None, [pb], [dstb[hp][tc]])
                    k += 1
        svp = [sb(g, st, [128, NB, 384], BF16, "svp") for _ in range(2)]
        svb = nbs(st, "sv", 2, NB)
        for s_ in range(2):
            MS(g, "pool", svp[s_][:].rearrange("p t c -> p (t c)"), 0.0, svb[s_])
        for tb in range(NB):
            ps, pb = psf(g, "proj", [0, 1, 2, 3, 4, 5])
            mm_tm(g, ps, 384, ws[2][0], ws[2][1], 0, hT, hTb, tb, pb)
            src = ps[:, 0:384].rearrange("p (m s d) -> p m s d", s=2, d=64)
            for s_ in range(2):
                dst = svp[s_][:, tb, :].rearrange("p (m s d) -> p m s d", s=2, d=64)
                CP(g, "act" if s_ == 0 else "dve", dst[:, :, s_, :], src[:, :, s_, :], [pb], [svb[s_][tb]])
        prefetch(g, ("wA", l), win_cols(g, l, C_DQ, C_DQ + 512), 512)
        R = 4
        mk2 = lambda shape, dt_, nm: [[sb(g, st, shape, dt_, nm) for _ in range(R)] for _ in range(2)]
        Et, SPt, SPs, At = mk2([128, 512], F32, "Et"), mk2([128, 512], BF16, "SPt"), mk2([128, 512], BF16, "SPs"), mk2([128, 512], BF16, "At")
        Etb, SPb, SPsb, Atb = nbs(st, "Et", 2, R), nbs(st, "SPt", 2, R), nbs(st, "SPs", 2, R), nbs(st, "At", 2, R)
        steps = []
        for m in range(3):
            for qc in range(4):
                for n_, kb in enumerate(range(4 * qc + 3, -1, -1)):
                    steps.append((m, qc, kb, n_))
        stt = {}
        pso = {}

        def info(t):
            m, qc, kb, n_ = steps[t]
            j0 = max(0, kb * 128 - qc * 512)
            return m, qc, kb, n_, j0, kb >= 4 * qc, qc * 512 + j0 - kb * 128, n_ == 0

        def opnds(t):
            m, qc, kb, n_, j0, diag, base, first = info(t)
            kk = [skT[64 * s_:64 * s_ + 64, m, kb * 128:(kb + 1) * 128] for s_ in range(2)]
            qq = [sqT[64 * s_:64 * s_ + 64, m, qc * 512 + j0:(qc + 1) * 512] for s_ in range(2)]
            return kk, qq, skb[m][kb // 4], sqb[m][qc]

        def s1(t):
            m, qc, kb, n_, j0, diag, base, first = info(t)
            kk, qq, rk, rq = opnds(t)
            pz = [psf(g, "sbZ", [0, 1, 2]) for _ in range(2)]
            stt[t] = {"pz": pz}
            for s_ in range(2):
                MM(g, [mmf(pz[s_][0][:, j0:512], kk[s_], qq[s_], True, True)], [rk, rq], [pz[s_][1]])

        def s2(t):
            m, qc, kb, n_, j0, diag, base, first = info(t)
            ib = t % R
            pz = stt[t]["pz"]
            for s_ in range(2):
                ACT(g, Et[s_][ib][:, j0:512], pz[s_][0][:, j0:512], AF.Exp, [pz[s_][1]], [Etb[s_][ib]])

        def s3(t):
            m, qc, kb, n_, j0, diag, base, first = info(t)
            ib = t % R
            for s_ in range(2):
                ACT(g, SPt[s_][ib][:, j0:512], Et[s_][ib][:, j0:512], AF.Ln, [Etb[s_][ib]], [SPb[s_][ib]], bias=1.0)
            if diag:
                for s_ in range(2):
                    S = SPt[s_][ib]
                    assert base == 0
                    TT(g, "pool", S[:, j0:512], S[:, j0:512], g.mlt[:, 0:512 - j0], ALU.mult, [SPb[s_][ib], g.cb], [SPb[s_][ib]])
            if kb > 0:
                for s_ in range(2):
                    S, Sb = SPt[s_][ib], SPb[s_][ib]
                    Sn, Snb = SPs[s_][n_ % R], SPsb[s_][n_ % R]
                    if first:
                        if j0 > 0:
                            MS(g, "pool", Sn[:, 0:j0], 0.0, [Snb])
                        CP(g, "pool", Sn[:, j0:512], S[:, j0:512], [Sb], [Snb])
                    else:
                        So, Sob = SPs[s_][(n_ - 1) % R], SPsb[s_][(n_ - 1) % R]
                        if j0 > 0:
                            CP(g, "pool", Sn[:, 0:j0], So[:, 0:j0], [Sob], [Snb])
                        TT(g, "dve", Sn[:, j0:512], So[:, j0:512], S[:, j0:512], ALU.add, [Sob, Sb], [Snb])

        def s4(t):
            m, qc, kb, n_, j0, diag, base, first = info(t)
            ib = t % R
            kk, qq, rk, rq = opnds(t)
            pc = [psf(g, "sbC", [3, 4]) for _ in range(2)]
            stt[t]["pc"] = pc
            for s_ in range(2):
                S = SPt[s_][ib]
                fns = [mmf(pc[s_][0][:, j0:512], kk[s_], qq[s_], True, False),
                       mmf(pc[s_][0][:, j0:512], g.negtri[:], S[:, j0:512], False, first)]
                rd = [rk, rq, SPb[s_][ib], g.cb]
                if not first:
                    So, Sob = SPs[s_][(n_ - 1) % R], SPsb[s_][(n_ - 1) % R]
                    fns.append(mmf(pc[s_][0][:, j0:512], g.negones[:], So[:, j0:512], False, True))
                    rd.append(Sob)
                MM(g, fns, rd, [pc[s_][1]])

        def s5(t):
            m, qc, kb, n_, j0, diag, base, first = info(t)
            ib = t % R
            pc = stt[t]["pc"]
            for s_ in range(2):
                ACT(g, At[s_][ib][:, j0:512], pc[s_][0][:, j0:512], AF.Exp, [pc[s_][1]], [Atb[s_][ib]])
            for s_ in range(2):
                A, Ab = At[s_][ib], Atb[s_][ib]
                if diag:
                    TT(g, "pool", A[:, j0:512], A[:, j0:512], g.mlt[:, 0:512 - j0], ALU.mult, [Ab, g.cb], [Ab])
                if first and j0 > 0:
                    MS(g, "pool", A[:, 0:j0], 0.0, [Ab])

        def s6(t):
            m, qc, kb, n_, j0, diag, base, first = info(t)
            ib = t % R
            if first:
                pso[(m, qc)] = psf(g, "sbO", [5])
            psO, pOb = pso[(m, qc)]
            for s_ in range(2):
                A, Ab = At[s_][ib], Atb[s_][ib]
                vv = svp[s_][:, kb, m * 128:(m + 1) * 128]
                c0 = 0 if first else j0
                MM(g, [mmf(psO[:, c0:512], vv, A[:, c0:512], first and s_ == 0, kb == 0 and s_ == 1)], [svb[s_][kb], Ab], [pOb])
            if kb == 0:
                CP(g, "act" if (m * 4 + qc) % 2 == 0 else "dve", osbT[:, m, qc * 512:(qc + 1) * 512], psO[:, :], [pOb], [osbb[m][qc]])
            del stt[t]

        nst = len(steps)
        stages = [s1, s2, s3, s4, s5, s6]
        for e in range(nst + len(stages) - 1):
            for k in range(len(stages) - 1, -1, -1):
                t = e - k
                if 0 <= t < nst:
                    stages[k](t)


def phase_dsa(g, l, hT, hTb, odT, odb):
    P = g.P
    with scope(g) as st:
        featT = sb(g, st, [128, 9, T], BF16, "featT")
        fb = nbs(st, "feat", NB)
        dvx = sb(g, st, [128, NB, 128], BF16, "dvx")
        dvb = nbs(st, "dvx", NB)
        sgn = sb(g, st, [128, NB, 8], F32, "sgn")
        sgb = nbs(st, "sgn", NB)
        qkg = sb(g, st, [128, 7, 64], F32, "qkg")
        qkgb = nbs(st, "qkg", 7)
        for hh in range(7):
            src = (g.qn if hh < 6 else g.kn)[l:l + 1, :].to_broadcast([128, 64])
            DMA(g, "sp", qkg[:, hh, :], src, (), [qkgb[hh]])
        with scope(g) as s2:
            wB = sb(g, s2, [128, DC, 512], BF16, "wB")
            wC = sb(g, s2, [128, DC, 72], BF16, "wC")
            wBb, wCb = nb(s2, "wB"), nb(s2, "wC")
            if take_pre(g, ("wA", l)):
                wA, wAb = g.pre, g.preb
            else:
                wA = sb(g, s2, [128, DC, 512], BF16, "wA")
                wAb = nb(s2, "wA")
                DMA(g, "pool", wA[:], win_cols(g, l, C_DQ, C_DQ + 512), (), [wAb])
            DMA(g, "pool", wB[:], win_cols(g, l, C_IQ, C_IQ + 512), (), [wBb])
            DMA(g, "pool", wC[:], win_cols(g, l, C_IK, C_IK + 72), (), [wCb])
            NR = 4
            tq = [sb(g, s2, [128, 18, 64], F32, "tokq") for _ in range(NR)]
            tqb = nbs(s2, "tokq", NR)
            tbq = [sb(g, s2, [128, 18, 64], BF16, "tokb") for _ in range(3)]
            tbb = nbs(s2, "tokb", 3)
            sqt = [sb(g, s2, [128, 448], F32, "sqt") for _ in range(2)]
            sqtb = nbs(s2, "sqt", 2)
            smq = [sb(g, s2, [128, 32], F32, "small") for _ in range(2)]
            smqb = nbs(s2, "small", 2)
            rt = [sb(g, s2, [128, 18, 8], F32, "ropet") for _ in range(4)]
            rtb = nbs(s2, "ropet", 4)
            pst = {}

            def p1(tb):
                MS(g, "pool", dvx[:, tb, 64:128], 1.0, [dvb[tb]])
                tk, tkb = tq[tb % NR], tqb[tb % NR]
                MS(g, "pool", tk[:, 7, :], 0.0, [tkb])
                MS(g, "pool", tk[:, 17, :], 0.0, [tkb])
                psA, pAb = psf(g, "dA", [0, 1])
                psBq, pBb = psf(g, "dB", [2, 3])
                psC, pCb = psf(g, "dC", [4, 5])
                pst[tb] = (psA, pAb, psBq, pBb, psC, pCb)
                mm_tm(g, psA, 512, wA, wAb, 0, hT, hTb, tb, pAb)
                mm_tm(g, psBq, 512, wB, wBb, 0, hT, hTb, tb, pBb)
                mm_tm(g, psC, 72, wC, wCb, 0, hT, hTb, tb, pCb)

            def p2(tb):
                psA, pAb, psBq, pBb, psC, pCb = pst.pop(tb)
                tk, tkb = tq[tb % NR], tqb[tb % NR]
                sq_, sq_b = sqt[tb % 2], sqtb[tb % 2]
                sm, smb = smq[tb % 2], smqb[tb % 2]
                ACT(g, sq_[:], psA[:, 0:448], AF.Square, [pAb], [sq_b])
                ss = sm[:, 0:7]
                P.op("dve", lambda e, ss=ss, sq_=sq_: e.tensor_reduce(out=ss, in_=sq_[:].rearrange("p (h d) -> p h d", d=64), axis=AX.X,
                                                                      op=ALU.add), [sq_b], [smb], 448)
                ACT(g, ss, ss, AF.Ln, [smb], [smb], scale=1.0 / 64, bias=EPS)
                ACT(g, ss, ss, AF.Exp, [smb], [smb], scale=-0.5)
                aw = sm[:, 8:16]
                TS(g, "dve", sgn[:, tb, :], psC[:, 64:72], 0.0, 2.0, ALU.is_gt, ALU.mult, [pCb], [sgb[tb]])
                TS(g, "dve", sgn[:, tb, :], sgn[:, tb, :], -1.0, 0.0, ALU.add, ALU.add, [sgb[tb]], [sgb[tb]])
                STT(g, aw, psC[:, 64:72], IDX_SCALE, sgn[:, tb, :], ALU.mult, ALU.mult, [pCb, sgb[tb]], [smb])
                TT(g, "dve", tk[:, 8:16, :], psBq[:, :].rearrange("p (h d) -> p h d", d=64),
                   aw.unsqueeze(2).to_broadcast([128, 8, 64]), ALU.mult, [pBb, smb], [tkb])
                CP(g, "act", tk[:, 16, :], psC[:, 0:64], [pCb], [tkb])
                CP(g, "act", dvx[:, tb, 0:64], psA[:, 448:512], [pAb], [dvb[tb]])
                TT(g, "dve", tk[:, 0:7, :], psA[:, 0:448].rearrange("p (h d) -> p h d", d=64),
                   ss.unsqueeze(2).to_broadcast([128, 7, 64]), ALU.mult, [pAb, smb], [tkb])
                TT(g, "dve", tk[:, 0:7, :], tk[:, 0:7, :], qkg[:], ALU.mult, [tkb] + qkgb, [tkb])

            def p3(tb):
                tk, tkb = tq[tb % NR], tqb[tb % NR]
                x1, x2 = tk[:, :, 0:8], tk[:, :, 8:16]
                cb_ = g.cs[:, tb, :].unsqueeze(1).to_broadcast([128, 18, 8])
                sb_ = g.sn[:, tb, :].unsqueeze(1).to_broadcast([128, 18, 8])
                TT(g, "dve", rt[0][:], x1, cb_, ALU.mult, [tkb, g.csb], [rtb[0]])
                TT(g, "pool", rt[1][:], x2, sb_, ALU.mult, [tkb, g.snb], [rtb[1]])
                TT(g, "dve", rt[2][:], x2, cb_, ALU.mult, [tkb, g.csb], [rtb[2]])
                TT(g, "pool", rt[3][:], x1, sb_, ALU.mult, [tkb, g.snb], [rtb[3]])
                TT(g, "dve", x1, rt[0][:], rt[1][:], ALU.subtract, [rtb[0], rtb[1]], [tkb])
                TT(g, "pool", x2, rt[2][:], rt[3][:], ALU.add, [rtb[2], rtb[3]], [tkb])

            def p4(tb):
                tk, tkb = tq[tb % NR], tqb[tb % NR]
                tkh, tkhb = tbq[tb % 3], tbb[tb % 3]
                CP(g, "act", tkh[:], tk[:], [tkb], [tkhb])
                CP(g, "pool", tkh[:, 7, :], tkh[:, 6, :], [tkhb], [tkhb])
                CP(g, "pool", tkh[:, 17, :], tkh[:, 16, :], [tkhb], [tkhb])

            def p5(tb):
                tkh, tkhb = tbq[tb % 3], tbb[tb % 3]
                flat = tkh[:].rearrange("p h d -> p (h d)")
                pt, ptb = psb(g)
                MM(g, [trf(pt[:, m * 128:(m + 1) * 128], flat[:, m * 128:(m + 1) * 128], g.ident[:]) for m in range(8)],
                   [tkhb, g.cb], [ptb])
                CP(g, "dve", featT[:, 0:8, tb * 128:(tb + 1) * 128], pt[:, :].rearrange("p (m j) -> p m j", j=128), [ptb], [fb[tb]])
                pt2, ptb2 = psb(g)
                MM(g, [trf(pt2[:, 0:128], flat[:, 1024:1152], g.ident[:])], [tkhb, g.cb], [ptb2])
                CP(g, "act", featT[:, 8, tb * 128:(tb + 1) * 128], pt2[:, 0:128], [ptb2], [fb[tb]])

            stages = [p1, p2, p3, p4, p5]
            for e in range(NB + len(stages) - 1):
                for k in range(len(stages) - 1, -1, -1):
                    t_ = e - k
                    if 0 <= t_ < NB:
                        stages[k](t_)
        prefetch(g, ("hihg", l), win_cols(g, l, C_HI, C_HI + 512), 512)
        sc = [sb(g, st, [128, T], F32, "sc") for _ in range(4)]
        scb = nbs(st, "sc", 4, 4)
        junk = sb(g, st, [128, T], BF16, "junk")
        maskq = [sb(g, st, [128, T], BF16, "maskq") for _ in range(2)]
        mqb = nbs(st, "maskq", 2)
        maskT = sb(g, st, [128, NB, 512], BF16, "maskT")
        mTb = nbs(st, "maskT", 4)
        rj = [sb(g, st, [128, 512], BF16, "rj") for _ in range(4)]
        rjb = nbs(st, "rj", 4)
        dg = [sb(g, st, [128, 8, 128], BF16, "dg") for _ in range(2)]
        dgb = nbs(st, "dg", 2)
        sm = [sb(g, st, [128, 8 + 2 * N_BISECT], F32, "bis") for _ in range(2)]
        smb = nbs(st, "bis", 2)
        cvec = sb(g, st, [128, N_BISECT], F32, "cvec")
        c255 = sb(g, st, [128, 1], F32, "c255")
        cvb = nb(st, "cvec")
        for n_ in range(N_BISECT):
            MS(g, "pool", cvec[:, n_:n_ + 1], 2.0 ** -(n_ + 1), [cvb])
        MS(g, "pool", c255[:], 255.5, [cvb])
        Pt = [sb(g, st, [128, 512], BF16, "Pt") for _ in range(4)]
        Ptb = nbs(st, "Pt", 4)
        Pm = [sb(g, st, [128, 512], BF16, "Pm") for _ in range(4)]
        Pmb = nbs(st, "Pm", 4)
        rs = sb(g, st, [64, 512], F32, "rs")
        rsb = nb(st, "rs")
        cnt_ = {"ri": 0, "pi": 0, "pm": 0}

        def idx_blocks(blocks):
            tiles = []
            for i in blocks:
                nk = (i + 1) * 128
                d_, d_b = dg[i % 2], dgb[i % 2]
                for j in range(8):
                    TS(g, "pool", d_[:, j, :], g.ident[:], sgn[:, i, j:j + 1], None, ALU.mult, None, [g.cb, sgb[i]], [d_b])
                for kc in range((nk + 511) // 512):
                    n = min(512, nk - kc * 512)
                    for j in range(8):
                        tiles.append((i, kc, n, j))
            stt = {}

            def s1(t):
                i, kc, n, j = tiles[t]
                po = 64 * (j % 2)
                psZ, pZb = psf(g, "ixZ", [0, 1, 2])
                stt[t] = [psZ, pZb]
                MM(g, [mmf(psZ[:, 0:n], featT[po:po + 64, 4 + j // 2, i * 128:(i + 1) * 128],
                           featT[po:po + 64, 8, kc * 512:kc * 512 + n], True, True)],
                   [fb[i]] + fb[kc * 4:(kc * 512 + n) // 128], [pZb])

            def s2(t):
                i, kc, n, j = tiles[t]
                psZ, pZb = stt[t]
                r_, r_b = rj[cnt_["ri"] % 4], rjb[cnt_["ri"] % 4]
                cnt_["ri"] += 1
                stt[t] += [r_, r_b]
                ACT(g, r_[:, 0:n], psZ[:, 0:n], AF.Relu, [pZb], [r_b])

            def s3(t):
                i, kc, n, j = tiles[t]
                r_, r_b = stt[t][2], stt[t][3]
                if j == 0:
                    cnt_["psS"] = psf(g, "ixS", [3, 4])
                psS, pSb = cnt_["psS"]
                d_, d_b = dg[i % 2], dgb[i % 2]
                MM(g, [mmf(psS[:, 0:n], d_[:, j, :], r_[:, 0:n], j == 0, j == 7)], [d_b, r_b], [pSb])
                if j == 7:
                    CP(g, "act", sc[i % 4][:, kc * 512:kc * 512 + n], psS[:, 0:n], [pSb], [scb[i % 4][kc]])
                del stt[t]

            pipeline([s1, s2, s3], len(tiles))

        def bis_pair(p):
            blocks = [2 * p, 2 * p + 1]
            st_ = []
            for bi, i in enumerate(blocks):
                nk = (i + 1) * 128
                s_, s_b = sc[i % 4], scb[i % 4]
                nkc = (nk + 511) // 512
                srd = s_b[0:nkc]
                m_, m_b = sm[bi], smb[bi]
                rmax, rmin, step0, mid, cntv, tt = (m_[:, c:c + 1] for c in range(6))
                stepc = m_[:, 8:8 + N_BISECT]
                if nk > 256:
                    P.op("dve", lambda e, s_=s_, nk=nk, rmax=rmax: e.tensor_reduce(out=rmax, in_=s_[:, 0:nk], axis=AX.X, op=ALU.max), srd, [m_b], nk)
                    P.op("dve", lambda e, s_=s_, nk=nk, rmin=rmin: e.tensor_reduce(out=rmin, in_=s_[:, 0:nk], axis=AX.X, op=ALU.min), srd, [m_b], nk)
                dsl = s_[:, i * 128:(i + 1) * 128]
                TT(g, "pool", dsl, dsl, g.caus01[:], ALU.mult, [s_b[i // 4], g.cb], [s_b[i // 4]])
                TT(g, "pool", dsl, dsl, g.negfill[:], ALU.add, [s_b[i // 4], g.cb], [s_b[i // 4]])
                st_.append((i, nk, s_, srd, m_, m_b, rmax, rmin, step0, mid, cntv, tt, stepc))
            act = [x for x in st_ if x[1] > 256]
            for (i, nk, s_, srd, m_, m_b, rmax, rmin, step0, mid, cntv, tt, stepc) in act:
                TT(g, "dve", step0, rmax, rmin, ALU.subtract, [m_b], [m_b])
            for (i, nk, s_, srd, m_, m_b, rmax, rmin, step0, mid, cntv, tt, stepc) in act:
                TS(g, "dve", stepc, cvec[:], step0, None, ALU.mult, None, [m_b, cvb], [m_b])
            for (i, nk, s_, srd, m_, m_b, rmax, rmin, step0, mid, cntv, tt, stepc) in act:
                TS(g, "dve", mid, stepc[:, 0:1], rmin, g.zc[:, 0:1], ALU.add, ALU.add, [m_b, g.cb], [m_b])
            for n_ in range(N_BISECT):
                for (i, nk, s_, srd, m_, m_b, rmax, rmin, step0, mid, cntv, tt, stepc) in act:
                    TS(g, "dve", junk[:, 0:nk], s_[:, 0:nk], mid, g.zc[:, 0:1], ALU.is_ge, ALU.add, srd + [m_b, g.cb], [m_b], accum_out=cntv)
                for (i, nk, s_, srd, m_, m_b, rmax, rmin, step0, mid, cntv, tt, stepc) in act:
                    TS(g, "dve", tt, cntv, c255[:, 0:1], stepc[:, n_:n_ + 1], ALU.is_ge, ALU.mult, [m_b, cvb], [m_b])
                for (i, nk, s_, srd, m_, m_b, rmax, rmin, step0, mid, cntv, tt, stepc) in act:
                    nn = min(n_ + 1, N_BISECT - 1)
                    TS(g, "dve", mid, tt, stepc[:, nn:nn + 1], mid, ALU.subtract, ALU.add, [m_b], [m_b])
            for (i, nk, s_, srd, m_, m_b, rmax, rmin, step0, mid, cntv, tt, stepc) in st_:
                thr = mid if nk > 256 else g.negbig[:, 0:1]
                mq, mq_b = maskq[i % 2], mqb[i % 2]
                TS(g, "dve", mq[:, 0:nk], s_[:, 0:nk], thr, None, ALU.is_ge, None, srd + [m_b, g.cb], [mq_b])

        def mT_pair(p):
            for i in (2 * p, 2 * p + 1):
                ii = i % 4
                mq, mq_b = maskq[i % 2], mqb[i % 2]
                for k0 in range(0, i + 1, 8):
                    k1 = min(i + 1, k0 + 8)
                    pt, ptb = psb(g)
                    MM(g, [trf(pt[:, (kb - k0) * 128:(kb - k0 + 1) * 128], mq[:, kb * 128:(kb + 1) * 128], g.ident[:])
                           for kb in range(k0, k1)], [mq_b, g.cb], [ptb])
                    CP(g, "act", maskT[:, k0:k1, ii * 128:(ii + 1) * 128],
                       pt[:, 0:(k1 - k0) * 128].rearrange("p (m j) -> p m j", j=128), [ptb], [mTb[ii]])

        def att_chunk(qc):
            last = 4 * qc + 3
            tiles = [(h, kb) for h in range(6) for kb in range(last + 1)]
            stt = {}
            pso = {}

            def s1(t):
                h, kb = tiles[t]
                hp, po = h // 2, 64 * (h % 2)
                j0 = max(0, kb * 128 - qc * 512)
                psL, pLb = psf(g, "dsL", [0, 1, 2])
                stt[t] = [psL, pLb]
                MM(g, [mmf(psL[:, j0:512], featT[po:po + 64, 3, kb * 128:(kb + 1) * 128],
                           featT[po:po + 64, hp, qc * 512 + j0:(qc + 1) * 512], True, True)],
                   [fb[kb]] + fb[qc * 4:qc * 4 + 4], [pLb])

            def s2(t):
                h, kb = tiles[t]
                j0 = max(0, kb * 128 - qc * 512)
                psL, pLb = stt[t][0], stt[t][1]
                pi = cnt_["pi"]
                cnt_["pi"] += 1
                p_, p_b = Pt[pi % 4], Ptb[pi % 4]
                stt[t] += [p_, p_b]
                ACT(g, p_[:, j0:512], psL[:, j0:512], AF.Exp, [pLb], [p_b], scale=0.125)

            def s3(t):
                h, kb = tiles[t]
                j0 = max(0, kb * 128 - qc * 512)
                p_, p_b = stt[t][2], stt[t][3]
                pm = cnt_["pm"]
                cnt_["pm"] += 1
                m_, m_b = Pm[pm % 4], Pmb[pm % 4]
                stt[t] += [m_, m_b]
                TT(g, "pool" if t % 3 == 0 else "dve", m_[:, j0:512], p_[:, j0:512], maskT[:, kb, j0:512], ALU.mult,
                   [p_b] + mTb[j0 // 128:4], [m_b])

            def s4(t):
                h, kb = tiles[t]
                hp, po = h // 2, 64 * (h % 2)
                j0 = max(0, kb * 128 - qc * 512)
                m_, m_b = stt[t][4], stt[t][5]
                if kb == 0:
                    pso[h] = psf(g, "dsO", [4, 5])
                psO, pOb = pso[h]
                MM(g, [mmf(psO[:, j0:512], dvx[:, kb, :], m_[:, j0:512], kb == 0, kb == last)], [dvb[kb], m_b], [pOb])
                if kb == last:
                    ACT(g, rs[0:64, :], psO[64:128, :], AF.Ln, [pOb], [rsb])
                    ACT(g, rs[0:64, :], rs[0:64, :], AF.Exp, [rsb], [rsb], scale=-1.0)
                    TT(g, "dve", odT[po:po + 64, hp, qc * 512:(qc + 1) * 512], psO[0:64, :], rs[0:64, :], ALU.mult, [pOb, rsb],
                       [odb[hp][qc]])
                del stt[t]

            pipeline([s1, s2, s3, s4], len(tiles))

        idx_blocks([14, 15])
        for p in range(7, -1, -1):
            if p > 0:
                idx_blocks([2 * p - 2, 2 * p - 1])
            bis_pair(p)
            mT_pair(p)
            if p % 2 == 0:
                att_chunk(p // 2)


def phase_hgrn(g, l, hT, hTb, ohT, ohb):
    P = g.P
    import os
    if int(os.environ.get("HGL", "9")) == 0:
        return
    with scope(g) as st:
        hi_tm = sb(g, st, [128, NB, 256], BF16, "hi_tm")
        hib = nbs(st, "hi", NB)
        hgs = sb(g, st, [128, NB, 256], BF16, "hgs")
        hgb = nbs(st, "hgs", NB)
        onb = sb(g, st, [128, 64], F32, "onorm")
        onbb = nb(st, "onorm")
        DMA(g, "sp", onb[:], g.onorm[l:l + 1, :].to_broadcast([128, 64]), (), [onbb])
        with scope(g) as s2:
            if take_pre(g, ("hihg", l)):
                w, wb = g.pre, g.preb
            else:
                w = sb(g, s2, [128, DC, 512], BF16, "whihg")
                wb = nb(s2, "whihg")
                DMA(g, "pool", w[:], win_cols(g, l, C_HI, C_HI + 512), (), [wb])
            sgs = [sb(g, s2, [128, 256], F32, "sgs") for _ in range(2)]
            sgsb = nbs(s2, "sgs", 2)
            for tb in range(NB):
                ps, pb = psf(g, "proj", [0, 1, 2, 3, 4, 5])
                hgv = int(os.environ.get("HGV", "15"))
                if hgv & 8:
                    mm_tm(g, ps, 512, w, wb, 0, hT, hTb, tb, pb)
                if hgv & 1:
                    CP(g, "dve", hi_tm[:, tb, :], ps[:, 0:256], [pb], [hib[tb]])
                sgt, sgtb = sgs[tb % 2], sgsb[tb % 2]
                if hgv & 2:
                    ACT(g, sgt[:], ps[:, 256:512], AF.Exp if hgv & 16 else AF.Sigmoid, [pb], [sgtb])
                if hgv & 4:
                    TT(g, "dve", hgs[:, tb, :], ps[:, 256:512], sgt[:], ALU.mult, [pb, sgtb], [hgb[tb]])
        with scope(g) as s3:
            NH = 4
            R = 4
            qtT = sb(g, s3, [128, NH, T], BF16, "qtT")
            ktT = sb(g, s3, [128, NH, T], BF16, "ktT")
            qtb = nbs(s3, "qt", NH)
            ktb = nbs(s3, "kt", NH)
            kt_tm = sb(g, s3, [128, NB, NH * 128], BF16, "kt_tm")
            kttb = nbs(s3, "kttm", NH)
            t1 = sb(g, s3, [128, T], F32, "t1")
            t2 = sb(g, s3, [128, T], F32, "t2")
            t3 = sb(g, s3, [128, T], F32, "t3")
            t1h, t2h, t3h = nbs(s3, "t1", 4), nbs(s3, "t2", 4), nbs(s3, "t3", 4)
            ebl = sb(g, s3, [128, NH, 32], F32, "ebl")
            eblb = nb(s3, "ebl")
            W = [sb(g, s3, [128, NH, 64], F32, "W") for _ in range(2)]
            Wb = nbs(s3, "W", 2)
            Sbf = [sb(g, s3, [128, NH, 64], BF16, "Sbf") for _ in range(R)]
            Sbfb = nbs(s3, "Sbf", R)
            attm = [sb(g, s3, [128, NH, 128], BF16, "attm") for _ in range(3)]
            attb = nbs(s3, "attm", 3)
            o_tm = [sb(g, s3, [128, NH * 64], F32, "o_tm") for _ in range(3)]
            otb = nbs(s3, "otm", 3)
            osq = sb(g, s3, [128, NH * 64], F32, "osq")
            osqb = nb(s3, "osq")
            og = [sb(g, s3, [128, NH * 64], BF16, "og") for _ in range(2)]
            ogb = nbs(s3, "og", 2)
            sm = [sb(g, s3, [128, 4], F32, "hsm") for _ in range(2)]
            smb = nbs(s3, "hsm", 2)
            wq = [sb(g, s3, [128, DC, 128], BF16, "wq") for _ in range(2)]
            wqb = nbs(s3, "wq", 2)
            wf = [sb(g, s3, [128, DC, 128], BF16, "wf") for _ in range(2)]
            wfb = nbs(s3, "wf", 2)

            def load_head(hd):
                DMA(g, "pool", wf[hd % 2][:], win_cols(g, l, C_HF + hd * 128, C_HF + hd * 128 + 128), (), [wfb[hd % 2]])
                DMA(g, "pool", wq[hd % 2][:], win_cols(g, l, C_HQ + hd * 128, C_HQ + hd * 128 + 128), (), [wqb[hd % 2]])

            def kt_transposes(hd):
                for k0 in (0, 8):
                    pt, ptb = psb(g)
                    MM(g, [trf(pt[:, m * 128:(m + 1) * 128], ktT[:, hd, (k0 + m) * 128:(k0 + m + 1) * 128], g.ident[:])
                           for m in range(8)], [ktb[hd], g.cb], [ptb])
                    CP(g, "act" if k0 == 0 else "dve", kt_tm[:, k0:k0 + 8, hd * 128:(hd + 1) * 128],
                       pt[:, :].rearrange("p (m j) -> p m j", j=128), [ptb], [kttb[hd]])

            load_head(0)
            for hd in range(NH):
                if hd + 1 < NH:
                    load_head(hd + 1)
                w_f, w_fb, w_q, w_qb = wf[hd % 2], wfb[hd % 2], wq[hd % 2], wqb[hd % 2]
                NQ = 4
                HS = [slice(q_ * (T // NQ), (q_ + 1) * (T // NQ)) for q_ in range(NQ)]
                for tc in range(4):
                    ps, pb = psf(g, "proj", [0, 1, 2, 3, 4, 5])
                    mm_fm(g, ps, 128, 512, w_f, w_fb, 0, hT, hTb, tc * 512, pb)
                    ACT(g, t1[:, tc * 512:(tc + 1) * 512], ps[:, :], AF.Sigmoid, [pb], [t1h[tc]])
                for hf in range(NQ):
                    TS(g, "dve", t1[:, HS[hf]], t1[:, HS[hf]], g.oml[:, l, hd:hd + 1], g.lbv[:, l, hd:hd + 1], ALU.mult, ALU.add,
                       [t1h[hf], g.cb], [t1h[hf]])
                for hf in range(NQ):
                    ACT(g, t2[:, HS[hf]], t1[:, HS[hf]], AF.Copy, [t1h[hf]], [t2h[hf]], scale=-1.0, bias=1.0)
                for hf in range(NQ):
                    TS(g, "dve", t1[:, HS[hf]], t1[:, HS[hf]], F_MIN, None, ALU.max, None, [t1h[hf]], [t1h[hf]])
                for hf in range(NQ):
                    ACT(g, t1[:, HS[hf]], t1[:, HS[hf]], AF.Ln, [t1h[hf]], [t1h[hf]])
                for hf in range(NQ):
                    P.op("dve", lambda e, hf=hf: e.tensor_tensor_scan(out=t3[:, HS[hf]], data0=g.resetm[:, HS[hf]], data1=t1[:, HS[hf]],
                                                                     initial=0.0, op0=ALU.mult, op1=ALU.add),
                         [t1h[hf], g.cb], [t3h[hf]], 2 * T // NQ)
                for hf in range(NQ):
                    TS(g, "dve", t3[:, HS[hf]], t3[:, HS[hf]], -80.0, None, ALU.max, None, [t3h[hf]], [t3h[hf]])
                for hf in range(NQ):
                    ACT(g, t1[:, HS[hf]], t3[:, HS[hf]], AF.Exp, [t3h[hf]], [t1h[hf]])
                for hf in range(NQ):
                    CP(g, "pool", ebl[:, hd, hf * (32 // NQ):(hf + 1) * (32 // NQ)].unsqueeze(2),
                       t1[:, HS[hf]].rearrange("p (c j) -> p c j", j=64)[:, :, 63:64], [t1h[hf]], [eblb])
                for hf in range(NQ):
                    ACT(g, t3[:, HS[hf]], t3[:, HS[hf]], AF.Exp, [t3h[hf]], [t3h[hf]], scale=-1.0)
                for hf in range(NQ):
                    TT(g, "dve", ktT[:, hd, HS[hf]], t2[:, HS[hf]], t3[:, HS[hf]], ALU.mult, [t2h[hf], t3h[hf]], [ktb[hd]])
                for tc in range(4):
                    ps, pb = psf(g, "proj", [0, 1, 2, 3, 4, 5])
                    mm_fm(g, ps, 128, 512, w_q, w_qb, 0, hT, hTb, tc * 512, pb)
                    ACT(g, t2[:, tc * 512:(tc + 1) * 512], ps[:, :], AF.Sigmoid, [pb], [t2h[tc]])
                    TT(g, "dve", t2[:, tc * 512:(tc + 1) * 512], ps[:, :], t2[:, tc * 512:(tc + 1) * 512], ALU.mult, [pb, t2h[tc]],
                       [t2h[tc]])
                for hf in range(NQ):
                    TT(g, "dve", qtT[:, hd, HS[hf]], t2[:, HS[hf]], t1[:, HS[hf]], ALU.mult, [t2h[hf], t1h[hf]], [qtb[hd]])
                if hd > 0:
                    kt_transposes(hd - 1)

            kt_transposes(NH - 1)
            NCH = 32
            xps = {}

            def o_transposes(tb):
                og_, og_b = og[tb % 2], ogb[tb % 2]
                pt, ptb = psb(g)
                MM(g, [trf(pt[:, m * 128:(m + 1) * 128], og_[:, m * 128:(m + 1) * 128], g.ident[:]) for m in range(2)],
                   [og_b, g.cb], [ptb])
                CP(g, "act", ohT[:, 0:2, tb * 128:(tb + 1) * 128], pt[:, 0:256].rearrange("p (m j) -> p m j", j=128), [ptb],
                   [ohb[0][tb // 4], ohb[1][tb // 4]])

            def emit_X(c):
                tb, pr = c // 2, (c % 2) * 64
                psX, pXb = psf(g, "hgX", [0, 1, 2])
                xps[c] = (psX, pXb)
                MM(g, [mmf(psX[:, hh * 64:(hh + 1) * 64], kt_tm[pr:pr + 64, tb, hh * 128:(hh + 1) * 128],
                           hi_tm[pr:pr + 64, tb, hh * 64:(hh + 1) * 64], True, True) for hh in range(NH)], kttb + [hib[tb]], [pXb])

            def emit_A(tb):
                psA, pAb = psf(g, "hgA", [3])
                MM(g, [mmf(psA[:, hh * 128:(hh + 1) * 128], ktT[:, hh, tb * 128:(tb + 1) * 128],
                           qtT[:, hh, tb * 128:(tb + 1) * 128], True, True) for hh in range(NH)], ktb + qtb, [pAb])
                TT(g, "dve", attm[tb % 3][:], psA[:, :].rearrange("p (h t) -> p h t", t=128),
                   g.maskbd[:].unsqueeze(1).to_broadcast([128, NH, 128]), ALU.mult, [pAb, g.cb], [attb[tb % 3]])

            emit_X(0)
            emit_X(1)
            emit_A(0)
            CP(g, "dve", W[0][:], xps[0][0][:, 0:NH * 64].rearrange("p (h v) -> p h v", v=64), [xps[0][1]], [Wb[0]])
            MS(g, "pool", Sbf[0][:], 0.0, [Sbfb[0]])
            for c in range(NCH):
                tb, half = c // 2, c % 2
                pr = half * 64
                if c + 2 < NCH:
                    emit_X(c + 2)
                if half == 0 and tb + 1 < NB:
                    emit_A(tb + 1)
                if c + 1 < NCH:
                    eb_ = ebl[:, :, c:c + 1].to_broadcast([128, NH, 64])
                    TT(g, "pool", Sbf[(c + 1) % R][:], W[c % 2][:], eb_, ALU.mult, [Wb[c % 2], eblb], [Sbfb[(c + 1) % R]])
                    psX, pXb = xps.pop(c + 1)
                    for hh in range(NH):
                        STT(g, W[(c + 1) % 2][:, hh, :], W[c % 2][:, hh, :], ebl[:, hh, c:c + 1], psX[:, hh * 64:(hh + 1) * 64],
                            ALU.mult, ALU.add, [Wb[c % 2], eblb, pXb], [Wb[(c + 1) % 2]])
                am, amb = attm[tb % 3], attb[tb % 3]
                ot, otbuf = o_tm[tb % 3], otb[tb % 3]
                psO, pOb = psf(g, "hgO", [4, 5])
                fns = []
                for hh in range(NH):
                    fns.append(mmf(psO[0:64, hh * 64:(hh + 1) * 64], am[pr:pr + 64, hh, pr:pr + 64],
                                   hi_tm[pr:pr + 64, tb, hh * 64:(hh + 1) * 64], True, False))
                    fns.append(mmf(psO[0:64, hh * 64:(hh + 1) * 64], qtT[:, hh, c * 64:(c + 1) * 64], Sbf[c % R][:, hh, :], False, True))
                MM(g, fns, [amb, hib[tb], Sbfb[c % R]] + qtb, [pOb])
                CP(g, "act", ot[pr:pr + 64, :], psO[0:64, 0:NH * 64], [pOb], [otbuf])
                if half == 1:
                    sm_, sm_b = sm[tb % 2], smb[tb % 2]
                    og_, og_b = og[tb % 2], ogb[tb % 2]
                    TT(g, "pool", osq[:], ot[:], ot[:], ALU.mult, [otbuf], [osqb])
                    ss = sm_[:, 0:NH]
                    P.op("dve", lambda e, ss=ss: e.tensor_reduce(out=ss, in_=osq[:].rearrange("p (h d) -> p h d", d=64), axis=AX.X,
                                                                 op=ALU.add), [osqb], [sm_b], NH * 64)
                    ACT(g, ss, ss, AF.Ln, [sm_b], [sm_b], scale=1.0 / 64, bias=EPS)
                    ACT(g, ss, ss, AF.Exp, [sm_b], [sm_b], scale=-0.5)
                    o3 = ot[:].rearrange("p (h d) -> p h d", d=64)
                    TT(g, "dve", o3, o3, ss.unsqueeze(2).to_broadcast([128, NH, 64]), ALU.mult, [otbuf, sm_b], [otbuf])
                    TT(g, "dve", o3, o3, onb[:].unsqueeze(1).to_broadcast([128, NH, 64]), ALU.mult, [otbuf, onbb], [otbuf])
                    TT(g, "dve", og_[:], ot[:], hgs[:, tb, :], ALU.mult, [otbuf, hgb[tb]], [og_b])
                    if tb > 0:
                        o_transposes(tb - 1)
                    if tb == NB - 1:
                        o_transposes(tb)


def phase_mix(g, l, hT, hTb, osbT, osbb, odT, odb, ohT, ohb, src_ap, src_bufs):
    P = g.P
    with scope(g) as st:
        mixT = sb(g, st, [128, DC, T], BF16, "mixT")
        mxb = nbs(st, "mix", DC, 4)
        wg = [sb(g, st, [128, DC, 3, 256], BF16, "wg") for _ in range(2)]
        wgb = nbs(st, "wg", 2, 3)
        wy = sb(g, st, [128, 8, D], BF16, "wy")
        wyb = nbs(st, "wy", 3)
        sg = [sb(g, st, [128, 512], F32, "sg") for _ in range(2)]
        sgb = nbs(st, "sg", 2)
        acc = [sb(g, st, [128, 512], F32, "acc") for _ in range(2)]
        accb = nbs(st, "acc", 2)
        tm = [sb(g, st, [128, 512], F32, "tm") for _ in range(2)]
        tmb = nbs(st, "tm", 2)
        k = 0

        def load_pair(dp):
            w_, w_b = wg[dp % 2], wgb[dp % 2]
            for gi in range(3):
                c0 = C_G + gi * 1024 + dp * 256
                DMA(g, "pool", w_[:, :, gi, :], win_cols(g, l, c0, c0 + 256), (), [w_b[gi]])

        load_pair(0)
        DMA(g, "pool", wy[:, 0:3, :], g.w_sb[l].rearrange("(c p) n -> p c n", p=128), (), [wyb[0]])
        DMA(g, "pool", wy[:, 3:6, :], g.w_dsa[l].rearrange("(c p) n -> p c n", p=128), (), [wyb[1]])
        DMA(g, "pool", wy[:, 6:8, :], g.w_hg[l].rearrange("(c p) n -> p c n", p=128), (), [wyb[2]])
        wo = sb(g, st, [128, DC, D], BF16, "wo")
        wob = nbs(st, "wo", 2)
        for dc in range(DC):
            dp, do = dc // 2, (dc % 2) * 128
            w_, w_b = wg[dp % 2], wgb[dp % 2]
            y_, y_b = wy, wyb
            if dc % 2 == 0:
                if dp + 1 < DC // 2:
                    load_pair(dp + 1)
                else:
                    for nh in range(2):
                        DMA(g, "pool", wo[:, :, nh * 512:(nh + 1) * 512],
                            g.w_out[l].rearrange("(c p) n -> p c n", p=128)[:, :, nh * 512:(nh + 1) * 512], (), [wob[nh]])
                    prefetch(g, ("wu", l, 0), g.w_up[l].rearrange("(c p) n -> p c n", p=128)[:, :, 0:512], 512)
            for tc in range(4):
                a_, a_b = acc[k % 2], accb[k % 2]
                for gi, (oT, obufs, nch, c0) in enumerate(((osbT, osbb, 3, 0), (odT, odb, 3, 3), (ohT, ohb, 2, 6))):
                    psG, pGb = psf(g, "mxG", [0, 1, 2])
                    MM(g, [mmf(psG[:, :], w_[:, c, gi, do:do + 128], hT[:, c, tc * 512:(tc + 1) * 512], c == 0, c == DC - 1) for c in range(DC)],
                       [w_b[gi]] + hTb[tc * 4:tc * 4 + 4], [pGb])
                    s_, s_b = sg[(k * 3 + gi) % 2], sgb[(k * 3 + gi) % 2]
                    ACT(g, s_[:], psG[:, :], AF.Sigmoid, [pGb], [s_b])
                    psY, pYb = psf(g, "mxY", [3, 4, 5])
                    MM(g, [mmf(psY[:, :], y_[:, c0 + c, dc * 128:(dc + 1) * 128], oT[:, c, tc * 512:(tc + 1) * 512], c == 0, c == nch - 1)
                           for c in range(nch)],
                       [y_b[gi]] + [obufs[c][tc] for c in range(nch)], [pYb])
                    if gi == 0:
                        TT(g, "dve", a_[:], psY[:, :], s_[:], ALU.mult, [pYb, s_b], [a_b])
                    else:
                        t_, t_b = tm[gi % 2], tmb[gi % 2]
                        TT(g, "dve", t_[:], psY[:, :], s_[:], ALU.mult, [pYb, s_b], [t_b])
                        if gi == 1:
                            TT(g, "pool", a_[:], a_[:], t_[:], ALU.add, [a_b, t_b], [a_b])
                        else:
                            TT(g, "pool", mixT[:, dc, tc * 512:(tc + 1) * 512], a_[:], t_[:], ALU.add, [a_b, t_b], [mxb[dc][tc]])
                k += 1
        xbs = [sb(g, st, [128, D], F32, "xb") for _ in range(3)]
        xbb = nbs(st, "xb", 3)
        nctx = norm_setup(g, st, g.norm_mlp[l:l + 1, :])
        def load_x(tb):
            DMA(g, "sp", xbs[tb % 3][:], src_ap[tb * 128:(tb + 1) * 128, :], [src_bufs[tb]], [xbb[tb % 3]])

        def part_a(tb):
            xb, xbuf = xbs[tb % 3], xbb[tb % 3]
            for nh in range(2):
                ps, pb = psf(g, "mxO", [0, 1, 2, 3, 4, 5])
                MM(g, [mmf(ps[:, :], mixT[:, c, tb * 128:(tb + 1) * 128], wo[:, c, nh * 512:(nh + 1) * 512], c == 0, c == DC - 1)
                       for c in range(DC)], [wob[nh]] + [mxb[c][tb // 4] for c in range(DC)], [pb])
                xs = xb[:, nh * 512:(nh + 1) * 512]
                TT(g, "dve", xs, ps[:, :], xs, ALU.add, [pb, xbuf], [xbuf])
            DMA(g, "sp", g.xres_d[tb * 128:(tb + 1) * 128, :], xb[:], [xbuf], [g.xres_b[tb]])

        load_x(0)
        load_x(1)
        part_a(0)
        for tb in range(NB):
            if tb + 2 < NB:
                load_x(tb + 2)
            if tb + 1 < NB:
                part_a(tb + 1)
            norm_block(g, nctx, xbs[tb % 3], xbb[tb % 3], tb, hT, hTb)


def phase_ffn(g, l, hT, hTb, dst_ap, dst_bufs):
    for half in range(2):
        last = half == 1
        with scope(g) as st:
            uT = sb(g, st, [128, 16, T], BF16, "uT")
            ub = nbs(st, "uT", 16, 4)
            wd = sb(g, st, [128, 16, D], BF16, "wd")
            wdb = nbs(st, "wd", 2)
            wdv = g.w_down[l].rearrange("(f p) n -> p f n", p=128)
            wu = [sb(g, st, [128, DC, 512], BF16, "wu") for _ in range(2)]
            wub = nbs(st, "wu", 2)
            rt = [sb(g, st, [128, 512], BF16, "rt") for _ in range(2)]
            rtb = nbs(st, "rt", 2)
            k = 0

            def load_wu(g4):
                c0 = half * 2048 + g4 * 512
                DMA(g, "pool", wu[g4 % 2][:], g.w_up[l].rearrange("(c p) n -> p c n", p=128)[:, :, c0:c0 + 512], (), [wub[g4 % 2]])

            pre0 = take_pre(g, ("wu", l, half))
            if not pre0:
                load_wu(0)
            for g4 in range(4):
                w_, w_b = wu[g4 % 2], wub[g4 % 2]
                if g4 == 0 and pre0:
                    w_, w_b = g.pre, g.preb
                if g4 + 1 < 4:
                    load_wu(g4 + 1)
                if g4 == 1:
                    for nh in range(2):
                        DMA(g, "pool", wd[:, :, nh * 512:(nh + 1) * 512], wdv[:, half * 16:(half + 1) * 16, nh * 512:(nh + 1) * 512], (),
                            [wdb[nh]])
                for fcl in range(4):
                    fc = g4 * 4 + fcl
                    for tc in range(4):
                        ps, pb = psf(g, "proj", [0, 1, 2, 3, 4, 5])
                        mm_fm(g, ps, 128, 512, w_, w_b, fcl * 128, hT, hTb, tc * 512, pb)
                        r_, r_b = rt[k % 2], rtb[k % 2]
                        k += 1
                        ACT(g, r_[:], ps[:, :], AF.Relu, [pb], [r_b])
                        TT(g, "pool", uT[:, fc, tc * 512:(tc + 1) * 512], r_[:], r_[:], ALU.mult, [r_b], [ub[fc][tc]])
            if half == 0:
                prefetch(g, ("wu", l, 1), g.w_up[l].rearrange("(c p) n -> p c n", p=128)[:, :, 2048:2560], 512)
            elif l + 1 < DEPTH:
                prefetch(g, ("sq", l + 1), win_cols(g, l + 1, C_SQ, C_SQ + 384), 384)
            xbs = [sb(g, st, [128, D], F32, "xb") for _ in range(4)]
            xbb = nbs(st, "xb", 4)
            nctx = norm_setup(g, st, g.norm_mix[l + 1:l + 2, :]) if (last and l + 1 < DEPTH) else None
            def load_xd(tb):
                DMA(g, "sp", xbs[tb % 4][:], g.xres_d[tb * 128:(tb + 1) * 128, :], [g.xres_b[tb]], [xbb[tb % 4]])

            def down_a(tb):
                xb, xbuf = xbs[tb % 4], xbb[tb % 4]
                for nh in range(2):
                    ps, pb = psf(g, "proj", [0, 1, 2, 3, 4, 5])
                    MM(g, [mmf(ps[:, :], uT[:, fc, tb * 128:(tb + 1) * 128], wd[:, fc, nh * 512:(nh + 1) * 512], fc == 0, fc == 15)
                           for fc in range(16)], [wdb[nh]] + [ub[fc][tb // 4] for fc in range(16)], [pb])
                    xs = xb[:, nh * 512:(nh + 1) * 512]
                    TT(g, "dve", xs, ps[:, :], xs, ALU.add, [pb, xbuf], [xbuf])
                if last:
                    DMA(g, "sp", dst_ap[tb * 128:(tb + 1) * 128, :], xb[:], [xbuf], [dst_bufs[tb]])
                else:
                    DMA(g, "sp", g.xres_d[tb * 128:(tb + 1) * 128, :], xb[:], [xbuf], [g.xres_b[tb]])

            load_xd(0)
            load_xd(1)
            down_a(0)
            for tb in range(NB):
                if tb + 2 < NB:
                    load_xd(tb + 2)
                if tb + 1 < NB:
                    down_a(tb + 1)
                if last and nctx is not None:
                    norm_block(g, nctx, xbs[tb % 4], xbb[tb % 4], tb, hT, hTb)


def dump_t(g, name, t, ncol):
    if g.dump == name:
        g.P.barrier()
        DMA(g, "sp", g.dbg_d[:, 0:ncol], t, [], [g.dbgb])
        g.P.wait_bufs("sp", [g.dbgb])
        g.P.barrier()


def build_layer(g, l):
    if l > 0 and g.stage < 7:
        return
    src_ap, src_bufs = (g.x_d, g.xin_b) if l == 0 else (g.xres_d, g.xres_b)
    with scope(g) as ls:
        hT, hTb = g.hT, g.hTb
        if l == 0:
            with scope(g) as st:
                norm_T(g, st, src_ap, src_bufs, g.norm_mix[l:l + 1, :], hT, hTb)
        if l == 0:
            dump_t(g, "hT", hT[:].rearrange("p c t -> p (c t)"), 8 * T)
        if g.stage < 2:
            return
        with scope(g) as ms:
            osbT = sb(g, ms, [128, 3, T], BF16, "osbT")
            odT = sb(g, ms, [128, 3, T], BF16, "odT")
            ohT = sb(g, ms, [128, 2, T], BF16, "ohT")
            osbb = nbs(ms, "osb", 3, 4)
            odb = nbs(ms, "od", 3, 4)
            ohb = nbs(ms, "oh", 2, 4)
            phase_sb(g, l, hT, hTb, osbT, osbb)
            if l == 0:
                dump_t(g, "osbT", osbT[:].rearrange("p c t -> p (c t)"), 3 * T)
            if g.stage < 3:
                return
            phase_dsa(g, l, hT, hTb, odT, odb)
            if l == 0:
                dump_t(g, "odT", odT[:].rearrange("p c t -> p (c t)"), 3 * T)
            if g.stage < 4:
                return
            phase_hgrn(g, l, hT, hTb, ohT, ohb)
            if l == 0:
                dump_t(g, "ohT", ohT[:].rearrange("p c t -> p (c t)"), 2 * T)
            if g.stage < 5:
                return
            phase_mix(g, l, hT, hTb, osbT, osbb, odT, odb, ohT, ohb, src_ap, src_bufs)
        if g.stage < 6:
            return
        if l == DEPTH - 1:
            phase_ffn(g, l, hT, hTb, g.out_d, g.out_b)
        else:
            phase_ffn(g, l, hT, hTb, g.xres_d, g.xres_b)


_NC_CACHE = {}


def rope_tables():
    half = 8
    inv = 500000.0 ** (-(np.arange(half, dtype=np.float32) * 2.0) / 16.0)
    ang = np.arange(T, dtype=np.float32)[:, None] * inv[None, :].astype(np.float32)
    return np.cos(ang).astype(np.float32), np.sin(ang).astype(np.float32)


def kernel(x, norm_mix, w_in, qn_dsa, kn_dsa, hgrn_lb, hgrn_onorm, w_br_sb, w_br_dsa, w_br_hgrn, w_out, norm_mlp, w_up, w_down):
    if "nc" not in _NC_CACHE:
        _NC_CACHE["nc"] = build_two_pass()
    nc = _NC_CACHE["nc"]
    f = lambda a: np.ascontiguousarray(np.asarray(a, dtype=np.float32))
    cs, sn = rope_tables()
    shared = dict(norm_mix=f(norm_mix), w_in=f(w_in), qn_dsa=f(qn_dsa), kn_dsa=f(kn_dsa), hgrn_lb=f(hgrn_lb),
                  hgrn_onorm=f(hgrn_onorm), w_br_sb=f(w_br_sb), w_br_dsa=f(w_br_dsa), w_br_hgrn=f(w_br_hgrn),
                  w_out=f(w_out), norm_mlp=f(norm_mlp), w_up=f(w_up), w_down=f(w_down), rope_cos=cs, rope_sin=sn)
    xs = f(x)
    in_maps = [dict(shared, x=xs[b]) for b in range(8)]
    res = run_bass_kernel_spmd(nc, in_maps, core_ids=list(range(8)))
    return np.stack([np.asarray(r["out"], dtype=np.float32) for r in res.results], axis=0)
```
